# Optimizing a Trainium2 kernel written in Bass

```python
import math
import jax, jax.numpy as jnp
from jax import lax
import numpy as np

D_MODEL = 1024
BATCH = 8
SEQ = 4096
DEPTH = 1

HY_WIDTH = 512
HY_ORDER = 2
SHORT_CONV = 3
FILTER_EMB = 33
FILTER_HIDDEN = 64
DECAY_TARGET = 1e-2
FAST_DECAY = 0.3
SLOW_DECAY = 1.5
MIN_DECAY = math.log(DECAY_TARGET) / SLOW_DECAY
MAX_DECAY = math.log(DECAY_TARGET) / FAST_DECAY

MLA_HEADS = 8
QK_NOPE = 64
QK_ROPE = 32
QK_DIM = QK_NOPE + QK_ROPE
V_DIM = 64
Q_LORA = 384
KV_LORA = 256
ATTN_WIDTH = MLA_HEADS * V_DIM
ROPE_THETA = 10000.0
Q_BLOCK = 128

EPS = 1e-6
IN_COLS = 4 * HY_WIDTH + Q_LORA + KV_LORA + QK_ROPE + ATTN_WIDTH + 2 * D_MODEL

kernel_name = 'hyena_mla_gated_parallel_encoder'


def rms_norm(x, g):
    xf = x.astype(jnp.float32)
    y = xf * lax.rsqrt(jnp.mean(xf * xf, axis=-1, keepdims=True) + EPS)
    return (y * g.astype(jnp.float32)).astype(x.dtype)


def short_conv(u, w, b):
    L = u.shape[1]
    pad = SHORT_CONV // 2
    up = jnp.pad(u, ((0, 0), (pad, pad), (0, 0)))
    return sum(up[:, j:j + L] * w[j] for j in range(SHORT_CONV)) + b


def hyena_filter_spectra(L, w_f1, b_f1, freq_1, w_f2, b_f2, freq_2, w_f3):
    t = jnp.linspace(0.0, 1.0, L, dtype=jnp.float32)[:, None]
    bands = (FILTER_EMB - 1) // 2
    f = jnp.linspace(1e-4, bands - 1, bands, dtype=jnp.float32)
    ang = (2.0 * jnp.pi / L) * jnp.arange(L, dtype=jnp.float32)[:, None] * f[None, :]
    z = jnp.concatenate([t, jnp.cos(ang), -jnp.sin(ang)], axis=-1)
    h = jnp.sin(freq_1 * (z @ w_f1 + b_f1))
    h = jnp.sin(freq_2 * (h @ w_f2 + b_f2))
    k = (h @ w_f3).reshape(L, HY_ORDER, 2, HY_WIDTH)
    deltas = jnp.abs(jnp.linspace(MIN_DECAY, MAX_DECAY, HY_WIDTH, dtype=jnp.float32))
    decay = jnp.exp(-t * deltas)
    k = k * decay[:, None, None, :]
    k_full = jnp.concatenate([k[:, :, 0], k[::-1, :, 1]], axis=0)
    return jnp.fft.rfft(k_full.astype(jnp.float32), axis=0)


def long_conv(u, k_spec, bias):
    L = u.shape[1]
    uf = u.astype(jnp.float32)
    y = jnp.fft.irfft(jnp.fft.rfft(uf, n=2 * L, axis=1) * k_spec[None], n=2 * L, axis=1)[:, :L]
    return (y + uf * bias.astype(jnp.float32)).astype(u.dtype)


def rotary(t, cos, sin):
    t1, t2 = jnp.split(t, 2, axis=-1)
    c = cos[:, None, :].astype(t.dtype)
    s = sin[:, None, :].astype(t.dtype)
    return jnp.concatenate([t1 * c - t2 * s, t1 * s + t2 * c], axis=-1)


def dense_bidirectional_attention(q, k, v):
    B, L, H, Dk = q.shape
    nb = L // Q_BLOCK
    scale = 1.0 / math.sqrt(Dk)
    kt = k.transpose(0, 2, 1, 3)
    vt = v.transpose(0, 2, 1, 3)
    qb = q.reshape(B, nb, Q_BLOCK, H, Dk).transpose(1, 0, 3, 2, 4)

    def one_block(qblk):
        s = jnp.einsum('bhqd,bhkd->bhqk', qblk, kt).astype(jnp.float32) * scale
        p = jax.nn.softmax(s, axis=-1).astype(vt.dtype)
        return jnp.einsum('bhqk,bhkd->bqhd', p, vt)

    o = lax.map(one_block, qb)
    return o.transpose(1, 0, 2, 3, 4).reshape(B, L, H * v.shape[-1])


def setup_inputs(seed: int = 0) -> dict:
    key = jax.random.key(seed)
    ks = jax.random.split(key, 24)
    f32 = jnp.float32

    def nrm(k, shape, scale):
        return jax.random.normal(k, shape, f32) * scale

    n = DEPTH
    return {
        'x': nrm(ks[0], (BATCH, SEQ, D_MODEL), 1.0),
        'g_norm': 1.0 + nrm(ks[1], (n, D_MODEL), 0.02),
        'w_in': nrm(ks[2], (n, D_MODEL, IN_COLS), D_MODEL ** -0.5),
        'b_gate': nrm(ks[3], (n, 2 * D_MODEL), 0.02),
        'w_short': nrm(ks[4], (n, SHORT_CONV, 3 * HY_WIDTH), SHORT_CONV ** -0.5),
        'b_short': nrm(ks[5], (n, 3 * HY_WIDTH), 0.02),
        'w_f1': nrm(ks[6], (n, FILTER_EMB, FILTER_HIDDEN), FILTER_EMB ** -0.5),
        'b_f1': nrm(ks[7], (n, FILTER_HIDDEN), 0.02),
        'freq_1': 1.0 + nrm(ks[8], (n, FILTER_HIDDEN), 0.02),
        'w_f2': nrm(ks[9], (n, FILTER_HIDDEN, FILTER_HIDDEN), FILTER_HIDDEN ** -0.5),
        'b_f2': nrm(ks[10], (n, FILTER_HIDDEN), 0.02),
        'freq_2': 1.0 + nrm(ks[11], (n, FILTER_HIDDEN), 0.02),
        'w_f3': nrm(ks[12], (n, FILTER_HIDDEN, HY_ORDER * 2 * HY_WIDTH), 2.0 * (FILTER_HIDDEN * SEQ) ** -0.5),
        'hy_bias': nrm(ks[13], (n, HY_ORDER, HY_WIDTH), 0.1),
        'w_hy_out': nrm(ks[14], (n, HY_WIDTH, D_MODEL), HY_WIDTH ** -0.5),
        'g_cq': 1.0 + nrm(ks[15], (n, Q_LORA), 0.02),
        'w_uq': nrm(ks[16], (n, Q_LORA, MLA_HEADS * QK_DIM), Q_LORA ** -0.5),
        'g_ckv': 1.0 + nrm(ks[17], (n, KV_LORA), 0.02),
        'w_ukv': nrm(ks[18], (n, KV_LORA, MLA_HEADS * (QK_NOPE + V_DIM)), KV_LORA ** -0.5),
        'g_qn': 1.0 + nrm(ks[19], (n, QK_DIM), 0.02),
        'g_kn': 1.0 + nrm(ks[20], (n, QK_DIM), 0.02),
        'w_attn_out': nrm(ks[21], (n, ATTN_WIDTH, D_MODEL), ATTN_WIDTH ** -0.5),
        'w_out': nrm(ks[22], (n, D_MODEL, D_MODEL), D_MODEL ** -0.5),
    }


def reference(x, g_norm, w_in, b_gate, w_short, b_short, w_f1, b_f1, freq_1, w_f2, b_f2, freq_2,
              w_f3, hy_bias, w_hy_out, g_cq, w_uq, g_ckv, w_ukv, g_qn, g_kn, w_attn_out, w_out):
    B, L, _ = x.shape
    pos = jnp.arange(L, dtype=jnp.float32)
    inv_freq = ROPE_THETA ** (-jnp.arange(0, QK_ROPE, 2, dtype=jnp.float32) / QK_ROPE)
    ang = pos[:, None] * inv_freq[None, :]
    cos, sin = jnp.cos(ang), jnp.sin(ang)

    split_at = list(np.cumsum([3 * HY_WIDTH, HY_WIDTH, Q_LORA, KV_LORA, QK_ROPE, ATTN_WIDTH, D_MODEL]))

    for l in range(DEPTH):
        h = rms_norm(x, g_norm[l])
        proj = h @ w_in[l]
        hy_in, z_hy, c_q, c_kv, k_rope, z_attn, gate_hy, gate_attn = jnp.split(proj, split_at, axis=-1)

        u = short_conv(hy_in, w_short[l], b_short[l])
        v_h, x1, x2 = jnp.split(u, 3, axis=-1)
        k_spec = hyena_filter_spectra(L, w_f1[l], b_f1[l], freq_1[l], w_f2[l], b_f2[l], freq_2[l], w_f3[l])
        zc = v_h
        for o, gate in enumerate((x1, x2)):
            zc = gate * long_conv(zc, k_spec[:, o], hy_bias[l, o])
        u_hy = (zc * jax.nn.silu(z_hy)) @ w_hy_out[l]

        q = (rms_norm(c_q, g_cq[l]) @ w_uq[l]).reshape(B, L, MLA_HEADS, QK_DIM)
        kv = (rms_norm(c_kv, g_ckv[l]) @ w_ukv[l]).reshape(B, L, MLA_HEADS, QK_NOPE + V_DIM)
        k_nope, v_a = kv[..., :QK_NOPE], kv[..., QK_NOPE:]
        k_r = jnp.broadcast_to(k_rope[:, :, None, :], (B, L, MLA_HEADS, QK_ROPE))
        k = jnp.concatenate([k_nope, k_r], axis=-1)
        q = rms_norm(q, g_qn[l])
        k = rms_norm(k, g_kn[l])
        q = jnp.concatenate([q[..., :QK_NOPE], rotary(q[..., QK_NOPE:], cos, sin)], axis=-1)
        k = jnp.concatenate([k[..., :QK_NOPE], rotary(k[..., QK_NOPE:], cos, sin)], axis=-1)
        attn = dense_bidirectional_attention(q, k, v_a)
        u_attn = (attn * jax.nn.silu(z_attn)) @ w_attn_out[l]

        gates = jax.nn.sigmoid(jnp.concatenate([gate_hy, gate_attn], axis=-1) + b_gate[l])
        g_hy, g_at = jnp.split(gates, 2, axis=-1)
        merged = g_hy * u_hy + g_at * u_attn
        x = x + merged @ w_out[l]
    return x
```

```python
import concourse.bass as bass
import concourse.mybir as mybir

_ESZ = {}


def _esize(dt):
    s = _ESZ.get(dt)
    if s is None:
        n = str(dt)
        if '32' in n:
            s = 4
        elif '16' in n:
            s = 2
        elif '8' in n:
            s = 1
        else:
            s = 4
        _ESZ[dt] = s
    return s


def footprint(ap):
    t = ap.tensor
    name = t.name
    es = _esize(ap.dtype)
    apl = ap.ap
    off = int(ap.offset) * es
    space = str(type(t).__name__)
    if 'DRam' in space:
        lo = off
        hi = off
        for st, cnt in apl:
            if cnt > 1:
                d = (cnt - 1) * st * es
                if d > 0:
                    hi += d
                else:
                    lo += d
        return (name, 0, 1, lo, hi + es)
    pstep, pcnt = apl[0]
    pstep_b = pstep * es
    if pstep_b > 0:
        p0 = off // pstep_b
        f0 = off % pstep_b
    else:
        p0 = 0
        f0 = off
    lo = f0
    hi = f0
    for st, cnt in apl[1:]:
        if cnt > 1:
            d = (cnt - 1) * st * es
            if d > 0:
                hi += d
            else:
                lo += d
    return (name, p0, p0 + pcnt, lo, hi + es)


COMPUTE = ('pe', 'act', 'dve', 'pool')
QUEUES = ('pe', 'act', 'dve', 'pool', 'sp')
QIDX = {q: i for i, q in enumerate(QUEUES)}


class _Op:
    __slots__ = ('q', 'fn', 'dma', 'idx', 'gid', 'waits_c', 'waits_d', 'signal', 'snap', 'slot', 'slot_cnt', 'prev_slot')


class Prog:
    def __init__(self, nc, dma_slots=None):
        self.nc = nc
        self.streams = {q: [] for q in QUEUES}
        self.recs = {}
        self.known = {q: [-1] * len(QUEUES) for q in QUEUES}
        self.known_dma = {q: set() for q in QUEUES}
        self.ops = []
        self.dma_slots = dma_slots or {'sp': 8, 'pool': 4, 'act': 4}
        self.dma_count = {q: 0 for q in QUEUES}
        self.dma_ops = {q: [] for q in QUEUES}
        self.n_comp = {q: 0 for q in QUEUES}

    def add(self, q, fn, reads=(), writes=(), dma=False):
        op = _Op()
        op.q = q
        op.fn = fn
        op.dma = dma
        op.gid = len(self.ops)
        op.signal = dma
        op.waits_c = []
        op.waits_d = []
        op.slot = None
        op.prev_slot = None
        stream = self.streams[q]
        if not dma:
            op.idx = self.n_comp[q]
            self.n_comp[q] += 1
        else:
            op.idx = -1
        deps_c = {}
        deps_d = set()

        def scan(fp, is_write):
            name, p0, p1, f0, f1 = fp
            lst = self.recs.get(name)
            if not lst:
                return
            for r in lst:
                (rp0, rp1, rf0, rf1, rw, rop) = r
                if not (is_write or rw):
                    continue
                if rp1 <= p0 or p1 <= rp0 or rf1 <= f0 or f1 <= rf0:
                    continue
                if rop.dma:
                    deps_d.add(rop)
                else:
                    e = rop.q
                    if deps_c.get(e, -1) < rop.idx:
                        deps_c[e] = rop.idx

        rfps = [footprint(a) for a in reads]
        wfps = [footprint(a) for a in writes]
        for fp in rfps:
            scan(fp, False)
        for fp in wfps:
            scan(fp, True)
        known = self.known[q]
        kd = self.known_dma[q]
        for e, i in deps_c.items():
            ei = QIDX[e]
            if e == q and not dma:
                if q == 'pe' or i < op.idx - 1:
                    continue
            if i <= known[ei]:
                continue
            op.waits_c.append((e, i))
            src = self.comp_ops[e][i]
            src.signal = True
            known[ei] = i
            for k, v in enumerate(src.snap):
                if v > known[k]:
                    known[k] = v
        for d in sorted(deps_d, key=lambda o: o.gid):
            if d.gid in kd:
                continue
            op.waits_d.append(d)
            kd.add(d.gid)
            for k, v in enumerate(d.snap):
                if v > known[k]:
                    known[k] = v
        if dma:
            n = self.dma_count[q]
            R = self.dma_slots[q]
            op.slot = n % R
            op.slot_cnt = n // R + 1
            if n >= R:
                prev = self.dma_ops[q][n - R]
                op.prev_slot = prev
                kd.add(prev.gid)
            self.dma_count[q] = n + 1
            self.dma_ops[q].append(op)
        op.snap = tuple(known)
        if not dma:
            self.comp_ops[q].append(op)
        for fp, is_write in [(f, False) for f in rfps] + [(f, True) for f in wfps]:
            name, p0, p1, f0, f1 = fp
            lst = self.recs.setdefault(name, [])
            if is_write:
                lst[:] = [r for r in lst if not (r[0] >= p0 and r[1] <= p1 and r[2] >= f0 and r[3] <= f1)]
            else:
                if not dma:
                    lst[:] = [r for r in lst if not (r[4] is False and (not r[5].dma) and r[5].q == q
                                                     and r[0] == p0 and r[1] == p1 and r[2] == f0 and r[3] == f1)]
            lst.append((p0, p1, f0, f1, is_write, op))
        stream.append(op)
        self.ops.append(op)
        return op

    comp_ops = None

    def start(self):
        self.comp_ops = {q: [] for q in QUEUES}

    def pe(self, fn, reads, writes):
        return self.add('pe', fn, reads, writes)

    def act(self, fn, reads, writes):
        return self.add('act', fn, reads, writes)

    def dve(self, fn, reads, writes):
        return self.add('dve', fn, reads, writes)

    def pool(self, fn, reads, writes):
        return self.add('pool', fn, reads, writes)

    def dma(self, out, in_, q='sp', **kw):
        return self.add(q, lambda e: e.dma_start(out=out, in_=in_, **kw), [in_], [out], dma=True)

    def emit(self, block, sems_c, sems_d):
        cum = {}
        for e in QUEUES:
            c = 0
            arr = []
            for o in self.comp_ops[e]:
                if o.signal:
                    c += 1
                arr.append(c)
            cum[e] = arr
        self.cum = cum

        def gen(q):
            def body(eng):
                for o in self.streams[q]:
                    for (e, i) in o.waits_c:
                        eng.wait_ge(sems_c[e], cum[e][i])
                    for d in o.waits_d:
                        eng.wait_ge(sems_d[d.q][d.slot], 16 * d.slot_cnt)
                    if o.prev_slot is not None:
                        p = o.prev_slot
                        eng.wait_ge(sems_d[p.q][p.slot], 16 * p.slot_cnt)
                    ins = o.fn(eng)
                    if o.dma:
                        ins.then_inc(sems_d[q][o.slot], 16)
                    elif o.signal:
                        ins.then_inc(sems_c[q], 1)
                R = self.dma_slots.get(q, 0)
                n = self.dma_count[q]
                for o in self.dma_ops[q][max(0, n - R):]:
                    eng.wait_ge(sems_d[q][o.slot], 16 * o.slot_cnt)
            return body

        if self.streams['pe']:
            block.tensor(gen('pe'))
        if self.streams['act']:
            block.scalar(gen('act'))
        if self.streams['dve']:
            block.vector(gen('dve'))
        if self.streams['pool']:
            block.gpsimd(gen('pool'))
        if self.streams['sp']:
            block.sync(gen('sp'))

import math
from contextlib import ExitStack
import numpy as np
import ml_dtypes
from concourse.bass_utils import run_bass_kernel_spmd

F32 = mybir.dt.float32
BF = mybir.dt.bfloat16
AF = mybir.ActivationFunctionType
ALU = mybir.AluOpType
AX = mybir.AxisListType

L = 4096
D = 1024
NT = 32
NB = 8
EPS = 1e-6
NF = 8192
HYW = 512
COL_V, COL_X1, COL_X2, COL_ZH = 0, 512, 1024, 1536
COL_CQ, COL_CKV, COL_KR, COL_ZA = 2048, 2432, 2688, 2720
COL_GH, COL_GA = 3232, 4256
MAGIC = 12582912.0
DBG = {}


def bcast(ap, axis, n):
    a = ap.unsqueeze(axis)
    shp = list(a.shape)
    shp[axis] = n
    return a.to_broadcast(shp)


class K:
    def __init__(self, P):
        self.P = P

    def mm(self, out, lhsT, rhs, start=True, stop=True):
        self.P.pe(lambda e: e.matmul(out, lhsT=lhsT, rhs=rhs, start=start, stop=stop), [lhsT, rhs], [out])

    def tr(self, out, in_, ident):
        self.P.pe(lambda e: e.transpose(out=out, in_=in_, identity=ident), [in_, ident], [out])

    def act(self, out, in_, func, bias=None, scale=None, accum_out=None):
        kw = {}
        reads = [in_]
        writes = [out]
        if bias is not None:
            kw['bias'] = bias
            if not isinstance(bias, (int, float)):
                reads.append(bias)
        if scale is not None:
            kw['scale'] = scale
            if not isinstance(scale, (int, float)):
                reads.append(scale)
        if accum_out is not None:
            kw['accum_out'] = accum_out
            writes.append(accum_out)
        self.P.act(lambda e: e.activation(out=out, in_=in_, func=func, **kw), reads, writes)

    def tt(self, eng, out, in0, in1, op):
        self.P.add(eng, lambda e: e.tensor_tensor(out=out, in0=in0, in1=in1, op=op), [in0, in1], [out])

    def ts(self, eng, out, in0, s1, s2=None, op0=ALU.mult, op1=None):
        reads = [in0]
        if not isinstance(s1, (int, float)):
            reads.append(s1)
        if s2 is not None and not isinstance(s2, (int, float)):
            reads.append(s2)
        if op1 is None:
            self.P.add(eng, lambda e: e.tensor_scalar(out=out, in0=in0, scalar1=s1, scalar2=None, op0=op0), reads, [out])
        else:
            self.P.add(eng, lambda e: e.tensor_scalar(out=out, in0=in0, scalar1=s1, scalar2=s2, op0=op0, op1=op1), reads, [out])

    def stt(self, out, in0, scalar, in1, op0, op1):
        reads = [in0, in1]
        if not isinstance(scalar, (int, float)):
            reads.append(scalar)
        self.P.dve(lambda e: e.scalar_tensor_tensor(out=out, in0=in0, scalar=scalar, in1=in1, op0=op0, op1=op1), reads, [out])

    def copy(self, eng, out, in_):
        if eng == 'act':
            self.act(out, in_, AF.Copy)
        else:
            self.P.add(eng, lambda e: e.tensor_copy(out=out, in_=in_), [in_], [out])

    def recip(self, out, in_):
        self.P.dve(lambda e: e.reciprocal(out=out, in_=in_), [in_], [out])

    def reduce_add(self, out, in_):
        self.P.dve(lambda e: e.tensor_reduce(out=out, in_=in_, axis=AX.X, op=ALU.add), [in_], [out])

    def memset(self, eng, ap, val):
        self.P.add(eng, lambda e: e.memset(ap, val), [], [ap])

    def dma(self, out, in_, q='sp'):
        self.P.dma(out, in_, q=q)


def _dump(P):
    cum = {}
    for e in QUEUES:
        c = 0
        arr = []
        for o in P.comp_ops[e]:
            if o.signal:
                c += 1
            arr.append(c)
        cum[e] = arr
    for q in QUEUES:
        print("== stream", q)
        for o in P.streams[q]:
            w = [f"{e}>={cum[e][i]}(op{i})" for e, i in o.waits_c] + [f"dma[{d.q}{d.slot}]>={16*d.slot_cnt}" for d in o.waits_d]
            if o.prev_slot is not None:
                w.append(f"prev dma[{o.prev_slot.q}{o.prev_slot.slot}]>={16*o.prev_slot.slot_cnt}")
            tag = f"DMA slot{o.slot} cnt{o.slot_cnt}" if o.dma else (f"op{o.idx} sig={cum[q][o.idx] if o.signal else '-'}")
            print("   ", tag, getattr(o, 'desc', ''), "waits:", w)


class Phase:
    def __init__(self, nc, name):
        self.nc = nc
        self.name = name
        self.es = ExitStack()

    def __enter__(self):
        nc = self.nc
        es = self.es
        es.__enter__()
        self.ps = [es.enter_context(nc.psum_tensor(f"{self.name}_ps{i}", [128, 512], F32)) for i in range(8)]
        self.sems_c = {e: es.enter_context(nc.semaphore(f"{self.name}_sc_{e}")) for e in QUEUES}
        self.sems_d = {q: [es.enter_context(nc.semaphore(f"{self.name}_sd_{q}{i}")) for i in range(n)]
                       for q, n in (('sp', 8), ('pool', 4), ('act', 4))}
        self.P = Prog(nc)
        self.P.start()
        self.k = K(self.P)
        return self

    def sb(self, name, shape, dt):
        return self.es.enter_context(self.nc.sbuf_tensor(f"{self.name}_{name}", shape, dt))

    def __exit__(self, *a):
        if a[0] is None:
            self.es.enter_context(self.nc.allow_low_precision("bf16 operands / intermediates by design"))
            block = self.es.enter_context(self.nc.Block())
            if DBG.get('dump') == self.name:
                _dump(self.P)
            self.P.emit(block, self.sems_c, self.sems_d)
        return self.es.__exit__(*a)


def make_ident(ph, ident):
    identf = ph.sb("identf", [128, 128], F32)
    ph.k.memset('pool', identf[:], 0.0)
    ph.P.pool(lambda e: e.affine_select(out=identf[:], in_=identf[:], pattern=[[-1, 128]], compare_op=ALU.not_equal,
                                        fill=1.0, base=0, channel_multiplier=1), [identf[:]], [identf[:]])
    ph.k.copy('dve', ident[:], identf[:])


def load_w(ph, dst, src_ap):
    ph.k.dma(dst, src_ap, q='pool')


def rstd_from_ss(k, rs_col, ss_col, n):
    k.act(rs_col, ss_col, AF.Sqrt, bias=EPS, scale=1.0 / n)
    k.recip(rs_col, rs_col)


def phase1(nc, T):
    with Phase(nc, "p1") as ph:
        k = ph.k
        xt = [ph.sb(f"xt{i}", [128, D], F32) for i in range(3)]
        xn = [ph.sb(f"xn{i}", [128, D], BF) for i in range(2)]
        junk = ph.sb("junk", [128, D], BF)
        ss = ph.sb("ss", [128, NT], F32)
        rs = ph.sb("rs", [128, NT], F32)
        gT = ph.sb("gT", [128, 8], F32)
        ident = ph.sb("ident", [128, 128], BF)
        hb = [ph.sb(f"hb{i}", [128, 8, 512], BF) for i in range(2)]
        make_ident(ph, ident)
        k.dma(gT[:], T['gT'][:, :])
        for i in range(NT):
            x_t = xt[i % 3]
            k.dma(x_t[:], T['x'][128 * i:128 * (i + 1), :])
            k.act(junk[:], x_t[:], AF.Square, accum_out=ss[:, i:i + 1])
            rstd_from_ss(k, rs[:, i:i + 1], ss[:, i:i + 1], D)
            x_n = xn[i % 2]
            k.ts('dve', x_n[:], x_t[:], rs[:, i:i + 1])
            bank = ph.ps[i % 4]
            pv = bank[:, :].bitcast(BF)
            for c in range(8):
                k.tr(pv[:, 128 * c:128 * (c + 1)], x_n[:, 128 * c:128 * (c + 1)], ident[:])
            h_b = hb[(i // 4) % 2]
            j = i % 4
            k.tt('dve', h_b[:, :, 128 * j:128 * (j + 1)], pv.rearrange("p (c t) -> p c t", c=8),
                 bcast(gT[:, :], 2, 128), ALU.mult)
            if j == 3:
                k.dma(T['hT_d'][i // 4], h_b[:].rearrange("p c t -> p (c t)"))


def qk_norm_rope(ph, W, src, dst, g_rep, cos_t, sin_t):
    k = ph.k
    sq, ssq, rk, ta, tb_ = W['sq'], W['ssq'], W['rk'], W['ta'], W['tb']
    k.tt('pool', sq[:], src[:], src[:], ALU.mult)
    k.reduce_add(ssq[:], sq[:])
    rstd_from_ss(k, rk[:], ssq[:], 96)
    k.tt('dve', src[:], src[:], bcast(rk[:, :], 2, 96), ALU.mult)
    k.tt('pool', src[:], src[:], bcast(g_rep, 1, 8), ALU.mult)
    t1 = src[:, :, 64:80]
    t2 = src[:, :, 80:96]
    cb = bcast(cos_t, 1, 8)
    sbb = bcast(sin_t, 1, 8)
    k.tt('pool', ta[:], t1, cb, ALU.mult)
    k.tt('pool', tb_[:], t2, sbb, ALU.mult)
    k.tt('dve', dst[:, :, 64:80], ta[:], tb_[:], ALU.subtract)
    k.tt('pool', ta[:], t1, sbb, ALU.mult)
    k.tt('pool', tb_[:], t2, cb, ALU.mult)
    k.tt('dve', dst[:, :, 80:96], ta[:], tb_[:], ALU.add)
    k.copy('pool', dst[:, :, 0:64], src[:, :, 0:64])


def phase3(nc, T):
    with Phase(nc, "p3") as ph:
        k = ph.k
        ps = ph.ps
        ident = ph.sb("ident", [128, 128], BF)
        make_ident(ph, ident)
        w_in_v = T['w_in'].rearrange("(k p) n -> p k n", p=128)
        Wkv = ph.sb("Wkv", [128, 8, 288], BF)
        Wq = ph.sb("Wq", [128, 8, 384], BF)
        Wuq = ph.sb("Wuq", [128, 3, 768], BF)
        Wukv = ph.sb("Wukv", [128, 2, 1024], BF)
        gcq = ph.sb("gcq", [128, 3], F32)
        gckv = ph.sb("gckv", [128, 2], F32)
        gqk = ph.sb("gqk", [128, 192], F32)
        cosT = ph.sb("cosT", [128, NT, 16], F32)
        sinT = ph.sb("sinT", [128, NT, 16], F32)
        load_w(ph, Wkv[:], w_in_v[:, :, COL_CKV:COL_CKV + 288])
        load_w(ph, Wq[:], w_in_v[:, :, COL_CQ:COL_CQ + 384])
        load_w(ph, Wuq[:], T['w_uq'].rearrange("(k p) n -> p k n", p=128))
        load_w(ph, Wukv[:], T['w_ukv'].rearrange("(k p) n -> p k n", p=128))
        k.dma(gcq[:], T['gcqT'][:, :])
        k.dma(gckv[:], T['gckvT'][:, :])
        k.dma(gqk[:], T['gqk'][0:1, :].partition_broadcast(128))
        k.dma(cosT[:].rearrange("p a b -> p (a b)"), T['cosT'][:, :])
        k.dma(sinT[:].rearrange("p a b -> p (a b)"), T['sinT'][:, :])
        k.tt('pool', Wuq[:], Wuq[:], bcast(gcq[:, :], 2, 768), ALU.mult)
        k.tt('pool', Wukv[:], Wukv[:], bcast(gckv[:, :], 2, 1024), ALU.mult)

        kT = ph.sb("kT", [128, 8, L], BF)
        vx = ph.sb("vx", [128, NT, 8, 65], BF)
        k.memset('pool', vx[:, :, :, 64:65], 1.0)
        ones = ph.sb("ones", [128, 64], BF)
        k.memset('pool', ones[:], 1.0)
        hbuf = [ph.sb(f"hbuf{i}", [128, 8, 512], BF) for i in range(2)]
        junk = ph.sb("junk", [128, 384], BF)
        ssl = ph.sb("ssl", [128, 2 * NT], F32)
        rsl = ph.sb("rsl", [128, 2 * NT], F32)
        latn = [ph.sb(f"latn{i}", [128, 384], BF) for i in range(2)]
        latT = [ph.sb(f"latT{i}", [128, 3, 128], BF) for i in range(2)]
        kr = ph.sb("kr", [128, 32], F32)
        qk32 = [ph.sb(f"qk32_{i}", [128, 8, 96], F32) for i in range(2)]
        qkbf = [ph.sb(f"qkbf{i}", [128, 8, 96], BF) for i in range(2)]
        Wk_ = dict(sq=ph.sb("sq", [128, 8, 96], F32), ssq=ph.sb("ssq", [128, 8], F32), rk=ph.sb("rk", [128, 8], F32),
                   ta=ph.sb("ta", [128, 8, 16], F32), tb=ph.sb("tb", [128, 8, 16], F32))
        qT = [ph.sb(f"qT{i}", [128, 8, 512], BF) for i in range(2)]
        pt = [ph.sb(f"pt{i}", [128, 512], BF) for i in range(4)]
        rsum = ph.sb("rsum", [128, 512], BF)
        bcs = ph.sb("bcs", [64, 512], F32)
        at = [ph.sb(f"at{i}", [64, 8, 512], BF) for i in range(2)]

        for tb in range(DBG.get('nprep', NB)):
            hb = hbuf[tb % 2]
            k.dma(hb[:].rearrange("p c t -> p (c t)"), T['hT_d'][tb])
            for j in range(4):
                i = tb * 4 + j
                lat = ps[i % 2][:, 0:288]
                for c in range(8):
                    k.mm(lat, hb[:, c, 128 * j:128 * (j + 1)], Wkv[:, c, :], start=(c == 0), stop=(c == 7))
                k.act(junk[:, 0:256], lat[:, 0:256], AF.Square, accum_out=ssl[:, i:i + 1])
                rstd_from_ss(k, rsl[:, i:i + 1], ssl[:, i:i + 1], 256)
                ln = latn[i % 2]
                k.ts('dve', ln[:, 0:256], lat[:, 0:256], rsl[:, i:i + 1])
                if DBG.get('st', 9) < 2: continue
                k.copy('act', kr[:], lat[:, 256:288])
                tbank = ps[2][:, :].bitcast(BF)
                for c in range(2):
                    k.tr(tbank[:, 128 * c:128 * (c + 1)], ln[:, 128 * c:128 * (c + 1)], ident[:])
                lT = latT[i % 2]
                k.copy('dve', lT[:, 0:2, :].rearrange("p c t -> p (c t)"), tbank[:, 0:256])
                if DBG.get('st', 9) < 3: continue
                kvb = [ps[3 + 2 * (i % 2)], ps[4 + 2 * (i % 2)]]
                for half in range(2):
                    for c in range(2):
                        k.mm(kvb[half][:, :], lT[:, c, :], Wukv[:, c, 512 * half:512 * (half + 1)],
                             start=(c == 0), stop=(c == 1))
                kk = qk32[i % 2]
                for half in range(2):
                    kvv = kvb[half][:, :].rearrange("p (h e) -> p h e", h=4)
                    k.copy('dve', kk[:, 4 * half:4 * half + 4, 0:64], kvv[:, :, 0:64])
                    k.copy('dve', vx[:, i, 4 * half:4 * half + 4, 0:64], kvv[:, :, 64:128])
                k.copy('pool', kk[:, :, 64:96], bcast(kr[:, :], 1, 8))
                if DBG.get('st', 9) < 4: continue
                kf = qkbf[i % 2]
                qk_norm_rope(ph, Wk_, kk, kf, gqk[:, 96:192], cosT[:, i, :], sinT[:, i, :])
                if DBG.get('st', 9) < 5: continue
                kbank = ps[7][:, :].bitcast(BF)
                for h in range(8):
                    k.tr(kbank[0:96, 128 * h:128 * (h + 1)], kf[:, h, :], ident[:])
                k.copy('dve', kT[0:96, :, 128 * i:128 * (i + 1)], kbank[0:96, :].rearrange("p (h t) -> p h t", h=8))

        scale = 1.0 / math.sqrt(96.0)
        for qc in range(DBG.get('nqc', NB)):
            hb = hbuf[qc % 2]
            k.dma(hb[:].rearrange("p c t -> p (c t)"), T['hT_d'][qc])
            q_T = qT[qc % 2]
            for j in range(4):
                i = qc * 4 + j
                lat = ps[6][:, 0:384]
                for c in range(8):
                    k.mm(lat, hb[:, c, 128 * j:128 * (j + 1)], Wq[:, c, :], start=(c == 0), stop=(c == 7))
                k.act(junk[:, 0:384], lat, AF.Square, accum_out=ssl[:, NT + i:NT + i + 1])
                rstd_from_ss(k, rsl[:, NT + i:NT + i + 1], ssl[:, NT + i:NT + i + 1], 384)
                ln = latn[i % 2]
                k.ts('dve', ln[:, 0:384], lat, rsl[:, NT + i:NT + i + 1])
                tbank = ps[7][:, :].bitcast(BF)
                for c in range(3):
                    k.tr(tbank[:, 128 * c:128 * (c + 1)], ln[:, 128 * c:128 * (c + 1)], ident[:])
                lT = latT[i % 2]
                k.copy('dve', lT[:].rearrange("p c t -> p (c t)"), tbank[:, 0:384])
                qq = qk32[i % 2]
                for half in range(2):
                    qb = ps[6][:, 0:384]
                    for c in range(3):
                        k.mm(qb, lT[:, c, :], Wuq[:, c, 384 * half:384 * (half + 1)], start=(c == 0), stop=(c == 2))
                    k.copy('dve', qq[:, 4 * half:4 * half + 4, :], qb.rearrange("p (h e) -> p h e", h=4))
                qf = qkbf[i % 2]
                qk_norm_rope(ph, Wk_, qq, qf, gqk[:, 0:96], cosT[:, i, :], sinT[:, i, :])
                qbank = ps[7][:, :].bitcast(BF)
                for h in range(8):
                    k.tr(qbank[0:96, 128 * h:128 * (h + 1)], qf[:, h, :], ident[:])
                k.copy('dve', q_T[0:96, :, 128 * j:128 * (j + 1)], qbank[0:96, :].rearrange("p (h t) -> p h t", h=8))
            a_t = at[qc % 2]
            for h in range(DBG.get('nh', 8)):
                ob = ps[3 + (h % 2)]
                for kt in range(NT):
                    sbk = ps[kt % 3]
                    k.mm(sbk[:, :], kT[0:96, h, 128 * kt:128 * (kt + 1)], q_T[0:96, h, :])
                    p_t = pt[kt % 4]
                    k.act(p_t[:], sbk[:, :], AF.Exp, scale=scale)
                    k.mm(ob[0:65, :], vx[:, kt, h, :], p_t[:], start=(kt == 0), stop=(kt == NT - 1))
                k.recip(rsum[64:65, :], ob[64:65, :])
                k.mm(ps[5][0:64, :], ones[64:65, :], rsum[64:65, :])
                k.copy('act', bcs[:], ps[5][0:64, :])
                k.tt('dve', a_t[:, h, :], ob[0:64, :], bcs[:], ALU.mult)
            k.dma(T['at_d'][qc], a_t[:].rearrange("p h t -> p (h t)"))


def phase4(nc, T):
    with Phase(nc, "p4") as ph:
        k = ph.k
        ps = ph.ps
        w_in_v = T['w_in'].rearrange("(k p) n -> p k n", p=128)
        Wz = ph.sb("Wz", [128, 8, 512], BF)
        Wg = ph.sb("Wg", [128, 8, 2048], BF)
        Wao = ph.sb("Wao", [64, 8, D], BF)
        Who = ph.sb("Who", [128, 4, D], BF)
        Wout = ph.sb("Wout", [128, 8, D], BF)
        bg = ph.sb("bg", [128, 16], F32)
        load_w(ph, Wz[:], w_in_v[:, :, COL_ZA:COL_ZA + 512])
        for q4 in range(4):
            load_w(ph, Wg[:, :, 512 * q4:512 * (q4 + 1)], w_in_v[:, :, COL_GH + 512 * q4:COL_GH + 512 * (q4 + 1)])
        load_w(ph, Wao[:], T['w_attn_out'].rearrange("(h p) n -> p h n", p=64))
        load_w(ph, Who[:], T['w_hy_out'].rearrange("(k p) n -> p k n", p=128))
        load_w(ph, Wout[:], T['w_out'].rearrange("(k p) n -> p k n", p=128))
        k.dma(bg[:], T['bgT'][:, :])
        hbuf = [ph.sb(f"hbuf{i}", [128, 8, 512], BF) for i in range(2)]
        atb = [ph.sb(f"atb{i}", [64, 8, 512], BF) for i in range(2)]
        yzb = [ph.sb(f"yzb{i}", [128, 4, 512], BF) for i in range(2)]
        xt = [ph.sb(f"xt{i}", [128, D], F32) for i in range(3)]
        ot = [ph.sb(f"ot{i}", [128, D], F32) for i in range(2)]
        sz = [ph.sb(f"sz{i}", [64, 512], BF) for i in range(2)]
        ya = ph.sb("ya", [64, 8, 512], BF)
        gh = [ph.sb(f"gh{i}", [128, 512], BF) for i in range(2)]
        ga = [ph.sb(f"ga{i}", [128, 512], BF) for i in range(2)]
        m1 = [ph.sb(f"m1{i}", [128, 512], F32) for i in range(2)]
        m2 = [ph.sb(f"m2{i}", [128, 512], F32) for i in range(2)]
        mg = [ph.sb(f"mg{i}", [128, 8, 512], BF) for i in range(2)]
        for tb in range(NB):
            hb = hbuf[tb % 2]
            a_b = atb[tb % 2]
            y_b = yzb[tb % 2]
            k.dma(hb[:].rearrange("p c t -> p (c t)"), T['hT_d'][tb])
            k.dma(a_b[:].rearrange("p h t -> p (h t)"), T['at_d'][tb])
            k.dma(y_b[:].rearrange("p c t -> p (c t)"), T['yz_d'][tb])
            for h in range(8):
                zb = ps[h % 2][0:64, :]
                for c in range(8):
                    k.mm(zb, Wz[:, c, 64 * h:64 * (h + 1)], hb[:, c, :], start=(c == 0), stop=(c == 7))
                s_z = sz[h % 2]
                k.act(s_z[:], zb, AF.Silu)
                k.tt('pool', ya[:, h, :], a_b[:, h, :], s_z[:], ALU.mult)
            m_g = mg[tb % 2]
            for dc in range(8):
                g1 = ps[2 + (dc % 2)]
                g2 = ps[4 + (dc % 2)]
                for c in range(8):
                    k.mm(g1[:, :], Wg[:, c, 128 * dc:128 * (dc + 1)], hb[:, c, :], start=(c == 0), stop=(c == 7))
                for c in range(8):
                    k.mm(g2[:, :], Wg[:, c, 1024 + 128 * dc:1024 + 128 * (dc + 1)], hb[:, c, :],
                         start=(c == 0), stop=(c == 7))
                k.act(gh[dc % 2][:], g1[:, :], AF.Sigmoid, bias=bg[:, dc:dc + 1])
                k.act(ga[dc % 2][:], g2[:, :], AF.Sigmoid, bias=bg[:, 8 + dc:9 + dc])
                uh = ps[6]
                ua = ps[7]
                for c in range(4):
                    k.mm(uh[:, :], Who[:, c, 128 * dc:128 * (dc + 1)], y_b[:, c, :], start=(c == 0), stop=(c == 3))
                for h in range(8):
                    k.mm(ua[:, :], Wao[:, h, 128 * dc:128 * (dc + 1)], ya[:, h, :], start=(h == 0), stop=(h == 7))
                k.tt('dve', m1[dc % 2][:], uh[:, :], gh[dc % 2][:], ALU.mult)
                k.tt('dve', m2[dc % 2][:], ua[:, :], ga[dc % 2][:], ALU.mult)
                k.tt('pool', m_g[:, dc, :], m1[dc % 2][:], m2[dc % 2][:], ALU.add)
            for j in range(4):
                i = tb * 4 + j
                x_t = xt[i % 3]
                k.dma(x_t[:], T['x'][128 * i:128 * (i + 1), :])
                o_t = ot[i % 2]
                for half in range(2):
                    fb = ps[half]
                    for c in range(8):
                        k.mm(fb[:, :], m_g[:, c, 128 * j:128 * (j + 1)], Wout[:, c, 512 * half:512 * (half + 1)],
                             start=(c == 0), stop=(c == 7))
                    k.tt('dve', o_t[:, 512 * half:512 * (half + 1)], fb[:, :], x_t[:, 512 * half:512 * (half + 1)], ALU.add)
                k.dma(T['out'][128 * i:128 * (i + 1), :], o_t[:])

def fft_constants():
    C = {}
    n = NF
    s2 = np.arange(128, dtype=np.float64)[:, None]
    f2 = np.arange(128, dtype=np.float64)[None, :]
    th = 2 * np.pi * (f2 + 0.5) * s2 / 256.0
    C['FA1'] = np.concatenate([np.cos(th), -np.sin(th)], 1)
    th2 = 2 * np.pi * (f2 + 0.5) * (s2 + 128) / 256.0
    C['FA2'] = -np.concatenate([np.cos(th2), -np.sin(th2)], 1)
    s1 = np.arange(32, dtype=np.float64)
    tw = np.exp(-2j * np.pi * (np.arange(128)[None, :] + 0.5) * s1[:, None] / n)
    twq = np.tile(tw, (4, 1))
    C['TWa'] = np.concatenate([twq.real, twq.real], 1)
    C['TWb'] = np.concatenate([-twq.imag, twq.imag], 1)
    W = np.exp(-2j * np.pi * np.outer(s1, s1) / 32.0)
    Wq = np.kron(np.eye(4), W)
    C['WBr'] = Wq.real
    C['WBi'] = Wq.imag
    C['WBni'] = -Wq.imag
    Wi = np.exp(2j * np.pi * np.outer(s1, s1) / 32.0)
    Wiq = np.kron(np.eye(4), Wi)
    C['WI1'] = np.concatenate([Wiq.real, Wiq.imag], 1)
    C['WI2'] = np.concatenate([-Wiq.imag, Wiq.real], 1)
    twi = np.exp(2j * np.pi * (np.arange(128)[:, None] + 0.5) * s1[None, :] / n)
    twiq = np.tile(twi, (1, 4))
    C['TIa'] = np.concatenate([twiq.real, twiq.real], 1)
    C['TIb'] = np.concatenate([-twiq.imag, twiq.imag], 1)
    t2 = np.arange(128, dtype=np.float64)[None, :]
    f2c = np.arange(128, dtype=np.float64)[:, None]
    th3 = 2 * np.pi * (f2c + 0.5) * t2 / 256.0
    C['FIr'] = (2.0 / n) * np.cos(th3)
    C['FIi'] = -(2.0 / n) * np.sin(th3)
    return C


def filter_constants():
    C = {}
    f32 = np.float32
    t = np.linspace(0.0, 1.0, L, dtype=f32)[:, None]
    bands = 16
    f = np.linspace(1e-4, bands - 1, bands, dtype=f32)
    ang = (f32(2.0 * np.pi / L) * np.arange(L, dtype=f32)[:, None] * f[None, :]).astype(f32)
    z = np.concatenate([t, np.cos(ang).astype(f32), -np.sin(ang).astype(f32)], axis=-1).astype(f32)
    zs = np.zeros((128, L), f32)
    zs[0:33, :] = z.T
    zs[64:97, :] = z[::-1].T
    hi = zs.astype(ml_dtypes.bfloat16)
    lo = (zs - hi.astype(f32)).astype(ml_dtypes.bfloat16)
    C['zs_hi'] = hi
    C['zs_lo'] = lo
    tl = t[:, 0]
    tf = np.zeros((128, 2, 32), f32)
    pidx = np.arange(128)[:, None] * 32 + np.arange(32)[None, :]
    tf[:, 0, :] = tl[pidx]
    tf[:, 1, :] = tl[4095 - pidx]
    C['tfull'] = tf.reshape(128, 64)
    MIN_DECAY = math.log(1e-2) / 1.5
    MAX_DECAY = math.log(1e-2) / 0.3
    deltas = np.abs(np.linspace(MIN_DECAY, MAX_DECAY, HYW, dtype=f32)).astype(f32)
    C['negd'] = (-deltas)[None, :].astype(f32)
    return C

def _sin_layer(ph, W, pre_ps, fr, fb, out32):
    k = ph.k
    a, kk = W['a'], W['kk']
    k.ts('dve', a[:], pre_ps, fr, fb, op0=ALU.mult, op1=ALU.add)
    k.ts('dve', kk[:], a[:], 1.0 / (2 * math.pi), MAGIC, op0=ALU.mult, op1=ALU.add)
    k.ts('dve', kk[:], kk[:], -MAGIC, None, op0=ALU.add)
    k.stt(a[:], kk[:], -2 * math.pi, a[:], ALU.mult, ALU.add)
    k.ts('dve', a[:], a[:], -3.14159, 3.14159, op0=ALU.max, op1=ALU.min)
    k.act(out32, a[:], AF.Sin)


def _hilo(ph, hi, lo, src32, tmp32):
    k = ph.k
    k.copy('dve', hi, src32)
    k.copy('pool', tmp32, hi)
    k.tt('pool', lo, src32, tmp32, ALU.subtract)


def phase2a(nc, T):
    with Phase(nc, "p2a") as ph:
        k = ph.k
        ps = ph.ps
        zs_hi = ph.sb("zs_hi", [128, L], BF)
        zs_lo = ph.sb("zs_lo", [128, L], BF)
        W1 = ph.sb("W1", [128, 128], F32)
        W2 = ph.sb("W2", [128, 128], F32)
        W1h = ph.sb("W1h", [128, 128], BF)
        W1l = ph.sb("W1l", [128, 128], BF)
        W2h = ph.sb("W2h", [128, 128], BF)
        W2l = ph.sb("W2l", [128, 128], BF)
        wt = ph.sb("wt", [128, 128], F32)
        mv = ph.sb("mv", [128, 4], F32)
        fb = ph.sb("fb", [128, 2], F32)
        Wk = dict(a=ph.sb("a", [128, 512], F32), kk=ph.sb("kk", [128, 512], F32))
        h1 = ph.sb("h1", [128, 512], F32)
        h1h = ph.sb("h1h", [128, 512], BF)
        h1l = ph.sb("h1l", [128, 512], BF)
        t32 = ph.sb("t32", [128, 512], F32)
        h2 = ph.sb("h2", [128, 512], F32)
        h2b = [ph.sb(f"h2b{i}", [128, 512], BF) for i in range(2)]
        k.dma(zs_hi[:], T['zs_hi'][:, :])
        k.dma(zs_lo[:], T['zs_lo'][:, :])
        k.dma(W1[:], T['W1blk'][:, :])
        k.dma(W2[:], T['W2blk'][:, :])
        k.dma(mv[:], T['mlpv'][:, :])
        _hilo(ph, W1h[:], W1l[:], W1[:], wt[:])
        _hilo(ph, W2h[:], W2l[:], W2[:], wt[:])
        k.tt('dve', fb[:, 0:1], mv[:, 0:1], mv[:, 1:2], ALU.mult)
        k.tt('dve', fb[:, 1:2], mv[:, 2:3], mv[:, 3:4], ALU.mult)
        for cch in range(DBG.get('n2a', NB)):
            sl = slice(512 * cch, 512 * (cch + 1))
            b1 = ps[cch % 2]
            k.mm(b1[:, :], W1h[:], zs_hi[:, sl], start=True, stop=False)
            k.mm(b1[:, :], W1h[:], zs_lo[:, sl], start=False, stop=False)
            k.mm(b1[:, :], W1l[:], zs_hi[:, sl], start=False, stop=True)
            if DBG.get('s2a', 9) < 1: continue
            _sin_layer(ph, Wk, b1[:, :], mv[:, 0:1], fb[:, 0:1], h1[:])
            if DBG.get('s2a', 9) < 2: continue
            _hilo(ph, h1h[:], h1l[:], h1[:], t32[:])
            if DBG.get('s2a', 9) < 3: continue
            b2 = ps[2 + cch % 2]
            k.mm(b2[:, :], W2h[:], h1h[:], start=True, stop=False)
            k.mm(b2[:, :], W2h[:], h1l[:], start=False, stop=False)
            k.mm(b2[:, :], W2l[:], h1h[:], start=False, stop=True)
            _sin_layer(ph, Wk, b2[:, :], mv[:, 2:3], fb[:, 1:2], h2[:])
            hb_ = h2b[cch % 2]
            k.copy('pool', hb_[:], h2[:])
            k.dma(T['h2_d'][:, sl], hb_[:])


def _cmul_tab(ph, W, src_ps, Ta, Tb, out_bf):
    k = ph.k
    P1, P2 = W
    sw = src_ps.rearrange("p (r f) -> p r f", r=2)[:, ::-1, :]
    s4 = DBG.get('s4', 9)
    if s4 < 2: return
    k.tt('dve', P1[:], src_ps, Ta, ALU.mult)
    if s4 < 3: return
    k.tt('dve', P2[:].rearrange("p (r f) -> p r f", r=2), sw, Tb.rearrange("p (r f) -> p r f", r=2), ALU.mult)
    if s4 < 4: return
    k.tt('pool', out_bf, P1[:], P2[:], ALU.add)


def phase2b(nc, T):
    with Phase(nc, "p2b") as ph:
        k = ph.k
        ps = ph.ps
        ident = ph.sb("ident", [128, 128], BF)
        make_ident(ph, ident)
        cb16 = {}
        for nm, w in (('FA1', 256), ('FA2', 256), ('WBr', 128), ('WBi', 128), ('WBni', 128), ('WI1', 256), ('WI2', 256),
                      ('FIr', 128), ('FIi', 128)):
            cb16[nm] = ph.sb(nm, [128, w], BF)
            k.dma(cb16[nm][:], T[nm][:, :])
        c32 = {}
        for nm in ('TWa', 'TWb', 'TIa', 'TIb'):
            c32[nm] = ph.sb(nm, [128, 256], F32)
            k.dma(c32[nm][:], T[nm][:, :])
        h2s = ph.sb("h2s", [128, L], BF)
        k.dma(h2s[:], T['h2_d'][:, :])
        h2p = ph.sb("h2p", [128, 32, 128], BF)
        k.copy('pool', h2p[:], h2s[:].rearrange("q (p s) -> q s p", s=32))
        W3 = ph.sb("W3", [128, 2048], BF)
        load_w(ph, W3[:], T['W3blk'][:, :])
        wsh = ph.sb("wsh", [128, 12, 4], F32)
        k.dma(wsh[:].rearrange("p a b -> p (a b)"), T['wsh'][:, :])
        biasT = ph.sb("biasT", [128, 2, 128], F32)
        k.dma(biasT[:].rearrange("p a b -> p (a b)"), T['biasT'][:, :])
        negd = ph.sb("negd", [128, HYW], F32)
        k.dma(negd[:], T['negd'][0:1, :].partition_broadcast(128))
        tfull = ph.sb("tfull", [128, 2, 32], F32)
        k.dma(tfull[:].rearrange("p a b -> p (a b)"), T['tfull'][:, :])
        w_in_v = T['w_in'].rearrange("(k p) n -> p k n", p=128)

        hbuf = [ph.sb(f"hbuf{i}", [128, 8, 512], BF) for i in range(2)]
        ar = ph.sb("arena", [128, 24592], BF)
        raw = [ar[:, 4098 * i:4098 * (i + 1)] for i in range(3)]
        ub_ = [ar[:, 12294 + 4096 * i:12294 + 4096 * (i + 1)] for i in range(2)]
        Wblk = ar[:, 20486:24582].rearrange("p (k w c) -> p k w c", k=8, w=4)
        k_tm = ar[:, 0:8192].rearrange("p (o d c s) -> p o d c s", o=2, d=2, c=64)
        AB = ar[:, 8192:12288].rearrange("p (d c s) -> p d c s", d=2, c=64)
        Ksp = ar[:, 12288:20480].rearrange("p (o g r f) -> p o g r f", o=2, g=16, r=2)
        Gbuf = ar[:, 20480:24576].rearrange("p (r c s) -> p r c s", r=2, c=64)
        sz = ph.sb("sz", [128, L], BF)
        tm = [ph.sb(f"tm{i}", [128, 128, 32], BF) for i in range(3)]
        z2_tm = ph.sb("z2_tm", [128, 128, 32], BF)
        y_sc = ph.sb("y_sc", [128, 32, 128], BF)
        yzb = ph.sb("yzb", [128, L], BF)
        arg32 = ph.sb("arg32", [128, 4096], F32)
        PW = [(ph.sb(f"P1_{i}", [128, 256], F32), ph.sb(f"P2_{i}", [128, 256], F32)) for i in range(2)]
        Zp = [ph.sb(f"Zp{i}", [128, 256], BF) for i in range(2)]
        Yb = [ph.sb(f"Yb{i}", [128, 256], BF) for i in range(2)]
        cols = (COL_V, COL_X1, COL_X2, COL_ZH)
        cnt = [0]

        def fwd_group(lhs1, lhs2):
            i = cnt[0]
            cnt[0] += 1
            za = ps[i % 2][:, 0:256]
            k.mm(za, lhs1, cb16['FA1'][:], start=True, stop=(lhs2 is None))
            if lhs2 is not None:
                k.mm(za, lhs2, cb16['FA2'][:], start=False, stop=True)
            z_p = Zp[i % 2]
            _cmul_tab(ph, PW[i % 2], za, c32['TWa'][:], c32['TWb'][:], z_p[:])
            ub = ps[2 + i % 2][:, 0:256]
            if DBG.get('s4', 9) < 5: return ub, i
            k.mm(ub[:, 0:128], cb16['WBr'][:], z_p[:, 0:128], start=True, stop=False)
            k.mm(ub[:, 0:128], cb16['WBni'][:], z_p[:, 128:256], start=False, stop=True)
            k.mm(ub[:, 128:256], cb16['WBi'][:], z_p[:, 0:128], start=True, stop=False)
            k.mm(ub[:, 128:256], cb16['WBr'][:], z_p[:, 128:256], start=False, stop=True)
            return ub, i

        for cb in range(DBG.get('ncb', 4)):
            for w in range(4):
                load_w(ph, Wblk[:, :, w, :], w_in_v[:, :, cols[w] + 128 * cb:cols[w] + 128 * (cb + 1)])
            for w in range(3):
                k.memset('pool', raw[w][:, 0:1], 0.0)
                k.memset('pool', raw[w][:, 4097:4098], 0.0)
            for tb in range(NB):
                hb = hbuf[tb % 2]
                k.dma(hb[:].rearrange("p c t -> p (c t)"), T['hT_d'][tb])
                for w in range(4):
                    bank = ps[(tb * 4 + w) % 2]
                    for c in range(8):
                        k.mm(bank[:, :], Wblk[:, c, w, :], hb[:, c, :], start=(c == 0), stop=(c == 7))
                    if w < 3:
                        k.act(raw[w][:, 1 + 512 * tb:1 + 512 * (tb + 1)], bank[:, :], AF.Copy)
                    else:
                        k.act(sz[:, 512 * tb:512 * (tb + 1)], bank[:, :], AF.Silu)
            if DBG.get('s2b', 9) < 2: continue
            for w in range(3):
                u = ub_[w % 2]
                j = 4 * w + cb
                k.ts('dve', u, raw[w][:, 1:4097], wsh[:, j, 1:2], wsh[:, j, 3:4], op0=ALU.mult, op1=ALU.add)
                k.stt(u, raw[w][:, 0:4096], wsh[:, j, 0:1], u, ALU.mult, ALU.add)
                k.stt(u, raw[w][:, 2:4098], wsh[:, j, 2:3], u, ALU.mult, ALU.add)
                for a in range(4):
                    pv = ps[2 + a % 2][:, :].bitcast(BF)
                    for e in range(8):
                        s1 = 8 * a + e
                        k.tr(pv[:, 128 * e:128 * (e + 1)], u[:, s1:4096:32], ident[:])
                    k.copy('dve', tm[w][:, :, 8 * a:8 * a + 8].rearrange("p c s -> p s c"),
                           pv.rearrange("p (s c) -> p s c", s=8))
            if DBG.get('dump_tm'):
                k.dma(T['dbg_tm'][:, :], tm[DBG['dump_tm'] - 1][:].rearrange("p c s -> p (c s)"))
            if DBG.get('s2b', 9) < 3: continue
            for hbk in range(DBG.get('nhbk', 2)):
                c0 = 64 * hbk
                gcol = 128 * cb + c0
                k.tt('dve', arg32[:].rearrange("p (d c s) -> p d c s", d=2, c=64),
                     bcast(bcast(negd[:, gcol:gcol + 64], 1, 2), 3, 32),
                     bcast(tfull[:, :, :], 2, 64), ALU.mult)
                if DBG.get('s3', 9) < 2: continue
                k.act(AB.rearrange("p d c s -> p (d c s)"), arg32[:], AF.Exp)
                if DBG.get('s3', 9) < 3: continue
                wc0 = 256 * (2 * cb + hbk)
                for s1 in range(32):
                    kb_ = ps[6 + s1 % 2][:, 0:256]
                    k.mm(kb_, h2p[:, s1, :], W3[:, wc0:wc0 + 256])
                    if DBG.get('s3', 9) < 4: continue
                    abv = AB[:, :, :, s1].rearrange("p d c -> p (d c)")
                    for o in range(2):
                        o_ap = k_tm[:, o, :, :, s1].rearrange("p d c -> p (d c)")
                        k.tt('dve', o_ap, kb_[:, 128 * o:128 * (o + 1)], abv, ALU.mult)
                if DBG.get('s2b', 9) < 4: continue
                for o in range(2):
                    for g in range(16):
                        l1 = k_tm[:, o, 0, 4 * g:4 * g + 4, :].rearrange("p c s -> p (c s)")
                        l2 = k_tm[:, o, 1, 4 * g:4 * g + 4, :].rearrange("p c s -> p (c s)")
                        ub, i = fwd_group(l1, l2)
                        gg = (gcol // 4) + g
                        if DBG.get('s4', 9) < 6: continue
                        if DBG.get('kev', 0) == 0:
                            k.ts('dve', Ksp[:, o, g, 0, :], ub[:, 0:128], biasT[:, o, gg:gg + 1], None, op0=ALU.add)
                            k.copy('dve', Ksp[:, o, g, 1, :], ub[:, 128:256])
                        else:
                            k.act(Ksp[:, o, g, 0, :], ub[:, 0:128], AF.Copy)
                            k.act(Ksp[:, o, g, 1, :], ub[:, 128:256], AF.Copy)
                if DBG.get('dump_k'):
                    k.dma(T['dbg_k'][:, :], ar[:, 0:8192])
                    k.dma(T['dbg_ks'][:, :], ar[:, 12288:20480])
                if DBG.get('s2b', 9) < 5: continue
                for o in range(DBG.get('nord', 2)):
                    src = tm[0] if o == 0 else z2_tm
                    gate = tm[1] if o == 0 else tm[2]
                    for g in range(16):
                        l1 = src[:, c0 + 4 * g:c0 + 4 * g + 4, :].rearrange("p c s -> p (c s)")
                        ub, i = fwd_group(l1, None)
                        P1, P2 = PW[i % 2]
                        usw = ub.rearrange("p (r f) -> p r f", r=2)[:, ::-1, :]
                        k.tt('dve', P1[:].rearrange("p (r f) -> p r f", r=2), ub.rearrange("p (r f) -> p r f", r=2),
                             bcast(Ksp[:, o, g, 0, :], 1, 2), ALU.mult)
                        k.tt('dve', P2[:].rearrange("p (r f) -> p r f", r=2), usw,
                             bcast(Ksp[:, o, g, 1, :], 1, 2), ALU.mult)
                        y_b = Yb[i % 2]
                        k.tt('pool', y_b[:, 0:128], P1[:, 0:128], P2[:, 0:128], ALU.subtract)
                        k.tt('pool', y_b[:, 128:256], P1[:, 128:256], P2[:, 128:256], ALU.add)
                        gb = ps[4 + i % 2][:, 0:256]
                        k.mm(gb, y_b[:, 0:128], cb16['WI1'][:], start=True, stop=False)
                        k.mm(gb, y_b[:, 128:256], cb16['WI2'][:], start=False, stop=True)
                        gsw = gb.rearrange("p (r f) -> p r f", r=2)[:, ::-1, :]
                        k.tt('dve', P1[:], gb, c32['TIa'][:], ALU.mult)
                        k.tt('dve', P2[:].rearrange("p (r f) -> p r f", r=2), gsw,
                             c32['TIb'][:].rearrange("p (r f) -> p r f", r=2), ALU.mult)
                        k.tt('pool', Gbuf[:, :, 4 * g:4 * g + 4, :].rearrange("p r c s -> p r (c s)"),
                             P1[:].rearrange("p (r f) -> p r f", r=2), P2[:].rearrange("p (r f) -> p r f", r=2), ALU.add)
                    for cc in range(4):
                        yb = ps[6 + cc % 2]
                        k.mm(yb[:, :], cb16['FIr'][:], Gbuf[:, 0, 16 * cc:16 * cc + 16, :].rearrange("p c s -> p (c s)"),
                             start=True, stop=False)
                        k.mm(yb[:, :], cb16['FIi'][:], Gbuf[:, 1, 16 * cc:16 * cc + 16, :].rearrange("p c s -> p (c s)"),
                             start=False, stop=True)
                        cs = slice(c0 + 16 * cc, c0 + 16 * cc + 16)
                        if o == 0:
                            k.tt('dve', z2_tm[:, cs, :], yb[:, :].rearrange("p (c s) -> p c s", c=16), gate[:, cs, :], ALU.mult)
                        else:
                            k.tt('dve', y_sc[:, :, cs].rearrange("p s c -> p c s"),
                                 yb[:, :].rearrange("p (c s) -> p c s", c=16), gate[:, cs, :], ALU.mult)
            if DBG.get('dump_z2'):
                k.dma(T['dbg_tm'][:, :], z2_tm[:].rearrange("p c s -> p (c s)"))
            if DBG.get('s2b', 9) < 6: continue
            for a in range(4):
                pv = ps[2 + a % 2][:, :].bitcast(BF)
                for e in range(8):
                    k.tr(pv[:, 128 * e:128 * (e + 1)], y_sc[:, 8 * a + e, :], ident[:])
                k.tt('dve', yzb[:].rearrange("c (p s) -> c s p", s=32)[:, 8 * a:8 * a + 8, :],
                     pv.rearrange("c (s p) -> c s p", s=8),
                     sz[:].rearrange("c (p s) -> c s p", s=32)[:, 8 * a:8 * a + 8, :], ALU.mult)
            for tb in range(NB):
                k.dma(T['yz_d'][tb][:, 512 * cb:512 * (cb + 1)], yzb[:, 512 * tb:512 * (tb + 1)])


def phase2(nc, T):
    if 'a' in DBG.get('p2', 'ab'):
        phase2a(nc, T)
    if 'b' in DBG.get('p2', 'ab'):
        phase2b(nc, T)

def _bf(a):
    return np.asarray(a, np.float32).astype(ml_dtypes.bfloat16)


_CONST_CACHE = {}


def host_constants():
    if _CONST_CACHE:
        return _CONST_CACHE
    C = {}
    pos = np.arange(L, dtype=np.float32)
    inv_freq = (np.float32(10000.0) ** (-np.arange(0, 32, 2, dtype=np.float32) / np.float32(32))).astype(np.float32)
    ang = (pos[:, None] * inv_freq[None, :]).astype(np.float32)
    C['cosT'] = np.ascontiguousarray(np.cos(ang).astype(np.float32).reshape(NT, 128, 16).transpose(1, 0, 2).reshape(128, NT * 16))
    C['sinT'] = np.ascontiguousarray(np.sin(ang).astype(np.float32).reshape(NT, 128, 16).transpose(1, 0, 2).reshape(128, NT * 16))
    F = fft_constants()
    for nm in ('FA1', 'FA2', 'WBr', 'WBi', 'WBni', 'WI1', 'WI2', 'FIr', 'FIi'):
        C[nm] = np.ascontiguousarray(_bf(F[nm]))
    for nm in ('TWa', 'TWb', 'TIa', 'TIb'):
        C[nm] = np.ascontiguousarray(F[nm].astype(np.float32))
    C.update(filter_constants())
    _CONST_CACHE.update(C)
    return _CONST_CACHE


def prep_inputs(inp, b):
    f32 = np.float32
    m = {}
    m['x'] = np.ascontiguousarray(inp['x'][b], dtype=f32)
    m['w_in'] = np.ascontiguousarray(inp['w_in'][0], dtype=f32)
    m['gT'] = np.ascontiguousarray(inp['g_norm'][0].reshape(8, 128).T, dtype=f32)
    m['bgT'] = np.ascontiguousarray(inp['b_gate'][0].reshape(16, 128).T, dtype=f32)
    m['w_uq'] = np.ascontiguousarray(inp['w_uq'][0], dtype=f32)
    m['w_ukv'] = np.ascontiguousarray(inp['w_ukv'][0], dtype=f32)
    m['gcqT'] = np.ascontiguousarray(inp['g_cq'][0].reshape(3, 128).T, dtype=f32)
    m['gckvT'] = np.ascontiguousarray(inp['g_ckv'][0].reshape(2, 128).T, dtype=f32)
    m['gqk'] = np.ascontiguousarray(np.concatenate([inp['g_qn'][0], inp['g_kn'][0]])[None, :], dtype=f32)
    m['w_attn_out'] = np.ascontiguousarray(inp['w_attn_out'][0], dtype=f32)
    m['w_hy_out'] = np.ascontiguousarray(inp['w_hy_out'][0], dtype=f32)
    m['w_out'] = np.ascontiguousarray(inp['w_out'][0], dtype=f32)
    wsh = np.zeros((128, 12, 4), f32)
    wsh[:, :, 0:3] = inp['w_short'][0].reshape(3, 12, 128).transpose(2, 1, 0)
    wsh[:, :, 3] = inp['b_short'][0].reshape(12, 128).T
    m['wsh'] = wsh.reshape(128, 48)
    hb = inp['hy_bias'][0]
    bT = hb.reshape(2, 128, 4).transpose(2, 0, 1)
    m['biasT'] = np.ascontiguousarray(np.repeat(bT[:, None], 32, axis=1).reshape(128, 256), dtype=f32)
    W1 = np.zeros((128, 128), f32)
    W1[0:33, 0:64] = inp['w_f1'][0]
    W1[64:97, 64:128] = inp['w_f1'][0]
    m['W1blk'] = W1
    W2 = np.zeros((128, 128), f32)
    W2[0:64, 0:64] = inp['w_f2'][0]
    W2[64:128, 64:128] = inp['w_f2'][0]
    m['W2blk'] = W2
    mv = np.zeros((128, 4), f32)
    for jj, nm in enumerate(('freq_1', 'b_f1', 'freq_2', 'b_f2')):
        mv[0:64, jj] = inp[nm][0]
        mv[64:128, jj] = inp[nm][0]
    m['mlpv'] = mv
    w3 = inp['w_f3'][0].reshape(64, 2, 2, 8, 64)
    W3 = np.zeros((128, 8, 2, 2, 64), f32)
    for dd in range(2):
        W3[64 * dd:64 * (dd + 1), :, :, dd, :] = w3[:, :, dd, :, :].transpose(0, 2, 1, 3)
    m['W3blk'] = W3.reshape(128, 2048)
    C = host_constants()
    for nm in CONST_NAMES:
        m[nm] = C[nm]
    return m


IN_SHAPES = {
    'x': ([L, D], F32), 'w_in': ([D, 5280], F32), 'gT': ([128, 8], F32), 'bgT': ([128, 16], F32),
    'w_uq': ([384, 768], F32), 'w_ukv': ([256, 1024], F32), 'gcqT': ([128, 3], F32), 'gckvT': ([128, 2], F32),
    'gqk': ([1, 192], F32), 'w_attn_out': ([512, D], F32), 'w_hy_out': ([512, D], F32), 'w_out': ([D, D], F32),
    'cosT': ([128, NT * 16], F32), 'sinT': ([128, NT * 16], F32),
    'wsh': ([128, 48], F32), 'biasT': ([128, 256], F32), 'W1blk': ([128, 128], F32), 'W2blk': ([128, 128], F32),
    'mlpv': ([128, 4], F32), 'W3blk': ([128, 2048], F32),
    'FA1': ([128, 256], BF), 'FA2': ([128, 256], BF), 'WBr': ([128, 128], BF), 'WBi': ([128, 128], BF),
    'WBni': ([128, 128], BF), 'WI1': ([128, 256], BF), 'WI2': ([128, 256], BF), 'FIr': ([128, 128], BF),
    'FIi': ([128, 128], BF), 'TWa': ([128, 256], F32), 'TWb': ([128, 256], F32), 'TIa': ([128, 256], F32),
    'TIb': ([128, 256], F32), 'zs_hi': ([128, L], BF), 'zs_lo': ([128, L], BF), 'tfull': ([128, 64], F32),
    'negd': ([1, HYW], F32),
}
CONST_NAMES = ('cosT', 'sinT', 'FA1', 'FA2', 'WBr', 'WBi', 'WBni', 'WI1', 'WI2', 'FIr', 'FIi', 'TWa', 'TWb', 'TIa', 'TIb',
               'zs_hi', 'zs_lo', 'tfull', 'negd')


def build_nc(debug=None):
    debug = debug or set()
    nc = bass.Bass("TRN2", target_bir_lowering=False)
    T = {}
    for name, (shape, dt) in IN_SHAPES.items():
        T[name] = nc.dram_tensor(name, shape, dt, kind="ExternalInput").ap()
    T['out'] = nc.dram_tensor("out", [L, D], F32, kind="ExternalOutput").ap()
    skind = dict(kind="ExternalOutput") if 'dump' in debug else {}
    T['hT_d'] = nc.dram_tensor("hT_d", [NB, 128, 8 * 512], BF, **skind).ap()
    T['at_d'] = nc.dram_tensor("at_d", [NB, 64, 8 * 512], BF, **skind).ap()
    T['h2_d'] = nc.dram_tensor("h2_d", [128, L], BF, **skind).ap()
    if 'dump' in debug:
        T['dbg_tm'] = nc.dram_tensor("dbg_tm", [128, 4096], BF, kind="ExternalOutput").ap()
        T['dbg_k'] = nc.dram_tensor("dbg_k", [128, 8192], BF, kind="ExternalOutput").ap()
        T['dbg_ks'] = nc.dram_tensor("dbg_ks", [128, 8192], BF, kind="ExternalOutput").ap()
    if 'yz_in' in debug:
        T['yz_d'] = nc.dram_tensor("yz_d", [NB, 128, 4 * 512], BF, kind="ExternalInput").ap()
    else:
        T['yz_d'] = nc.dram_tensor("yz_d", [NB, 128, 4 * 512], BF, **skind).ap()
    phases = debug & {'p1', 'p2', 'p3', 'p4'} or {'p1', 'p2', 'p3', 'p4'}
    if 'p1' in phases:
        phase1(nc, T)
    if 'p2' in phases and 'yz_in' not in debug:
        phase2(nc, T)
    if 'p3' in phases:
        phase3(nc, T)
    if 'p4' in phases:
        phase4(nc, T)
    return nc


def kernel(**inputs):
    inp = {k_: np.asarray(v) for k_, v in inputs.items()}
    nc = build_nc()
    in_maps = [prep_inputs(inp, b) for b in range(8)]
    res = run_bass_kernel_spmd(nc, in_maps, core_ids=list(range(8)))
    out = np.stack([np.asarray(r['out'], dtype=np.float32) for r in res.results], axis=0)
    return out
```

```python
import concourse.bass as bass
import concourse.mybir as mybir

_ESZ = {}


def _esize(dt):
    s = _ESZ.get(dt)
    if s is None:
        n = str(dt)
        if '32' in n:
            s = 4
        elif '16' in n:
            s = 2
        elif '8' in n:
            s = 1
        else:
            s = 4
        _ESZ[dt] = s
    return s


def footprint(ap):
    t = ap.tensor
    name = t.name
    es = _esize(ap.dtype)
    apl = ap.ap
    off = int(ap.offset) * es
    space = str(type(t).__name__)
    if 'DRam' in space:
        lo = off
        hi = off
        for st, cnt in apl:
            if cnt > 1:
                d = (cnt - 1) * st * es
                if d > 0:
                    hi += d
                else:
                    lo += d
        return (name, 0, 1, lo, hi + es)
    pstep, pcnt = apl[0]
    pstep_b = pstep * es
    if pstep_b > 0:
        p0 = off // pstep_b
        f0 = off % pstep_b
    else:
        p0 = 0
        f0 = off
    lo = f0
    hi = f0
    for st, cnt in apl[1:]:
        if cnt > 1:
            d = (cnt - 1) * st * es
            if d > 0:
                hi += d
            else:
                lo += d
    return (name, p0, p0 + pcnt, lo, hi + es)


COMPUTE = ('pe', 'act', 'dve', 'pool')
QUEUES = ('pe', 'act', 'dve', 'pool', 'sp')
QIDX = {q: i for i, q in enumerate(QUEUES)}


class _Op:
    __slots__ = ('q', 'fn', 'dma', 'idx', 'gid', 'waits_c', 'waits_d', 'signal', 'snap', 'slot', 'slot_cnt', 'prev_slot')


class Prog:
    def __init__(self, nc, dma_slots=None):
        self.nc = nc
        self.streams = {q: [] for q in QUEUES}
        self.recs = {}
        self.known = {q: [-1] * len(QUEUES) for q in QUEUES}
        self.known_dma = {q: set() for q in QUEUES}
        self.ops = []
        self.dma_slots = dma_slots or {'sp': 8, 'pool': 4, 'act': 4}
        self.dma_count = {q: 0 for q in QUEUES}
        self.dma_ops = {q: [] for q in QUEUES}
        self.n_comp = {q: 0 for q in QUEUES}

    def add(self, q, fn, reads=(), writes=(), dma=False):
        op = _Op()
        op.q = q
        op.fn = fn
        op.dma = dma
        op.gid = len(self.ops)
        op.signal = dma
        op.waits_c = []
        op.waits_d = []
        op.slot = None
        op.prev_slot = None
        stream = self.streams[q]
        if not dma:
            op.idx = self.n_comp[q]
            self.n_comp[q] += 1
        else:
            op.idx = -1
        deps_c = {}
        deps_d = set()

        def scan(fp, is_write):
            name, p0, p1, f0, f1 = fp
            lst = self.recs.get(name)
            if not lst:
                return
            for r in lst:
                (rp0, rp1, rf0, rf1, rw, rop) = r
                if not (is_write or rw):
                    continue
                if rp1 <= p0 or p1 <= rp0 or rf1 <= f0 or f1 <= rf0:
                    continue
                if rop.dma:
                    deps_d.add(rop)
                else:
                    e = rop.q
                    if deps_c.get(e, -1) < rop.idx:
                        deps_c[e] = rop.idx

        rfps = [footprint(a) for a in reads]
        wfps = [footprint(a) for a in writes]
        for fp in rfps:
            scan(fp, False)
        for fp in wfps:
            scan(fp, True)
        known = self.known[q]
        kd = self.known_dma[q]
        for e, i in deps_c.items():
            ei = QIDX[e]
            if e == q and not dma:
                if q == 'pe' or i < op.idx - 1:
                    continue
            if i <= known[ei]:
                continue
            op.waits_c.append((e, i))
            src = self.comp_ops[e][i]
            src.signal = True
            known[ei] = i
            for k, v in enumerate(src.snap):
                if v > known[k]:
                    known[k] = v
        for d in sorted(deps_d, key=lambda o: o.gid):
            if d.gid in kd:
                continue
            op.waits_d.append(d)
            kd.add(d.gid)
            for k, v in enumerate(d.snap):
                if v > known[k]:
                    known[k] = v
        if dma:
            n = self.dma_count[q]
            R = self.dma_slots[q]
            op.slot = n % R
            op.slot_cnt = n // R + 1
            if n >= R:
                prev = self.dma_ops[q][n - R]
                op.prev_slot = prev
                kd.add(prev.gid)
            self.dma_count[q] = n + 1
            self.dma_ops[q].append(op)
        op.snap = tuple(known)
        if not dma:
            self.comp_ops[q].append(op)
        for fp, is_write in [(f, False) for f in rfps] + [(f, True) for f in wfps]:
            name, p0, p1, f0, f1 = fp
            lst = self.recs.setdefault(name, [])
            if is_write:
                lst[:] = [r for r in lst if not (r[0] >= p0 and r[1] <= p1 and r[2] >= f0 and r[3] <= f1)]
            else:
                if not dma:
                    lst[:] = [r for r in lst if not (r[4] is False and (not r[5].dma) and r[5].q == q
                                                     and r[0] == p0 and r[1] == p1 and r[2] == f0 and r[3] == f1)]
            lst.append((p0, p1, f0, f1, is_write, op))
        stream.append(op)
        self.ops.append(op)
        return op

    comp_ops = None

    def start(self):
        self.comp_ops = {q: [] for q in QUEUES}

    def pe(self, fn, reads, writes):
        return self.add('pe', fn, reads, writes)

    def act(self, fn, reads, writes):
        return self.add('act', fn, reads, writes)

    def dve(self, fn, reads, writes):
        return self.add('dve', fn, reads, writes)

    def pool(self, fn, reads, writes):
        return self.add('pool', fn, reads, writes)

    def dma(self, out, in_, q='sp', **kw):
        return self.add(q, lambda e: e.dma_start(out=out, in_=in_, **kw), [in_], [out], dma=True)

    def emit(self, block, sems_c, sems_d):
        cum = {}
        for e in QUEUES:
            c = 0
            arr = []
            for o in self.comp_ops[e]:
                if o.signal:
                    c += 1
                arr.append(c)
            cum[e] = arr
        self.cum = cum

        def gen(q):
            def body(eng):
                for o in self.streams[q]:
                    for (e, i) in o.waits_c:
                        eng.wait_ge(sems_c[e], cum[e][i])
                    for d in o.waits_d:
                        eng.wait_ge(sems_d[d.q][d.slot], 16 * d.slot_cnt)
                    if o.prev_slot is not None:
                        p = o.prev_slot
                        eng.wait_ge(sems_d[p.q][p.slot], 16 * p.slot_cnt)
                    ins = o.fn(eng)
                    if o.dma:
                        ins.then_inc(sems_d[q][o.slot], 16)
                    elif o.signal:
                        ins.then_inc(sems_c[q], 1)
                R = self.dma_slots.get(q, 0)
                n = self.dma_count[q]
                for o in self.dma_ops[q][max(0, n - R):]:
                    eng.wait_ge(sems_d[q][o.slot], 16 * o.slot_cnt)
            return body

        if self.streams['pe']:
            block.tensor(gen('pe'))
        if self.streams['act']:
            block.scalar(gen('act'))
        if self.streams['dve']:
            block.vector(gen('dve'))
        if self.streams['pool']:
            block.gpsimd(gen('pool'))
        if self.streams['sp']:
            block.sync(gen('sp'))

import math
from contextlib import ExitStack
import numpy as np
import ml_dtypes
from concourse.bass_utils import run_bass_kernel_spmd

F32 = mybir.dt.float32
BF = mybir.dt.bfloat16
AF = mybir.ActivationFunctionType
ALU = mybir.AluOpType
AX = mybir.AxisListType

L = 4096
D = 1024
NT = 32
NB = 8
EPS = 1e-6
NF = 8192
HYW = 512
COL_V, COL_X1, COL_X2, COL_ZH = 0, 512, 1024, 1536
COL_CQ, COL_CKV, COL_KR, COL_ZA = 2048, 2432, 2688, 2720
COL_GH, COL_GA = 3232, 4256
MAGIC = 12582912.0
DBG = {}


def bcast(ap, axis, n):
    a = ap.unsqueeze(axis)
    shp = list(a.shape)
    shp[axis] = n
    return a.to_broadcast(shp)


class K:
    def __init__(self, P):
        self.P = P

    def mm(self, out, lhsT, rhs, start=True, stop=True):
        self.P.pe(lambda e: e.matmul(out, lhsT=lhsT, rhs=rhs, start=start, stop=stop), [lhsT, rhs], [out])

    def tr(self, out, in_, ident):
        self.P.pe(lambda e: e.transpose(out=out, in_=in_, identity=ident), [in_, ident], [out])

    def act(self, out, in_, func, bias=None, scale=None, accum_out=None):
        kw = {}
        reads = [in_]
        writes = [out]
        if bias is not None:
            kw['bias'] = bias
            if not isinstance(bias, (int, float)):
                reads.append(bias)
        if scale is not None:
            kw['scale'] = scale
            if not isinstance(scale, (int, float)):
                reads.append(scale)
        if accum_out is not None:
            kw['accum_out'] = accum_out
            writes.append(accum_out)
        self.P.act(lambda e: e.activation(out=out, in_=in_, func=func, **kw), reads, writes)

    def tt(self, eng, out, in0, in1, op):
        self.P.add(eng, lambda e: e.tensor_tensor(out=out, in0=in0, in1=in1, op=op), [in0, in1], [out])

    def ts(self, eng, out, in0, s1, s2=None, op0=ALU.mult, op1=None):
        reads = [in0]
        if not isinstance(s1, (int, float)):
            reads.append(s1)
        if s2 is not None and not isinstance(s2, (int, float)):
            reads.append(s2)
        if op1 is None:
            self.P.add(eng, lambda e: e.tensor_scalar(out=out, in0=in0, scalar1=s1, scalar2=None, op0=op0), reads, [out])
        else:
            self.P.add(eng, lambda e: e.tensor_scalar(out=out, in0=in0, scalar1=s1, scalar2=s2, op0=op0, op1=op1), reads, [out])

    def stt(self, out, in0, scalar, in1, op0, op1):
        reads = [in0, in1]
        if not isinstance(scalar, (int, float)):
            reads.append(scalar)
        self.P.dve(lambda e: e.scalar_tensor_tensor(out=out, in0=in0, scalar=scalar, in1=in1, op0=op0, op1=op1), reads, [out])

    def copy(self, eng, out, in_):
        if eng == 'act':
            self.act(out, in_, AF.Copy)
        else:
            self.P.add(eng, lambda e: e.tensor_copy(out=out, in_=in_), [in_], [out])

    def recip(self, out, in_):
        self.P.dve(lambda e: e.reciprocal(out=out, in_=in_), [in_], [out])

    def reduce_add(self, out, in_):
        self.P.dve(lambda e: e.tensor_reduce(out=out, in_=in_, axis=AX.X, op=ALU.add), [in_], [out])

    def memset(self, eng, ap, val):
        self.P.add(eng, lambda e: e.memset(ap, val), [], [ap])

    def dma(self, out, in_, q='sp'):
        self.P.dma(out, in_, q=q)


def _dump(P):
    cum = {}
    for e in QUEUES:
        c = 0
        arr = []
        for o in P.comp_ops[e]:
            if o.signal:
                c += 1
            arr.append(c)
        cum[e] = arr
    for q in QUEUES:
        print("== stream", q)
        for o in P.streams[q]:
            w = [f"{e}>={cum[e][i]}(op{i})" for e, i in o.waits_c] + [f"dma[{d.q}{d.slot}]>={16*d.slot_cnt}" for d in o.waits_d]
            if o.prev_slot is not None:
                w.append(f"prev dma[{o.prev_slot.q}{o.prev_slot.slot}]>={16*o.prev_slot.slot_cnt}")
            tag = f"DMA slot{o.slot} cnt{o.slot_cnt}" if o.dma else (f"op{o.idx} sig={cum[q][o.idx] if o.signal else '-'}")
            print("   ", tag, getattr(o, 'desc', ''), "waits:", w)


class Phase:
    def __init__(self, nc, name):
        self.nc = nc
        self.name = name
        self.es = ExitStack()

    def __enter__(self):
        nc = self.nc
        es = self.es
        es.__enter__()
        self.ps = [es.enter_context(nc.psum_tensor(f"{self.name}_ps{i}", [128, 512], F32)) for i in range(8)]
        self.sems_c = {e: es.enter_context(nc.semaphore(f"{self.name}_sc_{e}")) for e in QUEUES}
        self.sems_d = {q: [es.enter_context(nc.semaphore(f"{self.name}_sd_{q}{i}")) for i in range(n)]
                       for q, n in (('sp', 8), ('pool', 4), ('act', 4))}
        self.P = Prog(nc)
        self.P.start()
        self.k = K(self.P)
        return self

    def sb(self, name, shape, dt):
        return self.es.enter_context(self.nc.sbuf_tensor(f"{self.name}_{name}", shape, dt))

    def __exit__(self, *a):
        if a[0] is None:
            self.es.enter_context(self.nc.allow_low_precision("bf16 operands / intermediates by design"))
            block = self.es.enter_context(self.nc.Block())
            if DBG.get('dump') == self.name:
                _dump(self.P)
            self.P.emit(block, self.sems_c, self.sems_d)
        return self.es.__exit__(*a)


def make_ident(ph, ident):
    identf = ph.sb("identf", [128, 128], F32)
    ph.k.memset('pool', identf[:], 0.0)
    ph.P.pool(lambda e: e.affine_select(out=identf[:], in_=identf[:], pattern=[[-1, 128]], compare_op=ALU.not_equal,
                                        fill=1.0, base=0, channel_multiplier=1), [identf[:]], [identf[:]])
    ph.k.copy('dve', ident[:], identf[:])


def load_w(ph, dst, src_ap):
    ph.k.dma(dst, src_ap, q='pool')


def rstd_from_ss(k, rs_col, ss_col, n):
    k.act(rs_col, ss_col, AF.Sqrt, bias=EPS, scale=1.0 / n)
    k.recip(rs_col, rs_col)


def phase1(nc, T):
    with Phase(nc, "p1") as ph:
        k = ph.k
        xt = [ph.sb(f"xt{i}", [128, D], F32) for i in range(3)]
        xn = [ph.sb(f"xn{i}", [128, D], BF) for i in range(2)]
        junk = ph.sb("junk", [128, D], BF)
        ss = ph.sb("ss", [128, NT], F32)
        rs = ph.sb("rs", [128, NT], F32)
        gT = ph.sb("gT", [128, 8], F32)
        ident = ph.sb("ident", [128, 128], BF)
        hb = [ph.sb(f"hb{i}", [128, 8, 512], BF) for i in range(2)]
        make_ident(ph, ident)
        k.dma(gT[:], T['gT'][:, :])
        for i in range(NT):
            x_t = xt[i % 3]
            k.dma(x_t[:], T['x'][128 * i:128 * (i + 1), :])
            k.act(junk[:], x_t[:], AF.Square, accum_out=ss[:, i:i + 1])
            rstd_from_ss(k, rs[:, i:i + 1], ss[:, i:i + 1], D)
            x_n = xn[i % 2]
            k.ts('dve', x_n[:], x_t[:], rs[:, i:i + 1])
            bank = ph.ps[i % 4]
            pv = bank[:, :].bitcast(BF)
            for c in range(8):
                k.tr(pv[:, 128 * c:128 * (c + 1)], x_n[:, 128 * c:128 * (c + 1)], ident[:])
            h_b = hb[(i // 4) % 2]
            j = i % 4
            k.tt('dve', h_b[:, :, 128 * j:128 * (j + 1)], pv.rearrange("p (c t) -> p c t", c=8),
                 bcast(gT[:, :], 2, 128), ALU.mult)
            if j == 3:
                k.dma(T['hT_d'][i // 4], h_b[:].rearrange("p c t -> p (c t)"))


def run_interleaved(gens, width=2):
    active = []
    gens = list(gens)
    while gens or active:
        while gens and len(active) < width:
            active.append(gens.pop(0))
        for g in list(active):
            try:
                next(g)
            except StopIteration:
                active.remove(g)


def rstd_pool(k, rs, ss, n, mhalf, tmp):
    k.ts('dve', tmp, ss, 1.0 / n, EPS, op0=ALU.mult, op1=ALU.add)
    k.tt('pool', rs, tmp, mhalf, ALU.pow)


def qk_norm_rope(ph, W, src, dst, g_rep, cos_t, sin_t, mhalf):
    k = ph.k
    sq, ssq, rk, ta, tb_, tmp = W['sq'], W['ssq'], W['rk'], W['ta'], W['tb'], W['tmp']
    k.tt('pool', sq[:], src[:], src[:], ALU.mult)
    k.reduce_add(ssq[:], sq[:])
    rstd_pool(k, rk[:], ssq[:], 96, mhalf[:, 0:8], tmp[:])
    yield
    k.tt('dve', src[:], src[:], bcast(rk[:, :], 2, 96), ALU.mult)
    k.tt('pool', src[:], src[:], bcast(g_rep, 1, 8), ALU.mult)
    yield
    t1 = src[:, :, 64:80]
    t2 = src[:, :, 80:96]
    cb = bcast(cos_t, 1, 8)
    sbb = bcast(sin_t, 1, 8)
    k.tt('pool', ta[:, 0], t1, cb, ALU.mult)
    k.tt('pool', tb_[:, 0], t2, sbb, ALU.mult)
    k.tt('pool', ta[:, 1], t1, sbb, ALU.mult)
    k.tt('pool', tb_[:, 1], t2, cb, ALU.mult)
    k.tt('dve', dst[:, :, 64:80], ta[:, 0], tb_[:, 0], ALU.subtract)
    k.tt('dve', dst[:, :, 80:96], ta[:, 1], tb_[:, 1], ALU.add)
    k.copy('pool', dst[:, :, 0:64], src[:, :, 0:64])
    yield


def phase3(nc, T):
    with Phase(nc, "p3") as ph:
        k = ph.k
        ps = ph.ps
        ident = ph.sb("ident", [128, 128], BF)
        make_ident(ph, ident)
        w_in_v = T['w_in'].rearrange("(k p) n -> p k n", p=128)
        Wkv = ph.sb("Wkv", [128, 8, 288], BF)
        Wq = ph.sb("Wq", [128, 8, 384], BF)
        Wuq = ph.sb("Wuq", [128, 3, 768], BF)
        Wukv = ph.sb("Wukv", [128, 2, 1024], BF)
        gcq = ph.sb("gcq", [128, 3], F32)
        gckv = ph.sb("gckv", [128, 2], F32)
        gqk = ph.sb("gqk", [128, 192], F32)
        cosT = ph.sb("cosT", [128, NT, 16], F32)
        sinT = ph.sb("sinT", [128, NT, 16], F32)
        mhalf = ph.sb("mhalf", [128, 8], F32)
        k.memset('pool', mhalf[:], -0.5)
        load_w(ph, Wkv[:], w_in_v[:, :, COL_CKV:COL_CKV + 288])
        load_w(ph, Wq[:], w_in_v[:, :, COL_CQ:COL_CQ + 384])
        load_w(ph, Wuq[:], T['w_uq'].rearrange("(k p) n -> p k n", p=128))
        load_w(ph, Wukv[:], T['w_ukv'].rearrange("(k p) n -> p k n", p=128))
        k.dma(gcq[:], T['gcqT'][:, :])
        k.dma(gckv[:], T['gckvT'][:, :])
        k.dma(gqk[:], T['gqk'][0:1, :].partition_broadcast(128))
        k.dma(cosT[:].rearrange("p a b -> p (a b)"), T['cosT'][:, :])
        k.dma(sinT[:].rearrange("p a b -> p (a b)"), T['sinT'][:, :])
        k.tt('pool', Wuq[:], Wuq[:], bcast(gcq[:, :], 2, 768), ALU.mult)
        k.tt('pool', Wukv[:], Wukv[:], bcast(gckv[:, :], 2, 1024), ALU.mult)

        kT = ph.sb("kT", [128, 8, L], BF)
        vx = ph.sb("vx", [128, NT, 8, 65], BF)
        k.memset('pool', vx[:, :, :, 64:65], 1.0)
        ones = ph.sb("ones", [128, 64], BF)
        k.memset('pool', ones[:], 1.0)
        hbuf = [ph.sb(f"hbuf{i}", [128, 8, 512], BF) for i in range(2)]
        junk = [ph.sb(f"junk{i}", [128, 384], BF) for i in range(2)]
        ssl = ph.sb("ssl", [128, 2 * NT], F32)
        rsl = ph.sb("rsl", [128, 2 * NT], F32)
        tsl = ph.sb("tsl", [128, 2 * NT], F32)
        latn = [ph.sb(f"latn{i}", [128, 384], BF) for i in range(2)]
        latT = [ph.sb(f"latT{i}", [128, 3, 128], BF) for i in range(2)]
        kr = [ph.sb(f"kr{i}", [128, 32], F32) for i in range(2)]
        qk32 = [ph.sb(f"qk32_{i}", [128, 8, 96], F32) for i in range(2)]
        qkbf = [ph.sb(f"qkbf{i}", [128, 8, 96], BF) for i in range(2)]
        Wk_ = [dict(sq=ph.sb(f"sq{i}", [128, 8, 96], F32), ssq=ph.sb(f"ssq{i}", [128, 8], F32),
                    rk=ph.sb(f"rk{i}", [128, 8], F32), tmp=ph.sb(f"tmpn{i}", [128, 8], F32),
                    ta=ph.sb(f"ta{i}", [128, 2, 8, 16], F32), tb=ph.sb(f"tb{i}", [128, 2, 8, 16], F32)) for i in range(2)]
        qT = [ph.sb(f"qT{i}", [128, 8, 512], BF) for i in range(2)]
        pt = [ph.sb(f"pt{i}", [128, 512], BF) for i in range(4)]
        rsum = [ph.sb(f"rsum{i}", [128, 512], BF) for i in range(2)]
        bcs = [ph.sb(f"bcs{i}", [64, 512], F32) for i in range(2)]
        at = [ph.sb(f"at{i}", [64, 8, 512], BF) for i in range(2)]

        def sumsq(i, col, src_ps, n):
            k.act(junk[i % 2][:, 0:n], src_ps, AF.Square, accum_out=ssl[:, col:col + 1])
            rstd_pool(k, rsl[:, col:col + 1], ssl[:, col:col + 1], n, mhalf[:, 0:1], tsl[:, col:col + 1])

        def kv_tile(i, hb, j):
            par = i % 2
            if j == 0:
                k.dma(hb[:].rearrange("p c t -> p (c t)"), T['hT_d'][i // 4])
            lat = ps[par][:, 0:288]
            for c in range(8):
                k.mm(lat, hb[:, c, 128 * j:128 * (j + 1)], Wkv[:, c, :], start=(c == 0), stop=(c == 7))
            sumsq(i, i, lat[:, 0:256], 256)
            yield
            ln = latn[par]
            k.ts('dve', ln[:, 0:256], lat[:, 0:256], rsl[:, i:i + 1])
            k.copy('dve', kr[par][:], lat[:, 256:288])
            tbank = ps[2][:, :].bitcast(BF)
            for c in range(2):
                k.tr(tbank[:, 128 * c:128 * (c + 1)], ln[:, 128 * c:128 * (c + 1)], ident[:])
            lT = latT[par]
            k.copy('dve', lT[:, 0:2, :].rearrange("p c t -> p (c t)"), tbank[:, 0:256])
            yield
            kvb = [ps[3 + 2 * par], ps[4 + 2 * par]]
            for half in range(2):
                for c in range(2):
                    k.mm(kvb[half][:, :], lT[:, c, :], Wukv[:, c, 512 * half:512 * (half + 1)],
                         start=(c == 0), stop=(c == 1))
            kk = qk32[par]
            for half in range(2):
                kvv = kvb[half][:, :].rearrange("p (h e) -> p h e", h=4)
                k.copy('dve', kk[:, 4 * half:4 * half + 4, 0:64], kvv[:, :, 0:64])
                k.copy('dve', vx[:, i, 4 * half:4 * half + 4, 0:64], kvv[:, :, 64:128])
            k.copy('pool', kk[:, :, 64:96], bcast(kr[par][:, :], 1, 8))
            yield
            kf = qkbf[par]
            yield from qk_norm_rope(ph, Wk_[par], kk, kf, gqk[:, 96:192], cosT[:, i, :], sinT[:, i, :], mhalf)
            kbank = ps[7][:, :].bitcast(BF)
            for h in range(8):
                k.tr(kbank[0:96, 128 * h:128 * (h + 1)], kf[:, h, :], ident[:])
            k.copy('dve', kT[0:96, :, 128 * i:128 * (i + 1)], kbank[0:96, :].rearrange("p (h t) -> p h t", h=8))
            yield

        def q_tile(i, hb, j, q_T):
            par = i % 2
            lat = ps[6][:, 0:384]
            for c in range(8):
                k.mm(lat, hb[:, c, 128 * j:128 * (j + 1)], Wq[:, c, :], start=(c == 0), stop=(c == 7))
            lsb = Wk_[par]['sq'][:].rearrange("p a b -> p (a b)")[:, 0:384]
            k.copy('dve', lsb, lat)
            k.P.dve(lambda e: e.scalar_tensor_tensor(out=junk[par][:, 0:384], in0=lsb, scalar=1.0, in1=lsb, op0=ALU.mult,
                                                     op1=ALU.mult, accum_out=ssl[:, NT + i:NT + i + 1]),
                    [lsb], [junk[par][:, 0:384], ssl[:, NT + i:NT + i + 1]])
            rstd_pool(k, rsl[:, NT + i:NT + i + 1], ssl[:, NT + i:NT + i + 1], 384, mhalf[:, 0:1], tsl[:, NT + i:NT + i + 1])
            yield
            ln = latn[par]
            k.ts('dve', ln[:, 0:384], lsb, rsl[:, NT + i:NT + i + 1])
            tbank = ps[7][:, :].bitcast(BF)
            for c in range(3):
                k.tr(tbank[:, 128 * c:128 * (c + 1)], ln[:, 128 * c:128 * (c + 1)], ident[:])
            lT = latT[par]
            k.copy('dve', lT[:].rearrange("p c t -> p (c t)"), tbank[:, 0:384])
            yield
            qq = qk32[par]
            for half in range(2):
                qb = ps[6][:, 0:384]
                for c in range(3):
                    k.mm(qb, lT[:, c, :], Wuq[:, c, 384 * half:384 * (half + 1)], start=(c == 0), stop=(c == 2))
                k.copy('dve', qq[:, 4 * half:4 * half + 4, :], qb.rearrange("p (h e) -> p h e", h=4))
                yield
            qf = qkbf[par]
            yield from qk_norm_rope(ph, Wk_[par], qq, qf, gqk[:, 0:96], cosT[:, i, :], sinT[:, i, :], mhalf)
            qbank = ps[7][:, :].bitcast(BF)
            for h in range(8):
                k.tr(qbank[0:96, 128 * h:128 * (h + 1)], qf[:, h, :], ident[:])
            k.copy('dve', q_T[0:96, :, 128 * j:128 * (j + 1)], qbank[0:96, :].rearrange("p (h t) -> p h t", h=8))
            yield

        def kv_block(tb):
            hb = hbuf[tb % 2]
            return [kv_tile(tb * 4 + j, hb, j) for j in range(4)]

        def q_chunk_gens(qc):
            hb = hbuf[qc % 2]
            k.dma(hb[:].rearrange("p c t -> p (c t)"), T['hT_d'][qc])
            return [q_tile(qc * 4 + j, hb, j, qT[qc % 2]) for j in range(4)]

        gens = []
        for tb in range(DBG.get('nprep', NB)):
            gens += kv_block(tb)
        run_interleaved(gens, 2)
        nqc = DBG.get('nqc', NB)
        if nqc:
            run_interleaved(q_chunk_gens(0), 1)

        scale = 1.0 / math.sqrt(96.0)
        NH = DBG.get('nh', 8)
        for qc in range(nqc):
            q_T = qT[qc % 2]
            a_t = at[qc % 2]
            nxt = q_chunk_gens(qc + 1) if qc + 1 < nqc else []
            nxt_active = []
            steps = [(h, kt) for h in range(NH) for kt in range(NT)]

            def S(idx):
                h, kt = steps[idx]
                k.mm(ps[idx % 3][:, :], kT[0:96, h, 128 * kt:128 * (kt + 1)], q_T[0:96, h, :])

            def finish_head(h):
                ob = ps[3 + (h % 2)]
                r_s = rsum[h % 2]
                b_s = bcs[h % 2]
                k.recip(r_s[64:65, :], ob[64:65, :])
                k.mm(ps[5][0:64, :], ones[64:65, :], r_s[64:65, :])
                k.copy('dve', b_s[:], ps[5][0:64, :])
                k.tt('dve', a_t[:, h, :], ob[0:64, :], b_s[:], ALU.mult)

            S(0)
            S(1)
            pending = None
            for idx, (h, kt) in enumerate(steps):
                p_t = pt[idx % 4]
                k.act(p_t[:], ps[idx % 3][:, :], AF.Exp, scale=scale)
                if idx + 2 < len(steps):
                    S(idx + 2)
                ob = ps[3 + (h % 2)]
                k.mm(ob[0:65, :], vx[:, kt, h, :], p_t[:], start=(kt == 0), stop=(kt == NT - 1))
                if pending is not None and kt == 2:
                    finish_head(pending)
                    pending = None
                if kt == NT - 1:
                    pending = h
                if idx % 4 == 3:
                    while nxt and len(nxt_active) < 1:
                        nxt_active.append(nxt.pop(0))
                    for g in list(nxt_active):
                        try:
                            next(g)
                        except StopIteration:
                            nxt_active.remove(g)
            if pending is not None:
                finish_head(pending)
            run_interleaved(nxt_active + nxt, 1)
            k.dma(T['at_d'][qc], a_t[:].rearrange("p h t -> p (h t)"))


def phase4(nc, T):
    with Phase(nc, "p4") as ph:
        k = ph.k
        ps = ph.ps
        w_in_v = T['w_in'].rearrange("(k p) n -> p k n", p=128)
        Wz = ph.sb("Wz", [128, 8, 512], BF)
        Wg = ph.sb("Wg", [128, 8, 2048], BF)
        Wao = ph.sb("Wao", [64, 8, D], BF)
        Who = ph.sb("Who", [128, 4, D], BF)
        Wout = ph.sb("Wout", [128, 8, D], BF)
        bg = ph.sb("bg", [128, 16], F32)
        load_w(ph, Wz[:], w_in_v[:, :, COL_ZA:COL_ZA + 512])
        for q4 in range(4):
            load_w(ph, Wg[:, :, 512 * q4:512 * (q4 + 1)], w_in_v[:, :, COL_GH + 512 * q4:COL_GH + 512 * (q4 + 1)])
        load_w(ph, Wao[:], T['w_attn_out'].rearrange("(h p) n -> p h n", p=64))
        load_w(ph, Who[:], T['w_hy_out'].rearrange("(k p) n -> p k n", p=128))
        load_w(ph, Wout[:], T['w_out'].rearrange("(k p) n -> p k n", p=128))
        k.dma(bg[:], T['bgT'][:, :])
        hbuf = [ph.sb(f"hbuf{i}", [128, 8, 512], BF) for i in range(2)]
        atb = [ph.sb(f"atb{i}", [64, 8, 512], BF) for i in range(2)]
        yzb = [ph.sb(f"yzb{i}", [128, 4, 512], BF) for i in range(2)]
        xt = [ph.sb(f"xt{i}", [128, D], F32) for i in range(3)]
        ot = [ph.sb(f"ot{i}", [128, D], F32) for i in range(2)]
        sz = [ph.sb(f"sz{i}", [64, 512], BF) for i in range(2)]
        ya = ph.sb("ya", [64, 8, 512], BF)
        gh = [ph.sb(f"gh{i}", [128, 512], BF) for i in range(2)]
        ga = [ph.sb(f"ga{i}", [128, 512], BF) for i in range(2)]
        m1 = [ph.sb(f"m1{i}", [128, 512], F32) for i in range(2)]
        m2 = [ph.sb(f"m2{i}", [128, 512], F32) for i in range(2)]
        mg = [ph.sb(f"mg{i}", [128, 8, 512], BF) for i in range(2)]
        for tb in range(NB):
            hb = hbuf[tb % 2]
            a_b = atb[tb % 2]
            y_b = yzb[tb % 2]
            k.dma(hb[:].rearrange("p c t -> p (c t)"), T['hT_d'][tb])
            k.dma(a_b[:].rearrange("p h t -> p (h t)"), T['at_d'][tb])
            k.dma(y_b[:].rearrange("p c t -> p (c t)"), T['yz_d'][tb])
            for h in range(8):
                zb = ps[h % 2][0:64, :]
                for c in range(8):
                    k.mm(zb, Wz[:, c, 64 * h:64 * (h + 1)], hb[:, c, :], start=(c == 0), stop=(c == 7))
                s_z = sz[h % 2]
                k.act(s_z[:], zb, AF.Silu)
                k.tt('pool', ya[:, h, :], a_b[:, h, :], s_z[:], ALU.mult)
            m_g = mg[tb % 2]
            for dc in range(8):
                g1 = ps[2 + (dc % 2)]
                g2 = ps[4 + (dc % 2)]
                for c in range(8):
                    k.mm(g1[:, :], Wg[:, c, 128 * dc:128 * (dc + 1)], hb[:, c, :], start=(c == 0), stop=(c == 7))
                for c in range(8):
                    k.mm(g2[:, :], Wg[:, c, 1024 + 128 * dc:1024 + 128 * (dc + 1)], hb[:, c, :],
                         start=(c == 0), stop=(c == 7))
                k.act(gh[dc % 2][:], g1[:, :], AF.Sigmoid, bias=bg[:, dc:dc + 1])
                k.act(ga[dc % 2][:], g2[:, :], AF.Sigmoid, bias=bg[:, 8 + dc:9 + dc])
                uh = ps[6]
                ua = ps[7]
                for c in range(4):
                    k.mm(uh[:, :], Who[:, c, 128 * dc:128 * (dc + 1)], y_b[:, c, :], start=(c == 0), stop=(c == 3))
                for h in range(8):
                    k.mm(ua[:, :], Wao[:, h, 128 * dc:128 * (dc + 1)], ya[:, h, :], start=(h == 0), stop=(h == 7))
                k.tt('dve', m1[dc % 2][:], uh[:, :], gh[dc % 2][:], ALU.mult)
                k.tt('dve', m2[dc % 2][:], ua[:, :], ga[dc % 2][:], ALU.mult)
                k.tt('pool', m_g[:, dc, :], m1[dc % 2][:], m2[dc % 2][:], ALU.add)
            for j in range(4):
                i = tb * 4 + j
                x_t = xt[i % 3]
                k.dma(x_t[:], T['x'][128 * i:128 * (i + 1), :])
                o_t = ot[i % 2]
                for half in range(2):
                    fb = ps[half]
                    for c in range(8):
                        k.mm(fb[:, :], m_g[:, c, 128 * j:128 * (j + 1)], Wout[:, c, 512 * half:512 * (half + 1)],
                             start=(c == 0), stop=(c == 7))
                    k.tt('dve', o_t[:, 512 * half:512 * (half + 1)], fb[:, :], x_t[:, 512 * half:512 * (half + 1)], ALU.add)
                k.dma(T['out'][128 * i:128 * (i + 1), :], o_t[:])

def fft_constants():
    C = {}
    n = NF
    s2 = np.arange(128, dtype=np.float64)[:, None]
    f2 = np.arange(128, dtype=np.float64)[None, :]
    th = 2 * np.pi * (f2 + 0.5) * s2 / 256.0
    C['FA1'] = np.concatenate([np.cos(th), -np.sin(th)], 1)
    th2 = 2 * np.pi * (f2 + 0.5) * (s2 + 128) / 256.0
    C['FA2'] = -np.concatenate([np.cos(th2), -np.sin(th2)], 1)
    s1 = np.arange(32, dtype=np.float64)
    tw = np.exp(-2j * np.pi * (np.arange(128)[None, :] + 0.5) * s1[:, None] / n)
    twq = np.tile(tw, (4, 1))
    C['TWa'] = np.concatenate([twq.real, twq.real], 1)
    C['TWb'] = np.concatenate([-twq.imag, twq.imag], 1)
    W = np.exp(-2j * np.pi * np.outer(s1, s1) / 32.0)
    Wq = np.kron(np.eye(4), W)
    C['WBr'] = Wq.real
    C['WBi'] = Wq.imag
    C['WBni'] = -Wq.imag
    Wi = np.exp(2j * np.pi * np.outer(s1, s1) / 32.0)
    Wiq = np.kron(np.eye(4), Wi)
    C['WI1'] = np.concatenate([Wiq.real, Wiq.imag], 1)
    C['WI2'] = np.concatenate([-Wiq.imag, Wiq.real], 1)
    twi = np.exp(2j * np.pi * (np.arange(128)[:, None] + 0.5) * s1[None, :] / n)
    twiq = np.tile(twi, (1, 4))
    C['TIa'] = np.concatenate([twiq.real, twiq.real], 1)
    C['TIb'] = np.concatenate([-twiq.imag, twiq.imag], 1)
    t2 = np.arange(128, dtype=np.float64)[None, :]
    f2c = np.arange(128, dtype=np.float64)[:, None]
    th3 = 2 * np.pi * (f2c + 0.5) * t2 / 256.0
    C['FIr'] = (2.0 / n) * np.cos(th3)
    C['FIi'] = -(2.0 / n) * np.sin(th3)
    return C


def filter_constants():
    C = {}
    f32 = np.float32
    t = np.linspace(0.0, 1.0, L, dtype=f32)[:, None]
    bands = 16
    f = np.linspace(1e-4, bands - 1, bands, dtype=f32)
    ang = (f32(2.0 * np.pi / L) * np.arange(L, dtype=f32)[:, None] * f[None, :]).astype(f32)
    z = np.concatenate([t, np.cos(ang).astype(f32), -np.sin(ang).astype(f32)], axis=-1).astype(f32)
    zs = np.zeros((128, L), f32)
    zs[0:33, :] = z.T
    zs[64:97, :] = z[::-1].T
    hi = zs.astype(ml_dtypes.bfloat16)
    lo = (zs - hi.astype(f32)).astype(ml_dtypes.bfloat16)
    C['zs_hi'] = hi
    C['zs_lo'] = lo
    tl = t[:, 0]
    tf = np.zeros((128, 2, 32), f32)
    pidx = np.arange(128)[:, None] * 32 + np.arange(32)[None, :]
    tf[:, 0, :] = tl[pidx]
    tf[:, 1, :] = tl[4095 - pidx]
    C['tfull'] = tf.reshape(128, 64)
    MIN_DECAY = math.log(1e-2) / 1.5
    MAX_DECAY = math.log(1e-2) / 0.3
    deltas = np.abs(np.linspace(MIN_DECAY, MAX_DECAY, HYW, dtype=f32)).astype(f32)
    C['negd'] = (-deltas)[None, :].astype(f32)
    return C

def _sin_layer(ph, W, pre_ps, fr, fb, out32):
    k = ph.k
    a, kk = W['a'], W['kk']
    k.ts('dve', a[:], pre_ps, fr, fb, op0=ALU.mult, op1=ALU.add)
    k.ts('dve', kk[:], a[:], 1.0 / (2 * math.pi), MAGIC, op0=ALU.mult, op1=ALU.add)
    k.ts('dve', kk[:], kk[:], -MAGIC, None, op0=ALU.add)
    k.stt(a[:], kk[:], -2 * math.pi, a[:], ALU.mult, ALU.add)
    k.ts('dve', a[:], a[:], -3.14159, 3.14159, op0=ALU.max, op1=ALU.min)
    k.act(out32, a[:], AF.Sin)


def _hilo(ph, hi, lo, src32, tmp32):
    k = ph.k
    k.copy('dve', hi, src32)
    k.copy('pool', tmp32, hi)
    k.tt('pool', lo, src32, tmp32, ALU.subtract)


def phase2a(nc, T):
    with Phase(nc, "p2a") as ph:
        k = ph.k
        ps = ph.ps
        zs_hi = ph.sb("zs_hi", [128, L], BF)
        zs_lo = ph.sb("zs_lo", [128, L], BF)
        W1 = ph.sb("W1", [128, 128], F32)
        W2 = ph.sb("W2", [128, 128], F32)
        W1h = ph.sb("W1h", [128, 128], BF)
        W1l = ph.sb("W1l", [128, 128], BF)
        W2h = ph.sb("W2h", [128, 128], BF)
        W2l = ph.sb("W2l", [128, 128], BF)
        wt = ph.sb("wt", [128, 128], F32)
        mv = ph.sb("mv", [128, 4], F32)
        fb = ph.sb("fb", [128, 2], F32)
        Wk = dict(a=ph.sb("a", [128, 512], F32), kk=ph.sb("kk", [128, 512], F32))
        h1 = ph.sb("h1", [128, 512], F32)
        h1h = ph.sb("h1h", [128, 512], BF)
        h1l = ph.sb("h1l", [128, 512], BF)
        t32 = ph.sb("t32", [128, 512], F32)
        h2 = ph.sb("h2", [128, 512], F32)
        h2b = [ph.sb(f"h2b{i}", [128, 512], BF) for i in range(2)]
        k.dma(zs_hi[:], T['zs_hi'][:, :])
        k.dma(zs_lo[:], T['zs_lo'][:, :])
        k.dma(W1[:], T['W1blk'][:, :])
        k.dma(W2[:], T['W2blk'][:, :])
        k.dma(mv[:], T['mlpv'][:, :])
        _hilo(ph, W1h[:], W1l[:], W1[:], wt[:])
        _hilo(ph, W2h[:], W2l[:], W2[:], wt[:])
        k.tt('dve', fb[:, 0:1], mv[:, 0:1], mv[:, 1:2], ALU.mult)
        k.tt('dve', fb[:, 1:2], mv[:, 2:3], mv[:, 3:4], ALU.mult)
        for cch in range(DBG.get('n2a', NB)):
            sl = slice(512 * cch, 512 * (cch + 1))
            b1 = ps[cch % 2]
            k.mm(b1[:, :], W1h[:], zs_hi[:, sl], start=True, stop=False)
            k.mm(b1[:, :], W1h[:], zs_lo[:, sl], start=False, stop=False)
            k.mm(b1[:, :], W1l[:], zs_hi[:, sl], start=False, stop=True)
            if DBG.get('s2a', 9) < 1: continue
            _sin_layer(ph, Wk, b1[:, :], mv[:, 0:1], fb[:, 0:1], h1[:])
            if DBG.get('s2a', 9) < 2: continue
            _hilo(ph, h1h[:], h1l[:], h1[:], t32[:])
            if DBG.get('s2a', 9) < 3: continue
            b2 = ps[2 + cch % 2]
            k.mm(b2[:, :], W2h[:], h1h[:], start=True, stop=False)
            k.mm(b2[:, :], W2h[:], h1l[:], start=False, stop=False)
            k.mm(b2[:, :], W2l[:], h1h[:], start=False, stop=True)
            _sin_layer(ph, Wk, b2[:, :], mv[:, 2:3], fb[:, 1:2], h2[:])
            hb_ = h2b[cch % 2]
            k.copy('pool', hb_[:], h2[:])
            k.dma(T['h2_d'][:, sl], hb_[:])


def _cmul_tab(ph, W, src_ps, Ta, Tb, out_bf):
    k = ph.k
    P1, P2 = W
    sw = src_ps.rearrange("p (r f) -> p r f", r=2)[:, ::-1, :]
    s4 = DBG.get('s4', 9)
    if s4 < 2: return
    k.tt('dve', P1[:], src_ps, Ta, ALU.mult)
    if s4 < 3: return
    k.tt('dve', P2[:].rearrange("p (r f) -> p r f", r=2), sw, Tb.rearrange("p (r f) -> p r f", r=2), ALU.mult)
    if s4 < 4: return
    k.tt('pool', out_bf, P1[:], P2[:], ALU.add)


def phase2b(nc, T):
    with Phase(nc, "p2b") as ph:
        k = ph.k
        ps = ph.ps
        ident = ph.sb("ident", [128, 128], BF)
        make_ident(ph, ident)
        cb16 = {}
        for nm, w in (('FA1', 256), ('FA2', 256), ('WBr', 128), ('WBi', 128), ('WBni', 128), ('WI1', 256), ('WI2', 256),
                      ('FIr', 128), ('FIi', 128)):
            cb16[nm] = ph.sb(nm, [128, w], BF)
            k.dma(cb16[nm][:], T[nm][:, :])
        c32 = {}
        for nm in ('TWa', 'TWb', 'TIa', 'TIb'):
            c32[nm] = ph.sb(nm, [128, 256], F32)
            k.dma(c32[nm][:], T[nm][:, :])
        h2s = ph.sb("h2s", [128, L], BF)
        k.dma(h2s[:], T['h2_d'][:, :])
        h2p = ph.sb("h2p", [128, 32, 128], BF)
        k.copy('pool', h2p[:], h2s[:].rearrange("q (p s) -> q s p", s=32))
        W3 = ph.sb("W3", [128, 2048], BF)
        load_w(ph, W3[:], T['W3blk'][:, :])
        wsh = ph.sb("wsh", [128, 12, 4], F32)
        k.dma(wsh[:].rearrange("p a b -> p (a b)"), T['wsh'][:, :])
        biasT = ph.sb("biasT", [128, 2, 128], F32)
        k.dma(biasT[:].rearrange("p a b -> p (a b)"), T['biasT'][:, :])
        negd = ph.sb("negd", [128, HYW], F32)
        k.dma(negd[:], T['negd'][0:1, :].partition_broadcast(128))
        tfull = ph.sb("tfull", [128, 2, 32], F32)
        k.dma(tfull[:].rearrange("p a b -> p (a b)"), T['tfull'][:, :])
        w_in_v = T['w_in'].rearrange("(k p) n -> p k n", p=128)

        hbuf = [ph.sb(f"hbuf{i}", [128, 8, 512], BF) for i in range(2)]
        ar = ph.sb("arena", [128, 24592], BF)
        raw = [ar[:, 4098 * i:4098 * (i + 1)] for i in range(3)]
        ub_ = [ar[:, 12294 + 4096 * i:12294 + 4096 * (i + 1)] for i in range(2)]
        Wblk = ar[:, 20486:24582].rearrange("p (k w c) -> p k w c", k=8, w=4)
        k_tm = ar[:, 0:8192].rearrange("p (o d c s) -> p o d c s", o=2, d=2, c=64)
        AB = ar[:, 8192:12288].rearrange("p (d c s) -> p d c s", d=2, c=64)
        Ksp = ar[:, 12288:20480].rearrange("p (o g r f) -> p o g r f", o=2, g=16, r=2)
        Gbuf = ar[:, 20480:24576].rearrange("p (r c s) -> p r c s", r=2, c=64)
        sz = ph.sb("sz", [128, L], BF)
        tm = [ph.sb(f"tm{i}", [128, 128, 32], BF) for i in range(3)]
        z2_tm = ph.sb("z2_tm", [128, 128, 32], BF)
        y_sc = ph.sb("y_sc", [128, 32, 128], BF)
        yzb = ph.sb("yzb", [128, L], BF)
        arg32 = ph.sb("arg32", [128, 4096], F32)
        PW = [(ph.sb(f"P1_{i}", [128, 256], F32), ph.sb(f"P2_{i}", [128, 256], F32)) for i in range(2)]
        Zp = [ph.sb(f"Zp{i}", [128, 256], BF) for i in range(2)]
        Yb = [ph.sb(f"Yb{i}", [128, 256], BF) for i in range(2)]
        cols = (COL_V, COL_X1, COL_X2, COL_ZH)
        cnt = [0]

        PWs = [[(ph.sb(f"P1_{st}_{i}", [128, 256], F32), ph.sb(f"P2_{st}_{i}", [128, 256], F32)) for i in range(2)]
               for st in range(3)]

        def skewed(G, stages):
            ns = len(stages)
            for t in range(G + ns - 1):
                for s_ in reversed(range(ns)):
                    g = t - s_
                    if 0 <= g < G:
                        stages[s_](g)

        def st_za(lhs_of):
            def f(g):
                l1, l2 = lhs_of(g)
                za = ps[g % 2][:, 0:256]
                k.mm(za, l1, cb16['FA1'][:], start=True, stop=(l2 is None))
                if l2 is not None:
                    k.mm(za, l2, cb16['FA2'][:], start=False, stop=True)
            return f

        def st_tw(g):
            _cmul_tab(ph, PWs[0][g % 2], ps[g % 2][:, 0:256], c32['TWa'][:], c32['TWb'][:], Zp[g % 2][:])

        def st_ub(g):
            ub = ps[2 + g % 2][:, 0:256]
            z_p = Zp[g % 2]
            k.mm(ub[:, 0:128], cb16['WBr'][:], z_p[:, 0:128], start=True, stop=False)
            k.mm(ub[:, 0:128], cb16['WBni'][:], z_p[:, 128:256], start=False, stop=True)
            k.mm(ub[:, 128:256], cb16['WBi'][:], z_p[:, 0:128], start=True, stop=False)
            k.mm(ub[:, 128:256], cb16['WBr'][:], z_p[:, 128:256], start=False, stop=True)

        for cb in range(DBG.get('ncb', 4)):
            for w in range(4):
                load_w(ph, Wblk[:, :, w, :], w_in_v[:, :, cols[w] + 128 * cb:cols[w] + 128 * (cb + 1)])
            for w in range(3):
                k.memset('pool', raw[w][:, 0:1], 0.0)
                k.memset('pool', raw[w][:, 4097:4098], 0.0)
            for tb in range(NB):
                hb = hbuf[tb % 2]
                k.dma(hb[:].rearrange("p c t -> p (c t)"), T['hT_d'][tb])
                for w in range(4):
                    bank = ps[(tb * 4 + w) % 2]
                    for c in range(8):
                        k.mm(bank[:, :], Wblk[:, c, w, :], hb[:, c, :], start=(c == 0), stop=(c == 7))
                    if w < 3:
                        k.act(raw[w][:, 1 + 512 * tb:1 + 512 * (tb + 1)], bank[:, :], AF.Copy)
                    else:
                        k.act(sz[:, 512 * tb:512 * (tb + 1)], bank[:, :], AF.Silu)
            if DBG.get('s2b', 9) < 2: continue
            for w in range(3):
                u = ub_[w % 2]
                j = 4 * w + cb
                k.ts('dve', u, raw[w][:, 1:4097], wsh[:, j, 1:2], wsh[:, j, 3:4], op0=ALU.mult, op1=ALU.add)
                k.stt(u, raw[w][:, 0:4096], wsh[:, j, 0:1], u, ALU.mult, ALU.add)
                k.stt(u, raw[w][:, 2:4098], wsh[:, j, 2:3], u, ALU.mult, ALU.add)
                for a in range(4):
                    pv = ps[2 + a % 2][:, :].bitcast(BF)
                    for e in range(8):
                        s1 = 8 * a + e
                        k.tr(pv[:, 128 * e:128 * (e + 1)], u[:, s1:4096:32], ident[:])
                    k.copy('dve', tm[w][:, :, 8 * a:8 * a + 8].rearrange("p c s -> p s c"),
                           pv.rearrange("p (s c) -> p s c", s=8))
            if DBG.get('dump_tm'):
                k.dma(T['dbg_tm'][:, :], tm[DBG['dump_tm'] - 1][:].rearrange("p c s -> p (c s)"))
            if DBG.get('s2b', 9) < 3: continue
            for hbk in range(DBG.get('nhbk', 2)):
                c0 = 64 * hbk
                gcol = 128 * cb + c0
                k.tt('dve', arg32[:].rearrange("p (d c s) -> p d c s", d=2, c=64),
                     bcast(bcast(negd[:, gcol:gcol + 64], 1, 2), 3, 32),
                     bcast(tfull[:, :, :], 2, 64), ALU.mult)
                if DBG.get('s3', 9) < 2: continue
                k.act(AB.rearrange("p d c s -> p (d c s)"), arg32[:], AF.Exp)
                if DBG.get('s3', 9) < 3: continue
                wc0 = 256 * (2 * cb + hbk)
                for s1 in range(32):
                    kb_ = ps[6 + s1 % 2][:, 0:256]
                    k.mm(kb_, h2p[:, s1, :], W3[:, wc0:wc0 + 256])
                    if DBG.get('s3', 9) < 4: continue
                    abv = AB[:, :, :, s1].rearrange("p d c -> p (d c)")
                    for o in range(2):
                        o_ap = k_tm[:, o, :, :, s1].rearrange("p d c -> p (d c)")
                        k.tt('dve', o_ap, kb_[:, 128 * o:128 * (o + 1)], abv, ALU.mult)
                if DBG.get('s2b', 9) < 4: continue
                def spec_lhs(gi):
                    o, g = divmod(gi, 16)
                    return (k_tm[:, o, 0, 4 * g:4 * g + 4, :].rearrange("p c s -> p (c s)"),
                            k_tm[:, o, 1, 4 * g:4 * g + 4, :].rearrange("p c s -> p (c s)"))

                def st_kev(gi):
                    o, g = divmod(gi, 16)
                    ub = ps[2 + gi % 2][:, 0:256]
                    gg = (gcol // 4) + g
                    k.ts('dve', Ksp[:, o, g, 0, :], ub[:, 0:128], biasT[:, o, gg:gg + 1], None, op0=ALU.add)
                    k.copy('dve', Ksp[:, o, g, 1, :], ub[:, 128:256])

                skewed(32, [st_za(spec_lhs), st_tw, st_ub, st_kev])
                if DBG.get('dump_k'):
                    k.dma(T['dbg_k'][:, :], ar[:, 0:8192])
                    k.dma(T['dbg_ks'][:, :], ar[:, 12288:20480])
                if DBG.get('s2b', 9) < 5: continue
                for o in range(DBG.get('nord', 2)):
                    src = tm[0] if o == 0 else z2_tm
                    gate = tm[1] if o == 0 else tm[2]
                    def conv_lhs(g, src=src):
                        return (src[:, c0 + 4 * g:c0 + 4 * g + 4, :].rearrange("p c s -> p (c s)"), None)

                    def st_mul(g, o=o):
                        ub = ps[2 + g % 2][:, 0:256]
                        P1, P2 = PWs[1][g % 2]
                        usw = ub.rearrange("p (r f) -> p r f", r=2)[:, ::-1, :]
                        k.tt('dve', P1[:].rearrange("p (r f) -> p r f", r=2), ub.rearrange("p (r f) -> p r f", r=2),
                             bcast(Ksp[:, o, g, 0, :], 1, 2), ALU.mult)
                        k.tt('dve', P2[:].rearrange("p (r f) -> p r f", r=2), usw,
                             bcast(Ksp[:, o, g, 1, :], 1, 2), ALU.mult)
                        y_b = Yb[g % 2]
                        k.tt('pool', y_b[:, 0:128], P1[:, 0:128], P2[:, 0:128], ALU.subtract)
                        k.tt('pool', y_b[:, 128:256], P1[:, 128:256], P2[:, 128:256], ALU.add)

                    def st_gb(g):
                        gb = ps[4 + g % 2][:, 0:256]
                        y_b = Yb[g % 2]
                        k.mm(gb, y_b[:, 0:128], cb16['WI1'][:], start=True, stop=False)
                        k.mm(gb, y_b[:, 128:256], cb16['WI2'][:], start=False, stop=True)

                    def st_itw(g):
                        gb = ps[4 + g % 2][:, 0:256]
                        P1, P2 = PWs[2][g % 2]
                        gsw = gb.rearrange("p (r f) -> p r f", r=2)[:, ::-1, :]
                        k.tt('dve', P1[:], gb, c32['TIa'][:], ALU.mult)
                        k.tt('dve', P2[:].rearrange("p (r f) -> p r f", r=2), gsw,
                             c32['TIb'][:].rearrange("p (r f) -> p r f", r=2), ALU.mult)
                        k.tt('pool', Gbuf[:, :, 4 * g:4 * g + 4, :].rearrange("p r c s -> p r (c s)"),
                             P1[:].rearrange("p (r f) -> p r f", r=2), P2[:].rearrange("p (r f) -> p r f", r=2), ALU.add)

                    skewed(16, [st_za(conv_lhs), st_tw, st_ub, st_mul, st_gb, st_itw])
                    for cc in range(4):
                        yb = ps[6 + cc % 2]
                        k.mm(yb[:, :], cb16['FIr'][:], Gbuf[:, 0, 16 * cc:16 * cc + 16, :].rearrange("p c s -> p (c s)"),
                             start=True, stop=False)
                        k.mm(yb[:, :], cb16['FIi'][:], Gbuf[:, 1, 16 * cc:16 * cc + 16, :].rearrange("p c s -> p (c s)"),
                             start=False, stop=True)
                        cs = slice(c0 + 16 * cc, c0 + 16 * cc + 16)
                        if o == 0:
                            k.tt('dve', z2_tm[:, cs, :], yb[:, :].rearrange("p (c s) -> p c s", c=16), gate[:, cs, :], ALU.mult)
                        else:
                            k.tt('dve', y_sc[:, :, cs].rearrange("p s c -> p c s"),
                                 yb[:, :].rearrange("p (c s) -> p c s", c=16), gate[:, cs, :], ALU.mult)
            if DBG.get('dump_z2'):
                k.dma(T['dbg_tm'][:, :], z2_tm[:].rearrange("p c s -> p (c s)"))
            if DBG.get('s2b', 9) < 6: continue
            for a in range(4):
                pv = ps[2 + a % 2][:, :].bitcast(BF)
                for e in range(8):
                    k.tr(pv[:, 128 * e:128 * (e + 1)], y_sc[:, 8 * a + e, :], ident[:])
                k.tt('dve', yzb[:].rearrange("c (p s) -> c s p", s=32)[:, 8 * a:8 * a + 8, :],
                     pv.rearrange("c (s p) -> c s p", s=8),
                     sz[:].rearrange("c (p s) -> c s p", s=32)[:, 8 * a:8 * a + 8, :], ALU.mult)
            for tb in range(NB):
                k.dma(T['yz_d'][tb][:, 512 * cb:512 * (cb + 1)], yzb[:, 512 * tb:512 * (tb + 1)])


def phase2(nc, T):
    if 'a' in DBG.get('p2', 'ab'):
        phase2a(nc, T)
    if 'b' in DBG.get('p2', 'ab'):
        phase2b(nc, T)

def _bf(a):
    return np.asarray(a, np.float32).astype(ml_dtypes.bfloat16)


_CONST_CACHE = {}


def host_constants():
    if _CONST_CACHE:
        return _CONST_CACHE
    C = {}
    pos = np.arange(L, dtype=np.float32)
    inv_freq = (np.float32(10000.0) ** (-np.arange(0, 32, 2, dtype=np.float32) / np.float32(32))).astype(np.float32)
    ang = (pos[:, None] * inv_freq[None, :]).astype(np.float32)
    C['cosT'] = np.ascontiguousarray(np.cos(ang).astype(np.float32).reshape(NT, 128, 16).transpose(1, 0, 2).reshape(128, NT * 16))
    C['sinT'] = np.ascontiguousarray(np.sin(ang).astype(np.float32).reshape(NT, 128, 16).transpose(1, 0, 2).reshape(128, NT * 16))
    F = fft_constants()
    for nm in ('FA1', 'FA2', 'WBr', 'WBi', 'WBni', 'WI1', 'WI2', 'FIr', 'FIi'):
        C[nm] = np.ascontiguousarray(_bf(F[nm]))
    for nm in ('TWa', 'TWb', 'TIa', 'TIb'):
        C[nm] = np.ascontiguousarray(F[nm].astype(np.float32))
    C.update(filter_constants())
    _CONST_CACHE.update(C)
    return _CONST_CACHE


def prep_inputs(inp, b):
    f32 = np.float32
    m = {}
    m['x'] = np.ascontiguousarray(inp['x'][b], dtype=f32)
    m['w_in'] = np.ascontiguousarray(inp['w_in'][0], dtype=f32)
    m['gT'] = np.ascontiguousarray(inp['g_norm'][0].reshape(8, 128).T, dtype=f32)
    m['bgT'] = np.ascontiguousarray(inp['b_gate'][0].reshape(16, 128).T, dtype=f32)
    m['w_uq'] = np.ascontiguousarray(inp['w_uq'][0], dtype=f32)
    m['w_ukv'] = np.ascontiguousarray(inp['w_ukv'][0], dtype=f32)
    m['gcqT'] = np.ascontiguousarray(inp['g_cq'][0].reshape(3, 128).T, dtype=f32)
    m['gckvT'] = np.ascontiguousarray(inp['g_ckv'][0].reshape(2, 128).T, dtype=f32)
    m['gqk'] = np.ascontiguousarray(np.concatenate([inp['g_qn'][0], inp['g_kn'][0]])[None, :], dtype=f32)
    m['w_attn_out'] = np.ascontiguousarray(inp['w_attn_out'][0], dtype=f32)
    m['w_hy_out'] = np.ascontiguousarray(inp['w_hy_out'][0], dtype=f32)
    m['w_out'] = np.ascontiguousarray(inp['w_out'][0], dtype=f32)
    wsh = np.zeros((128, 12, 4), f32)
    wsh[:, :, 0:3] = inp['w_short'][0].reshape(3, 12, 128).transpose(2, 1, 0)
    wsh[:, :, 3] = inp['b_short'][0].reshape(12, 128).T
    m['wsh'] = wsh.reshape(128, 48)
    hb = inp['hy_bias'][0]
    bT = hb.reshape(2, 128, 4).transpose(2, 0, 1)
    m['biasT'] = np.ascontiguousarray(np.repeat(bT[:, None], 32, axis=1).reshape(128, 256), dtype=f32)
    W1 = np.zeros((128, 128), f32)
    W1[0:33, 0:64] = inp['w_f1'][0]
    W1[64:97, 64:128] = inp['w_f1'][0]
    m['W1blk'] = W1
    W2 = np.zeros((128, 128), f32)
    W2[0:64, 0:64] = inp['w_f2'][0]
    W2[64:128, 64:128] = inp['w_f2'][0]
    m['W2blk'] = W2
    mv = np.zeros((128, 4), f32)
    for jj, nm in enumerate(('freq_1', 'b_f1', 'freq_2', 'b_f2')):
        mv[0:64, jj] = inp[nm][0]
        mv[64:128, jj] = inp[nm][0]
    m['mlpv'] = mv
    w3 = inp['w_f3'][0].reshape(64, 2, 2, 8, 64)
    W3 = np.zeros((128, 8, 2, 2, 64), f32)
    for dd in range(2):
        W3[64 * dd:64 * (dd + 1), :, :, dd, :] = w3[:, :, dd, :, :].transpose(0, 2, 1, 3)
    m['W3blk'] = W3.reshape(128, 2048)
    C = host_constants()
    for nm in CONST_NAMES:
        m[nm] = C[nm]
    return m


IN_SHAPES = {
    'x': ([L, D], F32), 'w_in': ([D, 5280], F32), 'gT': ([128, 8], F32), 'bgT': ([128, 16], F32),
    'w_uq': ([384, 768], F32), 'w_ukv': ([256, 1024], F32), 'gcqT': ([128, 3], F32), 'gckvT': ([128, 2], F32),
    'gqk': ([1, 192], F32), 'w_attn_out': ([512, D], F32), 'w_hy_out': ([512, D], F32), 'w_out': ([D, D], F32),
    'cosT': ([128, NT * 16], F32), 'sinT': ([128, NT * 16], F32),
    'wsh': ([128, 48], F32), 'biasT': ([128, 256], F32), 'W1blk': ([128, 128], F32), 'W2blk': ([128, 128], F32),
    'mlpv': ([128, 4], F32), 'W3blk': ([128, 2048], F32),
    'FA1': ([128, 256], BF), 'FA2': ([128, 256], BF), 'WBr': ([128, 128], BF), 'WBi': ([128, 128], BF),
    'WBni': ([128, 128], BF), 'WI1': ([128, 256], BF), 'WI2': ([128, 256], BF), 'FIr': ([128, 128], BF),
    'FIi': ([128, 128], BF), 'TWa': ([128, 256], F32), 'TWb': ([128, 256], F32), 'TIa': ([128, 256], F32),
    'TIb': ([128, 256], F32), 'zs_hi': ([128, L], BF), 'zs_lo': ([128, L], BF), 'tfull': ([128, 64], F32),
    'negd': ([1, HYW], F32),
}
CONST_NAMES = ('cosT', 'sinT', 'FA1', 'FA2', 'WBr', 'WBi', 'WBni', 'WI1', 'WI2', 'FIr', 'FIi', 'TWa', 'TWb', 'TIa', 'TIb',
               'zs_hi', 'zs_lo', 'tfull', 'negd')


def build_nc(debug=None):
    debug = debug or set()
    nc = bass.Bass("TRN2", target_bir_lowering=False)
    T = {}
    for name, (shape, dt) in IN_SHAPES.items():
        T[name] = nc.dram_tensor(name, shape, dt, kind="ExternalInput").ap()
    T['out'] = nc.dram_tensor("out", [L, D], F32, kind="ExternalOutput").ap()
    skind = dict(kind="ExternalOutput") if 'dump' in debug else {}
    T['hT_d'] = nc.dram_tensor("hT_d", [NB, 128, 8 * 512], BF, **skind).ap()
    T['at_d'] = nc.dram_tensor("at_d", [NB, 64, 8 * 512], BF, **skind).ap()
    T['h2_d'] = nc.dram_tensor("h2_d", [128, L], BF, **skind).ap()
    if 'dump' in debug:
        T['dbg_tm'] = nc.dram_tensor("dbg_tm", [128, 4096], BF, kind="ExternalOutput").ap()
        T['dbg_k'] = nc.dram_tensor("dbg_k", [128, 8192], BF, kind="ExternalOutput").ap()
        T['dbg_ks'] = nc.dram_tensor("dbg_ks", [128, 8192], BF, kind="ExternalOutput").ap()
    if 'yz_in' in debug:
        T['yz_d'] = nc.dram_tensor("yz_d", [NB, 128, 4 * 512], BF, kind="ExternalInput").ap()
    else:
        T['yz_d'] = nc.dram_tensor("yz_d", [NB, 128, 4 * 512], BF, **skind).ap()
    phases = debug & {'p1', 'p2', 'p3', 'p4'} or {'p1', 'p2', 'p3', 'p4'}
    if 'p1' in phases:
        phase1(nc, T)
    if 'p2' in phases and 'yz_in' not in debug:
        phase2(nc, T)
    if 'p3' in phases:
        phase3(nc, T)
    if 'p4' in phases:
        phase4(nc, T)
    return nc


def kernel(**inputs):
    inp = {k_: np.asarray(v) for k_, v in inputs.items()}
    nc = build_nc()
    in_maps = [prep_inputs(inp, b) for b in range(8)]
    res = run_bass_kernel_spmd(nc, in_maps, core_ids=list(range(8)))
    out = np.stack([np.asarray(r['out'], dtype=np.float32) for r in res.results], axis=0)
    return out
```

```python
import concourse.bass as bass
import concourse.mybir as mybir

_ESZ = {}


def _esize(dt):
    s = _ESZ.get(dt)
    if s is None:
        n = str(dt)
        if '32' in n:
            s = 4
        elif '16' in n:
            s = 2
        elif '8' in n:
            s = 1
        else:
            s = 4
        _ESZ[dt] = s
    return s


def footprint(ap):
    t = ap.tensor
    name = t.name
    es = _esize(ap.dtype)
    apl = ap.ap
    off = int(ap.offset) * es
    space = str(type(t).__name__)
    if 'DRam' in space:
        lo = off
        hi = off
        for st, cnt in apl:
            if cnt > 1:
                d = (cnt - 1) * st * es
                if d > 0:
                    hi += d
                else:
                    lo += d
        return (name, 0, 1, lo, hi + es)
    pstep, pcnt = apl[0]
    pstep_b = pstep * es
    if pstep_b > 0:
        p0 = off // pstep_b
        f0 = off % pstep_b
    else:
        p0 = 0
        f0 = off
    lo = f0
    hi = f0
    for st, cnt in apl[1:]:
        if cnt > 1:
            d = (cnt - 1) * st * es
            if d > 0:
                hi += d
            else:
                lo += d
    return (name, p0, p0 + pcnt, lo, hi + es)


COMPUTE = ('pe', 'act', 'dve', 'pool')
QUEUES = ('pe', 'act', 'dve', 'pool', 'sp')
QIDX = {q: i for i, q in enumerate(QUEUES)}


class _Op:
    __slots__ = ('q', 'fn', 'dma', 'idx', 'gid', 'waits_c', 'waits_d', 'signal', 'snap', 'slot', 'slot_cnt', 'prev_slot')


class Prog:
    def __init__(self, nc, dma_slots=None):
        self.nc = nc
        self.streams = {q: [] for q in QUEUES}
        self.recs = {}
        self.known = {q: [-1] * len(QUEUES) for q in QUEUES}
        self.known_dma = {q: set() for q in QUEUES}
        self.ops = []
        self.dma_slots = dma_slots or {'sp': 8, 'pool': 4, 'act': 4}
        self.dma_count = {q: 0 for q in QUEUES}
        self.dma_ops = {q: [] for q in QUEUES}
        self.n_comp = {q: 0 for q in QUEUES}

    def add(self, q, fn, reads=(), writes=(), dma=False):
        op = _Op()
        op.q = q
        op.fn = fn
        op.dma = dma
        op.gid = len(self.ops)
        op.signal = dma
        op.waits_c = []
        op.waits_d = []
        op.slot = None
        op.prev_slot = None
        stream = self.streams[q]
        if not dma:
            op.idx = self.n_comp[q]
            self.n_comp[q] += 1
        else:
            op.idx = -1
        deps_c = {}
        deps_d = set()

        def scan(fp, is_write):
            name, p0, p1, f0, f1 = fp
            lst = self.recs.get(name)
            if not lst:
                return
            for r in lst:
                (rp0, rp1, rf0, rf1, rw, rop) = r
                if not (is_write or rw):
                    continue
                if rp1 <= p0 or p1 <= rp0 or rf1 <= f0 or f1 <= rf0:
                    continue
                if rop.dma:
                    deps_d.add(rop)
                else:
                    e = rop.q
                    if deps_c.get(e, -1) < rop.idx:
                        deps_c[e] = rop.idx

        rfps = [footprint(a) for a in reads]
        wfps = [footprint(a) for a in writes]
        for fp in rfps:
            scan(fp, False)
        for fp in wfps:
            scan(fp, True)
        known = self.known[q]
        kd = self.known_dma[q]
        for e, i in deps_c.items():
            ei = QIDX[e]
            if e == q and not dma:
                if q == 'pe':
                    continue
            if i <= known[ei]:
                continue
            op.waits_c.append((e, i))
            src = self.comp_ops[e][i]
            src.signal = True
            known[ei] = i
            for k, v in enumerate(src.snap):
                if v > known[k]:
                    known[k] = v
        for d in sorted(deps_d, key=lambda o: o.gid):
            if d.gid in kd:
                continue
            op.waits_d.append(d)
            kd.add(d.gid)
            for k, v in enumerate(d.snap):
                if v > known[k]:
                    known[k] = v
        if dma:
            n = self.dma_count[q]
            R = self.dma_slots[q]
            op.slot = n % R
            op.slot_cnt = n // R + 1
            if n >= R:
                prev = self.dma_ops[q][n - R]
                op.prev_slot = prev
                kd.add(prev.gid)
            self.dma_count[q] = n + 1
            self.dma_ops[q].append(op)
        op.snap = tuple(known)
        if not dma:
            self.comp_ops[q].append(op)
        for fp, is_write in [(f, False) for f in rfps] + [(f, True) for f in wfps]:
            name, p0, p1, f0, f1 = fp
            lst = self.recs.setdefault(name, [])
            if is_write:
                lst[:] = [r for r in lst if not (r[0] >= p0 and r[1] <= p1 and r[2] >= f0 and r[3] <= f1)]
            else:
                if not dma:
                    lst[:] = [r for r in lst if not (r[4] is False and (not r[5].dma) and r[5].q == q
                                                     and r[0] == p0 and r[1] == p1 and r[2] == f0 and r[3] == f1)]
            lst.append((p0, p1, f0, f1, is_write, op))
        stream.append(op)
        self.ops.append(op)
        return op

    comp_ops = None

    def start(self):
        self.comp_ops = {q: [] for q in QUEUES}

    def pe(self, fn, reads, writes):
        return self.add('pe', fn, reads, writes)

    def act(self, fn, reads, writes):
        return self.add('act', fn, reads, writes)

    def dve(self, fn, reads, writes):
        return self.add('dve', fn, reads, writes)

    def pool(self, fn, reads, writes):
        return self.add('pool', fn, reads, writes)

    def dma(self, out, in_, q='sp', **kw):
        return self.add(q, lambda e: e.dma_start(out=out, in_=in_, **kw), [in_], [out], dma=True)

    def emit(self, block, sems_c, sems_d):
        cum = {}
        for e in QUEUES:
            c = 0
            arr = []
            for o in self.comp_ops[e]:
                if o.signal:
                    c += 1
                arr.append(c)
            cum[e] = arr
        self.cum = cum

        def gen(q):
            def body(eng):
                for o in self.streams[q]:
                    for (e, i) in o.waits_c:
                        eng.wait_ge(sems_c[e], cum[e][i])
                    for d in o.waits_d:
                        eng.wait_ge(sems_d[d.q][d.slot], 16 * d.slot_cnt)
                    if o.prev_slot is not None:
                        p = o.prev_slot
                        eng.wait_ge(sems_d[p.q][p.slot], 16 * p.slot_cnt)
                    ins = o.fn(eng)
                    if o.dma:
                        ins.then_inc(sems_d[q][o.slot], 16)
                    elif o.signal:
                        ins.then_inc(sems_c[q], 1)
                R = self.dma_slots.get(q, 0)
                n = self.dma_count[q]
                for o in self.dma_ops[q][max(0, n - R):]:
                    eng.wait_ge(sems_d[q][o.slot], 16 * o.slot_cnt)
            return body

        if self.streams['pe']:
            block.tensor(gen('pe'))
        if self.streams['act']:
            block.scalar(gen('act'))
        if self.streams['dve']:
            block.vector(gen('dve'))
        if self.streams['pool']:
            block.gpsimd(gen('pool'))
        if self.streams['sp']:
            block.sync(gen('sp'))

import math
from contextlib import ExitStack
import numpy as np
import ml_dtypes
from concourse.bass_utils import run_bass_kernel_spmd

F32 = mybir.dt.float32
BF = mybir.dt.bfloat16
AF = mybir.ActivationFunctionType
ALU = mybir.AluOpType
AX = mybir.AxisListType

L = 4096
D = 1024
NT = 32
NB = 8
EPS = 1e-6
NF = 8192
HYW = 512
COL_V, COL_X1, COL_X2, COL_ZH = 0, 512, 1024, 1536
COL_CQ, COL_CKV, COL_KR, COL_ZA = 2048, 2432, 2688, 2720
COL_GH, COL_GA = 3232, 4256
MAGIC = 12582912.0
DBG = {}


def bcast(ap, axis, n):
    a = ap.unsqueeze(axis)
    shp = list(a.shape)
    shp[axis] = n
    return a.to_broadcast(shp)


class K:
    def __init__(self, P):
        self.P = P

    def mm(self, out, lhsT, rhs, start=True, stop=True):
        self.P.pe(lambda e: e.matmul(out, lhsT=lhsT, rhs=rhs, start=start, stop=stop), [lhsT, rhs], [out])

    def tr(self, out, in_, ident):
        self.P.pe(lambda e: e.transpose(out=out, in_=in_, identity=ident), [in_, ident], [out])

    def act(self, out, in_, func, bias=None, scale=None, accum_out=None):
        kw = {}
        reads = [in_]
        writes = [out]
        if bias is not None:
            kw['bias'] = bias
            if not isinstance(bias, (int, float)):
                reads.append(bias)
        if scale is not None:
            kw['scale'] = scale
            if not isinstance(scale, (int, float)):
                reads.append(scale)
        if accum_out is not None:
            kw['accum_out'] = accum_out
            writes.append(accum_out)
        self.P.act(lambda e: e.activation(out=out, in_=in_, func=func, **kw), reads, writes)

    def tt(self, eng, out, in0, in1, op):
        self.P.add(eng, lambda e: e.tensor_tensor(out=out, in0=in0, in1=in1, op=op), [in0, in1], [out])

    def ts(self, eng, out, in0, s1, s2=None, op0=ALU.mult, op1=None):
        reads = [in0]
        if not isinstance(s1, (int, float)):
            reads.append(s1)
        if s2 is not None and not isinstance(s2, (int, float)):
            reads.append(s2)
        if op1 is None:
            self.P.add(eng, lambda e: e.tensor_scalar(out=out, in0=in0, scalar1=s1, scalar2=None, op0=op0), reads, [out])
        else:
            self.P.add(eng, lambda e: e.tensor_scalar(out=out, in0=in0, scalar1=s1, scalar2=s2, op0=op0, op1=op1), reads, [out])

    def stt(self, out, in0, scalar, in1, op0, op1):
        reads = [in0, in1]
        if not isinstance(scalar, (int, float)):
            reads.append(scalar)
        self.P.dve(lambda e: e.scalar_tensor_tensor(out=out, in0=in0, scalar=scalar, in1=in1, op0=op0, op1=op1), reads, [out])

    def copy(self, eng, out, in_):
        if eng == 'act':
            self.act(out, in_, AF.Copy)
        else:
            self.P.add(eng, lambda e: e.tensor_copy(out=out, in_=in_), [in_], [out])

    def recip(self, out, in_):
        self.P.dve(lambda e: e.reciprocal(out=out, in_=in_), [in_], [out])

    def reduce_add(self, out, in_):
        self.P.dve(lambda e: e.tensor_reduce(out=out, in_=in_, axis=AX.X, op=ALU.add), [in_], [out])

    def memset(self, eng, ap, val):
        self.P.add(eng, lambda e: e.memset(ap, val), [], [ap])

    def dma(self, out, in_, q='sp'):
        self.P.dma(out, in_, q=q)


def _dump(P):
    cum = {}
    for e in QUEUES:
        c = 0
        arr = []
        for o in P.comp_ops[e]:
            if o.signal:
                c += 1
            arr.append(c)
        cum[e] = arr
    for q in QUEUES:
        print("== stream", q)
        for o in P.streams[q]:
            w = [f"{e}>={cum[e][i]}(op{i})" for e, i in o.waits_c] + [f"dma[{d.q}{d.slot}]>={16*d.slot_cnt}" for d in o.waits_d]
            if o.prev_slot is not None:
                w.append(f"prev dma[{o.prev_slot.q}{o.prev_slot.slot}]>={16*o.prev_slot.slot_cnt}")
            tag = f"DMA slot{o.slot} cnt{o.slot_cnt}" if o.dma else (f"op{o.idx} sig={cum[q][o.idx] if o.signal else '-'}")
            print("   ", tag, getattr(o, 'desc', ''), "waits:", w)


class Phase:
    def __init__(self, nc, name):
        self.nc = nc
        self.name = name
        self.es = ExitStack()

    def __enter__(self):
        nc = self.nc
        es = self.es
        es.__enter__()
        self.ps = [es.enter_context(nc.psum_tensor(f"{self.name}_ps{i}", [128, 512], F32)) for i in range(8)]
        self.sems_c = {e: es.enter_context(nc.semaphore(f"{self.name}_sc_{e}")) for e in QUEUES}
        self.sems_d = {q: [es.enter_context(nc.semaphore(f"{self.name}_sd_{q}{i}")) for i in range(n)]
                       for q, n in (('sp', 8), ('pool', 4), ('act', 4))}
        self.P = Prog(nc)
        self.P.start()
        self.k = K(self.P)
        return self

    def sb(self, name, shape, dt):
        return self.es.enter_context(self.nc.sbuf_tensor(f"{self.name}_{name}", shape, dt))

    def __exit__(self, *a):
        if a[0] is None:
            self.es.enter_context(self.nc.allow_low_precision("bf16 operands / intermediates by design"))
            block = self.es.enter_context(self.nc.Block())
            if DBG.get('dump') == self.name:
                _dump(self.P)
            self.P.emit(block, self.sems_c, self.sems_d)
        return self.es.__exit__(*a)


def make_ident(ph, ident):
    identf = ph.sb("identf", [128, 128], F32)
    ph.k.memset('pool', identf[:], 0.0)
    ph.P.pool(lambda e: e.affine_select(out=identf[:], in_=identf[:], pattern=[[-1, 128]], compare_op=ALU.not_equal,
                                        fill=1.0, base=0, channel_multiplier=1), [identf[:]], [identf[:]])
    ph.k.copy('dve', ident[:], identf[:])


def load_w(ph, dst, src_ap):
    ph.k.dma(dst, src_ap, q='pool')


def rstd_from_ss(k, rs_col, ss_col, n):
    k.act(rs_col, ss_col, AF.Sqrt, bias=EPS, scale=1.0 / n)
    k.recip(rs_col, rs_col)


def phase1(nc, T):
    with Phase(nc, "p1") as ph:
        k = ph.k
        xt = [ph.sb(f"xt{i}", [128, D], F32) for i in range(3)]
        xn = [ph.sb(f"xn{i}", [128, D], BF) for i in range(2)]
        junk = ph.sb("junk", [128, D], BF)
        ss = ph.sb("ss", [128, NT], F32)
        rs = ph.sb("rs", [128, NT], F32)
        gT = ph.sb("gT", [128, 8], F32)
        ident = ph.sb("ident", [128, 128], BF)
        hb = [ph.sb(f"hb{i}", [128, 8, 512], BF) for i in range(2)]
        make_ident(ph, ident)
        k.dma(gT[:], T['gT'][:, :])
        for i in range(NT):
            x_t = xt[i % 3]
            k.dma(x_t[:], T['x'][128 * i:128 * (i + 1), :])
            k.act(junk[:], x_t[:], AF.Square, accum_out=ss[:, i:i + 1])
            rstd_from_ss(k, rs[:, i:i + 1], ss[:, i:i + 1], D)
            x_n = xn[i % 2]
            k.ts('dve', x_n[:], x_t[:], rs[:, i:i + 1])
            bank = ph.ps[i % 4]
            pv = bank[:, :].bitcast(BF)
            for c in range(8):
                k.tr(pv[:, 128 * c:128 * (c + 1)], x_n[:, 128 * c:128 * (c + 1)], ident[:])
            h_b = hb[(i // 4) % 2]
            j = i % 4
            k.tt('dve', h_b[:, :, 128 * j:128 * (j + 1)], pv.rearrange("p (c t) -> p c t", c=8),
                 bcast(gT[:, :], 2, 128), ALU.mult)
            if j == 3:
                k.dma(T['hT_d'][i // 4], h_b[:].rearrange("p c t -> p (c t)"))


def run_interleaved(gens, width=2):
    active = []
    gens = list(gens)
    while gens or active:
        while gens and len(active) < width:
            active.append(gens.pop(0))
        for g in list(active):
            try:
                next(g)
            except StopIteration:
                active.remove(g)


def rstd_pool(k, rs, ss, n, mhalf, tmp):
    k.ts('dve', tmp, ss, 1.0 / n, EPS, op0=ALU.mult, op1=ALU.add)
    k.tt('pool', rs, tmp, mhalf, ALU.pow)


def qk_norm_rope(ph, W, src, dst, g_rep, cos_t, sin_t, mhalf):
    k = ph.k
    sq, ssq, rk, ta, tb_, tmp = W['sq'], W['ssq'], W['rk'], W['ta'], W['tb'], W['tmp']
    k.tt('pool', sq[:], src[:], src[:], ALU.mult)
    k.reduce_add(ssq[:], sq[:])
    rstd_pool(k, rk[:], ssq[:], 96, mhalf[:, 0:8], tmp[:])
    yield
    k.tt('dve', src[:], src[:], bcast(rk[:, :], 2, 96), ALU.mult)
    k.tt('pool', src[:], src[:], bcast(g_rep, 1, 8), ALU.mult)
    yield
    t1 = src[:, :, 64:80]
    t2 = src[:, :, 80:96]
    cb = bcast(cos_t, 1, 8)
    sbb = bcast(sin_t, 1, 8)
    k.tt('pool', ta[:, 0], t1, cb, ALU.mult)
    k.tt('pool', tb_[:, 0], t2, sbb, ALU.mult)
    k.tt('pool', ta[:, 1], t1, sbb, ALU.mult)
    k.tt('pool', tb_[:, 1], t2, cb, ALU.mult)
    k.tt('dve', dst[:, :, 64:80], ta[:, 0], tb_[:, 0], ALU.subtract)
    k.tt('dve', dst[:, :, 80:96], ta[:, 1], tb_[:, 1], ALU.add)
    k.copy('pool', dst[:, :, 0:64], src[:, :, 0:64])
    yield


def phase3(nc, T):
    with Phase(nc, "p3") as ph:
        k = ph.k
        ps = ph.ps
        ident = ph.sb("ident", [128, 128], BF)
        make_ident(ph, ident)
        w_in_v = T['w_in'].rearrange("(k p) n -> p k n", p=128)
        Wkv = ph.sb("Wkv", [128, 8, 288], BF)
        Wq = ph.sb("Wq", [128, 8, 384], BF)
        Wuq = ph.sb("Wuq", [128, 3, 768], BF)
        Wukv = ph.sb("Wukv", [128, 2, 1024], BF)
        gcq = ph.sb("gcq", [128, 3], F32)
        gckv = ph.sb("gckv", [128, 2], F32)
        gqk = ph.sb("gqk", [128, 192], F32)
        cosT = ph.sb("cosT", [128, NT, 16], F32)
        sinT = ph.sb("sinT", [128, NT, 16], F32)
        mhalf = ph.sb("mhalf", [128, 8], F32)
        k.memset('pool', mhalf[:], -0.5)
        load_w(ph, Wkv[:], w_in_v[:, :, COL_CKV:COL_CKV + 288])
        load_w(ph, Wq[:], w_in_v[:, :, COL_CQ:COL_CQ + 384])
        load_w(ph, Wuq[:], T['w_uq'].rearrange("(k p) n -> p k n", p=128))
        load_w(ph, Wukv[:], T['w_ukv'].rearrange("(k p) n -> p k n", p=128))
        k.dma(gcq[:], T['gcqT'][:, :])
        k.dma(gckv[:], T['gckvT'][:, :])
        k.dma(gqk[:], T['gqk'][0:1, :].partition_broadcast(128))
        k.dma(cosT[:].rearrange("p a b -> p (a b)"), T['cosT'][:, :])
        k.dma(sinT[:].rearrange("p a b -> p (a b)"), T['sinT'][:, :])
        k.tt('pool', Wuq[:], Wuq[:], bcast(gcq[:, :], 2, 768), ALU.mult)
        k.tt('pool', Wukv[:], Wukv[:], bcast(gckv[:, :], 2, 1024), ALU.mult)

        kT = ph.sb("kT", [128, 8, L], BF)
        vx = ph.sb("vx", [128, NT, 8, 65], BF)
        k.memset('pool', vx[:, :, :, 64:65], 1.0)
        ones = ph.sb("ones", [128, 64], BF)
        k.memset('pool', ones[:], 1.0)
        hbuf = [ph.sb(f"hbuf{i}", [128, 8, 512], BF) for i in range(2)]
        junk = [ph.sb(f"junk{i}", [128, 384], BF) for i in range(2)]
        ssl = ph.sb("ssl", [128, 2 * NT], F32)
        rsl = ph.sb("rsl", [128, 2 * NT], F32)
        tsl = ph.sb("tsl", [128, 2 * NT], F32)
        latn = [ph.sb(f"latn{i}", [128, 384], BF) for i in range(2)]
        latT = [ph.sb(f"latT{i}", [128, 3, 128], BF) for i in range(2)]
        kr = [ph.sb(f"kr{i}", [128, 32], F32) for i in range(2)]
        qk32 = [ph.sb(f"qk32_{i}", [128, 8, 96], F32) for i in range(2)]
        qkbf = [ph.sb(f"qkbf{i}", [128, 8, 96], BF) for i in range(2)]
        Wk_ = [dict(sq=ph.sb(f"sq{i}", [128, 8, 96], F32), ssq=ph.sb(f"ssq{i}", [128, 8], F32),
                    rk=ph.sb(f"rk{i}", [128, 8], F32), tmp=ph.sb(f"tmpn{i}", [128, 8], F32),
                    ta=ph.sb(f"ta{i}", [128, 2, 8, 16], F32), tb=ph.sb(f"tb{i}", [128, 2, 8, 16], F32)) for i in range(2)]
        qT = [ph.sb(f"qT{i}", [128, 8, 512], BF) for i in range(2)]
        pt = [ph.sb(f"pt{i}", [128, 512], BF) for i in range(4)]
        rsum = [ph.sb(f"rsum{i}", [128, 512], BF) for i in range(2)]
        bcs = [ph.sb(f"bcs{i}", [64, 512], F32) for i in range(2)]
        at = [ph.sb(f"at{i}", [64, 8, 512], BF) for i in range(2)]

        def sumsq(i, col, src_ps, n):
            k.act(junk[i % 2][:, 0:n], src_ps, AF.Square, accum_out=ssl[:, col:col + 1])
            rstd_pool(k, rsl[:, col:col + 1], ssl[:, col:col + 1], n, mhalf[:, 0:1], tsl[:, col:col + 1])

        def kv_tile(i, hb, j):
            par = i % 2
            if j == 0:
                k.dma(hb[:].rearrange("p c t -> p (c t)"), T['hT_d'][i // 4])
            lat = ps[par][:, 0:288]
            for c in range(8):
                k.mm(lat, hb[:, c, 128 * j:128 * (j + 1)], Wkv[:, c, :], start=(c == 0), stop=(c == 7))
            sumsq(i, i, lat[:, 0:256], 256)
            yield
            ln = latn[par]
            k.ts('dve', ln[:, 0:256], lat[:, 0:256], rsl[:, i:i + 1])
            k.copy('dve', kr[par][:], lat[:, 256:288])
            tbank = ps[2][:, :].bitcast(BF)
            for c in range(2):
                k.tr(tbank[:, 128 * c:128 * (c + 1)], ln[:, 128 * c:128 * (c + 1)], ident[:])
            lT = latT[par]
            k.copy('dve', lT[:, 0:2, :].rearrange("p c t -> p (c t)"), tbank[:, 0:256])
            yield
            kvb = [ps[3 + 2 * par], ps[4 + 2 * par]]
            for half in range(2):
                for c in range(2):
                    k.mm(kvb[half][:, :], lT[:, c, :], Wukv[:, c, 512 * half:512 * (half + 1)],
                         start=(c == 0), stop=(c == 1))
            kk = qk32[par]
            for half in range(2):
                kvv = kvb[half][:, :].rearrange("p (h e) -> p h e", h=4)
                k.copy('dve', kk[:, 4 * half:4 * half + 4, 0:64], kvv[:, :, 0:64])
                k.copy('dve', vx[:, i, 4 * half:4 * half + 4, 0:64], kvv[:, :, 64:128])
            k.copy('pool', kk[:, :, 64:96], bcast(kr[par][:, :], 1, 8))
            yield
            kf = qkbf[par]
            yield from qk_norm_rope(ph, Wk_[par], kk, kf, gqk[:, 96:192], cosT[:, i, :], sinT[:, i, :], mhalf)
            kbank = ps[7][:, :].bitcast(BF)
            for h in range(8):
                k.tr(kbank[0:96, 128 * h:128 * (h + 1)], kf[:, h, :], ident[:])
            k.copy('dve', kT[0:96, :, 128 * i:128 * (i + 1)], kbank[0:96, :].rearrange("p (h t) -> p h t", h=8))
            yield

        def q_tile(i, hb, j, q_T):
            par = i % 2
            lat = ps[6][:, 0:384]
            for c in range(8):
                k.mm(lat, hb[:, c, 128 * j:128 * (j + 1)], Wq[:, c, :], start=(c == 0), stop=(c == 7))
            yield
            lsb = Wk_[par]['sq'][:].rearrange("p a b -> p (a b)")[:, 0:384]
            k.copy('dve', lsb, lat)
            k.P.dve(lambda e: e.scalar_tensor_tensor(out=junk[par][:, 0:384], in0=lsb, scalar=1.0, in1=lsb, op0=ALU.mult,
                                                     op1=ALU.mult, accum_out=ssl[:, NT + i:NT + i + 1]),
                    [lsb], [junk[par][:, 0:384], ssl[:, NT + i:NT + i + 1]])
            rstd_pool(k, rsl[:, NT + i:NT + i + 1], ssl[:, NT + i:NT + i + 1], 384, mhalf[:, 0:1], tsl[:, NT + i:NT + i + 1])
            yield
            ln = latn[par]
            k.ts('dve', ln[:, 0:384], lsb, rsl[:, NT + i:NT + i + 1])
            yield
            tbank = ps[7][:, :].bitcast(BF)
            for c in range(3):
                k.tr(tbank[:, 128 * c:128 * (c + 1)], ln[:, 128 * c:128 * (c + 1)], ident[:])
            yield
            lT = latT[par]
            k.copy('dve', lT[:].rearrange("p c t -> p (c t)"), tbank[:, 0:384])
            yield
            qq = qk32[par]
            for half in range(2):
                qb = ps[6][:, 0:384]
                for c in range(3):
                    k.mm(qb, lT[:, c, :], Wuq[:, c, 384 * half:384 * (half + 1)], start=(c == 0), stop=(c == 2))
                yield
                k.copy('dve', qq[:, 4 * half:4 * half + 4, :], qb.rearrange("p (h e) -> p h e", h=4))
                yield
            qf = qkbf[par]
            yield from qk_norm_rope(ph, Wk_[par], qq, qf, gqk[:, 0:96], cosT[:, i, :], sinT[:, i, :], mhalf)
            yield
            yield
            yield
            yield
            qbank = ps[7][:, :].bitcast(BF)
            for h in range(8):
                k.tr(qbank[0:96, 128 * h:128 * (h + 1)], qf[:, h, :], ident[:])
            yield
            k.copy('dve', q_T[0:96, :, 128 * j:128 * (j + 1)], qbank[0:96, :].rearrange("p (h t) -> p h t", h=8))
            yield

        def kv_block(tb):
            hb = hbuf[tb % 2]
            return [kv_tile(tb * 4 + j, hb, j) for j in range(4)]

        def q_chunk_gens(qc):
            hb = hbuf[qc % 2]
            k.dma(hb[:].rearrange("p c t -> p (c t)"), T['hT_d'][qc])
            return [q_tile(qc * 4 + j, hb, j, qT[qc % 2]) for j in range(4)]

        gens = []
        for tb in range(DBG.get('nprep', NB)):
            gens += kv_block(tb)
        run_interleaved(gens, 2)
        nqc = DBG.get('nqc', NB)
        if nqc:
            run_interleaved(q_chunk_gens(0), 1)

        scale = 1.0 / math.sqrt(96.0)
        NH = DBG.get('nh', 8)
        pend = [None]
        for qc in range(nqc):
            q_T = qT[qc % 2]
            a_t = at[qc % 2]
            nxt = q_chunk_gens(qc + 1) if qc + 1 < nqc else []
            nxt_active = []
            steps = [(h, kt) for h in range(NH) for kt in range(NT)]

            def S(idx):
                h, kt = steps[idx]
                k.mm(ps[idx % 3][:, :], kT[0:96, h, 128 * kt:128 * (kt + 1)], q_T[0:96, h, :])

            def fin_a(h):
                k.recip(rsum[h % 2][64:65, :], ps[3 + (h % 2)][64:65, :])

            def fin_b(h):
                k.mm(ps[5][0:64, :], ones[64:65, :], rsum[h % 2][64:65, :])

            def fin_c(h, a_t):
                k.copy('dve', bcs[h % 2][:], ps[5][0:64, :])
                k.tt('dve', a_t[:, h, :], ps[3 + (h % 2)][0:64, :], bcs[h % 2][:], ALU.mult)

            S(0)
            S(1)
            for idx, (h, kt) in enumerate(steps):
                p_t = pt[idx % 4]
                k.act(p_t[:], ps[idx % 3][:, :], AF.Exp, scale=scale)
                if idx + 2 < len(steps):
                    S(idx + 2)
                ob = ps[3 + (h % 2)]
                k.mm(ob[0:65, :], vx[:, kt, h, :], p_t[:], start=(kt == 0), stop=(kt == NT - 1))
                if pend[0] is not None:
                    if kt == 1:
                        pend[0][0]()
                    elif kt == 10:
                        pend[0][1]()
                    elif kt == 14:
                        pend[0][2]()
                        pend[0] = None
                if kt == NT - 1:
                    last = (h == NH - 1)
                    pend[0] = (lambda h=h: fin_a(h), lambda h=h: fin_b(h),
                               (lambda h=h, a_t=a_t, qc=qc, last=last, fin_c=fin_c: (fin_c(h, a_t), k.dma(T['at_d'][qc], a_t[:].rearrange("p h t -> p (h t)")) if last else None)))
                if idx % 3 == 2:
                    while nxt and len(nxt_active) < 1:
                        nxt_active.append(nxt.pop(0))
                    for g in list(nxt_active):
                        try:
                            next(g)
                        except StopIteration:
                            nxt_active.remove(g)
            run_interleaved(nxt_active + nxt, 1)

        if pend[0] is not None:
            pend[0][0]()
            pend[0][1]()
            pend[0][2]()


def phase4(nc, T):
    with Phase(nc, "p4") as ph:
        k = ph.k
        ps = ph.ps
        w_in_v = T['w_in'].rearrange("(k p) n -> p k n", p=128)
        Wz = ph.sb("Wz", [128, 8, 512], BF)
        Wg = ph.sb("Wg", [128, 8, 2048], BF)
        Wao = ph.sb("Wao", [64, 8, D], BF)
        Who = ph.sb("Who", [128, 4, D], BF)
        Wout = ph.sb("Wout", [128, 8, D], BF)
        bg = ph.sb("bg", [128, 16], F32)
        load_w(ph, Wz[:], w_in_v[:, :, COL_ZA:COL_ZA + 512])
        for q4 in range(4):
            load_w(ph, Wg[:, :, 512 * q4:512 * (q4 + 1)], w_in_v[:, :, COL_GH + 512 * q4:COL_GH + 512 * (q4 + 1)])
        load_w(ph, Wao[:], T['w_attn_out'].rearrange("(h p) n -> p h n", p=64))
        load_w(ph, Who[:], T['w_hy_out'].rearrange("(k p) n -> p k n", p=128))
        load_w(ph, Wout[:], T['w_out'].rearrange("(k p) n -> p k n", p=128))
        k.dma(bg[:], T['bgT'][:, :])
        hbuf = [ph.sb(f"hbuf{i}", [128, 8, 512], BF) for i in range(2)]
        atb = [ph.sb(f"atb{i}", [64, 8, 512], BF) for i in range(2)]
        yzb = [ph.sb(f"yzb{i}", [128, 4, 512], BF) for i in range(2)]
        xt = [ph.sb(f"xt{i}", [128, D], F32) for i in range(3)]
        ot = [ph.sb(f"ot{i}", [128, D], F32) for i in range(2)]
        sz = [ph.sb(f"sz{i}", [64, 512], BF) for i in range(2)]
        ya = ph.sb("ya", [64, 8, 512], BF)
        gh = [ph.sb(f"gh{i}", [128, 512], BF) for i in range(2)]
        ga = [ph.sb(f"ga{i}", [128, 512], BF) for i in range(2)]
        m1 = [ph.sb(f"m1{i}", [128, 512], F32) for i in range(2)]
        m2 = [ph.sb(f"m2{i}", [128, 512], F32) for i in range(2)]
        mg = [ph.sb(f"mg{i}", [128, 8, 512], BF) for i in range(2)]
        for tb in range(NB):
            hb = hbuf[tb % 2]
            a_b = atb[tb % 2]
            y_b = yzb[tb % 2]
            k.dma(hb[:].rearrange("p c t -> p (c t)"), T['hT_d'][tb])
            k.dma(a_b[:].rearrange("p h t -> p (h t)"), T['at_d'][tb])
            k.dma(y_b[:].rearrange("p c t -> p (c t)"), T['yz_d'][tb])
            for h in range(8):
                zb = ps[h % 2][0:64, :]
                for c in range(8):
                    k.mm(zb, Wz[:, c, 64 * h:64 * (h + 1)], hb[:, c, :], start=(c == 0), stop=(c == 7))
                s_z = sz[h % 2]
                k.act(s_z[:], zb, AF.Silu)
                k.tt('pool', ya[:, h, :], a_b[:, h, :], s_z[:], ALU.mult)
            m_g = mg[tb % 2]
            for dc in range(8):
                g1 = ps[2 + (dc % 2)]
                g2 = ps[4 + (dc % 2)]
                for c in range(8):
                    k.mm(g1[:, :], Wg[:, c, 128 * dc:128 * (dc + 1)], hb[:, c, :], start=(c == 0), stop=(c == 7))
                for c in range(8):
                    k.mm(g2[:, :], Wg[:, c, 1024 + 128 * dc:1024 + 128 * (dc + 1)], hb[:, c, :],
                         start=(c == 0), stop=(c == 7))
                k.act(gh[dc % 2][:], g1[:, :], AF.Sigmoid, bias=bg[:, dc:dc + 1])
                k.act(ga[dc % 2][:], g2[:, :], AF.Sigmoid, bias=bg[:, 8 + dc:9 + dc])
                uh = ps[6]
                ua = ps[7]
                for c in range(4):
                    k.mm(uh[:, :], Who[:, c, 128 * dc:128 * (dc + 1)], y_b[:, c, :], start=(c == 0), stop=(c == 3))
                for h in range(8):
                    k.mm(ua[:, :], Wao[:, h, 128 * dc:128 * (dc + 1)], ya[:, h, :], start=(h == 0), stop=(h == 7))
                k.tt('dve', m1[dc % 2][:], uh[:, :], gh[dc % 2][:], ALU.mult)
                k.tt('dve', m2[dc % 2][:], ua[:, :], ga[dc % 2][:], ALU.mult)
                k.tt('pool', m_g[:, dc, :], m1[dc % 2][:], m2[dc % 2][:], ALU.add)
            for j in range(4):
                i = tb * 4 + j
                x_t = xt[i % 3]
                k.dma(x_t[:], T['x'][128 * i:128 * (i + 1), :])
                o_t = ot[i % 2]
                for half in range(2):
                    fb = ps[half]
                    for c in range(8):
                        k.mm(fb[:, :], m_g[:, c, 128 * j:128 * (j + 1)], Wout[:, c, 512 * half:512 * (half + 1)],
                             start=(c == 0), stop=(c == 7))
                    k.tt('dve', o_t[:, 512 * half:512 * (half + 1)], fb[:, :], x_t[:, 512 * half:512 * (half + 1)], ALU.add)
                k.dma(T['out'][128 * i:128 * (i + 1), :], o_t[:])

def fft_constants():
    C = {}
    n = NF
    s2 = np.arange(128, dtype=np.float64)[:, None]
    f2 = np.arange(128, dtype=np.float64)[None, :]
    th = 2 * np.pi * (f2 + 0.5) * s2 / 256.0
    C['FA1'] = np.concatenate([np.cos(th), -np.sin(th)], 1)
    th2 = 2 * np.pi * (f2 + 0.5) * (s2 + 128) / 256.0
    C['FA2'] = -np.concatenate([np.cos(th2), -np.sin(th2)], 1)
    s1 = np.arange(32, dtype=np.float64)
    tw = np.exp(-2j * np.pi * (np.arange(128)[None, :] + 0.5) * s1[:, None] / n)
    twq = np.tile(tw, (4, 1))
    C['TWa'] = np.concatenate([twq.real, twq.real], 1)
    C['TWb'] = np.concatenate([-twq.imag, twq.imag], 1)
    W = np.exp(-2j * np.pi * np.outer(s1, s1) / 32.0)
    Wq = np.kron(np.eye(4), W)
    C['WBr'] = Wq.real
    C['WBi'] = Wq.imag
    C['WBni'] = -Wq.imag
    Wi = np.exp(2j * np.pi * np.outer(s1, s1) / 32.0)
    Wiq = np.kron(np.eye(4), Wi)
    C['WI1'] = np.concatenate([Wiq.real, Wiq.imag], 1)
    C['WI2'] = np.concatenate([-Wiq.imag, Wiq.real], 1)
    twi = np.exp(2j * np.pi * (np.arange(128)[:, None] + 0.5) * s1[None, :] / n)
    twiq = np.tile(twi, (1, 4))
    C['TIa'] = np.concatenate([twiq.real, twiq.real], 1)
    C['TIb'] = np.concatenate([-twiq.imag, twiq.imag], 1)
    t2 = np.arange(128, dtype=np.float64)[None, :]
    f2c = np.arange(128, dtype=np.float64)[:, None]
    th3 = 2 * np.pi * (f2c + 0.5) * t2 / 256.0
    C['FIr'] = (2.0 / n) * np.cos(th3)
    C['FIi'] = -(2.0 / n) * np.sin(th3)
    return C


def filter_constants():
    C = {}
    f32 = np.float32
    t = np.linspace(0.0, 1.0, L, dtype=f32)[:, None]
    bands = 16
    f = np.linspace(1e-4, bands - 1, bands, dtype=f32)
    ang = (f32(2.0 * np.pi / L) * np.arange(L, dtype=f32)[:, None] * f[None, :]).astype(f32)
    z = np.concatenate([t, np.cos(ang).astype(f32), -np.sin(ang).astype(f32)], axis=-1).astype(f32)
    zs = np.zeros((128, L), f32)
    zs[0:33, :] = z.T
    zs[64:97, :] = z[::-1].T
    hi = zs.astype(ml_dtypes.bfloat16)
    lo = (zs - hi.astype(f32)).astype(ml_dtypes.bfloat16)
    C['zs_hi'] = hi
    C['zs_lo'] = lo
    tl = t[:, 0]
    tf = np.zeros((128, 2, 32), f32)
    pidx = np.arange(128)[:, None] * 32 + np.arange(32)[None, :]
    tf[:, 0, :] = tl[pidx]
    tf[:, 1, :] = tl[4095 - pidx]
    C['tfull'] = tf.reshape(128, 64)
    MIN_DECAY = math.log(1e-2) / 1.5
    MAX_DECAY = math.log(1e-2) / 0.3
    deltas = np.abs(np.linspace(MIN_DECAY, MAX_DECAY, HYW, dtype=f32)).astype(f32)
    C['negd'] = (-deltas)[None, :].astype(f32)
    return C

def _sin_layer(ph, W, pre_ps, fr, fb, out32):
    k = ph.k
    a, kk = W['a'], W['kk']
    k.ts('dve', a[:], pre_ps, fr, fb, op0=ALU.mult, op1=ALU.add)
    k.ts('dve', kk[:], a[:], 1.0 / (2 * math.pi), MAGIC, op0=ALU.mult, op1=ALU.add)
    k.ts('dve', kk[:], kk[:], -MAGIC, None, op0=ALU.add)
    k.stt(a[:], kk[:], -2 * math.pi, a[:], ALU.mult, ALU.add)
    k.ts('dve', a[:], a[:], -3.14159, 3.14159, op0=ALU.max, op1=ALU.min)
    k.act(out32, a[:], AF.Sin)


def _hilo(ph, hi, lo, src32, tmp32):
    k = ph.k
    k.copy('dve', hi, src32)
    k.copy('pool', tmp32, hi)
    k.tt('pool', lo, src32, tmp32, ALU.subtract)


def phase2a(nc, T):
    with Phase(nc, "p2a") as ph:
        k = ph.k
        ps = ph.ps
        zs_hi = ph.sb("zs_hi", [128, L], BF)
        zs_lo = ph.sb("zs_lo", [128, L], BF)
        W1 = ph.sb("W1", [128, 128], F32)
        W2 = ph.sb("W2", [128, 128], F32)
        W1h = ph.sb("W1h", [128, 128], BF)
        W1l = ph.sb("W1l", [128, 128], BF)
        W2h = ph.sb("W2h", [128, 128], BF)
        W2l = ph.sb("W2l", [128, 128], BF)
        wt = ph.sb("wt", [128, 128], F32)
        mv = ph.sb("mv", [128, 4], F32)
        fb = ph.sb("fb", [128, 2], F32)
        Wk = dict(a=ph.sb("a", [128, 512], F32), kk=ph.sb("kk", [128, 512], F32))
        h1 = ph.sb("h1", [128, 512], F32)
        h1h = ph.sb("h1h", [128, 512], BF)
        h1l = ph.sb("h1l", [128, 512], BF)
        t32 = ph.sb("t32", [128, 512], F32)
        h2 = ph.sb("h2", [128, 512], F32)
        h2b = [ph.sb(f"h2b{i}", [128, 512], BF) for i in range(2)]
        k.dma(zs_hi[:], T['zs_hi'][:, :])
        k.dma(zs_lo[:], T['zs_lo'][:, :])
        k.dma(W1[:], T['W1blk'][:, :])
        k.dma(W2[:], T['W2blk'][:, :])
        k.dma(mv[:], T['mlpv'][:, :])
        _hilo(ph, W1h[:], W1l[:], W1[:], wt[:])
        _hilo(ph, W2h[:], W2l[:], W2[:], wt[:])
        k.tt('dve', fb[:, 0:1], mv[:, 0:1], mv[:, 1:2], ALU.mult)
        k.tt('dve', fb[:, 1:2], mv[:, 2:3], mv[:, 3:4], ALU.mult)
        for cch in range(DBG.get('n2a', NB)):
            sl = slice(512 * cch, 512 * (cch + 1))
            b1 = ps[cch % 2]
            k.mm(b1[:, :], W1h[:], zs_hi[:, sl], start=True, stop=False)
            k.mm(b1[:, :], W1h[:], zs_lo[:, sl], start=False, stop=False)
            k.mm(b1[:, :], W1l[:], zs_hi[:, sl], start=False, stop=True)
            if DBG.get('s2a', 9) < 1: continue
            _sin_layer(ph, Wk, b1[:, :], mv[:, 0:1], fb[:, 0:1], h1[:])
            if DBG.get('s2a', 9) < 2: continue
            _hilo(ph, h1h[:], h1l[:], h1[:], t32[:])
            if DBG.get('s2a', 9) < 3: continue
            b2 = ps[2 + cch % 2]
            k.mm(b2[:, :], W2h[:], h1h[:], start=True, stop=False)
            k.mm(b2[:, :], W2h[:], h1l[:], start=False, stop=False)
            k.mm(b2[:, :], W2l[:], h1h[:], start=False, stop=True)
            _sin_layer(ph, Wk, b2[:, :], mv[:, 2:3], fb[:, 1:2], h2[:])
            hb_ = h2b[cch % 2]
            k.copy('pool', hb_[:], h2[:])
            k.dma(T['h2_d'][:, sl], hb_[:])


def _cmul_tab(ph, W, src, Ta, Tb, out_bf):
    k = ph.k
    P1, P2 = W
    sw = src.rearrange("p (r f) -> p r f", r=2)[:, ::-1, :]
    k.tt('dve', P1[:], src, Ta, ALU.mult)
    k.tt('dve', P2[:].rearrange("p (r f) -> p r f", r=2), sw, Tb.rearrange("p (r f) -> p r f", r=2), ALU.mult)
    k.tt('pool', out_bf, P1[:], P2[:], ALU.add)


def phase2b(nc, T):
    with Phase(nc, "p2b") as ph:
        k = ph.k
        ps = ph.ps
        ident = ph.sb("ident", [128, 128], BF)
        make_ident(ph, ident)
        cb16 = {}
        for nm, w in (('FA1', 256), ('FA2', 256), ('WBr', 128), ('WBi', 128), ('WBni', 128), ('WI1', 256), ('WI2', 256),
                      ('FIr', 128), ('FIi', 128)):
            cb16[nm] = ph.sb(nm, [128, w], BF)
            k.dma(cb16[nm][:], T[nm][:, :])
        c32 = {}
        for nm in ('TWa', 'TWb', 'TIa', 'TIb'):
            c32[nm] = ph.sb(nm, [128, 256], F32)
            k.dma(c32[nm][:], T[nm][:, :])
        h2s = ph.sb("h2s", [128, L], BF)
        k.dma(h2s[:], T['h2_d'][:, :])
        h2p = ph.sb("h2p", [128, 32, 128], BF)
        k.copy('pool', h2p[:], h2s[:].rearrange("q (p s) -> q s p", s=32))
        W3 = ph.sb("W3", [128, 2048], BF)
        load_w(ph, W3[:], T['W3blk'][:, :])
        wsh = ph.sb("wsh", [128, 12, 4], F32)
        k.dma(wsh[:].rearrange("p a b -> p (a b)"), T['wsh'][:, :])
        biasT = ph.sb("biasT", [128, 2, 128], F32)
        k.dma(biasT[:].rearrange("p a b -> p (a b)"), T['biasT'][:, :])
        negd = ph.sb("negd", [128, HYW], F32)
        k.dma(negd[:], T['negd'][0:1, :].partition_broadcast(128))
        tfull = ph.sb("tfull", [128, 2, 32], F32)
        k.dma(tfull[:].rearrange("p a b -> p (a b)"), T['tfull'][:, :])
        w_in_v = T['w_in'].rearrange("(k p) n -> p k n", p=128)

        hbuf = [ph.sb(f"hbuf{i}", [128, 8, 512], BF) for i in range(2)]
        ar = ph.sb("arena", [128, 24592], BF)
        raw = [ar[:, 4098 * i:4098 * (i + 1)] for i in range(3)]
        ub_ = [ar[:, 12294 + 4096 * i:12294 + 4096 * (i + 1)] for i in range(2)]
        Wblk = ar[:, 20486:24582].rearrange("p (k w c) -> p k w c", k=8, w=4)
        k_tm = ar[:, 0:8192].rearrange("p (o d c s) -> p o d c s", o=2, d=2, c=64)
        AB = ar[:, 8192:12288].rearrange("p (d c s) -> p d c s", d=2, c=64)
        Ksp = ar[:, 12288:20480].rearrange("p (o g r f) -> p o g r f", o=2, g=16, r=2)
        Gbuf = ar[:, 20480:24576].rearrange("p (r c s) -> p r c s", r=2, c=64)
        sz = ph.sb("sz", [128, L], BF)
        tm = [ph.sb(f"tm{i}", [128, 128, 32], BF) for i in range(3)]
        z2_tm = ph.sb("z2_tm", [128, 128, 32], BF)
        y_sc = ph.sb("y_sc", [128, 32, 128], BF)
        yzb = ph.sb("yzb", [128, L], BF)
        arg32 = ph.sb("arg32", [128, 4096], F32)
        PW = [(ph.sb(f"P1_{i}", [128, 256], F32), ph.sb(f"P2_{i}", [128, 256], F32)) for i in range(2)]
        Zp = [ph.sb(f"Zp{i}", [128, 256], BF) for i in range(2)]
        Yb = [ph.sb(f"Yb{i}", [128, 256], BF) for i in range(2)]
        Kev = [ph.sb(f"Kev{i}", [128, 256], BF) for i in range(2)]
        Esb = [[ph.sb(f"E{st}_{i}", [128, 256], BF) for i in range(2)] for st in range(3)]
        cols = (COL_V, COL_X1, COL_X2, COL_ZH)
        cnt = [0]

        PWs = [[(ph.sb(f"P1_{st}_{i}", [128, 512], F32), ph.sb(f"P2_{st}_{i}", [128, 512], F32)) for i in range(2)]
               for st in range(3)]
        Zp2 = [ph.sb(f"Zq{i}", [128, 512], BF) for i in range(2)]
        Yb2 = [ph.sb(f"Yq{i}", [128, 512], BF) for i in range(2)]

        def v4(ap):
            return ap.rearrange("p (u r f) -> p u r f", u=2, r=2)

        def tab4(t):
            return bcast(t.rearrange("p (r f) -> p r f", r=2), 1, 2)

        def skewed(G, stages):
            ns = len(stages)
            for t in range(G + ns - 1):
                for s_ in reversed(range(ns)):
                    g = t - s_
                    if 0 <= g < G:
                        stages[s_](g)

        def cmul_pair(W, bank, Ta, Tb, out512):
            P1, P2 = W
            k.tt('dve', v4(P1[:]), v4(bank), tab4(Ta), ALU.mult)
            k.tt('dve', v4(P2[:]), v4(bank)[:, :, ::-1, :], tab4(Tb), ALU.mult)
            k.tt('pool', out512, P1[:], P2[:], ALU.add)

        def st_za(lhs_of):
            def f(q):
                bank = ps[q % 2]
                for u in range(2):
                    l1, l2 = lhs_of(2 * q + u)
                    za = bank[:, 256 * u:256 * (u + 1)]
                    k.mm(za, l1, cb16['FA1'][:], start=True, stop=(l2 is None))
                    if l2 is not None:
                        k.mm(za, l2, cb16['FA2'][:], start=False, stop=True)
            return f

        def st_tw(q):
            cmul_pair(PWs[0][q % 2], ps[q % 2][:, :], c32['TWa'][:], c32['TWb'][:], Zp2[q % 2][:])

        def st_ub(q):
            bank = ps[2 + q % 2]
            z_p = Zp2[q % 2]
            for u in range(2):
                ub = bank[:, 256 * u:256 * (u + 1)]
                zr = z_p[:, 256 * u:256 * u + 128]
                zi = z_p[:, 256 * u + 128:256 * (u + 1)]
                k.mm(ub[:, 0:128], cb16['WBr'][:], zr, start=True, stop=False)
                k.mm(ub[:, 0:128], cb16['WBni'][:], zi, start=False, stop=True)
                k.mm(ub[:, 128:256], cb16['WBi'][:], zr, start=True, stop=False)
                k.mm(ub[:, 128:256], cb16['WBr'][:], zi, start=False, stop=True)

        for cb in range(DBG.get('ncb', 4)):
            for w in range(4):
                load_w(ph, Wblk[:, :, w, :], w_in_v[:, :, cols[w] + 128 * cb:cols[w] + 128 * (cb + 1)])
            for w in range(3):
                k.memset('pool', raw[w][:, 0:1], 0.0)
                k.memset('pool', raw[w][:, 4097:4098], 0.0)
            for tb in range(NB):
                hb = hbuf[tb % 2]
                k.dma(hb[:].rearrange("p c t -> p (c t)"), T['hT_d'][tb])
                for w in range(4):
                    bank = ps[(tb * 4 + w) % 2]
                    for c in range(8):
                        k.mm(bank[:, :], Wblk[:, c, w, :], hb[:, c, :], start=(c == 0), stop=(c == 7))
                    if w < 3:
                        k.act(raw[w][:, 1 + 512 * tb:1 + 512 * (tb + 1)], bank[:, :], AF.Copy)
                    else:
                        k.act(sz[:, 512 * tb:512 * (tb + 1)], bank[:, :], AF.Silu)
            if DBG.get('s2b', 9) < 2: continue
            for w in range(3):
                u = ub_[w % 2]
                j = 4 * w + cb
                k.ts('dve', u, raw[w][:, 1:4097], wsh[:, j, 1:2], wsh[:, j, 3:4], op0=ALU.mult, op1=ALU.add)
                k.stt(u, raw[w][:, 0:4096], wsh[:, j, 0:1], u, ALU.mult, ALU.add)
                k.stt(u, raw[w][:, 2:4098], wsh[:, j, 2:3], u, ALU.mult, ALU.add)
                for a in range(4):
                    pv = ps[2 + a % 2][:, :].bitcast(BF)
                    for e in range(8):
                        s1 = 8 * a + e
                        k.tr(pv[:, 128 * e:128 * (e + 1)], u[:, s1:4096:32], ident[:])
                    k.copy('dve', tm[w][:, :, 8 * a:8 * a + 8].rearrange("p c s -> p s c"),
                           pv.rearrange("p (s c) -> p s c", s=8))
            if DBG.get('dump_tm'):
                k.dma(T['dbg_tm'][:, :], tm[DBG['dump_tm'] - 1][:].rearrange("p c s -> p (c s)"))
            if DBG.get('s2b', 9) < 3: continue
            for hbk in range(DBG.get('nhbk', 2)):
                c0 = 64 * hbk
                gcol = 128 * cb + c0
                k.tt('dve', arg32[:].rearrange("p (d c s) -> p d c s", d=2, c=64),
                     bcast(bcast(negd[:, gcol:gcol + 64], 1, 2), 3, 32),
                     bcast(tfull[:, :, :], 2, 64), ALU.mult)
                if DBG.get('s3', 9) < 2: continue
                k.act(AB.rearrange("p d c s -> p (d c s)"), arg32[:], AF.Exp)
                if DBG.get('s3', 9) < 3: continue
                wc0 = 256 * (2 * cb + hbk)
                for s1 in range(32):
                    kb_ = ps[6 + s1 % 2][:, 0:256]
                    k.mm(kb_, h2p[:, s1, :], W3[:, wc0:wc0 + 256])
                    if DBG.get('s3', 9) < 4: continue
                    abv = AB[:, :, :, s1].rearrange("p d c -> p (d c)")
                    k.tt('dve', k_tm[:, :, :, :, s1].rearrange("p o d c -> p o (d c)"),
                         kb_.rearrange("p (o x) -> p o x", o=2), bcast(abv, 1, 2), ALU.mult)
                if DBG.get('s2b', 9) < 4: continue
                def spec_lhs(gi):
                    o, g = divmod(gi, 16)
                    return (k_tm[:, o, 0, 4 * g:4 * g + 4, :].rearrange("p c s -> p (c s)"),
                            k_tm[:, o, 1, 4 * g:4 * g + 4, :].rearrange("p c s -> p (c s)"))

                def st_kev(q):
                    o, gp = divmod(q, 8)
                    k.copy('dve', Ksp[:, o, 2 * gp:2 * gp + 2, :, :].rearrange("p g r f -> p (g r f)"), ps[2 + q % 2][:, :])

                skewed(16, [st_za(spec_lhs), st_tw, st_ub, st_kev])
                for o in range(2):
                    gg0 = gcol // 4
                    k.tt('pool', Ksp[:, o, :, 0, :], Ksp[:, o, :, 0, :], bcast(biasT[:, o, gg0:gg0 + 16], 2, 128), ALU.add)
                if DBG.get('dump_k'):
                    k.dma(T['dbg_k'][:, :], ar[:, 0:8192])
                    k.dma(T['dbg_ks'][:, :], ar[:, 12288:20480])
                if DBG.get('s2b', 9) < 5: continue
                for o in range(DBG.get('nord', 2)):
                    src = tm[0] if o == 0 else z2_tm
                    gate = tm[1] if o == 0 else tm[2]
                    def conv_lhs(g, src=src):
                        return (src[:, c0 + 4 * g:c0 + 4 * g + 4, :].rearrange("p c s -> p (c s)"), None)

                    def st_mul(q, o=o):
                        bank = ps[2 + q % 2][:, :]
                        P1, P2 = PWs[1][q % 2]
                        kr_ = bcast(Ksp[:, o, 2 * q:2 * q + 2, 0, :], 2, 2)
                        ki_ = bcast(Ksp[:, o, 2 * q:2 * q + 2, 1, :], 2, 2)
                        k.tt('dve', v4(P1[:]), v4(bank), kr_, ALU.mult)
                        k.tt('dve', v4(P2[:]), v4(bank)[:, :, ::-1, :], ki_, ALU.mult)
                        y_b = Yb2[q % 2]
                        k.tt('pool', v4(y_b[:])[:, :, 0, :], v4(P1[:])[:, :, 0, :], v4(P2[:])[:, :, 0, :], ALU.subtract)
                        k.tt('pool', v4(y_b[:])[:, :, 1, :], v4(P1[:])[:, :, 1, :], v4(P2[:])[:, :, 1, :], ALU.add)

                    def st_gb(q):
                        bank = ps[4 + q % 2]
                        y_b = Yb2[q % 2]
                        for u in range(2):
                            gb = bank[:, 256 * u:256 * (u + 1)]
                            k.mm(gb, y_b[:, 256 * u:256 * u + 128], cb16['WI1'][:], start=True, stop=False)
                            k.mm(gb, y_b[:, 256 * u + 128:256 * (u + 1)], cb16['WI2'][:], start=False, stop=True)

                    def st_itw(q):
                        bank = ps[4 + q % 2][:, :]
                        P1, P2 = PWs[2][q % 2]
                        k.tt('dve', v4(P1[:]), v4(bank), tab4(c32['TIa'][:]), ALU.mult)
                        k.tt('dve', v4(P2[:]), v4(bank)[:, :, ::-1, :], tab4(c32['TIb'][:]), ALU.mult)
                        k.tt('pool', Gbuf[:, :, 8 * q:8 * q + 8, :].rearrange("p r (u c) s -> p r u (c s)", u=2),
                             v4(P1[:]).rearrange("p u r f -> p r u f"), v4(P2[:]).rearrange("p u r f -> p r u f"), ALU.add)

                    skewed(8, [st_za(conv_lhs), st_tw, st_ub, st_mul, st_gb, st_itw])
                    for cc in range(4):
                        yb = ps[6 + cc % 2]
                        k.mm(yb[:, :], cb16['FIr'][:], Gbuf[:, 0, 16 * cc:16 * cc + 16, :].rearrange("p c s -> p (c s)"),
                             start=True, stop=False)
                        k.mm(yb[:, :], cb16['FIi'][:], Gbuf[:, 1, 16 * cc:16 * cc + 16, :].rearrange("p c s -> p (c s)"),
                             start=False, stop=True)
                        cs = slice(c0 + 16 * cc, c0 + 16 * cc + 16)
                        if o == 0:
                            k.tt('dve', z2_tm[:, cs, :], yb[:, :].rearrange("p (c s) -> p c s", c=16), gate[:, cs, :], ALU.mult)
                        else:
                            k.tt('dve', y_sc[:, :, cs].rearrange("p s c -> p c s"),
                                 yb[:, :].rearrange("p (c s) -> p c s", c=16), gate[:, cs, :], ALU.mult)
            if DBG.get('dump_z2'):
                k.dma(T['dbg_tm'][:, :], z2_tm[:].rearrange("p c s -> p (c s)"))
            if DBG.get('s2b', 9) < 6: continue
            for a in range(4):
                pv = ps[2 + a % 2][:, :].bitcast(BF)
                for e in range(8):
                    k.tr(pv[:, 128 * e:128 * (e + 1)], y_sc[:, 8 * a + e, :], ident[:])
                k.tt('dve', yzb[:].rearrange("c (p s) -> c s p", s=32)[:, 8 * a:8 * a + 8, :],
                     pv.rearrange("c (s p) -> c s p", s=8),
                     sz[:].rearrange("c (p s) -> c s p", s=32)[:, 8 * a:8 * a + 8, :], ALU.mult)
            for tb in range(NB):
                k.dma(T['yz_d'][tb][:, 512 * cb:512 * (cb + 1)], yzb[:, 512 * tb:512 * (tb + 1)])


def phase2(nc, T):
    if 'a' in DBG.get('p2', 'ab'):
        phase2a(nc, T)
    if 'b' in DBG.get('p2', 'ab'):
        phase2b(nc, T)

def _bf(a):
    return np.asarray(a, np.float32).astype(ml_dtypes.bfloat16)


_CONST_CACHE = {}


def host_constants():
    if _CONST_CACHE:
        return _CONST_CACHE
    C = {}
    pos = np.arange(L, dtype=np.float32)
    inv_freq = (np.float32(10000.0) ** (-np.arange(0, 32, 2, dtype=np.float32) / np.float32(32))).astype(np.float32)
    ang = (pos[:, None] * inv_freq[None, :]).astype(np.float32)
    C['cosT'] = np.ascontiguousarray(np.cos(ang).astype(np.float32).reshape(NT, 128, 16).transpose(1, 0, 2).reshape(128, NT * 16))
    C['sinT'] = np.ascontiguousarray(np.sin(ang).astype(np.float32).reshape(NT, 128, 16).transpose(1, 0, 2).reshape(128, NT * 16))
    F = fft_constants()
    for nm in ('FA1', 'FA2', 'WBr', 'WBi', 'WBni', 'WI1', 'WI2', 'FIr', 'FIi'):
        C[nm] = np.ascontiguousarray(_bf(F[nm]))
    for nm in ('TWa', 'TWb', 'TIa', 'TIb'):
        C[nm] = np.ascontiguousarray(F[nm].astype(np.float32))
    C.update(filter_constants())
    _CONST_CACHE.update(C)
    return _CONST_CACHE


def prep_inputs(inp, b):
    f32 = np.float32
    m = {}
    m['x'] = np.ascontiguousarray(inp['x'][b], dtype=f32)
    m['w_in'] = np.ascontiguousarray(inp['w_in'][0], dtype=f32)
    m['gT'] = np.ascontiguousarray(inp['g_norm'][0].reshape(8, 128).T, dtype=f32)
    m['bgT'] = np.ascontiguousarray(inp['b_gate'][0].reshape(16, 128).T, dtype=f32)
    m['w_uq'] = np.ascontiguousarray(inp['w_uq'][0], dtype=f32)
    m['w_ukv'] = np.ascontiguousarray(inp['w_ukv'][0], dtype=f32)
    m['gcqT'] = np.ascontiguousarray(inp['g_cq'][0].reshape(3, 128).T, dtype=f32)
    m['gckvT'] = np.ascontiguousarray(inp['g_ckv'][0].reshape(2, 128).T, dtype=f32)
    m['gqk'] = np.ascontiguousarray(np.concatenate([inp['g_qn'][0], inp['g_kn'][0]])[None, :], dtype=f32)
    m['w_attn_out'] = np.ascontiguousarray(inp['w_attn_out'][0], dtype=f32)
    m['w_hy_out'] = np.ascontiguousarray(inp['w_hy_out'][0], dtype=f32)
    m['w_out'] = np.ascontiguousarray(inp['w_out'][0], dtype=f32)
    wsh = np.zeros((128, 12, 4), f32)
    wsh[:, :, 0:3] = inp['w_short'][0].reshape(3, 12, 128).transpose(2, 1, 0)
    wsh[:, :, 3] = inp['b_short'][0].reshape(12, 128).T
    m['wsh'] = wsh.reshape(128, 48)
    hb = inp['hy_bias'][0]
    bT = hb.reshape(2, 128, 4).transpose(2, 0, 1)
    m['biasT'] = np.ascontiguousarray(np.repeat(bT[:, None], 32, axis=1).reshape(128, 256), dtype=f32)
    W1 = np.zeros((128, 128), f32)
    W1[0:33, 0:64] = inp['w_f1'][0]
    W1[64:97, 64:128] = inp['w_f1'][0]
    m['W1blk'] = W1
    W2 = np.zeros((128, 128), f32)
    W2[0:64, 0:64] = inp['w_f2'][0]
    W2[64:128, 64:128] = inp['w_f2'][0]
    m['W2blk'] = W2
    mv = np.zeros((128, 4), f32)
    for jj, nm in enumerate(('freq_1', 'b_f1', 'freq_2', 'b_f2')):
        mv[0:64, jj] = inp[nm][0]
        mv[64:128, jj] = inp[nm][0]
    m['mlpv'] = mv
    w3 = inp['w_f3'][0].reshape(64, 2, 2, 8, 64)
    W3 = np.zeros((128, 8, 2, 2, 64), f32)
    for dd in range(2):
        W3[64 * dd:64 * (dd + 1), :, :, dd, :] = w3[:, :, dd, :, :].transpose(0, 2, 1, 3)
    m['W3blk'] = W3.reshape(128, 2048)
    C = host_constants()
    for nm in CONST_NAMES:
        m[nm] = C[nm]
    return m


IN_SHAPES = {
    'x': ([L, D], F32), 'w_in': ([D, 5280], F32), 'gT': ([128, 8], F32), 'bgT': ([128, 16], F32),
    'w_uq': ([384, 768], F32), 'w_ukv': ([256, 1024], F32), 'gcqT': ([128, 3], F32), 'gckvT': ([128, 2], F32),
    'gqk': ([1, 192], F32), 'w_attn_out': ([512, D], F32), 'w_hy_out': ([512, D], F32), 'w_out': ([D, D], F32),
    'cosT': ([128, NT * 16], F32), 'sinT': ([128, NT * 16], F32),
    'wsh': ([128, 48], F32), 'biasT': ([128, 256], F32), 'W1blk': ([128, 128], F32), 'W2blk': ([128, 128], F32),
    'mlpv': ([128, 4], F32), 'W3blk': ([128, 2048], F32),
    'FA1': ([128, 256], BF), 'FA2': ([128, 256], BF), 'WBr': ([128, 128], BF), 'WBi': ([128, 128], BF),
    'WBni': ([128, 128], BF), 'WI1': ([128, 256], BF), 'WI2': ([128, 256], BF), 'FIr': ([128, 128], BF),
    'FIi': ([128, 128], BF), 'TWa': ([128, 256], F32), 'TWb': ([128, 256], F32), 'TIa': ([128, 256], F32),
    'TIb': ([128, 256], F32), 'zs_hi': ([128, L], BF), 'zs_lo': ([128, L], BF), 'tfull': ([128, 64], F32),
    'negd': ([1, HYW], F32),
}
CONST_NAMES = ('cosT', 'sinT', 'FA1', 'FA2', 'WBr', 'WBi', 'WBni', 'WI1', 'WI2', 'FIr', 'FIi', 'TWa', 'TWb', 'TIa', 'TIb',
               'zs_hi', 'zs_lo', 'tfull', 'negd')


def build_nc(debug=None):
    debug = debug or set()
    nc = bass.Bass("TRN2", target_bir_lowering=False)
    T = {}
    for name, (shape, dt) in IN_SHAPES.items():
        T[name] = nc.dram_tensor(name, shape, dt, kind="ExternalInput").ap()
    T['out'] = nc.dram_tensor("out", [L, D], F32, kind="ExternalOutput").ap()
    skind = dict(kind="ExternalOutput") if 'dump' in debug else {}
    T['hT_d'] = nc.dram_tensor("hT_d", [NB, 128, 8 * 512], BF, **skind).ap()
    T['at_d'] = nc.dram_tensor("at_d", [NB, 64, 8 * 512], BF, **skind).ap()
    T['h2_d'] = nc.dram_tensor("h2_d", [128, L], BF, **skind).ap()
    if 'dump' in debug:
        T['dbg_tm'] = nc.dram_tensor("dbg_tm", [128, 4096], BF, kind="ExternalOutput").ap()
        T['dbg_k'] = nc.dram_tensor("dbg_k", [128, 8192], BF, kind="ExternalOutput").ap()
        T['dbg_ks'] = nc.dram_tensor("dbg_ks", [128, 8192], BF, kind="ExternalOutput").ap()
    if 'yz_in' in debug:
        T['yz_d'] = nc.dram_tensor("yz_d", [NB, 128, 4 * 512], BF, kind="ExternalInput").ap()
    else:
        T['yz_d'] = nc.dram_tensor("yz_d", [NB, 128, 4 * 512], BF, **skind).ap()
    phases = debug & {'p1', 'p2', 'p3', 'p4'} or {'p1', 'p2', 'p3', 'p4'}
    if 'p1' in phases:
        phase1(nc, T)
    if 'p2' in phases and 'yz_in' not in debug:
        phase2(nc, T)
    if 'p3' in phases:
        phase3(nc, T)
    if 'p4' in phases:
        phase4(nc, T)
    return nc


def kernel(**inputs):
    inp = {k_: np.asarray(v) for k_, v in inputs.items()}
    nc = build_nc()
    in_maps = [prep_inputs(inp, b) for b in range(8)]
    res = run_bass_kernel_spmd(nc, in_maps, core_ids=list(range(8)))
    out = np.stack([np.asarray(r['out'], dtype=np.float32) for r in res.results], axis=0)
    return out
```

```python
import concourse.bass as bass
import concourse.mybir as mybir

_ESZ = {}


def _esize(dt):
    s = _ESZ.get(dt)
    if s is None:
        n = str(dt)
        if '32' in n:
            s = 4
        elif '16' in n:
            s = 2
        elif '8' in n:
            s = 1
        else:
            s = 4
        _ESZ[dt] = s
    return s


def footprint(ap):
    t = ap.tensor
    name = t.name
    es = _esize(ap.dtype)
    apl = ap.ap
    off = int(ap.offset) * es
    space = str(type(t).__name__)
    if 'DRam' in space:
        lo = off
        hi = off
        for st, cnt in apl:
            if cnt > 1:
                d = (cnt - 1) * st * es
                if d > 0:
                    hi += d
                else:
                    lo += d
        return (name, 0, 1, lo, hi + es)
    pstep, pcnt = apl[0]
    pstep_b = pstep * es
    if pstep_b > 0:
        p0 = off // pstep_b
        f0 = off % pstep_b
    else:
        p0 = 0
        f0 = off
    lo = f0
    hi = f0
    for st, cnt in apl[1:]:
        if cnt > 1:
            d = (cnt - 1) * st * es
            if d > 0:
                hi += d
            else:
                lo += d
    return (name, p0, p0 + pcnt, lo, hi + es)


COMPUTE = ('pe', 'act', 'dve', 'pool')
QUEUES = ('pe', 'act', 'dve', 'pool', 'sp')
QIDX = {q: i for i, q in enumerate(QUEUES)}


class _Op:
    __slots__ = ('q', 'fn', 'dma', 'idx', 'gid', 'waits_c', 'waits_d', 'signal', 'snap', 'slot', 'slot_cnt', 'prev_slot')


class Prog:
    def __init__(self, nc, dma_slots=None):
        self.nc = nc
        self.streams = {q: [] for q in QUEUES}
        self.recs = {}
        self.known = {q: [-1] * len(QUEUES) for q in QUEUES}
        self.known_dma = {q: set() for q in QUEUES}
        self.ops = []
        self.dma_slots = dma_slots or {'sp': 8, 'pool': 4, 'act': 4}
        self.dma_count = {q: 0 for q in QUEUES}
        self.dma_ops = {q: [] for q in QUEUES}
        self.n_comp = {q: 0 for q in QUEUES}

    def add(self, q, fn, reads=(), writes=(), dma=False):
        op = _Op()
        op.q = q
        op.fn = fn
        op.dma = dma
        op.gid = len(self.ops)
        op.signal = dma
        op.waits_c = []
        op.waits_d = []
        op.slot = None
        op.prev_slot = None
        stream = self.streams[q]
        if not dma:
            op.idx = self.n_comp[q]
            self.n_comp[q] += 1
        else:
            op.idx = -1
        deps_c = {}
        deps_d = set()

        def scan(fp, is_write):
            name, p0, p1, f0, f1 = fp
            lst = self.recs.get(name)
            if not lst:
                return
            for r in lst:
                (rp0, rp1, rf0, rf1, rw, rop) = r
                if not (is_write or rw):
                    continue
                if rp1 <= p0 or p1 <= rp0 or rf1 <= f0 or f1 <= rf0:
                    continue
                if rop.dma:
                    deps_d.add(rop)
                else:
                    e = rop.q
                    if deps_c.get(e, -1) < rop.idx:
                        deps_c[e] = rop.idx

        rfps = [footprint(a) for a in reads]
        wfps = [footprint(a) for a in writes]
        for fp in rfps:
            scan(fp, False)
        for fp in wfps:
            scan(fp, True)
        known = self.known[q]
        kd = self.known_dma[q]
        for e, i in deps_c.items():
            ei = QIDX[e]
            if e == q and not dma:
                if q == 'pe':
                    continue
            if i <= known[ei]:
                continue
            op.waits_c.append((e, i))
            src = self.comp_ops[e][i]
            src.signal = True
            known[ei] = i
            for k, v in enumerate(src.snap):
                if v > known[k]:
                    known[k] = v
        for d in sorted(deps_d, key=lambda o: o.gid):
            if d.gid in kd:
                continue
            op.waits_d.append(d)
            kd.add(d.gid)
            for k, v in enumerate(d.snap):
                if v > known[k]:
                    known[k] = v
        if dma:
            n = self.dma_count[q]
            R = self.dma_slots[q]
            op.slot = n % R
            op.slot_cnt = n // R + 1
            if n >= R:
                prev = self.dma_ops[q][n - R]
                op.prev_slot = prev
                kd.add(prev.gid)
            self.dma_count[q] = n + 1
            self.dma_ops[q].append(op)
        op.snap = tuple(known)
        if not dma:
            self.comp_ops[q].append(op)
        for fp, is_write in [(f, False) for f in rfps] + [(f, True) for f in wfps]:
            name, p0, p1, f0, f1 = fp
            lst = self.recs.setdefault(name, [])
            if is_write:
                lst[:] = [r for r in lst if not (r[0] >= p0 and r[1] <= p1 and r[2] >= f0 and r[3] <= f1)]
            else:
                if not dma:
                    lst[:] = [r for r in lst if not (r[4] is False and (not r[5].dma) and r[5].q == q
                                                     and r[0] == p0 and r[1] == p1 and r[2] == f0 and r[3] == f1)]
            lst.append((p0, p1, f0, f1, is_write, op))
        stream.append(op)
        self.ops.append(op)
        return op

    comp_ops = None

    def start(self):
        self.comp_ops = {q: [] for q in QUEUES}

    def pe(self, fn, reads, writes):
        return self.add('pe', fn, reads, writes)

    def act(self, fn, reads, writes):
        return self.add('act', fn, reads, writes)

    def dve(self, fn, reads, writes):
        return self.add('dve', fn, reads, writes)

    def pool(self, fn, reads, writes):
        return self.add('pool', fn, reads, writes)

    def dma(self, out, in_, q='sp', **kw):
        return self.add(q, lambda e: e.dma_start(out=out, in_=in_, **kw), [in_], [out], dma=True)

    def emit(self, block, sems_c, sems_d):
        cum = {}
        for e in QUEUES:
            c = 0
            arr = []
            for o in self.comp_ops[e]:
                if o.signal:
                    c += 1
                arr.append(c)
            cum[e] = arr
        self.cum = cum

        def gen(q):
            def body(eng):
                for o in self.streams[q]:
                    for (e, i) in o.waits_c:
                        eng.wait_ge(sems_c[e], cum[e][i])
                    for d in o.waits_d:
                        eng.wait_ge(sems_d[d.q][d.slot], 16 * d.slot_cnt)
                    if o.prev_slot is not None:
                        p = o.prev_slot
                        eng.wait_ge(sems_d[p.q][p.slot], 16 * p.slot_cnt)
                    ins = o.fn(eng)
                    if o.dma:
                        ins.then_inc(sems_d[q][o.slot], 16)
                    elif o.signal:
                        ins.then_inc(sems_c[q], 1)
                R = self.dma_slots.get(q, 0)
                n = self.dma_count[q]
                for o in self.dma_ops[q][max(0, n - R):]:
                    eng.wait_ge(sems_d[q][o.slot], 16 * o.slot_cnt)
            return body

        if self.streams['pe']:
            block.tensor(gen('pe'))
        if self.streams['act']:
            block.scalar(gen('act'))
        if self.streams['dve']:
            block.vector(gen('dve'))
        if self.streams['pool']:
            block.gpsimd(gen('pool'))
        if self.streams['sp']:
            block.sync(gen('sp'))

import math
from contextlib import ExitStack
import numpy as np
import ml_dtypes
from concourse.bass_utils import run_bass_kernel_spmd

F32 = mybir.dt.float32
BF = mybir.dt.bfloat16
AF = mybir.ActivationFunctionType
ALU = mybir.AluOpType
AX = mybir.AxisListType

L = 4096
D = 1024
NT = 32
NB = 8
EPS = 1e-6
NF = 8192
HYW = 512
COL_V, COL_X1, COL_X2, COL_ZH = 0, 512, 1024, 1536
COL_CQ, COL_CKV, COL_KR, COL_ZA = 2048, 2432, 2688, 2720
COL_GH, COL_GA = 3232, 4256
MAGIC = 12582912.0
DBG = {}


def bcast(ap, axis, n):
    a = ap.unsqueeze(axis)
    shp = list(a.shape)
    shp[axis] = n
    return a.to_broadcast(shp)


class K:
    def __init__(self, P):
        self.P = P

    def mm(self, out, lhsT, rhs, start=True, stop=True):
        self.P.pe(lambda e: e.matmul(out, lhsT=lhsT, rhs=rhs, start=start, stop=stop), [lhsT, rhs], [out])

    def tr(self, out, in_, ident):
        self.P.pe(lambda e: e.transpose(out=out, in_=in_, identity=ident), [in_, ident], [out])

    def act(self, out, in_, func, bias=None, scale=None, accum_out=None):
        kw = {}
        reads = [in_]
        writes = [out]
        if bias is not None:
            kw['bias'] = bias
            if not isinstance(bias, (int, float)):
                reads.append(bias)
        if scale is not None:
            kw['scale'] = scale
            if not isinstance(scale, (int, float)):
                reads.append(scale)
        if accum_out is not None:
            kw['accum_out'] = accum_out
            writes.append(accum_out)
        self.P.act(lambda e: e.activation(out=out, in_=in_, func=func, **kw), reads, writes)

    def tt(self, eng, out, in0, in1, op):
        self.P.add(eng, lambda e: e.tensor_tensor(out=out, in0=in0, in1=in1, op=op), [in0, in1], [out])

    def ts(self, eng, out, in0, s1, s2=None, op0=ALU.mult, op1=None):
        reads = [in0]
        if not isinstance(s1, (int, float)):
            reads.append(s1)
        if s2 is not None and not isinstance(s2, (int, float)):
            reads.append(s2)
        if op1 is None:
            self.P.add(eng, lambda e: e.tensor_scalar(out=out, in0=in0, scalar1=s1, scalar2=None, op0=op0), reads, [out])
        else:
            self.P.add(eng, lambda e: e.tensor_scalar(out=out, in0=in0, scalar1=s1, scalar2=s2, op0=op0, op1=op1), reads, [out])

    def stt(self, out, in0, scalar, in1, op0, op1):
        reads = [in0, in1]
        if not isinstance(scalar, (int, float)):
            reads.append(scalar)
        self.P.dve(lambda e: e.scalar_tensor_tensor(out=out, in0=in0, scalar=scalar, in1=in1, op0=op0, op1=op1), reads, [out])

    def copy(self, eng, out, in_):
        if eng == 'act':
            self.act(out, in_, AF.Copy)
        else:
            self.P.add(eng, lambda e: e.tensor_copy(out=out, in_=in_), [in_], [out])

    def recip(self, out, in_):
        self.P.dve(lambda e: e.reciprocal(out=out, in_=in_), [in_], [out])

    def reduce_add(self, out, in_):
        self.P.dve(lambda e: e.tensor_reduce(out=out, in_=in_, axis=AX.X, op=ALU.add), [in_], [out])

    def memset(self, eng, ap, val):
        self.P.add(eng, lambda e: e.memset(ap, val), [], [ap])

    def dma(self, out, in_, q='sp'):
        self.P.dma(out, in_, q=q)


def _dump(P):
    cum = {}
    for e in QUEUES:
        c = 0
        arr = []
        for o in P.comp_ops[e]:
            if o.signal:
                c += 1
            arr.append(c)
        cum[e] = arr
    for q in QUEUES:
        print("== stream", q)
        for o in P.streams[q]:
            w = [f"{e}>={cum[e][i]}(op{i})" for e, i in o.waits_c] + [f"dma[{d.q}{d.slot}]>={16*d.slot_cnt}" for d in o.waits_d]
            if o.prev_slot is not None:
                w.append(f"prev dma[{o.prev_slot.q}{o.prev_slot.slot}]>={16*o.prev_slot.slot_cnt}")
            tag = f"DMA slot{o.slot} cnt{o.slot_cnt}" if o.dma else (f"op{o.idx} sig={cum[q][o.idx] if o.signal else '-'}")
            print("   ", tag, getattr(o, 'desc', ''), "waits:", w)


class Phase:
    def __init__(self, nc, name):
        self.nc = nc
        self.name = name
        self.es = ExitStack()

    def __enter__(self):
        nc = self.nc
        es = self.es
        es.__enter__()
        self.ps = [es.enter_context(nc.psum_tensor(f"{self.name}_ps{i}", [128, 512], F32)) for i in range(8)]
        self.sems_c = {e: es.enter_context(nc.semaphore(f"{self.name}_sc_{e}")) for e in QUEUES}
        self.sems_d = {q: [es.enter_context(nc.semaphore(f"{self.name}_sd_{q}{i}")) for i in range(n)]
                       for q, n in (('sp', 8), ('pool', 4), ('act', 4))}
        self.P = Prog(nc)
        self.P.start()
        self.k = K(self.P)
        return self

    def sb(self, name, shape, dt):
        return self.es.enter_context(self.nc.sbuf_tensor(f"{self.name}_{name}", shape, dt))

    def __exit__(self, *a):
        if a[0] is None:
            self.es.enter_context(self.nc.allow_low_precision("bf16 operands / intermediates by design"))
            block = self.es.enter_context(self.nc.Block())
            if DBG.get('dump') == self.name:
                _dump(self.P)
            self.P.emit(block, self.sems_c, self.sems_d)
        return self.es.__exit__(*a)


def make_ident(ph, ident):
    identf = ph.sb("identf", [128, 128], F32)
    ph.k.memset('pool', identf[:], 0.0)
    ph.P.pool(lambda e: e.affine_select(out=identf[:], in_=identf[:], pattern=[[-1, 128]], compare_op=ALU.not_equal,
                                        fill=1.0, base=0, channel_multiplier=1), [identf[:]], [identf[:]])
    ph.k.copy('dve', ident[:], identf[:])


def load_w(ph, dst, src_ap):
    ph.k.dma(dst, src_ap, q='pool')


def rstd_from_ss(k, rs_col, ss_col, n):
    k.act(rs_col, ss_col, AF.Sqrt, bias=EPS, scale=1.0 / n)
    k.recip(rs_col, rs_col)


def phase1(nc, T):
    with Phase(nc, "p1") as ph:
        k = ph.k
        xt = [ph.sb(f"xt{i}", [128, D], F32) for i in range(3)]
        xn = [ph.sb(f"xn{i}", [128, D], BF) for i in range(2)]
        junk = ph.sb("junk", [128, D], BF)
        ss = ph.sb("ss", [128, NT], F32)
        rs = ph.sb("rs", [128, NT], F32)
        gT = ph.sb("gT", [128, 8], F32)
        ident = ph.sb("ident", [128, 128], BF)
        hb = [ph.sb(f"hb{i}", [128, 8, 512], BF) for i in range(2)]
        make_ident(ph, ident)
        k.dma(gT[:], T['gT'][:, :])
        for i in range(NT):
            x_t = xt[i % 3]
            k.dma(x_t[:], T['x'][128 * i:128 * (i + 1), :])
            k.act(junk[:], x_t[:], AF.Square, accum_out=ss[:, i:i + 1])
            rstd_from_ss(k, rs[:, i:i + 1], ss[:, i:i + 1], D)
            x_n = xn[i % 2]
            k.ts('dve', x_n[:], x_t[:], rs[:, i:i + 1])
            bank = ph.ps[i % 4]
            pv = bank[:, :].bitcast(BF)
            for c in range(8):
                k.tr(pv[:, 128 * c:128 * (c + 1)], x_n[:, 128 * c:128 * (c + 1)], ident[:])
            h_b = hb[(i // 4) % 2]
            j = i % 4
            k.tt('dve', h_b[:, :, 128 * j:128 * (j + 1)], pv.rearrange("p (c t) -> p c t", c=8),
                 bcast(gT[:, :], 2, 128), ALU.mult)
            if j == 3:
                k.dma(T['hT_d'][i // 4], h_b[:].rearrange("p c t -> p (c t)"))


def run_interleaved(gens, width=2):
    active = []
    gens = list(gens)
    while gens or active:
        while gens and len(active) < width:
            active.append(gens.pop(0))
        for g in list(active):
            try:
                next(g)
            except StopIteration:
                active.remove(g)


def rstd_pool(k, rs, ss, n, mhalf, tmp):
    k.ts('dve', tmp, ss, 1.0 / n, EPS, op0=ALU.mult, op1=ALU.add)
    k.tt('pool', rs, tmp, mhalf, ALU.pow)


def qk_norm_rope(ph, W, src, dst, g_rep, cos_t, sin_t, mhalf):
    k = ph.k
    sq, ssq, rk, ta, tb_, tmp = W['sq'], W['ssq'], W['rk'], W['ta'], W['tb'], W['tmp']
    k.tt('pool', sq[:], src[:], src[:], ALU.mult)
    k.reduce_add(ssq[:], sq[:])
    rstd_pool(k, rk[:], ssq[:], 96, mhalf[:, 0:8], tmp[:])
    yield
    k.tt('dve', src[:], src[:], bcast(rk[:, :], 2, 96), ALU.mult)
    k.tt('pool', src[:], src[:], bcast(g_rep, 1, 8), ALU.mult)
    yield
    t1 = src[:, :, 64:80]
    t2 = src[:, :, 80:96]
    cb = bcast(cos_t, 1, 8)
    sbb = bcast(sin_t, 1, 8)
    k.tt('pool', ta[:, 0], t1, cb, ALU.mult)
    k.tt('pool', tb_[:, 0], t2, sbb, ALU.mult)
    k.tt('pool', ta[:, 1], t1, sbb, ALU.mult)
    k.tt('pool', tb_[:, 1], t2, cb, ALU.mult)
    k.tt('dve', dst[:, :, 64:80], ta[:, 0], tb_[:, 0], ALU.subtract)
    k.tt('dve', dst[:, :, 80:96], ta[:, 1], tb_[:, 1], ALU.add)
    k.copy('pool', dst[:, :, 0:64], src[:, :, 0:64])
    yield


def phase3(nc, T):
    with Phase(nc, "p3") as ph:
        k = ph.k
        ps = ph.ps
        ident = ph.sb("ident", [128, 128], BF)
        make_ident(ph, ident)
        w_in_v = T['w_in'].rearrange("(k p) n -> p k n", p=128)
        Wkv = ph.sb("Wkv", [128, 8, 288], BF)
        Wq = ph.sb("Wq", [128, 8, 384], BF)
        Wuq = ph.sb("Wuq", [128, 3, 768], BF)
        Wukv = ph.sb("Wukv", [128, 2, 1024], BF)
        gcq = ph.sb("gcq", [128, 3], F32)
        gckv = ph.sb("gckv", [128, 2], F32)
        gqk = ph.sb("gqk", [128, 192], F32)
        cosT = ph.sb("cosT", [128, NT, 16], F32)
        sinT = ph.sb("sinT", [128, NT, 16], F32)
        mhalf = ph.sb("mhalf", [128, 8], F32)
        k.memset('pool', mhalf[:], -0.5)
        load_w(ph, Wkv[:], w_in_v[:, :, COL_CKV:COL_CKV + 288])
        load_w(ph, Wq[:], w_in_v[:, :, COL_CQ:COL_CQ + 384])
        load_w(ph, Wuq[:], T['w_uq'].rearrange("(k p) n -> p k n", p=128))
        load_w(ph, Wukv[:], T['w_ukv'].rearrange("(k p) n -> p k n", p=128))
        k.dma(gcq[:], T['gcqT'][:, :])
        k.dma(gckv[:], T['gckvT'][:, :])
        k.dma(gqk[:], T['gqk'][0:1, :].partition_broadcast(128))
        k.dma(cosT[:].rearrange("p a b -> p (a b)"), T['cosT'][:, :])
        k.dma(sinT[:].rearrange("p a b -> p (a b)"), T['sinT'][:, :])
        k.tt('pool', Wuq[:], Wuq[:], bcast(gcq[:, :], 2, 768), ALU.mult)
        k.tt('pool', Wukv[:], Wukv[:], bcast(gckv[:, :], 2, 1024), ALU.mult)

        kT = ph.sb("kT", [128, 8, L], BF)
        vx = ph.sb("vx", [128, NT, 8, 65], BF)
        k.memset('pool', vx[:, :, :, 64:65], 1.0)
        ones = ph.sb("ones", [128, 64], BF)
        k.memset('pool', ones[:], 1.0)
        hbuf = [ph.sb(f"hbuf{i}", [128, 8, 512], BF) for i in range(2)]
        junk = [ph.sb(f"junk{i}", [128, 384], BF) for i in range(2)]
        ssl = ph.sb("ssl", [128, 2 * NT], F32)
        rsl = ph.sb("rsl", [128, 2 * NT], F32)
        tsl = ph.sb("tsl", [128, 2 * NT], F32)
        latn = [ph.sb(f"latn{i}", [128, 384], BF) for i in range(2)]
        latT = [ph.sb(f"latT{i}", [128, 3, 128], BF) for i in range(2)]
        kr = [ph.sb(f"kr{i}", [128, 32], F32) for i in range(2)]
        qk32 = [ph.sb(f"qk32_{i}", [128, 8, 96], F32) for i in range(2)]
        qkbf = [ph.sb(f"qkbf{i}", [128, 8, 96], BF) for i in range(2)]
        Wk_ = [dict(sq=ph.sb(f"sq{i}", [128, 8, 96], F32), ssq=ph.sb(f"ssq{i}", [128, 8], F32),
                    rk=ph.sb(f"rk{i}", [128, 8], F32), tmp=ph.sb(f"tmpn{i}", [128, 8], F32),
                    ta=ph.sb(f"ta{i}", [128, 2, 8, 16], F32), tb=ph.sb(f"tb{i}", [128, 2, 8, 16], F32)) for i in range(2)]
        qT = [ph.sb(f"qT{i}", [128, 8, 512], BF) for i in range(2)]
        pt = [ph.sb(f"pt{i}", [128, 512], BF) for i in range(4)]
        rsum = [ph.sb(f"rsum{i}", [128, 512], BF) for i in range(2)]
        bcs = [ph.sb(f"bcs{i}", [64, 512], F32) for i in range(2)]
        at = [ph.sb(f"at{i}", [64, 8, 512], BF) for i in range(2)]

        def sumsq(i, col, src_ps, n):
            k.act(junk[i % 2][:, 0:n], src_ps, AF.Square, accum_out=ssl[:, col:col + 1])
            rstd_pool(k, rsl[:, col:col + 1], ssl[:, col:col + 1], n, mhalf[:, 0:1], tsl[:, col:col + 1])

        def kv_tile(i, hb, j):
            par = i % 2
            if j == 0:
                k.dma(hb[:].rearrange("p c t -> p (c t)"), T['hT_d'][i // 4])
            lat = ps[par][:, 0:288]
            for c in range(8):
                k.mm(lat, hb[:, c, 128 * j:128 * (j + 1)], Wkv[:, c, :], start=(c == 0), stop=(c == 7))
            sumsq(i, i, lat[:, 0:256], 256)
            yield
            ln = latn[par]
            k.ts('dve', ln[:, 0:256], lat[:, 0:256], rsl[:, i:i + 1])
            k.copy('dve', kr[par][:], lat[:, 256:288])
            tbank = ps[2][:, :].bitcast(BF)
            for c in range(2):
                k.tr(tbank[:, 128 * c:128 * (c + 1)], ln[:, 128 * c:128 * (c + 1)], ident[:])
            lT = latT[par]
            k.copy('dve', lT[:, 0:2, :].rearrange("p c t -> p (c t)"), tbank[:, 0:256])
            yield
            kvb = [ps[3 + 2 * par], ps[4 + 2 * par]]
            for half in range(2):
                for c in range(2):
                    k.mm(kvb[half][:, :], lT[:, c, :], Wukv[:, c, 512 * half:512 * (half + 1)],
                         start=(c == 0), stop=(c == 1))
            kk = qk32[par]
            for half in range(2):
                kvv = kvb[half][:, :].rearrange("p (h e) -> p h e", h=4)
                k.copy('dve', kk[:, 4 * half:4 * half + 4, 0:64], kvv[:, :, 0:64])
                k.copy('dve', vx[:, i, 4 * half:4 * half + 4, 0:64], kvv[:, :, 64:128])
            k.copy('pool', kk[:, :, 64:96], bcast(kr[par][:, :], 1, 8))
            yield
            kf = qkbf[par]
            yield from qk_norm_rope(ph, Wk_[par], kk, kf, gqk[:, 96:192], cosT[:, i, :], sinT[:, i, :], mhalf)
            kbank = ps[7][:, :].bitcast(BF)
            for h in range(8):
                k.tr(kbank[0:96, 128 * h:128 * (h + 1)], kf[:, h, :], ident[:])
            k.copy('dve', kT[0:96, :, 128 * i:128 * (i + 1)], kbank[0:96, :].rearrange("p (h t) -> p h t", h=8))
            yield

        def q_tile(i, hb, j, q_T):
            par = i % 2
            lat = ps[6][:, 0:384]
            for c in range(8):
                k.mm(lat, hb[:, c, 128 * j:128 * (j + 1)], Wq[:, c, :], start=(c == 0), stop=(c == 7))
            yield
            lsb = Wk_[par]['sq'][:].rearrange("p a b -> p (a b)")[:, 0:384]
            k.copy('dve', lsb, lat)
            k.P.dve(lambda e: e.scalar_tensor_tensor(out=junk[par][:, 0:384], in0=lsb, scalar=1.0, in1=lsb, op0=ALU.mult,
                                                     op1=ALU.mult, accum_out=ssl[:, NT + i:NT + i + 1]),
                    [lsb], [junk[par][:, 0:384], ssl[:, NT + i:NT + i + 1]])
            rstd_pool(k, rsl[:, NT + i:NT + i + 1], ssl[:, NT + i:NT + i + 1], 384, mhalf[:, 0:1], tsl[:, NT + i:NT + i + 1])
            yield
            ln = latn[par]
            k.ts('dve', ln[:, 0:384], lsb, rsl[:, NT + i:NT + i + 1])
            yield
            tbank = ps[7][:, :].bitcast(BF)
            for c in range(3):
                k.tr(tbank[:, 128 * c:128 * (c + 1)], ln[:, 128 * c:128 * (c + 1)], ident[:])
            yield
            lT = latT[par]
            k.copy('dve', lT[:].rearrange("p c t -> p (c t)"), tbank[:, 0:384])
            yield
            qq = qk32[par]
            for half in range(2):
                qb = ps[6][:, 0:384]
                for c in range(3):
                    k.mm(qb, lT[:, c, :], Wuq[:, c, 384 * half:384 * (half + 1)], start=(c == 0), stop=(c == 2))
                yield
                k.copy('dve', qq[:, 4 * half:4 * half + 4, :], qb.rearrange("p (h e) -> p h e", h=4))
                yield
            qf = qkbf[par]
            yield from qk_norm_rope(ph, Wk_[par], qq, qf, gqk[:, 0:96], cosT[:, i, :], sinT[:, i, :], mhalf)
            yield
            yield
            yield
            yield
            qbank = ps[7][:, :].bitcast(BF)
            for h in range(8):
                k.tr(qbank[0:96, 128 * h:128 * (h + 1)], qf[:, h, :], ident[:])
            yield
            k.copy('dve', q_T[0:96, :, 128 * j:128 * (j + 1)], qbank[0:96, :].rearrange("p (h t) -> p h t", h=8))
            yield

        def kv_block(tb):
            hb = hbuf[tb % 2]
            return [kv_tile(tb * 4 + j, hb, j) for j in range(4)]

        def q_chunk_gens(qc):
            hb = hbuf[qc % 2]
            k.dma(hb[:].rearrange("p c t -> p (c t)"), T['hT_d'][qc])
            return [q_tile(qc * 4 + j, hb, j, qT[qc % 2]) for j in range(4)]

        gens = []
        for tb in range(DBG.get('nprep', NB)):
            gens += kv_block(tb)
        run_interleaved(gens, 2)
        nqc = DBG.get('nqc', NB)
        if nqc:
            run_interleaved(q_chunk_gens(0), 1)

        scale = 1.0 / math.sqrt(96.0)
        NH = DBG.get('nh', 8)
        pend = [None]
        for qc in range(nqc):
            q_T = qT[qc % 2]
            a_t = at[qc % 2]
            nxt = q_chunk_gens(qc + 1) if qc + 1 < nqc else []
            nxt_active = []
            steps = [(h, kt) for h in range(NH) for kt in range(NT)]

            def S(idx):
                h, kt = steps[idx]
                k.mm(ps[idx % 3][:, :], kT[0:96, h, 128 * kt:128 * (kt + 1)], q_T[0:96, h, :])

            def fin_a(h):
                k.recip(rsum[h % 2][64:65, :], ps[3 + (h % 2)][64:65, :])

            def fin_b(h):
                k.mm(ps[5][0:64, :], ones[64:65, :], rsum[h % 2][64:65, :])

            def fin_c(h, a_t):
                k.copy('dve', bcs[h % 2][:], ps[5][0:64, :])
                k.tt('dve', a_t[:, h, :], ps[3 + (h % 2)][0:64, :], bcs[h % 2][:], ALU.mult)

            S(0)
            S(1)
            for idx, (h, kt) in enumerate(steps):
                p_t = pt[idx % 4]
                k.act(p_t[:], ps[idx % 3][:, :], AF.Exp, scale=scale)
                if idx + 2 < len(steps):
                    S(idx + 2)
                ob = ps[3 + (h % 2)]
                k.mm(ob[0:65, :], vx[:, kt, h, :], p_t[:], start=(kt == 0), stop=(kt == NT - 1))
                if pend[0] is not None:
                    if kt == 1:
                        pend[0][0]()
                    elif kt == 10:
                        pend[0][1]()
                    elif kt == 14:
                        pend[0][2]()
                        pend[0] = None
                if kt == NT - 1:
                    last = (h == NH - 1)
                    pend[0] = (lambda h=h: fin_a(h), lambda h=h: fin_b(h),
                               (lambda h=h, a_t=a_t, qc=qc, last=last, fin_c=fin_c: (fin_c(h, a_t), k.dma(T['at_d'][qc], a_t[:].rearrange("p h t -> p (h t)")) if last else None)))
                if idx % 3 == 2:
                    while nxt and len(nxt_active) < 1:
                        nxt_active.append(nxt.pop(0))
                    for g in list(nxt_active):
                        try:
                            next(g)
                        except StopIteration:
                            nxt_active.remove(g)
            run_interleaved(nxt_active + nxt, 1)

        if pend[0] is not None:
            pend[0][0]()
            pend[0][1]()
            pend[0][2]()


def phase4(nc, T):
    with Phase(nc, "p4") as ph:
        k = ph.k
        ps = ph.ps
        w_in_v = T['w_in'].rearrange("(k p) n -> p k n", p=128)
        Wz = ph.sb("Wz", [128, 8, 512], BF)
        Wg = ph.sb("Wg", [128, 8, 2048], BF)
        Wao = ph.sb("Wao", [128, 4, D], BF)
        Who = ph.sb("Who", [128, 4, D], BF)
        Wout = ph.sb("Wout", [128, 8, D], BF)
        bg = ph.sb("bg", [128, 16], F32)
        load_w(ph, Wz[:], w_in_v[:, :, COL_ZA:COL_ZA + 512])
        for q4 in range(4):
            load_w(ph, Wg[:, :, 512 * q4:512 * (q4 + 1)], w_in_v[:, :, COL_GH + 512 * q4:COL_GH + 512 * (q4 + 1)])
        load_w(ph, Wao[:], T['w_attn_out'].rearrange("(hp p) n -> p hp n", p=128))
        load_w(ph, Who[:], T['w_hy_out'].rearrange("(k p) n -> p k n", p=128))
        load_w(ph, Wout[:], T['w_out'].rearrange("(k p) n -> p k n", p=128))
        k.dma(bg[:], T['bgT'][:, :])
        hbuf = [ph.sb(f"hbuf{i}", [128, 8, 512], BF) for i in range(2)]
        atb = [ph.sb(f"atb{i}", [128, 4, 512], BF) for i in range(2)]
        yzb = [ph.sb(f"yzb{i}", [128, 4, 512], BF) for i in range(2)]
        xt = [ph.sb(f"xt{i}", [128, D], F32) for i in range(3)]
        ot = [ph.sb(f"ot{i}", [128, D], F32) for i in range(2)]
        sz = [ph.sb(f"sz{i}", [128, 512], BF) for i in range(2)]
        ya = ph.sb("ya", [128, 4, 512], BF)
        gh = [ph.sb(f"gh{i}", [128, 512], BF) for i in range(2)]
        ga = [ph.sb(f"ga{i}", [128, 512], BF) for i in range(2)]
        m1 = [ph.sb(f"m1{i}", [128, 512], F32) for i in range(2)]
        m2 = [ph.sb(f"m2{i}", [128, 512], F32) for i in range(2)]
        mg = [ph.sb(f"mg{i}", [128, 8, 512], BF) for i in range(2)]
        for tb in range(NB):
            hb = hbuf[tb % 2]
            a_b = atb[tb % 2]
            y_b = yzb[tb % 2]
            k.dma(hb[:].rearrange("p c t -> p (c t)"), T['hT_d'][tb])
            atv = T['at_d'][tb].rearrange("p (hp two t) -> p two hp t", two=2, t=512)
            k.dma(a_b[0:64, :, :], atv[:, 0, :, :])
            k.dma(a_b[64:128, :, :], atv[:, 1, :, :])
            k.dma(y_b[:].rearrange("p c t -> p (c t)"), T['yz_d'][tb])
            for hp in range(4):
                zb = ps[hp % 2][:, :]
                for c in range(8):
                    k.mm(zb, Wz[:, c, 128 * hp:128 * (hp + 1)], hb[:, c, :], start=(c == 0), stop=(c == 7))
                s_z = sz[hp % 2]
                k.act(s_z[:], zb, AF.Silu)
                k.tt('pool', ya[:, hp, :], a_b[:, hp, :], s_z[:], ALU.mult)
            m_g = mg[tb % 2]
            for dc in range(8):
                g1 = ps[2 + (dc % 2)]
                g2 = ps[4 + (dc % 2)]
                for c in range(8):
                    k.mm(g1[:, :], Wg[:, c, 128 * dc:128 * (dc + 1)], hb[:, c, :], start=(c == 0), stop=(c == 7))
                for c in range(8):
                    k.mm(g2[:, :], Wg[:, c, 1024 + 128 * dc:1024 + 128 * (dc + 1)], hb[:, c, :],
                         start=(c == 0), stop=(c == 7))
                k.act(gh[dc % 2][:], g1[:, :], AF.Sigmoid, bias=bg[:, dc:dc + 1])
                k.act(ga[dc % 2][:], g2[:, :], AF.Sigmoid, bias=bg[:, 8 + dc:9 + dc])
                uh = ps[6]
                ua = ps[7]
                for c in range(4):
                    k.mm(uh[:, :], Who[:, c, 128 * dc:128 * (dc + 1)], y_b[:, c, :], start=(c == 0), stop=(c == 3))
                for hp in range(4):
                    k.mm(ua[:, :], Wao[:, hp, 128 * dc:128 * (dc + 1)], ya[:, hp, :], start=(hp == 0), stop=(hp == 3))
                k.tt('dve', m1[dc % 2][:], uh[:, :], gh[dc % 2][:], ALU.mult)
                k.tt('dve', m2[dc % 2][:], ua[:, :], ga[dc % 2][:], ALU.mult)
                k.tt('pool', m_g[:, dc, :], m1[dc % 2][:], m2[dc % 2][:], ALU.add)
            for j in range(4):
                i = tb * 4 + j
                x_t = xt[i % 3]
                k.dma(x_t[:], T['x'][128 * i:128 * (i + 1), :])
                o_t = ot[i % 2]
                for half in range(2):
                    fb = ps[half]
                    for c in range(8):
                        k.mm(fb[:, :], m_g[:, c, 128 * j:128 * (j + 1)], Wout[:, c, 512 * half:512 * (half + 1)],
                             start=(c == 0), stop=(c == 7))
                    k.tt('dve', o_t[:, 512 * half:512 * (half + 1)], fb[:, :], x_t[:, 512 * half:512 * (half + 1)], ALU.add)
                k.dma(T['out'][128 * i:128 * (i + 1), :], o_t[:])


def fft_constants():
    C = {}
    n = NF
    s2 = np.arange(128, dtype=np.float64)[:, None]
    f2 = np.arange(128, dtype=np.float64)[None, :]
    th = 2 * np.pi * (f2 + 0.5) * s2 / 256.0
    C['FA1'] = np.concatenate([np.cos(th), -np.sin(th)], 1)
    th2 = 2 * np.pi * (f2 + 0.5) * (s2 + 128) / 256.0
    C['FA2'] = -np.concatenate([np.cos(th2), -np.sin(th2)], 1)
    s1 = np.arange(32, dtype=np.float64)
    tw = np.exp(-2j * np.pi * (np.arange(128)[None, :] + 0.5) * s1[:, None] / n)
    twq = np.tile(tw, (4, 1))
    C['TWa'] = np.concatenate([twq.real, twq.real], 1)
    C['TWb'] = np.concatenate([-twq.imag, twq.imag], 1)
    W = np.exp(-2j * np.pi * np.outer(s1, s1) / 32.0)
    Wq = np.kron(np.eye(4), W)
    C['WBr'] = Wq.real
    C['WBi'] = Wq.imag
    C['WBni'] = -Wq.imag
    Wi = np.exp(2j * np.pi * np.outer(s1, s1) / 32.0)
    Wiq = np.kron(np.eye(4), Wi)
    C['WI1'] = np.concatenate([Wiq.real, Wiq.imag], 1)
    C['WI2'] = np.concatenate([-Wiq.imag, Wiq.real], 1)
    twi = np.exp(2j * np.pi * (np.arange(128)[:, None] + 0.5) * s1[None, :] / n)
    twiq = np.tile(twi, (1, 4))
    C['TIa'] = np.concatenate([twiq.real, twiq.real], 1)
    C['TIb'] = np.concatenate([-twiq.imag, twiq.imag], 1)
    t2 = np.arange(128, dtype=np.float64)[None, :]
    f2c = np.arange(128, dtype=np.float64)[:, None]
    th3 = 2 * np.pi * (f2c + 0.5) * t2 / 256.0
    C['FIr'] = (2.0 / n) * np.cos(th3)
    C['FIi'] = -(2.0 / n) * np.sin(th3)
    return C


def filter_constants():
    C = {}
    f32 = np.float32
    t = np.linspace(0.0, 1.0, L, dtype=f32)[:, None]
    bands = 16
    f = np.linspace(1e-4, bands - 1, bands, dtype=f32)
    ang = (f32(2.0 * np.pi / L) * np.arange(L, dtype=f32)[:, None] * f[None, :]).astype(f32)
    z = np.concatenate([t, np.cos(ang).astype(f32), -np.sin(ang).astype(f32)], axis=-1).astype(f32)
    zs = np.zeros((128, L), f32)
    zs[0:33, :] = z.T
    zs[64:97, :] = z[::-1].T
    hi = zs.astype(ml_dtypes.bfloat16)
    lo = (zs - hi.astype(f32)).astype(ml_dtypes.bfloat16)
    C['zs_hi'] = hi
    C['zs_lo'] = lo
    tl = t[:, 0]
    tf = np.zeros((128, 2, 32), f32)
    pidx = np.arange(128)[:, None] * 32 + np.arange(32)[None, :]
    tf[:, 0, :] = tl[pidx]
    tf[:, 1, :] = tl[4095 - pidx]
    C['tfull'] = tf.reshape(128, 64)
    MIN_DECAY = math.log(1e-2) / 1.5
    MAX_DECAY = math.log(1e-2) / 0.3
    deltas = np.abs(np.linspace(MIN_DECAY, MAX_DECAY, HYW, dtype=f32)).astype(f32)
    C['negd'] = (-deltas)[None, :].astype(f32)
    return C


def _sin_layer(ph, W, pre_ps, fr, fb, out32):
    k = ph.k
    a, kk = W['a'], W['kk']
    k.ts('dve', a[:], pre_ps, fr, fb, op0=ALU.mult, op1=ALU.add)
    k.ts('dve', kk[:], a[:], 1.0 / (2 * math.pi), MAGIC, op0=ALU.mult, op1=ALU.add)
    k.ts('dve', kk[:], kk[:], -MAGIC, None, op0=ALU.add)
    k.stt(a[:], kk[:], -2 * math.pi, a[:], ALU.mult, ALU.add)
    k.ts('dve', a[:], a[:], -3.14159, 3.14159, op0=ALU.max, op1=ALU.min)
    k.act(out32, a[:], AF.Sin)


def _hilo(ph, hi, lo, src32, tmp32):
    k = ph.k
    k.copy('dve', hi, src32)
    k.copy('pool', tmp32, hi)
    k.tt('pool', lo, src32, tmp32, ALU.subtract)


def phase2a(nc, T):
    with Phase(nc, "p2a") as ph:
        k = ph.k
        ps = ph.ps
        zs_hi = ph.sb("zs_hi", [128, L], BF)
        zs_lo = ph.sb("zs_lo", [128, L], BF)
        W1 = ph.sb("W1", [128, 128], F32)
        W2 = ph.sb("W2", [128, 128], F32)
        W1h = ph.sb("W1h", [128, 128], BF)
        W1l = ph.sb("W1l", [128, 128], BF)
        W2h = ph.sb("W2h", [128, 128], BF)
        W2l = ph.sb("W2l", [128, 128], BF)
        wt = ph.sb("wt", [128, 128], F32)
        mv = ph.sb("mv", [128, 4], F32)
        fb = ph.sb("fb", [128, 2], F32)
        Wk = dict(a=ph.sb("a", [128, 512], F32), kk=ph.sb("kk", [128, 512], F32))
        h1 = ph.sb("h1", [128, 512], F32)
        h1h = ph.sb("h1h", [128, 512], BF)
        h1l = ph.sb("h1l", [128, 512], BF)
        t32 = ph.sb("t32", [128, 512], F32)
        h2 = ph.sb("h2", [128, 512], F32)
        h2b = [ph.sb(f"h2b{i}", [128, 512], BF) for i in range(2)]
        k.dma(zs_hi[:], T['zs_hi'][:, :])
        k.dma(zs_lo[:], T['zs_lo'][:, :])
        k.dma(W1[:], T['W1blk'][:, :])
        k.dma(W2[:], T['W2blk'][:, :])
        k.dma(mv[:], T['mlpv'][:, :])
        _hilo(ph, W1h[:], W1l[:], W1[:], wt[:])
        _hilo(ph, W2h[:], W2l[:], W2[:], wt[:])
        k.tt('dve', fb[:, 0:1], mv[:, 0:1], mv[:, 1:2], ALU.mult)
        k.tt('dve', fb[:, 1:2], mv[:, 2:3], mv[:, 3:4], ALU.mult)
        for cch in range(DBG.get('n2a', NB)):
            sl = slice(512 * cch, 512 * (cch + 1))
            b1 = ps[cch % 2]
            k.mm(b1[:, :], W1h[:], zs_hi[:, sl], start=True, stop=False)
            k.mm(b1[:, :], W1h[:], zs_lo[:, sl], start=False, stop=False)
            k.mm(b1[:, :], W1l[:], zs_hi[:, sl], start=False, stop=True)
            if DBG.get('s2a', 9) < 1: continue
            _sin_layer(ph, Wk, b1[:, :], mv[:, 0:1], fb[:, 0:1], h1[:])
            if DBG.get('s2a', 9) < 2: continue
            _hilo(ph, h1h[:], h1l[:], h1[:], t32[:])
            if DBG.get('s2a', 9) < 3: continue
            b2 = ps[2 + cch % 2]
            k.mm(b2[:, :], W2h[:], h1h[:], start=True, stop=False)
            k.mm(b2[:, :], W2h[:], h1l[:], start=False, stop=False)
            k.mm(b2[:, :], W2l[:], h1h[:], start=False, stop=True)
            _sin_layer(ph, Wk, b2[:, :], mv[:, 2:3], fb[:, 1:2], h2[:])
            hb_ = h2b[cch % 2]
            k.copy('pool', hb_[:], h2[:])
            k.dma(T['h2_d'][:, sl], hb_[:])


def _cmul_tab(ph, W, src, Ta, Tb, out_bf):
    k = ph.k
    P1, P2 = W
    sw = src.rearrange("p (r f) -> p r f", r=2)[:, ::-1, :]
    k.tt('dve', P1[:], src, Ta, ALU.mult)
    k.tt('dve', P2[:].rearrange("p (r f) -> p r f", r=2), sw, Tb.rearrange("p (r f) -> p r f", r=2), ALU.mult)
    k.tt('pool', out_bf, P1[:], P2[:], ALU.add)


def phase2b(nc, T):
    with Phase(nc, "p2b") as ph:
        k = ph.k
        ps = ph.ps
        ident = ph.sb("ident", [128, 128], BF)
        make_ident(ph, ident)
        cb16 = {}
        for nm, w in (('FA1', 256), ('FA2', 256), ('WBr', 128), ('WBi', 128), ('WBni', 128), ('WI1', 256), ('WI2', 256),
                      ('FIr', 128), ('FIi', 128)):
            cb16[nm] = ph.sb(nm, [128, w], BF)
            k.dma(cb16[nm][:], T[nm][:, :])
        c32 = {}
        for nm in ('TWa', 'TWb', 'TIa', 'TIb'):
            c32[nm] = ph.sb(nm, [128, 256], F32)
            k.dma(c32[nm][:], T[nm][:, :])
        h2s = ph.sb("h2s", [128, L], BF)
        k.dma(h2s[:], T['h2_d'][:, :])
        h2p = ph.sb("h2p", [128, 32, 128], BF)
        k.copy('pool', h2p[:], h2s[:].rearrange("q (p s) -> q s p", s=32))
        W3 = ph.sb("W3", [128, 2048], BF)
        load_w(ph, W3[:], T['W3blk'][:, :])
        wsh = ph.sb("wsh", [128, 12, 4], F32)
        k.dma(wsh[:].rearrange("p a b -> p (a b)"), T['wsh'][:, :])
        biasT = ph.sb("biasT", [128, 2, 128], F32)
        k.dma(biasT[:].rearrange("p a b -> p (a b)"), T['biasT'][:, :])
        negd = ph.sb("negd", [128, HYW], F32)
        k.dma(negd[:], T['negd'][0:1, :].partition_broadcast(128))
        tfull = ph.sb("tfull", [128, 2, 32], F32)
        k.dma(tfull[:].rearrange("p a b -> p (a b)"), T['tfull'][:, :])
        w_in_v = T['w_in'].rearrange("(k p) n -> p k n", p=128)

        hbuf = [ph.sb(f"hbuf{i}", [128, 8, 512], BF) for i in range(2)]
        ar = ph.sb("arena", [128, 24592], BF)
        raw = [ar[:, 4098 * i:4098 * (i + 1)] for i in range(3)]
        ub_ = [ar[:, 12294 + 4096 * i:12294 + 4096 * (i + 1)] for i in range(2)]
        Wblk = ar[:, 20486:24582].rearrange("p (k w c) -> p k w c", k=8, w=4)
        k_tm = ar[:, 0:8192].rearrange("p (o d c s) -> p o d c s", o=2, d=2, c=64)
        AB = ar[:, 8192:12288].rearrange("p (d c s) -> p d c s", d=2, c=64)
        Ksp = ar[:, 12288:20480].rearrange("p (o g r f) -> p o g r f", o=2, g=16, r=2)
        Gbuf = ar[:, 20480:24576].rearrange("p (r c s) -> p r c s", r=2, c=64)
        sz = ph.sb("sz", [128, L], BF)
        tm = [ph.sb(f"tm{i}", [128, 128, 32], BF) for i in range(3)]
        z2_tm = ph.sb("z2_tm", [128, 128, 32], BF)
        y_sc = ph.sb("y_sc", [128, 32, 128], BF)
        yzb = ph.sb("yzb", [128, L], BF)
        arg32 = ph.sb("arg32", [128, 4096], F32)
        PW = [(ph.sb(f"P1_{i}", [128, 256], F32), ph.sb(f"P2_{i}", [128, 256], F32)) for i in range(2)]
        Zp = [ph.sb(f"Zp{i}", [128, 256], BF) for i in range(2)]
        Yb = [ph.sb(f"Yb{i}", [128, 256], BF) for i in range(2)]
        Kev = [ph.sb(f"Kev{i}", [128, 256], BF) for i in range(2)]
        Esb = [[ph.sb(f"E{st}_{i}", [128, 256], BF) for i in range(2)] for st in range(3)]
        cols = (COL_V, COL_X1, COL_X2, COL_ZH)
        cnt = [0]

        PWs = [[(ph.sb(f"P1_{st}_{i}", [128, 512], F32), ph.sb(f"P2_{st}_{i}", [128, 512], F32)) for i in range(2)]
               for st in range(3)]
        Zp2 = [ph.sb(f"Zq{i}", [128, 512], BF) for i in range(2)]
        Yb2 = [ph.sb(f"Yq{i}", [128, 512], BF) for i in range(2)]

        def v4(ap):
            return ap.rearrange("p (u r f) -> p u r f", u=2, r=2)

        def tab4(t):
            return bcast(t.rearrange("p (r f) -> p r f", r=2), 1, 2)

        def skewed(G, stages):
            ns = len(stages)
            for t in range(G + ns - 1):
                for s_ in reversed(range(ns)):
                    g = t - s_
                    if 0 <= g < G:
                        stages[s_](g)

        def cmul_pair(W, bank, Ta, Tb, out512):
            P1, P2 = W
            k.tt('dve', v4(P1[:]), v4(bank), tab4(Ta), ALU.mult)
            k.tt('dve', v4(P2[:]), v4(bank)[:, :, ::-1, :], tab4(Tb), ALU.mult)
            k.tt('pool', out512, P1[:], P2[:], ALU.add)

        def st_za(lhs_of):
            def f(q):
                bank = ps[q % 2]
                for u in range(2):
                    l1, l2 = lhs_of(2 * q + u)
                    za = bank[:, 256 * u:256 * (u + 1)]
                    k.mm(za, l1, cb16['FA1'][:], start=True, stop=(l2 is None))
                    if l2 is not None:
                        k.mm(za, l2, cb16['FA2'][:], start=False, stop=True)
            return f

        def st_tw(q):
            cmul_pair(PWs[0][q % 2], ps[q % 2][:, :], c32['TWa'][:], c32['TWb'][:], Zp2[q % 2][:])

        def st_ub(q):
            bank = ps[2 + q % 2]
            z_p = Zp2[q % 2]
            for u in range(2):
                ub = bank[:, 256 * u:256 * (u + 1)]
                zr = z_p[:, 256 * u:256 * u + 128]
                zi = z_p[:, 256 * u + 128:256 * (u + 1)]
                k.mm(ub[:, 0:128], cb16['WBr'][:], zr, start=True, stop=False)
                k.mm(ub[:, 0:128], cb16['WBni'][:], zi, start=False, stop=True)
                k.mm(ub[:, 128:256], cb16['WBi'][:], zr, start=True, stop=False)
                k.mm(ub[:, 128:256], cb16['WBr'][:], zi, start=False, stop=True)

        for cb in range(DBG.get('ncb', 4)):
            for w in range(4):
                load_w(ph, Wblk[:, :, w, :], w_in_v[:, :, cols[w] + 128 * cb:cols[w] + 128 * (cb + 1)])
            for w in range(3):
                k.memset('pool', raw[w][:, 0:1], 0.0)
                k.memset('pool', raw[w][:, 4097:4098], 0.0)
            for tb in range(NB):
                hb = hbuf[tb % 2]
                k.dma(hb[:].rearrange("p c t -> p (c t)"), T['hT_d'][tb])
                for w in range(4):
                    bank = ps[(tb * 4 + w) % 2]
                    for c in range(8):
                        k.mm(bank[:, :], Wblk[:, c, w, :], hb[:, c, :], start=(c == 0), stop=(c == 7))
                    if w < 3:
                        k.act(raw[w][:, 1 + 512 * tb:1 + 512 * (tb + 1)], bank[:, :], AF.Copy)
                    else:
                        k.act(sz[:, 512 * tb:512 * (tb + 1)], bank[:, :], AF.Silu)
            if DBG.get('s2b', 9) < 2: continue
            for w in range(3):
                u = ub_[w % 2]
                j = 4 * w + cb
                k.ts('dve', u, raw[w][:, 1:4097], wsh[:, j, 1:2], wsh[:, j, 3:4], op0=ALU.mult, op1=ALU.add)
                k.stt(u, raw[w][:, 0:4096], wsh[:, j, 0:1], u, ALU.mult, ALU.add)
                k.stt(u, raw[w][:, 2:4098], wsh[:, j, 2:3], u, ALU.mult, ALU.add)
                for a in range(4):
                    pv = ps[2 + a % 2][:, :].bitcast(BF)
                    for e in range(8):
                        s1 = 8 * a + e
                        k.tr(pv[:, 128 * e:128 * (e + 1)], u[:, s1:4096:32], ident[:])
                    k.copy('dve', tm[w][:, :, 8 * a:8 * a + 8].rearrange("p c s -> p s c"),
                           pv.rearrange("p (s c) -> p s c", s=8))
            if DBG.get('dump_tm'):
                k.dma(T['dbg_tm'][:, :], tm[DBG['dump_tm'] - 1][:].rearrange("p c s -> p (c s)"))
            if DBG.get('s2b', 9) < 3: continue
            for hbk in range(DBG.get('nhbk', 2)):
                c0 = 64 * hbk
                gcol = 128 * cb + c0
                k.tt('dve', arg32[:].rearrange("p (d c s) -> p d c s", d=2, c=64),
                     bcast(bcast(negd[:, gcol:gcol + 64], 1, 2), 3, 32),
                     bcast(tfull[:, :, :], 2, 64), ALU.mult)
                if DBG.get('s3', 9) < 2: continue
                k.act(AB.rearrange("p d c s -> p (d c s)"), arg32[:], AF.Exp)
                if DBG.get('s3', 9) < 3: continue
                wc0 = 256 * (2 * cb + hbk)
                for s1 in range(32):
                    kb_ = ps[6 + s1 % 2][:, 0:256]
                    k.mm(kb_, h2p[:, s1, :], W3[:, wc0:wc0 + 256])
                    if DBG.get('s3', 9) < 4: continue
                    abv = AB[:, :, :, s1].rearrange("p d c -> p (d c)")
                    k.tt('dve', k_tm[:, :, :, :, s1].rearrange("p o d c -> p o (d c)"),
                         kb_.rearrange("p (o x) -> p o x", o=2), bcast(abv, 1, 2), ALU.mult)
                if DBG.get('s2b', 9) < 4: continue
                def spec_lhs(gi):
                    o, g = divmod(gi, 16)
                    return (k_tm[:, o, 0, 4 * g:4 * g + 4, :].rearrange("p c s -> p (c s)"),
                            k_tm[:, o, 1, 4 * g:4 * g + 4, :].rearrange("p c s -> p (c s)"))

                def st_kev(q):
                    o, gp = divmod(q, 8)
                    k.copy('dve', Ksp[:, o, 2 * gp:2 * gp + 2, :, :].rearrange("p g r f -> p (g r f)"), ps[2 + q % 2][:, :])

                skewed(16, [st_za(spec_lhs), st_tw, st_ub, st_kev])
                for o in range(2):
                    gg0 = gcol // 4
                    k.tt('pool', Ksp[:, o, :, 0, :], Ksp[:, o, :, 0, :], bcast(biasT[:, o, gg0:gg0 + 16], 2, 128), ALU.add)
                if DBG.get('dump_k'):
                    k.dma(T['dbg_k'][:, :], ar[:, 0:8192])
                    k.dma(T['dbg_ks'][:, :], ar[:, 12288:20480])
                if DBG.get('s2b', 9) < 5: continue
                for o in range(DBG.get('nord', 2)):
                    src = tm[0] if o == 0 else z2_tm
                    gate = tm[1] if o == 0 else tm[2]
                    def conv_lhs(g, src=src):
                        return (src[:, c0 + 4 * g:c0 + 4 * g + 4, :].rearrange("p c s -> p (c s)"), None)

                    def st_mul(q, o=o):
                        bank = ps[2 + q % 2][:, :]
                        P1, P2 = PWs[1][q % 2]
                        kr_ = bcast(Ksp[:, o, 2 * q:2 * q + 2, 0, :], 2, 2)
                        ki_ = bcast(Ksp[:, o, 2 * q:2 * q + 2, 1, :], 2, 2)
                        k.tt('dve', v4(P1[:]), v4(bank), kr_, ALU.mult)
                        k.tt('dve', v4(P2[:]), v4(bank)[:, :, ::-1, :], ki_, ALU.mult)
                        y_b = Yb2[q % 2]
                        k.tt('pool', v4(y_b[:])[:, :, 0, :], v4(P1[:])[:, :, 0, :], v4(P2[:])[:, :, 0, :], ALU.subtract)
                        k.tt('pool', v4(y_b[:])[:, :, 1, :], v4(P1[:])[:, :, 1, :], v4(P2[:])[:, :, 1, :], ALU.add)

                    def st_gb(q):
                        bank = ps[4 + q % 2]
                        y_b = Yb2[q % 2]
                        for u in range(2):
                            gb = bank[:, 256 * u:256 * (u + 1)]
                            k.mm(gb, y_b[:, 256 * u:256 * u + 128], cb16['WI1'][:], start=True, stop=False)
                            k.mm(gb, y_b[:, 256 * u + 128:256 * (u + 1)], cb16['WI2'][:], start=False, stop=True)

                    def st_itw(q):
                        bank = ps[4 + q % 2][:, :]
                        P1, P2 = PWs[2][q % 2]
                        k.tt('dve', v4(P1[:]), v4(bank), tab4(c32['TIa'][:]), ALU.mult)
                        k.tt('dve', v4(P2[:]), v4(bank)[:, :, ::-1, :], tab4(c32['TIb'][:]), ALU.mult)
                        k.tt('pool', Gbuf[:, :, 8 * q:8 * q + 8, :].rearrange("p r (u c) s -> p r u (c s)", u=2),
                             v4(P1[:]).rearrange("p u r f -> p r u f"), v4(P2[:]).rearrange("p u r f -> p r u f"), ALU.add)

                    skewed(8, [st_za(conv_lhs), st_tw, st_ub, st_mul, st_gb, st_itw])
                    for cc in range(4):
                        yb = ps[6 + cc % 2]
                        k.mm(yb[:, :], cb16['FIr'][:], Gbuf[:, 0, 16 * cc:16 * cc + 16, :].rearrange("p c s -> p (c s)"),
                             start=True, stop=False)
                        k.mm(yb[:, :], cb16['FIi'][:], Gbuf[:, 1, 16 * cc:16 * cc + 16, :].rearrange("p c s -> p (c s)"),
                             start=False, stop=True)
                        cs = slice(c0 + 16 * cc, c0 + 16 * cc + 16)
                        if o == 0:
                            k.tt('dve', z2_tm[:, cs, :], yb[:, :].rearrange("p (c s) -> p c s", c=16), gate[:, cs, :], ALU.mult)
                        else:
                            k.tt('dve', y_sc[:, :, cs].rearrange("p s c -> p c s"),
                                 yb[:, :].rearrange("p (c s) -> p c s", c=16), gate[:, cs, :], ALU.mult)
            if DBG.get('dump_z2'):
                k.dma(T['dbg_tm'][:, :], z2_tm[:].rearrange("p c s -> p (c s)"))
            if DBG.get('s2b', 9) < 6: continue
            for a in range(4):
                pv = ps[2 + a % 2][:, :].bitcast(BF)
                for e in range(8):
                    k.tr(pv[:, 128 * e:128 * (e + 1)], y_sc[:, 8 * a + e, :], ident[:])
                k.tt('dve', yzb[:].rearrange("c (p s) -> c s p", s=32)[:, 8 * a:8 * a + 8, :],
                     pv.rearrange("c (s p) -> c s p", s=8),
                     sz[:].rearrange("c (p s) -> c s p", s=32)[:, 8 * a:8 * a + 8, :], ALU.mult)
            for tb in range(NB):
                k.dma(T['yz_d'][tb][:, 512 * cb:512 * (cb + 1)], yzb[:, 512 * tb:512 * (tb + 1)])


def phase2(nc, T):
    if 'a' in DBG.get('p2', 'ab'):
        phase2a(nc, T)
    if 'b' in DBG.get('p2', 'ab'):
        phase2b(nc, T)


def _bf(a):
    return np.asarray(a, np.float32).astype(ml_dtypes.bfloat16)


_CONST_CACHE = {}


def host_constants():
    if _CONST_CACHE:
        return _CONST_CACHE
    C = {}
    pos = np.arange(L, dtype=np.float32)
    inv_freq = (np.float32(10000.0) ** (-np.arange(0, 32, 2, dtype=np.float32) / np.float32(32))).astype(np.float32)
    ang = (pos[:, None] * inv_freq[None, :]).astype(np.float32)
    C['cosT'] = np.ascontiguousarray(np.cos(ang).astype(np.float32).reshape(NT, 128, 16).transpose(1, 0, 2).reshape(128, NT * 16))
    C['sinT'] = np.ascontiguousarray(np.sin(ang).astype(np.float32).reshape(NT, 128, 16).transpose(1, 0, 2).reshape(128, NT * 16))
    F = fft_constants()
    for nm in ('FA1', 'FA2', 'WBr', 'WBi', 'WBni', 'WI1', 'WI2', 'FIr', 'FIi'):
        C[nm] = np.ascontiguousarray(_bf(F[nm]))
    for nm in ('TWa', 'TWb', 'TIa', 'TIb'):
        C[nm] = np.ascontiguousarray(F[nm].astype(np.float32))
    C.update(filter_constants())
    _CONST_CACHE.update(C)
    return _CONST_CACHE


def prep_inputs(inp, b):
    f32 = np.float32
    m = {}
    m['x'] = np.ascontiguousarray(inp['x'][b], dtype=f32)
    m['w_in'] = np.ascontiguousarray(inp['w_in'][0], dtype=f32)
    m['gT'] = np.ascontiguousarray(inp['g_norm'][0].reshape(8, 128).T, dtype=f32)
    m['bgT'] = np.ascontiguousarray(inp['b_gate'][0].reshape(16, 128).T, dtype=f32)
    m['w_uq'] = np.ascontiguousarray(inp['w_uq'][0], dtype=f32)
    m['w_ukv'] = np.ascontiguousarray(inp['w_ukv'][0], dtype=f32)
    m['gcqT'] = np.ascontiguousarray(inp['g_cq'][0].reshape(3, 128).T, dtype=f32)
    m['gckvT'] = np.ascontiguousarray(inp['g_ckv'][0].reshape(2, 128).T, dtype=f32)
    m['gqk'] = np.ascontiguousarray(np.concatenate([inp['g_qn'][0], inp['g_kn'][0]])[None, :], dtype=f32)
    m['w_attn_out'] = np.ascontiguousarray(inp['w_attn_out'][0], dtype=f32)
    m['w_hy_out'] = np.ascontiguousarray(inp['w_hy_out'][0], dtype=f32)
    m['w_out'] = np.ascontiguousarray(inp['w_out'][0], dtype=f32)
    wsh = np.zeros((128, 12, 4), f32)
    wsh[:, :, 0:3] = inp['w_short'][0].reshape(3, 12, 128).transpose(2, 1, 0)
    wsh[:, :, 3] = inp['b_short'][0].reshape(12, 128).T
    m['wsh'] = wsh.reshape(128, 48)
    hb = inp['hy_bias'][0]
    bT = hb.reshape(2, 128, 4).transpose(2, 0, 1)
    m['biasT'] = np.ascontiguousarray(np.repeat(bT[:, None], 32, axis=1).reshape(128, 256), dtype=f32)
    W1 = np.zeros((128, 128), f32)
    W1[0:33, 0:64] = inp['w_f1'][0]
    W1[64:97, 64:128] = inp['w_f1'][0]
    m['W1blk'] = W1
    W2 = np.zeros((128, 128), f32)
    W2[0:64, 0:64] = inp['w_f2'][0]
    W2[64:128, 64:128] = inp['w_f2'][0]
    m['W2blk'] = W2
    mv = np.zeros((128, 4), f32)
    for jj, nm in enumerate(('freq_1', 'b_f1', 'freq_2', 'b_f2')):
        mv[0:64, jj] = inp[nm][0]
        mv[64:128, jj] = inp[nm][0]
    m['mlpv'] = mv
    w3 = inp['w_f3'][0].reshape(64, 2, 2, 8, 64)
    W3 = np.zeros((128, 8, 2, 2, 64), f32)
    for dd in range(2):
        W3[64 * dd:64 * (dd + 1), :, :, dd, :] = w3[:, :, dd, :, :].transpose(0, 2, 1, 3)
    m['W3blk'] = W3.reshape(128, 2048)
    C = host_constants()
    for nm in CONST_NAMES:
        m[nm] = C[nm]
    return m


IN_SHAPES = {
    'x': ([L, D], F32), 'w_in': ([D, 5280], F32), 'gT': ([128, 8], F32), 'bgT': ([128, 16], F32),
    'w_uq': ([384, 768], F32), 'w_ukv': ([256, 1024], F32), 'gcqT': ([128, 3], F32), 'gckvT': ([128, 2], F32),
    'gqk': ([1, 192], F32), 'w_attn_out': ([512, D], F32), 'w_hy_out': ([512, D], F32), 'w_out': ([D, D], F32),
    'cosT': ([128, NT * 16], F32), 'sinT': ([128, NT * 16], F32),
    'wsh': ([128, 48], F32), 'biasT': ([128, 256], F32), 'W1blk': ([128, 128], F32), 'W2blk': ([128, 128], F32),
    'mlpv': ([128, 4], F32), 'W3blk': ([128, 2048], F32),
    'FA1': ([128, 256], BF), 'FA2': ([128, 256], BF), 'WBr': ([128, 128], BF), 'WBi': ([128, 128], BF),
    'WBni': ([128, 128], BF), 'WI1': ([128, 256], BF), 'WI2': ([128, 256], BF), 'FIr': ([128, 128], BF),
    'FIi': ([128, 128], BF), 'TWa': ([128, 256], F32), 'TWb': ([128, 256], F32), 'TIa': ([128, 256], F32),
    'TIb': ([128, 256], F32), 'zs_hi': ([128, L], BF), 'zs_lo': ([128, L], BF), 'tfull': ([128, 64], F32),
    'negd': ([1, HYW], F32),
}
CONST_NAMES = ('cosT', 'sinT', 'FA1', 'FA2', 'WBr', 'WBi', 'WBni', 'WI1', 'WI2', 'FIr', 'FIi', 'TWa', 'TWb', 'TIa', 'TIb',
               'zs_hi', 'zs_lo', 'tfull', 'negd')


def build_nc(debug=None):
    debug = debug or set()
    nc = bass.Bass("TRN2", target_bir_lowering=False)
    T = {}
    for name, (shape, dt) in IN_SHAPES.items():
        T[name] = nc.dram_tensor(name, shape, dt, kind="ExternalInput").ap()
    T['out'] = nc.dram_tensor("out", [L, D], F32, kind="ExternalOutput").ap()
    skind = dict(kind="ExternalOutput") if 'dump' in debug else {}
    T['hT_d'] = nc.dram_tensor("hT_d", [NB, 128, 8 * 512], BF, **skind).ap()
    T['at_d'] = nc.dram_tensor("at_d", [NB, 64, 8 * 512], BF, **skind).ap()
    T['h2_d'] = nc.dram_tensor("h2_d", [128, L], BF, **skind).ap()
    if 'dump' in debug:
        T['dbg_tm'] = nc.dram_tensor("dbg_tm", [128, 4096], BF, kind="ExternalOutput").ap()
        T['dbg_k'] = nc.dram_tensor("dbg_k", [128, 8192], BF, kind="ExternalOutput").ap()
        T['dbg_ks'] = nc.dram_tensor("dbg_ks", [128, 8192], BF, kind="ExternalOutput").ap()
    if 'yz_in' in debug:
        T['yz_d'] = nc.dram_tensor("yz_d", [NB, 128, 4 * 512], BF, kind="ExternalInput").ap()
    else:
        T['yz_d'] = nc.dram_tensor("yz_d", [NB, 128, 4 * 512], BF, **skind).ap()
    phases = debug & {'p1', 'p2', 'p3', 'p4'} or {'p1', 'p2', 'p3', 'p4'}
    if 'p1' in phases:
        phase1(nc, T)
    if 'p2' in phases and 'yz_in' not in debug:
        phase2(nc, T)
    if 'p3' in phases:
        phase3(nc, T)
    if 'p4' in phases:
        phase4(nc, T)
    return nc


def kernel(**inputs):
    inp = {k_: np.asarray(v) for k_, v in inputs.items()}
    nc = build_nc()
    in_maps = [prep_inputs(inp, b) for b in range(8)]
    res = run_bass_kernel_spmd(nc, in_maps, core_ids=list(range(8)))
    out = np.stack([np.asarray(r['out'], dtype=np.float32) for r in res.results], axis=0)
    return out
```

```python
import concourse.bass as bass
import concourse.mybir as mybir

_ESZ = {}


def _esize(dt):
    s = _ESZ.get(dt)
    if s is None:
        n = str(dt)
        if '32' in n:
            s = 4
        elif '16' in n:
            s = 2
        elif '8' in n:
            s = 1
        else:
            s = 4
        _ESZ[dt] = s
    return s


def footprint(ap):
    t = ap.tensor
    name = t.name
    es = _esize(ap.dtype)
    apl = ap.ap
    off = int(ap.offset) * es
    space = str(type(t).__name__)
    if 'DRam' in space:
        lo = off
        hi = off
        for st, cnt in apl:
            if cnt > 1:
                d = (cnt - 1) * st * es
                if d > 0:
                    hi += d
                else:
                    lo += d
        return (name, 0, 1, lo, hi + es)
    pstep, pcnt = apl[0]
    pstep_b = pstep * es
    if pstep_b > 0:
        p0 = off // pstep_b
        f0 = off % pstep_b
    else:
        p0 = 0
        f0 = off
    lo = f0
    hi = f0
    for st, cnt in apl[1:]:
        if cnt > 1:
            d = (cnt - 1) * st * es
            if d > 0:
                hi += d
            else:
                lo += d
    return (name, p0, p0 + pcnt, lo, hi + es)


COMPUTE = ('pe', 'act', 'dve', 'pool')
QUEUES = ('pe', 'act', 'dve', 'pool', 'sp')
QIDX = {q: i for i, q in enumerate(QUEUES)}


class _Op:
    __slots__ = ('q', 'fn', 'dma', 'idx', 'gid', 'waits_c', 'waits_d', 'signal', 'snap', 'slot', 'slot_cnt', 'prev_slot')


class Prog:
    def __init__(self, nc, dma_slots=None):
        self.nc = nc
        self.streams = {q: [] for q in QUEUES}
        self.recs = {}
        self.known = {q: [-1] * len(QUEUES) for q in QUEUES}
        self.known_dma = {q: set() for q in QUEUES}
        self.ops = []
        self.dma_slots = dma_slots or {'sp': 8, 'pool': 4, 'act': 4}
        self.dma_count = {q: 0 for q in QUEUES}
        self.dma_ops = {q: [] for q in QUEUES}
        self.n_comp = {q: 0 for q in QUEUES}

    def add(self, q, fn, reads=(), writes=(), dma=False):
        op = _Op()
        op.q = q
        op.fn = fn
        op.dma = dma
        op.gid = len(self.ops)
        op.signal = dma
        op.waits_c = []
        op.waits_d = []
        op.slot = None
        op.prev_slot = None
        stream = self.streams[q]
        if not dma:
            op.idx = self.n_comp[q]
            self.n_comp[q] += 1
        else:
            op.idx = -1
        deps_c = {}
        deps_d = set()

        def scan(fp, is_write):
            name, p0, p1, f0, f1 = fp
            lst = self.recs.get(name)
            if not lst:
                return
            for r in lst:
                (rp0, rp1, rf0, rf1, rw, rop) = r
                if not (is_write or rw):
                    continue
                if rp1 <= p0 or p1 <= rp0 or rf1 <= f0 or f1 <= rf0:
                    continue
                if rop.dma:
                    deps_d.add(rop)
                else:
                    e = rop.q
                    if deps_c.get(e, -1) < rop.idx:
                        deps_c[e] = rop.idx

        rfps = [footprint(a) for a in reads]
        wfps = [footprint(a) for a in writes]
        for fp in rfps:
            scan(fp, False)
        for fp in wfps:
            scan(fp, True)
        known = self.known[q]
        kd = self.known_dma[q]
        for e, i in deps_c.items():
            ei = QIDX[e]
            if e == q and not dma:
                if q == 'pe':
                    continue
            if i <= known[ei]:
                continue
            op.waits_c.append((e, i))
            src = self.comp_ops[e][i]
            src.signal = True
            known[ei] = i
            for k, v in enumerate(src.snap):
                if v > known[k]:
                    known[k] = v
        for d in sorted(deps_d, key=lambda o: o.gid):
            if d.gid in kd:
                continue
            op.waits_d.append(d)
            kd.add(d.gid)
            for k, v in enumerate(d.snap):
                if v > known[k]:
                    known[k] = v
        if dma:
            n = self.dma_count[q]
            R = self.dma_slots[q]
            op.slot = n % R
            op.slot_cnt = n // R + 1
            if n >= R:
                prev = self.dma_ops[q][n - R]
                op.prev_slot = prev
                kd.add(prev.gid)
            self.dma_count[q] = n + 1
            self.dma_ops[q].append(op)
        op.snap = tuple(known)
        if not dma:
            self.comp_ops[q].append(op)
        for fp, is_write in [(f, False) for f in rfps] + [(f, True) for f in wfps]:
            name, p0, p1, f0, f1 = fp
            lst = self.recs.setdefault(name, [])
            if is_write:
                lst[:] = [r for r in lst if not (r[0] >= p0 and r[1] <= p1 and r[2] >= f0 and r[3] <= f1)]
            else:
                if not dma:
                    lst[:] = [r for r in lst if not (r[4] is False and (not r[5].dma) and r[5].q == q
                                                     and r[0] == p0 and r[1] == p1 and r[2] == f0 and r[3] == f1)]
            lst.append((p0, p1, f0, f1, is_write, op))
        stream.append(op)
        self.ops.append(op)
        return op

    comp_ops = None

    def start(self):
        self.comp_ops = {q: [] for q in QUEUES}

    def pe(self, fn, reads, writes):
        return self.add('pe', fn, reads, writes)

    def act(self, fn, reads, writes):
        return self.add('act', fn, reads, writes)

    def dve(self, fn, reads, writes):
        return self.add('dve', fn, reads, writes)

    def pool(self, fn, reads, writes):
        return self.add('pool', fn, reads, writes)

    def dma(self, out, in_, q='sp', **kw):
        return self.add(q, lambda e: e.dma_start(out=out, in_=in_, **kw), [in_], [out], dma=True)

    def emit(self, block, sems_c, sems_d):
        cum = {}
        for e in QUEUES:
            c = 0
            arr = []
            for o in self.comp_ops[e]:
                if o.signal:
                    c += 1
                arr.append(c)
            cum[e] = arr
        self.cum = cum

        def gen(q):
            def body(eng):
                for o in self.streams[q]:
                    for (e, i) in o.waits_c:
                        eng.wait_ge(sems_c[e], cum[e][i])
                    for d in o.waits_d:
                        eng.wait_ge(sems_d[d.q][d.slot], 16 * d.slot_cnt)
                    if o.prev_slot is not None:
                        p = o.prev_slot
                        eng.wait_ge(sems_d[p.q][p.slot], 16 * p.slot_cnt)
                    ins = o.fn(eng)
                    if o.dma:
                        ins.then_inc(sems_d[q][o.slot], 16)
                    elif o.signal:
                        ins.then_inc(sems_c[q], 1)
                R = self.dma_slots.get(q, 0)
                n = self.dma_count[q]
                for o in self.dma_ops[q][max(0, n - R):]:
                    eng.wait_ge(sems_d[q][o.slot], 16 * o.slot_cnt)
            return body

        if self.streams['pe']:
            block.tensor(gen('pe'))
        if self.streams['act']:
            block.scalar(gen('act'))
        if self.streams['dve']:
            block.vector(gen('dve'))
        if self.streams['pool']:
            block.gpsimd(gen('pool'))
        if self.streams['sp']:
            block.sync(gen('sp'))

import math
from contextlib import ExitStack
import numpy as np
import ml_dtypes
from concourse.bass_utils import run_bass_kernel_spmd

F32 = mybir.dt.float32
BF = mybir.dt.bfloat16
AF = mybir.ActivationFunctionType
ALU = mybir.AluOpType
AX = mybir.AxisListType

L = 4096
D = 1024
NT = 32
NB = 8
EPS = 1e-6
NF = 8192
HYW = 512
COL_V, COL_X1, COL_X2, COL_ZH = 0, 512, 1024, 1536
COL_CQ, COL_CKV, COL_KR, COL_ZA = 2048, 2432, 2688, 2720
COL_GH, COL_GA = 3232, 4256
MAGIC = 12582912.0
DBG = {}


def bcast(ap, axis, n):
    a = ap.unsqueeze(axis)
    shp = list(a.shape)
    shp[axis] = n
    return a.to_broadcast(shp)


class K:
    def __init__(self, P):
        self.P = P

    def mm(self, out, lhsT, rhs, start=True, stop=True):
        self.P.pe(lambda e: e.matmul(out, lhsT=lhsT, rhs=rhs, start=start, stop=stop), [lhsT, rhs], [out])

    def tr(self, out, in_, ident):
        self.P.pe(lambda e: e.transpose(out=out, in_=in_, identity=ident), [in_, ident], [out])

    def act(self, out, in_, func, bias=None, scale=None, accum_out=None):
        kw = {}
        reads = [in_]
        writes = [out]
        if bias is not None:
            kw['bias'] = bias
            if not isinstance(bias, (int, float)):
                reads.append(bias)
        if scale is not None:
            kw['scale'] = scale
            if not isinstance(scale, (int, float)):
                reads.append(scale)
        if accum_out is not None:
            kw['accum_out'] = accum_out
            writes.append(accum_out)
        self.P.act(lambda e: e.activation(out=out, in_=in_, func=func, **kw), reads, writes)

    def tt(self, eng, out, in0, in1, op):
        self.P.add(eng, lambda e: e.tensor_tensor(out=out, in0=in0, in1=in1, op=op), [in0, in1], [out])

    def ts(self, eng, out, in0, s1, s2=None, op0=ALU.mult, op1=None):
        reads = [in0]
        if not isinstance(s1, (int, float)):
            reads.append(s1)
        if s2 is not None and not isinstance(s2, (int, float)):
            reads.append(s2)
        if op1 is None:
            self.P.add(eng, lambda e: e.tensor_scalar(out=out, in0=in0, scalar1=s1, scalar2=None, op0=op0), reads, [out])
        else:
            self.P.add(eng, lambda e: e.tensor_scalar(out=out, in0=in0, scalar1=s1, scalar2=s2, op0=op0, op1=op1), reads, [out])

    def stt(self, out, in0, scalar, in1, op0, op1):
        reads = [in0, in1]
        if not isinstance(scalar, (int, float)):
            reads.append(scalar)
        self.P.dve(lambda e: e.scalar_tensor_tensor(out=out, in0=in0, scalar=scalar, in1=in1, op0=op0, op1=op1), reads, [out])

    def copy(self, eng, out, in_):
        if eng == 'act':
            self.act(out, in_, AF.Copy)
        else:
            self.P.add(eng, lambda e: e.tensor_copy(out=out, in_=in_), [in_], [out])

    def recip(self, out, in_):
        self.P.dve(lambda e: e.reciprocal(out=out, in_=in_), [in_], [out])

    def reduce_add(self, out, in_):
        self.P.dve(lambda e: e.tensor_reduce(out=out, in_=in_, axis=AX.X, op=ALU.add), [in_], [out])

    def memset(self, eng, ap, val):
        self.P.add(eng, lambda e: e.memset(ap, val), [], [ap])

    def dma(self, out, in_, q='sp'):
        self.P.dma(out, in_, q=q)


def _dump(P):
    cum = {}
    for e in QUEUES:
        c = 0
        arr = []
        for o in P.comp_ops[e]:
            if o.signal:
                c += 1
            arr.append(c)
        cum[e] = arr
    for q in QUEUES:
        print("== stream", q)
        for o in P.streams[q]:
            w = [f"{e}>={cum[e][i]}(op{i})" for e, i in o.waits_c] + [f"dma[{d.q}{d.slot}]>={16*d.slot_cnt}" for d in o.waits_d]
            if o.prev_slot is not None:
                w.append(f"prev dma[{o.prev_slot.q}{o.prev_slot.slot}]>={16*o.prev_slot.slot_cnt}")
            tag = f"DMA slot{o.slot} cnt{o.slot_cnt}" if o.dma else (f"op{o.idx} sig={cum[q][o.idx] if o.signal else '-'}")
            print("   ", tag, getattr(o, 'desc', ''), "waits:", w)


class Phase:
    def __init__(self, nc, name):
        self.nc = nc
        self.name = name
        self.es = ExitStack()

    def __enter__(self):
        nc = self.nc
        es = self.es
        es.__enter__()
        self.ps = [es.enter_context(nc.psum_tensor(f"{self.name}_ps{i}", [128, 512], F32)) for i in range(8)]
        self.sems_c = {e: es.enter_context(nc.semaphore(f"{self.name}_sc_{e}")) for e in QUEUES}
        self.sems_d = {q: [es.enter_context(nc.semaphore(f"{self.name}_sd_{q}{i}")) for i in range(n)]
                       for q, n in (('sp', 8), ('pool', 4), ('act', 4))}
        self.P = Prog(nc)
        self.P.start()
        self.k = K(self.P)
        return self

    def sb(self, name, shape, dt):
        return self.es.enter_context(self.nc.sbuf_tensor(f"{self.name}_{name}", shape, dt))

    def __exit__(self, *a):
        if a[0] is None:
            self.es.enter_context(self.nc.allow_low_precision("bf16 operands / intermediates by design"))
            block = self.es.enter_context(self.nc.Block())
            if DBG.get('dump') == self.name:
                _dump(self.P)
            self.P.emit(block, self.sems_c, self.sems_d)
        return self.es.__exit__(*a)


def make_ident(ph, ident):
    identf = ph.sb("identf", [128, 128], F32)
    ph.k.memset('pool', identf[:], 0.0)
    ph.P.pool(lambda e: e.affine_select(out=identf[:], in_=identf[:], pattern=[[-1, 128]], compare_op=ALU.not_equal,
                                        fill=1.0, base=0, channel_multiplier=1), [identf[:]], [identf[:]])
    ph.k.copy('dve', ident[:], identf[:])


def load_w(ph, dst, src_ap):
    ph.k.dma(dst, src_ap, q='pool')


def rstd_from_ss(k, rs_col, ss_col, n):
    k.act(rs_col, ss_col, AF.Sqrt, bias=EPS, scale=1.0 / n)
    k.recip(rs_col, rs_col)


def phase1_gen(ph, T, banks):
    k = ph.k
    xt = [ph.sb(f"xt{i}", [128, D], F32) for i in range(3)]
    xn = [ph.sb(f"xn{i}", [128, D], BF) for i in range(2)]
    junk = ph.sb("junk", [128, D], BF)
    ss = ph.sb("ss", [128, NT], F32)
    rs = ph.sb("rs", [128, NT], F32)
    ts_ = ph.sb("ts_", [128, NT], F32)
    mh = ph.sb("mh", [128, 1], F32)
    gT = ph.sb("gT", [128, 8], F32)
    ident = ph.sb("ident", [128, 128], BF)
    hb = [ph.sb(f"hb{i}", [128, 8, 512], BF) for i in range(2)]
    make_ident(ph, ident)
    k.memset('pool', mh[:], -0.5)
    k.dma(gT[:], T['gT'][:, :])
    for i in range(NT):
        x_t = xt[i % 3]
        k.dma(x_t[:], T['x'][128 * i:128 * (i + 1), :])
        k.act(junk[:], x_t[:], AF.Square, accum_out=ss[:, i:i + 1])
        rstd_pool(k, rs[:, i:i + 1], ss[:, i:i + 1], D, mh[:, 0:1], ts_[:, i:i + 1])
        x_n = xn[i % 2]
        k.ts('dve', x_n[:], x_t[:], rs[:, i:i + 1])
        bank = banks[i % len(banks)]
        pv = bank[:, :].bitcast(BF)
        for c in range(8):
            k.tr(pv[:, 128 * c:128 * (c + 1)], x_n[:, 128 * c:128 * (c + 1)], ident[:])
        h_b = hb[(i // 4) % 2]
        j = i % 4
        k.tt('dve', h_b[:, :, 128 * j:128 * (j + 1)], pv.rearrange("p (c t) -> p c t", c=8),
             bcast(gT[:, :], 2, 128), ALU.mult)
        if j == 3:
            k.dma(T['hT_d'][i // 4], h_b[:].rearrange("p c t -> p (c t)"))
        yield


def rstd_pool(k, rs, ss, n, mhalf, tmp):
    k.ts('dve', tmp, ss, 1.0 / n, EPS, op0=ALU.mult, op1=ALU.add)
    k.tt('pool', rs, tmp, mhalf, ALU.pow)


def phase12(nc, T, with_p1=True, with_p2a=True):
    with Phase(nc, "p12") as ph:
        g1 = phase1_gen(ph, T, ph.ps[0:4]) if with_p1 else iter(())
        g2 = phase2a_gen(ph, T, ph.ps[4:8]) if with_p2a else iter(())
        alive1, alive2 = True, True
        while alive1 or alive2:
            for _ in range(4):
                if alive1:
                    try:
                        next(g1)
                    except StopIteration:
                        alive1 = False
            if alive2:
                try:
                    next(g2)
                except StopIteration:
                    alive2 = False


def run_interleaved(gens, width=2):
    active = []
    gens = list(gens)
    while gens or active:
        while gens and len(active) < width:
            active.append(gens.pop(0))
        for g in list(active):
            try:
                next(g)
            except StopIteration:
                active.remove(g)


def qk_norm_rope(ph, W, src, dst, g_rep, cs_t, sc_t, mhalf, use_act=False):
    k = ph.k
    sq, ssq, rk, ta, tb_, tmp = W['sq'], W['ssq'], W['rk'], W['ta'], W['tb'], W['tmp']
    if use_act:
        k.act(sq[:].rearrange("p a b -> p (a b)"), src[:].rearrange("p a b -> p (a b)"), AF.Square)
    else:
        k.tt('pool', sq[:], src[:], src[:], ALU.mult)
    k.reduce_add(ssq[:], sq[:])
    rstd_pool(k, rk[:], ssq[:], 96, mhalf[:, 0:8], tmp[:])
    yield
    k.tt('dve', src[:], src[:], bcast(rk[:, :], 2, 96), ALU.mult)
    k.tt('dve', src[:], src[:], bcast(g_rep, 1, 8), ALU.mult)
    yield
    t1 = bcast(src[:, :, 64:80], 1, 2)
    t2 = bcast(src[:, :, 80:96], 1, 2)
    k.tt('pool', ta[:], t1, bcast(cs_t, 2, 8), ALU.mult)
    k.tt('pool', tb_[:], t2, bcast(sc_t, 2, 8), ALU.mult)
    k.tt('dve', dst[:, :, 64:80], ta[:, 0], tb_[:, 0], ALU.subtract)
    k.tt('dve', dst[:, :, 80:96], ta[:, 1], tb_[:, 1], ALU.add)
    k.copy('pool', dst[:, :, 0:64], src[:, :, 0:64])
    yield


def phase3(nc, T):
    with Phase(nc, "p3") as ph:
        k = ph.k
        ps = ph.ps
        ident = ph.sb("ident", [128, 128], BF)
        make_ident(ph, ident)
        w_in_v = T['w_in'].rearrange("(k p) n -> p k n", p=128)
        Wkv = ph.sb("Wkv", [128, 8, 288], BF)
        Wq = ph.sb("Wq", [128, 8, 384], BF)
        Wuq = ph.sb("Wuq", [128, 3, 768], BF)
        Wukv = ph.sb("Wukv", [128, 2, 1024], BF)
        gcq = ph.sb("gcq", [128, 3], F32)
        gckv = ph.sb("gckv", [128, 2], F32)
        gqk = ph.sb("gqk", [128, 192], F32)
        csT = ph.sb("csT", [128, NT, 2, 16], F32)
        scT = ph.sb("scT", [128, NT, 2, 16], F32)
        mhalf = ph.sb("mhalf", [128, 8], F32)
        k.memset('pool', mhalf[:], -0.5)
        load_w(ph, Wkv[:], w_in_v[:, :, COL_CKV:COL_CKV + 288])
        load_w(ph, Wq[:], w_in_v[:, :, COL_CQ:COL_CQ + 384])
        load_w(ph, Wuq[:], T['w_uq'].rearrange("(k p) n -> p k n", p=128))
        load_w(ph, Wukv[:], T['w_ukv'].rearrange("(k p) n -> p k n", p=128))
        k.dma(gcq[:], T['gcqT'][:, :])
        k.dma(gckv[:], T['gckvT'][:, :])
        k.dma(gqk[:], T['gqk'][0:1, :].partition_broadcast(128))
        cosv = T['cosT'].rearrange("p (a b) -> p a b", b=16)
        sinv = T['sinT'].rearrange("p (a b) -> p a b", b=16)
        k.dma(csT[:, :, 0, :], cosv)
        k.dma(csT[:, :, 1, :], sinv)
        k.dma(scT[:, :, 0, :], sinv)
        k.dma(scT[:, :, 1, :], cosv)
        k.tt('pool', Wuq[:], Wuq[:], bcast(gcq[:, :], 2, 768), ALU.mult)
        k.tt('pool', Wukv[:], Wukv[:], bcast(gckv[:, :], 2, 1024), ALU.mult)

        kT = ph.sb("kT", [128, 8, L], BF)
        vx = ph.sb("vx", [128, NT, 8, 65], BF)
        k.memset('pool', vx[:, :, :, 64:65], 1.0)
        ones = ph.sb("ones", [128, 64], BF)
        k.memset('pool', ones[:], 1.0)
        hbuf = [ph.sb(f"hbuf{i}", [128, 8, 512], BF) for i in range(2)]
        junk = [ph.sb(f"junk{i}", [128, 384], BF) for i in range(2)]
        ssl = ph.sb("ssl", [128, 2 * NT], F32)
        rsl = ph.sb("rsl", [128, 2 * NT], F32)
        tsl = ph.sb("tsl", [128, 2 * NT], F32)
        latn = [ph.sb(f"latn{i}", [128, 384], BF) for i in range(2)]
        latT = [ph.sb(f"latT{i}", [128, 3, 128], BF) for i in range(2)]
        kr = [ph.sb(f"kr{i}", [128, 32], F32) for i in range(2)]
        qk32 = [ph.sb(f"qk32_{i}", [128, 8, 96], F32) for i in range(2)]
        qkbf = [ph.sb(f"qkbf{i}", [128, 8, 96], BF) for i in range(2)]
        Wk_ = [dict(sq=ph.sb(f"sq{i}", [128, 8, 96], F32), ssq=ph.sb(f"ssq{i}", [128, 8], F32),
                    rk=ph.sb(f"rk{i}", [128, 8], F32), tmp=ph.sb(f"tmpn{i}", [128, 8], F32),
                    ta=ph.sb(f"ta{i}", [128, 2, 8, 16], F32), tb=ph.sb(f"tb{i}", [128, 2, 8, 16], F32)) for i in range(2)]
        qT = [ph.sb(f"qT{i}", [128, 8, 512], BF) for i in range(2)]
        pt = [ph.sb(f"pt{i}", [128, 512], BF) for i in range(4)]
        rsum = [ph.sb(f"rsum{i}", [128, 512], BF) for i in range(2)]
        bcs = [ph.sb(f"bcs{i}", [64, 512], BF) for i in range(2)]
        at = [ph.sb(f"at{i}", [64, 8, 512], BF) for i in range(2)]

        def sumsq(i, col, src_ps, n):
            k.act(junk[i % 2][:, 0:n], src_ps, AF.Square, accum_out=ssl[:, col:col + 1])
            rstd_pool(k, rsl[:, col:col + 1], ssl[:, col:col + 1], n, mhalf[:, 0:1], tsl[:, col:col + 1])

        def kv_tile(i, hb, j):
            par = i % 2
            if j == 0:
                k.dma(hb[:].rearrange("p c t -> p (c t)"), T['hT_d'][i // 4])
            lat = ps[par][:, 0:288]
            for c in range(8):
                k.mm(lat, hb[:, c, 128 * j:128 * (j + 1)], Wkv[:, c, :], start=(c == 0), stop=(c == 7))
            sumsq(i, i, lat[:, 0:256], 256)
            yield
            ln = latn[par]
            k.ts('dve', ln[:, 0:256], lat[:, 0:256], rsl[:, i:i + 1])
            k.copy('dve', kr[par][:], lat[:, 256:288])
            tbank = ps[2][:, :].bitcast(BF)
            for c in range(2):
                k.tr(tbank[:, 128 * c:128 * (c + 1)], ln[:, 128 * c:128 * (c + 1)], ident[:])
            lT = latT[par]
            k.copy('dve', lT[:, 0:2, :].rearrange("p c t -> p (c t)"), tbank[:, 0:256])
            yield
            kvb = [ps[3 + 2 * par], ps[4 + 2 * par]]
            for half in range(2):
                for c in range(2):
                    k.mm(kvb[half][:, :], lT[:, c, :], Wukv[:, c, 512 * half:512 * (half + 1)],
                         start=(c == 0), stop=(c == 1))
            kk = qk32[par]
            for half in range(2):
                kvv = kvb[half][:, :].rearrange("p (h e) -> p h e", h=4)
                k.copy('dve', kk[:, 4 * half:4 * half + 4, 0:64], kvv[:, :, 0:64])
                k.copy('dve', vx[:, i, 4 * half:4 * half + 4, 0:64], kvv[:, :, 64:128])
            k.copy('pool', kk[:, :, 64:96], bcast(kr[par][:, :], 1, 8))
            yield
            kf = qkbf[par]
            yield from qk_norm_rope(ph, Wk_[par], kk, kf, gqk[:, 96:192], csT[:, i, :, :], scT[:, i, :, :], mhalf, use_act=True)
            kbank = ps[7][:, :].bitcast(BF)
            for h in range(8):
                k.tr(kbank[0:96, 128 * h:128 * (h + 1)], kf[:, h, :], ident[:])
            k.copy('dve', kT[0:96, :, 128 * i:128 * (i + 1)], kbank[0:96, :].rearrange("p (h t) -> p h t", h=8))
            yield

        def q_tile(i, hb, j, q_T):
            par = i % 2
            lat = ps[6][:, 0:384]
            for c in range(8):
                k.mm(lat, hb[:, c, 128 * j:128 * (j + 1)], Wq[:, c, :], start=(c == 0), stop=(c == 7))
            yield
            lsb = Wk_[par]['sq'][:].rearrange("p a b -> p (a b)")[:, 0:384]
            k.copy('dve', lsb, lat)
            k.P.dve(lambda e: e.scalar_tensor_tensor(out=junk[par][:, 0:384], in0=lsb, scalar=1.0, in1=lsb, op0=ALU.mult,
                                                     op1=ALU.mult, accum_out=ssl[:, NT + i:NT + i + 1]),
                    [lsb], [junk[par][:, 0:384], ssl[:, NT + i:NT + i + 1]])
            rstd_pool(k, rsl[:, NT + i:NT + i + 1], ssl[:, NT + i:NT + i + 1], 384, mhalf[:, 0:1], tsl[:, NT + i:NT + i + 1])
            yield
            ln = latn[par]
            k.ts('dve', ln[:, 0:384], lsb, rsl[:, NT + i:NT + i + 1])
            yield
            tbank = ps[7][:, :].bitcast(BF)
            for c in range(3):
                k.tr(tbank[:, 128 * c:128 * (c + 1)], ln[:, 128 * c:128 * (c + 1)], ident[:])
            yield
            lT = latT[par]
            k.copy('dve', lT[:].rearrange("p c t -> p (c t)"), tbank[:, 0:384])
            yield
            qq = qk32[par]
            for half in range(2):
                qb = ps[6][:, 0:384]
                for c in range(3):
                    k.mm(qb, lT[:, c, :], Wuq[:, c, 384 * half:384 * (half + 1)], start=(c == 0), stop=(c == 2))
                yield
                k.copy('dve', qq[:, 4 * half:4 * half + 4, :], qb.rearrange("p (h e) -> p h e", h=4))
                yield
            qf = qkbf[par]
            yield from qk_norm_rope(ph, Wk_[par], qq, qf, gqk[:, 0:96], csT[:, i, :, :], scT[:, i, :, :], mhalf, use_act=(i < 4))
            yield
            yield
            yield
            yield
            qbank = ps[7][:, :].bitcast(BF)
            for h in range(8):
                k.tr(qbank[0:96, 128 * h:128 * (h + 1)], qf[:, h, :], ident[:])
            yield
            k.copy('dve', q_T[0:96, :, 128 * j:128 * (j + 1)], qbank[0:96, :].rearrange("p (h t) -> p h t", h=8))
            yield

        def kv_block(tb):
            hb = hbuf[tb % 2]
            return [kv_tile(tb * 4 + j, hb, j) for j in range(4)]

        def q_chunk_gens(qc):
            hb = hbuf[qc % 2]
            k.dma(hb[:].rearrange("p c t -> p (c t)"), T['hT_d'][qc])
            return [q_tile(qc * 4 + j, hb, j, qT[qc % 2]) for j in range(4)]

        gens = []
        for tb in range(DBG.get('nprep', NB)):
            gens += kv_block(tb)
        run_interleaved(gens, 2)
        nqc = DBG.get('nqc', NB)
        if nqc:
            run_interleaved(q_chunk_gens(0), 1)

        scale = 1.0 / math.sqrt(96.0)
        NH = DBG.get('nh', 8)
        pend = [None]
        for qc in range(nqc):
            q_T = qT[qc % 2]
            a_t = at[qc % 2]
            nxt = q_chunk_gens(qc + 1) if qc + 1 < nqc else []
            nxt_active = []
            steps = [(h, kt) for h in range(NH) for kt in range(NT)]

            def S(idx):
                h, kt = steps[idx]
                k.mm(ps[idx % 3][:, :], kT[0:96, h, 128 * kt:128 * (kt + 1)], q_T[0:96, h, :])

            def fin_a(h):
                k.recip(rsum[h % 2][64:65, :], ps[3 + (h % 2)][64:65, :])

            def fin_b(h):
                k.mm(ps[5][0:64, :], ones[64:65, :], rsum[h % 2][64:65, :])

            def fin_c(h, a_t):
                k.copy('dve', bcs[h % 2][:], ps[5][0:64, :])
                k.tt('dve', a_t[:, h, :], ps[3 + (h % 2)][0:64, :], bcs[h % 2][:], ALU.mult)

            S(0)
            S(1)
            for idx, (h, kt) in enumerate(steps):
                p_t = pt[idx % 4]
                k.act(p_t[:], ps[idx % 3][:, :], AF.Exp, scale=scale)
                if idx + 2 < len(steps):
                    S(idx + 2)
                ob = ps[3 + (h % 2)]
                k.mm(ob[0:65, :], vx[:, kt, h, :], p_t[:], start=(kt == 0), stop=(kt == NT - 1))
                if pend[0] is not None:
                    if kt == 1:
                        pend[0][0]()
                    elif kt == 10:
                        pend[0][1]()
                    elif kt == 14:
                        pend[0][2]()
                        pend[0] = None
                if kt == NT - 1:
                    last = (h == NH - 1)
                    pend[0] = (lambda h=h: fin_a(h), lambda h=h: fin_b(h),
                               (lambda h=h, a_t=a_t, qc=qc, last=last, fin_c=fin_c: (fin_c(h, a_t), k.dma(T['at_d'][qc], a_t[:].rearrange("p h t -> p (h t)")) if last else None)))
                if idx % 3 == 2:
                    while nxt and len(nxt_active) < 1:
                        nxt_active.append(nxt.pop(0))
                    for g in list(nxt_active):
                        try:
                            next(g)
                        except StopIteration:
                            nxt_active.remove(g)
            run_interleaved(nxt_active + nxt, 1)

        if pend[0] is not None:
            pend[0][0]()
            pend[0][1]()
            pend[0][2]()


def phase4(nc, T):
    with Phase(nc, "p4") as ph:
        k = ph.k
        ps = ph.ps
        w_in_v = T['w_in'].rearrange("(k p) n -> p k n", p=128)
        Wz = ph.sb("Wz", [128, 8, 512], BF)
        Wg = ph.sb("Wg", [128, 8, 2048], BF)
        Wao = ph.sb("Wao", [128, 4, D], BF)
        Who = ph.sb("Who", [128, 4, D], BF)
        Wout = ph.sb("Wout", [128, 8, D], BF)
        bg = ph.sb("bg", [128, 16], F32)
        load_w(ph, Wz[:], w_in_v[:, :, COL_ZA:COL_ZA + 512])
        for q4 in range(4):
            load_w(ph, Wg[:, :, 512 * q4:512 * (q4 + 1)], w_in_v[:, :, COL_GH + 512 * q4:COL_GH + 512 * (q4 + 1)])
        load_w(ph, Wao[:], T['w_attn_out'].rearrange("(hp p) n -> p hp n", p=128))
        load_w(ph, Who[:], T['w_hy_out'].rearrange("(k p) n -> p k n", p=128))
        load_w(ph, Wout[:], T['w_out'].rearrange("(k p) n -> p k n", p=128))
        k.dma(bg[:], T['bgT'][:, :])
        hbuf = [ph.sb(f"hbuf{i}", [128, 8, 512], BF) for i in range(2)]
        atb = [ph.sb(f"atb{i}", [128, 4, 512], BF) for i in range(2)]
        yzb = [ph.sb(f"yzb{i}", [128, 4, 512], BF) for i in range(2)]
        xt = [ph.sb(f"xt{i}", [128, D], F32) for i in range(3)]
        ot = [ph.sb(f"ot{i}", [128, D], F32) for i in range(2)]
        sz = [ph.sb(f"sz{i}", [128, 512], BF) for i in range(2)]
        ya = ph.sb("ya", [128, 4, 512], BF)
        gh = [ph.sb(f"gh{i}", [128, 512], BF) for i in range(2)]
        ga = [ph.sb(f"ga{i}", [128, 512], BF) for i in range(2)]
        m1 = [ph.sb(f"m1{i}", [128, 512], F32) for i in range(2)]
        m2 = [ph.sb(f"m2{i}", [128, 512], F32) for i in range(2)]
        mg = [ph.sb(f"mg{i}", [128, 8, 512], BF) for i in range(2)]
        for tb in range(NB):
            hb = hbuf[tb % 2]
            a_b = atb[tb % 2]
            y_b = yzb[tb % 2]
            k.dma(hb[:].rearrange("p c t -> p (c t)"), T['hT_d'][tb])
            atv = T['at_d'][tb].rearrange("p (hp two t) -> p two hp t", two=2, t=512)
            k.dma(a_b[0:64, :, :], atv[:, 0, :, :])
            k.dma(a_b[64:128, :, :], atv[:, 1, :, :])
            k.dma(y_b[:].rearrange("p c t -> p (c t)"), T['yz_d'][tb])
            for hp in range(4):
                zb = ps[hp % 2][:, :]
                for c in range(8):
                    k.mm(zb, Wz[:, c, 128 * hp:128 * (hp + 1)], hb[:, c, :], start=(c == 0), stop=(c == 7))
                s_z = sz[hp % 2]
                k.act(s_z[:], zb, AF.Silu)
                k.tt('pool', ya[:, hp, :], a_b[:, hp, :], s_z[:], ALU.mult)
            m_g = mg[tb % 2]
            for dc in range(8):
                g1 = ps[2 + (dc % 2)]
                g2 = ps[4 + (dc % 2)]
                for c in range(8):
                    k.mm(g1[:, :], Wg[:, c, 128 * dc:128 * (dc + 1)], hb[:, c, :], start=(c == 0), stop=(c == 7))
                for c in range(8):
                    k.mm(g2[:, :], Wg[:, c, 1024 + 128 * dc:1024 + 128 * (dc + 1)], hb[:, c, :],
                         start=(c == 0), stop=(c == 7))
                k.act(gh[dc % 2][:], g1[:, :], AF.Sigmoid, bias=bg[:, dc:dc + 1])
                k.act(ga[dc % 2][:], g2[:, :], AF.Sigmoid, bias=bg[:, 8 + dc:9 + dc])
                uh = ps[6]
                ua = ps[7]
                for c in range(4):
                    k.mm(uh[:, :], Who[:, c, 128 * dc:128 * (dc + 1)], y_b[:, c, :], start=(c == 0), stop=(c == 3))
                for hp in range(4):
                    k.mm(ua[:, :], Wao[:, hp, 128 * dc:128 * (dc + 1)], ya[:, hp, :], start=(hp == 0), stop=(hp == 3))
                k.tt('dve', m1[dc % 2][:], uh[:, :], gh[dc % 2][:], ALU.mult)
                k.tt('dve', m2[dc % 2][:], ua[:, :], ga[dc % 2][:], ALU.mult)
                k.tt('pool', m_g[:, dc, :], m1[dc % 2][:], m2[dc % 2][:], ALU.add)
            for j in range(4):
                i = tb * 4 + j
                x_t = xt[i % 3]
                k.dma(x_t[:], T['x'][128 * i:128 * (i + 1), :])
                o_t = ot[i % 2]
                for half in range(2):
                    fb = ps[half]
                    for c in range(8):
                        k.mm(fb[:, :], m_g[:, c, 128 * j:128 * (j + 1)], Wout[:, c, 512 * half:512 * (half + 1)],
                             start=(c == 0), stop=(c == 7))
                    k.tt('dve', o_t[:, 512 * half:512 * (half + 1)], fb[:, :], x_t[:, 512 * half:512 * (half + 1)], ALU.add)
                k.dma(T['out'][128 * i:128 * (i + 1), :], o_t[:])


def fft_constants():
    C = {}
    n = NF
    s2 = np.arange(128, dtype=np.float64)[:, None]
    f2 = np.arange(128, dtype=np.float64)[None, :]
    th = 2 * np.pi * (f2 + 0.5) * s2 / 256.0
    C['FA1'] = np.concatenate([np.cos(th), -np.sin(th)], 1)
    th2 = 2 * np.pi * (f2 + 0.5) * (s2 + 128) / 256.0
    C['FA2'] = -np.concatenate([np.cos(th2), -np.sin(th2)], 1)
    s1 = np.arange(32, dtype=np.float64)
    tw = np.exp(-2j * np.pi * (np.arange(128)[None, :] + 0.5) * s1[:, None] / n)
    twq = np.tile(tw, (4, 1))
    C['TWa'] = np.concatenate([twq.real, twq.real], 1)
    C['TWb'] = np.concatenate([-twq.imag, twq.imag], 1)
    W = np.exp(-2j * np.pi * np.outer(s1, s1) / 32.0)
    Wq = np.kron(np.eye(4), W)
    C['WBr'] = Wq.real
    C['WBi'] = Wq.imag
    C['WBni'] = -Wq.imag
    Wi = np.exp(2j * np.pi * np.outer(s1, s1) / 32.0)
    Wiq = np.kron(np.eye(4), Wi)
    C['WI1'] = np.concatenate([Wiq.real, Wiq.imag], 1)
    C['WI2'] = np.concatenate([-Wiq.imag, Wiq.real], 1)
    twi = np.exp(2j * np.pi * (np.arange(128)[:, None] + 0.5) * s1[None, :] / n)
    twiq = np.tile(twi, (1, 4))
    C['TIa'] = np.concatenate([twiq.real, twiq.real], 1)
    C['TIb'] = np.concatenate([-twiq.imag, twiq.imag], 1)
    t2 = np.arange(128, dtype=np.float64)[None, :]
    f2c = np.arange(128, dtype=np.float64)[:, None]
    th3 = 2 * np.pi * (f2c + 0.5) * t2 / 256.0
    C['FIr'] = (2.0 / n) * np.cos(th3)
    C['FIi'] = -(2.0 / n) * np.sin(th3)
    return C


def filter_constants():
    C = {}
    f32 = np.float32
    t = np.linspace(0.0, 1.0, L, dtype=f32)[:, None]
    bands = 16
    f = np.linspace(1e-4, bands - 1, bands, dtype=f32)
    ang = (f32(2.0 * np.pi / L) * np.arange(L, dtype=f32)[:, None] * f[None, :]).astype(f32)
    z = np.concatenate([t, np.cos(ang).astype(f32), -np.sin(ang).astype(f32)], axis=-1).astype(f32)
    zs = np.zeros((128, L), f32)
    zs[0:33, :] = z.T
    zs[64:97, :] = z[::-1].T
    hi = zs.astype(ml_dtypes.bfloat16)
    lo = (zs - hi.astype(f32)).astype(ml_dtypes.bfloat16)
    C['zs_hi'] = hi
    C['zs_lo'] = lo
    tl = t[:, 0]
    tf = np.zeros((128, 2, 32), f32)
    pidx = np.arange(128)[:, None] * 32 + np.arange(32)[None, :]
    tf[:, 0, :] = tl[pidx]
    tf[:, 1, :] = tl[4095 - pidx]
    C['tfull'] = tf.reshape(128, 64)
    MIN_DECAY = math.log(1e-2) / 1.5
    MAX_DECAY = math.log(1e-2) / 0.3
    deltas = np.abs(np.linspace(MIN_DECAY, MAX_DECAY, HYW, dtype=f32)).astype(f32)
    C['negd'] = (-deltas)[None, :].astype(f32)
    return C


def _sin_layer(ph, W, pre_ps, fr, fb, out32):
    k = ph.k
    a, kk = W['a'], W['kk']
    k.ts('dve', a[:], pre_ps, fr, fb, op0=ALU.mult, op1=ALU.add)
    k.ts('dve', kk[:], a[:], 1.0 / (2 * math.pi), MAGIC, op0=ALU.mult, op1=ALU.add)
    k.ts('dve', kk[:], kk[:], -MAGIC, None, op0=ALU.add)
    k.stt(a[:], kk[:], -2 * math.pi, a[:], ALU.mult, ALU.add)
    k.ts('dve', a[:], a[:], -3.14159, 3.14159, op0=ALU.max, op1=ALU.min)
    k.act(out32, a[:], AF.Sin)


def _hilo(ph, hi, lo, src32, tmp32):
    k = ph.k
    k.copy('dve', hi, src32)
    k.copy('pool', tmp32, hi)
    k.tt('pool', lo, src32, tmp32, ALU.subtract)


def phase2a_gen(ph, T, banks):
    k = ph.k
    zs_hi = ph.sb("zs_hi", [128, L], BF)
    zs_lo = ph.sb("zs_lo", [128, L], BF)
    W1 = ph.sb("W1", [128, 128], F32)
    W2 = ph.sb("W2", [128, 128], F32)
    W1h = ph.sb("W1h", [128, 128], BF)
    W1l = ph.sb("W1l", [128, 128], BF)
    W2h = ph.sb("W2h", [128, 128], BF)
    W2l = ph.sb("W2l", [128, 128], BF)
    wt = ph.sb("wt", [128, 128], F32)
    mv = ph.sb("mv", [128, 4], F32)
    fb = ph.sb("fb", [128, 2], F32)
    Wk = dict(a=ph.sb("a", [128, 512], F32), kk=ph.sb("kk", [128, 512], F32))
    h1 = ph.sb("h1", [128, 512], F32)
    h1h = ph.sb("h1h", [128, 512], BF)
    h1l = ph.sb("h1l", [128, 512], BF)
    t32 = ph.sb("t32", [128, 512], F32)
    h2 = ph.sb("h2", [128, 512], F32)
    h2b = [ph.sb(f"h2b{i}", [128, 512], BF) for i in range(2)]
    k.dma(zs_hi[:], T['zs_hi'][:, :])
    k.dma(zs_lo[:], T['zs_lo'][:, :])
    k.dma(W1[:], T['W1blk'][:, :])
    k.dma(W2[:], T['W2blk'][:, :])
    k.dma(mv[:], T['mlpv'][:, :])
    _hilo(ph, W1h[:], W1l[:], W1[:], wt[:])
    _hilo(ph, W2h[:], W2l[:], W2[:], wt[:])
    k.tt('dve', fb[:, 0:1], mv[:, 0:1], mv[:, 1:2], ALU.mult)
    k.tt('dve', fb[:, 1:2], mv[:, 2:3], mv[:, 3:4], ALU.mult)
    yield
    for cch in range(NB):
        sl = slice(512 * cch, 512 * (cch + 1))
        b1 = banks[cch % 2]
        k.mm(b1[:, :], W1h[:], zs_hi[:, sl], start=True, stop=False)
        k.mm(b1[:, :], W1h[:], zs_lo[:, sl], start=False, stop=False)
        k.mm(b1[:, :], W1l[:], zs_hi[:, sl], start=False, stop=True)
        _sin_layer(ph, Wk, b1[:, :], mv[:, 0:1], fb[:, 0:1], h1[:])
        _hilo(ph, h1h[:], h1l[:], h1[:], t32[:])
        yield
        b2 = banks[2 + cch % 2]
        k.mm(b2[:, :], W2h[:], h1h[:], start=True, stop=False)
        k.mm(b2[:, :], W2h[:], h1l[:], start=False, stop=False)
        k.mm(b2[:, :], W2l[:], h1h[:], start=False, stop=True)
        _sin_layer(ph, Wk, b2[:, :], mv[:, 2:3], fb[:, 1:2], h2[:])
        hb_ = h2b[cch % 2]
        k.copy('pool', hb_[:], h2[:])
        k.dma(T['h2_d'][:, sl], hb_[:])
        yield


def _cmul_tab(ph, W, src, Ta, Tb, out_bf):
    k = ph.k
    P1, P2 = W
    sw = src.rearrange("p (r f) -> p r f", r=2)[:, ::-1, :]
    k.tt('dve', P1[:], src, Ta, ALU.mult)
    k.tt('dve', P2[:].rearrange("p (r f) -> p r f", r=2), sw, Tb.rearrange("p (r f) -> p r f", r=2), ALU.mult)
    k.tt('pool', out_bf, P1[:], P2[:], ALU.add)


def phase2b(nc, T):
    with Phase(nc, "p2b") as ph:
        k = ph.k
        ps = ph.ps
        ident = ph.sb("ident", [128, 128], BF)
        make_ident(ph, ident)
        cb16 = {}
        for nm, w in (('FA1', 256), ('FA2', 256), ('WBr', 128), ('WBi', 128), ('WBni', 128), ('WI1', 256), ('WI2', 256),
                      ('FIr', 128), ('FIi', 128)):
            cb16[nm] = ph.sb(nm, [128, w], BF)
            k.dma(cb16[nm][:], T[nm][:, :])
        c32 = {}
        for nm in ('TWa', 'TWb', 'TIa', 'TIb'):
            c32[nm] = ph.sb(nm, [128, 256], F32)
            k.dma(c32[nm][:], T[nm][:, :])
        h2s = ph.sb("h2s", [128, L], BF)
        k.dma(h2s[:], T['h2_d'][:, :])
        h2p = ph.sb("h2p", [128, 32, 128], BF)
        k.copy('pool', h2p[:], h2s[:].rearrange("q (p s) -> q s p", s=32))
        W3 = ph.sb("W3", [128, 2048], BF)
        load_w(ph, W3[:], T['W3blk'][:, :])
        wsh = ph.sb("wsh", [128, 12, 4], F32)
        k.dma(wsh[:].rearrange("p a b -> p (a b)"), T['wsh'][:, :])
        biasT = ph.sb("biasT", [128, 2, 128], F32)
        k.dma(biasT[:].rearrange("p a b -> p (a b)"), T['biasT'][:, :])
        negd = ph.sb("negd", [128, HYW], F32)
        k.dma(negd[:], T['negd'][0:1, :].partition_broadcast(128))
        tfull = ph.sb("tfull", [128, 2, 32], F32)
        k.dma(tfull[:].rearrange("p a b -> p (a b)"), T['tfull'][:, :])
        w_in_v = T['w_in'].rearrange("(k p) n -> p k n", p=128)

        hbuf = [ph.sb(f"hbuf{i}", [128, 8, 512], BF) for i in range(2)]
        ar = ph.sb("arena", [128, 24592], BF)
        raw = [ar[:, 4098 * i:4098 * (i + 1)] for i in range(3)]
        ub_ = [ar[:, 12294 + 4096 * i:12294 + 4096 * (i + 1)] for i in range(2)]
        Wblk = ar[:, 20486:24582].rearrange("p (k w c) -> p k w c", k=8, w=4)
        k_tm = ar[:, 0:8192].rearrange("p (o d c s) -> p o d c s", o=2, d=2, c=64)
        AB = ar[:, 8192:12288].rearrange("p (d c s) -> p d c s", d=2, c=64)
        Ksp = ar[:, 12288:20480].rearrange("p (o g r f) -> p o g r f", o=2, g=16, r=2)
        Gbuf = ar[:, 20480:24576].rearrange("p (r c s) -> p r c s", r=2, c=64)
        sz = ph.sb("sz", [128, L], BF)
        tm = [ph.sb(f"tm{i}", [128, 128, 32], BF) for i in range(3)]
        z2_tm = ph.sb("z2_tm", [128, 128, 32], BF)
        y_sc = ph.sb("y_sc", [128, 32, 128], BF)
        yzb = ph.sb("yzb", [128, L], BF)
        arg32 = ph.sb("arg32", [128, 4096], F32)
        PW = [(ph.sb(f"P1_{i}", [128, 256], F32), ph.sb(f"P2_{i}", [128, 256], F32)) for i in range(2)]
        Zp = [ph.sb(f"Zp{i}", [128, 256], BF) for i in range(2)]
        Yb = [ph.sb(f"Yb{i}", [128, 256], BF) for i in range(2)]
        Kev = [ph.sb(f"Kev{i}", [128, 256], BF) for i in range(2)]
        Esb = [[ph.sb(f"E{st}_{i}", [128, 256], BF) for i in range(2)] for st in range(3)]
        cols = (COL_V, COL_X1, COL_X2, COL_ZH)
        cnt = [0]

        PWs = [[(ph.sb(f"P1_{st}_{i}", [128, 512], F32), ph.sb(f"P2_{st}_{i}", [128, 512], F32)) for i in range(2)]
               for st in range(3)]
        Zp2 = [ph.sb(f"Zq{i}", [128, 512], BF) for i in range(2)]
        Yb2 = [ph.sb(f"Yq{i}", [128, 512], BF) for i in range(2)]

        def v4(ap):
            return ap.rearrange("p (u r f) -> p u r f", u=2, r=2)

        def tab4(t):
            return bcast(t.rearrange("p (r f) -> p r f", r=2), 1, 2)

        def run_skewed(items):
            n = len(items)
            depth = max(len(it) for it in items)
            for t in range(n + depth - 1):
                for s_ in reversed(range(depth)):
                    i = t - s_
                    if 0 <= i < n and s_ < len(items[i]):
                        items[i][s_](i)

        def cmul_pair(W, bank, Ta, Tb, out512):
            P1, P2 = W
            k.tt('dve', v4(P1[:]), v4(bank), tab4(Ta), ALU.mult)
            k.tt('dve', v4(P2[:]), v4(bank)[:, :, ::-1, :], tab4(Tb), ALU.mult)
            k.tt('pool', out512, P1[:], P2[:], ALU.add)

        def st_za(lhs_of, q):
            def f(i):
                bank = ps[i % 2]
                for u in range(2):
                    l1, l2 = lhs_of(2 * q + u)
                    za = bank[:, 256 * u:256 * (u + 1)]
                    k.mm(za, l1, cb16['FA1'][:], start=True, stop=(l2 is None))
                    if l2 is not None:
                        k.mm(za, l2, cb16['FA2'][:], start=False, stop=True)
            return f

        def st_tw(i):
            cmul_pair(PWs[0][i % 2], ps[i % 2][:, :], c32['TWa'][:], c32['TWb'][:], Zp2[i % 2][:])

        def st_ub(i):
            bank = ps[2 + i % 2]
            z_p = Zp2[i % 2]
            for u in range(2):
                ub = bank[:, 256 * u:256 * (u + 1)]
                zr = z_p[:, 256 * u:256 * u + 128]
                zi = z_p[:, 256 * u + 128:256 * (u + 1)]
                k.mm(ub[:, 0:128], cb16['WBr'][:], zr, start=True, stop=False)
                k.mm(ub[:, 0:128], cb16['WBni'][:], zi, start=False, stop=True)
                k.mm(ub[:, 128:256], cb16['WBi'][:], zr, start=True, stop=False)
                k.mm(ub[:, 128:256], cb16['WBr'][:], zi, start=False, stop=True)

        for cb in range(DBG.get('ncb', 4)):
            for w in range(4):
                load_w(ph, Wblk[:, :, w, :], w_in_v[:, :, cols[w] + 128 * cb:cols[w] + 128 * (cb + 1)])
            for w in range(3):
                k.memset('pool', raw[w][:, 0:1], 0.0)
                k.memset('pool', raw[w][:, 4097:4098], 0.0)
            for tb in range(NB):
                hb = hbuf[tb % 2]
                k.dma(hb[:].rearrange("p c t -> p (c t)"), T['hT_d'][tb])
                for w in range(4):
                    bank = ps[(tb * 4 + w) % 2]
                    for c in range(8):
                        k.mm(bank[:, :], Wblk[:, c, w, :], hb[:, c, :], start=(c == 0), stop=(c == 7))
                    if w < 3:
                        k.act(raw[w][:, 1 + 512 * tb:1 + 512 * (tb + 1)], bank[:, :], AF.Copy)
                    else:
                        k.act(sz[:, 512 * tb:512 * (tb + 1)], bank[:, :], AF.Silu)
            if DBG.get('s2b', 9) < 2: continue
            for w in range(3):
                u = ub_[w % 2]
                j = 4 * w + cb
                k.ts('dve', u, raw[w][:, 1:4097], wsh[:, j, 1:2], wsh[:, j, 3:4], op0=ALU.mult, op1=ALU.add)
                k.stt(u, raw[w][:, 0:4096], wsh[:, j, 0:1], u, ALU.mult, ALU.add)
                k.stt(u, raw[w][:, 2:4098], wsh[:, j, 2:3], u, ALU.mult, ALU.add)
                for a in range(4):
                    pv = ps[2 + a % 2][:, :].bitcast(BF)
                    for e in range(8):
                        s1 = 8 * a + e
                        k.tr(pv[:, 128 * e:128 * (e + 1)], u[:, s1:4096:32], ident[:])
                    k.copy('dve', tm[w][:, :, 8 * a:8 * a + 8], pv.rearrange("p (s c) -> p c s", s=8))
            if DBG.get('dump_tm'):
                k.dma(T['dbg_tm'][:, :], tm[DBG['dump_tm'] - 1][:].rearrange("p c s -> p (c s)"))
            if DBG.get('s2b', 9) < 3: continue
            for hbk in range(DBG.get('nhbk', 2)):
                c0 = 64 * hbk
                gcol = 128 * cb + c0
                k.tt('dve', arg32[:].rearrange("p (d c s) -> p d c s", d=2, c=64),
                     bcast(bcast(negd[:, gcol:gcol + 64], 1, 2), 3, 32),
                     bcast(tfull[:, :, :], 2, 64), ALU.mult)
                if DBG.get('s3', 9) < 2: continue
                k.act(AB.rearrange("p d c s -> p (d c s)"), arg32[:], AF.Exp)
                if DBG.get('s3', 9) < 3: continue
                wc0 = 256 * (2 * cb + hbk)
                for s1 in range(32):
                    kb_ = ps[6 + s1 % 2][:, 0:256]
                    k.mm(kb_, h2p[:, s1, :], W3[:, wc0:wc0 + 256])
                    if DBG.get('s3', 9) < 4: continue
                    abv = AB[:, :, :, s1].rearrange("p d c -> p (d c)")
                    k.tt('dve', k_tm[:, :, :, :, s1].rearrange("p o d c -> p o (d c)"),
                         kb_.rearrange("p (o x) -> p o x", o=2), bcast(abv, 1, 2), ALU.mult)
                if DBG.get('s2b', 9) < 4: continue
                def spec_lhs(gi):
                    o, g = divmod(gi, 16)
                    return (k_tm[:, o, 0, 4 * g:4 * g + 4, :].rearrange("p c s -> p (c s)"),
                            k_tm[:, o, 1, 4 * g:4 * g + 4, :].rearrange("p c s -> p (c s)"))

                def st_kev(q):
                    def f(i):
                        o, gp = divmod(q, 8)
                        k.copy('act', Ksp[:, o, 2 * gp:2 * gp + 2, :, :].rearrange("p g r f -> p (g r f)"), ps[2 + i % 2][:, :])
                        if gp == 7:
                            gg0 = gcol // 4
                            k.tt('pool', Ksp[:, o, :, 0, :], Ksp[:, o, :, 0, :], bcast(biasT[:, o, gg0:gg0 + 16], 2, 128), ALU.add)
                    return f

                def conv_lhs_of(src):
                    def f(g):
                        return (src[:, c0 + 4 * g:c0 + 4 * g + 4, :].rearrange("p c s -> p (c s)"), None)
                    return f

                def st_mul(o, q):
                    def f(i):
                        bank = ps[2 + i % 2][:, :]
                        P1, P2 = PWs[1][i % 2]
                        kr_ = bcast(Ksp[:, o, 2 * q:2 * q + 2, 0, :], 2, 2)
                        ki_ = bcast(Ksp[:, o, 2 * q:2 * q + 2, 1, :], 2, 2)
                        k.tt('dve', v4(P1[:]), v4(bank), kr_, ALU.mult)
                        k.tt('dve', v4(P2[:]), v4(bank)[:, :, ::-1, :], ki_, ALU.mult)
                        y_b = Yb2[i % 2]
                        k.tt('pool', v4(y_b[:])[:, :, 0, :], v4(P1[:])[:, :, 0, :], v4(P2[:])[:, :, 0, :], ALU.subtract)
                        k.tt('pool', v4(y_b[:])[:, :, 1, :], v4(P1[:])[:, :, 1, :], v4(P2[:])[:, :, 1, :], ALU.add)
                    return f

                def st_gb(i):
                    bank = ps[4 + i % 2]
                    y_b = Yb2[i % 2]
                    for u in range(2):
                        gb = bank[:, 256 * u:256 * (u + 1)]
                        k.mm(gb, y_b[:, 256 * u:256 * u + 128], cb16['WI1'][:], start=True, stop=False)
                        k.mm(gb, y_b[:, 256 * u + 128:256 * (u + 1)], cb16['WI2'][:], start=False, stop=True)

                def st_itw(o, q):
                    def f(i):
                        bank = ps[4 + i % 2][:, :]
                        P1, P2 = PWs[2][i % 2]
                        k.tt('dve', v4(P1[:]), v4(bank), tab4(c32['TIa'][:]), ALU.mult)
                        k.tt('dve', v4(P2[:]), v4(bank)[:, :, ::-1, :], tab4(c32['TIb'][:]), ALU.mult)
                        k.tt('pool', Gbuf[:, :, 8 * q:8 * q + 8, :].rearrange("p r (u c) s -> p r u (c s)", u=2),
                             v4(P1[:]).rearrange("p u r f -> p r u f"), v4(P2[:]).rearrange("p u r f -> p r u f"), ALU.add)
                    return f

                def st_inva(o, q):
                    def f(i):
                        if q % 2 == 1:
                            cc = q // 2
                            gate = tm[1] if o == 0 else tm[2]
                            yb = ps[6 + cc % 2]
                            k.mm(yb[:, :], cb16['FIr'][:], Gbuf[:, 0, 16 * cc:16 * cc + 16, :].rearrange("p c s -> p (c s)"),
                                 start=True, stop=False)
                            k.mm(yb[:, :], cb16['FIi'][:], Gbuf[:, 1, 16 * cc:16 * cc + 16, :].rearrange("p c s -> p (c s)"),
                                 start=False, stop=True)
                            cs = slice(c0 + 16 * cc, c0 + 16 * cc + 16)
                            if o == 0:
                                k.tt('dve', z2_tm[:, cs, :], yb[:, :].rearrange("p (c s) -> p c s", c=16), gate[:, cs, :], ALU.mult)
                            else:
                                k.tt('dve', y_sc[:, :, cs].rearrange("p s c -> p c s"),
                                     yb[:, :].rearrange("p (c s) -> p c s", c=16), gate[:, cs, :], ALU.mult)
                    return f

                items = []
                for q in range(16):
                    items.append([st_za(spec_lhs, q), st_tw, st_ub, st_kev(q)])
                for o in range(DBG.get('nord', 2)):
                    src = tm[0] if o == 0 else z2_tm
                    if o == 1:
                        items += [[] for _ in range(DBG.get('gap', 0))]
                    for q in range(8):
                        items.append([st_za(conv_lhs_of(src), q), st_tw, st_ub, st_mul(o, q), st_gb, st_itw(o, q), st_inva(o, q)])
                run_skewed(items)
            if DBG.get('dump_z2'):
                k.dma(T['dbg_tm'][:, :], z2_tm[:].rearrange("p c s -> p (c s)"))
            if DBG.get('s2b', 9) < 6: continue
            for a in range(4):
                pv = ps[2 + a % 2][:, :].bitcast(BF)
                for e in range(8):
                    k.tr(pv[:, 128 * e:128 * (e + 1)], y_sc[:, 8 * a + e, :], ident[:])
                k.tt('dve', yzb[:].rearrange("c (p s) -> c p s", s=32)[:, :, 8 * a:8 * a + 8],
                     pv.rearrange("c (s p) -> c p s", s=8),
                     sz[:].rearrange("c (p s) -> c p s", s=32)[:, :, 8 * a:8 * a + 8], ALU.mult)
            for tb in range(NB):
                k.dma(T['yz_d'][tb][:, 512 * cb:512 * (cb + 1)], yzb[:, 512 * tb:512 * (tb + 1)])


def phase2(nc, T):
    if 'b' in DBG.get('p2', 'ab'):
        phase2b(nc, T)


def _bf(a):
    return np.asarray(a, np.float32).astype(ml_dtypes.bfloat16)


_CONST_CACHE = {}


def host_constants():
    if _CONST_CACHE:
        return _CONST_CACHE
    C = {}
    pos = np.arange(L, dtype=np.float32)
    inv_freq = (np.float32(10000.0) ** (-np.arange(0, 32, 2, dtype=np.float32) / np.float32(32))).astype(np.float32)
    ang = (pos[:, None] * inv_freq[None, :]).astype(np.float32)
    C['cosT'] = np.ascontiguousarray(np.cos(ang).astype(np.float32).reshape(NT, 128, 16).transpose(1, 0, 2).reshape(128, NT * 16))
    C['sinT'] = np.ascontiguousarray(np.sin(ang).astype(np.float32).reshape(NT, 128, 16).transpose(1, 0, 2).reshape(128, NT * 16))
    F = fft_constants()
    for nm in ('FA1', 'FA2', 'WBr', 'WBi', 'WBni', 'WI1', 'WI2', 'FIr', 'FIi'):
        C[nm] = np.ascontiguousarray(_bf(F[nm]))
    for nm in ('TWa', 'TWb', 'TIa', 'TIb'):
        C[nm] = np.ascontiguousarray(F[nm].astype(np.float32))
    C.update(filter_constants())
    _CONST_CACHE.update(C)
    return _CONST_CACHE


def prep_inputs(inp, b):
    f32 = np.float32
    m = {}
    m['x'] = np.ascontiguousarray(inp['x'][b], dtype=f32)
    m['w_in'] = np.ascontiguousarray(inp['w_in'][0], dtype=f32)
    m['gT'] = np.ascontiguousarray(inp['g_norm'][0].reshape(8, 128).T, dtype=f32)
    m['bgT'] = np.ascontiguousarray(inp['b_gate'][0].reshape(16, 128).T, dtype=f32)
    m['w_uq'] = np.ascontiguousarray(inp['w_uq'][0], dtype=f32)
    m['w_ukv'] = np.ascontiguousarray(inp['w_ukv'][0], dtype=f32)
    m['gcqT'] = np.ascontiguousarray(inp['g_cq'][0].reshape(3, 128).T, dtype=f32)
    m['gckvT'] = np.ascontiguousarray(inp['g_ckv'][0].reshape(2, 128).T, dtype=f32)
    m['gqk'] = np.ascontiguousarray(np.concatenate([inp['g_qn'][0], inp['g_kn'][0]])[None, :], dtype=f32)
    m['w_attn_out'] = np.ascontiguousarray(inp['w_attn_out'][0], dtype=f32)
    m['w_hy_out'] = np.ascontiguousarray(inp['w_hy_out'][0], dtype=f32)
    m['w_out'] = np.ascontiguousarray(inp['w_out'][0], dtype=f32)
    wsh = np.zeros((128, 12, 4), f32)
    wsh[:, :, 0:3] = inp['w_short'][0].reshape(3, 12, 128).transpose(2, 1, 0)
    wsh[:, :, 3] = inp['b_short'][0].reshape(12, 128).T
    m['wsh'] = wsh.reshape(128, 48)
    hb = inp['hy_bias'][0]
    bT = hb.reshape(2, 128, 4).transpose(2, 0, 1)
    m['biasT'] = np.ascontiguousarray(np.repeat(bT[:, None], 32, axis=1).reshape(128, 256), dtype=f32)
    W1 = np.zeros((128, 128), f32)
    W1[0:33, 0:64] = inp['w_f1'][0]
    W1[64:97, 64:128] = inp['w_f1'][0]
    m['W1blk'] = W1
    W2 = np.zeros((128, 128), f32)
    W2[0:64, 0:64] = inp['w_f2'][0]
    W2[64:128, 64:128] = inp['w_f2'][0]
    m['W2blk'] = W2
    mv = np.zeros((128, 4), f32)
    for jj, nm in enumerate(('freq_1', 'b_f1', 'freq_2', 'b_f2')):
        mv[0:64, jj] = inp[nm][0]
        mv[64:128, jj] = inp[nm][0]
    m['mlpv'] = mv
    w3 = inp['w_f3'][0].reshape(64, 2, 2, 8, 64)
    W3 = np.zeros((128, 8, 2, 2, 64), f32)
    for dd in range(2):
        W3[64 * dd:64 * (dd + 1), :, :, dd, :] = w3[:, :, dd, :, :].transpose(0, 2, 1, 3)
    m['W3blk'] = W3.reshape(128, 2048)
    C = host_constants()
    for nm in CONST_NAMES:
        m[nm] = C[nm]
    return m


IN_SHAPES = {
    'x': ([L, D], F32), 'w_in': ([D, 5280], F32), 'gT': ([128, 8], F32), 'bgT': ([128, 16], F32),
    'w_uq': ([384, 768], F32), 'w_ukv': ([256, 1024], F32), 'gcqT': ([128, 3], F32), 'gckvT': ([128, 2], F32),
    'gqk': ([1, 192], F32), 'w_attn_out': ([512, D], F32), 'w_hy_out': ([512, D], F32), 'w_out': ([D, D], F32),
    'cosT': ([128, NT * 16], F32), 'sinT': ([128, NT * 16], F32),
    'wsh': ([128, 48], F32), 'biasT': ([128, 256], F32), 'W1blk': ([128, 128], F32), 'W2blk': ([128, 128], F32),
    'mlpv': ([128, 4], F32), 'W3blk': ([128, 2048], F32),
    'FA1': ([128, 256], BF), 'FA2': ([128, 256], BF), 'WBr': ([128, 128], BF), 'WBi': ([128, 128], BF),
    'WBni': ([128, 128], BF), 'WI1': ([128, 256], BF), 'WI2': ([128, 256], BF), 'FIr': ([128, 128], BF),
    'FIi': ([128, 128], BF), 'TWa': ([128, 256], F32), 'TWb': ([128, 256], F32), 'TIa': ([128, 256], F32),
    'TIb': ([128, 256], F32), 'zs_hi': ([128, L], BF), 'zs_lo': ([128, L], BF), 'tfull': ([128, 64], F32),
    'negd': ([1, HYW], F32),
}
CONST_NAMES = ('cosT', 'sinT', 'FA1', 'FA2', 'WBr', 'WBi', 'WBni', 'WI1', 'WI2', 'FIr', 'FIi', 'TWa', 'TWb', 'TIa', 'TIb',
               'zs_hi', 'zs_lo', 'tfull', 'negd')


def build_nc(debug=None):
    debug = debug or set()
    nc = bass.Bass("TRN2", target_bir_lowering=False)
    T = {}
    for name, (shape, dt) in IN_SHAPES.items():
        T[name] = nc.dram_tensor(name, shape, dt, kind="ExternalInput").ap()
    T['out'] = nc.dram_tensor("out", [L, D], F32, kind="ExternalOutput").ap()
    skind = dict(kind="ExternalOutput") if 'dump' in debug else {}
    T['hT_d'] = nc.dram_tensor("hT_d", [NB, 128, 8 * 512], BF, **skind).ap()
    T['at_d'] = nc.dram_tensor("at_d", [NB, 64, 8 * 512], BF, **skind).ap()
    T['h2_d'] = nc.dram_tensor("h2_d", [128, L], BF, **skind).ap()
    if 'dump' in debug:
        T['dbg_tm'] = nc.dram_tensor("dbg_tm", [128, 4096], BF, kind="ExternalOutput").ap()
        T['dbg_k'] = nc.dram_tensor("dbg_k", [128, 8192], BF, kind="ExternalOutput").ap()
        T['dbg_ks'] = nc.dram_tensor("dbg_ks", [128, 8192], BF, kind="ExternalOutput").ap()
    if 'yz_in' in debug:
        T['yz_d'] = nc.dram_tensor("yz_d", [NB, 128, 4 * 512], BF, kind="ExternalInput").ap()
    else:
        T['yz_d'] = nc.dram_tensor("yz_d", [NB, 128, 4 * 512], BF, **skind).ap()
    phases = debug & {'p1', 'p2', 'p3', 'p4'} or {'p1', 'p2', 'p3', 'p4'}
    do_p2 = 'p2' in phases and 'yz_in' not in debug
    if 'p1' in phases or do_p2:
        phase12(nc, T, with_p1=('p1' in phases), with_p2a=do_p2)
    if do_p2:
        phase2(nc, T)
    if 'p3' in phases:
        phase3(nc, T)
    if 'p4' in phases:
        phase4(nc, T)
    return nc


def kernel(**inputs):
    inp = {k_: np.asarray(v) for k_, v in inputs.items()}
    nc = build_nc()
    in_maps = [prep_inputs(inp, b) for b in range(8)]
    res = run_bass_kernel_spmd(nc, in_maps, core_ids=list(range(8)))
    out = np.stack([np.asarray(r['out'], dtype=np.float32) for r in res.results], axis=0)
    return out
```

```python
import concourse.bass as bass
import concourse.mybir as mybir

_ESZ = {}


def _esize(dt):
    s = _ESZ.get(dt)
    if s is None:
        n = str(dt)
        if '32' in n:
            s = 4
        elif '16' in n:
            s = 2
        elif '8' in n:
            s = 1
        else:
            s = 4
        _ESZ[dt] = s
    return s


def footprint(ap):
    t = ap.tensor
    name = t.name
    es = _esize(ap.dtype)
    apl = ap.ap
    off = int(ap.offset) * es
    space = str(type(t).__name__)
    if 'DRam' in space:
        lo = off
        hi = off
        for st, cnt in apl:
            if cnt > 1:
                d = (cnt - 1) * st * es
                if d > 0:
                    hi += d
                else:
                    lo += d
        return (name, 0, 1, lo, hi + es)
    pstep, pcnt = apl[0]
    pstep_b = pstep * es
    if pstep_b > 0:
        p0 = off // pstep_b
        f0 = off % pstep_b
    else:
        p0 = 0
        f0 = off
    lo = f0
    hi = f0
    for st, cnt in apl[1:]:
        if cnt > 1:
            d = (cnt - 1) * st * es
            if d > 0:
                hi += d
            else:
                lo += d
    return (name, p0, p0 + pcnt, lo, hi + es)


COMPUTE = ('pe', 'act', 'dve', 'pool')
QUEUES = ('pe', 'act', 'dve', 'pool', 'sp')
QIDX = {q: i for i, q in enumerate(QUEUES)}


class _Op:
    __slots__ = ('q', 'fn', 'dma', 'idx', 'gid', 'waits_c', 'waits_d', 'signal', 'snap', 'slot', 'slot_cnt', 'prev_slot')


class Prog:
    def __init__(self, nc, dma_slots=None):
        self.nc = nc
        self.streams = {q: [] for q in QUEUES}
        self.recs = {}
        self.known = {q: [-1] * len(QUEUES) for q in QUEUES}
        self.known_dma = {q: set() for q in QUEUES}
        self.ops = []
        self.dma_slots = dma_slots or {'sp': 8, 'pool': 4, 'act': 4}
        self.dma_count = {q: 0 for q in QUEUES}
        self.dma_ops = {q: [] for q in QUEUES}
        self.n_comp = {q: 0 for q in QUEUES}

    def add(self, q, fn, reads=(), writes=(), dma=False):
        op = _Op()
        op.q = q
        op.fn = fn
        op.dma = dma
        op.gid = len(self.ops)
        op.signal = dma
        op.waits_c = []
        op.waits_d = []
        op.slot = None
        op.prev_slot = None
        stream = self.streams[q]
        if not dma:
            op.idx = self.n_comp[q]
            self.n_comp[q] += 1
        else:
            op.idx = -1
        deps_c = {}
        deps_d = set()

        def scan(fp, is_write):
            name, p0, p1, f0, f1 = fp
            lst = self.recs.get(name)
            if not lst:
                return
            for r in lst:
                (rp0, rp1, rf0, rf1, rw, rop) = r
                if not (is_write or rw):
                    continue
                if rp1 <= p0 or p1 <= rp0 or rf1 <= f0 or f1 <= rf0:
                    continue
                if rop.dma:
                    deps_d.add(rop)
                else:
                    e = rop.q
                    if deps_c.get(e, -1) < rop.idx:
                        deps_c[e] = rop.idx

        rfps = [footprint(a) for a in reads]
        wfps = [footprint(a) for a in writes]
        for fp in rfps:
            scan(fp, False)
        for fp in wfps:
            scan(fp, True)
        known = self.known[q]
        kd = self.known_dma[q]
        for e, i in deps_c.items():
            ei = QIDX[e]
            if e == q and not dma:
                if q == 'pe':
                    continue
            if i <= known[ei]:
                continue
            op.waits_c.append((e, i))
            src = self.comp_ops[e][i]
            src.signal = True
            known[ei] = i
            for k, v in enumerate(src.snap):
                if v > known[k]:
                    known[k] = v
        for d in sorted(deps_d, key=lambda o: o.gid):
            if d.gid in kd:
                continue
            op.waits_d.append(d)
            kd.add(d.gid)
            for k, v in enumerate(d.snap):
                if v > known[k]:
                    known[k] = v
        if dma:
            n = self.dma_count[q]
            R = self.dma_slots[q]
            op.slot = n % R
            op.slot_cnt = n // R + 1
            if n >= R:
                prev = self.dma_ops[q][n - R]
                op.prev_slot = prev
                kd.add(prev.gid)
            self.dma_count[q] = n + 1
            self.dma_ops[q].append(op)
        op.snap = tuple(known)
        if not dma:
            self.comp_ops[q].append(op)
        for fp, is_write in [(f, False) for f in rfps] + [(f, True) for f in wfps]:
            name, p0, p1, f0, f1 = fp
            lst = self.recs.setdefault(name, [])
            if is_write:
                lst[:] = [r for r in lst if not (r[0] >= p0 and r[1] <= p1 and r[2] >= f0 and r[3] <= f1)]
            else:
                if not dma:
                    lst[:] = [r for r in lst if not (r[4] is False and (not r[5].dma) and r[5].q == q
                                                     and r[0] == p0 and r[1] == p1 and r[2] == f0 and r[3] == f1)]
            lst.append((p0, p1, f0, f1, is_write, op))
        stream.append(op)
        self.ops.append(op)
        return op

    comp_ops = None

    def start(self):
        self.comp_ops = {q: [] for q in QUEUES}

    def pe(self, fn, reads, writes):
        return self.add('pe', fn, reads, writes)

    def act(self, fn, reads, writes):
        return self.add('act', fn, reads, writes)

    def dve(self, fn, reads, writes):
        return self.add('dve', fn, reads, writes)

    def pool(self, fn, reads, writes):
        return self.add('pool', fn, reads, writes)

    def dma(self, out, in_, q='sp', **kw):
        return self.add(q, lambda e: e.dma_start(out=out, in_=in_, **kw), [in_], [out], dma=True)

    def emit(self, block, sems_c, sems_d):
        cum = {}
        for e in QUEUES:
            c = 0
            arr = []
            for o in self.comp_ops[e]:
                if o.signal:
                    c += 1
                arr.append(c)
            cum[e] = arr
        self.cum = cum

        def gen(q):
            def body(eng):
                for o in self.streams[q]:
                    for (e, i) in o.waits_c:
                        eng.wait_ge(sems_c[e], cum[e][i])
                    for d in o.waits_d:
                        eng.wait_ge(sems_d[d.q][d.slot], 16 * d.slot_cnt)
                    if o.prev_slot is not None:
                        p = o.prev_slot
                        eng.wait_ge(sems_d[p.q][p.slot], 16 * p.slot_cnt)
                    ins = o.fn(eng)
                    if o.dma:
                        ins.then_inc(sems_d[q][o.slot], 16)
                    elif o.signal:
                        ins.then_inc(sems_c[q], 1)
                R = self.dma_slots.get(q, 0)
                n = self.dma_count[q]
                for o in self.dma_ops[q][max(0, n - R):]:
                    eng.wait_ge(sems_d[q][o.slot], 16 * o.slot_cnt)
            return body

        if self.streams['pe']:
            block.tensor(gen('pe'))
        if self.streams['act']:
            block.scalar(gen('act'))
        if self.streams['dve']:
            block.vector(gen('dve'))
        if self.streams['pool']:
            block.gpsimd(gen('pool'))
        if self.streams['sp']:
            block.sync(gen('sp'))

import math
from contextlib import ExitStack
import numpy as np
import ml_dtypes
from concourse.bass_utils import run_bass_kernel_spmd

F32 = mybir.dt.float32
BF = mybir.dt.bfloat16
AF = mybir.ActivationFunctionType
ALU = mybir.AluOpType
AX = mybir.AxisListType

L = 4096
D = 1024
NT = 32
NB = 8
EPS = 1e-6
NF = 8192
HYW = 512
COL_V, COL_X1, COL_X2, COL_ZH = 0, 512, 1024, 1536
COL_CQ, COL_CKV, COL_KR, COL_ZA = 2048, 2432, 2688, 2720
COL_GH, COL_GA = 3232, 4256
MAGIC = 12582912.0
DBG = {}


def bcast(ap, axis, n):
    a = ap.unsqueeze(axis)
    shp = list(a.shape)
    shp[axis] = n
    return a.to_broadcast(shp)


class K:
    def __init__(self, P):
        self.P = P

    def mm(self, out, lhsT, rhs, start=True, stop=True):
        self.P.pe(lambda e: e.matmul(out, lhsT=lhsT, rhs=rhs, start=start, stop=stop), [lhsT, rhs], [out])

    def tr(self, out, in_, ident):
        self.P.pe(lambda e: e.transpose(out=out, in_=in_, identity=ident), [in_, ident], [out])

    def act(self, out, in_, func, bias=None, scale=None, accum_out=None):
        kw = {}
        reads = [in_]
        writes = [out]
        if bias is not None:
            kw['bias'] = bias
            if not isinstance(bias, (int, float)):
                reads.append(bias)
        if scale is not None:
            kw['scale'] = scale
            if not isinstance(scale, (int, float)):
                reads.append(scale)
        if accum_out is not None:
            kw['accum_out'] = accum_out
            writes.append(accum_out)
        self.P.act(lambda e: e.activation(out=out, in_=in_, func=func, **kw), reads, writes)

    def tt(self, eng, out, in0, in1, op):
        self.P.add(eng, lambda e: e.tensor_tensor(out=out, in0=in0, in1=in1, op=op), [in0, in1], [out])

    def ts(self, eng, out, in0, s1, s2=None, op0=ALU.mult, op1=None):
        reads = [in0]
        if not isinstance(s1, (int, float)):
            reads.append(s1)
        if s2 is not None and not isinstance(s2, (int, float)):
            reads.append(s2)
        if op1 is None:
            self.P.add(eng, lambda e: e.tensor_scalar(out=out, in0=in0, scalar1=s1, scalar2=None, op0=op0), reads, [out])
        else:
            self.P.add(eng, lambda e: e.tensor_scalar(out=out, in0=in0, scalar1=s1, scalar2=s2, op0=op0, op1=op1), reads, [out])

    def stt(self, out, in0, scalar, in1, op0, op1):
        reads = [in0, in1]
        if not isinstance(scalar, (int, float)):
            reads.append(scalar)
        self.P.dve(lambda e: e.scalar_tensor_tensor(out=out, in0=in0, scalar=scalar, in1=in1, op0=op0, op1=op1), reads, [out])

    def copy(self, eng, out, in_):
        if eng == 'act':
            self.act(out, in_, AF.Copy)
        else:
            self.P.add(eng, lambda e: e.tensor_copy(out=out, in_=in_), [in_], [out])

    def recip(self, out, in_):
        self.P.dve(lambda e: e.reciprocal(out=out, in_=in_), [in_], [out])

    def reduce_add(self, out, in_):
        self.P.dve(lambda e: e.tensor_reduce(out=out, in_=in_, axis=AX.X, op=ALU.add), [in_], [out])

    def memset(self, eng, ap, val):
        self.P.add(eng, lambda e: e.memset(ap, val), [], [ap])

    def dma(self, out, in_, q='sp'):
        self.P.dma(out, in_, q=q)


def _dump(P):
    cum = {}
    for e in QUEUES:
        c = 0
        arr = []
        for o in P.comp_ops[e]:
            if o.signal:
                c += 1
            arr.append(c)
        cum[e] = arr
    for q in QUEUES:
        print("== stream", q)
        for o in P.streams[q]:
            w = [f"{e}>={cum[e][i]}(op{i})" for e, i in o.waits_c] + [f"dma[{d.q}{d.slot}]>={16*d.slot_cnt}" for d in o.waits_d]
            if o.prev_slot is not None:
                w.append(f"prev dma[{o.prev_slot.q}{o.prev_slot.slot}]>={16*o.prev_slot.slot_cnt}")
            tag = f"DMA slot{o.slot} cnt{o.slot_cnt}" if o.dma else (f"op{o.idx} sig={cum[q][o.idx] if o.signal else '-'}")
            print("   ", tag, getattr(o, 'desc', ''), "waits:", w)


class Phase:
    def __init__(self, nc, name):
        self.nc = nc
        self.name = name
        self.es = ExitStack()

    def __enter__(self):
        nc = self.nc
        es = self.es
        es.__enter__()
        self.ps = [es.enter_context(nc.psum_tensor(f"{self.name}_ps{i}", [128, 512], F32)) for i in range(8)]
        self.sems_c = {e: es.enter_context(nc.semaphore(f"{self.name}_sc_{e}")) for e in QUEUES}
        self.sems_d = {q: [es.enter_context(nc.semaphore(f"{self.name}_sd_{q}{i}")) for i in range(n)]
                       for q, n in (('sp', 8), ('pool', 4), ('act', 4))}
        self.P = Prog(nc)
        self.P.start()
        self.k = K(self.P)
        return self

    def sb(self, name, shape, dt):
        return self.es.enter_context(self.nc.sbuf_tensor(f"{self.name}_{name}", shape, dt))

    def __exit__(self, *a):
        if a[0] is None:
            self.es.enter_context(self.nc.allow_low_precision("bf16 operands / intermediates by design"))
            block = self.es.enter_context(self.nc.Block())
            if DBG.get('dump') == self.name:
                _dump(self.P)
            self.P.emit(block, self.sems_c, self.sems_d)
        return self.es.__exit__(*a)


def make_ident(ph, ident):
    identf = ph.sb("identf", [128, 128], F32)
    ph.k.memset('pool', identf[:], 0.0)
    ph.P.pool(lambda e: e.affine_select(out=identf[:], in_=identf[:], pattern=[[-1, 128]], compare_op=ALU.not_equal,
                                        fill=1.0, base=0, channel_multiplier=1), [identf[:]], [identf[:]])
    ph.k.copy('dve', ident[:], identf[:])


def load_w(ph, dst, src_ap):
    ph.k.dma(dst, src_ap, q='pool')


def rstd_from_ss(k, rs_col, ss_col, n):
    k.act(rs_col, ss_col, AF.Sqrt, bias=EPS, scale=1.0 / n)
    k.recip(rs_col, rs_col)


def phase1_gen(ph, T, banks):
    k = ph.k
    xt = [ph.sb(f"xt{i}", [128, D], F32) for i in range(3)]
    xn = [ph.sb(f"xn{i}", [128, D], BF) for i in range(2)]
    junk = ph.sb("junk", [128, D], BF)
    ss = ph.sb("ss", [128, NT], F32)
    rs = ph.sb("rs", [128, NT], F32)
    ts_ = ph.sb("ts_", [128, NT], F32)
    mh = ph.sb("mh", [128, 1], F32)
    gT = ph.sb("gT", [128, 8], F32)
    ident = ph.sb("ident", [128, 128], BF)
    hb = [ph.sb(f"hb{i}", [128, 8, 512], BF) for i in range(2)]
    make_ident(ph, ident)
    k.memset('pool', mh[:], -0.5)
    k.dma(gT[:], T['gT'][:, :])
    pend = [None]
    for i in range(NT):
        x_t = xt[i % 3]
        k.dma(x_t[:], T['x'][128 * i:128 * (i + 1), :])
        k.act(junk[:], x_t[:], AF.Square, accum_out=ss[:, i:i + 1])
        rstd_pool(k, rs[:, i:i + 1], ss[:, i:i + 1], D, mh[:, 0:1], ts_[:, i:i + 1])
        x_n = xn[i % 2]
        k.ts('dve', x_n[:], x_t[:], rs[:, i:i + 1])
        bank = banks[i % len(banks)]
        pv = bank[:, :].bitcast(BF)
        for c in range(8):
            k.tr(pv[:, 128 * c:128 * (c + 1)], x_n[:, 128 * c:128 * (c + 1)], ident[:])
        if pend[0] is not None:
            pend[0]()

        def evac(i=i, pv=pv):
            h_b = hb[(i // 4) % 2]
            j = i % 4
            k.tt('dve', h_b[:, :, 128 * j:128 * (j + 1)], pv.rearrange("p (c t) -> p c t", c=8),
                 bcast(gT[:, :], 2, 128), ALU.mult)
            if j == 3:
                k.dma(T['hT_d'][i // 4], h_b[:].rearrange("p c t -> p (c t)"))
        pend[0] = evac
        yield
    pend[0]()
    yield


def rstd_pool(k, rs, ss, n, mhalf, tmp):
    k.ts('dve', tmp, ss, 1.0 / n, EPS, op0=ALU.mult, op1=ALU.add)
    k.tt('pool', rs, tmp, mhalf, ALU.pow)


def phase12(nc, T, with_p1=True, with_p2a=True):
    with Phase(nc, "p12") as ph:
        g1 = phase1_gen(ph, T, ph.ps[0:4]) if with_p1 else iter(())
        g2 = phase2a_gen(ph, T, ph.ps[4:8]) if with_p2a else iter(())
        alive1, alive2 = True, True
        while alive1 or alive2:
            for _ in range(4):
                if alive1:
                    try:
                        next(g1)
                    except StopIteration:
                        alive1 = False
            if alive2:
                try:
                    next(g2)
                except StopIteration:
                    alive2 = False


def run_interleaved(gens, width=2):
    active = []
    gens = list(gens)
    while gens or active:
        while gens and len(active) < width:
            active.append(gens.pop(0))
        for g in list(active):
            try:
                next(g)
            except StopIteration:
                active.remove(g)


def qk_norm_rope(ph, W, src, dst, g_rep, cs_t, sc_t, mhalf, use_act=False):
    k = ph.k
    sq, ssq, rk, ta, tb_, tmp = W['sq'], W['ssq'], W['rk'], W['ta'], W['tb'], W['tmp']
    if use_act:
        k.act(sq[:].rearrange("p a b -> p (a b)"), src[:].rearrange("p a b -> p (a b)"), AF.Square)
    else:
        k.tt('pool', sq[:], src[:], src[:], ALU.mult)
    k.reduce_add(ssq[:], sq[:])
    rstd_pool(k, rk[:], ssq[:], 96, mhalf[:, 0:8], tmp[:])
    yield
    k.tt('dve', src[:], src[:], bcast(rk[:, :], 2, 96), ALU.mult)
    k.tt('dve', src[:], src[:], bcast(g_rep, 1, 8), ALU.mult)
    yield
    t1 = bcast(src[:, :, 64:80], 1, 2)
    t2 = bcast(src[:, :, 80:96], 1, 2)
    k.tt('pool', ta[:], t1, bcast(cs_t, 2, 8), ALU.mult)
    k.tt('pool', tb_[:], t2, bcast(sc_t, 2, 8), ALU.mult)
    k.tt('dve', dst[:, :, 64:80], ta[:, 0], tb_[:, 0], ALU.subtract)
    k.tt('dve', dst[:, :, 80:96], ta[:, 1], tb_[:, 1], ALU.add)
    k.copy('pool', dst[:, :, 0:64], src[:, :, 0:64])
    yield


def phase3(nc, T):
    with Phase(nc, "p3") as ph:
        k = ph.k
        ps = ph.ps
        ident = ph.sb("ident", [128, 128], BF)
        make_ident(ph, ident)
        w_in_v = T['w_in'].rearrange("(k p) n -> p k n", p=128)
        Wkv = ph.sb("Wkv", [128, 8, 288], BF)
        Wq = ph.sb("Wq", [128, 8, 384], BF)
        Wuq = ph.sb("Wuq", [128, 3, 768], BF)
        Wukv = ph.sb("Wukv", [128, 2, 1024], BF)
        gcq = ph.sb("gcq", [128, 3], F32)
        gckv = ph.sb("gckv", [128, 2], F32)
        gqk = ph.sb("gqk", [128, 192], F32)
        csT = ph.sb("csT", [128, NT, 2, 16], F32)
        mhalf = ph.sb("mhalf", [128, 8], F32)
        k.memset('pool', mhalf[:], -0.5)
        load_w(ph, Wkv[:], w_in_v[:, :, COL_CKV:COL_CKV + 288])
        load_w(ph, Wq[:], w_in_v[:, :, COL_CQ:COL_CQ + 384])
        load_w(ph, Wuq[:], T['w_uq'].rearrange("(k p) n -> p k n", p=128))
        load_w(ph, Wukv[:], T['w_ukv'].rearrange("(k p) n -> p k n", p=128))
        k.dma(gcq[:], T['gcqT'][:, :])
        k.dma(gckv[:], T['gckvT'][:, :])
        k.dma(gqk[:], T['gqk'][0:1, :].partition_broadcast(128))
        cosv = T['cosT'].rearrange("p (a b) -> p a b", b=16)
        sinv = T['sinT'].rearrange("p (a b) -> p a b", b=16)
        k.dma(csT[:, :, 0, :], cosv)
        k.dma(csT[:, :, 1, :], sinv)
        k.tt('pool', Wuq[:], Wuq[:], bcast(gcq[:, :], 2, 768), ALU.mult)
        k.tt('pool', Wukv[:], Wukv[:], bcast(gckv[:, :], 2, 1024), ALU.mult)

        kT = ph.sb("kT", [128, 8, L], BF)
        vx = ph.sb("vx", [128, NT, 8, 65], BF)
        k.memset('pool', vx[:, :, :, 64:65], 1.0)
        ones = ph.sb("ones", [128, 64], BF)
        k.memset('pool', ones[:], 1.0)
        hbuf = [ph.sb(f"hbuf{i}", [128, 8, 512], BF) for i in range(2)]
        junk = [ph.sb(f"junk{i}", [128, 384], BF) for i in range(3)]
        ssl = ph.sb("ssl", [128, 2 * NT], F32)
        rsl = ph.sb("rsl", [128, 2 * NT], F32)
        tsl = ph.sb("tsl", [128, 2 * NT], F32)
        latn = [ph.sb(f"latn{i}", [128, 384], BF) for i in range(3)]
        latT = [ph.sb(f"latT{i}", [128, 3, 128], BF) for i in range(3)]
        kr = [ph.sb(f"kr{i}", [128, 32], F32) for i in range(3)]
        qk32 = [ph.sb(f"qk32_{i}", [128, 8, 96], F32) for i in range(3)]
        qkbf = [ph.sb(f"qkbf{i}", [128, 8, 96], BF) for i in range(3)]
        Wk_ = [dict(sq=ph.sb(f"sq{i}", [128, 8, 96], F32), ssq=ph.sb(f"ssq{i}", [128, 8], F32),
                    rk=ph.sb(f"rk{i}", [128, 8], F32), tmp=ph.sb(f"tmpn{i}", [128, 8], F32),
                    ta=ph.sb(f"ta{i}", [128, 2, 8, 16], F32), tb=ph.sb(f"tb{i}", [128, 2, 8, 16], F32)) for i in range(3)]
        qT = [ph.sb(f"qT{i}", [128, 8, 512], BF) for i in range(2)]
        pt = [ph.sb(f"pt{i}", [128, 512], BF) for i in range(3)]
        rsum = [ph.sb(f"rsum{i}", [128, 512], BF) for i in range(2)]
        bcs = [ph.sb(f"bcs{i}", [64, 512], BF) for i in range(2)]
        at = [ph.sb(f"at{i}", [64, 8, 512], BF) for i in range(1)]

        def sumsq(i, col, src_ps, n):
            k.act(junk[i % 3][:, 0:n], src_ps, AF.Square, accum_out=ssl[:, col:col + 1])
            rstd_pool(k, rsl[:, col:col + 1], ssl[:, col:col + 1], n, mhalf[:, 0:1], tsl[:, col:col + 1])

        def kv_tile(i, hb, j):
            par = i % 3
            if j == 0:
                k.dma(hb[:].rearrange("p c t -> p (c t)"), T['hT_d'][i // 4])
            lat = ps[par][:, 0:288]
            for c in range(8):
                k.mm(lat, hb[:, c, 128 * j:128 * (j + 1)], Wkv[:, c, :], start=(c == 0), stop=(c == 7))
            sumsq(i, i, lat[:, 0:256], 256)
            yield
            ln = latn[par]
            k.ts('dve', ln[:, 0:256], lat[:, 0:256], rsl[:, i:i + 1])
            k.copy('dve', kr[par][:], lat[:, 256:288])
            tbank = ps[7][:, :].bitcast(BF)
            for c in range(2):
                k.tr(tbank[:, 128 * c:128 * (c + 1)], ln[:, 128 * c:128 * (c + 1)], ident[:])
            lT = latT[par]
            k.copy('dve', lT[:, 0:2, :].rearrange("p c t -> p (c t)"), tbank[:, 0:256])
            yield
            kvb = [ps[3 + 2 * (i % 2)], ps[4 + 2 * (i % 2)]]
            for half in range(2):
                for c in range(2):
                    k.mm(kvb[half][:, :], lT[:, c, :], Wukv[:, c, 512 * half:512 * (half + 1)],
                         start=(c == 0), stop=(c == 1))
            kk = qk32[par]
            for half in range(2):
                kvv = kvb[half][:, :].rearrange("p (h e) -> p h e", h=4)
                k.copy('dve', kk[:, 4 * half:4 * half + 4, 0:64], kvv[:, :, 0:64])
                k.copy('dve', vx[:, i, 4 * half:4 * half + 4, 0:64], kvv[:, :, 64:128])
            k.copy('pool', kk[:, :, 64:96], bcast(kr[par][:, :], 1, 8))
            yield
            kf = qkbf[par]
            yield from qk_norm_rope(ph, Wk_[par], kk, kf, gqk[:, 96:192], csT[:, i, :, :], csT[:, i, ::-1, :], mhalf, use_act=True)
            kbank = ps[7][:, :].bitcast(BF)
            for h in range(8):
                k.tr(kbank[0:96, 128 * h:128 * (h + 1)], kf[:, h, :], ident[:])
            k.copy('dve', kT[0:96, :, 128 * i:128 * (i + 1)], kbank[0:96, :].rearrange("p (h t) -> p h t", h=8))
            yield

        def q_tile(i, hb, j, q_T):
            par = i % 2
            lat = ps[6][:, 0:384]
            for c in range(8):
                k.mm(lat, hb[:, c, 128 * j:128 * (j + 1)], Wq[:, c, :], start=(c == 0), stop=(c == 7))
            yield
            lsb = Wk_[par]['sq'][:].rearrange("p a b -> p (a b)")[:, 0:384]
            k.copy('dve', lsb, lat)
            k.P.dve(lambda e: e.scalar_tensor_tensor(out=junk[par][:, 0:384], in0=lsb, scalar=1.0, in1=lsb, op0=ALU.mult,
                                                     op1=ALU.mult, accum_out=ssl[:, NT + i:NT + i + 1]),
                    [lsb], [junk[par][:, 0:384], ssl[:, NT + i:NT + i + 1]])
            rstd_pool(k, rsl[:, NT + i:NT + i + 1], ssl[:, NT + i:NT + i + 1], 384, mhalf[:, 0:1], tsl[:, NT + i:NT + i + 1])
            yield
            ln = latn[par]
            k.ts('dve', ln[:, 0:384], lsb, rsl[:, NT + i:NT + i + 1])
            yield
            tbank = ps[7][:, :].bitcast(BF)
            for c in range(3):
                k.tr(tbank[:, 128 * c:128 * (c + 1)], ln[:, 128 * c:128 * (c + 1)], ident[:])
            yield
            lT = latT[par]
            k.copy('dve', lT[:].rearrange("p c t -> p (c t)"), tbank[:, 0:384])
            yield
            qq = qk32[par]
            for half in range(2):
                qb = ps[6][:, 0:384]
                for c in range(3):
                    k.mm(qb, lT[:, c, :], Wuq[:, c, 384 * half:384 * (half + 1)], start=(c == 0), stop=(c == 2))
                yield
                k.copy('dve', qq[:, 4 * half:4 * half + 4, :], qb.rearrange("p (h e) -> p h e", h=4))
                yield
            qf = qkbf[par]
            yield from qk_norm_rope(ph, Wk_[par], qq, qf, gqk[:, 0:96], csT[:, i, :, :], csT[:, i, ::-1, :], mhalf, use_act=(i < 4))
            yield
            yield
            yield
            yield
            qbank = ps[7][:, :].bitcast(BF)
            for h in range(8):
                k.tr(qbank[0:96, 128 * h:128 * (h + 1)], qf[:, h, :], ident[:])
            yield
            k.copy('dve', q_T[0:96, :, 128 * j:128 * (j + 1)], qbank[0:96, :].rearrange("p (h t) -> p h t", h=8))
            yield

        def kv_block(tb):
            hb = hbuf[tb % 2]
            return [kv_tile(tb * 4 + j, hb, j) for j in range(4)]

        def q_chunk_gens(qc):
            hb = hbuf[qc % 2]
            k.dma(hb[:].rearrange("p c t -> p (c t)"), T['hT_d'][qc])
            return [q_tile(qc * 4 + j, hb, j, qT[qc % 2]) for j in range(4)]

        gens = []
        for tb in range(DBG.get('nprep', NB)):
            gens += kv_block(tb)
        run_interleaved(gens, 3)
        nqc = DBG.get('nqc', NB)
        if nqc:
            run_interleaved(q_chunk_gens(0), 1)

        scale = 1.0 / math.sqrt(96.0)
        NH = DBG.get('nh', 8)
        pend = [None]
        for qc in range(nqc):
            q_T = qT[qc % 2]
            a_t = at[0]
            nxt = q_chunk_gens(qc + 1) if qc + 1 < nqc else []
            nxt_active = []
            steps = [(h, kt) for h in range(NH) for kt in range(NT)]

            def S(idx):
                h, kt = steps[idx]
                k.mm(ps[idx % 3][:, :], kT[0:96, h, 128 * kt:128 * (kt + 1)], q_T[0:96, h, :])

            def fin_a(h):
                k.recip(rsum[h % 2][64:65, :], ps[3 + (h % 2)][64:65, :])

            def fin_b(h):
                k.mm(ps[5][0:64, :], ones[64:65, :], rsum[h % 2][64:65, :])

            def fin_c(h, a_t):
                k.copy('dve', bcs[h % 2][:], ps[5][0:64, :])
                k.tt('dve', a_t[:, h, :], ps[3 + (h % 2)][0:64, :], bcs[h % 2][:], ALU.mult)

            S(0)
            S(1)
            for idx, (h, kt) in enumerate(steps):
                p_t = pt[idx % 3]
                k.act(p_t[:], ps[idx % 3][:, :], AF.Exp, scale=scale)
                if idx + 2 < len(steps):
                    S(idx + 2)
                ob = ps[3 + (h % 2)]
                k.mm(ob[0:65, :], vx[:, kt, h, :], p_t[:], start=(kt == 0), stop=(kt == NT - 1))
                if pend[0] is not None:
                    if kt == 1:
                        pend[0][0]()
                    elif kt == 10:
                        pend[0][1]()
                    elif kt == 14:
                        pend[0][2]()
                        pend[0] = None
                if kt == NT - 1:
                    last = (h == NH - 1)
                    pend[0] = (lambda h=h: fin_a(h), lambda h=h: fin_b(h),
                               (lambda h=h, a_t=a_t, qc=qc, last=last, fin_c=fin_c: (fin_c(h, a_t), k.dma(T['at_d'][qc], a_t[:].rearrange("p h t -> p (h t)")) if last else None)))
                if idx % 3 == 2:
                    while nxt and len(nxt_active) < 1:
                        nxt_active.append(nxt.pop(0))
                    for g in list(nxt_active):
                        try:
                            next(g)
                        except StopIteration:
                            nxt_active.remove(g)
            run_interleaved(nxt_active + nxt, 1)

        if pend[0] is not None:
            pend[0][0]()
            pend[0][1]()
            pend[0][2]()


def phase4(nc, T):
    with Phase(nc, "p4") as ph:
        k = ph.k
        ps = ph.ps
        w_in_v = T['w_in'].rearrange("(k p) n -> p k n", p=128)
        Wz = ph.sb("Wz", [128, 8, 512], BF)
        Wg = ph.sb("Wg", [128, 8, 2048], BF)
        Wao = ph.sb("Wao", [128, 4, D], BF)
        Who = ph.sb("Who", [128, 4, D], BF)
        Wout = ph.sb("Wout", [128, 8, D], BF)
        bg = ph.sb("bg", [128, 16], F32)
        load_w(ph, Wz[:], w_in_v[:, :, COL_ZA:COL_ZA + 512])
        for q4 in range(4):
            load_w(ph, Wg[:, :, 512 * q4:512 * (q4 + 1)], w_in_v[:, :, COL_GH + 512 * q4:COL_GH + 512 * (q4 + 1)])
        load_w(ph, Wao[:], T['w_attn_out'].rearrange("(hp p) n -> p hp n", p=128))
        load_w(ph, Who[:], T['w_hy_out'].rearrange("(k p) n -> p k n", p=128))
        load_w(ph, Wout[:], T['w_out'].rearrange("(k p) n -> p k n", p=128))
        k.dma(bg[:], T['bgT'][:, :])
        hbuf = [ph.sb(f"hbuf{i}", [128, 8, 512], BF) for i in range(2)]
        atb = [ph.sb(f"atb{i}", [128, 4, 512], BF) for i in range(2)]
        yzb = [ph.sb(f"yzb{i}", [128, 4, 512], BF) for i in range(2)]
        xt = [ph.sb(f"xt{i}", [128, D], F32) for i in range(3)]
        ot = [ph.sb(f"ot{i}", [128, D], F32) for i in range(2)]
        sz = [ph.sb(f"sz{i}", [128, 512], BF) for i in range(2)]
        ya = ph.sb("ya", [128, 4, 512], BF)
        gh = [ph.sb(f"gh{i}", [128, 512], BF) for i in range(2)]
        ga = [ph.sb(f"ga{i}", [128, 512], BF) for i in range(2)]
        m1 = [ph.sb(f"m1{i}", [128, 512], F32) for i in range(2)]
        m2 = [ph.sb(f"m2{i}", [128, 512], F32) for i in range(2)]
        mg = [ph.sb(f"mg{i}", [128, 8, 512], BF) for i in range(2)]
        for tb in range(NB):
            hb = hbuf[tb % 2]
            a_b = atb[tb % 2]
            y_b = yzb[tb % 2]
            k.dma(hb[:].rearrange("p c t -> p (c t)"), T['hT_d'][tb])
            atv = T['at_d'][tb].rearrange("p (hp two t) -> p two hp t", two=2, t=512)
            k.dma(a_b[0:64, :, :], atv[:, 0, :, :])
            k.dma(a_b[64:128, :, :], atv[:, 1, :, :])
            k.dma(y_b[:].rearrange("p c t -> p (c t)"), T['yz_d'][tb])
            for hp in range(4):
                zb = ps[hp % 2][:, :]
                for c in range(8):
                    k.mm(zb, Wz[:, c, 128 * hp:128 * (hp + 1)], hb[:, c, :], start=(c == 0), stop=(c == 7))
                s_z = sz[hp % 2]
                k.act(s_z[:], zb, AF.Silu)
                k.tt('pool', ya[:, hp, :], a_b[:, hp, :], s_z[:], ALU.mult)
            m_g = mg[tb % 2]
            for dc in range(8):
                g1 = ps[2 + (dc % 2)]
                g2 = ps[4 + (dc % 2)]
                for c in range(8):
                    k.mm(g1[:, :], Wg[:, c, 128 * dc:128 * (dc + 1)], hb[:, c, :], start=(c == 0), stop=(c == 7))
                for c in range(8):
                    k.mm(g2[:, :], Wg[:, c, 1024 + 128 * dc:1024 + 128 * (dc + 1)], hb[:, c, :],
                         start=(c == 0), stop=(c == 7))
                k.act(gh[dc % 2][:], g1[:, :], AF.Sigmoid, bias=bg[:, dc:dc + 1])
                k.act(ga[dc % 2][:], g2[:, :], AF.Sigmoid, bias=bg[:, 8 + dc:9 + dc])
                uh = ps[6]
                ua = ps[7]
                for c in range(4):
                    k.mm(uh[:, :], Who[:, c, 128 * dc:128 * (dc + 1)], y_b[:, c, :], start=(c == 0), stop=(c == 3))
                for hp in range(4):
                    k.mm(ua[:, :], Wao[:, hp, 128 * dc:128 * (dc + 1)], ya[:, hp, :], start=(hp == 0), stop=(hp == 3))
                k.tt('dve', m1[dc % 2][:], uh[:, :], gh[dc % 2][:], ALU.mult)
                k.tt('dve', m2[dc % 2][:], ua[:, :], ga[dc % 2][:], ALU.mult)
                k.tt('pool', m_g[:, dc, :], m1[dc % 2][:], m2[dc % 2][:], ALU.add)
            for j in range(4):
                i = tb * 4 + j
                x_t = xt[i % 3]
                k.dma(x_t[:], T['x'][128 * i:128 * (i + 1), :])
                o_t = ot[i % 2]
                for half in range(2):
                    fb = ps[half]
                    for c in range(8):
                        k.mm(fb[:, :], m_g[:, c, 128 * j:128 * (j + 1)], Wout[:, c, 512 * half:512 * (half + 1)],
                             start=(c == 0), stop=(c == 7))
                    k.tt('dve', o_t[:, 512 * half:512 * (half + 1)], fb[:, :], x_t[:, 512 * half:512 * (half + 1)], ALU.add)
                k.dma(T['out'][128 * i:128 * (i + 1), :], o_t[:])


def fft_constants():
    C = {}
    n = NF
    s2 = np.arange(128, dtype=np.float64)[:, None]
    f2 = np.arange(128, dtype=np.float64)[None, :]
    th = 2 * np.pi * (f2 + 0.5) * s2 / 256.0
    C['FA1'] = np.concatenate([np.cos(th), -np.sin(th)], 1)
    th2 = 2 * np.pi * (f2 + 0.5) * (s2 + 128) / 256.0
    C['FA2'] = -np.concatenate([np.cos(th2), -np.sin(th2)], 1)
    s1 = np.arange(32, dtype=np.float64)
    tw = np.exp(-2j * np.pi * (np.arange(128)[None, :] + 0.5) * s1[:, None] / n)
    twq = np.tile(tw, (4, 1))
    C['TWa'] = np.concatenate([twq.real, twq.real], 1)
    C['TWb'] = np.concatenate([-twq.imag, twq.imag], 1)
    W = np.exp(-2j * np.pi * np.outer(s1, s1) / 32.0)
    Wq = np.kron(np.eye(4), W)
    C['WBr'] = Wq.real
    C['WBi'] = Wq.imag
    C['WBni'] = -Wq.imag
    Wi = np.exp(2j * np.pi * np.outer(s1, s1) / 32.0)
    Wiq = np.kron(np.eye(4), Wi)
    C['WI1'] = np.concatenate([Wiq.real, Wiq.imag], 1)
    C['WI2'] = np.concatenate([-Wiq.imag, Wiq.real], 1)
    twi = np.exp(2j * np.pi * (np.arange(128)[:, None] + 0.5) * s1[None, :] / n)
    twiq = np.tile(twi, (1, 4))
    C['TIa'] = np.concatenate([twiq.real, twiq.real], 1)
    C['TIb'] = np.concatenate([-twiq.imag, twiq.imag], 1)
    t2 = np.arange(128, dtype=np.float64)[None, :]
    f2c = np.arange(128, dtype=np.float64)[:, None]
    th3 = 2 * np.pi * (f2c + 0.5) * t2 / 256.0
    C['FIr'] = (2.0 / n) * np.cos(th3)
    C['FIi'] = -(2.0 / n) * np.sin(th3)
    return C


def filter_constants():
    C = {}
    f32 = np.float32
    t = np.linspace(0.0, 1.0, L, dtype=f32)[:, None]
    bands = 16
    f = np.linspace(1e-4, bands - 1, bands, dtype=f32)
    ang = (f32(2.0 * np.pi / L) * np.arange(L, dtype=f32)[:, None] * f[None, :]).astype(f32)
    z = np.concatenate([t, np.cos(ang).astype(f32), -np.sin(ang).astype(f32)], axis=-1).astype(f32)
    zs = np.zeros((128, L), f32)
    zs[0:33, :] = z.T
    zs[64:97, :] = z[::-1].T
    hi = zs.astype(ml_dtypes.bfloat16)
    lo = (zs - hi.astype(f32)).astype(ml_dtypes.bfloat16)
    C['zs_hi'] = hi
    C['zs_lo'] = lo
    tl = t[:, 0]
    tf = np.zeros((128, 2, 32), f32)
    pidx = np.arange(128)[:, None] * 32 + np.arange(32)[None, :]
    tf[:, 0, :] = tl[pidx]
    tf[:, 1, :] = tl[4095 - pidx]
    C['tfull'] = tf.reshape(128, 64)
    MIN_DECAY = math.log(1e-2) / 1.5
    MAX_DECAY = math.log(1e-2) / 0.3
    deltas = np.abs(np.linspace(MIN_DECAY, MAX_DECAY, HYW, dtype=f32)).astype(f32)
    C['negd'] = (-deltas)[None, :].astype(f32)
    return C


def _sin_layer(ph, W, pre_ps, fr, fb, out32):
    k = ph.k
    a, kk = W['a'], W['kk']
    k.ts('dve', a[:], pre_ps, fr, fb, op0=ALU.mult, op1=ALU.add)
    k.ts('dve', kk[:], a[:], 1.0 / (2 * math.pi), MAGIC, op0=ALU.mult, op1=ALU.add)
    k.ts('dve', kk[:], kk[:], -MAGIC, None, op0=ALU.add)
    k.stt(a[:], kk[:], -2 * math.pi, a[:], ALU.mult, ALU.add)
    k.ts('dve', a[:], a[:], -3.14159, 3.14159, op0=ALU.max, op1=ALU.min)
    k.act(out32, a[:], AF.Sin)


def _hilo(ph, hi, lo, src32, tmp32):
    k = ph.k
    k.copy('dve', hi, src32)
    k.copy('pool', tmp32, hi)
    k.tt('pool', lo, src32, tmp32, ALU.subtract)


def phase2a_gen(ph, T, banks):
    k = ph.k
    zs_hi = ph.sb("zs_hi", [128, L], BF)
    zs_lo = ph.sb("zs_lo", [128, L], BF)
    W1 = ph.sb("W1", [128, 128], F32)
    W2 = ph.sb("W2", [128, 128], F32)
    W1h = ph.sb("W1h", [128, 128], BF)
    W1l = ph.sb("W1l", [128, 128], BF)
    W2h = ph.sb("W2h", [128, 128], BF)
    W2l = ph.sb("W2l", [128, 128], BF)
    wt = ph.sb("wt", [128, 128], F32)
    mv = ph.sb("mv", [128, 4], F32)
    fb = ph.sb("fb", [128, 2], F32)
    Wk = dict(a=ph.sb("a", [128, 512], F32), kk=ph.sb("kk", [128, 512], F32))
    h1 = ph.sb("h1", [128, 512], F32)
    h1h = ph.sb("h1h", [128, 512], BF)
    h1l = ph.sb("h1l", [128, 512], BF)
    t32 = ph.sb("t32", [128, 512], F32)
    h2 = ph.sb("h2", [128, 512], F32)
    h2b = [ph.sb(f"h2b{i}", [128, 512], BF) for i in range(2)]
    k.dma(zs_hi[:], T['zs_hi'][:, :])
    k.dma(zs_lo[:], T['zs_lo'][:, :])
    k.dma(W1[:], T['W1blk'][:, :])
    k.dma(W2[:], T['W2blk'][:, :])
    k.dma(mv[:], T['mlpv'][:, :])
    _hilo(ph, W1h[:], W1l[:], W1[:], wt[:])
    _hilo(ph, W2h[:], W2l[:], W2[:], wt[:])
    k.tt('dve', fb[:, 0:1], mv[:, 0:1], mv[:, 1:2], ALU.mult)
    k.tt('dve', fb[:, 1:2], mv[:, 2:3], mv[:, 3:4], ALU.mult)
    yield
    for cch in range(NB):
        sl = slice(512 * cch, 512 * (cch + 1))
        b1 = banks[cch % 2]
        k.mm(b1[:, :], W1h[:], zs_hi[:, sl], start=True, stop=False)
        k.mm(b1[:, :], W1h[:], zs_lo[:, sl], start=False, stop=False)
        k.mm(b1[:, :], W1l[:], zs_hi[:, sl], start=False, stop=True)
        _sin_layer(ph, Wk, b1[:, :], mv[:, 0:1], fb[:, 0:1], h1[:])
        _hilo(ph, h1h[:], h1l[:], h1[:], t32[:])
        yield
        b2 = banks[2 + cch % 2]
        k.mm(b2[:, :], W2h[:], h1h[:], start=True, stop=False)
        k.mm(b2[:, :], W2h[:], h1l[:], start=False, stop=False)
        k.mm(b2[:, :], W2l[:], h1h[:], start=False, stop=True)
        _sin_layer(ph, Wk, b2[:, :], mv[:, 2:3], fb[:, 1:2], h2[:])
        hb_ = h2b[cch % 2]
        k.copy('pool', hb_[:], h2[:])
        k.dma(T['h2_d'][:, sl], hb_[:])
        yield


def _cmul_tab(ph, W, src, Ta, Tb, out_bf):
    k = ph.k
    P1, P2 = W
    sw = src.rearrange("p (r f) -> p r f", r=2)[:, ::-1, :]
    k.tt('dve', P1[:], src, Ta, ALU.mult)
    k.tt('dve', P2[:].rearrange("p (r f) -> p r f", r=2), sw, Tb.rearrange("p (r f) -> p r f", r=2), ALU.mult)
    k.tt('pool', out_bf, P1[:], P2[:], ALU.add)


def phase2b(nc, T):
    with Phase(nc, "p2b") as ph:
        k = ph.k
        ps = ph.ps
        ident = ph.sb("ident", [128, 128], BF)
        make_ident(ph, ident)
        cb16 = {}
        for nm, w in (('FA1', 256), ('FA2', 256), ('WBr', 128), ('WBi', 128), ('WBni', 128), ('WI1', 256), ('WI2', 256),
                      ('FIr', 128), ('FIi', 128)):
            cb16[nm] = ph.sb(nm, [128, w], BF)
            k.dma(cb16[nm][:], T[nm][:, :])
        c32 = {}
        for nm in ('TWa', 'TWb', 'TIa', 'TIb'):
            c32[nm] = ph.sb(nm, [128, 256], F32)
            k.dma(c32[nm][:], T[nm][:, :])
        h2s = ph.sb("h2s", [128, L], BF)
        k.dma(h2s[:], T['h2_d'][:, :])
        h2p = ph.sb("h2p", [128, 32, 128], BF)
        k.copy('pool', h2p[:], h2s[:].rearrange("q (p s) -> q s p", s=32))
        W3 = ph.sb("W3", [128, 2048], BF)
        load_w(ph, W3[:], T['W3blk'][:, :])
        wsh = ph.sb("wsh", [128, 12, 4], F32)
        k.dma(wsh[:].rearrange("p a b -> p (a b)"), T['wsh'][:, :])
        biasT = ph.sb("biasT", [128, 2, 128], F32)
        k.dma(biasT[:].rearrange("p a b -> p (a b)"), T['biasT'][:, :])
        negd = ph.sb("negd", [128, HYW], F32)
        k.dma(negd[:], T['negd'][0:1, :].partition_broadcast(128))
        tfull = ph.sb("tfull", [128, 2, 32], F32)
        k.dma(tfull[:].rearrange("p a b -> p (a b)"), T['tfull'][:, :])
        w_in_v = T['w_in'].rearrange("(k p) n -> p k n", p=128)

        hbuf = [ph.sb(f"hbuf{i}", [128, 8, 512], BF) for i in range(2)]
        ar = ph.sb("arena", [128, 24592], BF)
        raw = [ar[:, 4098 * i:4098 * (i + 1)] for i in range(3)]
        ub_ = [ar[:, 12294 + 4096 * i:12294 + 4096 * (i + 1)] for i in range(2)]
        Wblk = ar[:, 20486:24582].rearrange("p (k w c) -> p k w c", k=8, w=4)
        k_tm = ar[:, 0:8192].rearrange("p (o d c s) -> p o d c s", o=2, d=2, c=64)
        AB = ar[:, 8192:12288].rearrange("p (d c s) -> p d c s", d=2, c=64)
        Ksp = ar[:, 12288:20480].rearrange("p (o g r f) -> p o g r f", o=2, g=16, r=2)
        Gbuf = ar[:, 20480:24576].rearrange("p (r c s) -> p r c s", r=2, c=64)
        sz = ph.sb("sz", [128, L], BF)
        tm = [ph.sb(f"tm{i}", [128, 128, 32], BF) for i in range(3)]
        z2_tm = ph.sb("z2_tm", [128, 128, 32], BF)
        y_sc = ph.sb("y_sc", [128, 32, 128], BF)
        yzb = ph.sb("yzb", [128, L], BF)
        arg32 = ph.sb("arg32", [128, 4096], F32)
        PW = [(ph.sb(f"P1_{i}", [128, 256], F32), ph.sb(f"P2_{i}", [128, 256], F32)) for i in range(2)]
        Zp = [ph.sb(f"Zp{i}", [128, 256], BF) for i in range(2)]
        Yb = [ph.sb(f"Yb{i}", [128, 256], BF) for i in range(2)]
        Kev = [ph.sb(f"Kev{i}", [128, 256], BF) for i in range(2)]
        Esb = [[ph.sb(f"E{st}_{i}", [128, 256], BF) for i in range(2)] for st in range(3)]
        cols = (COL_V, COL_X1, COL_X2, COL_ZH)
        cnt = [0]

        PWs = [[(ph.sb(f"P1_{st}_{i}", [128, 512], F32), ph.sb(f"P2_{st}_{i}", [128, 512], F32)) for i in range(2)]
               for st in range(3)]
        Zp2 = [ph.sb(f"Zq{i}", [128, 512], BF) for i in range(2)]
        Yb2 = [ph.sb(f"Yq{i}", [128, 512], BF) for i in range(2)]

        def v4(ap):
            return ap.rearrange("p (u r f) -> p u r f", u=2, r=2)

        def tab4(t):
            return bcast(t.rearrange("p (r f) -> p r f", r=2), 1, 2)

        def run_skewed(items):
            n = len(items)
            depth = max(len(it) for it in items)
            for t in range(n + depth - 1):
                for s_ in reversed(range(depth)):
                    i = t - s_
                    if 0 <= i < n and s_ < len(items[i]):
                        items[i][s_](i)

        def cmul_pair(W, bank, Ta, Tb, out512):
            P1, P2 = W
            k.tt('dve', v4(P1[:]), v4(bank), tab4(Ta), ALU.mult)
            k.tt('dve', v4(P2[:]), v4(bank)[:, :, ::-1, :], tab4(Tb), ALU.mult)
            k.tt('pool', out512, P1[:], P2[:], ALU.add)

        def st_za(lhs_of, q):
            def f(i):
                bank = ps[i % 2]
                for u in range(2):
                    l1, l2 = lhs_of(2 * q + u)
                    za = bank[:, 256 * u:256 * (u + 1)]
                    k.mm(za, l1, cb16['FA1'][:], start=True, stop=(l2 is None))
                    if l2 is not None:
                        k.mm(za, l2, cb16['FA2'][:], start=False, stop=True)
            return f

        def st_tw(i):
            cmul_pair(PWs[0][i % 2], ps[i % 2][:, :], c32['TWa'][:], c32['TWb'][:], Zp2[i % 2][:])

        def st_ub(i):
            bank = ps[2 + i % 2]
            z_p = Zp2[i % 2]
            for u in range(2):
                ub = bank[:, 256 * u:256 * (u + 1)]
                zr = z_p[:, 256 * u:256 * u + 128]
                zi = z_p[:, 256 * u + 128:256 * (u + 1)]
                k.mm(ub[:, 0:128], cb16['WBr'][:], zr, start=True, stop=False)
                k.mm(ub[:, 0:128], cb16['WBni'][:], zi, start=False, stop=True)
                k.mm(ub[:, 128:256], cb16['WBi'][:], zr, start=True, stop=False)
                k.mm(ub[:, 128:256], cb16['WBr'][:], zi, start=False, stop=True)

        for cb in range(DBG.get('ncb', 4)):
            for w in range(4):
                load_w(ph, Wblk[:, :, w, :], w_in_v[:, :, cols[w] + 128 * cb:cols[w] + 128 * (cb + 1)])
            for w in range(3):
                k.memset('pool', raw[w][:, 0:1], 0.0)
                k.memset('pool', raw[w][:, 4097:4098], 0.0)
            for tb in range(NB):
                hb = hbuf[tb % 2]
                k.dma(hb[:].rearrange("p c t -> p (c t)"), T['hT_d'][tb])
                for w in range(4):
                    bank = ps[(tb * 4 + w) % 2]
                    for c in range(8):
                        k.mm(bank[:, :], Wblk[:, c, w, :], hb[:, c, :], start=(c == 0), stop=(c == 7))
                    if w < 3:
                        k.act(raw[w][:, 1 + 512 * tb:1 + 512 * (tb + 1)], bank[:, :], AF.Copy)
                    else:
                        k.act(sz[:, 512 * tb:512 * (tb + 1)], bank[:, :], AF.Silu)
            if DBG.get('s2b', 9) < 2: continue
            for w in range(3):
                u = ub_[w % 2]
                j = 4 * w + cb
                k.ts('dve', u, raw[w][:, 1:4097], wsh[:, j, 1:2], wsh[:, j, 3:4], op0=ALU.mult, op1=ALU.add)
                k.stt(u, raw[w][:, 0:4096], wsh[:, j, 0:1], u, ALU.mult, ALU.add)
                k.stt(u, raw[w][:, 2:4098], wsh[:, j, 2:3], u, ALU.mult, ALU.add)
                for a in range(4):
                    pv = ps[2 + a % 2][:, :].bitcast(BF)
                    for e in range(8):
                        s1 = 8 * a + e
                        k.tr(pv[:, 128 * e:128 * (e + 1)], u[:, s1:4096:32], ident[:])
                    k.copy('dve', tm[w][:, :, 8 * a:8 * a + 8], pv.rearrange("p (s c) -> p c s", s=8))
            if DBG.get('dump_tm'):
                k.dma(T['dbg_tm'][:, :], tm[DBG['dump_tm'] - 1][:].rearrange("p c s -> p (c s)"))
            if DBG.get('s2b', 9) < 3: continue
            for hbk in range(DBG.get('nhbk', 2)):
                c0 = 64 * hbk
                gcol = 128 * cb + c0
                k.tt('dve', arg32[:].rearrange("p (d c s) -> p d c s", d=2, c=64),
                     bcast(bcast(negd[:, gcol:gcol + 64], 1, 2), 3, 32),
                     bcast(tfull[:, :, :], 2, 64), ALU.mult)
                if DBG.get('s3', 9) < 2: continue
                k.act(AB.rearrange("p d c s -> p (d c s)"), arg32[:], AF.Exp)
                if DBG.get('s3', 9) < 3: continue
                wc0 = 256 * (2 * cb + hbk)
                for s1 in range(32):
                    kb_ = ps[6 + s1 % 2][:, 0:256]
                    k.mm(kb_, h2p[:, s1, :], W3[:, wc0:wc0 + 256])
                    if DBG.get('s3', 9) < 4: continue
                    abv = AB[:, :, :, s1].rearrange("p d c -> p (d c)")
                    k.tt('dve', k_tm[:, :, :, :, s1].rearrange("p o d c -> p o (d c)"),
                         kb_.rearrange("p (o x) -> p o x", o=2), bcast(abv, 1, 2), ALU.mult)
                if DBG.get('s2b', 9) < 4: continue
                def spec_lhs(gi):
                    o, g = divmod(gi, 16)
                    return (k_tm[:, o, 0, 4 * g:4 * g + 4, :].rearrange("p c s -> p (c s)"),
                            k_tm[:, o, 1, 4 * g:4 * g + 4, :].rearrange("p c s -> p (c s)"))

                def st_kev(q):
                    def f(i):
                        o, gp = divmod(q, 8)
                        k.copy('act', Ksp[:, o, 2 * gp:2 * gp + 2, :, :].rearrange("p g r f -> p (g r f)"), ps[2 + i % 2][:, :])
                        if gp == 7:
                            gg0 = gcol // 4
                            k.tt('pool', Ksp[:, o, :, 0, :], Ksp[:, o, :, 0, :], bcast(biasT[:, o, gg0:gg0 + 16], 2, 128), ALU.add)
                    return f

                def conv_lhs_of(src):
                    def f(g):
                        return (src[:, c0 + 4 * g:c0 + 4 * g + 4, :].rearrange("p c s -> p (c s)"), None)
                    return f

                def st_mul(o, q):
                    def f(i):
                        bank = ps[2 + i % 2][:, :]
                        P1, P2 = PWs[1][i % 2]
                        kr_ = bcast(Ksp[:, o, 2 * q:2 * q + 2, 0, :], 2, 2)
                        ki_ = bcast(Ksp[:, o, 2 * q:2 * q + 2, 1, :], 2, 2)
                        k.tt('dve', v4(P1[:]), v4(bank), kr_, ALU.mult)
                        k.tt('dve', v4(P2[:]), v4(bank)[:, :, ::-1, :], ki_, ALU.mult)
                        y_b = Yb2[i % 2]
                        k.tt('pool', v4(y_b[:])[:, :, 0, :], v4(P1[:])[:, :, 0, :], v4(P2[:])[:, :, 0, :], ALU.subtract)
                        k.tt('pool', v4(y_b[:])[:, :, 1, :], v4(P1[:])[:, :, 1, :], v4(P2[:])[:, :, 1, :], ALU.add)
                    return f

                def st_gb(i):
                    bank = ps[4 + i % 2]
                    y_b = Yb2[i % 2]
                    for u in range(2):
                        gb = bank[:, 256 * u:256 * (u + 1)]
                        k.mm(gb, y_b[:, 256 * u:256 * u + 128], cb16['WI1'][:], start=True, stop=False)
                        k.mm(gb, y_b[:, 256 * u + 128:256 * (u + 1)], cb16['WI2'][:], start=False, stop=True)

                def st_itw(o, q):
                    def f(i):
                        bank = ps[4 + i % 2][:, :]
                        P1, P2 = PWs[2][i % 2]
                        k.tt('dve', v4(P1[:]), v4(bank), tab4(c32['TIa'][:]), ALU.mult)
                        k.tt('dve', v4(P2[:]), v4(bank)[:, :, ::-1, :], tab4(c32['TIb'][:]), ALU.mult)
                        k.tt('pool', Gbuf[:, :, 8 * q:8 * q + 8, :].rearrange("p r (u c) s -> p r u (c s)", u=2),
                             v4(P1[:]).rearrange("p u r f -> p r u f"), v4(P2[:]).rearrange("p u r f -> p r u f"), ALU.add)
                    return f

                def st_inva(o, q):
                    def f(i):
                        if q % 2 == 1:
                            cc = q // 2
                            gate = tm[1] if o == 0 else tm[2]
                            yb = ps[6 + cc % 2]
                            k.mm(yb[:, :], cb16['FIr'][:], Gbuf[:, 0, 16 * cc:16 * cc + 16, :].rearrange("p c s -> p (c s)"),
                                 start=True, stop=False)
                            k.mm(yb[:, :], cb16['FIi'][:], Gbuf[:, 1, 16 * cc:16 * cc + 16, :].rearrange("p c s -> p (c s)"),
                                 start=False, stop=True)
                            cs = slice(c0 + 16 * cc, c0 + 16 * cc + 16)
                            if o == 0:
                                k.tt('dve', z2_tm[:, cs, :], yb[:, :].rearrange("p (c s) -> p c s", c=16), gate[:, cs, :], ALU.mult)
                            else:
                                k.tt('dve', y_sc[:, :, cs].rearrange("p s c -> p c s"),
                                     yb[:, :].rearrange("p (c s) -> p c s", c=16), gate[:, cs, :], ALU.mult)
                    return f

                items = []
                for q in range(16):
                    items.append([st_za(spec_lhs, q), st_tw, st_ub, st_kev(q)])
                for o in range(DBG.get('nord', 2)):
                    src = tm[0] if o == 0 else z2_tm
                    if o == 1:
                        items += [[] for _ in range(DBG.get('gap', 0))]
                    for q in range(8):
                        items.append([st_za(conv_lhs_of(src), q), st_tw, st_ub, st_mul(o, q), st_gb, st_itw(o, q), st_inva(o, q)])
                run_skewed(items)
            if DBG.get('dump_z2'):
                k.dma(T['dbg_tm'][:, :], z2_tm[:].rearrange("p c s -> p (c s)"))
            if DBG.get('s2b', 9) < 6: continue
            for a in range(4):
                pv = ps[2 + a % 2][:, :].bitcast(BF)
                for e in range(8):
                    k.tr(pv[:, 128 * e:128 * (e + 1)], y_sc[:, 8 * a + e, :], ident[:])
                k.tt('dve', yzb[:].rearrange("c (p s) -> c p s", s=32)[:, :, 8 * a:8 * a + 8],
                     pv.rearrange("c (s p) -> c p s", s=8),
                     sz[:].rearrange("c (p s) -> c p s", s=32)[:, :, 8 * a:8 * a + 8], ALU.mult)
            for tb in range(NB):
                k.dma(T['yz_d'][tb][:, 512 * cb:512 * (cb + 1)], yzb[:, 512 * tb:512 * (tb + 1)])


def phase2(nc, T):
    if 'b' in DBG.get('p2', 'ab'):
        phase2b(nc, T)


def _bf(a):
    return np.asarray(a, np.float32).astype(ml_dtypes.bfloat16)


_CONST_CACHE = {}


def host_constants():
    if _CONST_CACHE:
        return _CONST_CACHE
    C = {}
    pos = np.arange(L, dtype=np.float32)
    inv_freq = (np.float32(10000.0) ** (-np.arange(0, 32, 2, dtype=np.float32) / np.float32(32))).astype(np.float32)
    ang = (pos[:, None] * inv_freq[None, :]).astype(np.float32)
    C['cosT'] = np.ascontiguousarray(np.cos(ang).astype(np.float32).reshape(NT, 128, 16).transpose(1, 0, 2).reshape(128, NT * 16))
    C['sinT'] = np.ascontiguousarray(np.sin(ang).astype(np.float32).reshape(NT, 128, 16).transpose(1, 0, 2).reshape(128, NT * 16))
    F = fft_constants()
    for nm in ('FA1', 'FA2', 'WBr', 'WBi', 'WBni', 'WI1', 'WI2', 'FIr', 'FIi'):
        C[nm] = np.ascontiguousarray(_bf(F[nm]))
    for nm in ('TWa', 'TWb', 'TIa', 'TIb'):
        C[nm] = np.ascontiguousarray(F[nm].astype(np.float32))
    C.update(filter_constants())
    _CONST_CACHE.update(C)
    return _CONST_CACHE


def prep_inputs(inp, b):
    f32 = np.float32
    m = {}
    m['x'] = np.ascontiguousarray(inp['x'][b], dtype=f32)
    m['w_in'] = np.ascontiguousarray(inp['w_in'][0], dtype=f32)
    m['gT'] = np.ascontiguousarray(inp['g_norm'][0].reshape(8, 128).T, dtype=f32)
    m['bgT'] = np.ascontiguousarray(inp['b_gate'][0].reshape(16, 128).T, dtype=f32)
    m['w_uq'] = np.ascontiguousarray(inp['w_uq'][0], dtype=f32)
    m['w_ukv'] = np.ascontiguousarray(inp['w_ukv'][0], dtype=f32)
    m['gcqT'] = np.ascontiguousarray(inp['g_cq'][0].reshape(3, 128).T, dtype=f32)
    m['gckvT'] = np.ascontiguousarray(inp['g_ckv'][0].reshape(2, 128).T, dtype=f32)
    m['gqk'] = np.ascontiguousarray(np.concatenate([inp['g_qn'][0], inp['g_kn'][0]])[None, :], dtype=f32)
    m['w_attn_out'] = np.ascontiguousarray(inp['w_attn_out'][0], dtype=f32)
    m['w_hy_out'] = np.ascontiguousarray(inp['w_hy_out'][0], dtype=f32)
    m['w_out'] = np.ascontiguousarray(inp['w_out'][0], dtype=f32)
    wsh = np.zeros((128, 12, 4), f32)
    wsh[:, :, 0:3] = inp['w_short'][0].reshape(3, 12, 128).transpose(2, 1, 0)
    wsh[:, :, 3] = inp['b_short'][0].reshape(12, 128).T
    m['wsh'] = wsh.reshape(128, 48)
    hb = inp['hy_bias'][0]
    bT = hb.reshape(2, 128, 4).transpose(2, 0, 1)
    m['biasT'] = np.ascontiguousarray(np.repeat(bT[:, None], 32, axis=1).reshape(128, 256), dtype=f32)
    W1 = np.zeros((128, 128), f32)
    W1[0:33, 0:64] = inp['w_f1'][0]
    W1[64:97, 64:128] = inp['w_f1'][0]
    m['W1blk'] = W1
    W2 = np.zeros((128, 128), f32)
    W2[0:64, 0:64] = inp['w_f2'][0]
    W2[64:128, 64:128] = inp['w_f2'][0]
    m['W2blk'] = W2
    mv = np.zeros((128, 4), f32)
    for jj, nm in enumerate(('freq_1', 'b_f1', 'freq_2', 'b_f2')):
        mv[0:64, jj] = inp[nm][0]
        mv[64:128, jj] = inp[nm][0]
    m['mlpv'] = mv
    w3 = inp['w_f3'][0].reshape(64, 2, 2, 8, 64)
    W3 = np.zeros((128, 8, 2, 2, 64), f32)
    for dd in range(2):
        W3[64 * dd:64 * (dd + 1), :, :, dd, :] = w3[:, :, dd, :, :].transpose(0, 2, 1, 3)
    m['W3blk'] = W3.reshape(128, 2048)
    C = host_constants()
    for nm in CONST_NAMES:
        m[nm] = C[nm]
    return m


IN_SHAPES = {
    'x': ([L, D], F32), 'w_in': ([D, 5280], F32), 'gT': ([128, 8], F32), 'bgT': ([128, 16], F32),
    'w_uq': ([384, 768], F32), 'w_ukv': ([256, 1024], F32), 'gcqT': ([128, 3], F32), 'gckvT': ([128, 2], F32),
    'gqk': ([1, 192], F32), 'w_attn_out': ([512, D], F32), 'w_hy_out': ([512, D], F32), 'w_out': ([D, D], F32),
    'cosT': ([128, NT * 16], F32), 'sinT': ([128, NT * 16], F32),
    'wsh': ([128, 48], F32), 'biasT': ([128, 256], F32), 'W1blk': ([128, 128], F32), 'W2blk': ([128, 128], F32),
    'mlpv': ([128, 4], F32), 'W3blk': ([128, 2048], F32),
    'FA1': ([128, 256], BF), 'FA2': ([128, 256], BF), 'WBr': ([128, 128], BF), 'WBi': ([128, 128], BF),
    'WBni': ([128, 128], BF), 'WI1': ([128, 256], BF), 'WI2': ([128, 256], BF), 'FIr': ([128, 128], BF),
    'FIi': ([128, 128], BF), 'TWa': ([128, 256], F32), 'TWb': ([128, 256], F32), 'TIa': ([128, 256], F32),
    'TIb': ([128, 256], F32), 'zs_hi': ([128, L], BF), 'zs_lo': ([128, L], BF), 'tfull': ([128, 64], F32),
    'negd': ([1, HYW], F32),
}
CONST_NAMES = ('cosT', 'sinT', 'FA1', 'FA2', 'WBr', 'WBi', 'WBni', 'WI1', 'WI2', 'FIr', 'FIi', 'TWa', 'TWb', 'TIa', 'TIb',
               'zs_hi', 'zs_lo', 'tfull', 'negd')


def build_nc(debug=None):
    debug = debug or set()
    nc = bass.Bass("TRN2", target_bir_lowering=False)
    T = {}
    for name, (shape, dt) in IN_SHAPES.items():
        T[name] = nc.dram_tensor(name, shape, dt, kind="ExternalInput").ap()
    T['out'] = nc.dram_tensor("out", [L, D], F32, kind="ExternalOutput").ap()
    skind = dict(kind="ExternalOutput") if 'dump' in debug else {}
    T['hT_d'] = nc.dram_tensor("hT_d", [NB, 128, 8 * 512], BF, **skind).ap()
    T['at_d'] = nc.dram_tensor("at_d", [NB, 64, 8 * 512], BF, **skind).ap()
    T['h2_d'] = nc.dram_tensor("h2_d", [128, L], BF, **skind).ap()
    if 'dump' in debug:
        T['dbg_tm'] = nc.dram_tensor("dbg_tm", [128, 4096], BF, kind="ExternalOutput").ap()
        T['dbg_k'] = nc.dram_tensor("dbg_k", [128, 8192], BF, kind="ExternalOutput").ap()
        T['dbg_ks'] = nc.dram_tensor("dbg_ks", [128, 8192], BF, kind="ExternalOutput").ap()
    if 'yz_in' in debug:
        T['yz_d'] = nc.dram_tensor("yz_d", [NB, 128, 4 * 512], BF, kind="ExternalInput").ap()
    else:
        T['yz_d'] = nc.dram_tensor("yz_d", [NB, 128, 4 * 512], BF, **skind).ap()
    phases = debug & {'p1', 'p2', 'p3', 'p4'} or {'p1', 'p2', 'p3', 'p4'}
    do_p2 = 'p2' in phases and 'yz_in' not in debug
    if 'p1' in phases or do_p2:
        phase12(nc, T, with_p1=('p1' in phases), with_p2a=do_p2)
    if do_p2:
        phase2(nc, T)
    if 'p3' in phases:
        phase3(nc, T)
    if 'p4' in phases:
        phase4(nc, T)
    return nc


def kernel(**inputs):
    inp = {k_: np.asarray(v) for k_, v in inputs.items()}
    nc = build_nc()
    in_maps = [prep_inputs(inp, b) for b in range(8)]
    res = run_bass_kernel_spmd(nc, in_maps, core_ids=list(range(8)))
    out = np.stack([np.asarray(r['out'], dtype=np.float32) for r in res.results], axis=0)
    return out
```

```python
import concourse.bass as bass
import concourse.mybir as mybir

_ESZ = {}


def _esize(dt):
    s = _ESZ.get(dt)
    if s is None:
        n = str(dt)
        if '32' in n:
            s = 4
        elif '16' in n:
            s = 2
        elif '8' in n:
            s = 1
        else:
            s = 4
        _ESZ[dt] = s
    return s


def footprint(ap):
    t = ap.tensor
    name = t.name
    es = _esize(ap.dtype)
    apl = ap.ap
    off = int(ap.offset) * es
    space = str(type(t).__name__)
    if 'DRam' in space:
        lo = off
        hi = off
        for st, cnt in apl:
            if cnt > 1:
                d = (cnt - 1) * st * es
                if d > 0:
                    hi += d
                else:
                    lo += d
        return (name, 0, 1, lo, hi + es)
    pstep, pcnt = apl[0]
    pstep_b = pstep * es
    if pstep_b > 0:
        p0 = off // pstep_b
        f0 = off % pstep_b
    else:
        p0 = 0
        f0 = off
    lo = f0
    hi = f0
    for st, cnt in apl[1:]:
        if cnt > 1:
            d = (cnt - 1) * st * es
            if d > 0:
                hi += d
            else:
                lo += d
    return (name, p0, p0 + pcnt, lo, hi + es)


COMPUTE = ('pe', 'act', 'dve', 'pool')
QUEUES = ('pe', 'act', 'dve', 'pool', 'sp')
QIDX = {q: i for i, q in enumerate(QUEUES)}


class _Op:
    __slots__ = ('q', 'fn', 'dma', 'idx', 'gid', 'waits_c', 'waits_d', 'signal', 'snap', 'slot', 'slot_cnt', 'prev_slot')


class Prog:
    def __init__(self, nc, dma_slots=None):
        self.nc = nc
        self.streams = {q: [] for q in QUEUES}
        self.recs = {}
        self.known = {q: [-1] * len(QUEUES) for q in QUEUES}
        self.known_dma = {q: set() for q in QUEUES}
        self.ops = []
        self.dma_slots = dma_slots or {'sp': 8, 'pool': 4, 'act': 4}
        self.dma_count = {q: 0 for q in QUEUES}
        self.dma_ops = {q: [] for q in QUEUES}
        self.n_comp = {q: 0 for q in QUEUES}

    def add(self, q, fn, reads=(), writes=(), dma=False):
        op = _Op()
        op.q = q
        op.fn = fn
        op.dma = dma
        op.gid = len(self.ops)
        op.signal = dma
        op.waits_c = []
        op.waits_d = []
        op.slot = None
        op.prev_slot = None
        stream = self.streams[q]
        if not dma:
            op.idx = self.n_comp[q]
            self.n_comp[q] += 1
        else:
            op.idx = -1
        deps_c = {}
        deps_d = set()

        def scan(fp, is_write):
            name, p0, p1, f0, f1 = fp
            lst = self.recs.get(name)
            if not lst:
                return
            for r in lst:
                (rp0, rp1, rf0, rf1, rw, rop) = r
                if not (is_write or rw):
                    continue
                if rp1 <= p0 or p1 <= rp0 or rf1 <= f0 or f1 <= rf0:
                    continue
                if rop.dma:
                    deps_d.add(rop)
                else:
                    e = rop.q
                    if deps_c.get(e, -1) < rop.idx:
                        deps_c[e] = rop.idx

        rfps = [footprint(a) for a in reads]
        wfps = [footprint(a) for a in writes]
        for fp in rfps:
            scan(fp, False)
        for fp in wfps:
            scan(fp, True)
        known = self.known[q]
        kd = self.known_dma[q]
        for e, i in deps_c.items():
            ei = QIDX[e]
            if e == q and not dma:
                if q == 'pe':
                    continue
            if i <= known[ei]:
                continue
            op.waits_c.append((e, i))
            src = self.comp_ops[e][i]
            src.signal = True
            known[ei] = i
            for k, v in enumerate(src.snap):
                if v > known[k]:
                    known[k] = v
        for d in sorted(deps_d, key=lambda o: o.gid):
            if d.gid in kd:
                continue
            op.waits_d.append(d)
            kd.add(d.gid)
            for k, v in enumerate(d.snap):
                if v > known[k]:
                    known[k] = v
        if dma:
            n = self.dma_count[q]
            R = self.dma_slots[q]
            op.slot = n % R
            op.slot_cnt = n // R + 1
            if n >= R:
                prev = self.dma_ops[q][n - R]
                op.prev_slot = prev
                kd.add(prev.gid)
            self.dma_count[q] = n + 1
            self.dma_ops[q].append(op)
        op.snap = tuple(known)
        if not dma:
            self.comp_ops[q].append(op)
        for fp, is_write in [(f, False) for f in rfps] + [(f, True) for f in wfps]:
            name, p0, p1, f0, f1 = fp
            lst = self.recs.setdefault(name, [])
            if is_write:
                lst[:] = [r for r in lst if not (r[0] >= p0 and r[1] <= p1 and r[2] >= f0 and r[3] <= f1)]
            else:
                if not dma:
                    lst[:] = [r for r in lst if not (r[4] is False and (not r[5].dma) and r[5].q == q
                                                     and r[0] == p0 and r[1] == p1 and r[2] == f0 and r[3] == f1)]
            lst.append((p0, p1, f0, f1, is_write, op))
        stream.append(op)
        self.ops.append(op)
        return op

    comp_ops = None

    def start(self):
        self.comp_ops = {q: [] for q in QUEUES}

    def pe(self, fn, reads, writes):
        return self.add('pe', fn, reads, writes)

    def act(self, fn, reads, writes):
        return self.add('act', fn, reads, writes)

    def dve(self, fn, reads, writes):
        return self.add('dve', fn, reads, writes)

    def pool(self, fn, reads, writes):
        return self.add('pool', fn, reads, writes)

    def dma(self, out, in_, q='sp', **kw):
        return self.add(q, lambda e: e.dma_start(out=out, in_=in_, **kw), [in_], [out], dma=True)

    def emit(self, block, sems_c, sems_d):
        cum = {}
        for e in QUEUES:
            c = 0
            arr = []
            for o in self.comp_ops[e]:
                if o.signal:
                    c += 1
                arr.append(c)
            cum[e] = arr
        self.cum = cum

        def gen(q):
            def body(eng):
                for o in self.streams[q]:
                    for (e, i) in o.waits_c:
                        eng.wait_ge(sems_c[e], cum[e][i])
                    for d in o.waits_d:
                        eng.wait_ge(sems_d[d.q][d.slot], 16 * d.slot_cnt)
                    if o.prev_slot is not None:
                        p = o.prev_slot
                        eng.wait_ge(sems_d[p.q][p.slot], 16 * p.slot_cnt)
                    ins = o.fn(eng)
                    if o.dma:
                        ins.then_inc(sems_d[q][o.slot], 16)
                    elif o.signal:
                        ins.then_inc(sems_c[q], 1)
                R = self.dma_slots.get(q, 0)
                n = self.dma_count[q]
                for o in self.dma_ops[q][max(0, n - R):]:
                    eng.wait_ge(sems_d[q][o.slot], 16 * o.slot_cnt)
            return body

        if self.streams['pe']:
            block.tensor(gen('pe'))
        if self.streams['act']:
            block.scalar(gen('act'))
        if self.streams['dve']:
            block.vector(gen('dve'))
        if self.streams['pool']:
            block.gpsimd(gen('pool'))
        if self.streams['sp']:
            block.sync(gen('sp'))

import math
from contextlib import ExitStack
import numpy as np
import ml_dtypes
from concourse.bass_utils import run_bass_kernel_spmd

F32 = mybir.dt.float32
BF = mybir.dt.bfloat16
AF = mybir.ActivationFunctionType
ALU = mybir.AluOpType
AX = mybir.AxisListType

L = 4096
D = 1024
NT = 32
NB = 8
EPS = 1e-6
NF = 8192
HYW = 512
COL_V, COL_X1, COL_X2, COL_ZH = 0, 512, 1024, 1536
COL_CQ, COL_CKV, COL_KR, COL_ZA = 2048, 2432, 2688, 2720
COL_GH, COL_GA = 3232, 4256
MAGIC = 12582912.0
DBG = {}


def bcast(ap, axis, n):
    a = ap.unsqueeze(axis)
    shp = list(a.shape)
    shp[axis] = n
    return a.to_broadcast(shp)


class K:
    def __init__(self, P):
        self.P = P

    def mm(self, out, lhsT, rhs, start=True, stop=True):
        self.P.pe(lambda e: e.matmul(out, lhsT=lhsT, rhs=rhs, start=start, stop=stop), [lhsT, rhs], [out])

    def tr(self, out, in_, ident):
        self.P.pe(lambda e: e.transpose(out=out, in_=in_, identity=ident), [in_, ident], [out])

    def act(self, out, in_, func, bias=None, scale=None, accum_out=None):
        kw = {}
        reads = [in_]
        writes = [out]
        if bias is not None:
            kw['bias'] = bias
            if not isinstance(bias, (int, float)):
                reads.append(bias)
        if scale is not None:
            kw['scale'] = scale
            if not isinstance(scale, (int, float)):
                reads.append(scale)
        if accum_out is not None:
            kw['accum_out'] = accum_out
            writes.append(accum_out)
        self.P.act(lambda e: e.activation(out=out, in_=in_, func=func, **kw), reads, writes)

    def tt(self, eng, out, in0, in1, op):
        self.P.add(eng, lambda e: e.tensor_tensor(out=out, in0=in0, in1=in1, op=op), [in0, in1], [out])

    def ts(self, eng, out, in0, s1, s2=None, op0=ALU.mult, op1=None):
        reads = [in0]
        if not isinstance(s1, (int, float)):
            reads.append(s1)
        if s2 is not None and not isinstance(s2, (int, float)):
            reads.append(s2)
        if op1 is None:
            self.P.add(eng, lambda e: e.tensor_scalar(out=out, in0=in0, scalar1=s1, scalar2=None, op0=op0), reads, [out])
        else:
            self.P.add(eng, lambda e: e.tensor_scalar(out=out, in0=in0, scalar1=s1, scalar2=s2, op0=op0, op1=op1), reads, [out])

    def stt(self, out, in0, scalar, in1, op0, op1):
        reads = [in0, in1]
        if not isinstance(scalar, (int, float)):
            reads.append(scalar)
        self.P.dve(lambda e: e.scalar_tensor_tensor(out=out, in0=in0, scalar=scalar, in1=in1, op0=op0, op1=op1), reads, [out])

    def copy(self, eng, out, in_):
        if eng == 'act':
            self.act(out, in_, AF.Copy)
        else:
            self.P.add(eng, lambda e: e.tensor_copy(out=out, in_=in_), [in_], [out])

    def recip(self, out, in_):
        self.P.dve(lambda e: e.reciprocal(out=out, in_=in_), [in_], [out])

    def reduce_add(self, out, in_):
        self.P.dve(lambda e: e.tensor_reduce(out=out, in_=in_, axis=AX.X, op=ALU.add), [in_], [out])

    def memset(self, eng, ap, val):
        self.P.add(eng, lambda e: e.memset(ap, val), [], [ap])

    def dma(self, out, in_, q='sp'):
        self.P.dma(out, in_, q=q)


def _dump(P):
    cum = {}
    for e in QUEUES:
        c = 0
        arr = []
        for o in P.comp_ops[e]:
            if o.signal:
                c += 1
            arr.append(c)
        cum[e] = arr
    for q in QUEUES:
        print("== stream", q)
        for o in P.streams[q]:
            w = [f"{e}>={cum[e][i]}(op{i})" for e, i in o.waits_c] + [f"dma[{d.q}{d.slot}]>={16*d.slot_cnt}" for d in o.waits_d]
            if o.prev_slot is not None:
                w.append(f"prev dma[{o.prev_slot.q}{o.prev_slot.slot}]>={16*o.prev_slot.slot_cnt}")
            tag = f"DMA slot{o.slot} cnt{o.slot_cnt}" if o.dma else (f"op{o.idx} sig={cum[q][o.idx] if o.signal else '-'}")
            print("   ", tag, getattr(o, 'desc', ''), "waits:", w)


class Phase:
    def __init__(self, nc, name):
        self.nc = nc
        self.name = name
        self.es = ExitStack()

    def __enter__(self):
        nc = self.nc
        es = self.es
        es.__enter__()
        self.ps = [es.enter_context(nc.psum_tensor(f"{self.name}_ps{i}", [128, 512], F32)) for i in range(8)]
        self.sems_c = {e: es.enter_context(nc.semaphore(f"{self.name}_sc_{e}")) for e in QUEUES}
        self.sems_d = {q: [es.enter_context(nc.semaphore(f"{self.name}_sd_{q}{i}")) for i in range(n)]
                       for q, n in (('sp', 8), ('pool', 4), ('act', 4))}
        self.P = Prog(nc)
        self.P.start()
        self.k = K(self.P)
        return self

    def sb(self, name, shape, dt):
        return self.es.enter_context(self.nc.sbuf_tensor(f"{self.name}_{name}", shape, dt))

    def __exit__(self, *a):
        if a[0] is None:
            self.es.enter_context(self.nc.allow_low_precision("bf16 operands / intermediates by design"))
            block = self.es.enter_context(self.nc.Block())
            if DBG.get('dump') == self.name:
                _dump(self.P)
            self.P.emit(block, self.sems_c, self.sems_d)
        return self.es.__exit__(*a)


def make_ident(ph, ident):
    identf = ph.sb("identf", [128, 128], F32)
    ph.k.memset('pool', identf[:], 0.0)
    ph.P.pool(lambda e: e.affine_select(out=identf[:], in_=identf[:], pattern=[[-1, 128]], compare_op=ALU.not_equal,
                                        fill=1.0, base=0, channel_multiplier=1), [identf[:]], [identf[:]])
    ph.k.copy('dve', ident[:], identf[:])


def load_w(ph, dst, src_ap):
    ph.k.dma(dst, src_ap, q='pool')


def rstd_from_ss(k, rs_col, ss_col, n):
    k.act(rs_col, ss_col, AF.Sqrt, bias=EPS, scale=1.0 / n)
    k.recip(rs_col, rs_col)


def phase1_gen(ph, T, banks):
    k = ph.k
    xt = [ph.sb(f"xt{i}", [128, D], F32) for i in range(3)]
    xn = [ph.sb(f"xn{i}", [128, D], BF) for i in range(2)]
    junk = ph.sb("junk", [128, D], BF)
    ss = ph.sb("ss", [128, NT], F32)
    rs = ph.sb("rs", [128, NT], F32)
    ts_ = ph.sb("ts_", [128, NT], F32)
    mh = ph.sb("mh", [128, 1], F32)
    gT = ph.sb("gT", [128, 8], F32)
    ident = ph.sb("ident", [128, 128], BF)
    hb = [ph.sb(f"hb{i}", [128, 8, 512], BF) for i in range(2)]
    make_ident(ph, ident)
    k.memset('pool', mh[:], -0.5)
    k.dma(gT[:], T['gT'][:, :])
    pend = [None]
    for i in range(NT):
        x_t = xt[i % 3]
        k.dma(x_t[:], T['x'][128 * i:128 * (i + 1), :])
        k.act(junk[:], x_t[:], AF.Square, accum_out=ss[:, i:i + 1])
        rstd_pool(k, rs[:, i:i + 1], ss[:, i:i + 1], D, mh[:, 0:1], ts_[:, i:i + 1])
        x_n = xn[i % 2]
        k.ts('dve', x_n[:], x_t[:], rs[:, i:i + 1])
        bank = banks[i % len(banks)]
        pv = bank[:, :].bitcast(BF)
        for c in range(8):
            k.tr(pv[:, 128 * c:128 * (c + 1)], x_n[:, 128 * c:128 * (c + 1)], ident[:])
        if pend[0] is not None:
            pend[0]()

        def evac(i=i, pv=pv):
            h_b = hb[(i // 4) % 2]
            j = i % 4
            k.tt('dve', h_b[:, :, 128 * j:128 * (j + 1)], pv.rearrange("p (c t) -> p c t", c=8),
                 bcast(gT[:, :], 2, 128), ALU.mult)
            if j == 3:
                k.dma(T['hT_d'][i // 4], h_b[:].rearrange("p c t -> p (c t)"))
        pend[0] = evac
        yield
    pend[0]()
    yield


def rstd_pool(k, rs, ss, n, mhalf, tmp):
    k.ts('dve', tmp, ss, 1.0 / n, EPS, op0=ALU.mult, op1=ALU.add)
    k.tt('pool', rs, tmp, mhalf, ALU.pow)


def phase12(nc, T, with_p1=True, with_p2a=True):
    with Phase(nc, "p12") as ph:
        g1 = phase1_gen(ph, T, ph.ps[0:4]) if with_p1 else iter(())
        g2 = phase2a_gen(ph, T, ph.ps[4:8]) if with_p2a else iter(())
        alive1, alive2 = True, True
        while alive1 or alive2:
            if alive1:
                try:
                    next(g1)
                except StopIteration:
                    alive1 = False
            for _ in range(4):
                if alive2:
                    try:
                        next(g2)
                    except StopIteration:
                        alive2 = False


def run_interleaved(gens, width=2):
    active = []
    gens = list(gens)
    while gens or active:
        while gens and len(active) < width:
            active.append(gens.pop(0))
        for g in list(active):
            try:
                next(g)
            except StopIteration:
                active.remove(g)


def qk_norm_rope(ph, W, src, dst, g_rep, cs_t, sc_t, mhalf, use_act=False):
    k = ph.k
    sq, ssq, rk, ta, tb_, tmp = W['sq'], W['ssq'], W['rk'], W['ta'], W['tb'], W['tmp']
    if use_act:
        k.act(sq[:].rearrange("p a b -> p (a b)"), src[:].rearrange("p a b -> p (a b)"), AF.Square)
    else:
        k.tt('pool', sq[:], src[:], src[:], ALU.mult)
    k.reduce_add(ssq[:], sq[:])
    rstd_pool(k, rk[:], ssq[:], 96, mhalf[:, 0:8], tmp[:])
    yield
    k.tt('dve', src[:], src[:], bcast(rk[:, :], 2, 96), ALU.mult)
    k.tt('dve', src[:], src[:], bcast(g_rep, 1, 8), ALU.mult)
    yield
    t1 = bcast(src[:, :, 64:80], 1, 2)
    t2 = bcast(src[:, :, 80:96], 1, 2)
    k.tt('pool', ta[:], t1, bcast(cs_t, 2, 8), ALU.mult)
    k.tt('pool', tb_[:], t2, bcast(sc_t, 2, 8), ALU.mult)
    k.tt('dve', dst[:, :, 64:80], ta[:, 0], tb_[:, 0], ALU.subtract)
    k.tt('dve', dst[:, :, 80:96], ta[:, 1], tb_[:, 1], ALU.add)
    k.copy('pool', dst[:, :, 0:64], src[:, :, 0:64])
    yield


def phase3(nc, T):
    with Phase(nc, "p3") as ph:
        k = ph.k
        ps = ph.ps
        ident = ph.sb("ident", [128, 128], BF)
        make_ident(ph, ident)
        w_in_v = T['w_in'].rearrange("(k p) n -> p k n", p=128)
        Wkv = ph.sb("Wkv", [128, 8, 288], BF)
        Wq = ph.sb("Wq", [128, 8, 384], BF)
        Wuq = ph.sb("Wuq", [128, 3, 768], BF)
        Wukv = ph.sb("Wukv", [128, 2, 1024], BF)
        gcq = ph.sb("gcq", [128, 3], F32)
        gckv = ph.sb("gckv", [128, 2], F32)
        gqk = ph.sb("gqk", [128, 192], F32)
        csT = ph.sb("csT", [128, NT, 2, 16], F32)
        mhalf = ph.sb("mhalf", [128, 8], F32)
        k.memset('pool', mhalf[:], -0.5)
        load_w(ph, Wkv[:], w_in_v[:, :, COL_CKV:COL_CKV + 288])
        load_w(ph, Wq[:], w_in_v[:, :, COL_CQ:COL_CQ + 384])
        load_w(ph, Wuq[:], T['w_uq'].rearrange("(k p) n -> p k n", p=128))
        load_w(ph, Wukv[:], T['w_ukv'].rearrange("(k p) n -> p k n", p=128))
        k.dma(gcq[:], T['gcqT'][:, :])
        k.dma(gckv[:], T['gckvT'][:, :])
        k.dma(gqk[:], T['gqk'][0:1, :].partition_broadcast(128))
        cosv = T['cosT'].rearrange("p (a b) -> p a b", b=16)
        sinv = T['sinT'].rearrange("p (a b) -> p a b", b=16)
        k.dma(csT[:, :, 0, :], cosv)
        k.dma(csT[:, :, 1, :], sinv)
        k.tt('pool', Wuq[:], Wuq[:], bcast(gcq[:, :], 2, 768), ALU.mult)
        k.tt('pool', Wukv[:], Wukv[:], bcast(gckv[:, :], 2, 1024), ALU.mult)

        kT = ph.sb("kT", [128, 8, L], BF)
        vx = ph.sb("vx", [128, NT, 8, 65], BF)
        k.memset('pool', vx[:, :, :, 64:65], 1.0)
        ones = ph.sb("ones", [128, 64], BF)
        k.memset('pool', ones[:], 1.0)
        hbuf = [ph.sb(f"hbuf{i}", [128, 8, 512], BF) for i in range(2)]
        junk = [ph.sb(f"junk{i}", [128, 384], BF) for i in range(3)]
        ssl = ph.sb("ssl", [128, 2 * NT], F32)
        rsl = ph.sb("rsl", [128, 2 * NT], F32)
        tsl = ph.sb("tsl", [128, 2 * NT], F32)
        latn = [ph.sb(f"latn{i}", [128, 384], BF) for i in range(3)]
        latT = [ph.sb(f"latT{i}", [128, 3, 128], BF) for i in range(3)]
        kr = [ph.sb(f"kr{i}", [128, 32], F32) for i in range(3)]
        qk32 = [ph.sb(f"qk32_{i}", [128, 8, 96], F32) for i in range(3)]
        qkbf = [ph.sb(f"qkbf{i}", [128, 8, 96], BF) for i in range(3)]
        Wk_ = [dict(sq=ph.sb(f"sq{i}", [128, 8, 96], F32), ssq=ph.sb(f"ssq{i}", [128, 8], F32),
                    rk=ph.sb(f"rk{i}", [128, 8], F32), tmp=ph.sb(f"tmpn{i}", [128, 8], F32),
                    ta=ph.sb(f"ta{i}", [128, 2, 8, 16], F32), tb=ph.sb(f"tb{i}", [128, 2, 8, 16], F32)) for i in range(3)]
        qT = [ph.sb(f"qT{i}", [128, 8, 512], BF) for i in range(2)]
        pt = [ph.sb(f"pt{i}", [128, 512], BF) for i in range(3)]
        rsum = [ph.sb(f"rsum{i}", [128, 512], BF) for i in range(2)]
        bcs = [ph.sb(f"bcs{i}", [64, 512], BF) for i in range(2)]
        at = [ph.sb(f"at{i}", [64, 8, 512], BF) for i in range(1)]

        def sumsq(i, col, src_ps, n):
            k.act(junk[i % 3][:, 0:n], src_ps, AF.Square, accum_out=ssl[:, col:col + 1])
            rstd_pool(k, rsl[:, col:col + 1], ssl[:, col:col + 1], n, mhalf[:, 0:1], tsl[:, col:col + 1])

        def kv_tile(i, hb, j):
            par = i % 3
            if j == 0:
                k.dma(hb[:].rearrange("p c t -> p (c t)"), T['hT_d'][i // 4])
            lat = ps[par][:, 0:288]
            for c in range(8):
                k.mm(lat, hb[:, c, 128 * j:128 * (j + 1)], Wkv[:, c, :], start=(c == 0), stop=(c == 7))
            sumsq(i, i, lat[:, 0:256], 256)
            yield
            ln = latn[par]
            k.ts('dve', ln[:, 0:256], lat[:, 0:256], rsl[:, i:i + 1])
            k.copy('dve', kr[par][:], lat[:, 256:288])
            tbank = ps[7][:, :].bitcast(BF)
            for c in range(2):
                k.tr(tbank[:, 128 * c:128 * (c + 1)], ln[:, 128 * c:128 * (c + 1)], ident[:])
            lT = latT[par]
            k.copy('dve', lT[:, 0:2, :].rearrange("p c t -> p (c t)"), tbank[:, 0:256])
            yield
            kvb = [ps[3 + 2 * (i % 2)], ps[4 + 2 * (i % 2)]]
            for half in range(2):
                for c in range(2):
                    k.mm(kvb[half][:, :], lT[:, c, :], Wukv[:, c, 512 * half:512 * (half + 1)],
                         start=(c == 0), stop=(c == 1))
            kk = qk32[par]
            for half in range(2):
                kvv = kvb[half][:, :].rearrange("p (h e) -> p h e", h=4)
                k.copy('dve', kk[:, 4 * half:4 * half + 4, 0:64], kvv[:, :, 0:64])
                k.copy('dve', vx[:, i, 4 * half:4 * half + 4, 0:64], kvv[:, :, 64:128])
            k.copy('pool', kk[:, :, 64:96], bcast(kr[par][:, :], 1, 8))
            yield
            kf = qkbf[par]
            yield from qk_norm_rope(ph, Wk_[par], kk, kf, gqk[:, 96:192], csT[:, i, :, :], csT[:, i, ::-1, :], mhalf, use_act=True)
            kbank = ps[7][:, :].bitcast(BF)
            for h in range(8):
                k.tr(kbank[0:96, 128 * h:128 * (h + 1)], kf[:, h, :], ident[:])
            k.copy('dve', kT[0:96, :, 128 * i:128 * (i + 1)], kbank[0:96, :].rearrange("p (h t) -> p h t", h=8))
            yield

        def q_tile(i, hb, j, q_T):
            par = i % 2
            lat = ps[6][:, 0:384]
            for c in range(8):
                k.mm(lat, hb[:, c, 128 * j:128 * (j + 1)], Wq[:, c, :], start=(c == 0), stop=(c == 7))
            yield
            lsb = Wk_[par]['sq'][:].rearrange("p a b -> p (a b)")[:, 0:384]
            k.copy('dve', lsb, lat)
            k.P.dve(lambda e: e.scalar_tensor_tensor(out=junk[par][:, 0:384], in0=lsb, scalar=1.0, in1=lsb, op0=ALU.mult,
                                                     op1=ALU.mult, accum_out=ssl[:, NT + i:NT + i + 1]),
                    [lsb], [junk[par][:, 0:384], ssl[:, NT + i:NT + i + 1]])
            rstd_pool(k, rsl[:, NT + i:NT + i + 1], ssl[:, NT + i:NT + i + 1], 384, mhalf[:, 0:1], tsl[:, NT + i:NT + i + 1])
            yield
            ln = latn[par]
            k.ts('dve', ln[:, 0:384], lsb, rsl[:, NT + i:NT + i + 1])
            yield
            tbank = ps[7][:, :].bitcast(BF)
            for c in range(3):
                k.tr(tbank[:, 128 * c:128 * (c + 1)], ln[:, 128 * c:128 * (c + 1)], ident[:])
            yield
            lT = latT[par]
            k.copy('dve', lT[:].rearrange("p c t -> p (c t)"), tbank[:, 0:384])
            yield
            qq = qk32[par]
            for half in range(2):
                qb = ps[6][:, 0:384]
                for c in range(3):
                    k.mm(qb, lT[:, c, :], Wuq[:, c, 384 * half:384 * (half + 1)], start=(c == 0), stop=(c == 2))
                yield
                k.copy('dve', qq[:, 4 * half:4 * half + 4, :], qb.rearrange("p (h e) -> p h e", h=4))
                yield
            qf = qkbf[par]
            yield from qk_norm_rope(ph, Wk_[par], qq, qf, gqk[:, 0:96], csT[:, i, :, :], csT[:, i, ::-1, :], mhalf, use_act=(i < 4))
            yield
            yield
            yield
            yield
            qbank = ps[7][:, :].bitcast(BF)
            for h in range(8):
                k.tr(qbank[0:96, 128 * h:128 * (h + 1)], qf[:, h, :], ident[:])
            yield
            k.copy('dve', q_T[0:96, :, 128 * j:128 * (j + 1)], qbank[0:96, :].rearrange("p (h t) -> p h t", h=8))
            yield

        def kv_block(tb):
            hb = hbuf[tb % 2]
            return [kv_tile(tb * 4 + j, hb, j) for j in range(4)]

        def q_chunk_gens(qc):
            hb = hbuf[qc % 2]
            k.dma(hb[:].rearrange("p c t -> p (c t)"), T['hT_d'][qc])
            return [q_tile(qc * 4 + j, hb, j, qT[qc % 2]) for j in range(4)]

        gens = []
        for tb in range(DBG.get('nprep', NB)):
            gens += kv_block(tb)
        run_interleaved(gens, 3)
        nqc = DBG.get('nqc', NB)
        if nqc:
            run_interleaved(q_chunk_gens(0), 1)

        scale = 1.0 / math.sqrt(96.0)
        NH = DBG.get('nh', 8)
        pend = [None]
        for qc in range(nqc):
            q_T = qT[qc % 2]
            a_t = at[0]
            nxt = q_chunk_gens(qc + 1) if qc + 1 < nqc else []
            nxt_active = []
            steps = [(h, kt) for h in range(NH) for kt in range(NT)]

            def S(idx):
                h, kt = steps[idx]
                k.mm(ps[idx % 3][:, :], kT[0:96, h, 128 * kt:128 * (kt + 1)], q_T[0:96, h, :])

            def fin_a(h):
                k.recip(rsum[h % 2][64:65, :], ps[3 + (h % 2)][64:65, :])

            def fin_b(h):
                k.mm(ps[5][0:64, :], ones[64:65, :], rsum[h % 2][64:65, :])

            def fin_c(h, a_t):
                k.copy('dve', bcs[h % 2][:], ps[5][0:64, :])
                k.tt('dve', a_t[:, h, :], ps[3 + (h % 2)][0:64, :], bcs[h % 2][:], ALU.mult)

            S(0)
            S(1)
            for idx, (h, kt) in enumerate(steps):
                p_t = pt[idx % 3]
                k.act(p_t[:], ps[idx % 3][:, :], AF.Exp, scale=scale)
                if idx + 2 < len(steps):
                    S(idx + 2)
                ob = ps[3 + (h % 2)]
                k.mm(ob[0:65, :], vx[:, kt, h, :], p_t[:], start=(kt == 0), stop=(kt == NT - 1))
                if pend[0] is not None:
                    if kt == 1:
                        pend[0][0]()
                    elif kt == 10:
                        pend[0][1]()
                    elif kt == 14:
                        pend[0][2]()
                        pend[0] = None
                if kt == NT - 1:
                    last = (h == NH - 1)
                    pend[0] = (lambda h=h: fin_a(h), lambda h=h: fin_b(h),
                               (lambda h=h, a_t=a_t, qc=qc, last=last, fin_c=fin_c: (fin_c(h, a_t), k.dma(T['at_d'][qc], a_t[:].rearrange("p h t -> p (h t)")) if last else None)))
                if idx % 3 == 2:
                    while nxt and len(nxt_active) < 1:
                        nxt_active.append(nxt.pop(0))
                    for g in list(nxt_active):
                        try:
                            next(g)
                        except StopIteration:
                            nxt_active.remove(g)
            run_interleaved(nxt_active + nxt, 1)

        if pend[0] is not None:
            pend[0][0]()
            pend[0][1]()
            pend[0][2]()


def phase4(nc, T):
    with Phase(nc, "p4") as ph:
        k = ph.k
        ps = ph.ps
        w_in_v = T['w_in'].rearrange("(k p) n -> p k n", p=128)
        Wz = ph.sb("Wz", [128, 8, 512], BF)
        Wg = ph.sb("Wg", [128, 8, 2048], BF)
        Wao = ph.sb("Wao", [128, 4, D], BF)
        Who = ph.sb("Who", [128, 4, D], BF)
        Wout = ph.sb("Wout", [128, 8, D], BF)
        bg = ph.sb("bg", [128, 16], F32)
        load_w(ph, Wz[:], w_in_v[:, :, COL_ZA:COL_ZA + 512])
        for q4 in range(4):
            load_w(ph, Wg[:, :, 512 * q4:512 * (q4 + 1)], w_in_v[:, :, COL_GH + 512 * q4:COL_GH + 512 * (q4 + 1)])
        load_w(ph, Wao[:], T['w_attn_out'].rearrange("(hp p) n -> p hp n", p=128))
        load_w(ph, Who[:], T['w_hy_out'].rearrange("(k p) n -> p k n", p=128))
        load_w(ph, Wout[:], T['w_out'].rearrange("(k p) n -> p k n", p=128))
        k.dma(bg[:], T['bgT'][:, :])
        hbuf = [ph.sb(f"hbuf{i}", [128, 8, 512], BF) for i in range(2)]
        atb = [ph.sb(f"atb{i}", [128, 4, 512], BF) for i in range(2)]
        yzb = [ph.sb(f"yzb{i}", [128, 4, 512], BF) for i in range(2)]
        xt = [ph.sb(f"xt{i}", [128, D], F32) for i in range(3)]
        ot = [ph.sb(f"ot{i}", [128, D], F32) for i in range(2)]
        sz = [ph.sb(f"sz{i}", [128, 512], BF) for i in range(2)]
        ya = ph.sb("ya", [128, 4, 512], BF)
        gh = [ph.sb(f"gh{i}", [128, 512], BF) for i in range(2)]
        ga = [ph.sb(f"ga{i}", [128, 512], BF) for i in range(2)]
        m1 = [ph.sb(f"m1{i}", [128, 512], F32) for i in range(2)]
        m2 = [ph.sb(f"m2{i}", [128, 512], F32) for i in range(2)]
        mg = [ph.sb(f"mg{i}", [128, 8, 512], BF) for i in range(2)]
        for tb in range(NB):
            hb = hbuf[tb % 2]
            a_b = atb[tb % 2]
            y_b = yzb[tb % 2]
            k.dma(hb[:].rearrange("p c t -> p (c t)"), T['hT_d'][tb])
            atv = T['at_d'][tb].rearrange("p (hp two t) -> p two hp t", two=2, t=512)
            k.dma(a_b[0:64, :, :], atv[:, 0, :, :])
            k.dma(a_b[64:128, :, :], atv[:, 1, :, :])
            k.dma(y_b[:].rearrange("p c t -> p (c t)"), T['yz_d'][tb])
            for hp in range(4):
                zb = ps[hp % 2][:, :]
                for c in range(8):
                    k.mm(zb, Wz[:, c, 128 * hp:128 * (hp + 1)], hb[:, c, :], start=(c == 0), stop=(c == 7))
                s_z = sz[hp % 2]
                k.act(s_z[:], zb, AF.Silu)
                k.tt('pool', ya[:, hp, :], a_b[:, hp, :], s_z[:], ALU.mult)
            m_g = mg[tb % 2]
            for dc in range(8):
                g1 = ps[2 + (dc % 2)]
                g2 = ps[4 + (dc % 2)]
                for c in range(8):
                    k.mm(g1[:, :], Wg[:, c, 128 * dc:128 * (dc + 1)], hb[:, c, :], start=(c == 0), stop=(c == 7))
                for c in range(8):
                    k.mm(g2[:, :], Wg[:, c, 1024 + 128 * dc:1024 + 128 * (dc + 1)], hb[:, c, :],
                         start=(c == 0), stop=(c == 7))
                k.act(gh[dc % 2][:], g1[:, :], AF.Sigmoid, bias=bg[:, dc:dc + 1])
                k.act(ga[dc % 2][:], g2[:, :], AF.Sigmoid, bias=bg[:, 8 + dc:9 + dc])
                uh = ps[6]
                ua = ps[7]
                for c in range(4):
                    k.mm(uh[:, :], Who[:, c, 128 * dc:128 * (dc + 1)], y_b[:, c, :], start=(c == 0), stop=(c == 3))
                for hp in range(4):
                    k.mm(ua[:, :], Wao[:, hp, 128 * dc:128 * (dc + 1)], ya[:, hp, :], start=(hp == 0), stop=(hp == 3))
                k.tt('dve', m1[dc % 2][:], uh[:, :], gh[dc % 2][:], ALU.mult)
                k.tt('dve', m2[dc % 2][:], ua[:, :], ga[dc % 2][:], ALU.mult)
                k.tt('pool', m_g[:, dc, :], m1[dc % 2][:], m2[dc % 2][:], ALU.add)
            for j in range(4):
                i = tb * 4 + j
                x_t = xt[i % 3]
                k.dma(x_t[:], T['x'][128 * i:128 * (i + 1), :])
                o_t = ot[i % 2]
                for half in range(2):
                    fb = ps[half]
                    for c in range(8):
                        k.mm(fb[:, :], m_g[:, c, 128 * j:128 * (j + 1)], Wout[:, c, 512 * half:512 * (half + 1)],
                             start=(c == 0), stop=(c == 7))
                    k.tt('dve', o_t[:, 512 * half:512 * (half + 1)], fb[:, :], x_t[:, 512 * half:512 * (half + 1)], ALU.add)
                k.dma(T['out'][128 * i:128 * (i + 1), :], o_t[:])


def fft_constants():
    C = {}
    n = NF
    s2 = np.arange(128, dtype=np.float64)[:, None]
    f2 = np.arange(128, dtype=np.float64)[None, :]
    th = 2 * np.pi * (f2 + 0.5) * s2 / 256.0
    C['FA1'] = np.concatenate([np.cos(th), -np.sin(th)], 1)
    th2 = 2 * np.pi * (f2 + 0.5) * (s2 + 128) / 256.0
    C['FA2'] = -np.concatenate([np.cos(th2), -np.sin(th2)], 1)
    s1 = np.arange(32, dtype=np.float64)
    tw = np.exp(-2j * np.pi * (np.arange(128)[None, :] + 0.5) * s1[:, None] / n)
    twq = np.tile(tw, (4, 1))
    C['TWa'] = np.concatenate([twq.real, twq.real], 1)
    C['TWb'] = np.concatenate([-twq.imag, twq.imag], 1)
    W = np.exp(-2j * np.pi * np.outer(s1, s1) / 32.0)
    Wq = np.kron(np.eye(4), W)
    C['WBr'] = Wq.real
    C['WBi'] = Wq.imag
    C['WBni'] = -Wq.imag
    Wi = np.exp(2j * np.pi * np.outer(s1, s1) / 32.0)
    Wiq = np.kron(np.eye(4), Wi)
    C['WI1'] = np.concatenate([Wiq.real, Wiq.imag], 1)
    C['WI2'] = np.concatenate([-Wiq.imag, Wiq.real], 1)
    twi = np.exp(2j * np.pi * (np.arange(128)[:, None] + 0.5) * s1[None, :] / n)
    twiq = np.tile(twi, (1, 4))
    C['TIa'] = np.concatenate([twiq.real, twiq.real], 1)
    C['TIb'] = np.concatenate([-twiq.imag, twiq.imag], 1)
    t2 = np.arange(128, dtype=np.float64)[None, :]
    f2c = np.arange(128, dtype=np.float64)[:, None]
    th3 = 2 * np.pi * (f2c + 0.5) * t2 / 256.0
    C['FIr'] = (2.0 / n) * np.cos(th3)
    C['FIi'] = -(2.0 / n) * np.sin(th3)
    return C


def filter_constants():
    C = {}
    f32 = np.float32
    t = np.linspace(0.0, 1.0, L, dtype=f32)[:, None]
    bands = 16
    f = np.linspace(1e-4, bands - 1, bands, dtype=f32)
    ang = (f32(2.0 * np.pi / L) * np.arange(L, dtype=f32)[:, None] * f[None, :]).astype(f32)
    z = np.concatenate([t, np.cos(ang).astype(f32), -np.sin(ang).astype(f32)], axis=-1).astype(f32)
    zs = np.zeros((128, L), f32)
    zs[0:33, :] = z.T
    zs[64:97, :] = z[::-1].T
    hi = zs.astype(ml_dtypes.bfloat16)
    lo = (zs - hi.astype(f32)).astype(ml_dtypes.bfloat16)
    C['zs_hi'] = hi
    C['zs_lo'] = lo
    tl = t[:, 0]
    tf = np.zeros((128, 2, 32), f32)
    pidx = np.arange(128)[:, None] * 32 + np.arange(32)[None, :]
    tf[:, 0, :] = tl[pidx]
    tf[:, 1, :] = tl[4095 - pidx]
    C['tfull'] = tf.reshape(128, 64)
    MIN_DECAY = math.log(1e-2) / 1.5
    MAX_DECAY = math.log(1e-2) / 0.3
    deltas = np.abs(np.linspace(MIN_DECAY, MAX_DECAY, HYW, dtype=f32)).astype(f32)
    C['negd'] = (-deltas)[None, :].astype(f32)
    return C


def _sin_layer(ph, W, pre_ps, fr, fb, out32):
    k = ph.k
    a, kk = W['a'], W['kk']
    k.ts('dve', a[:], pre_ps, fr, fb, op0=ALU.mult, op1=ALU.add)
    yield
    k.ts('dve', kk[:], a[:], 1.0 / (2 * math.pi), MAGIC, op0=ALU.mult, op1=ALU.add)
    yield
    k.ts('dve', kk[:], kk[:], -MAGIC, None, op0=ALU.add)
    yield
    k.stt(a[:], kk[:], -2 * math.pi, a[:], ALU.mult, ALU.add)
    yield
    k.ts('dve', a[:], a[:], -3.14159, 3.14159, op0=ALU.max, op1=ALU.min)
    yield
    k.act(out32, a[:], AF.Sin)
    yield


def _hilo(ph, hi, lo, src32, tmp32):
    k = ph.k
    k.copy('dve', hi, src32)
    k.copy('pool', tmp32, hi)
    k.tt('pool', lo, src32, tmp32, ALU.subtract)


def phase2a_gen(ph, T, banks):
    k = ph.k
    zs_hi = ph.sb("zs_hi", [128, L], BF)
    zs_lo = ph.sb("zs_lo", [128, L], BF)
    W1 = ph.sb("W1", [128, 128], F32)
    W2 = ph.sb("W2", [128, 128], F32)
    W1h = ph.sb("W1h", [128, 128], BF)
    W1l = ph.sb("W1l", [128, 128], BF)
    W2h = ph.sb("W2h", [128, 128], BF)
    W2l = ph.sb("W2l", [128, 128], BF)
    wt = ph.sb("wt", [128, 128], F32)
    mv = ph.sb("mv", [128, 4], F32)
    fb = ph.sb("fb", [128, 2], F32)
    Wk = dict(a=ph.sb("a", [128, 512], F32), kk=ph.sb("kk", [128, 512], F32))
    h1 = ph.sb("h1", [128, 512], F32)
    h1h = ph.sb("h1h", [128, 512], BF)
    h1l = ph.sb("h1l", [128, 512], BF)
    t32 = ph.sb("t32", [128, 512], F32)
    h2 = ph.sb("h2", [128, 512], F32)
    h2b = [ph.sb(f"h2b{i}", [128, 512], BF) for i in range(2)]
    k.dma(zs_hi[:], T['zs_hi'][:, :])
    k.dma(zs_lo[:], T['zs_lo'][:, :])
    k.dma(W1[:], T['W1blk'][:, :])
    k.dma(W2[:], T['W2blk'][:, :])
    k.dma(mv[:], T['mlpv'][:, :])
    _hilo(ph, W1h[:], W1l[:], W1[:], wt[:])
    _hilo(ph, W2h[:], W2l[:], W2[:], wt[:])
    k.tt('dve', fb[:, 0:1], mv[:, 0:1], mv[:, 1:2], ALU.mult)
    k.tt('dve', fb[:, 1:2], mv[:, 2:3], mv[:, 3:4], ALU.mult)
    yield
    for cch in range(NB):
        sl = slice(512 * cch, 512 * (cch + 1))
        b1 = banks[cch % 2]
        k.mm(b1[:, :], W1h[:], zs_hi[:, sl], start=True, stop=False)
        k.mm(b1[:, :], W1h[:], zs_lo[:, sl], start=False, stop=False)
        k.mm(b1[:, :], W1l[:], zs_hi[:, sl], start=False, stop=True)
        yield from _sin_layer(ph, Wk, b1[:, :], mv[:, 0:1], fb[:, 0:1], h1[:])
        _hilo(ph, h1h[:], h1l[:], h1[:], t32[:])
        yield
        b2 = banks[2 + cch % 2]
        k.mm(b2[:, :], W2h[:], h1h[:], start=True, stop=False)
        k.mm(b2[:, :], W2h[:], h1l[:], start=False, stop=False)
        k.mm(b2[:, :], W2l[:], h1h[:], start=False, stop=True)
        yield from _sin_layer(ph, Wk, b2[:, :], mv[:, 2:3], fb[:, 1:2], h2[:])
        hb_ = h2b[cch % 2]
        k.copy('pool', hb_[:], h2[:])
        k.dma(T['h2_d'][:, sl], hb_[:])
        yield


def _cmul_tab(ph, W, src, Ta, Tb, out_bf):
    k = ph.k
    P1, P2 = W
    sw = src.rearrange("p (r f) -> p r f", r=2)[:, ::-1, :]
    k.tt('dve', P1[:], src, Ta, ALU.mult)
    k.tt('dve', P2[:].rearrange("p (r f) -> p r f", r=2), sw, Tb.rearrange("p (r f) -> p r f", r=2), ALU.mult)
    k.tt('pool', out_bf, P1[:], P2[:], ALU.add)


def phase2b(nc, T):
    with Phase(nc, "p2b") as ph:
        k = ph.k
        ps = ph.ps
        ident = ph.sb("ident", [128, 128], BF)
        make_ident(ph, ident)
        w_in_v = T['w_in'].rearrange("(k p) n -> p k n", p=128)
        cols = (COL_V, COL_X1, COL_X2, COL_ZH)
        ar = ph.sb("arena", [128, 24592], BF)
        Wblk = ar[:, 20486:24582].rearrange("p (k w c) -> p k w c", k=8, w=4)
        for w in range(4):
            load_w(ph, Wblk[:, :, w, :], w_in_v[:, :, cols[w]:cols[w] + 128])
        cb16 = {}
        for nm, w in (('FA1', 256), ('FA2', 256), ('WBr', 128), ('WBi', 128), ('WBni', 128), ('WI1', 256), ('WI2', 256),
                      ('FIr', 128), ('FIi', 128)):
            cb16[nm] = ph.sb(nm, [128, w], BF)
            k.dma(cb16[nm][:], T[nm][:, :])
        c32 = {}
        for nm in ('TWa', 'TWb', 'TIa', 'TIb'):
            c32[nm] = ph.sb(nm, [128, 256], F32)
            k.dma(c32[nm][:], T[nm][:, :])
        h2s = ph.sb("h2s", [128, L], BF)
        k.dma(h2s[:], T['h2_d'][:, :])
        h2p = ph.sb("h2p", [128, 32, 128], BF)
        k.copy('pool', h2p[:], h2s[:].rearrange("q (p s) -> q s p", s=32))
        W3 = ph.sb("W3", [128, 2048], BF)
        load_w(ph, W3[:], T['W3blk'][:, :])
        wsh = ph.sb("wsh", [128, 12, 4], F32)
        k.dma(wsh[:].rearrange("p a b -> p (a b)"), T['wsh'][:, :])
        biasT = ph.sb("biasT", [128, 2, 128], F32)
        k.dma(biasT[:].rearrange("p a b -> p (a b)"), T['biasT'][:, :])
        negd = ph.sb("negd", [128, HYW], F32)
        k.dma(negd[:], T['negd'][0:1, :].partition_broadcast(128))
        tfull = ph.sb("tfull", [128, 2, 32], F32)
        k.dma(tfull[:].rearrange("p a b -> p (a b)"), T['tfull'][:, :])

        hbuf = [ph.sb(f"hbuf{i}", [128, 8, 512], BF) for i in range(2)]
        raw = [ar[:, 4098 * i:4098 * (i + 1)] for i in range(3)]
        ub_ = [ar[:, 12294 + 4096 * i:12294 + 4096 * (i + 1)] for i in range(2)]
        k_tm = ar[:, 0:8192].rearrange("p (o d c s) -> p o d c s", o=2, d=2, c=64)
        AB = ar[:, 8192:12288].rearrange("p (d c s) -> p d c s", d=2, c=64)
        Ksp = ar[:, 12288:20480].rearrange("p (o g r f) -> p o g r f", o=2, g=16, r=2)
        Gbuf = ar[:, 20480:24576].rearrange("p (r c s) -> p r c s", r=2, c=64)
        sz = ph.sb("sz", [128, L], BF)
        tm = [ph.sb(f"tm{i}", [128, 128, 32], BF) for i in range(3)]
        z2_tm = ph.sb("z2_tm", [128, 128, 32], BF)
        y_sc = ph.sb("y_sc", [128, 32, 128], BF)
        yzb = ph.sb("yzb", [128, L], BF)
        arg32 = ph.sb("arg32", [128, 4096], F32)
        PW = [(ph.sb(f"P1_{i}", [128, 256], F32), ph.sb(f"P2_{i}", [128, 256], F32)) for i in range(2)]
        Zp = [ph.sb(f"Zp{i}", [128, 256], BF) for i in range(2)]
        Yb = [ph.sb(f"Yb{i}", [128, 256], BF) for i in range(2)]
        Kev = [ph.sb(f"Kev{i}", [128, 256], BF) for i in range(2)]
        Esb = [[ph.sb(f"E{st}_{i}", [128, 256], BF) for i in range(2)] for st in range(3)]
        cnt = [0]

        ZA = [ph.sb(f"ZA{i}", [128, 512], BF) for i in range(2)]
        ZB = [ph.sb(f"ZB{i}", [128, 512], BF) for i in range(2)]
        YA = [ph.sb(f"YA{i}", [128, 512], BF) for i in range(2)]
        YB = [ph.sb(f"YB{i}", [128, 512], BF) for i in range(2)]
        G2 = ph.sb("G2", [128, 2, 64, 32], BF)
        WI1n = ph.sb("WI1n", [128, 256], BF)
        k.ts('pool', WI1n[:], cb16['WI1'][:], -1.0, None, op0=ALU.mult)

        def v4(ap):
            return ap.rearrange("p (u r f) -> p u r f", u=2, r=2)

        def tab4(t):
            return bcast(t.rearrange("p (r f) -> p r f", r=2), 1, 2)

        def run_skewed(items, hook=None):
            n = len(items)
            depth = max(len(it) for it in items)
            for t in range(n + depth - 1):
                for s_ in reversed(range(depth)):
                    i = t - s_
                    if 0 <= i < n and s_ < len(items[i]):
                        items[i][s_](i)
                if hook is not None:
                    hook(t)

        def cmul_pair(bank, Ta, Tb, outA, outB):
            k.tt('dve', outA, v4(bank), tab4(Ta), ALU.mult)
            k.tt('dve', outB, v4(bank)[:, :, ::-1, :], tab4(Tb), ALU.mult)

        def st_za(lhs_of, q):
            def f(i):
                bank = ps[i % 2]
                for u in range(2):
                    l1, l2 = lhs_of(2 * q + u)
                    za = bank[:, 256 * u:256 * (u + 1)]
                    k.mm(za, l1, cb16['FA1'][:], start=True, stop=(l2 is None))
                    if l2 is not None:
                        k.mm(za, l2, cb16['FA2'][:], start=False, stop=True)
            return f

        def st_tw(i):
            cmul_pair(ps[i % 2][:, :], c32['TWa'][:], c32['TWb'][:], v4(ZA[i % 2][:]), v4(ZB[i % 2][:]))

        def st_ub(i):
            bank = ps[2 + i % 2]
            for u in range(2):
                ub = bank[:, 256 * u:256 * (u + 1)]
                for n_, z_p in enumerate((ZA[i % 2], ZB[i % 2])):
                    zr = z_p[:, 256 * u:256 * u + 128]
                    zi = z_p[:, 256 * u + 128:256 * (u + 1)]
                    k.mm(ub[:, 0:128], cb16['WBr'][:], zr, start=(n_ == 0), stop=False)
                    k.mm(ub[:, 0:128], cb16['WBni'][:], zi, start=False, stop=(n_ == 1))
                for n_, z_p in enumerate((ZA[i % 2], ZB[i % 2])):
                    zr = z_p[:, 256 * u:256 * u + 128]
                    zi = z_p[:, 256 * u + 128:256 * (u + 1)]
                    k.mm(ub[:, 128:256], cb16['WBi'][:], zr, start=(n_ == 0), stop=False)
                    k.mm(ub[:, 128:256], cb16['WBr'][:], zi, start=False, stop=(n_ == 1))

        def filt_gen(cb, hbk, bank_fixed):
            gcol = 128 * cb + 64 * hbk
            k.tt('dve', arg32[:].rearrange("p (d c s) -> p d c s", d=2, c=64),
                 bcast(bcast(negd[:, gcol:gcol + 64], 1, 2), 3, 32),
                 bcast(tfull[:, :, :], 2, 64), ALU.mult)
            yield
            k.act(AB.rearrange("p d c s -> p (d c s)"), arg32[:], AF.Exp)
            yield
            wc0 = 256 * (2 * cb + hbk)
            for s1 in range(32):
                kb_ = (bank_fixed if bank_fixed is not None else ps[6 + s1 % 2])[:, 0:256]
                k.mm(kb_, h2p[:, s1, :], W3[:, wc0:wc0 + 256])
                abv = AB[:, :, :, s1].rearrange("p d c -> p (d c)")
                k.tt('dve', k_tm[:, :, :, :, s1].rearrange("p o d c -> p o (d c)"),
                     kb_.rearrange("p (o x) -> p o x", o=2), bcast(abv, 1, 2), ALU.mult)
                yield

        pref = [None]
        for cb in range(DBG.get('ncb', 4)):
            for w in range(4):
                if cb > 0:
                    load_w(ph, Wblk[:, :, w, :], w_in_v[:, :, cols[w] + 128 * cb:cols[w] + 128 * (cb + 1)])
            for w in range(3):
                k.memset('pool', raw[w][:, 0:1], 0.0)
                k.memset('pool', raw[w][:, 4097:4098], 0.0)
            for tb in range(NB):
                hb = hbuf[tb % 2]
                k.dma(hb[:].rearrange("p c t -> p (c t)"), T['hT_d'][tb])
                for w in range(4):
                    bank = ps[(tb * 4 + w) % 2]
                    for c in range(8):
                        k.mm(bank[:, :], Wblk[:, c, w, :], hb[:, c, :], start=(c == 0), stop=(c == 7))
                    if w < 3:
                        k.act(raw[w][:, 1 + 512 * tb:1 + 512 * (tb + 1)], bank[:, :], AF.Copy)
                    else:
                        k.act(sz[:, 512 * tb:512 * (tb + 1)], bank[:, :], AF.Silu)
            if DBG.get('s2b', 9) < 2: continue
            for w in range(3):
                u = ub_[w % 2]
                j = 4 * w + cb
                k.ts('dve', u, raw[w][:, 1:4097], wsh[:, j, 1:2], wsh[:, j, 3:4], op0=ALU.mult, op1=ALU.add)
                k.stt(u, raw[w][:, 0:4096], wsh[:, j, 0:1], u, ALU.mult, ALU.add)
                k.stt(u, raw[w][:, 2:4098], wsh[:, j, 2:3], u, ALU.mult, ALU.add)
                for a in range(4):
                    pv = ps[2 + a % 2][:, :].bitcast(BF)
                    for e in range(8):
                        s1 = 8 * a + e
                        k.tr(pv[:, 128 * e:128 * (e + 1)], u[:, s1:4096:32], ident[:])
                    k.copy('dve', tm[w][:, :, 8 * a:8 * a + 8], pv.rearrange("p (s c) -> p c s", s=8))
            if DBG.get('dump_tm'):
                k.dma(T['dbg_tm'][:, :], tm[DBG['dump_tm'] - 1][:].rearrange("p c s -> p (c s)"))
            if DBG.get('s2b', 9) < 3: continue
            for hbk in range(DBG.get('nhbk', 2)):
                c0 = 64 * hbk
                gcol = 128 * cb + c0
                if hbk == 0 or not DBG.get('pref', 1):
                    for _ in filt_gen(cb, hbk, None):
                        pass
                else:
                    for _ in pref[0]:
                        pass
                if DBG.get('s2b', 9) < 4: continue
                def spec_lhs(gi):
                    o, g = divmod(gi, 16)
                    return (k_tm[:, o, 0, 4 * g:4 * g + 4, :].rearrange("p c s -> p (c s)"),
                            k_tm[:, o, 1, 4 * g:4 * g + 4, :].rearrange("p c s -> p (c s)"))

                def st_kev(q):
                    def f(i):
                        o, gp = divmod(q, 8)
                        k.copy('act', Ksp[:, o, 2 * gp:2 * gp + 2, :, :].rearrange("p g r f -> p (g r f)"), ps[2 + i % 2][:, :])
                        if gp == 7:
                            gg0 = gcol // 4
                            k.tt('pool', Ksp[:, o, :, 0, :], Ksp[:, o, :, 0, :], bcast(biasT[:, o, gg0:gg0 + 16], 2, 128), ALU.add)
                    return f

                def conv_lhs_of(src):
                    def f(g):
                        return (src[:, c0 + 4 * g:c0 + 4 * g + 4, :].rearrange("p c s -> p (c s)"), None)
                    return f

                def st_mul(o, q):
                    def f(i):
                        bank = ps[2 + i % 2][:, :]
                        kr_ = bcast(Ksp[:, o, 2 * q:2 * q + 2, 0, :], 2, 2)
                        ki_ = bcast(Ksp[:, o, 2 * q:2 * q + 2, 1, :], 2, 2)
                        k.tt('dve', v4(YA[i % 2][:]), v4(bank), kr_, ALU.mult)
                        k.tt('dve', v4(YB[i % 2][:]), v4(bank)[:, :, ::-1, :], ki_, ALU.mult)
                    return f

                def st_gb(i):
                    bank = ps[4 + i % 2]
                    ya, yb_ = YA[i % 2], YB[i % 2]
                    for u in range(2):
                        gb = bank[:, 256 * u:256 * (u + 1)]
                        k.mm(gb, ya[:, 256 * u:256 * u + 128], cb16['WI1'][:], start=True, stop=False)
                        k.mm(gb, yb_[:, 256 * u:256 * u + 128], WI1n[:], start=False, stop=False)
                        k.mm(gb, ya[:, 256 * u + 128:256 * (u + 1)], cb16['WI2'][:], start=False, stop=False)
                        k.mm(gb, yb_[:, 256 * u + 128:256 * (u + 1)], cb16['WI2'][:], start=False, stop=True)

                def st_itw(o, q):
                    def f(i):
                        bank = ps[4 + i % 2][:, :]
                        g1o = Gbuf[:, :, 8 * q:8 * q + 8, :].rearrange("p r (u c) s -> p r u (c s)", u=2)
                        g2o = G2[:, :, 8 * q:8 * q + 8, :].rearrange("p r (u c) s -> p r u (c s)", u=2)
                        k.tt('dve', g1o, v4(bank).rearrange("p u r f -> p r u f"),
                             tab4(c32['TIa'][:]).rearrange("p u r f -> p r u f"), ALU.mult)
                        k.tt('dve', g2o, v4(bank)[:, :, ::-1, :].rearrange("p u r f -> p r u f"),
                             tab4(c32['TIb'][:]).rearrange("p u r f -> p r u f"), ALU.mult)
                    return f

                def st_inva(o, q):
                    def f(i):
                        if q % 2 == 1:
                            cc = q // 2
                            gate = tm[1] if o == 0 else tm[2]
                            yb = ps[6]
                            for n_, gsrc in enumerate((Gbuf, G2)):
                                k.mm(yb[:, :], cb16['FIr'][:], gsrc[:, 0, 16 * cc:16 * cc + 16, :].rearrange("p c s -> p (c s)"),
                                     start=(n_ == 0), stop=False)
                                k.mm(yb[:, :], cb16['FIi'][:], gsrc[:, 1, 16 * cc:16 * cc + 16, :].rearrange("p c s -> p (c s)"),
                                     start=False, stop=(n_ == 1))
                            cs = slice(c0 + 16 * cc, c0 + 16 * cc + 16)
                            if o == 0:
                                k.tt('dve', z2_tm[:, cs, :], yb[:, :].rearrange("p (c s) -> p c s", c=16), gate[:, cs, :], ALU.mult)
                            else:
                                k.tt('dve', y_sc[:, :, cs].rearrange("p s c -> p c s"),
                                     yb[:, :].rearrange("p (c s) -> p c s", c=16), gate[:, cs, :], ALU.mult)
                    return f

                items = []
                for q in range(16):
                    items.append([st_za(spec_lhs, q), st_tw, st_ub, st_kev(q)])
                for o in range(DBG.get('nord', 2)):
                    src = tm[0] if o == 0 else z2_tm
                    if o == 1:
                        items += [[] for _ in range(DBG.get('gap', 0))]
                    for q in range(8):
                        items.append([st_za(conv_lhs_of(src), q), st_tw, st_ub, st_mul(o, q), st_gb, st_itw(o, q), st_inva(o, q)])
                hook = None
                if hbk == 0 and DBG.get('pref', 1):
                    pref[0] = filt_gen(cb, 1, ps[7])

                    def hook(t, g=pref[0]):
                        if t >= 19:
                            next(g, None)
                            next(g, None)
                run_skewed(items, hook)
            if DBG.get('dump_z2'):
                k.dma(T['dbg_tm'][:, :], z2_tm[:].rearrange("p c s -> p (c s)"))
            if DBG.get('s2b', 9) < 6: continue
            for a in range(4):
                pv = ps[2 + a % 2][:, :].bitcast(BF)
                for e in range(8):
                    k.tr(pv[:, 128 * e:128 * (e + 1)], y_sc[:, 8 * a + e, :], ident[:])
                k.tt('dve', yzb[:].rearrange("c (p s) -> c p s", s=32)[:, :, 8 * a:8 * a + 8],
                     pv.rearrange("c (s p) -> c p s", s=8),
                     sz[:].rearrange("c (p s) -> c p s", s=32)[:, :, 8 * a:8 * a + 8], ALU.mult)
            for tb in range(NB):
                k.dma(T['yz_d'][tb][:, 512 * cb:512 * (cb + 1)], yzb[:, 512 * tb:512 * (tb + 1)])


def phase2(nc, T):
    if 'b' in DBG.get('p2', 'ab'):
        phase2b(nc, T)


def _bf(a):
    return np.asarray(a, np.float32).astype(ml_dtypes.bfloat16)


_CONST_CACHE = {}


def host_constants():
    if _CONST_CACHE:
        return _CONST_CACHE
    C = {}
    pos = np.arange(L, dtype=np.float32)
    inv_freq = (np.float32(10000.0) ** (-np.arange(0, 32, 2, dtype=np.float32) / np.float32(32))).astype(np.float32)
    ang = (pos[:, None] * inv_freq[None, :]).astype(np.float32)
    C['cosT'] = np.ascontiguousarray(np.cos(ang).astype(np.float32).reshape(NT, 128, 16).transpose(1, 0, 2).reshape(128, NT * 16))
    C['sinT'] = np.ascontiguousarray(np.sin(ang).astype(np.float32).reshape(NT, 128, 16).transpose(1, 0, 2).reshape(128, NT * 16))
    F = fft_constants()
    for nm in ('FA1', 'FA2', 'WBr', 'WBi', 'WBni', 'WI1', 'WI2', 'FIr', 'FIi'):
        C[nm] = np.ascontiguousarray(_bf(F[nm]))
    for nm in ('TWa', 'TWb', 'TIa', 'TIb'):
        C[nm] = np.ascontiguousarray(F[nm].astype(np.float32))
    C.update(filter_constants())
    _CONST_CACHE.update(C)
    return _CONST_CACHE


def prep_inputs(inp, b):
    f32 = np.float32
    m = {}
    m['x'] = np.ascontiguousarray(inp['x'][b], dtype=f32)
    m['w_in'] = np.ascontiguousarray(inp['w_in'][0], dtype=f32)
    m['gT'] = np.ascontiguousarray(inp['g_norm'][0].reshape(8, 128).T, dtype=f32)
    m['bgT'] = np.ascontiguousarray(inp['b_gate'][0].reshape(16, 128).T, dtype=f32)
    m['w_uq'] = np.ascontiguousarray(inp['w_uq'][0], dtype=f32)
    m['w_ukv'] = np.ascontiguousarray(inp['w_ukv'][0], dtype=f32)
    m['gcqT'] = np.ascontiguousarray(inp['g_cq'][0].reshape(3, 128).T, dtype=f32)
    m['gckvT'] = np.ascontiguousarray(inp['g_ckv'][0].reshape(2, 128).T, dtype=f32)
    m['gqk'] = np.ascontiguousarray(np.concatenate([inp['g_qn'][0], inp['g_kn'][0]])[None, :], dtype=f32)
    m['w_attn_out'] = np.ascontiguousarray(inp['w_attn_out'][0], dtype=f32)
    m['w_hy_out'] = np.ascontiguousarray(inp['w_hy_out'][0], dtype=f32)
    m['w_out'] = np.ascontiguousarray(inp['w_out'][0], dtype=f32)
    wsh = np.zeros((128, 12, 4), f32)
    wsh[:, :, 0:3] = inp['w_short'][0].reshape(3, 12, 128).transpose(2, 1, 0)
    wsh[:, :, 3] = inp['b_short'][0].reshape(12, 128).T
    m['wsh'] = wsh.reshape(128, 48)
    hb = inp['hy_bias'][0]
    bT = hb.reshape(2, 128, 4).transpose(2, 0, 1)
    m['biasT'] = np.ascontiguousarray(np.repeat(bT[:, None], 32, axis=1).reshape(128, 256), dtype=f32)
    W1 = np.zeros((128, 128), f32)
    W1[0:33, 0:64] = inp['w_f1'][0]
    W1[64:97, 64:128] = inp['w_f1'][0]
    m['W1blk'] = W1
    W2 = np.zeros((128, 128), f32)
    W2[0:64, 0:64] = inp['w_f2'][0]
    W2[64:128, 64:128] = inp['w_f2'][0]
    m['W2blk'] = W2
    mv = np.zeros((128, 4), f32)
    for jj, nm in enumerate(('freq_1', 'b_f1', 'freq_2', 'b_f2')):
        mv[0:64, jj] = inp[nm][0]
        mv[64:128, jj] = inp[nm][0]
    m['mlpv'] = mv
    w3 = inp['w_f3'][0].reshape(64, 2, 2, 8, 64)
    W3 = np.zeros((128, 8, 2, 2, 64), f32)
    for dd in range(2):
        W3[64 * dd:64 * (dd + 1), :, :, dd, :] = w3[:, :, dd, :, :].transpose(0, 2, 1, 3)
    m['W3blk'] = W3.reshape(128, 2048)
    C = host_constants()
    for nm in CONST_NAMES:
        m[nm] = C[nm]
    return m


IN_SHAPES = {
    'x': ([L, D], F32), 'w_in': ([D, 5280], F32), 'gT': ([128, 8], F32), 'bgT': ([128, 16], F32),
    'w_uq': ([384, 768], F32), 'w_ukv': ([256, 1024], F32), 'gcqT': ([128, 3], F32), 'gckvT': ([128, 2], F32),
    'gqk': ([1, 192], F32), 'w_attn_out': ([512, D], F32), 'w_hy_out': ([512, D], F32), 'w_out': ([D, D], F32),
    'cosT': ([128, NT * 16], F32), 'sinT': ([128, NT * 16], F32),
    'wsh': ([128, 48], F32), 'biasT': ([128, 256], F32), 'W1blk': ([128, 128], F32), 'W2blk': ([128, 128], F32),
    'mlpv': ([128, 4], F32), 'W3blk': ([128, 2048], F32),
    'FA1': ([128, 256], BF), 'FA2': ([128, 256], BF), 'WBr': ([128, 128], BF), 'WBi': ([128, 128], BF),
    'WBni': ([128, 128], BF), 'WI1': ([128, 256], BF), 'WI2': ([128, 256], BF), 'FIr': ([128, 128], BF),
    'FIi': ([128, 128], BF), 'TWa': ([128, 256], F32), 'TWb': ([128, 256], F32), 'TIa': ([128, 256], F32),
    'TIb': ([128, 256], F32), 'zs_hi': ([128, L], BF), 'zs_lo': ([128, L], BF), 'tfull': ([128, 64], F32),
    'negd': ([1, HYW], F32),
}
CONST_NAMES = ('cosT', 'sinT', 'FA1', 'FA2', 'WBr', 'WBi', 'WBni', 'WI1', 'WI2', 'FIr', 'FIi', 'TWa', 'TWb', 'TIa', 'TIb',
               'zs_hi', 'zs_lo', 'tfull', 'negd')


def build_nc(debug=None):
    debug = debug or set()
    nc = bass.Bass("TRN2", target_bir_lowering=False)
    T = {}
    for name, (shape, dt) in IN_SHAPES.items():
        T[name] = nc.dram_tensor(name, shape, dt, kind="ExternalInput").ap()
    T['out'] = nc.dram_tensor("out", [L, D], F32, kind="ExternalOutput").ap()
    skind = dict(kind="ExternalOutput") if 'dump' in debug else {}
    T['hT_d'] = nc.dram_tensor("hT_d", [NB, 128, 8 * 512], BF, **skind).ap()
    T['at_d'] = nc.dram_tensor("at_d", [NB, 64, 8 * 512], BF, **skind).ap()
    T['h2_d'] = nc.dram_tensor("h2_d", [128, L], BF, **skind).ap()
    if 'dump' in debug:
        T['dbg_tm'] = nc.dram_tensor("dbg_tm", [128, 4096], BF, kind="ExternalOutput").ap()
        T['dbg_k'] = nc.dram_tensor("dbg_k", [128, 8192], BF, kind="ExternalOutput").ap()
        T['dbg_ks'] = nc.dram_tensor("dbg_ks", [128, 8192], BF, kind="ExternalOutput").ap()
    if 'yz_in' in debug:
        T['yz_d'] = nc.dram_tensor("yz_d", [NB, 128, 4 * 512], BF, kind="ExternalInput").ap()
    else:
        T['yz_d'] = nc.dram_tensor("yz_d", [NB, 128, 4 * 512], BF, **skind).ap()
    phases = debug & {'p1', 'p2', 'p3', 'p4'} or {'p1', 'p2', 'p3', 'p4'}
    do_p2 = 'p2' in phases and 'yz_in' not in debug
    if 'p1' in phases or do_p2:
        phase12(nc, T, with_p1=('p1' in phases), with_p2a=do_p2)
    if do_p2:
        phase2(nc, T)
    if 'p3' in phases:
        phase3(nc, T)
    if 'p4' in phases:
        phase4(nc, T)
    return nc


def kernel(**inputs):
    inp = {k_: np.asarray(v) for k_, v in inputs.items()}
    nc = build_nc()
    in_maps = [prep_inputs(inp, b) for b in range(8)]
    res = run_bass_kernel_spmd(nc, in_maps, core_ids=list(range(8)))
    out = np.stack([np.asarray(r['out'], dtype=np.float32) for r in res.results], axis=0)
    return out
```

```python
import concourse.bass as bass
import concourse.mybir as mybir

_ESZ = {}


def _esize(dt):
    s = _ESZ.get(dt)
    if s is None:
        n = str(dt)
        if '32' in n:
            s = 4
        elif '16' in n:
            s = 2
        elif '8' in n:
            s = 1
        else:
            s = 4
        _ESZ[dt] = s
    return s


def footprint(ap):
    t = ap.tensor
    name = t.name
    es = _esize(ap.dtype)
    apl = ap.ap
    off = int(ap.offset) * es
    space = str(type(t).__name__)
    if 'DRam' in space:
        lo = off
        hi = off
        for st, cnt in apl:
            if cnt > 1:
                d = (cnt - 1) * st * es
                if d > 0:
                    hi += d
                else:
                    lo += d
        return (name, 0, 1, lo, hi + es)
    pstep, pcnt = apl[0]
    pstep_b = pstep * es
    if pstep_b > 0:
        p0 = off // pstep_b
        f0 = off % pstep_b
    else:
        p0 = 0
        f0 = off
    lo = f0
    hi = f0
    for st, cnt in apl[1:]:
        if cnt > 1:
            d = (cnt - 1) * st * es
            if d > 0:
                hi += d
            else:
                lo += d
    return (name, p0, p0 + pcnt, lo, hi + es)


COMPUTE = ('pe', 'act', 'dve', 'pool')
QUEUES = ('pe', 'act', 'dve', 'pool', 'sp')
QIDX = {q: i for i, q in enumerate(QUEUES)}


class _Op:
    __slots__ = ('q', 'fn', 'dma', 'idx', 'gid', 'waits_c', 'waits_d', 'signal', 'snap', 'slot', 'slot_cnt', 'prev_slot')


class Prog:
    def __init__(self, nc, dma_slots=None):
        self.nc = nc
        self.streams = {q: [] for q in QUEUES}
        self.recs = {}
        self.known = {q: [-1] * len(QUEUES) for q in QUEUES}
        self.known_dma = {q: set() for q in QUEUES}
        self.ops = []
        self.dma_slots = dma_slots or {'sp': 8, 'pool': 4, 'act': 4}
        self.dma_count = {q: 0 for q in QUEUES}
        self.dma_ops = {q: [] for q in QUEUES}
        self.n_comp = {q: 0 for q in QUEUES}

    def add(self, q, fn, reads=(), writes=(), dma=False):
        op = _Op()
        op.q = q
        op.fn = fn
        op.dma = dma
        op.gid = len(self.ops)
        op.signal = dma
        op.waits_c = []
        op.waits_d = []
        op.slot = None
        op.prev_slot = None
        stream = self.streams[q]
        if not dma:
            op.idx = self.n_comp[q]
            self.n_comp[q] += 1
        else:
            op.idx = -1
        deps_c = {}
        deps_d = set()

        def scan(fp, is_write):
            name, p0, p1, f0, f1 = fp
            lst = self.recs.get(name)
            if not lst:
                return
            for r in lst:
                (rp0, rp1, rf0, rf1, rw, rop) = r
                if not (is_write or rw):
                    continue
                if rp1 <= p0 or p1 <= rp0 or rf1 <= f0 or f1 <= rf0:
                    continue
                if rop.dma:
                    deps_d.add(rop)
                else:
                    e = rop.q
                    if deps_c.get(e, -1) < rop.idx:
                        deps_c[e] = rop.idx

        rfps = [footprint(a) for a in reads]
        wfps = [footprint(a) for a in writes]
        for fp in rfps:
            scan(fp, False)
        for fp in wfps:
            scan(fp, True)
        known = self.known[q]
        kd = self.known_dma[q]
        for e, i in deps_c.items():
            ei = QIDX[e]
            if e == q and not dma:
                if q == 'pe':
                    continue
            if i <= known[ei]:
                continue
            op.waits_c.append((e, i))
            src = self.comp_ops[e][i]
            src.signal = True
            known[ei] = i
            for k, v in enumerate(src.snap):
                if v > known[k]:
                    known[k] = v
        for d in sorted(deps_d, key=lambda o: o.gid):
            if d.gid in kd:
                continue
            op.waits_d.append(d)
            kd.add(d.gid)
            for k, v in enumerate(d.snap):
                if v > known[k]:
                    known[k] = v
        if dma:
            n = self.dma_count[q]
            R = self.dma_slots[q]
            op.slot = n % R
            op.slot_cnt = n // R + 1
            if n >= R:
                prev = self.dma_ops[q][n - R]
                op.prev_slot = prev
                kd.add(prev.gid)
            self.dma_count[q] = n + 1
            self.dma_ops[q].append(op)
        op.snap = tuple(known)
        if not dma:
            self.comp_ops[q].append(op)
        for fp, is_write in [(f, False) for f in rfps] + [(f, True) for f in wfps]:
            name, p0, p1, f0, f1 = fp
            lst = self.recs.setdefault(name, [])
            if is_write:
                lst[:] = [r for r in lst if not (r[0] >= p0 and r[1] <= p1 and r[2] >= f0 and r[3] <= f1)]
            else:
                if not dma:
                    lst[:] = [r for r in lst if not (r[4] is False and (not r[5].dma) and r[5].q == q
                                                     and r[0] == p0 and r[1] == p1 and r[2] == f0 and r[3] == f1)]
            lst.append((p0, p1, f0, f1, is_write, op))
        stream.append(op)
        self.ops.append(op)
        return op

    comp_ops = None

    def start(self):
        self.comp_ops = {q: [] for q in QUEUES}

    def pe(self, fn, reads, writes):
        return self.add('pe', fn, reads, writes)

    def act(self, fn, reads, writes):
        return self.add('act', fn, reads, writes)

    def dve(self, fn, reads, writes):
        return self.add('dve', fn, reads, writes)

    def pool(self, fn, reads, writes):
        return self.add('pool', fn, reads, writes)

    def dma(self, out, in_, q='sp', **kw):
        return self.add(q, lambda e: e.dma_start(out=out, in_=in_, **kw), [in_], [out], dma=True)

    def emit(self, block, sems_c, sems_d):
        cum = {}
        for e in QUEUES:
            c = 0
            arr = []
            for o in self.comp_ops[e]:
                if o.signal:
                    c += 1
                arr.append(c)
            cum[e] = arr
        self.cum = cum

        def gen(q):
            def body(eng):
                for o in self.streams[q]:
                    for (e, i) in o.waits_c:
                        eng.wait_ge(sems_c[e], cum[e][i])
                    for d in o.waits_d:
                        eng.wait_ge(sems_d[d.q][d.slot], 16 * d.slot_cnt)
                    if o.prev_slot is not None:
                        p = o.prev_slot
                        eng.wait_ge(sems_d[p.q][p.slot], 16 * p.slot_cnt)
                    ins = o.fn(eng)
                    if o.dma:
                        ins.then_inc(sems_d[q][o.slot], 16)
                    elif o.signal:
                        ins.then_inc(sems_c[q], 1)
                R = self.dma_slots.get(q, 0)
                n = self.dma_count[q]
                for o in self.dma_ops[q][max(0, n - R):]:
                    eng.wait_ge(sems_d[q][o.slot], 16 * o.slot_cnt)
            return body

        if self.streams['pe']:
            block.tensor(gen('pe'))
        if self.streams['act']:
            block.scalar(gen('act'))
        if self.streams['dve']:
            block.vector(gen('dve'))
        if self.streams['pool']:
            block.gpsimd(gen('pool'))
        if self.streams['sp']:
            block.sync(gen('sp'))

import math
from contextlib import ExitStack
import numpy as np
import ml_dtypes
from concourse.bass_utils import run_bass_kernel_spmd

F32 = mybir.dt.float32
BF = mybir.dt.bfloat16
AF = mybir.ActivationFunctionType
ALU = mybir.AluOpType
AX = mybir.AxisListType

L = 4096
D = 1024
NT = 32
NB = 8
EPS = 1e-6
NF = 8192
HYW = 512
COL_V, COL_X1, COL_X2, COL_ZH = 0, 512, 1024, 1536
COL_CQ, COL_CKV, COL_KR, COL_ZA = 2048, 2432, 2688, 2720
COL_GH, COL_GA = 3232, 4256
MAGIC = 12582912.0
DBG = {}


def bcast(ap, axis, n):
    a = ap.unsqueeze(axis)
    shp = list(a.shape)
    shp[axis] = n
    return a.to_broadcast(shp)


class K:
    def __init__(self, P):
        self.P = P

    def mm(self, out, lhsT, rhs, start=True, stop=True):
        self.P.pe(lambda e: e.matmul(out, lhsT=lhsT, rhs=rhs, start=start, stop=stop), [lhsT, rhs], [out])

    def tr(self, out, in_, ident):
        self.P.pe(lambda e: e.transpose(out=out, in_=in_, identity=ident), [in_, ident], [out])

    def act(self, out, in_, func, bias=None, scale=None, accum_out=None):
        kw = {}
        reads = [in_]
        writes = [out]
        if bias is not None:
            kw['bias'] = bias
            if not isinstance(bias, (int, float)):
                reads.append(bias)
        if scale is not None:
            kw['scale'] = scale
            if not isinstance(scale, (int, float)):
                reads.append(scale)
        if accum_out is not None:
            kw['accum_out'] = accum_out
            writes.append(accum_out)
        self.P.act(lambda e: e.activation(out=out, in_=in_, func=func, **kw), reads, writes)

    def tt(self, eng, out, in0, in1, op):
        self.P.add(eng, lambda e: e.tensor_tensor(out=out, in0=in0, in1=in1, op=op), [in0, in1], [out])

    def ts(self, eng, out, in0, s1, s2=None, op0=ALU.mult, op1=None):
        reads = [in0]
        if not isinstance(s1, (int, float)):
            reads.append(s1)
        if s2 is not None and not isinstance(s2, (int, float)):
            reads.append(s2)
        if op1 is None:
            self.P.add(eng, lambda e: e.tensor_scalar(out=out, in0=in0, scalar1=s1, scalar2=None, op0=op0), reads, [out])
        else:
            self.P.add(eng, lambda e: e.tensor_scalar(out=out, in0=in0, scalar1=s1, scalar2=s2, op0=op0, op1=op1), reads, [out])

    def stt(self, out, in0, scalar, in1, op0, op1):
        reads = [in0, in1]
        if not isinstance(scalar, (int, float)):
            reads.append(scalar)
        self.P.dve(lambda e: e.scalar_tensor_tensor(out=out, in0=in0, scalar=scalar, in1=in1, op0=op0, op1=op1), reads, [out])

    def copy(self, eng, out, in_):
        if eng == 'act':
            self.act(out, in_, AF.Copy)
        else:
            self.P.add(eng, lambda e: e.tensor_copy(out=out, in_=in_), [in_], [out])

    def recip(self, out, in_):
        self.P.dve(lambda e: e.reciprocal(out=out, in_=in_), [in_], [out])

    def reduce_add(self, out, in_):
        self.P.dve(lambda e: e.tensor_reduce(out=out, in_=in_, axis=AX.X, op=ALU.add), [in_], [out])

    def memset(self, eng, ap, val):
        self.P.add(eng, lambda e: e.memset(ap, val), [], [ap])

    def dma(self, out, in_, q='sp'):
        self.P.dma(out, in_, q=q)


def _dump(P):
    cum = {}
    for e in QUEUES:
        c = 0
        arr = []
        for o in P.comp_ops[e]:
            if o.signal:
                c += 1
            arr.append(c)
        cum[e] = arr
    for q in QUEUES:
        print("== stream", q)
        for o in P.streams[q]:
            w = [f"{e}>={cum[e][i]}(op{i})" for e, i in o.waits_c] + [f"dma[{d.q}{d.slot}]>={16*d.slot_cnt}" for d in o.waits_d]
            if o.prev_slot is not None:
                w.append(f"prev dma[{o.prev_slot.q}{o.prev_slot.slot}]>={16*o.prev_slot.slot_cnt}")
            tag = f"DMA slot{o.slot} cnt{o.slot_cnt}" if o.dma else (f"op{o.idx} sig={cum[q][o.idx] if o.signal else '-'}")
            print("   ", tag, getattr(o, 'desc', ''), "waits:", w)


class Phase:
    def __init__(self, nc, name):
        self.nc = nc
        self.name = name
        self.es = ExitStack()

    def __enter__(self):
        nc = self.nc
        es = self.es
        es.__enter__()
        self.ps = [es.enter_context(nc.psum_tensor(f"{self.name}_ps{i}", [128, 512], F32)) for i in range(8)]
        self.sems_c = {e: es.enter_context(nc.semaphore(f"{self.name}_sc_{e}")) for e in QUEUES}
        self.sems_d = {q: [es.enter_context(nc.semaphore(f"{self.name}_sd_{q}{i}")) for i in range(n)]
                       for q, n in (('sp', 8), ('pool', 4), ('act', 4))}
        self.P = Prog(nc)
        self.P.start()
        self.k = K(self.P)
        return self

    def sb(self, name, shape, dt):
        return self.es.enter_context(self.nc.sbuf_tensor(f"{self.name}_{name}", shape, dt))

    def __exit__(self, *a):
        if a[0] is None:
            self.es.enter_context(self.nc.allow_low_precision("bf16 operands / intermediates by design"))
            block = self.es.enter_context(self.nc.Block())
            if DBG.get('dump') == self.name:
                _dump(self.P)
            self.P.emit(block, self.sems_c, self.sems_d)
        return self.es.__exit__(*a)


def make_ident(ph, ident):
    identf = ph.sb("identf", [128, 128], F32)
    ph.k.memset('pool', identf[:], 0.0)
    ph.P.pool(lambda e: e.affine_select(out=identf[:], in_=identf[:], pattern=[[-1, 128]], compare_op=ALU.not_equal,
                                        fill=1.0, base=0, channel_multiplier=1), [identf[:]], [identf[:]])
    ph.k.copy('dve', ident[:], identf[:])


def load_w(ph, dst, src_ap):
    ph.k.dma(dst, src_ap, q='pool')


def rstd_from_ss(k, rs_col, ss_col, n):
    k.act(rs_col, ss_col, AF.Sqrt, bias=EPS, scale=1.0 / n)
    k.recip(rs_col, rs_col)


def phase1_gen(ph, T, banks):
    k = ph.k
    xt = [ph.sb(f"xt{i}", [128, D], F32) for i in range(3)]
    xn = [ph.sb(f"xn{i}", [128, D], BF) for i in range(2)]
    junk = ph.sb("junk", [128, D], BF)
    ss = ph.sb("ss", [128, NT], F32)
    rs = ph.sb("rs", [128, NT], F32)
    ts_ = ph.sb("ts_", [128, NT], F32)
    mh = ph.sb("mh", [128, 1], F32)
    gT = ph.sb("gT", [128, 8], F32)
    ident = ph.sb("ident", [128, 128], BF)
    hb = [ph.sb(f"hb{i}", [128, 8, 512], BF) for i in range(2)]
    make_ident(ph, ident)
    k.memset('pool', mh[:], -0.5)
    k.dma(gT[:], T['gT'][:, :])
    pend = [None]
    for i in range(NT):
        x_t = xt[i % 3]
        k.dma(x_t[:], T['x'][128 * i:128 * (i + 1), :])
        k.act(junk[:], x_t[:], AF.Square, accum_out=ss[:, i:i + 1])
        rstd_pool(k, rs[:, i:i + 1], ss[:, i:i + 1], D, mh[:, 0:1], ts_[:, i:i + 1])
        x_n = xn[i % 2]
        k.ts('dve', x_n[:], x_t[:], rs[:, i:i + 1])
        bank = banks[i % len(banks)]
        pv = bank[:, :].bitcast(BF)
        for c in range(8):
            k.tr(pv[:, 128 * c:128 * (c + 1)], x_n[:, 128 * c:128 * (c + 1)], ident[:])
        if pend[0] is not None:
            pend[0]()

        def evac(i=i, pv=pv):
            h_b = hb[(i // 4) % 2]
            j = i % 4
            k.tt('dve', h_b[:, :, 128 * j:128 * (j + 1)], pv.rearrange("p (c t) -> p c t", c=8),
                 bcast(gT[:, :], 2, 128), ALU.mult)
            if j == 3:
                k.dma(T['hT_d'][i // 4], h_b[:].rearrange("p c t -> p (c t)"))
        pend[0] = evac
        yield
    pend[0]()
    yield


def rstd_pool(k, rs, ss, n, mhalf, tmp):
    k.ts('dve', tmp, ss, 1.0 / n, EPS, op0=ALU.mult, op1=ALU.add)
    k.tt('pool', rs, tmp, mhalf, ALU.pow)


def phase12(nc, T, with_p1=True, with_p2a=True):
    with Phase(nc, "p12") as ph:
        g1 = phase1_gen(ph, T, ph.ps[0:4]) if with_p1 else iter(())
        g2 = phase2a_gen(ph, T, ph.ps[4:8]) if with_p2a else iter(())
        alive1, alive2 = True, True
        while alive1 or alive2:
            if alive1:
                try:
                    next(g1)
                except StopIteration:
                    alive1 = False
            for _ in range(4):
                if alive2:
                    try:
                        next(g2)
                    except StopIteration:
                        alive2 = False


def run_interleaved(gens, width=2):
    active = []
    gens = list(gens)
    while gens or active:
        while gens and len(active) < width:
            active.append(gens.pop(0))
        for g in list(active):
            try:
                next(g)
            except StopIteration:
                active.remove(g)


def qk_norm_rope(ph, W, src, dst, g_rep, cs_t, sc_t, mhalf, use_act=False):
    k = ph.k
    sq, ssq, rk, ta, tb_, tmp = W['sq'], W['ssq'], W['rk'], W['ta'], W['tb'], W['tmp']
    if use_act:
        k.act(sq[:].rearrange("p a b -> p (a b)"), src[:].rearrange("p a b -> p (a b)"), AF.Square)
    else:
        k.tt('pool', sq[:], src[:], src[:], ALU.mult)
    k.reduce_add(ssq[:], sq[:])
    rstd_pool(k, rk[:], ssq[:], 96, mhalf[:, 0:8], tmp[:])
    yield
    k.tt('dve', src[:], src[:], bcast(rk[:, :], 2, 96), ALU.mult)
    k.tt('dve', src[:], src[:], bcast(g_rep, 1, 8), ALU.mult)
    yield
    t1 = bcast(src[:, :, 64:80], 1, 2)
    t2 = bcast(src[:, :, 80:96], 1, 2)
    k.tt('pool', ta[:], t1, bcast(cs_t, 2, 8), ALU.mult)
    k.tt('pool', tb_[:], t2, bcast(sc_t, 2, 8), ALU.mult)
    k.tt('dve', dst[:, :, 64:80], ta[:, 0], tb_[:, 0], ALU.subtract)
    k.tt('dve', dst[:, :, 80:96], ta[:, 1], tb_[:, 1], ALU.add)
    k.copy('pool', dst[:, :, 0:64], src[:, :, 0:64])
    yield


def phase3(nc, T):
    with Phase(nc, "p3") as ph:
        k = ph.k
        ps = ph.ps
        ident = ph.sb("ident", [128, 128], BF)
        make_ident(ph, ident)
        w_in_v = T['w_in'].rearrange("(k p) n -> p k n", p=128)
        Wkv = ph.sb("Wkv", [128, 8, 288], BF)
        Wq = ph.sb("Wq", [128, 8, 384], BF)
        Wuq = ph.sb("Wuq", [128, 3, 768], BF)
        Wukv = ph.sb("Wukv", [128, 2, 1024], BF)
        gcq = ph.sb("gcq", [128, 3], F32)
        gckv = ph.sb("gckv", [128, 2], F32)
        gqk = ph.sb("gqk", [128, 192], F32)
        csT = ph.sb("csT", [128, NT, 2, 16], F32)
        mhalf = ph.sb("mhalf", [128, 8], F32)
        k.memset('pool', mhalf[:], -0.5)
        load_w(ph, Wkv[:], w_in_v[:, :, COL_CKV:COL_CKV + 288])
        load_w(ph, Wq[:], w_in_v[:, :, COL_CQ:COL_CQ + 384])
        load_w(ph, Wuq[:], T['w_uq'].rearrange("(k p) n -> p k n", p=128))
        load_w(ph, Wukv[:], T['w_ukv'].rearrange("(k p) n -> p k n", p=128))
        k.dma(gcq[:], T['gcqT'][:, :])
        k.dma(gckv[:], T['gckvT'][:, :])
        k.dma(gqk[:], T['gqk'][0:1, :].partition_broadcast(128))
        cosv = T['cosT'].rearrange("p (a b) -> p a b", b=16)
        sinv = T['sinT'].rearrange("p (a b) -> p a b", b=16)
        k.dma(csT[:, :, 0, :], cosv)
        k.dma(csT[:, :, 1, :], sinv)
        k.tt('pool', Wuq[:], Wuq[:], bcast(gcq[:, :], 2, 768), ALU.mult)
        k.tt('pool', Wukv[:], Wukv[:], bcast(gckv[:, :], 2, 1024), ALU.mult)

        kT = ph.sb("kT", [128, 8, L], BF)
        vx = ph.sb("vx", [128, NT, 8, 65], BF)
        k.memset('pool', vx[:, :, :, 64:65], 1.0)
        ones = ph.sb("ones", [128, 64], BF)
        k.memset('pool', ones[:], 1.0)
        hbuf = [ph.sb(f"hbuf{i}", [128, 8, 512], BF) for i in range(2)]
        junk = [ph.sb(f"junk{i}", [128, 384], BF) for i in range(3)]
        ssl = ph.sb("ssl", [128, 2 * NT], F32)
        rsl = ph.sb("rsl", [128, 2 * NT], F32)
        tsl = ph.sb("tsl", [128, 2 * NT], F32)
        latn = [ph.sb(f"latn{i}", [128, 384], BF) for i in range(3)]
        latT = [ph.sb(f"latT{i}", [128, 3, 128], BF) for i in range(3)]
        kr = [ph.sb(f"kr{i}", [128, 32], F32) for i in range(3)]
        qk32 = [ph.sb(f"qk32_{i}", [128, 8, 96], F32) for i in range(3)]
        qkbf = [ph.sb(f"qkbf{i}", [128, 8, 96], BF) for i in range(3)]
        Wk_ = [dict(sq=ph.sb(f"sq{i}", [128, 8, 96], F32), ssq=ph.sb(f"ssq{i}", [128, 8], F32),
                    rk=ph.sb(f"rk{i}", [128, 8], F32), tmp=ph.sb(f"tmpn{i}", [128, 8], F32),
                    ta=ph.sb(f"ta{i}", [128, 2, 8, 16], F32), tb=ph.sb(f"tb{i}", [128, 2, 8, 16], F32)) for i in range(3)]
        qT = [ph.sb(f"qT{i}", [128, 8, 512], BF) for i in range(2)]
        pt = [ph.sb(f"pt{i}", [128, 512], BF) for i in range(3)]
        rsum = [ph.sb(f"rsum{i}", [128, 512], BF) for i in range(2)]
        bcs = [ph.sb(f"bcs{i}", [64, 512], BF) for i in range(2)]
        at = [ph.sb(f"at{i}", [64, 8, 512], BF) for i in range(1)]

        def sumsq(i, col, src_ps, n):
            k.act(junk[i % 3][:, 0:n], src_ps, AF.Square, accum_out=ssl[:, col:col + 1])
            rstd_pool(k, rsl[:, col:col + 1], ssl[:, col:col + 1], n, mhalf[:, 0:1], tsl[:, col:col + 1])

        def kv_tile(i, hb, j):
            par = i % 3
            if j == 0:
                k.dma(hb[:].rearrange("p c t -> p (c t)"), T['hT_d'][i // 4])
            lat = ps[par][:, 0:288]
            for c in range(8):
                k.mm(lat, hb[:, c, 128 * j:128 * (j + 1)], Wkv[:, c, :], start=(c == 0), stop=(c == 7))
            sumsq(i, i, lat[:, 0:256], 256)
            yield
            ln = latn[par]
            k.ts('dve', ln[:, 0:256], lat[:, 0:256], rsl[:, i:i + 1])
            k.copy('dve', kr[par][:], lat[:, 256:288])
            tbank = ps[7][:, :].bitcast(BF)
            for c in range(2):
                k.tr(tbank[:, 128 * c:128 * (c + 1)], ln[:, 128 * c:128 * (c + 1)], ident[:])
            lT = latT[par]
            k.copy('dve', lT[:, 0:2, :].rearrange("p c t -> p (c t)"), tbank[:, 0:256])
            yield
            kvb = [ps[3 + 2 * (i % 2)], ps[4 + 2 * (i % 2)]]
            for half in range(2):
                for c in range(2):
                    k.mm(kvb[half][:, :], lT[:, c, :], Wukv[:, c, 512 * half:512 * (half + 1)],
                         start=(c == 0), stop=(c == 1))
            kk = qk32[par]
            for half in range(2):
                kvv = kvb[half][:, :].rearrange("p (h e) -> p h e", h=4)
                k.copy('dve', kk[:, 4 * half:4 * half + 4, 0:64], kvv[:, :, 0:64])
                k.copy('dve', vx[:, i, 4 * half:4 * half + 4, 0:64], kvv[:, :, 64:128])
            k.copy('pool', kk[:, :, 64:96], bcast(kr[par][:, :], 1, 8))
            yield
            kf = qkbf[par]
            yield from qk_norm_rope(ph, Wk_[par], kk, kf, gqk[:, 96:192], csT[:, i, :, :], csT[:, i, ::-1, :], mhalf, use_act=True)
            kbank = ps[7][:, :].bitcast(BF)
            for h in range(8):
                k.tr(kbank[0:96, 128 * h:128 * (h + 1)], kf[:, h, :], ident[:])
            k.copy('dve', kT[0:96, :, 128 * i:128 * (i + 1)], kbank[0:96, :].rearrange("p (h t) -> p h t", h=8))
            yield

        def q_tile(i, hb, j, q_T):
            par = i % 2
            lat = ps[6][:, 0:384]
            for c in range(8):
                k.mm(lat, hb[:, c, 128 * j:128 * (j + 1)], Wq[:, c, :], start=(c == 0), stop=(c == 7))
            yield
            lsb = Wk_[par]['sq'][:].rearrange("p a b -> p (a b)")[:, 0:384]
            k.copy('dve', lsb, lat)
            k.P.dve(lambda e: e.scalar_tensor_tensor(out=junk[par][:, 0:384], in0=lsb, scalar=1.0, in1=lsb, op0=ALU.mult,
                                                     op1=ALU.mult, accum_out=ssl[:, NT + i:NT + i + 1]),
                    [lsb], [junk[par][:, 0:384], ssl[:, NT + i:NT + i + 1]])
            rstd_pool(k, rsl[:, NT + i:NT + i + 1], ssl[:, NT + i:NT + i + 1], 384, mhalf[:, 0:1], tsl[:, NT + i:NT + i + 1])
            yield
            ln = latn[par]
            k.ts('dve', ln[:, 0:384], lsb, rsl[:, NT + i:NT + i + 1])
            yield
            tbank = ps[7][:, :].bitcast(BF)
            for c in range(3):
                k.tr(tbank[:, 128 * c:128 * (c + 1)], ln[:, 128 * c:128 * (c + 1)], ident[:])
            yield
            lT = latT[par]
            k.copy('dve', lT[:].rearrange("p c t -> p (c t)"), tbank[:, 0:384])
            yield
            qq = qk32[par]
            for half in range(2):
                qb = ps[6][:, 0:384]
                for c in range(3):
                    k.mm(qb, lT[:, c, :], Wuq[:, c, 384 * half:384 * (half + 1)], start=(c == 0), stop=(c == 2))
                yield
                k.copy('dve', qq[:, 4 * half:4 * half + 4, :], qb.rearrange("p (h e) -> p h e", h=4))
                yield
            qf = qkbf[par]
            yield from qk_norm_rope(ph, Wk_[par], qq, qf, gqk[:, 0:96], csT[:, i, :, :], csT[:, i, ::-1, :], mhalf, use_act=(i < 4))
            yield
            yield
            yield
            yield
            qbank = ps[7][:, :].bitcast(BF)
            for h in range(8):
                k.tr(qbank[0:96, 128 * h:128 * (h + 1)], qf[:, h, :], ident[:])
            yield
            k.copy('dve', q_T[0:96, :, 128 * j:128 * (j + 1)], qbank[0:96, :].rearrange("p (h t) -> p h t", h=8))
            yield

        def kv_block(tb):
            hb = hbuf[tb % 2]
            return [kv_tile(tb * 4 + j, hb, j) for j in range(4)]

        def q_chunk_gens(qc):
            hb = hbuf[qc % 2]
            k.dma(hb[:].rearrange("p c t -> p (c t)"), T['hT_d'][qc])
            return [q_tile(qc * 4 + j, hb, j, qT[qc % 2]) for j in range(4)]

        gens = []
        for tb in range(DBG.get('nprep', NB)):
            gens += kv_block(tb)
        run_interleaved(gens, 3)
        nqc = DBG.get('nqc', NB)
        if nqc:
            run_interleaved(q_chunk_gens(0), 1)

        scale = 1.0 / math.sqrt(96.0)
        NH = DBG.get('nh', 8)
        pend = [None]
        for qc in range(nqc):
            q_T = qT[qc % 2]
            a_t = at[0]
            nxt = q_chunk_gens(qc + 1) if qc + 1 < nqc else []
            nxt_active = []
            steps = [(h, kt) for h in range(NH) for kt in range(NT)]

            def S(idx):
                h, kt = steps[idx]
                k.mm(ps[idx % 3][:, :], kT[0:96, h, 128 * kt:128 * (kt + 1)], q_T[0:96, h, :])

            def fin_a(h):
                k.recip(rsum[h % 2][64:65, :], ps[3 + (h % 2)][64:65, :])

            def fin_b(h):
                k.mm(ps[5][0:64, :], ones[64:65, :], rsum[h % 2][64:65, :])

            def fin_c(h, a_t):
                k.copy('dve', bcs[h % 2][:], ps[5][0:64, :])
                k.tt('dve', a_t[:, h, :], ps[3 + (h % 2)][0:64, :], bcs[h % 2][:], ALU.mult)

            S(0)
            S(1)
            for idx, (h, kt) in enumerate(steps):
                p_t = pt[idx % 3]
                k.act(p_t[:], ps[idx % 3][:, :], AF.Exp, scale=scale)
                if idx + 2 < len(steps):
                    S(idx + 2)
                ob = ps[3 + (h % 2)]
                k.mm(ob[0:65, :], vx[:, kt, h, :], p_t[:], start=(kt == 0), stop=(kt == NT - 1))
                if pend[0] is not None:
                    if kt == 1:
                        pend[0][0]()
                    elif kt == 10:
                        pend[0][1]()
                    elif kt == 14:
                        pend[0][2]()
                        pend[0] = None
                if kt == NT - 1:
                    last = (h == NH - 1)
                    pend[0] = (lambda h=h: fin_a(h), lambda h=h: fin_b(h),
                               (lambda h=h, a_t=a_t, qc=qc, last=last, fin_c=fin_c: (fin_c(h, a_t), k.dma(T['at_d'][qc], a_t[:].rearrange("p h t -> p (h t)")) if last else None)))
                if idx % 3 == 2:
                    while nxt and len(nxt_active) < 1:
                        nxt_active.append(nxt.pop(0))
                    for g in list(nxt_active):
                        try:
                            next(g)
                        except StopIteration:
                            nxt_active.remove(g)
            run_interleaved(nxt_active + nxt, 1)

        if pend[0] is not None:
            pend[0][0]()
            pend[0][1]()
            pend[0][2]()


def phase4(nc, T):
    with Phase(nc, "p4") as ph:
        k = ph.k
        ps = ph.ps
        w_in_v = T['w_in'].rearrange("(k p) n -> p k n", p=128)
        Wz = ph.sb("Wz", [128, 8, 512], BF)
        Wg = ph.sb("Wg", [128, 8, 2048], BF)
        Wao = ph.sb("Wao", [128, 4, D], BF)
        Who = ph.sb("Who", [128, 4, D], BF)
        Wout = ph.sb("Wout", [128, 8, D], BF)
        bg = ph.sb("bg", [128, 16], F32)
        load_w(ph, Wz[:], w_in_v[:, :, COL_ZA:COL_ZA + 512])
        for q4 in range(4):
            load_w(ph, Wg[:, :, 512 * q4:512 * (q4 + 1)], w_in_v[:, :, COL_GH + 512 * q4:COL_GH + 512 * (q4 + 1)])
        load_w(ph, Wao[:], T['w_attn_out'].rearrange("(hp p) n -> p hp n", p=128))
        load_w(ph, Who[:], T['w_hy_out'].rearrange("(k p) n -> p k n", p=128))
        load_w(ph, Wout[:], T['w_out'].rearrange("(k p) n -> p k n", p=128))
        k.dma(bg[:], T['bgT'][:, :])
        hbuf = [ph.sb(f"hbuf{i}", [128, 8, 512], BF) for i in range(2)]
        atb = [ph.sb(f"atb{i}", [128, 4, 512], BF) for i in range(2)]
        yzb = [ph.sb(f"yzb{i}", [128, 4, 512], BF) for i in range(2)]
        xt = [ph.sb(f"xt{i}", [128, D], F32) for i in range(3)]
        ot = [ph.sb(f"ot{i}", [128, D], F32) for i in range(2)]
        sz = [ph.sb(f"sz{i}", [128, 512], BF) for i in range(2)]
        ya = ph.sb("ya", [128, 4, 512], BF)
        gh = [ph.sb(f"gh{i}", [128, 512], BF) for i in range(2)]
        ga = [ph.sb(f"ga{i}", [128, 512], BF) for i in range(2)]
        m1 = [ph.sb(f"m1{i}", [128, 512], F32) for i in range(2)]
        m2 = [ph.sb(f"m2{i}", [128, 512], F32) for i in range(2)]
        mg = [ph.sb(f"mg{i}", [128, 8, 512], BF) for i in range(2)]
        for tb in range(NB):
            hb = hbuf[tb % 2]
            a_b = atb[tb % 2]
            y_b = yzb[tb % 2]
            k.dma(hb[:].rearrange("p c t -> p (c t)"), T['hT_d'][tb])
            atv = T['at_d'][tb].rearrange("p (hp two t) -> p two hp t", two=2, t=512)
            k.dma(a_b[0:64, :, :], atv[:, 0, :, :])
            k.dma(a_b[64:128, :, :], atv[:, 1, :, :])
            k.dma(y_b[:].rearrange("p c t -> p (c t)"), T['yz_d'][tb])
            for hp in range(4):
                zb = ps[hp % 2][:, :]
                for c in range(8):
                    k.mm(zb, Wz[:, c, 128 * hp:128 * (hp + 1)], hb[:, c, :], start=(c == 0), stop=(c == 7))
                s_z = sz[hp % 2]
                k.act(s_z[:], zb, AF.Silu)
                k.tt('pool', ya[:, hp, :], a_b[:, hp, :], s_z[:], ALU.mult)
            m_g = mg[tb % 2]
            for dc in range(8):
                g1 = ps[2 + (dc % 2)]
                g2 = ps[4 + (dc % 2)]
                for c in range(8):
                    k.mm(g1[:, :], Wg[:, c, 128 * dc:128 * (dc + 1)], hb[:, c, :], start=(c == 0), stop=(c == 7))
                for c in range(8):
                    k.mm(g2[:, :], Wg[:, c, 1024 + 128 * dc:1024 + 128 * (dc + 1)], hb[:, c, :],
                         start=(c == 0), stop=(c == 7))
                k.act(gh[dc % 2][:], g1[:, :], AF.Sigmoid, bias=bg[:, dc:dc + 1])
                k.act(ga[dc % 2][:], g2[:, :], AF.Sigmoid, bias=bg[:, 8 + dc:9 + dc])
                uh = ps[6]
                ua = ps[7]
                for c in range(4):
                    k.mm(uh[:, :], Who[:, c, 128 * dc:128 * (dc + 1)], y_b[:, c, :], start=(c == 0), stop=(c == 3))
                for hp in range(4):
                    k.mm(ua[:, :], Wao[:, hp, 128 * dc:128 * (dc + 1)], ya[:, hp, :], start=(hp == 0), stop=(hp == 3))
                k.tt('dve', m1[dc % 2][:], uh[:, :], gh[dc % 2][:], ALU.mult)
                k.tt('dve', m2[dc % 2][:], ua[:, :], ga[dc % 2][:], ALU.mult)
                k.tt('pool', m_g[:, dc, :], m1[dc % 2][:], m2[dc % 2][:], ALU.add)
            for j in range(4):
                i = tb * 4 + j
                x_t = xt[i % 3]
                k.dma(x_t[:], T['x'][128 * i:128 * (i + 1), :])
                o_t = ot[i % 2]
                for half in range(2):
                    fb = ps[half]
                    for c in range(8):
                        k.mm(fb[:, :], m_g[:, c, 128 * j:128 * (j + 1)], Wout[:, c, 512 * half:512 * (half + 1)],
                             start=(c == 0), stop=(c == 7))
                    k.tt('dve', o_t[:, 512 * half:512 * (half + 1)], fb[:, :], x_t[:, 512 * half:512 * (half + 1)], ALU.add)
                k.dma(T['out'][128 * i:128 * (i + 1), :], o_t[:])


def fft_constants():
    C = {}
    n = NF
    s2 = np.arange(128, dtype=np.float64)[:, None]
    f2 = np.arange(128, dtype=np.float64)[None, :]
    th = 2 * np.pi * (f2 + 0.5) * s2 / 256.0
    C['FA1'] = np.concatenate([np.cos(th), -np.sin(th)], 1)
    th2 = 2 * np.pi * (f2 + 0.5) * (s2 + 128) / 256.0
    C['FA2'] = -np.concatenate([np.cos(th2), -np.sin(th2)], 1)
    s1 = np.arange(32, dtype=np.float64)
    tw = np.exp(-2j * np.pi * (np.arange(128)[None, :] + 0.5) * s1[:, None] / n)
    twq = np.tile(tw, (4, 1))
    C['TWa'] = np.concatenate([twq.real, twq.real], 1)
    C['TWb'] = np.concatenate([-twq.imag, twq.imag], 1)
    W = np.exp(-2j * np.pi * np.outer(s1, s1) / 32.0)
    Wq = np.kron(np.eye(4), W)
    C['WBr'] = Wq.real
    C['WBi'] = Wq.imag
    C['WBni'] = -Wq.imag
    Wi = np.exp(2j * np.pi * np.outer(s1, s1) / 32.0)
    Wiq = np.kron(np.eye(4), Wi)
    C['WI1'] = np.concatenate([Wiq.real, Wiq.imag], 1)
    C['WI2'] = np.concatenate([-Wiq.imag, Wiq.real], 1)
    twi = np.exp(2j * np.pi * (np.arange(128)[:, None] + 0.5) * s1[None, :] / n)
    twiq = np.tile(twi, (1, 4))
    C['TIa'] = np.concatenate([twiq.real, twiq.real], 1)
    C['TIb'] = np.concatenate([-twiq.imag, twiq.imag], 1)
    t2 = np.arange(128, dtype=np.float64)[None, :]
    f2c = np.arange(128, dtype=np.float64)[:, None]
    th3 = 2 * np.pi * (f2c + 0.5) * t2 / 256.0
    C['FIr'] = (2.0 / n) * np.cos(th3)
    C['FIi'] = -(2.0 / n) * np.sin(th3)
    return C


def filter_constants():
    C = {}
    f32 = np.float32
    t = np.linspace(0.0, 1.0, L, dtype=f32)[:, None]
    bands = 16
    f = np.linspace(1e-4, bands - 1, bands, dtype=f32)
    ang = (f32(2.0 * np.pi / L) * np.arange(L, dtype=f32)[:, None] * f[None, :]).astype(f32)
    z = np.concatenate([t, np.cos(ang).astype(f32), -np.sin(ang).astype(f32)], axis=-1).astype(f32)
    zs = np.zeros((128, L), f32)
    zs[0:33, :] = z.T
    zs[64:97, :] = z[::-1].T
    hi = zs.astype(ml_dtypes.bfloat16)
    lo = (zs - hi.astype(f32)).astype(ml_dtypes.bfloat16)
    C['zs_hi'] = hi
    C['zs_lo'] = lo
    tl = t[:, 0]
    tf = np.zeros((128, 2, 32), f32)
    pidx = np.arange(128)[:, None] * 32 + np.arange(32)[None, :]
    tf[:, 0, :] = tl[pidx]
    tf[:, 1, :] = tl[4095 - pidx]
    C['tfull'] = tf.reshape(128, 64)
    MIN_DECAY = math.log(1e-2) / 1.5
    MAX_DECAY = math.log(1e-2) / 0.3
    deltas = np.abs(np.linspace(MIN_DECAY, MAX_DECAY, HYW, dtype=f32)).astype(f32)
    C['negd'] = (-deltas)[None, :].astype(f32)
    return C


def _sin_layer(ph, W, pre_ps, fr, fb, out32):
    k = ph.k
    a, kk = W['a'], W['kk']
    k.ts('dve', a[:], pre_ps, fr, fb, op0=ALU.mult, op1=ALU.add)
    yield
    k.ts('dve', kk[:], a[:], 1.0 / (2 * math.pi), MAGIC, op0=ALU.mult, op1=ALU.add)
    yield
    k.ts('dve', kk[:], kk[:], -MAGIC, None, op0=ALU.add)
    yield
    k.stt(a[:], kk[:], -2 * math.pi, a[:], ALU.mult, ALU.add)
    yield
    k.ts('dve', a[:], a[:], -3.14159, 3.14159, op0=ALU.max, op1=ALU.min)
    yield
    k.act(out32, a[:], AF.Sin)
    yield


def _hilo(ph, hi, lo, src32, tmp32):
    k = ph.k
    k.copy('dve', hi, src32)
    k.copy('pool', tmp32, hi)
    k.tt('pool', lo, src32, tmp32, ALU.subtract)


def phase2a_gen(ph, T, banks):
    k = ph.k
    zs_hi = ph.sb("zs_hi", [128, L], BF)
    zs_lo = ph.sb("zs_lo", [128, L], BF)
    W1 = ph.sb("W1", [128, 128], F32)
    W2 = ph.sb("W2", [128, 128], F32)
    W1h = ph.sb("W1h", [128, 128], BF)
    W1l = ph.sb("W1l", [128, 128], BF)
    W2h = ph.sb("W2h", [128, 128], BF)
    W2l = ph.sb("W2l", [128, 128], BF)
    wt = ph.sb("wt", [128, 128], F32)
    mv = ph.sb("mv", [128, 4], F32)
    fb = ph.sb("fb", [128, 2], F32)
    S = [dict(a=ph.sb(f"a{i}", [128, 512], F32), kk=ph.sb(f"kk{i}", [128, 512], F32), h1=ph.sb(f"h1_{i}", [128, 512], F32),
              h1h=ph.sb(f"h1h{i}", [128, 512], BF), h1l=ph.sb(f"h1l{i}", [128, 512], BF), t32=ph.sb(f"t32_{i}", [128, 512], F32),
              h2=ph.sb(f"h2_{i}", [128, 512], F32), h2b=ph.sb(f"h2b{i}", [128, 512], BF)) for i in range(2)]
    k.dma(zs_hi[:], T['zs_hi'][:, :])
    k.dma(zs_lo[:], T['zs_lo'][:, :])
    k.dma(W1[:], T['W1blk'][:, :])
    k.dma(W2[:], T['W2blk'][:, :])
    k.dma(mv[:], T['mlpv'][:, :])
    _hilo(ph, W1h[:], W1l[:], W1[:], wt[:])
    _hilo(ph, W2h[:], W2l[:], W2[:], wt[:])
    k.tt('dve', fb[:, 0:1], mv[:, 0:1], mv[:, 1:2], ALU.mult)
    k.tt('dve', fb[:, 1:2], mv[:, 2:3], mv[:, 3:4], ALU.mult)
    yield

    def chunk(cch):
        s_ = S[cch % 2]
        sl = slice(512 * cch, 512 * (cch + 1))
        b1 = banks[cch % 2]
        k.mm(b1[:, :], W1h[:], zs_hi[:, sl], start=True, stop=False)
        k.mm(b1[:, :], W1h[:], zs_lo[:, sl], start=False, stop=False)
        k.mm(b1[:, :], W1l[:], zs_hi[:, sl], start=False, stop=True)
        yield
        yield from _sin_layer(ph, s_, b1[:, :], mv[:, 0:1], fb[:, 0:1], s_['h1'][:])
        _hilo(ph, s_['h1h'][:], s_['h1l'][:], s_['h1'][:], s_['t32'][:])
        yield
        b2 = banks[2 + cch % 2]
        k.mm(b2[:, :], W2h[:], s_['h1h'][:], start=True, stop=False)
        k.mm(b2[:, :], W2h[:], s_['h1l'][:], start=False, stop=False)
        k.mm(b2[:, :], W2l[:], s_['h1h'][:], start=False, stop=True)
        yield
        yield from _sin_layer(ph, s_, b2[:, :], mv[:, 2:3], fb[:, 1:2], s_['h2'][:])
        k.copy('pool', s_['h2b'][:], s_['h2'][:])
        k.dma(T['h2_d'][:, sl], s_['h2b'][:])
        yield

    gens = [chunk(c) for c in range(NB)]
    active = []
    while gens or active:
        while gens and len(active) < 2:
            active.append(gens.pop(0))
        for g in list(active):
            try:
                next(g)
            except StopIteration:
                active.remove(g)
        yield


def _cmul_tab(ph, W, src, Ta, Tb, out_bf):
    k = ph.k
    P1, P2 = W
    sw = src.rearrange("p (r f) -> p r f", r=2)[:, ::-1, :]
    k.tt('dve', P1[:], src, Ta, ALU.mult)
    k.tt('dve', P2[:].rearrange("p (r f) -> p r f", r=2), sw, Tb.rearrange("p (r f) -> p r f", r=2), ALU.mult)
    k.tt('pool', out_bf, P1[:], P2[:], ALU.add)


def phase2b(nc, T):
    with Phase(nc, "p2b") as ph:
        k = ph.k
        ps = ph.ps
        ident = ph.sb("ident", [128, 128], BF)
        make_ident(ph, ident)
        w_in_v = T['w_in'].rearrange("(k p) n -> p k n", p=128)
        cols = (COL_V, COL_X1, COL_X2, COL_ZH)
        ar = ph.sb("arena", [128, 24592], BF)
        Wblk = ar[:, 20486:24582].rearrange("p (k w c) -> p k w c", k=8, w=4)
        for w in range(4):
            load_w(ph, Wblk[:, :, w, :], w_in_v[:, :, cols[w]:cols[w] + 128])
        cb16 = {}
        for nm, w in (('FA1', 256), ('FA2', 256), ('WBr', 128), ('WBi', 128), ('WBni', 128), ('WI1', 256), ('WI2', 256),
                      ('FIr', 128), ('FIi', 128)):
            cb16[nm] = ph.sb(nm, [128, w], BF)
            k.dma(cb16[nm][:], T[nm][:, :])
        c32 = {}
        for nm in ('TWa', 'TWb', 'TIa', 'TIb'):
            c32[nm] = ph.sb(nm, [128, 256], F32)
            k.dma(c32[nm][:], T[nm][:, :])
        h2s = ph.sb("h2s", [128, L], BF)
        k.dma(h2s[:], T['h2_d'][:, :])
        h2p = ph.sb("h2p", [128, 32, 128], BF)
        k.copy('pool', h2p[:], h2s[:].rearrange("q (p s) -> q s p", s=32))
        W3 = ph.sb("W3", [128, 2048], BF)
        load_w(ph, W3[:], T['W3blk'][:, :])
        wsh = ph.sb("wsh", [128, 12, 4], F32)
        k.dma(wsh[:].rearrange("p a b -> p (a b)"), T['wsh'][:, :])
        biasT = ph.sb("biasT", [128, 2, 128], F32)
        k.dma(biasT[:].rearrange("p a b -> p (a b)"), T['biasT'][:, :])
        negd = ph.sb("negd", [128, HYW], F32)
        k.dma(negd[:], T['negd'][0:1, :].partition_broadcast(128))
        tfull = ph.sb("tfull", [128, 2, 32], F32)
        k.dma(tfull[:].rearrange("p a b -> p (a b)"), T['tfull'][:, :])

        hbuf = [ph.sb(f"hbuf{i}", [128, 8, 512], BF) for i in range(2)]
        raw = [ar[:, 4098 * i:4098 * (i + 1)] for i in range(3)]
        ub_ = [ar[:, 12294 + 4096 * i:12294 + 4096 * (i + 1)] for i in range(2)]
        k_tm = ar[:, 0:8192].rearrange("p (o d c s) -> p o d c s", o=2, d=2, c=64)
        AB = ar[:, 8192:12288].rearrange("p (d c s) -> p d c s", d=2, c=64)
        Ksp = ar[:, 12288:20480].rearrange("p (o g r f) -> p o g r f", o=2, g=16, r=2)
        Gbuf = ar[:, 20480:24576].rearrange("p (r c s) -> p r c s", r=2, c=64)
        sz = ph.sb("sz", [128, L], BF)
        tm = [ph.sb(f"tm{i}", [128, 128, 32], BF) for i in range(3)]
        z2_tm = ph.sb("z2_tm", [128, 128, 32], BF)
        y_sc = ph.sb("y_sc", [128, 32, 128], BF)
        yzb = ph.sb("yzb", [128, L], BF)
        arg32 = ph.sb("arg32", [128, 4096], F32)
        PW = [(ph.sb(f"P1_{i}", [128, 256], F32), ph.sb(f"P2_{i}", [128, 256], F32)) for i in range(2)]
        Zp = [ph.sb(f"Zp{i}", [128, 256], BF) for i in range(2)]
        Yb = [ph.sb(f"Yb{i}", [128, 256], BF) for i in range(2)]
        Kev = [ph.sb(f"Kev{i}", [128, 256], BF) for i in range(2)]
        Esb = [[ph.sb(f"E{st}_{i}", [128, 256], BF) for i in range(2)] for st in range(3)]
        cnt = [0]

        ZA = [ph.sb(f"ZA{i}", [128, 512], BF) for i in range(2)]
        ZB = [ph.sb(f"ZB{i}", [128, 512], BF) for i in range(2)]
        YA = [ph.sb(f"YA{i}", [128, 512], BF) for i in range(2)]
        YB = [ph.sb(f"YB{i}", [128, 512], BF) for i in range(2)]
        G2 = ph.sb("G2", [128, 2, 64, 32], BF)
        WI1n = ph.sb("WI1n", [128, 256], BF)
        k.ts('pool', WI1n[:], cb16['WI1'][:], -1.0, None, op0=ALU.mult)

        def v4(ap):
            return ap.rearrange("p (u r f) -> p u r f", u=2, r=2)

        def tab4(t):
            return bcast(t.rearrange("p (r f) -> p r f", r=2), 1, 2)

        def run_skewed(items, hook=None):
            n = len(items)
            depth = max(len(it) for it in items)
            for t in range(n + depth - 1):
                for s_ in reversed(range(depth)):
                    i = t - s_
                    if 0 <= i < n and s_ < len(items[i]):
                        items[i][s_](i)
                if hook is not None:
                    hook(t)

        def cmul_pair(bank, Ta, Tb, outA, outB):
            k.tt('dve', outA, v4(bank), tab4(Ta), ALU.mult)
            k.tt('dve', outB, v4(bank)[:, :, ::-1, :], tab4(Tb), ALU.mult)

        def st_za(lhs_of, q):
            def f(i):
                bank = ps[i % 2]
                for u in range(2):
                    l1, l2 = lhs_of(2 * q + u)
                    za = bank[:, 256 * u:256 * (u + 1)]
                    k.mm(za, l1, cb16['FA1'][:], start=True, stop=(l2 is None))
                    if l2 is not None:
                        k.mm(za, l2, cb16['FA2'][:], start=False, stop=True)
            return f

        def st_tw(i):
            cmul_pair(ps[i % 2][:, :], c32['TWa'][:], c32['TWb'][:], v4(ZA[i % 2][:]), v4(ZB[i % 2][:]))

        def st_ub(i):
            bank = ps[2 + i % 2]
            for u in range(2):
                ub = bank[:, 256 * u:256 * (u + 1)]
                for n_, z_p in enumerate((ZA[i % 2], ZB[i % 2])):
                    zr = z_p[:, 256 * u:256 * u + 128]
                    zi = z_p[:, 256 * u + 128:256 * (u + 1)]
                    k.mm(ub[:, 0:128], cb16['WBr'][:], zr, start=(n_ == 0), stop=False)
                    k.mm(ub[:, 0:128], cb16['WBni'][:], zi, start=False, stop=(n_ == 1))
                for n_, z_p in enumerate((ZA[i % 2], ZB[i % 2])):
                    zr = z_p[:, 256 * u:256 * u + 128]
                    zi = z_p[:, 256 * u + 128:256 * (u + 1)]
                    k.mm(ub[:, 128:256], cb16['WBi'][:], zr, start=(n_ == 0), stop=False)
                    k.mm(ub[:, 128:256], cb16['WBr'][:], zi, start=False, stop=(n_ == 1))

        def filt_gen(cb, hbk, bank_fixed):
            gcol = 128 * cb + 64 * hbk
            k.tt('dve', arg32[:].rearrange("p (d c s) -> p d c s", d=2, c=64),
                 bcast(bcast(negd[:, gcol:gcol + 64], 1, 2), 3, 32),
                 bcast(tfull[:, :, :], 2, 64), ALU.mult)
            yield
            k.act(AB.rearrange("p d c s -> p (d c s)"), arg32[:], AF.Exp)
            yield
            wc0 = 256 * (2 * cb + hbk)
            for s1 in range(32):
                kb_ = (bank_fixed if bank_fixed is not None else ps[6 + s1 % 2])[:, 0:256]
                k.mm(kb_, h2p[:, s1, :], W3[:, wc0:wc0 + 256])
                abv = AB[:, :, :, s1].rearrange("p d c -> p (d c)")
                k.tt('dve', k_tm[:, :, :, :, s1].rearrange("p o d c -> p o (d c)"),
                     kb_.rearrange("p (o x) -> p o x", o=2), bcast(abv, 1, 2), ALU.mult)
                yield

        pref = [None]
        for cb in range(DBG.get('ncb', 4)):
            for w in range(4):
                if cb > 0:
                    load_w(ph, Wblk[:, :, w, :], w_in_v[:, :, cols[w] + 128 * cb:cols[w] + 128 * (cb + 1)])
            for w in range(3):
                k.memset('pool', raw[w][:, 0:1], 0.0)
                k.memset('pool', raw[w][:, 4097:4098], 0.0)
            for tb in range(NB):
                hb = hbuf[tb % 2]
                k.dma(hb[:].rearrange("p c t -> p (c t)"), T['hT_d'][tb])
                for w in range(4):
                    bank = ps[(tb * 4 + w) % 2]
                    for c in range(8):
                        k.mm(bank[:, :], Wblk[:, c, w, :], hb[:, c, :], start=(c == 0), stop=(c == 7))
                    if w < 3:
                        k.act(raw[w][:, 1 + 512 * tb:1 + 512 * (tb + 1)], bank[:, :], AF.Copy)
                    else:
                        k.act(sz[:, 512 * tb:512 * (tb + 1)], bank[:, :], AF.Silu)
            if DBG.get('s2b', 9) < 2: continue
            for w in range(3):
                u = ub_[w % 2]
                j = 4 * w + cb
                k.ts('dve', u, raw[w][:, 1:4097], wsh[:, j, 1:2], wsh[:, j, 3:4], op0=ALU.mult, op1=ALU.add)
                k.stt(u, raw[w][:, 0:4096], wsh[:, j, 0:1], u, ALU.mult, ALU.add)
                k.stt(u, raw[w][:, 2:4098], wsh[:, j, 2:3], u, ALU.mult, ALU.add)
                for a in range(4):
                    pv = ps[2 + a % 2][:, :].bitcast(BF)
                    for e in range(8):
                        s1 = 8 * a + e
                        k.tr(pv[:, 128 * e:128 * (e + 1)], u[:, s1:4096:32], ident[:])
                    k.copy('dve', tm[w][:, :, 8 * a:8 * a + 8], pv.rearrange("p (s c) -> p c s", s=8))
            if DBG.get('dump_tm'):
                k.dma(T['dbg_tm'][:, :], tm[DBG['dump_tm'] - 1][:].rearrange("p c s -> p (c s)"))
            if DBG.get('s2b', 9) < 3: continue
            for hbk in range(DBG.get('nhbk', 2)):
                c0 = 64 * hbk
                gcol = 128 * cb + c0
                if hbk == 0 or not DBG.get('pref', 1):
                    for _ in filt_gen(cb, hbk, None):
                        pass
                else:
                    for _ in pref[0]:
                        pass
                if DBG.get('s2b', 9) < 4: continue
                def spec_lhs(gi):
                    o, g = divmod(gi, 16)
                    return (k_tm[:, o, 0, 4 * g:4 * g + 4, :].rearrange("p c s -> p (c s)"),
                            k_tm[:, o, 1, 4 * g:4 * g + 4, :].rearrange("p c s -> p (c s)"))

                def st_kev(q):
                    def f(i):
                        o, gp = divmod(q, 8)
                        k.copy('act', Ksp[:, o, 2 * gp:2 * gp + 2, :, :].rearrange("p g r f -> p (g r f)"), ps[2 + i % 2][:, :])
                        if gp == 7:
                            gg0 = gcol // 4
                            k.tt('pool', Ksp[:, o, :, 0, :], Ksp[:, o, :, 0, :], bcast(biasT[:, o, gg0:gg0 + 16], 2, 128), ALU.add)
                    return f

                def conv_lhs_of(src):
                    def f(g):
                        return (src[:, c0 + 4 * g:c0 + 4 * g + 4, :].rearrange("p c s -> p (c s)"), None)
                    return f

                def st_mul(o, q):
                    def f(i):
                        bank = ps[2 + i % 2][:, :]
                        kr_ = bcast(Ksp[:, o, 2 * q:2 * q + 2, 0, :], 2, 2)
                        ki_ = bcast(Ksp[:, o, 2 * q:2 * q + 2, 1, :], 2, 2)
                        k.tt('dve', v4(YA[i % 2][:]), v4(bank), kr_, ALU.mult)
                        k.tt('dve', v4(YB[i % 2][:]), v4(bank)[:, :, ::-1, :], ki_, ALU.mult)
                    return f

                def st_gb(i):
                    bank = ps[4 + i % 2]
                    ya, yb_ = YA[i % 2], YB[i % 2]
                    for u in range(2):
                        gb = bank[:, 256 * u:256 * (u + 1)]
                        k.mm(gb, ya[:, 256 * u:256 * u + 128], cb16['WI1'][:], start=True, stop=False)
                        k.mm(gb, yb_[:, 256 * u:256 * u + 128], WI1n[:], start=False, stop=False)
                        k.mm(gb, ya[:, 256 * u + 128:256 * (u + 1)], cb16['WI2'][:], start=False, stop=False)
                        k.mm(gb, yb_[:, 256 * u + 128:256 * (u + 1)], cb16['WI2'][:], start=False, stop=True)

                def st_itw(o, q):
                    def f(i):
                        bank = ps[4 + i % 2][:, :]
                        g1o = Gbuf[:, :, 8 * q:8 * q + 8, :].rearrange("p r (u c) s -> p r u (c s)", u=2)
                        g2o = G2[:, :, 8 * q:8 * q + 8, :].rearrange("p r (u c) s -> p r u (c s)", u=2)
                        k.tt('dve', g1o, v4(bank).rearrange("p u r f -> p r u f"),
                             tab4(c32['TIa'][:]).rearrange("p u r f -> p r u f"), ALU.mult)
                        k.tt('dve', g2o, v4(bank)[:, :, ::-1, :].rearrange("p u r f -> p r u f"),
                             tab4(c32['TIb'][:]).rearrange("p u r f -> p r u f"), ALU.mult)
                    return f

                def st_inva(o, q):
                    def f(i):
                        if q % 2 == 1:
                            cc = q // 2
                            gate = tm[1] if o == 0 else tm[2]
                            yb = ps[6]
                            for n_, gsrc in enumerate((Gbuf, G2)):
                                k.mm(yb[:, :], cb16['FIr'][:], gsrc[:, 0, 16 * cc:16 * cc + 16, :].rearrange("p c s -> p (c s)"),
                                     start=(n_ == 0), stop=False)
                                k.mm(yb[:, :], cb16['FIi'][:], gsrc[:, 1, 16 * cc:16 * cc + 16, :].rearrange("p c s -> p (c s)"),
                                     start=False, stop=(n_ == 1))
                            cs = slice(c0 + 16 * cc, c0 + 16 * cc + 16)
                            if o == 0:
                                k.tt('dve', z2_tm[:, cs, :], yb[:, :].rearrange("p (c s) -> p c s", c=16), gate[:, cs, :], ALU.mult)
                            else:
                                k.tt('dve', y_sc[:, :, cs].rearrange("p s c -> p c s"),
                                     yb[:, :].rearrange("p (c s) -> p c s", c=16), gate[:, cs, :], ALU.mult)
                    return f

                items = []
                for q in range(16):
                    items.append([st_za(spec_lhs, q), st_tw, st_ub, st_kev(q)])
                for o in range(DBG.get('nord', 2)):
                    src = tm[0] if o == 0 else z2_tm
                    if o == 1:
                        items += [[] for _ in range(DBG.get('gap', 0))]
                    for q in range(8):
                        items.append([st_za(conv_lhs_of(src), q), st_tw, st_ub, st_mul(o, q), st_gb, st_itw(o, q), st_inva(o, q)])
                hook = None
                if hbk == 0 and DBG.get('pref', 1):
                    pref[0] = filt_gen(cb, 1, ps[7])

                    def hook(t, g=pref[0]):
                        if t >= 19:
                            next(g, None)
                            next(g, None)
                run_skewed(items, hook)
            if DBG.get('dump_z2'):
                k.dma(T['dbg_tm'][:, :], z2_tm[:].rearrange("p c s -> p (c s)"))
            if DBG.get('s2b', 9) < 6: continue
            for a in range(4):
                pv = ps[2 + a % 2][:, :].bitcast(BF)
                for e in range(8):
                    k.tr(pv[:, 128 * e:128 * (e + 1)], y_sc[:, 8 * a + e, :], ident[:])
                k.tt('dve', yzb[:].rearrange("c (p s) -> c p s", s=32)[:, :, 8 * a:8 * a + 8],
                     pv.rearrange("c (s p) -> c p s", s=8),
                     sz[:].rearrange("c (p s) -> c p s", s=32)[:, :, 8 * a:8 * a + 8], ALU.mult)
            for tb in range(NB):
                k.dma(T['yz_d'][tb][:, 512 * cb:512 * (cb + 1)], yzb[:, 512 * tb:512 * (tb + 1)])


def phase2(nc, T):
    if 'b' in DBG.get('p2', 'ab'):
        phase2b(nc, T)


def _bf(a):
    return np.asarray(a, np.float32).astype(ml_dtypes.bfloat16)


_CONST_CACHE = {}


def host_constants():
    if _CONST_CACHE:
        return _CONST_CACHE
    C = {}
    pos = np.arange(L, dtype=np.float32)
    inv_freq = (np.float32(10000.0) ** (-np.arange(0, 32, 2, dtype=np.float32) / np.float32(32))).astype(np.float32)
    ang = (pos[:, None] * inv_freq[None, :]).astype(np.float32)
    C['cosT'] = np.ascontiguousarray(np.cos(ang).astype(np.float32).reshape(NT, 128, 16).transpose(1, 0, 2).reshape(128, NT * 16))
    C['sinT'] = np.ascontiguousarray(np.sin(ang).astype(np.float32).reshape(NT, 128, 16).transpose(1, 0, 2).reshape(128, NT * 16))
    F = fft_constants()
    for nm in ('FA1', 'FA2', 'WBr', 'WBi', 'WBni', 'WI1', 'WI2', 'FIr', 'FIi'):
        C[nm] = np.ascontiguousarray(_bf(F[nm]))
    for nm in ('TWa', 'TWb', 'TIa', 'TIb'):
        C[nm] = np.ascontiguousarray(F[nm].astype(np.float32))
    C.update(filter_constants())
    _CONST_CACHE.update(C)
    return _CONST_CACHE


def prep_inputs(inp, b):
    f32 = np.float32
    m = {}
    m['x'] = np.ascontiguousarray(inp['x'][b], dtype=f32)
    m['w_in'] = np.ascontiguousarray(inp['w_in'][0], dtype=f32)
    m['gT'] = np.ascontiguousarray(inp['g_norm'][0].reshape(8, 128).T, dtype=f32)
    m['bgT'] = np.ascontiguousarray(inp['b_gate'][0].reshape(16, 128).T, dtype=f32)
    m['w_uq'] = np.ascontiguousarray(inp['w_uq'][0], dtype=f32)
    m['w_ukv'] = np.ascontiguousarray(inp['w_ukv'][0], dtype=f32)
    m['gcqT'] = np.ascontiguousarray(inp['g_cq'][0].reshape(3, 128).T, dtype=f32)
    m['gckvT'] = np.ascontiguousarray(inp['g_ckv'][0].reshape(2, 128).T, dtype=f32)
    m['gqk'] = np.ascontiguousarray(np.concatenate([inp['g_qn'][0], inp['g_kn'][0]])[None, :], dtype=f32)
    m['w_attn_out'] = np.ascontiguousarray(inp['w_attn_out'][0], dtype=f32)
    m['w_hy_out'] = np.ascontiguousarray(inp['w_hy_out'][0], dtype=f32)
    m['w_out'] = np.ascontiguousarray(inp['w_out'][0], dtype=f32)
    wsh = np.zeros((128, 12, 4), f32)
    wsh[:, :, 0:3] = inp['w_short'][0].reshape(3, 12, 128).transpose(2, 1, 0)
    wsh[:, :, 3] = inp['b_short'][0].reshape(12, 128).T
    m['wsh'] = wsh.reshape(128, 48)
    hb = inp['hy_bias'][0]
    bT = hb.reshape(2, 128, 4).transpose(2, 0, 1)
    m['biasT'] = np.ascontiguousarray(np.repeat(bT[:, None], 32, axis=1).reshape(128, 256), dtype=f32)
    W1 = np.zeros((128, 128), f32)
    W1[0:33, 0:64] = inp['w_f1'][0]
    W1[64:97, 64:128] = inp['w_f1'][0]
    m['W1blk'] = W1
    W2 = np.zeros((128, 128), f32)
    W2[0:64, 0:64] = inp['w_f2'][0]
    W2[64:128, 64:128] = inp['w_f2'][0]
    m['W2blk'] = W2
    mv = np.zeros((128, 4), f32)
    for jj, nm in enumerate(('freq_1', 'b_f1', 'freq_2', 'b_f2')):
        mv[0:64, jj] = inp[nm][0]
        mv[64:128, jj] = inp[nm][0]
    m['mlpv'] = mv
    w3 = inp['w_f3'][0].reshape(64, 2, 2, 8, 64)
    W3 = np.zeros((128, 8, 2, 2, 64), f32)
    for dd in range(2):
        W3[64 * dd:64 * (dd + 1), :, :, dd, :] = w3[:, :, dd, :, :].transpose(0, 2, 1, 3)
    m['W3blk'] = W3.reshape(128, 2048)
    C = host_constants()
    for nm in CONST_NAMES:
        m[nm] = C[nm]
    return m


IN_SHAPES = {
    'x': ([L, D], F32), 'w_in': ([D, 5280], F32), 'gT': ([128, 8], F32), 'bgT': ([128, 16], F32),
    'w_uq': ([384, 768], F32), 'w_ukv': ([256, 1024], F32), 'gcqT': ([128, 3], F32), 'gckvT': ([128, 2], F32),
    'gqk': ([1, 192], F32), 'w_attn_out': ([512, D], F32), 'w_hy_out': ([512, D], F32), 'w_out': ([D, D], F32),
    'cosT': ([128, NT * 16], F32), 'sinT': ([128, NT * 16], F32),
    'wsh': ([128, 48], F32), 'biasT': ([128, 256], F32), 'W1blk': ([128, 128], F32), 'W2blk': ([128, 128], F32),
    'mlpv': ([128, 4], F32), 'W3blk': ([128, 2048], F32),
    'FA1': ([128, 256], BF), 'FA2': ([128, 256], BF), 'WBr': ([128, 128], BF), 'WBi': ([128, 128], BF),
    'WBni': ([128, 128], BF), 'WI1': ([128, 256], BF), 'WI2': ([128, 256], BF), 'FIr': ([128, 128], BF),
    'FIi': ([128, 128], BF), 'TWa': ([128, 256], F32), 'TWb': ([128, 256], F32), 'TIa': ([128, 256], F32),
    'TIb': ([128, 256], F32), 'zs_hi': ([128, L], BF), 'zs_lo': ([128, L], BF), 'tfull': ([128, 64], F32),
    'negd': ([1, HYW], F32),
}
CONST_NAMES = ('cosT', 'sinT', 'FA1', 'FA2', 'WBr', 'WBi', 'WBni', 'WI1', 'WI2', 'FIr', 'FIi', 'TWa', 'TWb', 'TIa', 'TIb',
               'zs_hi', 'zs_lo', 'tfull', 'negd')


def build_nc(debug=None):
    debug = debug or set()
    nc = bass.Bass("TRN2", target_bir_lowering=False)
    T = {}
    for name, (shape, dt) in IN_SHAPES.items():
        T[name] = nc.dram_tensor(name, shape, dt, kind="ExternalInput").ap()
    T['out'] = nc.dram_tensor("out", [L, D], F32, kind="ExternalOutput").ap()
    skind = dict(kind="ExternalOutput") if 'dump' in debug else {}
    T['hT_d'] = nc.dram_tensor("hT_d", [NB, 128, 8 * 512], BF, **skind).ap()
    T['at_d'] = nc.dram_tensor("at_d", [NB, 64, 8 * 512], BF, **skind).ap()
    T['h2_d'] = nc.dram_tensor("h2_d", [128, L], BF, **skind).ap()
    if 'dump' in debug:
        T['dbg_tm'] = nc.dram_tensor("dbg_tm", [128, 4096], BF, kind="ExternalOutput").ap()
        T['dbg_k'] = nc.dram_tensor("dbg_k", [128, 8192], BF, kind="ExternalOutput").ap()
        T['dbg_ks'] = nc.dram_tensor("dbg_ks", [128, 8192], BF, kind="ExternalOutput").ap()
    if 'yz_in' in debug:
        T['yz_d'] = nc.dram_tensor("yz_d", [NB, 128, 4 * 512], BF, kind="ExternalInput").ap()
    else:
        T['yz_d'] = nc.dram_tensor("yz_d", [NB, 128, 4 * 512], BF, **skind).ap()
    phases = debug & {'p1', 'p2', 'p3', 'p4'} or {'p1', 'p2', 'p3', 'p4'}
    do_p2 = 'p2' in phases and 'yz_in' not in debug
    if 'p1' in phases or do_p2:
        phase12(nc, T, with_p1=('p1' in phases), with_p2a=do_p2)
    if do_p2:
        phase2(nc, T)
    if 'p3' in phases:
        phase3(nc, T)
    if 'p4' in phases:
        phase4(nc, T)
    return nc


def kernel(**inputs):
    inp = {k_: np.asarray(v) for k_, v in inputs.items()}
    nc = build_nc()
    in_maps = [prep_inputs(inp, b) for b in range(8)]
    res = run_bass_kernel_spmd(nc, in_maps, core_ids=list(range(8)))
    out = np.stack([np.asarray(r['out'], dtype=np.float32) for r in res.results], axis=0)
    return out
```

```python
import concourse.bass as bass
import concourse.mybir as mybir

_ESZ = {}


def _esize(dt):
    s = _ESZ.get(dt)
    if s is None:
        n = str(dt)
        if '32' in n:
            s = 4
        elif '16' in n:
            s = 2
        elif '8' in n:
            s = 1
        else:
            s = 4
        _ESZ[dt] = s
    return s


def footprint(ap):
    t = ap.tensor
    name = t.name
    es = _esize(ap.dtype)
    apl = ap.ap
    off = int(ap.offset) * es
    space = str(type(t).__name__)
    if 'DRam' in space:
        lo = off
        hi = off
        for st, cnt in apl:
            if cnt > 1:
                d = (cnt - 1) * st * es
                if d > 0:
                    hi += d
                else:
                    lo += d
        return (name, 0, 1, lo, hi + es)
    pstep, pcnt = apl[0]
    pstep_b = pstep * es
    if pstep_b > 0:
        p0 = off // pstep_b
        f0 = off % pstep_b
    else:
        p0 = 0
        f0 = off
    lo = f0
    hi = f0
    for st, cnt in apl[1:]:
        if cnt > 1:
            d = (cnt - 1) * st * es
            if d > 0:
                hi += d
            else:
                lo += d
    return (name, p0, p0 + pcnt, lo, hi + es)


COMPUTE = ('pe', 'act', 'dve', 'pool')
QUEUES = ('pe', 'act', 'dve', 'pool', 'sp')
QIDX = {q: i for i, q in enumerate(QUEUES)}


class _Op:
    __slots__ = ('q', 'fn', 'dma', 'idx', 'gid', 'waits_c', 'waits_d', 'signal', 'snap', 'slot', 'slot_cnt', 'prev_slot')


class Prog:
    def __init__(self, nc, dma_slots=None):
        self.nc = nc
        self.streams = {q: [] for q in QUEUES}
        self.recs = {}
        self.known = {q: [-1] * len(QUEUES) for q in QUEUES}
        self.known_dma = {q: set() for q in QUEUES}
        self.ops = []
        self.dma_slots = dma_slots or {'sp': 8, 'pool': 4, 'act': 4}
        self.dma_count = {q: 0 for q in QUEUES}
        self.dma_ops = {q: [] for q in QUEUES}
        self.n_comp = {q: 0 for q in QUEUES}

    def add(self, q, fn, reads=(), writes=(), dma=False):
        op = _Op()
        op.q = q
        op.fn = fn
        op.dma = dma
        op.gid = len(self.ops)
        op.signal = dma
        op.waits_c = []
        op.waits_d = []
        op.slot = None
        op.prev_slot = None
        stream = self.streams[q]
        if not dma:
            op.idx = self.n_comp[q]
            self.n_comp[q] += 1
        else:
            op.idx = -1
        deps_c = {}
        deps_d = set()

        def scan(fp, is_write):
            name, p0, p1, f0, f1 = fp
            lst = self.recs.get(name)
            if not lst:
                return
            for r in lst:
                (rp0, rp1, rf0, rf1, rw, rop) = r
                if not (is_write or rw):
                    continue
                if rp1 <= p0 or p1 <= rp0 or rf1 <= f0 or f1 <= rf0:
                    continue
                if rop.dma:
                    deps_d.add(rop)
                else:
                    e = rop.q
                    if deps_c.get(e, -1) < rop.idx:
                        deps_c[e] = rop.idx

        rfps = [footprint(a) for a in reads]
        wfps = [footprint(a) for a in writes]
        for fp in rfps:
            scan(fp, False)
        for fp in wfps:
            scan(fp, True)
        known = self.known[q]
        kd = self.known_dma[q]
        for e, i in deps_c.items():
            ei = QIDX[e]
            if e == q and not dma:
                if q == 'pe':
                    continue
            if i <= known[ei]:
                continue
            op.waits_c.append((e, i))
            src = self.comp_ops[e][i]
            src.signal = True
            known[ei] = i
            for k, v in enumerate(src.snap):
                if v > known[k]:
                    known[k] = v
        for d in sorted(deps_d, key=lambda o: o.gid):
            if d.gid in kd:
                continue
            op.waits_d.append(d)
            kd.add(d.gid)
            for k, v in enumerate(d.snap):
                if v > known[k]:
                    known[k] = v
        if dma:
            n = self.dma_count[q]
            R = self.dma_slots[q]
            op.slot = n % R
            op.slot_cnt = n // R + 1
            if n >= R:
                prev = self.dma_ops[q][n - R]
                op.prev_slot = prev
                kd.add(prev.gid)
            self.dma_count[q] = n + 1
            self.dma_ops[q].append(op)
        op.snap = tuple(known)
        if not dma:
            self.comp_ops[q].append(op)
        for fp, is_write in [(f, False) for f in rfps] + [(f, True) for f in wfps]:
            name, p0, p1, f0, f1 = fp
            lst = self.recs.setdefault(name, [])
            if is_write:
                lst[:] = [r for r in lst if not (r[0] >= p0 and r[1] <= p1 and r[2] >= f0 and r[3] <= f1)]
            else:
                if not dma:
                    lst[:] = [r for r in lst if not (r[4] is False and (not r[5].dma) and r[5].q == q
                                                     and r[0] == p0 and r[1] == p1 and r[2] == f0 and r[3] == f1)]
            lst.append((p0, p1, f0, f1, is_write, op))
        stream.append(op)
        self.ops.append(op)
        return op

    comp_ops = None

    def start(self):
        self.comp_ops = {q: [] for q in QUEUES}

    def pe(self, fn, reads, writes):
        return self.add('pe', fn, reads, writes)

    def act(self, fn, reads, writes):
        return self.add('act', fn, reads, writes)

    def dve(self, fn, reads, writes):
        return self.add('dve', fn, reads, writes)

    def pool(self, fn, reads, writes):
        return self.add('pool', fn, reads, writes)

    def dma(self, out, in_, q='sp', **kw):
        return self.add(q, lambda e: e.dma_start(out=out, in_=in_, **kw), [in_], [out], dma=True)

    def emit(self, block, sems_c, sems_d):
        cum = {}
        for e in QUEUES:
            c = 0
            arr = []
            for o in self.comp_ops[e]:
                if o.signal:
                    c += 1
                arr.append(c)
            cum[e] = arr
        self.cum = cum

        def gen(q):
            def body(eng):
                for o in self.streams[q]:
                    for (e, i) in o.waits_c:
                        eng.wait_ge(sems_c[e], cum[e][i])
                    for d in o.waits_d:
                        eng.wait_ge(sems_d[d.q][d.slot], 16 * d.slot_cnt)
                    if o.prev_slot is not None:
                        p = o.prev_slot
                        eng.wait_ge(sems_d[p.q][p.slot], 16 * p.slot_cnt)
                    ins = o.fn(eng)
                    if o.dma:
                        ins.then_inc(sems_d[q][o.slot], 16)
                    elif o.signal:
                        ins.then_inc(sems_c[q], 1)
                R = self.dma_slots.get(q, 0)
                n = self.dma_count[q]
                for o in self.dma_ops[q][max(0, n - R):]:
                    eng.wait_ge(sems_d[q][o.slot], 16 * o.slot_cnt)
            return body

        if self.streams['pe']:
            block.tensor(gen('pe'))
        if self.streams['act']:
            block.scalar(gen('act'))
        if self.streams['dve']:
            block.vector(gen('dve'))
        if self.streams['pool']:
            block.gpsimd(gen('pool'))
        if self.streams['sp']:
            block.sync(gen('sp'))

import math
from contextlib import ExitStack
import numpy as np
import ml_dtypes
from concourse.bass_utils import run_bass_kernel_spmd

F32 = mybir.dt.float32
BF = mybir.dt.bfloat16
AF = mybir.ActivationFunctionType
ALU = mybir.AluOpType
AX = mybir.AxisListType

L = 4096
D = 1024
NT = 32
NB = 8
EPS = 1e-6
NF = 8192
HYW = 512
COL_V, COL_X1, COL_X2, COL_ZH = 0, 512, 1024, 1536
COL_CQ, COL_CKV, COL_KR, COL_ZA = 2048, 2432, 2688, 2720
COL_GH, COL_GA = 3232, 4256
MAGIC = 12582912.0
DBG = {}


def bcast(ap, axis, n):
    a = ap.unsqueeze(axis)
    shp = list(a.shape)
    shp[axis] = n
    return a.to_broadcast(shp)


class K:
    def __init__(self, P):
        self.P = P

    def mm(self, out, lhsT, rhs, start=True, stop=True):
        self.P.pe(lambda e: e.matmul(out, lhsT=lhsT, rhs=rhs, start=start, stop=stop), [lhsT, rhs], [out])

    def tr(self, out, in_, ident):
        self.P.pe(lambda e: e.transpose(out=out, in_=in_, identity=ident), [in_, ident], [out])

    def act(self, out, in_, func, bias=None, scale=None, accum_out=None):
        kw = {}
        reads = [in_]
        writes = [out]
        if bias is not None:
            kw['bias'] = bias
            if not isinstance(bias, (int, float)):
                reads.append(bias)
        if scale is not None:
            kw['scale'] = scale
            if not isinstance(scale, (int, float)):
                reads.append(scale)
        if accum_out is not None:
            kw['accum_out'] = accum_out
            writes.append(accum_out)
        self.P.act(lambda e: e.activation(out=out, in_=in_, func=func, **kw), reads, writes)

    def tt(self, eng, out, in0, in1, op):
        self.P.add(eng, lambda e: e.tensor_tensor(out=out, in0=in0, in1=in1, op=op), [in0, in1], [out])

    def ts(self, eng, out, in0, s1, s2=None, op0=ALU.mult, op1=None):
        reads = [in0]
        if not isinstance(s1, (int, float)):
            reads.append(s1)
        if s2 is not None and not isinstance(s2, (int, float)):
            reads.append(s2)
        if op1 is None:
            self.P.add(eng, lambda e: e.tensor_scalar(out=out, in0=in0, scalar1=s1, scalar2=None, op0=op0), reads, [out])
        else:
            self.P.add(eng, lambda e: e.tensor_scalar(out=out, in0=in0, scalar1=s1, scalar2=s2, op0=op0, op1=op1), reads, [out])

    def stt(self, out, in0, scalar, in1, op0, op1):
        reads = [in0, in1]
        if not isinstance(scalar, (int, float)):
            reads.append(scalar)
        self.P.dve(lambda e: e.scalar_tensor_tensor(out=out, in0=in0, scalar=scalar, in1=in1, op0=op0, op1=op1), reads, [out])

    def copy(self, eng, out, in_):
        if eng == 'act':
            self.act(out, in_, AF.Copy)
        else:
            self.P.add(eng, lambda e: e.tensor_copy(out=out, in_=in_), [in_], [out])

    def recip(self, out, in_):
        self.P.dve(lambda e: e.reciprocal(out=out, in_=in_), [in_], [out])

    def reduce_add(self, out, in_):
        self.P.dve(lambda e: e.tensor_reduce(out=out, in_=in_, axis=AX.X, op=ALU.add), [in_], [out])

    def memset(self, eng, ap, val):
        self.P.add(eng, lambda e: e.memset(ap, val), [], [ap])

    def dma(self, out, in_, q='sp'):
        self.P.dma(out, in_, q=q)


def _dump(P):
    cum = {}
    for e in QUEUES:
        c = 0
        arr = []
        for o in P.comp_ops[e]:
            if o.signal:
                c += 1
            arr.append(c)
        cum[e] = arr
    for q in QUEUES:
        print("== stream", q)
        for o in P.streams[q]:
            w = [f"{e}>={cum[e][i]}(op{i})" for e, i in o.waits_c] + [f"dma[{d.q}{d.slot}]>={16*d.slot_cnt}" for d in o.waits_d]
            if o.prev_slot is not None:
                w.append(f"prev dma[{o.prev_slot.q}{o.prev_slot.slot}]>={16*o.prev_slot.slot_cnt}")
            tag = f"DMA slot{o.slot} cnt{o.slot_cnt}" if o.dma else (f"op{o.idx} sig={cum[q][o.idx] if o.signal else '-'}")
            print("   ", tag, getattr(o, 'desc', ''), "waits:", w)


class Phase:
    def __init__(self, nc, name):
        self.nc = nc
        self.name = name
        self.es = ExitStack()

    def __enter__(self):
        nc = self.nc
        es = self.es
        es.__enter__()
        self.ps = [es.enter_context(nc.psum_tensor(f"{self.name}_ps{i}", [128, 512], F32)) for i in range(8)]
        self.sems_c = {e: es.enter_context(nc.semaphore(f"{self.name}_sc_{e}")) for e in QUEUES}
        self.sems_d = {q: [es.enter_context(nc.semaphore(f"{self.name}_sd_{q}{i}")) for i in range(n)]
                       for q, n in (('sp', 8), ('pool', 4), ('act', 4))}
        self.P = Prog(nc)
        self.P.start()
        self.k = K(self.P)
        return self

    def sb(self, name, shape, dt):
        return self.es.enter_context(self.nc.sbuf_tensor(f"{self.name}_{name}", shape, dt))

    def __exit__(self, *a):
        if a[0] is None:
            self.es.enter_context(self.nc.allow_low_precision("bf16 operands / intermediates by design"))
            block = self.es.enter_context(self.nc.Block())
            if DBG.get('dump') == self.name:
                _dump(self.P)
            self.P.emit(block, self.sems_c, self.sems_d)
        return self.es.__exit__(*a)


def make_ident(ph, ident):
    identf = ph.sb("identf", [128, 128], F32)
    ph.k.memset('pool', identf[:], 0.0)
    ph.P.pool(lambda e: e.affine_select(out=identf[:], in_=identf[:], pattern=[[-1, 128]], compare_op=ALU.not_equal,
                                        fill=1.0, base=0, channel_multiplier=1), [identf[:]], [identf[:]])
    ph.k.copy('dve', ident[:], identf[:])


def load_w(ph, dst, src_ap):
    ph.k.dma(dst, src_ap, q='pool')


def rstd_from_ss(k, rs_col, ss_col, n):
    k.act(rs_col, ss_col, AF.Sqrt, bias=EPS, scale=1.0 / n)
    k.recip(rs_col, rs_col)


def phase1_gen(ph, T, banks):
    k = ph.k
    xt = [ph.sb(f"xt{i}", [128, D], F32) for i in range(3)]
    xn = [ph.sb(f"xn{i}", [128, D], BF) for i in range(2)]
    junk = ph.sb("junk", [128, D], BF)
    ss = ph.sb("ss", [128, NT], F32)
    rs = ph.sb("rs", [128, NT], F32)
    ts_ = ph.sb("ts_", [128, NT], F32)
    mh = ph.sb("mh", [128, 1], F32)
    gT = ph.sb("gT", [128, 8], F32)
    ident = ph.sb("ident", [128, 128], BF)
    hb = [ph.sb(f"hb{i}", [128, 8, 512], BF) for i in range(2)]
    make_ident(ph, ident)
    k.memset('pool', mh[:], -0.5)
    k.dma(gT[:], T['gT'][:, :])
    pend = [None]
    for i in range(NT):
        x_t = xt[i % 3]
        k.dma(x_t[:], T['x'][128 * i:128 * (i + 1), :])
        k.act(junk[:], x_t[:], AF.Square, accum_out=ss[:, i:i + 1])
        rstd_pool(k, rs[:, i:i + 1], ss[:, i:i + 1], D, mh[:, 0:1], ts_[:, i:i + 1])
        x_n = xn[i % 2]
        k.ts('dve', x_n[:], x_t[:], rs[:, i:i + 1])
        bank = banks[i % len(banks)]
        pv = bank[:, :].bitcast(BF)
        for c in range(8):
            k.tr(pv[:, 128 * c:128 * (c + 1)], x_n[:, 128 * c:128 * (c + 1)], ident[:])
        if pend[0] is not None:
            pend[0]()

        def evac(i=i, pv=pv):
            h_b = hb[(i // 4) % 2]
            j = i % 4
            k.tt('dve', h_b[:, :, 128 * j:128 * (j + 1)], pv.rearrange("p (c t) -> p c t", c=8),
                 bcast(gT[:, :], 2, 128), ALU.mult)
            if j == 3:
                k.dma(T['hT_d'][i // 4], h_b[:].rearrange("p c t -> p (c t)"))
        pend[0] = evac
        yield
    pend[0]()
    yield


def rstd_pool(k, rs, ss, n, mhalf, tmp):
    k.ts('dve', tmp, ss, 1.0 / n, EPS, op0=ALU.mult, op1=ALU.add)
    k.tt('pool', rs, tmp, mhalf, ALU.pow)


def phase12(nc, T, with_p1=True, with_p2a=True):
    with Phase(nc, "p12") as ph:
        g1 = phase1_gen(ph, T, ph.ps[0:4]) if with_p1 else iter(())
        g2 = phase2a_gen(ph, T, ph.ps[4:8]) if with_p2a else iter(())
        alive1, alive2 = True, True
        while alive1 or alive2:
            if alive1:
                try:
                    next(g1)
                except StopIteration:
                    alive1 = False
            for _ in range(4):
                if alive2:
                    try:
                        next(g2)
                    except StopIteration:
                        alive2 = False


def run_interleaved(gens, width=2):
    active = []
    gens = list(gens)
    while gens or active:
        while gens and len(active) < width:
            active.append(gens.pop(0))
        for g in list(active):
            try:
                next(g)
            except StopIteration:
                active.remove(g)


def qk_norm_rope(ph, W, src, dst, g_rep, cs_t, sc_t, mhalf, use_act=False):
    k = ph.k
    sq, ssq, rk, ta, tb_, tmp = W['sq'], W['ssq'], W['rk'], W['ta'], W['tb'], W['tmp']
    if use_act:
        k.act(sq[:].rearrange("p a b -> p (a b)"), src[:].rearrange("p a b -> p (a b)"), AF.Square)
    else:
        k.tt('pool', sq[:], src[:], src[:], ALU.mult)
    k.reduce_add(ssq[:], sq[:])
    rstd_pool(k, rk[:], ssq[:], 96, mhalf[:, 0:8], tmp[:])
    yield
    k.tt('dve', src[:], src[:], bcast(rk[:, :], 2, 96), ALU.mult)
    k.tt('dve', src[:], src[:], bcast(g_rep, 1, 8), ALU.mult)
    yield
    t1 = bcast(src[:, :, 64:80], 1, 2)
    t2 = bcast(src[:, :, 80:96], 1, 2)
    k.tt('pool', ta[:], t1, bcast(cs_t, 2, 8), ALU.mult)
    k.tt('pool', tb_[:], t2, bcast(sc_t, 2, 8), ALU.mult)
    k.tt('dve', dst[:, :, 64:80], ta[:, 0], tb_[:, 0], ALU.subtract)
    k.tt('dve', dst[:, :, 80:96], ta[:, 1], tb_[:, 1], ALU.add)
    k.copy('pool', dst[:, :, 0:64], src[:, :, 0:64])
    yield


def phase3(nc, T):
    with Phase(nc, "p3") as ph:
        k = ph.k
        ps = ph.ps
        ident = ph.sb("ident", [128, 128], BF)
        make_ident(ph, ident)
        w_in_v = T['w_in'].rearrange("(k p) n -> p k n", p=128)
        Wkv = ph.sb("Wkv", [128, 8, 288], BF)
        Wq = ph.sb("Wq", [128, 8, 384], BF)
        Wuq = ph.sb("Wuq", [128, 3, 768], BF)
        Wukv = ph.sb("Wukv", [128, 2, 1024], BF)
        gcq = ph.sb("gcq", [128, 3], F32)
        gckv = ph.sb("gckv", [128, 2], F32)
        gqk = ph.sb("gqk", [128, 192], F32)
        csT = ph.sb("csT", [128, NT, 2, 16], F32)
        mhalf = ph.sb("mhalf", [128, 8], F32)
        k.memset('pool', mhalf[:], -0.5)
        load_w(ph, Wkv[:], w_in_v[:, :, COL_CKV:COL_CKV + 288])
        load_w(ph, Wq[:], w_in_v[:, :, COL_CQ:COL_CQ + 384])
        load_w(ph, Wuq[:], T['w_uq'].rearrange("(k p) n -> p k n", p=128))
        load_w(ph, Wukv[:], T['w_ukv'].rearrange("(k p) n -> p k n", p=128))
        k.dma(gcq[:], T['gcqT'][:, :])
        k.dma(gckv[:], T['gckvT'][:, :])
        k.dma(gqk[:], T['gqk'][0:1, :].partition_broadcast(128))
        cosv = T['cosT'].rearrange("p (a b) -> p a b", b=16)
        sinv = T['sinT'].rearrange("p (a b) -> p a b", b=16)
        k.dma(csT[:, :, 0, :], cosv)
        k.dma(csT[:, :, 1, :], sinv)
        k.tt('pool', Wuq[:], Wuq[:], bcast(gcq[:, :], 2, 768), ALU.mult)
        k.tt('pool', Wukv[:], Wukv[:], bcast(gckv[:, :], 2, 1024), ALU.mult)

        kT = ph.sb("kT", [128, 8, L], BF)
        vx = ph.sb("vx", [128, NT, 8, 65], BF)
        k.memset('pool', vx[:, :, :, 64:65], 1.0)
        ones = ph.sb("ones", [128, 64], BF)
        k.memset('pool', ones[:], 1.0)
        hbuf = [ph.sb(f"hbuf{i}", [128, 8, 512], BF) for i in range(2)]
        junk = [ph.sb(f"junk{i}", [128, 384], BF) for i in range(3)]
        ssl = ph.sb("ssl", [128, 2 * NT], F32)
        rsl = ph.sb("rsl", [128, 2 * NT], F32)
        tsl = ph.sb("tsl", [128, 2 * NT], F32)
        latn = [ph.sb(f"latn{i}", [128, 384], BF) for i in range(3)]
        latT = [ph.sb(f"latT{i}", [128, 3, 128], BF) for i in range(3)]
        kr = [ph.sb(f"kr{i}", [128, 32], F32) for i in range(3)]
        qk32 = [ph.sb(f"qk32_{i}", [128, 8, 96], F32) for i in range(3)]
        qkbf = [ph.sb(f"qkbf{i}", [128, 8, 96], BF) for i in range(3)]
        Wk_ = [dict(sq=ph.sb(f"sq{i}", [128, 8, 96], F32), ssq=ph.sb(f"ssq{i}", [128, 8], F32),
                    rk=ph.sb(f"rk{i}", [128, 8], F32), tmp=ph.sb(f"tmpn{i}", [128, 8], F32),
                    ta=ph.sb(f"ta{i}", [128, 2, 8, 16], F32), tb=ph.sb(f"tb{i}", [128, 2, 8, 16], F32)) for i in range(3)]
        qT = [ph.sb(f"qT{i}", [128, 8, 512], BF) for i in range(2)]
        pt = [ph.sb(f"pt{i}", [128, 512], BF) for i in range(3)]
        rsum = [ph.sb(f"rsum{i}", [128, 512], BF) for i in range(2)]
        bcs = [ph.sb(f"bcs{i}", [64, 512], BF) for i in range(2)]
        at = [ph.sb(f"at{i}", [64, 8, 512], BF) for i in range(1)]

        def sumsq(i, col, src_ps, n):
            k.act(junk[i % 3][:, 0:n], src_ps, AF.Square, accum_out=ssl[:, col:col + 1])
            rstd_pool(k, rsl[:, col:col + 1], ssl[:, col:col + 1], n, mhalf[:, 0:1], tsl[:, col:col + 1])

        def kv_tile(i, hb, j):
            par = i % 3
            if j == 0:
                k.dma(hb[:].rearrange("p c t -> p (c t)"), T['hT_d'][i // 4])
            lat = ps[par][:, 0:288]
            for c in range(8):
                k.mm(lat, hb[:, c, 128 * j:128 * (j + 1)], Wkv[:, c, :], start=(c == 0), stop=(c == 7))
            sumsq(i, i, lat[:, 0:256], 256)
            yield
            ln = latn[par]
            k.ts('dve', ln[:, 0:256], lat[:, 0:256], rsl[:, i:i + 1])
            k.copy('dve', kr[par][:], lat[:, 256:288])
            tbank = ps[7][:, :].bitcast(BF)
            for c in range(2):
                k.tr(tbank[:, 128 * c:128 * (c + 1)], ln[:, 128 * c:128 * (c + 1)], ident[:])
            lT = latT[par]
            k.copy('dve', lT[:, 0:2, :].rearrange("p c t -> p (c t)"), tbank[:, 0:256])
            yield
            kvb = [ps[3 + 2 * (i % 2)], ps[4 + 2 * (i % 2)]]
            for half in range(2):
                for c in range(2):
                    k.mm(kvb[half][:, :], lT[:, c, :], Wukv[:, c, 512 * half:512 * (half + 1)],
                         start=(c == 0), stop=(c == 1))
            kk = qk32[par]
            for half in range(2):
                kvv = kvb[half][:, :].rearrange("p (h e) -> p h e", h=4)
                k.copy('dve', kk[:, 4 * half:4 * half + 4, 0:64], kvv[:, :, 0:64])
                k.copy('dve', vx[:, i, 4 * half:4 * half + 4, 0:64], kvv[:, :, 64:128])
            k.copy('pool', kk[:, :, 64:96], bcast(kr[par][:, :], 1, 8))
            yield
            kf = qkbf[par]
            yield from qk_norm_rope(ph, Wk_[par], kk, kf, gqk[:, 96:192], csT[:, i, :, :], csT[:, i, ::-1, :], mhalf, use_act=True)
            kbank = ps[7][:, :].bitcast(BF)
            for h in range(8):
                k.tr(kbank[0:96, 128 * h:128 * (h + 1)], kf[:, h, :], ident[:])
            k.copy('dve', kT[0:96, :, 128 * i:128 * (i + 1)], kbank[0:96, :].rearrange("p (h t) -> p h t", h=8))
            yield

        def q_tile(i, hb, j, q_T, bk=None):
            par = i % 2
            bA, bB = bk if bk is not None else (ps[6], ps[7])
            lat = bA[:, 0:384]
            for c in range(8):
                k.mm(lat, hb[:, c, 128 * j:128 * (j + 1)], Wq[:, c, :], start=(c == 0), stop=(c == 7))
            yield
            lsb = Wk_[par]['sq'][:].rearrange("p a b -> p (a b)")[:, 0:384]
            k.copy('dve', lsb, lat)
            k.P.dve(lambda e: e.scalar_tensor_tensor(out=junk[par][:, 0:384], in0=lsb, scalar=1.0, in1=lsb, op0=ALU.mult,
                                                     op1=ALU.mult, accum_out=ssl[:, NT + i:NT + i + 1]),
                    [lsb], [junk[par][:, 0:384], ssl[:, NT + i:NT + i + 1]])
            rstd_pool(k, rsl[:, NT + i:NT + i + 1], ssl[:, NT + i:NT + i + 1], 384, mhalf[:, 0:1], tsl[:, NT + i:NT + i + 1])
            yield
            ln = latn[par]
            k.ts('dve', ln[:, 0:384], lsb, rsl[:, NT + i:NT + i + 1])
            yield
            tbank = bB[:, :].bitcast(BF)
            for c in range(3):
                k.tr(tbank[:, 128 * c:128 * (c + 1)], ln[:, 128 * c:128 * (c + 1)], ident[:])
            yield
            lT = latT[par]
            k.copy('dve', lT[:].rearrange("p c t -> p (c t)"), tbank[:, 0:384])
            yield
            qq = qk32[par]
            for half in range(2):
                qb = bA[:, 0:384]
                for c in range(3):
                    k.mm(qb, lT[:, c, :], Wuq[:, c, 384 * half:384 * (half + 1)], start=(c == 0), stop=(c == 2))
                yield
                k.copy('dve', qq[:, 4 * half:4 * half + 4, :], qb.rearrange("p (h e) -> p h e", h=4))
                yield
            qf = qkbf[par]
            yield from qk_norm_rope(ph, Wk_[par], qq, qf, gqk[:, 0:96], csT[:, i, :, :], csT[:, i, ::-1, :], mhalf, use_act=(i < 4))
            yield
            yield
            yield
            yield
            qbank = bB[:, :].bitcast(BF)
            for h in range(8):
                k.tr(qbank[0:96, 128 * h:128 * (h + 1)], qf[:, h, :], ident[:])
            yield
            k.copy('dve', q_T[0:96, :, 128 * j:128 * (j + 1)], qbank[0:96, :].rearrange("p (h t) -> p h t", h=8))
            yield

        def kv_block(tb):
            hb = hbuf[tb % 2]
            return [kv_tile(tb * 4 + j, hb, j) for j in range(4)]

        def q_chunk_gens(qc, two_sets=False):
            hb = hbuf[qc % 2]
            k.dma(hb[:].rearrange("p c t -> p (c t)"), T['hT_d'][qc])
            bks = [(ps[6], ps[7]), (ps[0], ps[1])]
            return [q_tile(qc * 4 + j, hb, j, qT[qc % 2], bks[j % 2] if two_sets else None) for j in range(4)]

        gens = []
        for tb in range(DBG.get('nprep', NB)):
            gens += kv_block(tb)
        run_interleaved(gens, 3)
        nqc = DBG.get('nqc', NB)
        if nqc:
            run_interleaved(q_chunk_gens(0, two_sets=True), 2)

        scale = 1.0 / math.sqrt(96.0)
        NH = DBG.get('nh', 8)
        pend = [None]
        for qc in range(nqc):
            q_T = qT[qc % 2]
            a_t = at[0]
            nxt = q_chunk_gens(qc + 1) if qc + 1 < nqc else []
            nxt_active = []
            steps = [(h, kt) for h in range(NH) for kt in range(NT)]

            def S(idx):
                h, kt = steps[idx]
                k.mm(ps[idx % 3][:, :], kT[0:96, h, 128 * kt:128 * (kt + 1)], q_T[0:96, h, :])

            def fin_a(h):
                k.recip(rsum[h % 2][64:65, :], ps[3 + (h % 2)][64:65, :])

            def fin_b(h):
                k.mm(ps[5][0:64, :], ones[64:65, :], rsum[h % 2][64:65, :])

            def fin_c(h, a_t):
                k.copy('dve', bcs[h % 2][:], ps[5][0:64, :])
                k.tt('dve', a_t[:, h, :], ps[3 + (h % 2)][0:64, :], bcs[h % 2][:], ALU.mult)

            S(0)
            S(1)
            for idx, (h, kt) in enumerate(steps):
                p_t = pt[idx % 3]
                k.act(p_t[:], ps[idx % 3][:, :], AF.Exp, scale=scale)
                if idx + 2 < len(steps):
                    S(idx + 2)
                ob = ps[3 + (h % 2)]
                k.mm(ob[0:65, :], vx[:, kt, h, :], p_t[:], start=(kt == 0), stop=(kt == NT - 1))
                if pend[0] is not None:
                    if kt == 1:
                        pend[0][0]()
                    elif kt == 10:
                        pend[0][1]()
                    elif kt == 14:
                        pend[0][2]()
                        pend[0] = None
                if kt == NT - 1:
                    last = (h == NH - 1)
                    pend[0] = (lambda h=h: fin_a(h), lambda h=h: fin_b(h),
                               (lambda h=h, a_t=a_t, qc=qc, last=last, fin_c=fin_c: (fin_c(h, a_t), k.dma(T['at_d'][qc], a_t[:].rearrange("p h t -> p (h t)")) if last else None)))
                if idx % 3 == 2:
                    while nxt and len(nxt_active) < 1:
                        nxt_active.append(nxt.pop(0))
                    for g in list(nxt_active):
                        try:
                            next(g)
                        except StopIteration:
                            nxt_active.remove(g)
            run_interleaved(nxt_active + nxt, 1)

        if pend[0] is not None:
            pend[0][0]()
            pend[0][1]()
            pend[0][2]()


def phase4(nc, T):
    with Phase(nc, "p4") as ph:
        k = ph.k
        ps = ph.ps
        w_in_v = T['w_in'].rearrange("(k p) n -> p k n", p=128)
        Wz = ph.sb("Wz", [128, 8, 512], BF)
        Wg = ph.sb("Wg", [128, 8, 2048], BF)
        Wao = ph.sb("Wao", [128, 4, D], BF)
        Who = ph.sb("Who", [128, 4, D], BF)
        Wout = ph.sb("Wout", [128, 8, D], BF)
        bg = ph.sb("bg", [128, 16], F32)
        load_w(ph, Wz[:], w_in_v[:, :, COL_ZA:COL_ZA + 512])
        for q4 in range(4):
            load_w(ph, Wg[:, :, 512 * q4:512 * (q4 + 1)], w_in_v[:, :, COL_GH + 512 * q4:COL_GH + 512 * (q4 + 1)])
        load_w(ph, Wao[:], T['w_attn_out'].rearrange("(hp p) n -> p hp n", p=128))
        load_w(ph, Who[:], T['w_hy_out'].rearrange("(k p) n -> p k n", p=128))
        load_w(ph, Wout[:], T['w_out'].rearrange("(k p) n -> p k n", p=128))
        k.dma(bg[:], T['bgT'][:, :])
        hbuf = [ph.sb(f"hbuf{i}", [128, 8, 512], BF) for i in range(2)]
        atb = [ph.sb(f"atb{i}", [128, 4, 512], BF) for i in range(2)]
        yzb = [ph.sb(f"yzb{i}", [128, 4, 512], BF) for i in range(2)]
        xt = [ph.sb(f"xt{i}", [128, D], F32) for i in range(3)]
        ot = [ph.sb(f"ot{i}", [128, D], F32) for i in range(2)]
        sz = [ph.sb(f"sz{i}", [128, 512], BF) for i in range(2)]
        ya = ph.sb("ya", [128, 4, 512], BF)
        gh = [ph.sb(f"gh{i}", [128, 512], BF) for i in range(2)]
        ga = [ph.sb(f"ga{i}", [128, 512], BF) for i in range(2)]
        m1 = [ph.sb(f"m1{i}", [128, 512], F32) for i in range(2)]
        m2 = [ph.sb(f"m2{i}", [128, 512], F32) for i in range(2)]
        mg = [ph.sb(f"mg{i}", [128, 8, 512], BF) for i in range(2)]
        for tb in range(NB):
            hb = hbuf[tb % 2]
            a_b = atb[tb % 2]
            y_b = yzb[tb % 2]
            k.dma(hb[:].rearrange("p c t -> p (c t)"), T['hT_d'][tb])
            atv = T['at_d'][tb].rearrange("p (hp two t) -> p two hp t", two=2, t=512)
            k.dma(a_b[0:64, :, :], atv[:, 0, :, :])
            k.dma(a_b[64:128, :, :], atv[:, 1, :, :])
            k.dma(y_b[:].rearrange("p c t -> p (c t)"), T['yz_d'][tb])
            for hp in range(4):
                zb = ps[hp % 2][:, :]
                for c in range(8):
                    k.mm(zb, Wz[:, c, 128 * hp:128 * (hp + 1)], hb[:, c, :], start=(c == 0), stop=(c == 7))
                s_z = sz[hp % 2]
                k.act(s_z[:], zb, AF.Silu)
                k.tt('pool', ya[:, hp, :], a_b[:, hp, :], s_z[:], ALU.mult)
            m_g = mg[tb % 2]
            for dc in range(8):
                g1 = ps[2 + (dc % 2)]
                g2 = ps[4 + (dc % 2)]
                for c in range(8):
                    k.mm(g1[:, :], Wg[:, c, 128 * dc:128 * (dc + 1)], hb[:, c, :], start=(c == 0), stop=(c == 7))
                for c in range(8):
                    k.mm(g2[:, :], Wg[:, c, 1024 + 128 * dc:1024 + 128 * (dc + 1)], hb[:, c, :],
                         start=(c == 0), stop=(c == 7))
                k.act(gh[dc % 2][:], g1[:, :], AF.Sigmoid, bias=bg[:, dc:dc + 1])
                k.act(ga[dc % 2][:], g2[:, :], AF.Sigmoid, bias=bg[:, 8 + dc:9 + dc])
                uh = ps[6]
                ua = ps[7]
                for c in range(4):
                    k.mm(uh[:, :], Who[:, c, 128 * dc:128 * (dc + 1)], y_b[:, c, :], start=(c == 0), stop=(c == 3))
                for hp in range(4):
                    k.mm(ua[:, :], Wao[:, hp, 128 * dc:128 * (dc + 1)], ya[:, hp, :], start=(hp == 0), stop=(hp == 3))
                k.tt('dve', m1[dc % 2][:], uh[:, :], gh[dc % 2][:], ALU.mult)
                k.tt('dve', m2[dc % 2][:], ua[:, :], ga[dc % 2][:], ALU.mult)
                k.tt('pool', m_g[:, dc, :], m1[dc % 2][:], m2[dc % 2][:], ALU.add)
            for j in range(4):
                i = tb * 4 + j
                x_t = xt[i % 3]
                k.dma(x_t[:], T['x'][128 * i:128 * (i + 1), :])
                o_t = ot[i % 2]
                for half in range(2):
                    fb = ps[half]
                    for c in range(8):
                        k.mm(fb[:, :], m_g[:, c, 128 * j:128 * (j + 1)], Wout[:, c, 512 * half:512 * (half + 1)],
                             start=(c == 0), stop=(c == 7))
                    k.tt('dve', o_t[:, 512 * half:512 * (half + 1)], fb[:, :], x_t[:, 512 * half:512 * (half + 1)], ALU.add)
                k.dma(T['out'][128 * i:128 * (i + 1), :], o_t[:])


def fft_constants():
    C = {}
    n = NF
    s2 = np.arange(128, dtype=np.float64)[:, None]
    f2 = np.arange(128, dtype=np.float64)[None, :]
    th = 2 * np.pi * (f2 + 0.5) * s2 / 256.0
    C['FA1'] = np.concatenate([np.cos(th), -np.sin(th)], 1)
    th2 = 2 * np.pi * (f2 + 0.5) * (s2 + 128) / 256.0
    C['FA2'] = -np.concatenate([np.cos(th2), -np.sin(th2)], 1)
    s1 = np.arange(32, dtype=np.float64)
    tw = np.exp(-2j * np.pi * (np.arange(128)[None, :] + 0.5) * s1[:, None] / n)
    twq = np.tile(tw, (4, 1))
    C['TWa'] = np.concatenate([twq.real, twq.real], 1)
    C['TWb'] = np.concatenate([-twq.imag, twq.imag], 1)
    W = np.exp(-2j * np.pi * np.outer(s1, s1) / 32.0)
    Wq = np.kron(np.eye(4), W)
    C['WBr'] = Wq.real
    C['WBi'] = Wq.imag
    C['WBni'] = -Wq.imag
    Wi = np.exp(2j * np.pi * np.outer(s1, s1) / 32.0)
    Wiq = np.kron(np.eye(4), Wi)
    C['WI1'] = np.concatenate([Wiq.real, Wiq.imag], 1)
    C['WI2'] = np.concatenate([-Wiq.imag, Wiq.real], 1)
    twi = np.exp(2j * np.pi * (np.arange(128)[:, None] + 0.5) * s1[None, :] / n)
    twiq = np.tile(twi, (1, 4))
    C['TIa'] = np.concatenate([twiq.real, twiq.real], 1)
    C['TIb'] = np.concatenate([-twiq.imag, twiq.imag], 1)
    t2 = np.arange(128, dtype=np.float64)[None, :]
    f2c = np.arange(128, dtype=np.float64)[:, None]
    th3 = 2 * np.pi * (f2c + 0.5) * t2 / 256.0
    C['FIr'] = (2.0 / n) * np.cos(th3)
    C['FIi'] = -(2.0 / n) * np.sin(th3)
    return C


def filter_constants():
    C = {}
    f32 = np.float32
    t = np.linspace(0.0, 1.0, L, dtype=f32)[:, None]
    bands = 16
    f = np.linspace(1e-4, bands - 1, bands, dtype=f32)
    ang = (f32(2.0 * np.pi / L) * np.arange(L, dtype=f32)[:, None] * f[None, :]).astype(f32)
    z = np.concatenate([t, np.cos(ang).astype(f32), -np.sin(ang).astype(f32)], axis=-1).astype(f32)
    zs = np.zeros((128, L), f32)
    zs[0:33, :] = z.T
    zs[64:97, :] = z[::-1].T
    hi = zs.astype(ml_dtypes.bfloat16)
    lo = (zs - hi.astype(f32)).astype(ml_dtypes.bfloat16)
    C['zs_hi'] = hi
    C['zs_lo'] = lo
    tl = t[:, 0]
    tf = np.zeros((128, 2, 32), f32)
    pidx = np.arange(128)[:, None] * 32 + np.arange(32)[None, :]
    tf[:, 0, :] = tl[pidx]
    tf[:, 1, :] = tl[4095 - pidx]
    C['tfull'] = tf.reshape(128, 64)
    MIN_DECAY = math.log(1e-2) / 1.5
    MAX_DECAY = math.log(1e-2) / 0.3
    deltas = np.abs(np.linspace(MIN_DECAY, MAX_DECAY, HYW, dtype=f32)).astype(f32)
    C['negd'] = (-deltas)[None, :].astype(f32)
    return C


def _sin_layer(ph, W, pre_ps, fr, fb, out32):
    k = ph.k
    a, kk = W['a'], W['kk']
    k.ts('dve', a[:], pre_ps, fr, fb, op0=ALU.mult, op1=ALU.add)
    yield
    k.ts('dve', kk[:], a[:], 1.0 / (2 * math.pi), MAGIC, op0=ALU.mult, op1=ALU.add)
    yield
    k.ts('dve', kk[:], kk[:], -MAGIC, None, op0=ALU.add)
    yield
    k.stt(a[:], kk[:], -2 * math.pi, a[:], ALU.mult, ALU.add)
    yield
    k.ts('dve', a[:], a[:], -3.14159, 3.14159, op0=ALU.max, op1=ALU.min)
    yield
    k.act(out32, a[:], AF.Sin)
    yield


def _hilo(ph, hi, lo, src32, tmp32):
    k = ph.k
    k.copy('dve', hi, src32)
    k.copy('pool', tmp32, hi)
    k.tt('pool', lo, src32, tmp32, ALU.subtract)


def phase2a_gen(ph, T, banks):
    k = ph.k
    zs_hi = ph.sb("zs_hi", [128, L], BF)
    zs_lo = ph.sb("zs_lo", [128, L], BF)
    W1 = ph.sb("W1", [128, 128], F32)
    W2 = ph.sb("W2", [128, 128], F32)
    W1h = ph.sb("W1h", [128, 128], BF)
    W1l = ph.sb("W1l", [128, 128], BF)
    W2h = ph.sb("W2h", [128, 128], BF)
    W2l = ph.sb("W2l", [128, 128], BF)
    wt = ph.sb("wt", [128, 128], F32)
    mv = ph.sb("mv", [128, 4], F32)
    fb = ph.sb("fb", [128, 2], F32)
    S = [dict(a=ph.sb(f"a{i}", [128, 512], F32), kk=ph.sb(f"kk{i}", [128, 512], F32), h1=ph.sb(f"h1_{i}", [128, 512], F32),
              h1h=ph.sb(f"h1h{i}", [128, 512], BF), h1l=ph.sb(f"h1l{i}", [128, 512], BF), t32=ph.sb(f"t32_{i}", [128, 512], F32),
              h2=ph.sb(f"h2_{i}", [128, 512], F32), h2b=ph.sb(f"h2b{i}", [128, 512], BF)) for i in range(2)]
    k.dma(zs_hi[:], T['zs_hi'][:, :])
    k.dma(zs_lo[:], T['zs_lo'][:, :])
    k.dma(W1[:], T['W1blk'][:, :])
    k.dma(W2[:], T['W2blk'][:, :])
    k.dma(mv[:], T['mlpv'][:, :])
    _hilo(ph, W1h[:], W1l[:], W1[:], wt[:])
    _hilo(ph, W2h[:], W2l[:], W2[:], wt[:])
    k.tt('dve', fb[:, 0:1], mv[:, 0:1], mv[:, 1:2], ALU.mult)
    k.tt('dve', fb[:, 1:2], mv[:, 2:3], mv[:, 3:4], ALU.mult)
    yield

    def chunk(cch):
        s_ = S[cch % 2]
        sl = slice(512 * cch, 512 * (cch + 1))
        b1 = banks[cch % 2]
        k.mm(b1[:, :], W1h[:], zs_hi[:, sl], start=True, stop=False)
        k.mm(b1[:, :], W1h[:], zs_lo[:, sl], start=False, stop=False)
        k.mm(b1[:, :], W1l[:], zs_hi[:, sl], start=False, stop=True)
        yield
        yield from _sin_layer(ph, s_, b1[:, :], mv[:, 0:1], fb[:, 0:1], s_['h1'][:])
        _hilo(ph, s_['h1h'][:], s_['h1l'][:], s_['h1'][:], s_['t32'][:])
        yield
        b2 = banks[2 + cch % 2]
        k.mm(b2[:, :], W2h[:], s_['h1h'][:], start=True, stop=False)
        k.mm(b2[:, :], W2h[:], s_['h1l'][:], start=False, stop=False)
        k.mm(b2[:, :], W2l[:], s_['h1h'][:], start=False, stop=True)
        yield
        yield from _sin_layer(ph, s_, b2[:, :], mv[:, 2:3], fb[:, 1:2], s_['h2'][:])
        k.copy('pool', s_['h2b'][:], s_['h2'][:])
        k.dma(T['h2_d'][:, sl], s_['h2b'][:])
        yield

    gens = [chunk(c) for c in range(NB)]
    active = []
    while gens or active:
        while gens and len(active) < 2:
            active.append(gens.pop(0))
        for g in list(active):
            try:
                next(g)
            except StopIteration:
                active.remove(g)
        yield


def _cmul_tab(ph, W, src, Ta, Tb, out_bf):
    k = ph.k
    P1, P2 = W
    sw = src.rearrange("p (r f) -> p r f", r=2)[:, ::-1, :]
    k.tt('dve', P1[:], src, Ta, ALU.mult)
    k.tt('dve', P2[:].rearrange("p (r f) -> p r f", r=2), sw, Tb.rearrange("p (r f) -> p r f", r=2), ALU.mult)
    k.tt('pool', out_bf, P1[:], P2[:], ALU.add)


def phase2b(nc, T):
    with Phase(nc, "p2b") as ph:
        k = ph.k
        ps = ph.ps
        ident = ph.sb("ident", [128, 128], BF)
        make_ident(ph, ident)
        w_in_v = T['w_in'].rearrange("(k p) n -> p k n", p=128)
        cols = (COL_V, COL_X1, COL_X2, COL_ZH)
        ar = ph.sb("arena", [128, 24592], BF)
        Wblk = ar[:, 20486:24582].rearrange("p (k w c) -> p k w c", k=8, w=4)
        for w in range(4):
            load_w(ph, Wblk[:, :, w, :], w_in_v[:, :, cols[w]:cols[w] + 128])
        cb16 = {}
        for nm, w in (('FA1', 256), ('FA2', 256), ('WBr', 128), ('WBi', 128), ('WBni', 128), ('WI1', 256), ('WI2', 256),
                      ('FIr', 128), ('FIi', 128)):
            cb16[nm] = ph.sb(nm, [128, w], BF)
            k.dma(cb16[nm][:], T[nm][:, :])
        c32 = {}
        for nm in ('TWa', 'TWb', 'TIa', 'TIb'):
            c32[nm] = ph.sb(nm, [128, 256], F32)
            k.dma(c32[nm][:], T[nm][:, :])
        h2s = ph.sb("h2s", [128, L], BF)
        k.dma(h2s[:], T['h2_d'][:, :])
        h2p = ph.sb("h2p", [128, 32, 128], BF)
        k.copy('pool', h2p[:], h2s[:].rearrange("q (p s) -> q s p", s=32))
        W3 = ph.sb("W3", [128, 2048], BF)
        load_w(ph, W3[:], T['W3blk'][:, :])
        wsh = ph.sb("wsh", [128, 12, 4], F32)
        k.dma(wsh[:].rearrange("p a b -> p (a b)"), T['wsh'][:, :])
        biasT = ph.sb("biasT", [128, 2, 128], F32)
        k.dma(biasT[:].rearrange("p a b -> p (a b)"), T['biasT'][:, :])
        negd = ph.sb("negd", [128, HYW], F32)
        k.dma(negd[:], T['negd'][0:1, :].partition_broadcast(128))
        tfull = ph.sb("tfull", [128, 2, 32], F32)
        k.dma(tfull[:].rearrange("p a b -> p (a b)"), T['tfull'][:, :])

        hbuf = [ph.sb(f"hbuf{i}", [128, 8, 512], BF) for i in range(2)]
        raw = [ar[:, 4098 * i:4098 * (i + 1)] for i in range(3)]
        ub_ = [ar[:, 12294 + 4096 * i:12294 + 4096 * (i + 1)] for i in range(2)]
        k_tm = ar[:, 0:8192].rearrange("p (o d c s) -> p o d c s", o=2, d=2, c=64)
        AB = ar[:, 8192:12288].rearrange("p (d c s) -> p d c s", d=2, c=64)
        Ksp = ar[:, 12288:20480].rearrange("p (o g r f) -> p o g r f", o=2, g=16, r=2)
        Gbuf = ar[:, 20480:24576].rearrange("p (r c s) -> p r c s", r=2, c=64)
        sz = ph.sb("sz", [128, L], BF)
        tm = [ph.sb(f"tm{i}", [128, 128, 32], BF) for i in range(3)]
        z2_tm = ph.sb("z2_tm", [128, 128, 32], BF)
        y_sc = ph.sb("y_sc", [128, 32, 128], BF)
        yzb = ph.sb("yzb", [128, L], BF)
        arg32 = ph.sb("arg32", [128, 4096], F32)
        PW = [(ph.sb(f"P1_{i}", [128, 256], F32), ph.sb(f"P2_{i}", [128, 256], F32)) for i in range(2)]
        Zp = [ph.sb(f"Zp{i}", [128, 256], BF) for i in range(2)]
        Yb = [ph.sb(f"Yb{i}", [128, 256], BF) for i in range(2)]
        Kev = [ph.sb(f"Kev{i}", [128, 256], BF) for i in range(2)]
        Esb = [[ph.sb(f"E{st}_{i}", [128, 256], BF) for i in range(2)] for st in range(3)]
        cnt = [0]

        ZA = [ph.sb(f"ZA{i}", [128, 512], BF) for i in range(2)]
        ZB = [ph.sb(f"ZB{i}", [128, 512], BF) for i in range(2)]
        YA = [ph.sb(f"YA{i}", [128, 512], BF) for i in range(2)]
        YB = [ph.sb(f"YB{i}", [128, 512], BF) for i in range(2)]
        G2 = ph.sb("G2", [128, 2, 64, 32], BF)
        WI1n = ph.sb("WI1n", [128, 256], BF)
        k.ts('pool', WI1n[:], cb16['WI1'][:], -1.0, None, op0=ALU.mult)

        def v4(ap):
            return ap.rearrange("p (u r f) -> p u r f", u=2, r=2)

        def tab4(t):
            return bcast(t.rearrange("p (r f) -> p r f", r=2), 1, 2)

        def run_skewed(items, hook=None):
            n = len(items)
            depth = max(len(it) for it in items)
            for t in range(n + depth - 1):
                for s_ in reversed(range(depth)):
                    i = t - s_
                    if 0 <= i < n and s_ < len(items[i]):
                        items[i][s_](i)
                if hook is not None:
                    hook(t)

        def cmul_pair(bank, Ta, Tb, outA, outB):
            k.tt('dve', outA, v4(bank), tab4(Ta), ALU.mult)
            k.tt('dve', outB, v4(bank)[:, :, ::-1, :], tab4(Tb), ALU.mult)

        def st_za(lhs_of, q):
            def f(i):
                bank = ps[i % 2]
                for u in range(2):
                    l1, l2 = lhs_of(2 * q + u)
                    za = bank[:, 256 * u:256 * (u + 1)]
                    k.mm(za, l1, cb16['FA1'][:], start=True, stop=(l2 is None))
                    if l2 is not None:
                        k.mm(za, l2, cb16['FA2'][:], start=False, stop=True)
            return f

        def st_tw(i):
            cmul_pair(ps[i % 2][:, :], c32['TWa'][:], c32['TWb'][:], v4(ZA[i % 2][:]), v4(ZB[i % 2][:]))

        def st_ub(i):
            bank = ps[2 + i % 2]
            for u in range(2):
                ub = bank[:, 256 * u:256 * (u + 1)]
                for n_, z_p in enumerate((ZA[i % 2], ZB[i % 2])):
                    zr = z_p[:, 256 * u:256 * u + 128]
                    zi = z_p[:, 256 * u + 128:256 * (u + 1)]
                    k.mm(ub[:, 0:128], cb16['WBr'][:], zr, start=(n_ == 0), stop=False)
                    k.mm(ub[:, 0:128], cb16['WBni'][:], zi, start=False, stop=(n_ == 1))
                for n_, z_p in enumerate((ZA[i % 2], ZB[i % 2])):
                    zr = z_p[:, 256 * u:256 * u + 128]
                    zi = z_p[:, 256 * u + 128:256 * (u + 1)]
                    k.mm(ub[:, 128:256], cb16['WBi'][:], zr, start=(n_ == 0), stop=False)
                    k.mm(ub[:, 128:256], cb16['WBr'][:], zi, start=False, stop=(n_ == 1))

        def filt_gen(cb, hbk, bank_fixed):
            gcol = 128 * cb + 64 * hbk
            k.tt('dve', arg32[:].rearrange("p (d c s) -> p d c s", d=2, c=64),
                 bcast(bcast(negd[:, gcol:gcol + 64], 1, 2), 3, 32),
                 bcast(tfull[:, :, :], 2, 64), ALU.mult)
            yield
            k.act(AB.rearrange("p d c s -> p (d c s)"), arg32[:], AF.Exp)
            yield
            wc0 = 256 * (2 * cb + hbk)
            for s1 in range(32):
                kb_ = (bank_fixed if bank_fixed is not None else ps[6 + s1 % 2])[:, 0:256]
                k.mm(kb_, h2p[:, s1, :], W3[:, wc0:wc0 + 256])
                abv = AB[:, :, :, s1].rearrange("p d c -> p (d c)")
                k.tt('dve', k_tm[:, :, :, :, s1].rearrange("p o d c -> p o (d c)"),
                     kb_.rearrange("p (o x) -> p o x", o=2), bcast(abv, 1, 2), ALU.mult)
                yield

        pref = [None]
        for cb in range(DBG.get('ncb', 4)):
            for w in range(4):
                if cb > 0:
                    load_w(ph, Wblk[:, :, w, :], w_in_v[:, :, cols[w] + 128 * cb:cols[w] + 128 * (cb + 1)])
            for w in range(3):
                k.memset('pool', raw[w][:, 0:1], 0.0)
                k.memset('pool', raw[w][:, 4097:4098], 0.0)
            for tb in range(NB):
                hb = hbuf[tb % 2]
                k.dma(hb[:].rearrange("p c t -> p (c t)"), T['hT_d'][tb])
                for w in range(4):
                    bank = ps[(tb * 4 + w) % 2]
                    for c in range(8):
                        k.mm(bank[:, :], Wblk[:, c, w, :], hb[:, c, :], start=(c == 0), stop=(c == 7))
                    if w < 3:
                        k.act(raw[w][:, 1 + 512 * tb:1 + 512 * (tb + 1)], bank[:, :], AF.Copy)
                    else:
                        k.act(sz[:, 512 * tb:512 * (tb + 1)], bank[:, :], AF.Silu)
            if DBG.get('s2b', 9) < 2: continue
            for w in range(3):
                u = ub_[w % 2]
                j = 4 * w + cb
                k.ts('dve', u, raw[w][:, 1:4097], wsh[:, j, 1:2], wsh[:, j, 3:4], op0=ALU.mult, op1=ALU.add)
                k.stt(u, raw[w][:, 0:4096], wsh[:, j, 0:1], u, ALU.mult, ALU.add)
                k.stt(u, raw[w][:, 2:4098], wsh[:, j, 2:3], u, ALU.mult, ALU.add)
                for a in range(4):
                    pv = ps[2 + a % 2][:, :].bitcast(BF)
                    for e in range(8):
                        s1 = 8 * a + e
                        k.tr(pv[:, 128 * e:128 * (e + 1)], u[:, s1:4096:32], ident[:])
                    k.copy('dve', tm[w][:, :, 8 * a:8 * a + 8], pv.rearrange("p (s c) -> p c s", s=8))
            if DBG.get('dump_tm'):
                k.dma(T['dbg_tm'][:, :], tm[DBG['dump_tm'] - 1][:].rearrange("p c s -> p (c s)"))
            if DBG.get('s2b', 9) < 3: continue
            for hbk in range(DBG.get('nhbk', 2)):
                c0 = 64 * hbk
                gcol = 128 * cb + c0
                if hbk == 0 or not DBG.get('pref', 1):
                    for _ in filt_gen(cb, hbk, None):
                        pass
                else:
                    for _ in pref[0]:
                        pass
                if DBG.get('s2b', 9) < 4: continue
                def spec_lhs(gi):
                    o, g = divmod(gi, 16)
                    return (k_tm[:, o, 0, 4 * g:4 * g + 4, :].rearrange("p c s -> p (c s)"),
                            k_tm[:, o, 1, 4 * g:4 * g + 4, :].rearrange("p c s -> p (c s)"))

                def st_kev(q):
                    def f(i):
                        o, gp = divmod(q, 8)
                        k.copy('act', Ksp[:, o, 2 * gp:2 * gp + 2, :, :].rearrange("p g r f -> p (g r f)"), ps[2 + i % 2][:, :])
                        if gp == 7:
                            gg0 = gcol // 4
                            k.tt('pool', Ksp[:, o, :, 0, :], Ksp[:, o, :, 0, :], bcast(biasT[:, o, gg0:gg0 + 16], 2, 128), ALU.add)
                    return f

                def conv_lhs_of(src):
                    def f(g):
                        return (src[:, c0 + 4 * g:c0 + 4 * g + 4, :].rearrange("p c s -> p (c s)"), None)
                    return f

                def st_mul(o, q):
                    def f(i):
                        bank = ps[2 + i % 2][:, :]
                        kr_ = bcast(Ksp[:, o, 2 * q:2 * q + 2, 0, :], 2, 2)
                        ki_ = bcast(Ksp[:, o, 2 * q:2 * q + 2, 1, :], 2, 2)
                        k.tt('dve', v4(YA[i % 2][:]), v4(bank), kr_, ALU.mult)
                        k.tt('dve', v4(YB[i % 2][:]), v4(bank)[:, :, ::-1, :], ki_, ALU.mult)
                    return f

                def st_gb(i):
                    bank = ps[4 + i % 2]
                    ya, yb_ = YA[i % 2], YB[i % 2]
                    for u in range(2):
                        gb = bank[:, 256 * u:256 * (u + 1)]
                        k.mm(gb, ya[:, 256 * u:256 * u + 128], cb16['WI1'][:], start=True, stop=False)
                        k.mm(gb, yb_[:, 256 * u:256 * u + 128], WI1n[:], start=False, stop=False)
                        k.mm(gb, ya[:, 256 * u + 128:256 * (u + 1)], cb16['WI2'][:], start=False, stop=False)
                        k.mm(gb, yb_[:, 256 * u + 128:256 * (u + 1)], cb16['WI2'][:], start=False, stop=True)

                def st_itw(o, q):
                    def f(i):
                        bank = ps[4 + i % 2][:, :]
                        g1o = Gbuf[:, :, 8 * q:8 * q + 8, :].rearrange("p r (u c) s -> p r u (c s)", u=2)
                        g2o = G2[:, :, 8 * q:8 * q + 8, :].rearrange("p r (u c) s -> p r u (c s)", u=2)
                        k.tt('dve', g1o, v4(bank).rearrange("p u r f -> p r u f"),
                             tab4(c32['TIa'][:]).rearrange("p u r f -> p r u f"), ALU.mult)
                        k.tt('dve', g2o, v4(bank)[:, :, ::-1, :].rearrange("p u r f -> p r u f"),
                             tab4(c32['TIb'][:]).rearrange("p u r f -> p r u f"), ALU.mult)
                    return f

                def st_inva(o, q):
                    def f(i):
                        if q % 2 == 1:
                            cc = q // 2
                            gate = tm[1] if o == 0 else tm[2]
                            yb = ps[6]
                            for n_, gsrc in enumerate((Gbuf, G2)):
                                k.mm(yb[:, :], cb16['FIr'][:], gsrc[:, 0, 16 * cc:16 * cc + 16, :].rearrange("p c s -> p (c s)"),
                                     start=(n_ == 0), stop=False)
                                k.mm(yb[:, :], cb16['FIi'][:], gsrc[:, 1, 16 * cc:16 * cc + 16, :].rearrange("p c s -> p (c s)"),
                                     start=False, stop=(n_ == 1))
                            cs = slice(c0 + 16 * cc, c0 + 16 * cc + 16)
                            if o == 0:
                                k.tt('dve', z2_tm[:, cs, :], yb[:, :].rearrange("p (c s) -> p c s", c=16), gate[:, cs, :], ALU.mult)
                            else:
                                k.tt('dve', y_sc[:, :, cs].rearrange("p s c -> p c s"),
                                     yb[:, :].rearrange("p (c s) -> p c s", c=16), gate[:, cs, :], ALU.mult)
                    return f

                items = []
                for q in range(16):
                    items.append([st_za(spec_lhs, q), st_tw, st_ub, st_kev(q)])
                for o in range(DBG.get('nord', 2)):
                    src = tm[0] if o == 0 else z2_tm
                    if o == 1:
                        items += [[] for _ in range(DBG.get('gap', 0))]
                    for q in range(8):
                        items.append([st_za(conv_lhs_of(src), q), st_tw, st_ub, st_mul(o, q), st_gb, st_itw(o, q), st_inva(o, q)])
                hook = None
                if hbk == 0 and DBG.get('pref', 1):
                    pref[0] = filt_gen(cb, 1, ps[7])

                    def hook(t, g=pref[0]):
                        if t >= 19:
                            next(g, None)
                            next(g, None)
                run_skewed(items, hook)
            if DBG.get('dump_z2'):
                k.dma(T['dbg_tm'][:, :], z2_tm[:].rearrange("p c s -> p (c s)"))
            if DBG.get('s2b', 9) < 6: continue
            for a in range(4):
                pv = ps[2 + a % 2][:, :].bitcast(BF)
                for e in range(8):
                    k.tr(pv[:, 128 * e:128 * (e + 1)], y_sc[:, 8 * a + e, :], ident[:])
                k.tt('dve', yzb[:].rearrange("c (p s) -> c p s", s=32)[:, :, 8 * a:8 * a + 8],
                     pv.rearrange("c (s p) -> c p s", s=8),
                     sz[:].rearrange("c (p s) -> c p s", s=32)[:, :, 8 * a:8 * a + 8], ALU.mult)
            for tb in range(NB):
                k.dma(T['yz_d'][tb][:, 512 * cb:512 * (cb + 1)], yzb[:, 512 * tb:512 * (tb + 1)])


def phase2(nc, T):
    if 'b' in DBG.get('p2', 'ab'):
        phase2b(nc, T)


def _bf(a):
    return np.asarray(a, np.float32).astype(ml_dtypes.bfloat16)


_CONST_CACHE = {}


def host_constants():
    if _CONST_CACHE:
        return _CONST_CACHE
    C = {}
    pos = np.arange(L, dtype=np.float32)
    inv_freq = (np.float32(10000.0) ** (-np.arange(0, 32, 2, dtype=np.float32) / np.float32(32))).astype(np.float32)
    ang = (pos[:, None] * inv_freq[None, :]).astype(np.float32)
    C['cosT'] = np.ascontiguousarray(np.cos(ang).astype(np.float32).reshape(NT, 128, 16).transpose(1, 0, 2).reshape(128, NT * 16))
    C['sinT'] = np.ascontiguousarray(np.sin(ang).astype(np.float32).reshape(NT, 128, 16).transpose(1, 0, 2).reshape(128, NT * 16))
    F = fft_constants()
    for nm in ('FA1', 'FA2', 'WBr', 'WBi', 'WBni', 'WI1', 'WI2', 'FIr', 'FIi'):
        C[nm] = np.ascontiguousarray(_bf(F[nm]))
    for nm in ('TWa', 'TWb', 'TIa', 'TIb'):
        C[nm] = np.ascontiguousarray(F[nm].astype(np.float32))
    C.update(filter_constants())
    _CONST_CACHE.update(C)
    return _CONST_CACHE


def prep_inputs(inp, b):
    f32 = np.float32
    m = {}
    m['x'] = np.ascontiguousarray(inp['x'][b], dtype=f32)
    m['w_in'] = np.ascontiguousarray(inp['w_in'][0], dtype=f32)
    m['gT'] = np.ascontiguousarray(inp['g_norm'][0].reshape(8, 128).T, dtype=f32)
    m['bgT'] = np.ascontiguousarray(inp['b_gate'][0].reshape(16, 128).T, dtype=f32)
    m['w_uq'] = np.ascontiguousarray(inp['w_uq'][0], dtype=f32)
    m['w_ukv'] = np.ascontiguousarray(inp['w_ukv'][0], dtype=f32)
    m['gcqT'] = np.ascontiguousarray(inp['g_cq'][0].reshape(3, 128).T, dtype=f32)
    m['gckvT'] = np.ascontiguousarray(inp['g_ckv'][0].reshape(2, 128).T, dtype=f32)
    m['gqk'] = np.ascontiguousarray(np.concatenate([inp['g_qn'][0], inp['g_kn'][0]])[None, :], dtype=f32)
    m['w_attn_out'] = np.ascontiguousarray(inp['w_attn_out'][0], dtype=f32)
    m['w_hy_out'] = np.ascontiguousarray(inp['w_hy_out'][0], dtype=f32)
    m['w_out'] = np.ascontiguousarray(inp['w_out'][0], dtype=f32)
    wsh = np.zeros((128, 12, 4), f32)
    wsh[:, :, 0:3] = inp['w_short'][0].reshape(3, 12, 128).transpose(2, 1, 0)
    wsh[:, :, 3] = inp['b_short'][0].reshape(12, 128).T
    m['wsh'] = wsh.reshape(128, 48)
    hb = inp['hy_bias'][0]
    bT = hb.reshape(2, 128, 4).transpose(2, 0, 1)
    m['biasT'] = np.ascontiguousarray(np.repeat(bT[:, None], 32, axis=1).reshape(128, 256), dtype=f32)
    W1 = np.zeros((128, 128), f32)
    W1[0:33, 0:64] = inp['w_f1'][0]
    W1[64:97, 64:128] = inp['w_f1'][0]
    m['W1blk'] = W1
    W2 = np.zeros((128, 128), f32)
    W2[0:64, 0:64] = inp['w_f2'][0]
    W2[64:128, 64:128] = inp['w_f2'][0]
    m['W2blk'] = W2
    mv = np.zeros((128, 4), f32)
    for jj, nm in enumerate(('freq_1', 'b_f1', 'freq_2', 'b_f2')):
        mv[0:64, jj] = inp[nm][0]
        mv[64:128, jj] = inp[nm][0]
    m['mlpv'] = mv
    w3 = inp['w_f3'][0].reshape(64, 2, 2, 8, 64)
    W3 = np.zeros((128, 8, 2, 2, 64), f32)
    for dd in range(2):
        W3[64 * dd:64 * (dd + 1), :, :, dd, :] = w3[:, :, dd, :, :].transpose(0, 2, 1, 3)
    m['W3blk'] = W3.reshape(128, 2048)
    C = host_constants()
    for nm in CONST_NAMES:
        m[nm] = C[nm]
    return m


IN_SHAPES = {
    'x': ([L, D], F32), 'w_in': ([D, 5280], F32), 'gT': ([128, 8], F32), 'bgT': ([128, 16], F32),
    'w_uq': ([384, 768], F32), 'w_ukv': ([256, 1024], F32), 'gcqT': ([128, 3], F32), 'gckvT': ([128, 2], F32),
    'gqk': ([1, 192], F32), 'w_attn_out': ([512, D], F32), 'w_hy_out': ([512, D], F32), 'w_out': ([D, D], F32),
    'cosT': ([128, NT * 16], F32), 'sinT': ([128, NT * 16], F32),
    'wsh': ([128, 48], F32), 'biasT': ([128, 256], F32), 'W1blk': ([128, 128], F32), 'W2blk': ([128, 128], F32),
    'mlpv': ([128, 4], F32), 'W3blk': ([128, 2048], F32),
    'FA1': ([128, 256], BF), 'FA2': ([128, 256], BF), 'WBr': ([128, 128], BF), 'WBi': ([128, 128], BF),
    'WBni': ([128, 128], BF), 'WI1': ([128, 256], BF), 'WI2': ([128, 256], BF), 'FIr': ([128, 128], BF),
    'FIi': ([128, 128], BF), 'TWa': ([128, 256], F32), 'TWb': ([128, 256], F32), 'TIa': ([128, 256], F32),
    'TIb': ([128, 256], F32), 'zs_hi': ([128, L], BF), 'zs_lo': ([128, L], BF), 'tfull': ([128, 64], F32),
    'negd': ([1, HYW], F32),
}
CONST_NAMES = ('cosT', 'sinT', 'FA1', 'FA2', 'WBr', 'WBi', 'WBni', 'WI1', 'WI2', 'FIr', 'FIi', 'TWa', 'TWb', 'TIa', 'TIb',
               'zs_hi', 'zs_lo', 'tfull', 'negd')


def build_nc(debug=None):
    debug = debug or set()
    nc = bass.Bass("TRN2", target_bir_lowering=False)
    T = {}
    for name, (shape, dt) in IN_SHAPES.items():
        T[name] = nc.dram_tensor(name, shape, dt, kind="ExternalInput").ap()
    T['out'] = nc.dram_tensor("out", [L, D], F32, kind="ExternalOutput").ap()
    skind = dict(kind="ExternalOutput") if 'dump' in debug else {}
    T['hT_d'] = nc.dram_tensor("hT_d", [NB, 128, 8 * 512], BF, **skind).ap()
    T['at_d'] = nc.dram_tensor("at_d", [NB, 64, 8 * 512], BF, **skind).ap()
    T['h2_d'] = nc.dram_tensor("h2_d", [128, L], BF, **skind).ap()
    if 'dump' in debug:
        T['dbg_tm'] = nc.dram_tensor("dbg_tm", [128, 4096], BF, kind="ExternalOutput").ap()
        T['dbg_k'] = nc.dram_tensor("dbg_k", [128, 8192], BF, kind="ExternalOutput").ap()
        T['dbg_ks'] = nc.dram_tensor("dbg_ks", [128, 8192], BF, kind="ExternalOutput").ap()
    if 'yz_in' in debug:
        T['yz_d'] = nc.dram_tensor("yz_d", [NB, 128, 4 * 512], BF, kind="ExternalInput").ap()
    else:
        T['yz_d'] = nc.dram_tensor("yz_d", [NB, 128, 4 * 512], BF, **skind).ap()
    phases = debug & {'p1', 'p2', 'p3', 'p4'} or {'p1', 'p2', 'p3', 'p4'}
    do_p2 = 'p2' in phases and 'yz_in' not in debug
    if 'p1' in phases or do_p2:
        phase12(nc, T, with_p1=('p1' in phases), with_p2a=do_p2)
    if do_p2:
        phase2(nc, T)
    if 'p3' in phases:
        phase3(nc, T)
    if 'p4' in phases:
        phase4(nc, T)
    return nc


def kernel(**inputs):
    inp = {k_: np.asarray(v) for k_, v in inputs.items()}
    nc = build_nc()
    in_maps = [prep_inputs(inp, b) for b in range(8)]
    res = run_bass_kernel_spmd(nc, in_maps, core_ids=list(range(8)))
    out = np.stack([np.asarray(r['out'], dtype=np.float32) for r in res.results], axis=0)
    return out
```

```python
import concourse.bass as bass
import concourse.mybir as mybir

_ESZ = {}


def _esize(dt):
    s = _ESZ.get(dt)
    if s is None:
        n = str(dt)
        if '32' in n:
            s = 4
        elif '16' in n:
            s = 2
        elif '8' in n:
            s = 1
        else:
            s = 4
        _ESZ[dt] = s
    return s


def footprint(ap):
    t = ap.tensor
    name = t.name
    es = _esize(ap.dtype)
    apl = ap.ap
    off = int(ap.offset) * es
    space = str(type(t).__name__)
    if 'DRam' in space:
        lo = off
        hi = off
        for st, cnt in apl:
            if cnt > 1:
                d = (cnt - 1) * st * es
                if d > 0:
                    hi += d
                else:
                    lo += d
        return (name, 0, 1, lo, hi + es)
    pstep, pcnt = apl[0]
    pstep_b = pstep * es
    if pstep_b > 0:
        p0 = off // pstep_b
        f0 = off % pstep_b
    else:
        p0 = 0
        f0 = off
    lo = f0
    hi = f0
    for st, cnt in apl[1:]:
        if cnt > 1:
            d = (cnt - 1) * st * es
            if d > 0:
                hi += d
            else:
                lo += d
    return (name, p0, p0 + pcnt, lo, hi + es)


COMPUTE = ('pe', 'act', 'dve', 'pool')
QUEUES = ('pe', 'act', 'dve', 'pool', 'sp')
QIDX = {q: i for i, q in enumerate(QUEUES)}


class _Op:
    __slots__ = ('q', 'fn', 'dma', 'idx', 'gid', 'waits_c', 'waits_d', 'signal', 'snap', 'slot', 'slot_cnt', 'prev_slot')


class Prog:
    def __init__(self, nc, dma_slots=None):
        self.nc = nc
        self.streams = {q: [] for q in QUEUES}
        self.recs = {}
        self.known = {q: [-1] * len(QUEUES) for q in QUEUES}
        self.known_dma = {q: set() for q in QUEUES}
        self.ops = []
        self.dma_slots = dma_slots or {'sp': 8, 'pool': 4, 'act': 4}
        self.dma_count = {q: 0 for q in QUEUES}
        self.dma_ops = {q: [] for q in QUEUES}
        self.n_comp = {q: 0 for q in QUEUES}

    def add(self, q, fn, reads=(), writes=(), dma=False):
        op = _Op()
        op.q = q
        op.fn = fn
        op.dma = dma
        op.gid = len(self.ops)
        op.signal = dma
        op.waits_c = []
        op.waits_d = []
        op.slot = None
        op.prev_slot = None
        stream = self.streams[q]
        if not dma:
            op.idx = self.n_comp[q]
            self.n_comp[q] += 1
        else:
            op.idx = -1
        deps_c = {}
        deps_d = set()

        def scan(fp, is_write):
            name, p0, p1, f0, f1 = fp
            lst = self.recs.get(name)
            if not lst:
                return
            for r in lst:
                (rp0, rp1, rf0, rf1, rw, rop) = r
                if not (is_write or rw):
                    continue
                if rp1 <= p0 or p1 <= rp0 or rf1 <= f0 or f1 <= rf0:
                    continue
                if rop.dma:
                    deps_d.add(rop)
                else:
                    e = rop.q
                    if deps_c.get(e, -1) < rop.idx:
                        deps_c[e] = rop.idx

        rfps = [footprint(a) for a in reads]
        wfps = [footprint(a) for a in writes]
        for fp in rfps:
            scan(fp, False)
        for fp in wfps:
            scan(fp, True)
        known = self.known[q]
        kd = self.known_dma[q]
        for e, i in deps_c.items():
            ei = QIDX[e]
            if e == q and not dma:
                if q == 'pe':
                    continue
            if i <= known[ei]:
                continue
            op.waits_c.append((e, i))
            src = self.comp_ops[e][i]
            src.signal = True
            known[ei] = i
            for k, v in enumerate(src.snap):
                if v > known[k]:
                    known[k] = v
        for d in sorted(deps_d, key=lambda o: o.gid):
            if d.gid in kd:
                continue
            op.waits_d.append(d)
            kd.add(d.gid)
            for k, v in enumerate(d.snap):
                if v > known[k]:
                    known[k] = v
        if dma:
            n = self.dma_count[q]
            R = self.dma_slots[q]
            op.slot = n % R
            op.slot_cnt = n // R + 1
            if n >= R:
                prev = self.dma_ops[q][n - R]
                op.prev_slot = prev
                kd.add(prev.gid)
            self.dma_count[q] = n + 1
            self.dma_ops[q].append(op)
        op.snap = tuple(known)
        if not dma:
            self.comp_ops[q].append(op)
        for fp, is_write in [(f, False) for f in rfps] + [(f, True) for f in wfps]:
            name, p0, p1, f0, f1 = fp
            lst = self.recs.setdefault(name, [])
            if is_write:
                lst[:] = [r for r in lst if not (r[0] >= p0 and r[1] <= p1 and r[2] >= f0 and r[3] <= f1)]
            else:
                if not dma:
                    lst[:] = [r for r in lst if not (r[4] is False and (not r[5].dma) and r[5].q == q
                                                     and r[0] == p0 and r[1] == p1 and r[2] == f0 and r[3] == f1)]
            lst.append((p0, p1, f0, f1, is_write, op))
        stream.append(op)
        self.ops.append(op)
        return op

    comp_ops = None

    def start(self):
        self.comp_ops = {q: [] for q in QUEUES}

    def pe(self, fn, reads, writes):
        return self.add('pe', fn, reads, writes)

    def act(self, fn, reads, writes):
        return self.add('act', fn, reads, writes)

    def dve(self, fn, reads, writes):
        return self.add('dve', fn, reads, writes)

    def pool(self, fn, reads, writes):
        return self.add('pool', fn, reads, writes)

    def dma(self, out, in_, q='sp', **kw):
        return self.add(q, lambda e: e.dma_start(out=out, in_=in_, **kw), [in_], [out], dma=True)

    def emit(self, block, sems_c, sems_d):
        cum = {}
        for e in QUEUES:
            c = 0
            arr = []
            for o in self.comp_ops[e]:
                if o.signal:
                    c += 1
                arr.append(c)
            cum[e] = arr
        self.cum = cum

        def gen(q):
            def body(eng):
                for o in self.streams[q]:
                    for (e, i) in o.waits_c:
                        eng.wait_ge(sems_c[e], cum[e][i])
                    for d in o.waits_d:
                        eng.wait_ge(sems_d[d.q][d.slot], 16 * d.slot_cnt)
                    if o.prev_slot is not None:
                        p = o.prev_slot
                        eng.wait_ge(sems_d[p.q][p.slot], 16 * p.slot_cnt)
                    ins = o.fn(eng)
                    if o.dma:
                        ins.then_inc(sems_d[q][o.slot], 16)
                    elif o.signal:
                        ins.then_inc(sems_c[q], 1)
                R = self.dma_slots.get(q, 0)
                n = self.dma_count[q]
                for o in self.dma_ops[q][max(0, n - R):]:
                    eng.wait_ge(sems_d[q][o.slot], 16 * o.slot_cnt)
            return body

        if self.streams['pe']:
            block.tensor(gen('pe'))
        if self.streams['act']:
            block.scalar(gen('act'))
        if self.streams['dve']:
            block.vector(gen('dve'))
        if self.streams['pool']:
            block.gpsimd(gen('pool'))
        if self.streams['sp']:
            block.sync(gen('sp'))

import math
from contextlib import ExitStack
import numpy as np
import ml_dtypes
from concourse.bass_utils import run_bass_kernel_spmd

F32 = mybir.dt.float32
BF = mybir.dt.bfloat16
AF = mybir.ActivationFunctionType
ALU = mybir.AluOpType
AX = mybir.AxisListType

L = 4096
D = 1024
NT = 32
NB = 8
EPS = 1e-6
NF = 8192
HYW = 512
COL_V, COL_X1, COL_X2, COL_ZH = 0, 512, 1024, 1536
COL_CQ, COL_CKV, COL_KR, COL_ZA = 2048, 2432, 2688, 2720
COL_GH, COL_GA = 3232, 4256
MAGIC = 12582912.0
DBG = {}


def bcast(ap, axis, n):
    a = ap.unsqueeze(axis)
    shp = list(a.shape)
    shp[axis] = n
    return a.to_broadcast(shp)


class K:
    def __init__(self, P):
        self.P = P

    def mm(self, out, lhsT, rhs, start=True, stop=True):
        self.P.pe(lambda e: e.matmul(out, lhsT=lhsT, rhs=rhs, start=start, stop=stop), [lhsT, rhs], [out])

    def tr(self, out, in_, ident):
        self.P.pe(lambda e: e.transpose(out=out, in_=in_, identity=ident), [in_, ident], [out])

    def act(self, out, in_, func, bias=None, scale=None, accum_out=None):
        kw = {}
        reads = [in_]
        writes = [out]
        if bias is not None:
            kw['bias'] = bias
            if not isinstance(bias, (int, float)):
                reads.append(bias)
        if scale is not None:
            kw['scale'] = scale
            if not isinstance(scale, (int, float)):
                reads.append(scale)
        if accum_out is not None:
            kw['accum_out'] = accum_out
            writes.append(accum_out)
        self.P.act(lambda e: e.activation(out=out, in_=in_, func=func, **kw), reads, writes)

    def tt(self, eng, out, in0, in1, op):
        self.P.add(eng, lambda e: e.tensor_tensor(out=out, in0=in0, in1=in1, op=op), [in0, in1], [out])

    def ts(self, eng, out, in0, s1, s2=None, op0=ALU.mult, op1=None):
        reads = [in0]
        if not isinstance(s1, (int, float)):
            reads.append(s1)
        if s2 is not None and not isinstance(s2, (int, float)):
            reads.append(s2)
        if op1 is None:
            self.P.add(eng, lambda e: e.tensor_scalar(out=out, in0=in0, scalar1=s1, scalar2=None, op0=op0), reads, [out])
        else:
            self.P.add(eng, lambda e: e.tensor_scalar(out=out, in0=in0, scalar1=s1, scalar2=s2, op0=op0, op1=op1), reads, [out])

    def stt(self, out, in0, scalar, in1, op0, op1):
        reads = [in0, in1]
        if not isinstance(scalar, (int, float)):
            reads.append(scalar)
        self.P.dve(lambda e: e.scalar_tensor_tensor(out=out, in0=in0, scalar=scalar, in1=in1, op0=op0, op1=op1), reads, [out])

    def copy(self, eng, out, in_):
        if eng == 'act':
            self.act(out, in_, AF.Copy)
        else:
            self.P.add(eng, lambda e: e.tensor_copy(out=out, in_=in_), [in_], [out])

    def recip(self, out, in_):
        self.P.dve(lambda e: e.reciprocal(out=out, in_=in_), [in_], [out])

    def reduce_add(self, out, in_):
        self.P.dve(lambda e: e.tensor_reduce(out=out, in_=in_, axis=AX.X, op=ALU.add), [in_], [out])

    def memset(self, eng, ap, val):
        self.P.add(eng, lambda e: e.memset(ap, val), [], [ap])

    def dma(self, out, in_, q='sp'):
        self.P.dma(out, in_, q=q)


def _dump(P):
    cum = {}
    for e in QUEUES:
        c = 0
        arr = []
        for o in P.comp_ops[e]:
            if o.signal:
                c += 1
            arr.append(c)
        cum[e] = arr
    for q in QUEUES:
        print("== stream", q)
        for o in P.streams[q]:
            w = [f"{e}>={cum[e][i]}(op{i})" for e, i in o.waits_c] + [f"dma[{d.q}{d.slot}]>={16*d.slot_cnt}" for d in o.waits_d]
            if o.prev_slot is not None:
                w.append(f"prev dma[{o.prev_slot.q}{o.prev_slot.slot}]>={16*o.prev_slot.slot_cnt}")
            tag = f"DMA slot{o.slot} cnt{o.slot_cnt}" if o.dma else (f"op{o.idx} sig={cum[q][o.idx] if o.signal else '-'}")
            print("   ", tag, getattr(o, 'desc', ''), "waits:", w)


class Phase:
    def __init__(self, nc, name):
        self.nc = nc
        self.name = name
        self.es = ExitStack()

    def __enter__(self):
        nc = self.nc
        es = self.es
        es.__enter__()
        self.ps = [es.enter_context(nc.psum_tensor(f"{self.name}_ps{i}", [128, 512], F32)) for i in range(8)]
        self.sems_c = {e: es.enter_context(nc.semaphore(f"{self.name}_sc_{e}")) for e in QUEUES}
        self.sems_d = {q: [es.enter_context(nc.semaphore(f"{self.name}_sd_{q}{i}")) for i in range(n)]
                       for q, n in (('sp', 8), ('pool', 4), ('act', 4))}
        self.P = Prog(nc)
        self.P.start()
        self.k = K(self.P)
        return self

    def sb(self, name, shape, dt):
        return self.es.enter_context(self.nc.sbuf_tensor(f"{self.name}_{name}", shape, dt))

    def __exit__(self, *a):
        if a[0] is None:
            self.es.enter_context(self.nc.allow_low_precision("bf16 operands / intermediates by design"))
            block = self.es.enter_context(self.nc.Block())
            if DBG.get('dump') == self.name:
                _dump(self.P)
            self.P.emit(block, self.sems_c, self.sems_d)
        return self.es.__exit__(*a)


def make_ident(ph, ident):
    identf = ph.sb("identf", [128, 128], F32)
    ph.k.memset('pool', identf[:], 0.0)
    ph.P.pool(lambda e: e.affine_select(out=identf[:], in_=identf[:], pattern=[[-1, 128]], compare_op=ALU.not_equal,
                                        fill=1.0, base=0, channel_multiplier=1), [identf[:]], [identf[:]])
    ph.k.copy('dve', ident[:], identf[:])


def load_w(ph, dst, src_ap):
    ph.k.dma(dst, src_ap, q='pool')


def rstd_from_ss(k, rs_col, ss_col, n):
    k.act(rs_col, ss_col, AF.Sqrt, bias=EPS, scale=1.0 / n)
    k.recip(rs_col, rs_col)


def phase1_gen(ph, T, banks):
    k = ph.k
    xt = [ph.sb(f"xt{i}", [128, D], F32) for i in range(3)]
    xn = [ph.sb(f"xn{i}", [128, D], BF) for i in range(2)]
    junk = ph.sb("junk", [128, D], BF)
    ss = ph.sb("ss", [128, NT], F32)
    rs = ph.sb("rs", [128, NT], F32)
    ts_ = ph.sb("ts_", [128, NT], F32)
    mh = ph.sb("mh", [128, 1], F32)
    gT = ph.sb("gT", [128, 8], F32)
    ident = ph.sb("ident", [128, 128], BF)
    hb = [ph.sb(f"hb{i}", [128, 8, 512], BF) for i in range(2)]
    make_ident(ph, ident)
    k.memset('pool', mh[:], -0.5)
    k.dma(gT[:], T['gT'][:, :])
    pend = [None]
    for i in range(NT):
        x_t = xt[i % 3]
        k.dma(x_t[:], T['x'][128 * i:128 * (i + 1), :])
        k.act(junk[:], x_t[:], AF.Square, accum_out=ss[:, i:i + 1])
        rstd_pool(k, rs[:, i:i + 1], ss[:, i:i + 1], D, mh[:, 0:1], ts_[:, i:i + 1])
        x_n = xn[i % 2]
        k.ts('dve', x_n[:], x_t[:], rs[:, i:i + 1])
        bank = banks[i % len(banks)]
        pv = bank[:, :].bitcast(BF)
        for c in range(8):
            k.tr(pv[:, 128 * c:128 * (c + 1)], x_n[:, 128 * c:128 * (c + 1)], ident[:])
        if pend[0] is not None:
            pend[0]()

        def evac(i=i, pv=pv):
            h_b = hb[(i // 4) % 2]
            j = i % 4
            k.tt('dve', h_b[:, :, 128 * j:128 * (j + 1)], pv.rearrange("p (c t) -> p c t", c=8),
                 bcast(gT[:, :], 2, 128), ALU.mult)
            if j == 3:
                k.dma(T['hT_d'][i // 4], h_b[:].rearrange("p c t -> p (c t)"))
        pend[0] = evac
        yield
    pend[0]()
    yield


def rstd_pool(k, rs, ss, n, mhalf, tmp):
    k.ts('dve', tmp, ss, 1.0 / n, EPS, op0=ALU.mult, op1=ALU.add)
    k.tt('pool', rs, tmp, mhalf, ALU.pow)


def phase12(nc, T, with_p1=True, with_p2a=True):
    with Phase(nc, "p12") as ph:
        g1 = phase1_gen(ph, T, ph.ps[0:4]) if with_p1 else iter(())
        g2 = phase2a_gen(ph, T, ph.ps[4:8]) if with_p2a else iter(())
        alive1, alive2 = True, True
        while alive1 or alive2:
            if alive1:
                try:
                    next(g1)
                except StopIteration:
                    alive1 = False
            for _ in range(4):
                if alive2:
                    try:
                        next(g2)
                    except StopIteration:
                        alive2 = False


def run_interleaved(gens, width=2):
    active = []
    gens = list(gens)
    while gens or active:
        while gens and len(active) < width:
            active.append(gens.pop(0))
        for g in list(active):
            try:
                next(g)
            except StopIteration:
                active.remove(g)


def qk_norm_rope(ph, W, src, dst, g_rep, cs_t, sc_t, mhalf, use_act=False):
    k = ph.k
    sq, ssq, rk, ta, tb_, tmp = W['sq'], W['ssq'], W['rk'], W['ta'], W['tb'], W['tmp']
    if use_act:
        k.act(sq[:].rearrange("p a b -> p (a b)"), src[:].rearrange("p a b -> p (a b)"), AF.Square)
    else:
        k.tt('pool', sq[:], src[:], src[:], ALU.mult)
    k.reduce_add(ssq[:], sq[:])
    rstd_pool(k, rk[:], ssq[:], 96, mhalf[:, 0:8], tmp[:])
    yield
    k.tt('dve', src[:], src[:], bcast(rk[:, :], 2, 96), ALU.mult)
    k.tt('dve', src[:], src[:], bcast(g_rep, 1, 8), ALU.mult)
    yield
    t1 = bcast(src[:, :, 64:80], 1, 2)
    t2 = bcast(src[:, :, 80:96], 1, 2)
    k.tt('pool', ta[:], t1, bcast(cs_t, 2, 8), ALU.mult)
    k.tt('pool', tb_[:], t2, bcast(sc_t, 2, 8), ALU.mult)
    k.tt('dve', dst[:, :, 64:80], ta[:, 0], tb_[:, 0], ALU.subtract)
    k.tt('dve', dst[:, :, 80:96], ta[:, 1], tb_[:, 1], ALU.add)
    k.copy('pool', dst[:, :, 0:64], src[:, :, 0:64])
    yield


def phase3(nc, T):
    with Phase(nc, "p3") as ph:
        k = ph.k
        ps = ph.ps
        ident = ph.sb("ident", [128, 128], BF)
        make_ident(ph, ident)
        w_in_v = T['w_in'].rearrange("(k p) n -> p k n", p=128)
        Wkv = ph.sb("Wkv", [128, 8, 288], BF)
        Wq = ph.sb("Wq", [128, 8, 384], BF)
        Wuq = ph.sb("Wuq", [128, 3, 768], BF)
        Wukv = ph.sb("Wukv", [128, 2, 1024], BF)
        gcq = ph.sb("gcq", [128, 3], F32)
        gckv = ph.sb("gckv", [128, 2], F32)
        gqk = ph.sb("gqk", [128, 192], F32)
        csT = ph.sb("csT", [128, NT, 2, 16], F32)
        mhalf = ph.sb("mhalf", [128, 8], F32)
        k.memset('pool', mhalf[:], -0.5)
        load_w(ph, Wkv[:], w_in_v[:, :, COL_CKV:COL_CKV + 288])
        load_w(ph, Wq[:], w_in_v[:, :, COL_CQ:COL_CQ + 384])
        load_w(ph, Wuq[:], T['w_uq'].rearrange("(k p) n -> p k n", p=128))
        load_w(ph, Wukv[:], T['w_ukv'].rearrange("(k p) n -> p k n", p=128))
        k.dma(gcq[:], T['gcqT'][:, :])
        k.dma(gckv[:], T['gckvT'][:, :])
        k.dma(gqk[:], T['gqk'][0:1, :].partition_broadcast(128))
        cosv = T['cosT'].rearrange("p (a b) -> p a b", b=16)
        sinv = T['sinT'].rearrange("p (a b) -> p a b", b=16)
        k.dma(csT[:, :, 0, :], cosv)
        k.dma(csT[:, :, 1, :], sinv)
        k.tt('pool', Wuq[:], Wuq[:], bcast(gcq[:, :], 2, 768), ALU.mult)
        k.tt('pool', Wukv[:], Wukv[:], bcast(gckv[:, :], 2, 1024), ALU.mult)

        kT = ph.sb("kT", [128, 8, L], BF)
        vx = ph.sb("vx", [128, NT, 8, 65], BF)
        k.memset('pool', vx[:, :, :, 64:65], 1.0)
        ones = ph.sb("ones", [128, 64], BF)
        k.memset('pool', ones[:], 1.0)
        hbuf = [ph.sb(f"hbuf{i}", [128, 8, 512], BF) for i in range(2)]
        junk = [ph.sb(f"junk{i}", [128, 384], BF) for i in range(3)]
        ssl = ph.sb("ssl", [128, 2 * NT], F32)
        rsl = ph.sb("rsl", [128, 2 * NT], F32)
        tsl = ph.sb("tsl", [128, 2 * NT], F32)
        latn = [ph.sb(f"latn{i}", [128, 384], BF) for i in range(3)]
        latT = [ph.sb(f"latT{i}", [128, 3, 128], BF) for i in range(3)]
        kr = [ph.sb(f"kr{i}", [128, 32], F32) for i in range(3)]
        qk32 = [ph.sb(f"qk32_{i}", [128, 8, 96], F32) for i in range(3)]
        qkbf = [ph.sb(f"qkbf{i}", [128, 8, 96], BF) for i in range(3)]
        Wk_ = [dict(sq=ph.sb(f"sq{i}", [128, 8, 96], F32), ssq=ph.sb(f"ssq{i}", [128, 8], F32),
                    rk=ph.sb(f"rk{i}", [128, 8], F32), tmp=ph.sb(f"tmpn{i}", [128, 8], F32),
                    ta=ph.sb(f"ta{i}", [128, 2, 8, 16], F32), tb=ph.sb(f"tb{i}", [128, 2, 8, 16], F32)) for i in range(3)]
        qT = [ph.sb(f"qT{i}", [128, 8, 512], BF) for i in range(2)]
        pt = [ph.sb(f"pt{i}", [128, 512], BF) for i in range(3)]
        rsum = [ph.sb(f"rsum{i}", [128, 512], BF) for i in range(2)]
        bcs = [ph.sb(f"bcs{i}", [64, 512], BF) for i in range(2)]
        at = [ph.sb(f"at{i}", [64, 8, 512], BF) for i in range(1)]

        def sumsq(i, col, src_ps, n):
            k.act(junk[i % 3][:, 0:n], src_ps, AF.Square, accum_out=ssl[:, col:col + 1])
            rstd_pool(k, rsl[:, col:col + 1], ssl[:, col:col + 1], n, mhalf[:, 0:1], tsl[:, col:col + 1])

        def kv_tile(i, hb, j):
            par = i % 3
            if j == 0:
                k.dma(hb[:].rearrange("p c t -> p (c t)"), T['hT_d'][i // 4])
            lat = ps[par][:, 0:288]
            for c in range(8):
                k.mm(lat, hb[:, c, 128 * j:128 * (j + 1)], Wkv[:, c, :], start=(c == 0), stop=(c == 7))
            sumsq(i, i, lat[:, 0:256], 256)
            yield
            ln = latn[par]
            k.act(ln[:, 0:256], lat[:, 0:256], AF.Copy, scale=rsl[:, i:i + 1])
            k.copy('act', kr[par][:], lat[:, 256:288])
            tbank = ps[7][:, :].bitcast(BF)
            for c in range(2):
                k.tr(tbank[:, 128 * c:128 * (c + 1)], ln[:, 128 * c:128 * (c + 1)], ident[:])
            lT = latT[par]
            k.copy('dve', lT[:, 0:2, :].rearrange("p c t -> p (c t)"), tbank[:, 0:256])
            yield
            kvb = [ps[3 + 2 * (i % 2)], ps[4 + 2 * (i % 2)]]
            for half in range(2):
                for c in range(2):
                    k.mm(kvb[half][:, :], lT[:, c, :], Wukv[:, c, 512 * half:512 * (half + 1)],
                         start=(c == 0), stop=(c == 1))
            kk = qk32[par]
            for half in range(2):
                kvv = kvb[half][:, :].rearrange("p (h e) -> p h e", h=4)
                k.copy('dve', kk[:, 4 * half:4 * half + 4, 0:64], kvv[:, :, 0:64])
                k.copy('dve', vx[:, i, 4 * half:4 * half + 4, 0:64], kvv[:, :, 64:128])
            k.copy('pool', kk[:, :, 64:96], bcast(kr[par][:, :], 1, 8))
            yield
            kf = qkbf[par]
            yield from qk_norm_rope(ph, Wk_[par], kk, kf, gqk[:, 96:192], csT[:, i, :, :], csT[:, i, ::-1, :], mhalf, use_act=True)
            kbank = ps[7][:, :].bitcast(BF)
            for h in range(8):
                k.tr(kbank[0:96, 128 * h:128 * (h + 1)], kf[:, h, :], ident[:])
            k.copy('dve', kT[0:96, :, 128 * i:128 * (i + 1)], kbank[0:96, :].rearrange("p (h t) -> p h t", h=8))
            yield

        def q_tile(i, hb, j, q_T, bk=None):
            par = i % 2
            bA, bB = bk if bk is not None else (ps[6], ps[7])
            lat = bA[:, 0:384]
            for c in range(8):
                k.mm(lat, hb[:, c, 128 * j:128 * (j + 1)], Wq[:, c, :], start=(c == 0), stop=(c == 7))
            yield
            lsb = Wk_[par]['sq'][:].rearrange("p a b -> p (a b)")[:, 0:384]
            k.copy('dve', lsb, lat)
            k.P.dve(lambda e: e.scalar_tensor_tensor(out=junk[par][:, 0:384], in0=lsb, scalar=1.0, in1=lsb, op0=ALU.mult,
                                                     op1=ALU.mult, accum_out=ssl[:, NT + i:NT + i + 1]),
                    [lsb], [junk[par][:, 0:384], ssl[:, NT + i:NT + i + 1]])
            rstd_pool(k, rsl[:, NT + i:NT + i + 1], ssl[:, NT + i:NT + i + 1], 384, mhalf[:, 0:1], tsl[:, NT + i:NT + i + 1])
            yield
            ln = latn[par]
            k.ts('dve', ln[:, 0:384], lsb, rsl[:, NT + i:NT + i + 1])
            yield
            tbank = bB[:, :].bitcast(BF)
            for c in range(3):
                k.tr(tbank[:, 128 * c:128 * (c + 1)], ln[:, 128 * c:128 * (c + 1)], ident[:])
            yield
            lT = latT[par]
            k.copy('dve', lT[:].rearrange("p c t -> p (c t)"), tbank[:, 0:384])
            yield
            qq = qk32[par]
            for half in range(2):
                qb = bA[:, 0:384]
                for c in range(3):
                    k.mm(qb, lT[:, c, :], Wuq[:, c, 384 * half:384 * (half + 1)], start=(c == 0), stop=(c == 2))
                yield
                k.copy('dve', qq[:, 4 * half:4 * half + 4, :], qb.rearrange("p (h e) -> p h e", h=4))
                yield
            qf = qkbf[par]
            yield from qk_norm_rope(ph, Wk_[par], qq, qf, gqk[:, 0:96], csT[:, i, :, :], csT[:, i, ::-1, :], mhalf, use_act=(i < 4))
            yield
            yield
            yield
            yield
            qbank = bB[:, :].bitcast(BF)
            for h in range(8):
                k.tr(qbank[0:96, 128 * h:128 * (h + 1)], qf[:, h, :], ident[:])
            yield
            k.copy('dve', q_T[0:96, :, 128 * j:128 * (j + 1)], qbank[0:96, :].rearrange("p (h t) -> p h t", h=8))
            yield

        def kv_block(tb):
            hb = hbuf[tb % 2]
            return [kv_tile(tb * 4 + j, hb, j) for j in range(4)]

        def q_chunk_gens(qc, two_sets=False):
            hb = hbuf[qc % 2]
            k.dma(hb[:].rearrange("p c t -> p (c t)"), T['hT_d'][qc])
            bks = [(ps[6], ps[7]), (ps[0], ps[1])]
            return [q_tile(qc * 4 + j, hb, j, qT[qc % 2], bks[j % 2] if two_sets else None) for j in range(4)]

        gens = []
        for tb in range(DBG.get('nprep', NB)):
            gens += kv_block(tb)
        run_interleaved(gens, 3)
        nqc = DBG.get('nqc', NB)
        if nqc:
            run_interleaved(q_chunk_gens(0, two_sets=True), 2)

        scale = 1.0 / math.sqrt(96.0)
        NH = DBG.get('nh', 8)
        pend = [None]
        for qc in range(nqc):
            q_T = qT[qc % 2]
            a_t = at[0]
            nxt = q_chunk_gens(qc + 1) if qc + 1 < nqc else []
            nxt_active = []
            steps = [(h, kt) for h in range(NH) for kt in range(NT)]

            def S(idx):
                h, kt = steps[idx]
                k.mm(ps[idx % 3][:, :], kT[0:96, h, 128 * kt:128 * (kt + 1)], q_T[0:96, h, :])

            def fin_a(h):
                k.recip(rsum[h % 2][64:65, :], ps[3 + (h % 2)][64:65, :])

            def fin_b(h):
                k.mm(ps[5][0:64, :], ones[64:65, :], rsum[h % 2][64:65, :])

            def fin_c(h, a_t):
                k.copy('dve', bcs[h % 2][:], ps[5][0:64, :])
                k.tt('dve', a_t[:, h, :], ps[3 + (h % 2)][0:64, :], bcs[h % 2][:], ALU.mult)

            S(0)
            S(1)
            for idx, (h, kt) in enumerate(steps):
                p_t = pt[idx % 3]
                k.act(p_t[:], ps[idx % 3][:, :], AF.Exp, scale=scale)
                if idx + 2 < len(steps):
                    S(idx + 2)
                ob = ps[3 + (h % 2)]
                k.mm(ob[0:65, :], vx[:, kt, h, :], p_t[:], start=(kt == 0), stop=(kt == NT - 1))
                if pend[0] is not None:
                    if kt == 1:
                        pend[0][0]()
                    elif kt == 10:
                        pend[0][1]()
                    elif kt == 14:
                        pend[0][2]()
                        pend[0] = None
                if kt == NT - 1:
                    last = (h == NH - 1)
                    pend[0] = (lambda h=h: fin_a(h), lambda h=h: fin_b(h),
                               (lambda h=h, a_t=a_t, qc=qc, last=last, fin_c=fin_c: (fin_c(h, a_t), k.dma(T['at_d'][qc], a_t[:].rearrange("p h t -> p (h t)")) if last else None)))
                if idx % 3 == 2:
                    while nxt and len(nxt_active) < 1:
                        nxt_active.append(nxt.pop(0))
                    for g in list(nxt_active):
                        try:
                            next(g)
                        except StopIteration:
                            nxt_active.remove(g)
            run_interleaved(nxt_active + nxt, 1)

        if pend[0] is not None:
            pend[0][0]()
            pend[0][1]()
            pend[0][2]()


def phase4(nc, T):
    with Phase(nc, "p4") as ph:
        k = ph.k
        ps = ph.ps
        w_in_v = T['w_in'].rearrange("(k p) n -> p k n", p=128)
        Wz = ph.sb("Wz", [128, 8, 512], BF)
        Wg = ph.sb("Wg", [128, 8, 2048], BF)
        Wao = ph.sb("Wao", [128, 4, D], BF)
        Who = ph.sb("Who", [128, 4, D], BF)
        Wout = ph.sb("Wout", [128, 8, D], BF)
        bg = ph.sb("bg", [128, 16], F32)
        load_w(ph, Wz[:], w_in_v[:, :, COL_ZA:COL_ZA + 512])
        for q4 in range(4):
            load_w(ph, Wg[:, :, 512 * q4:512 * (q4 + 1)], w_in_v[:, :, COL_GH + 512 * q4:COL_GH + 512 * (q4 + 1)])
        load_w(ph, Wao[:], T['w_attn_out'].rearrange("(hp p) n -> p hp n", p=128))
        load_w(ph, Who[:], T['w_hy_out'].rearrange("(k p) n -> p k n", p=128))
        load_w(ph, Wout[:], T['w_out'].rearrange("(k p) n -> p k n", p=128))
        k.dma(bg[:], T['bgT'][:, :])
        hbuf = [ph.sb(f"hbuf{i}", [128, 8, 512], BF) for i in range(2)]
        atb = [ph.sb(f"atb{i}", [128, 4, 512], BF) for i in range(2)]
        yzb = [ph.sb(f"yzb{i}", [128, 4, 512], BF) for i in range(2)]
        xt = [ph.sb(f"xt{i}", [128, D], F32) for i in range(4)]
        ot = [ph.sb(f"ot{i}", [128, D], F32) for i in range(2)]
        sz = [ph.sb(f"sz{i}", [128, 512], BF) for i in range(2)]
        ya = ph.sb("ya", [128, 4, 512], BF)
        gh = [ph.sb(f"gh{i}", [128, 512], BF) for i in range(2)]
        ga = [ph.sb(f"ga{i}", [128, 512], BF) for i in range(2)]
        m1 = [ph.sb(f"m1{i}", [128, 512], F32) for i in range(2)]
        m2 = [ph.sb(f"m2{i}", [128, 512], F32) for i in range(2)]
        mg = [ph.sb(f"mg{i}", [128, 8, 512], BF) for i in range(2)]
        for tb in range(NB):
            hb = hbuf[tb % 2]
            a_b = atb[tb % 2]
            y_b = yzb[tb % 2]
            k.dma(hb[:].rearrange("p c t -> p (c t)"), T['hT_d'][tb])
            atv = T['at_d'][tb].rearrange("p (hp two t) -> p two hp t", two=2, t=512)
            k.dma(a_b[0:64, :, :], atv[:, 0, :, :])
            k.dma(a_b[64:128, :, :], atv[:, 1, :, :])
            k.dma(y_b[:].rearrange("p c t -> p (c t)"), T['yz_d'][tb])
            for j in range(4):
                k.dma(xt[j][:], T['x'][128 * (tb * 4 + j):128 * (tb * 4 + j + 1), :])
            for hp in range(4):
                zb = ps[hp % 2][:, :]
                for c in range(8):
                    k.mm(zb, Wz[:, c, 128 * hp:128 * (hp + 1)], hb[:, c, :], start=(c == 0), stop=(c == 7))
                s_z = sz[hp % 2]
                k.act(s_z[:], zb, AF.Silu)
                k.tt('pool', ya[:, hp, :], a_b[:, hp, :], s_z[:], ALU.mult)
            m_g = mg[tb % 2]
            for dc in range(8):
                g1 = ps[2 + (dc % 2)]
                g2 = ps[4 + (dc % 2)]
                for c in range(8):
                    k.mm(g1[:, :], Wg[:, c, 128 * dc:128 * (dc + 1)], hb[:, c, :], start=(c == 0), stop=(c == 7))
                for c in range(8):
                    k.mm(g2[:, :], Wg[:, c, 1024 + 128 * dc:1024 + 128 * (dc + 1)], hb[:, c, :],
                         start=(c == 0), stop=(c == 7))
                k.act(gh[dc % 2][:], g1[:, :], AF.Sigmoid, bias=bg[:, dc:dc + 1])
                k.act(ga[dc % 2][:], g2[:, :], AF.Sigmoid, bias=bg[:, 8 + dc:9 + dc])
                uh = ps[6]
                ua = ps[7]
                for c in range(4):
                    k.mm(uh[:, :], Who[:, c, 128 * dc:128 * (dc + 1)], y_b[:, c, :], start=(c == 0), stop=(c == 3))
                for hp in range(4):
                    k.mm(ua[:, :], Wao[:, hp, 128 * dc:128 * (dc + 1)], ya[:, hp, :], start=(hp == 0), stop=(hp == 3))
                k.tt('dve', m1[dc % 2][:], uh[:, :], gh[dc % 2][:], ALU.mult)
                k.tt('dve', m2[dc % 2][:], ua[:, :], ga[dc % 2][:], ALU.mult)
                k.tt('pool', m_g[:, dc, :], m1[dc % 2][:], m2[dc % 2][:], ALU.add)
            for j in range(4):
                i = tb * 4 + j
                x_t = xt[j]
                o_t = ot[i % 2]
                for half in range(2):
                    fb = ps[half]
                    for c in range(8):
                        k.mm(fb[:, :], m_g[:, c, 128 * j:128 * (j + 1)], Wout[:, c, 512 * half:512 * (half + 1)],
                             start=(c == 0), stop=(c == 7))
                    k.tt('dve', o_t[:, 512 * half:512 * (half + 1)], fb[:, :], x_t[:, 512 * half:512 * (half + 1)], ALU.add)
                k.dma(T['out'][128 * i:128 * (i + 1), :], o_t[:])


def fft_constants():
    C = {}
    n = NF
    s2 = np.arange(128, dtype=np.float64)[:, None]
    f2 = np.arange(128, dtype=np.float64)[None, :]
    th = 2 * np.pi * (f2 + 0.5) * s2 / 256.0
    C['FA1'] = np.concatenate([np.cos(th), -np.sin(th)], 1)
    th2 = 2 * np.pi * (f2 + 0.5) * (s2 + 128) / 256.0
    C['FA2'] = -np.concatenate([np.cos(th2), -np.sin(th2)], 1)
    s1 = np.arange(32, dtype=np.float64)
    tw = np.exp(-2j * np.pi * (np.arange(128)[None, :] + 0.5) * s1[:, None] / n)
    twq = np.tile(tw, (4, 1))
    C['TWa'] = np.concatenate([twq.real, twq.real], 1)
    C['TWb'] = np.concatenate([-twq.imag, twq.imag], 1)
    W = np.exp(-2j * np.pi * np.outer(s1, s1) / 32.0)
    Wq = np.kron(np.eye(4), W)
    C['WBr'] = Wq.real
    C['WBi'] = Wq.imag
    C['WBni'] = -Wq.imag
    Wi = np.exp(2j * np.pi * np.outer(s1, s1) / 32.0)
    Wiq = np.kron(np.eye(4), Wi)
    C['WI1'] = np.concatenate([Wiq.real, Wiq.imag], 1)
    C['WI2'] = np.concatenate([-Wiq.imag, Wiq.real], 1)
    twi = np.exp(2j * np.pi * (np.arange(128)[:, None] + 0.5) * s1[None, :] / n)
    twiq = np.tile(twi, (1, 4))
    C['TIa'] = np.concatenate([twiq.real, twiq.real], 1)
    C['TIb'] = np.concatenate([-twiq.imag, twiq.imag], 1)
    t2 = np.arange(128, dtype=np.float64)[None, :]
    f2c = np.arange(128, dtype=np.float64)[:, None]
    th3 = 2 * np.pi * (f2c + 0.5) * t2 / 256.0
    C['FIr'] = (2.0 / n) * np.cos(th3)
    C['FIi'] = -(2.0 / n) * np.sin(th3)
    return C


def filter_constants():
    C = {}
    f32 = np.float32
    t = np.linspace(0.0, 1.0, L, dtype=f32)[:, None]
    bands = 16
    f = np.linspace(1e-4, bands - 1, bands, dtype=f32)
    ang = (f32(2.0 * np.pi / L) * np.arange(L, dtype=f32)[:, None] * f[None, :]).astype(f32)
    z = np.concatenate([t, np.cos(ang).astype(f32), -np.sin(ang).astype(f32)], axis=-1).astype(f32)
    zs = np.zeros((128, L), f32)
    zs[0:33, :] = z.T
    zs[64:97, :] = z[::-1].T
    hi = zs.astype(ml_dtypes.bfloat16)
    lo = (zs - hi.astype(f32)).astype(ml_dtypes.bfloat16)
    C['zs_hi'] = hi
    C['zs_lo'] = lo
    tl = t[:, 0]
    tf = np.zeros((128, 2, 32), f32)
    pidx = np.arange(128)[:, None] * 32 + np.arange(32)[None, :]
    tf[:, 0, :] = tl[pidx]
    tf[:, 1, :] = tl[4095 - pidx]
    C['tfull'] = tf.reshape(128, 64)
    MIN_DECAY = math.log(1e-2) / 1.5
    MAX_DECAY = math.log(1e-2) / 0.3
    deltas = np.abs(np.linspace(MIN_DECAY, MAX_DECAY, HYW, dtype=f32)).astype(f32)
    C['negd'] = (-deltas)[None, :].astype(f32)
    return C


def _sin_layer(ph, W, pre_ps, fr, fb, out32):
    k = ph.k
    a, kk = W['a'], W['kk']
    k.ts('dve', a[:], pre_ps, fr, fb, op0=ALU.mult, op1=ALU.add)
    yield
    k.ts('dve', kk[:], a[:], 1.0 / (2 * math.pi), MAGIC, op0=ALU.mult, op1=ALU.add)
    yield
    k.ts('dve', kk[:], kk[:], -MAGIC, None, op0=ALU.add)
    yield
    k.stt(a[:], kk[:], -2 * math.pi, a[:], ALU.mult, ALU.add)
    yield
    k.ts('dve', a[:], a[:], -3.14159, 3.14159, op0=ALU.max, op1=ALU.min)
    yield
    k.act(out32, a[:], AF.Sin)
    yield


def _hilo(ph, hi, lo, src32, tmp32):
    k = ph.k
    k.copy('dve', hi, src32)
    k.copy('pool', tmp32, hi)
    k.tt('pool', lo, src32, tmp32, ALU.subtract)


def phase2a_gen(ph, T, banks):
    k = ph.k
    zs_hi = ph.sb("zs_hi", [128, L], BF)
    zs_lo = ph.sb("zs_lo", [128, L], BF)
    W1 = ph.sb("W1", [128, 128], F32)
    W2 = ph.sb("W2", [128, 128], F32)
    W1h = ph.sb("W1h", [128, 128], BF)
    W1l = ph.sb("W1l", [128, 128], BF)
    W2h = ph.sb("W2h", [128, 128], BF)
    W2l = ph.sb("W2l", [128, 128], BF)
    wt = ph.sb("wt", [128, 128], F32)
    mv = ph.sb("mv", [128, 4], F32)
    fb = ph.sb("fb", [128, 2], F32)
    S = [dict(a=ph.sb(f"a{i}", [128, 512], F32), kk=ph.sb(f"kk{i}", [128, 512], F32), h1=ph.sb(f"h1_{i}", [128, 512], F32),
              h1h=ph.sb(f"h1h{i}", [128, 512], BF), h1l=ph.sb(f"h1l{i}", [128, 512], BF), t32=ph.sb(f"t32_{i}", [128, 512], F32),
              h2=ph.sb(f"h2_{i}", [128, 512], F32), h2b=ph.sb(f"h2b{i}", [128, 512], BF)) for i in range(2)]
    k.dma(zs_hi[:], T['zs_hi'][:, :])
    k.dma(zs_lo[:], T['zs_lo'][:, :])
    k.dma(W1[:], T['W1blk'][:, :])
    k.dma(W2[:], T['W2blk'][:, :])
    k.dma(mv[:], T['mlpv'][:, :])
    _hilo(ph, W1h[:], W1l[:], W1[:], wt[:])
    _hilo(ph, W2h[:], W2l[:], W2[:], wt[:])
    k.tt('dve', fb[:, 0:1], mv[:, 0:1], mv[:, 1:2], ALU.mult)
    k.tt('dve', fb[:, 1:2], mv[:, 2:3], mv[:, 3:4], ALU.mult)
    yield

    def chunk(cch):
        s_ = S[cch % 2]
        sl = slice(512 * cch, 512 * (cch + 1))
        b1 = banks[cch % 2]
        k.mm(b1[:, :], W1h[:], zs_hi[:, sl], start=True, stop=False)
        k.mm(b1[:, :], W1h[:], zs_lo[:, sl], start=False, stop=False)
        k.mm(b1[:, :], W1l[:], zs_hi[:, sl], start=False, stop=True)
        yield
        yield from _sin_layer(ph, s_, b1[:, :], mv[:, 0:1], fb[:, 0:1], s_['h1'][:])
        _hilo(ph, s_['h1h'][:], s_['h1l'][:], s_['h1'][:], s_['t32'][:])
        yield
        b2 = banks[2 + cch % 2]
        k.mm(b2[:, :], W2h[:], s_['h1h'][:], start=True, stop=False)
        k.mm(b2[:, :], W2h[:], s_['h1l'][:], start=False, stop=False)
        k.mm(b2[:, :], W2l[:], s_['h1h'][:], start=False, stop=True)
        yield
        yield from _sin_layer(ph, s_, b2[:, :], mv[:, 2:3], fb[:, 1:2], s_['h2'][:])
        k.copy('pool', s_['h2b'][:], s_['h2'][:])
        k.dma(T['h2_d'][:, sl], s_['h2b'][:])
        yield

    gens = [chunk(c) for c in range(NB)]
    active = []
    while gens or active:
        while gens and len(active) < 2:
            active.append(gens.pop(0))
        for g in list(active):
            try:
                next(g)
            except StopIteration:
                active.remove(g)
        yield


def _cmul_tab(ph, W, src, Ta, Tb, out_bf):
    k = ph.k
    P1, P2 = W
    sw = src.rearrange("p (r f) -> p r f", r=2)[:, ::-1, :]
    k.tt('dve', P1[:], src, Ta, ALU.mult)
    k.tt('dve', P2[:].rearrange("p (r f) -> p r f", r=2), sw, Tb.rearrange("p (r f) -> p r f", r=2), ALU.mult)
    k.tt('pool', out_bf, P1[:], P2[:], ALU.add)


def phase2b(nc, T):
    with Phase(nc, "p2b") as ph:
        k = ph.k
        ps = ph.ps
        ident = ph.sb("ident", [128, 128], BF)
        make_ident(ph, ident)
        w_in_v = T['w_in'].rearrange("(k p) n -> p k n", p=128)
        cols = (COL_V, COL_X1, COL_X2, COL_ZH)
        ar = ph.sb("arena", [128, 24592], BF)
        Wblk = ar[:, 20486:24582].rearrange("p (k w c) -> p k w c", k=8, w=4)
        for w in range(4):
            load_w(ph, Wblk[:, :, w, :], w_in_v[:, :, cols[w]:cols[w] + 128])
        cb16 = {}
        for nm, w in (('FA1', 256), ('FA2', 256), ('WBr', 128), ('WBi', 128), ('WBni', 128), ('WI1', 256), ('WI2', 256),
                      ('FIr', 128), ('FIi', 128)):
            cb16[nm] = ph.sb(nm, [128, w], BF)
            k.dma(cb16[nm][:], T[nm][:, :])
        c32 = {}
        for nm in ('TWa', 'TWb', 'TIa', 'TIb'):
            c32[nm] = ph.sb(nm, [128, 256], F32)
            k.dma(c32[nm][:], T[nm][:, :])
        h2s = ph.sb("h2s", [128, L], BF)
        k.dma(h2s[:], T['h2_d'][:, :])
        h2p = ph.sb("h2p", [128, 32, 128], BF)
        k.copy('pool', h2p[:], h2s[:].rearrange("q (p s) -> q s p", s=32))
        W3 = ph.sb("W3", [128, 2048], BF)
        load_w(ph, W3[:], T['W3blk'][:, :])
        wsh = ph.sb("wsh", [128, 12, 4], F32)
        k.dma(wsh[:].rearrange("p a b -> p (a b)"), T['wsh'][:, :])
        biasT = ph.sb("biasT", [128, 2, 128], F32)
        k.dma(biasT[:].rearrange("p a b -> p (a b)"), T['biasT'][:, :])
        negd = ph.sb("negd", [128, HYW], F32)
        k.dma(negd[:], T['negd'][0:1, :].partition_broadcast(128))
        tfull = ph.sb("tfull", [128, 2, 32], F32)
        k.dma(tfull[:].rearrange("p a b -> p (a b)"), T['tfull'][:, :])

        hbuf = [ph.sb(f"hbuf{i}", [128, 8, 512], BF) for i in range(2)]
        raw = [ar[:, 4098 * i:4098 * (i + 1)] for i in range(3)]
        ub_ = [ar[:, 12294 + 4096 * i:12294 + 4096 * (i + 1)] for i in range(2)]
        k_tm = ar[:, 0:8192].rearrange("p (o d c s) -> p o d c s", o=2, d=2, c=64)
        AB = ar[:, 8192:12288].rearrange("p (d c s) -> p d c s", d=2, c=64)
        Ksp = ar[:, 12288:20480].rearrange("p (o g r f) -> p o g r f", o=2, g=16, r=2)
        Gbuf = ar[:, 20480:24576].rearrange("p (r c s) -> p r c s", r=2, c=64)
        sz = ph.sb("sz", [128, L], BF)
        tm = [ph.sb(f"tm{i}", [128, 128, 32], BF) for i in range(3)]
        z2_tm = ph.sb("z2_tm", [128, 128, 32], BF)
        y_sc = ph.sb("y_sc", [128, 32, 128], BF)
        yzb = ph.sb("yzb", [128, L], BF)
        arg32 = ph.sb("arg32", [128, 4096], F32)
        PW = [(ph.sb(f"P1_{i}", [128, 256], F32), ph.sb(f"P2_{i}", [128, 256], F32)) for i in range(2)]
        Zp = [ph.sb(f"Zp{i}", [128, 256], BF) for i in range(2)]
        Yb = [ph.sb(f"Yb{i}", [128, 256], BF) for i in range(2)]
        Kev = [ph.sb(f"Kev{i}", [128, 256], BF) for i in range(2)]
        Esb = [[ph.sb(f"E{st}_{i}", [128, 256], BF) for i in range(2)] for st in range(3)]
        cnt = [0]

        ZA = [ph.sb(f"ZA{i}", [128, 512], BF) for i in range(2)]
        ZB = [ph.sb(f"ZB{i}", [128, 512], BF) for i in range(2)]
        YA = [ph.sb(f"YA{i}", [128, 512], BF) for i in range(2)]
        YB = [ph.sb(f"YB{i}", [128, 512], BF) for i in range(2)]
        G2 = ph.sb("G2", [128, 2, 64, 32], BF)
        WI1n = ph.sb("WI1n", [128, 256], BF)
        k.ts('pool', WI1n[:], cb16['WI1'][:], -1.0, None, op0=ALU.mult)

        def v4(ap):
            return ap.rearrange("p (u r f) -> p u r f", u=2, r=2)

        def tab4(t):
            return bcast(t.rearrange("p (r f) -> p r f", r=2), 1, 2)

        def run_skewed(items, hook=None):
            n = len(items)
            depth = max(len(it) for it in items)
            for t in range(n + depth - 1):
                for s_ in reversed(range(depth)):
                    i = t - s_
                    if 0 <= i < n and s_ < len(items[i]):
                        items[i][s_](i)
                if hook is not None:
                    hook(t)

        def cmul_pair(bank, Ta, Tb, outA, outB):
            k.tt('dve', outA, v4(bank), tab4(Ta), ALU.mult)
            k.tt('dve', outB, v4(bank)[:, :, ::-1, :], tab4(Tb), ALU.mult)

        def st_za(lhs_of, q):
            def f(i):
                bank = ps[i % 2]
                for u in range(2):
                    l1, l2 = lhs_of(2 * q + u)
                    za = bank[:, 256 * u:256 * (u + 1)]
                    k.mm(za, l1, cb16['FA1'][:], start=True, stop=(l2 is None))
                    if l2 is not None:
                        k.mm(za, l2, cb16['FA2'][:], start=False, stop=True)
            return f

        def st_tw(i):
            cmul_pair(ps[i % 2][:, :], c32['TWa'][:], c32['TWb'][:], v4(ZA[i % 2][:]), v4(ZB[i % 2][:]))

        def st_ub(i):
            bank = ps[2 + i % 2]
            for u in range(2):
                ub = bank[:, 256 * u:256 * (u + 1)]
                for n_, z_p in enumerate((ZA[i % 2], ZB[i % 2])):
                    zr = z_p[:, 256 * u:256 * u + 128]
                    zi = z_p[:, 256 * u + 128:256 * (u + 1)]
                    k.mm(ub[:, 0:128], cb16['WBr'][:], zr, start=(n_ == 0), stop=False)
                    k.mm(ub[:, 0:128], cb16['WBni'][:], zi, start=False, stop=(n_ == 1))
                for n_, z_p in enumerate((ZA[i % 2], ZB[i % 2])):
                    zr = z_p[:, 256 * u:256 * u + 128]
                    zi = z_p[:, 256 * u + 128:256 * (u + 1)]
                    k.mm(ub[:, 128:256], cb16['WBi'][:], zr, start=(n_ == 0), stop=False)
                    k.mm(ub[:, 128:256], cb16['WBr'][:], zi, start=False, stop=(n_ == 1))

        def filt_gen(cb, hbk, bank_fixed):
            gcol = 128 * cb + 64 * hbk
            k.tt('dve', arg32[:].rearrange("p (d c s) -> p d c s", d=2, c=64),
                 bcast(bcast(negd[:, gcol:gcol + 64], 1, 2), 3, 32),
                 bcast(tfull[:, :, :], 2, 64), ALU.mult)
            yield
            k.act(AB.rearrange("p d c s -> p (d c s)"), arg32[:], AF.Exp)
            yield
            wc0 = 256 * (2 * cb + hbk)
            for s1 in range(32):
                kb_ = (bank_fixed if bank_fixed is not None else ps[6 + s1 % 2])[:, 0:256]
                k.mm(kb_, h2p[:, s1, :], W3[:, wc0:wc0 + 256])
                abv = AB[:, :, :, s1].rearrange("p d c -> p (d c)")
                k.tt('dve', k_tm[:, :, :, :, s1].rearrange("p o d c -> p o (d c)"),
                     kb_.rearrange("p (o x) -> p o x", o=2), bcast(abv, 1, 2), ALU.mult)
                yield

        pref = [None]
        for cb in range(DBG.get('ncb', 4)):
            for w in range(4):
                if cb > 0:
                    load_w(ph, Wblk[:, :, w, :], w_in_v[:, :, cols[w] + 128 * cb:cols[w] + 128 * (cb + 1)])
            for w in range(3):
                k.memset('pool', raw[w][:, 0:1], 0.0)
                k.memset('pool', raw[w][:, 4097:4098], 0.0)
            for tb in range(NB):
                hb = hbuf[tb % 2]
                k.dma(hb[:].rearrange("p c t -> p (c t)"), T['hT_d'][tb])
                for w in range(4):
                    bank = ps[(tb * 4 + w) % 2]
                    for c in range(8):
                        k.mm(bank[:, :], Wblk[:, c, w, :], hb[:, c, :], start=(c == 0), stop=(c == 7))
                    if w < 3:
                        k.act(raw[w][:, 1 + 512 * tb:1 + 512 * (tb + 1)], bank[:, :], AF.Copy)
                    else:
                        k.act(sz[:, 512 * tb:512 * (tb + 1)], bank[:, :], AF.Silu)
            if DBG.get('s2b', 9) < 2: continue
            for w in range(3):
                u = ub_[w % 2]
                j = 4 * w + cb
                k.ts('dve', u, raw[w][:, 1:4097], wsh[:, j, 1:2], wsh[:, j, 3:4], op0=ALU.mult, op1=ALU.add)
                k.stt(u, raw[w][:, 0:4096], wsh[:, j, 0:1], u, ALU.mult, ALU.add)
                k.stt(u, raw[w][:, 2:4098], wsh[:, j, 2:3], u, ALU.mult, ALU.add)
                for a in range(4):
                    pv = ps[2 + a % 2][:, :].bitcast(BF)
                    for e in range(8):
                        s1 = 8 * a + e
                        k.tr(pv[:, 128 * e:128 * (e + 1)], u[:, s1:4096:32], ident[:])
                    k.copy('dve', tm[w][:, :, 8 * a:8 * a + 8], pv.rearrange("p (s c) -> p c s", s=8))
            if DBG.get('dump_tm'):
                k.dma(T['dbg_tm'][:, :], tm[DBG['dump_tm'] - 1][:].rearrange("p c s -> p (c s)"))
            if DBG.get('s2b', 9) < 3: continue
            for hbk in range(DBG.get('nhbk', 2)):
                c0 = 64 * hbk
                gcol = 128 * cb + c0
                if hbk == 0 or not DBG.get('pref', 1):
                    for _ in filt_gen(cb, hbk, None):
                        pass
                else:
                    for _ in pref[0]:
                        pass
                if DBG.get('s2b', 9) < 4: continue
                def spec_lhs(gi):
                    o, g = divmod(gi, 16)
                    return (k_tm[:, o, 0, 4 * g:4 * g + 4, :].rearrange("p c s -> p (c s)"),
                            k_tm[:, o, 1, 4 * g:4 * g + 4, :].rearrange("p c s -> p (c s)"))

                def st_kev(q):
                    def f(i):
                        o, gp = divmod(q, 8)
                        k.copy('act', Ksp[:, o, 2 * gp:2 * gp + 2, :, :].rearrange("p g r f -> p (g r f)"), ps[2 + i % 2][:, :])
                        if gp == 7:
                            gg0 = gcol // 4
                            k.tt('pool', Ksp[:, o, :, 0, :], Ksp[:, o, :, 0, :], bcast(biasT[:, o, gg0:gg0 + 16], 2, 128), ALU.add)
                    return f

                def conv_lhs_of(src):
                    def f(g):
                        return (src[:, c0 + 4 * g:c0 + 4 * g + 4, :].rearrange("p c s -> p (c s)"), None)
                    return f

                def st_mul(o, q):
                    def f(i):
                        bank = ps[2 + i % 2][:, :]
                        kr_ = bcast(Ksp[:, o, 2 * q:2 * q + 2, 0, :], 2, 2)
                        ki_ = bcast(Ksp[:, o, 2 * q:2 * q + 2, 1, :], 2, 2)
                        k.tt('dve', v4(YA[i % 2][:]), v4(bank), kr_, ALU.mult)
                        k.tt('dve', v4(YB[i % 2][:]), v4(bank)[:, :, ::-1, :], ki_, ALU.mult)
                    return f

                def st_gb(i):
                    bank = ps[4 + i % 2]
                    ya, yb_ = YA[i % 2], YB[i % 2]
                    for u in range(2):
                        gb = bank[:, 256 * u:256 * (u + 1)]
                        k.mm(gb, ya[:, 256 * u:256 * u + 128], cb16['WI1'][:], start=True, stop=False)
                        k.mm(gb, yb_[:, 256 * u:256 * u + 128], WI1n[:], start=False, stop=False)
                        k.mm(gb, ya[:, 256 * u + 128:256 * (u + 1)], cb16['WI2'][:], start=False, stop=False)
                        k.mm(gb, yb_[:, 256 * u + 128:256 * (u + 1)], cb16['WI2'][:], start=False, stop=True)

                def st_itw(o, q):
                    def f(i):
                        bank = ps[4 + i % 2][:, :]
                        g1o = Gbuf[:, :, 8 * q:8 * q + 8, :].rearrange("p r (u c) s -> p r u (c s)", u=2)
                        g2o = G2[:, :, 8 * q:8 * q + 8, :].rearrange("p r (u c) s -> p r u (c s)", u=2)
                        k.tt('dve', g1o, v4(bank).rearrange("p u r f -> p r u f"),
                             tab4(c32['TIa'][:]).rearrange("p u r f -> p r u f"), ALU.mult)
                        k.tt('dve', g2o, v4(bank)[:, :, ::-1, :].rearrange("p u r f -> p r u f"),
                             tab4(c32['TIb'][:]).rearrange("p u r f -> p r u f"), ALU.mult)
                    return f

                def st_inva(o, q):
                    def f(i):
                        if q % 2 == 1:
                            cc = q // 2
                            gate = tm[1] if o == 0 else tm[2]
                            yb = ps[6]
                            for n_, gsrc in enumerate((Gbuf, G2)):
                                k.mm(yb[:, :], cb16['FIr'][:], gsrc[:, 0, 16 * cc:16 * cc + 16, :].rearrange("p c s -> p (c s)"),
                                     start=(n_ == 0), stop=False)
                                k.mm(yb[:, :], cb16['FIi'][:], gsrc[:, 1, 16 * cc:16 * cc + 16, :].rearrange("p c s -> p (c s)"),
                                     start=False, stop=(n_ == 1))
                            cs = slice(c0 + 16 * cc, c0 + 16 * cc + 16)
                            if o == 0:
                                k.tt('dve', z2_tm[:, cs, :], yb[:, :].rearrange("p (c s) -> p c s", c=16), gate[:, cs, :], ALU.mult)
                            else:
                                k.tt('dve', y_sc[:, :, cs].rearrange("p s c -> p c s"),
                                     yb[:, :].rearrange("p (c s) -> p c s", c=16), gate[:, cs, :], ALU.mult)
                    return f

                items = []
                for q in range(16):
                    items.append([st_za(spec_lhs, q), st_tw, st_ub, st_kev(q)])
                for o in range(DBG.get('nord', 2)):
                    src = tm[0] if o == 0 else z2_tm
                    if o == 1:
                        items += [[] for _ in range(DBG.get('gap', 0))]
                    for q in range(8):
                        items.append([st_za(conv_lhs_of(src), q), st_tw, st_ub, st_mul(o, q), st_gb, st_itw(o, q), st_inva(o, q)])
                hook = None
                if hbk == 0 and DBG.get('pref', 1):
                    pref[0] = filt_gen(cb, 1, ps[7])

                    def hook(t, g=pref[0]):
                        if t >= 19:
                            next(g, None)
                            next(g, None)
                run_skewed(items, hook)
            if DBG.get('dump_z2'):
                k.dma(T['dbg_tm'][:, :], z2_tm[:].rearrange("p c s -> p (c s)"))
            if DBG.get('s2b', 9) < 6: continue
            for a in range(4):
                pv = ps[2 + a % 2][:, :].bitcast(BF)
                for e in range(8):
                    k.tr(pv[:, 128 * e:128 * (e + 1)], y_sc[:, 8 * a + e, :], ident[:])
                k.tt('dve', yzb[:].rearrange("c (p s) -> c p s", s=32)[:, :, 8 * a:8 * a + 8],
                     pv.rearrange("c (s p) -> c p s", s=8),
                     sz[:].rearrange("c (p s) -> c p s", s=32)[:, :, 8 * a:8 * a + 8], ALU.mult)
            for tb in range(NB):
                k.dma(T['yz_d'][tb][:, 512 * cb:512 * (cb + 1)], yzb[:, 512 * tb:512 * (tb + 1)])


def phase2(nc, T):
    if 'b' in DBG.get('p2', 'ab'):
        phase2b(nc, T)


def _bf(a):
    return np.asarray(a, np.float32).astype(ml_dtypes.bfloat16)


_CONST_CACHE = {}


def host_constants():
    if _CONST_CACHE:
        return _CONST_CACHE
    C = {}
    pos = np.arange(L, dtype=np.float32)
    inv_freq = (np.float32(10000.0) ** (-np.arange(0, 32, 2, dtype=np.float32) / np.float32(32))).astype(np.float32)
    ang = (pos[:, None] * inv_freq[None, :]).astype(np.float32)
    C['cosT'] = np.ascontiguousarray(np.cos(ang).astype(np.float32).reshape(NT, 128, 16).transpose(1, 0, 2).reshape(128, NT * 16))
    C['sinT'] = np.ascontiguousarray(np.sin(ang).astype(np.float32).reshape(NT, 128, 16).transpose(1, 0, 2).reshape(128, NT * 16))
    F = fft_constants()
    for nm in ('FA1', 'FA2', 'WBr', 'WBi', 'WBni', 'WI1', 'WI2', 'FIr', 'FIi'):
        C[nm] = np.ascontiguousarray(_bf(F[nm]))
    for nm in ('TWa', 'TWb', 'TIa', 'TIb'):
        C[nm] = np.ascontiguousarray(F[nm].astype(np.float32))
    C.update(filter_constants())
    _CONST_CACHE.update(C)
    return _CONST_CACHE


def prep_inputs(inp, b):
    f32 = np.float32
    m = {}
    m['x'] = np.ascontiguousarray(inp['x'][b], dtype=f32)
    m['w_in'] = np.ascontiguousarray(inp['w_in'][0], dtype=f32)
    m['gT'] = np.ascontiguousarray(inp['g_norm'][0].reshape(8, 128).T, dtype=f32)
    m['bgT'] = np.ascontiguousarray(inp['b_gate'][0].reshape(16, 128).T, dtype=f32)
    m['w_uq'] = np.ascontiguousarray(inp['w_uq'][0], dtype=f32)
    m['w_ukv'] = np.ascontiguousarray(inp['w_ukv'][0], dtype=f32)
    m['gcqT'] = np.ascontiguousarray(inp['g_cq'][0].reshape(3, 128).T, dtype=f32)
    m['gckvT'] = np.ascontiguousarray(inp['g_ckv'][0].reshape(2, 128).T, dtype=f32)
    m['gqk'] = np.ascontiguousarray(np.concatenate([inp['g_qn'][0], inp['g_kn'][0]])[None, :], dtype=f32)
    m['w_attn_out'] = np.ascontiguousarray(inp['w_attn_out'][0], dtype=f32)
    m['w_hy_out'] = np.ascontiguousarray(inp['w_hy_out'][0], dtype=f32)
    m['w_out'] = np.ascontiguousarray(inp['w_out'][0], dtype=f32)
    wsh = np.zeros((128, 12, 4), f32)
    wsh[:, :, 0:3] = inp['w_short'][0].reshape(3, 12, 128).transpose(2, 1, 0)
    wsh[:, :, 3] = inp['b_short'][0].reshape(12, 128).T
    m['wsh'] = wsh.reshape(128, 48)
    hb = inp['hy_bias'][0]
    bT = hb.reshape(2, 128, 4).transpose(2, 0, 1)
    m['biasT'] = np.ascontiguousarray(np.repeat(bT[:, None], 32, axis=1).reshape(128, 256), dtype=f32)
    W1 = np.zeros((128, 128), f32)
    W1[0:33, 0:64] = inp['w_f1'][0]
    W1[64:97, 64:128] = inp['w_f1'][0]
    m['W1blk'] = W1
    W2 = np.zeros((128, 128), f32)
    W2[0:64, 0:64] = inp['w_f2'][0]
    W2[64:128, 64:128] = inp['w_f2'][0]
    m['W2blk'] = W2
    mv = np.zeros((128, 4), f32)
    for jj, nm in enumerate(('freq_1', 'b_f1', 'freq_2', 'b_f2')):
        mv[0:64, jj] = inp[nm][0]
        mv[64:128, jj] = inp[nm][0]
    m['mlpv'] = mv
    w3 = inp['w_f3'][0].reshape(64, 2, 2, 8, 64)
    W3 = np.zeros((128, 8, 2, 2, 64), f32)
    for dd in range(2):
        W3[64 * dd:64 * (dd + 1), :, :, dd, :] = w3[:, :, dd, :, :].transpose(0, 2, 1, 3)
    m['W3blk'] = W3.reshape(128, 2048)
    C = host_constants()
    for nm in CONST_NAMES:
        m[nm] = C[nm]
    return m


IN_SHAPES = {
    'x': ([L, D], F32), 'w_in': ([D, 5280], F32), 'gT': ([128, 8], F32), 'bgT': ([128, 16], F32),
    'w_uq': ([384, 768], F32), 'w_ukv': ([256, 1024], F32), 'gcqT': ([128, 3], F32), 'gckvT': ([128, 2], F32),
    'gqk': ([1, 192], F32), 'w_attn_out': ([512, D], F32), 'w_hy_out': ([512, D], F32), 'w_out': ([D, D], F32),
    'cosT': ([128, NT * 16], F32), 'sinT': ([128, NT * 16], F32),
    'wsh': ([128, 48], F32), 'biasT': ([128, 256], F32), 'W1blk': ([128, 128], F32), 'W2blk': ([128, 128], F32),
    'mlpv': ([128, 4], F32), 'W3blk': ([128, 2048], F32),
    'FA1': ([128, 256], BF), 'FA2': ([128, 256], BF), 'WBr': ([128, 128], BF), 'WBi': ([128, 128], BF),
    'WBni': ([128, 128], BF), 'WI1': ([128, 256], BF), 'WI2': ([128, 256], BF), 'FIr': ([128, 128], BF),
    'FIi': ([128, 128], BF), 'TWa': ([128, 256], F32), 'TWb': ([128, 256], F32), 'TIa': ([128, 256], F32),
    'TIb': ([128, 256], F32), 'zs_hi': ([128, L], BF), 'zs_lo': ([128, L], BF), 'tfull': ([128, 64], F32),
    'negd': ([1, HYW], F32),
}
CONST_NAMES = ('cosT', 'sinT', 'FA1', 'FA2', 'WBr', 'WBi', 'WBni', 'WI1', 'WI2', 'FIr', 'FIi', 'TWa', 'TWb', 'TIa', 'TIb',
               'zs_hi', 'zs_lo', 'tfull', 'negd')


def build_nc(debug=None):
    debug = debug or set()
    nc = bass.Bass("TRN2", target_bir_lowering=False)
    T = {}
    for name, (shape, dt) in IN_SHAPES.items():
        T[name] = nc.dram_tensor(name, shape, dt, kind="ExternalInput").ap()
    T['out'] = nc.dram_tensor("out", [L, D], F32, kind="ExternalOutput").ap()
    skind = dict(kind="ExternalOutput") if 'dump' in debug else {}
    T['hT_d'] = nc.dram_tensor("hT_d", [NB, 128, 8 * 512], BF, **skind).ap()
    T['at_d'] = nc.dram_tensor("at_d", [NB, 64, 8 * 512], BF, **skind).ap()
    T['h2_d'] = nc.dram_tensor("h2_d", [128, L], BF, **skind).ap()
    if 'dump' in debug:
        T['dbg_tm'] = nc.dram_tensor("dbg_tm", [128, 4096], BF, kind="ExternalOutput").ap()
        T['dbg_k'] = nc.dram_tensor("dbg_k", [128, 8192], BF, kind="ExternalOutput").ap()
        T['dbg_ks'] = nc.dram_tensor("dbg_ks", [128, 8192], BF, kind="ExternalOutput").ap()
    if 'yz_in' in debug:
        T['yz_d'] = nc.dram_tensor("yz_d", [NB, 128, 4 * 512], BF, kind="ExternalInput").ap()
    else:
        T['yz_d'] = nc.dram_tensor("yz_d", [NB, 128, 4 * 512], BF, **skind).ap()
    phases = debug & {'p1', 'p2', 'p3', 'p4'} or {'p1', 'p2', 'p3', 'p4'}
    do_p2 = 'p2' in phases and 'yz_in' not in debug
    if 'p1' in phases or do_p2:
        phase12(nc, T, with_p1=('p1' in phases), with_p2a=do_p2)
    if do_p2:
        phase2(nc, T)
    if 'p3' in phases:
        phase3(nc, T)
    if 'p4' in phases:
        phase4(nc, T)
    return nc


def kernel(**inputs):
    inp = {k_: np.asarray(v) for k_, v in inputs.items()}
    nc = build_nc()
    in_maps = [prep_inputs(inp, b) for b in range(8)]
    res = run_bass_kernel_spmd(nc, in_maps, core_ids=list(range(8)))
    out = np.stack([np.asarray(r['out'], dtype=np.float32) for r in res.results], axis=0)
    return out
```

```python
import concourse.bass as bass
import concourse.mybir as mybir

_ESZ = {}


def _esize(dt):
    s = _ESZ.get(dt)
    if s is None:
        n = str(dt)
        if '32' in n:
            s = 4
        elif '16' in n:
            s = 2
        elif '8' in n:
            s = 1
        else:
            s = 4
        _ESZ[dt] = s
    return s


def footprint(ap):
    t = ap.tensor
    name = t.name
    es = _esize(ap.dtype)
    apl = ap.ap
    off = int(ap.offset) * es
    space = str(type(t).__name__)
    if 'DRam' in space:
        lo = off
        hi = off
        for st, cnt in apl:
            if cnt > 1:
                d = (cnt - 1) * st * es
                if d > 0:
                    hi += d
                else:
                    lo += d
        return (name, 0, 1, lo, hi + es)
    pstep, pcnt = apl[0]
    pstep_b = pstep * es
    if pstep_b > 0:
        p0 = off // pstep_b
        f0 = off % pstep_b
    else:
        p0 = 0
        f0 = off
    lo = f0
    hi = f0
    for st, cnt in apl[1:]:
        if cnt > 1:
            d = (cnt - 1) * st * es
            if d > 0:
                hi += d
            else:
                lo += d
    return (name, p0, p0 + pcnt, lo, hi + es)


COMPUTE = ('pe', 'act', 'dve', 'pool')
QUEUES = ('pe', 'act', 'dve', 'pool', 'sp')
QIDX = {q: i for i, q in enumerate(QUEUES)}


class _Op:
    __slots__ = ('q', 'fn', 'dma', 'idx', 'gid', 'waits_c', 'waits_d', 'signal', 'snap', 'slot', 'slot_cnt', 'prev_slot')


class Prog:
    def __init__(self, nc, dma_slots=None):
        self.nc = nc
        self.streams = {q: [] for q in QUEUES}
        self.recs = {}
        self.known = {q: [-1] * len(QUEUES) for q in QUEUES}
        self.known_dma = {q: set() for q in QUEUES}
        self.ops = []
        self.dma_slots = dma_slots or {'sp': 8, 'pool': 4, 'act': 4}
        self.dma_count = {q: 0 for q in QUEUES}
        self.dma_ops = {q: [] for q in QUEUES}
        self.n_comp = {q: 0 for q in QUEUES}

    def add(self, q, fn, reads=(), writes=(), dma=False):
        op = _Op()
        op.q = q
        op.fn = fn
        op.dma = dma
        op.gid = len(self.ops)
        op.signal = dma
        op.waits_c = []
        op.waits_d = []
        op.slot = None
        op.prev_slot = None
        stream = self.streams[q]
        if not dma:
            op.idx = self.n_comp[q]
            self.n_comp[q] += 1
        else:
            op.idx = -1
        deps_c = {}
        deps_d = set()

        def scan(fp, is_write):
            name, p0, p1, f0, f1 = fp
            lst = self.recs.get(name)
            if not lst:
                return
            for r in lst:
                (rp0, rp1, rf0, rf1, rw, rop) = r
                if not (is_write or rw):
                    continue
                if rp1 <= p0 or p1 <= rp0 or rf1 <= f0 or f1 <= rf0:
                    continue
                if rop.dma:
                    deps_d.add(rop)
                else:
                    e = rop.q
                    if deps_c.get(e, -1) < rop.idx:
                        deps_c[e] = rop.idx

        rfps = [footprint(a) for a in reads]
        wfps = [footprint(a) for a in writes]
        for fp in rfps:
            scan(fp, False)
        for fp in wfps:
            scan(fp, True)
        known = self.known[q]
        kd = self.known_dma[q]
        for e, i in deps_c.items():
            ei = QIDX[e]
            if e == q and not dma:
                if q == 'pe':
                    continue
            if i <= known[ei]:
                continue
            op.waits_c.append((e, i))
            src = self.comp_ops[e][i]
            src.signal = True
            known[ei] = i
            for k, v in enumerate(src.snap):
                if v > known[k]:
                    known[k] = v
        for d in sorted(deps_d, key=lambda o: o.gid):
            if d.gid in kd:
                continue
            op.waits_d.append(d)
            kd.add(d.gid)
            for k, v in enumerate(d.snap):
                if v > known[k]:
                    known[k] = v
        if dma:
            n = self.dma_count[q]
            R = self.dma_slots[q]
            op.slot = n % R
            op.slot_cnt = n // R + 1
            if n >= R:
                prev = self.dma_ops[q][n - R]
                op.prev_slot = prev
                kd.add(prev.gid)
            self.dma_count[q] = n + 1
            self.dma_ops[q].append(op)
        op.snap = tuple(known)
        if not dma:
            self.comp_ops[q].append(op)
        for fp, is_write in [(f, False) for f in rfps] + [(f, True) for f in wfps]:
            name, p0, p1, f0, f1 = fp
            lst = self.recs.setdefault(name, [])
            if is_write:
                lst[:] = [r for r in lst if not (r[0] >= p0 and r[1] <= p1 and r[2] >= f0 and r[3] <= f1)]
            else:
                if not dma:
                    lst[:] = [r for r in lst if not (r[4] is False and (not r[5].dma) and r[5].q == q
                                                     and r[0] == p0 and r[1] == p1 and r[2] == f0 and r[3] == f1)]
            lst.append((p0, p1, f0, f1, is_write, op))
        stream.append(op)
        self.ops.append(op)
        return op

    comp_ops = None

    def start(self):
        self.comp_ops = {q: [] for q in QUEUES}

    def pe(self, fn, reads, writes):
        return self.add('pe', fn, reads, writes)

    def act(self, fn, reads, writes):
        return self.add('act', fn, reads, writes)

    def dve(self, fn, reads, writes):
        return self.add('dve', fn, reads, writes)

    def pool(self, fn, reads, writes):
        return self.add('pool', fn, reads, writes)

    def dma(self, out, in_, q='sp', **kw):
        return self.add(q, lambda e: e.dma_start(out=out, in_=in_, **kw), [in_], [out], dma=True)

    def emit(self, block, sems_c, sems_d):
        cum = {}
        for e in QUEUES:
            c = 0
            arr = []
            for o in self.comp_ops[e]:
                if o.signal:
                    c += 1
                arr.append(c)
            cum[e] = arr
        self.cum = cum

        def gen(q):
            def body(eng):
                for o in self.streams[q]:
                    for (e, i) in o.waits_c:
                        eng.wait_ge(sems_c[e], cum[e][i])
                    for d in o.waits_d:
                        eng.wait_ge(sems_d[d.q][d.slot], 16 * d.slot_cnt)
                    if o.prev_slot is not None:
                        p = o.prev_slot
                        eng.wait_ge(sems_d[p.q][p.slot], 16 * p.slot_cnt)
                    ins = o.fn(eng)
                    if o.dma:
                        ins.then_inc(sems_d[q][o.slot], 16)
                    elif o.signal:
                        ins.then_inc(sems_c[q], 1)
                R = self.dma_slots.get(q, 0)
                n = self.dma_count[q]
                for o in self.dma_ops[q][max(0, n - R):]:
                    eng.wait_ge(sems_d[q][o.slot], 16 * o.slot_cnt)
            return body

        if self.streams['pe']:
            block.tensor(gen('pe'))
        if self.streams['act']:
            block.scalar(gen('act'))
        if self.streams['dve']:
            block.vector(gen('dve'))
        if self.streams['pool']:
            block.gpsimd(gen('pool'))
        if self.streams['sp']:
            block.sync(gen('sp'))

import math
from contextlib import ExitStack
import numpy as np
import ml_dtypes
from concourse.bass_utils import run_bass_kernel_spmd

F32 = mybir.dt.float32
BF = mybir.dt.bfloat16
AF = mybir.ActivationFunctionType
ALU = mybir.AluOpType
AX = mybir.AxisListType

L = 4096
D = 1024
NT = 32
NB = 8
EPS = 1e-6
NF = 8192
HYW = 512
COL_V, COL_X1, COL_X2, COL_ZH = 0, 512, 1024, 1536
COL_CQ, COL_CKV, COL_KR, COL_ZA = 2048, 2432, 2688, 2720
COL_GH, COL_GA = 3232, 4256
MAGIC = 12582912.0
DBG = {}


def bcast(ap, axis, n):
    a = ap.unsqueeze(axis)
    shp = list(a.shape)
    shp[axis] = n
    return a.to_broadcast(shp)


class K:
    def __init__(self, P):
        self.P = P

    def mm(self, out, lhsT, rhs, start=True, stop=True):
        self.P.pe(lambda e: e.matmul(out, lhsT=lhsT, rhs=rhs, start=start, stop=stop), [lhsT, rhs], [out])

    def tr(self, out, in_, ident):
        self.P.pe(lambda e: e.transpose(out=out, in_=in_, identity=ident), [in_, ident], [out])

    def act(self, out, in_, func, bias=None, scale=None, accum_out=None):
        kw = {}
        reads = [in_]
        writes = [out]
        if bias is not None:
            kw['bias'] = bias
            if not isinstance(bias, (int, float)):
                reads.append(bias)
        if scale is not None:
            kw['scale'] = scale
            if not isinstance(scale, (int, float)):
                reads.append(scale)
        if accum_out is not None:
            kw['accum_out'] = accum_out
            writes.append(accum_out)
        self.P.act(lambda e: e.activation(out=out, in_=in_, func=func, **kw), reads, writes)

    def tt(self, eng, out, in0, in1, op):
        self.P.add(eng, lambda e: e.tensor_tensor(out=out, in0=in0, in1=in1, op=op), [in0, in1], [out])

    def ts(self, eng, out, in0, s1, s2=None, op0=ALU.mult, op1=None):
        reads = [in0]
        if not isinstance(s1, (int, float)):
            reads.append(s1)
        if s2 is not None and not isinstance(s2, (int, float)):
            reads.append(s2)
        if op1 is None:
            self.P.add(eng, lambda e: e.tensor_scalar(out=out, in0=in0, scalar1=s1, scalar2=None, op0=op0), reads, [out])
        else:
            self.P.add(eng, lambda e: e.tensor_scalar(out=out, in0=in0, scalar1=s1, scalar2=s2, op0=op0, op1=op1), reads, [out])

    def stt(self, out, in0, scalar, in1, op0, op1):
        reads = [in0, in1]
        if not isinstance(scalar, (int, float)):
            reads.append(scalar)
        self.P.dve(lambda e: e.scalar_tensor_tensor(out=out, in0=in0, scalar=scalar, in1=in1, op0=op0, op1=op1), reads, [out])

    def copy(self, eng, out, in_):
        if eng == 'act':
            self.act(out, in_, AF.Copy)
        else:
            self.P.add(eng, lambda e: e.tensor_copy(out=out, in_=in_), [in_], [out])

    def recip(self, out, in_):
        self.P.dve(lambda e: e.reciprocal(out=out, in_=in_), [in_], [out])

    def reduce_add(self, out, in_):
        self.P.dve(lambda e: e.tensor_reduce(out=out, in_=in_, axis=AX.X, op=ALU.add), [in_], [out])

    def memset(self, eng, ap, val):
        self.P.add(eng, lambda e: e.memset(ap, val), [], [ap])

    def dma(self, out, in_, q='sp'):
        self.P.dma(out, in_, q=q)


def _dump(P):
    cum = {}
    for e in QUEUES:
        c = 0
        arr = []
        for o in P.comp_ops[e]:
            if o.signal:
                c += 1
            arr.append(c)
        cum[e] = arr
    for q in QUEUES:
        print("== stream", q)
        for o in P.streams[q]:
            w = [f"{e}>={cum[e][i]}(op{i})" for e, i in o.waits_c] + [f"dma[{d.q}{d.slot}]>={16*d.slot_cnt}" for d in o.waits_d]
            if o.prev_slot is not None:
                w.append(f"prev dma[{o.prev_slot.q}{o.prev_slot.slot}]>={16*o.prev_slot.slot_cnt}")
            tag = f"DMA slot{o.slot} cnt{o.slot_cnt}" if o.dma else (f"op{o.idx} sig={cum[q][o.idx] if o.signal else '-'}")
            print("   ", tag, getattr(o, 'desc', ''), "waits:", w)


class Phase:
    def __init__(self, nc, name):
        self.nc = nc
        self.name = name
        self.es = ExitStack()

    def __enter__(self):
        nc = self.nc
        es = self.es
        es.__enter__()
        self.ps = [es.enter_context(nc.psum_tensor(f"{self.name}_ps{i}", [128, 512], F32)) for i in range(8)]
        self.sems_c = {e: es.enter_context(nc.semaphore(f"{self.name}_sc_{e}")) for e in QUEUES}
        self.sems_d = {q: [es.enter_context(nc.semaphore(f"{self.name}_sd_{q}{i}")) for i in range(n)]
                       for q, n in (('sp', 8), ('pool', 4), ('act', 4))}
        self.P = Prog(nc)
        self.P.start()
        self.k = K(self.P)
        return self

    def sb(self, name, shape, dt):
        return self.es.enter_context(self.nc.sbuf_tensor(f"{self.name}_{name}", shape, dt))

    def __exit__(self, *a):
        if a[0] is None:
            self.es.enter_context(self.nc.allow_low_precision("bf16 operands / intermediates by design"))
            block = self.es.enter_context(self.nc.Block())
            if DBG.get('dump') == self.name:
                _dump(self.P)
            self.P.emit(block, self.sems_c, self.sems_d)
        return self.es.__exit__(*a)


def make_ident(ph, ident):
    identf = ph.sb("identf", [128, 128], F32)
    ph.k.memset('pool', identf[:], 0.0)
    ph.P.pool(lambda e: e.affine_select(out=identf[:], in_=identf[:], pattern=[[-1, 128]], compare_op=ALU.not_equal,
                                        fill=1.0, base=0, channel_multiplier=1), [identf[:]], [identf[:]])
    ph.k.copy('dve', ident[:], identf[:])


def load_w(ph, dst, src_ap):
    ph.k.dma(dst, src_ap, q='pool')


def rstd_from_ss(k, rs_col, ss_col, n):
    k.act(rs_col, ss_col, AF.Sqrt, bias=EPS, scale=1.0 / n)
    k.recip(rs_col, rs_col)


def phase1_gen(ph, T, banks):
    k = ph.k
    xt = [ph.sb(f"xt{i}", [128, D], F32) for i in range(3)]
    xn = [ph.sb(f"xn{i}", [128, D], BF) for i in range(2)]
    junk = ph.sb("junk", [128, D], BF)
    ss = ph.sb("ss", [128, NT], F32)
    rs = ph.sb("rs", [128, NT], F32)
    ts_ = ph.sb("ts_", [128, NT], F32)
    mh = ph.sb("mh", [128, 1], F32)
    gT = ph.sb("gT", [128, 8], F32)
    ident = ph.sb("ident", [128, 128], BF)
    hb = [ph.sb(f"hb{i}", [128, 8, 512], BF) for i in range(2)]
    make_ident(ph, ident)
    k.memset('pool', mh[:], -0.5)
    k.dma(gT[:], T['gT'][:, :])
    pend = [None]
    for i0 in range(2):
        k.dma(xt[i0 % 3][:], T['x'][128 * i0:128 * (i0 + 1), :])
    for i in range(NT):
        x_t = xt[i % 3]
        if i + 2 < NT:
            k.dma(xt[(i + 2) % 3][:], T['x'][128 * (i + 2):128 * (i + 3), :])
        k.act(junk[:], x_t[:], AF.Square, accum_out=ss[:, i:i + 1])
        rstd_pool(k, rs[:, i:i + 1], ss[:, i:i + 1], D, mh[:, 0:1], ts_[:, i:i + 1])
        x_n = xn[i % 2]
        k.ts('dve', x_n[:], x_t[:], rs[:, i:i + 1])
        bank = banks[i % len(banks)]
        pv = bank[:, :].bitcast(BF)
        for c in range(8):
            k.tr(pv[:, 128 * c:128 * (c + 1)], x_n[:, 128 * c:128 * (c + 1)], ident[:])
        if pend[0] is not None:
            pend[0]()

        def evac(i=i, pv=pv):
            h_b = hb[(i // 4) % 2]
            j = i % 4
            k.tt('dve', h_b[:, :, 128 * j:128 * (j + 1)], pv.rearrange("p (c t) -> p c t", c=8),
                 bcast(gT[:, :], 2, 128), ALU.mult)
            if j == 3:
                k.dma(T['hT_d'][i // 4], h_b[:].rearrange("p c t -> p (c t)"))
        pend[0] = evac
        yield
    pend[0]()
    yield


def rstd_pool(k, rs, ss, n, mhalf, tmp):
    k.ts('dve', tmp, ss, 1.0 / n, EPS, op0=ALU.mult, op1=ALU.add)
    k.tt('pool', rs, tmp, mhalf, ALU.pow)


def phase12(nc, T, with_p1=True, with_p2a=True):
    with Phase(nc, "p12") as ph:
        g1 = phase1_gen(ph, T, ph.ps[0:4]) if with_p1 else iter(())
        g2 = phase2a_gen(ph, T, ph.ps[4:8]) if with_p2a else iter(())
        alive1, alive2 = True, True
        while alive1 or alive2:
            if alive1:
                try:
                    next(g1)
                except StopIteration:
                    alive1 = False
            for _ in range(4):
                if alive2:
                    try:
                        next(g2)
                    except StopIteration:
                        alive2 = False


def run_interleaved(gens, width=2):
    active = []
    gens = list(gens)
    while gens or active:
        while gens and len(active) < width:
            active.append(gens.pop(0))
        for g in list(active):
            try:
                next(g)
            except StopIteration:
                active.remove(g)


def qk_norm_rope(ph, W, src, dst, g_rep, cs_t, sc_t, mhalf, use_act=False):
    k = ph.k
    sq, ssq, rk, ta, tb_, tmp = W['sq'], W['ssq'], W['rk'], W['ta'], W['tb'], W['tmp']
    if use_act:
        k.act(sq[:].rearrange("p a b -> p (a b)"), src[:].rearrange("p a b -> p (a b)"), AF.Square)
    else:
        k.tt('pool', sq[:], src[:], src[:], ALU.mult)
    k.reduce_add(ssq[:], sq[:])
    rstd_pool(k, rk[:], ssq[:], 96, mhalf[:, 0:8], tmp[:])
    yield
    k.tt('dve', src[:], src[:], bcast(rk[:, :], 2, 96), ALU.mult)
    k.tt('dve', src[:], src[:], bcast(g_rep, 1, 8), ALU.mult)
    yield
    t1 = bcast(src[:, :, 64:80], 1, 2)
    t2 = bcast(src[:, :, 80:96], 1, 2)
    k.tt('pool', ta[:], t1, bcast(cs_t, 2, 8), ALU.mult)
    k.tt('pool', tb_[:], t2, bcast(sc_t, 2, 8), ALU.mult)
    k.tt('dve', dst[:, :, 64:80], ta[:, 0], tb_[:, 0], ALU.subtract)
    k.tt('dve', dst[:, :, 80:96], ta[:, 1], tb_[:, 1], ALU.add)
    k.copy('pool', dst[:, :, 0:64], src[:, :, 0:64])
    yield


def phase3(nc, T):
    with Phase(nc, "p3") as ph:
        k = ph.k
        ps = ph.ps
        ident = ph.sb("ident", [128, 128], BF)
        make_ident(ph, ident)
        w_in_v = T['w_in'].rearrange("(k p) n -> p k n", p=128)
        Wkv = ph.sb("Wkv", [128, 8, 288], BF)
        Wq = ph.sb("Wq", [128, 8, 384], BF)
        Wuq = ph.sb("Wuq", [128, 3, 768], BF)
        Wukv = ph.sb("Wukv", [128, 2, 1024], BF)
        gcq = ph.sb("gcq", [128, 3], F32)
        gckv = ph.sb("gckv", [128, 2], F32)
        gqk = ph.sb("gqk", [128, 192], F32)
        csT = ph.sb("csT", [128, NT, 2, 16], F32)
        mhalf = ph.sb("mhalf", [128, 8], F32)
        k.memset('pool', mhalf[:], -0.5)
        load_w(ph, Wkv[:], w_in_v[:, :, COL_CKV:COL_CKV + 288])
        load_w(ph, Wq[:], w_in_v[:, :, COL_CQ:COL_CQ + 384])
        load_w(ph, Wuq[:], T['w_uq'].rearrange("(k p) n -> p k n", p=128))
        load_w(ph, Wukv[:], T['w_ukv'].rearrange("(k p) n -> p k n", p=128))
        k.dma(gcq[:], T['gcqT'][:, :])
        k.dma(gckv[:], T['gckvT'][:, :])
        k.dma(gqk[:], T['gqk'][0:1, :].partition_broadcast(128))
        cosv = T['cosT'].rearrange("p (a b) -> p a b", b=16)
        sinv = T['sinT'].rearrange("p (a b) -> p a b", b=16)
        k.dma(csT[:, :, 0, :], cosv)
        k.dma(csT[:, :, 1, :], sinv)
        k.tt('pool', Wuq[:], Wuq[:], bcast(gcq[:, :], 2, 768), ALU.mult)
        k.tt('pool', Wukv[:], Wukv[:], bcast(gckv[:, :], 2, 1024), ALU.mult)

        kT = ph.sb("kT", [128, 8, L], BF)
        vx = ph.sb("vx", [128, NT, 8, 65], BF)
        k.memset('pool', vx[:, :, :, 64:65], 1.0)
        ones = ph.sb("ones", [128, 64], BF)
        k.memset('pool', ones[:], 1.0)
        hbuf = [ph.sb(f"hbuf{i}", [128, 8, 512], BF) for i in range(2)]
        junk = [ph.sb(f"junk{i}", [128, 384], BF) for i in range(3)]
        ssl = ph.sb("ssl", [128, 2 * NT], F32)
        rsl = ph.sb("rsl", [128, 2 * NT], F32)
        tsl = ph.sb("tsl", [128, 2 * NT], F32)
        latn = [ph.sb(f"latn{i}", [128, 384], BF) for i in range(3)]
        latT = [ph.sb(f"latT{i}", [128, 3, 128], BF) for i in range(3)]
        kr = [ph.sb(f"kr{i}", [128, 32], F32) for i in range(3)]
        qk32 = [ph.sb(f"qk32_{i}", [128, 8, 96], F32) for i in range(3)]
        qkbf = [ph.sb(f"qkbf{i}", [128, 8, 96], BF) for i in range(3)]
        Wk_ = [dict(sq=ph.sb(f"sq{i}", [128, 8, 96], F32), ssq=ph.sb(f"ssq{i}", [128, 8], F32),
                    rk=ph.sb(f"rk{i}", [128, 8], F32), tmp=ph.sb(f"tmpn{i}", [128, 8], F32),
                    ta=ph.sb(f"ta{i}", [128, 2, 8, 16], F32), tb=ph.sb(f"tb{i}", [128, 2, 8, 16], F32)) for i in range(3)]
        qT = [ph.sb(f"qT{i}", [128, 8, 512], BF) for i in range(2)]
        pt = [ph.sb(f"pt{i}", [128, 512], BF) for i in range(3)]
        rsum = [ph.sb(f"rsum{i}", [128, 512], BF) for i in range(2)]
        bcs = [ph.sb(f"bcs{i}", [64, 512], BF) for i in range(2)]
        at = [ph.sb(f"at{i}", [64, 8, 512], BF) for i in range(1)]

        def sumsq(i, col, src_ps, n):
            k.act(junk[i % 3][:, 0:n], src_ps, AF.Square, accum_out=ssl[:, col:col + 1])
            rstd_pool(k, rsl[:, col:col + 1], ssl[:, col:col + 1], n, mhalf[:, 0:1], tsl[:, col:col + 1])

        def kv_tile(i, hb, j):
            par = i % 3
            if j == 0:
                k.dma(hb[:].rearrange("p c t -> p (c t)"), T['hT_d'][i // 4])
            lat = ps[par][:, 0:288]
            for c in range(8):
                k.mm(lat, hb[:, c, 128 * j:128 * (j + 1)], Wkv[:, c, :], start=(c == 0), stop=(c == 7))
            sumsq(i, i, lat[:, 0:256], 256)
            yield
            ln = latn[par]
            k.act(ln[:, 0:256], lat[:, 0:256], AF.Copy, scale=rsl[:, i:i + 1])
            k.copy('act', kr[par][:], lat[:, 256:288])
            tbank = ps[7][:, :].bitcast(BF)
            for c in range(2):
                k.tr(tbank[:, 128 * c:128 * (c + 1)], ln[:, 128 * c:128 * (c + 1)], ident[:])
            lT = latT[par]
            k.copy('dve', lT[:, 0:2, :].rearrange("p c t -> p (c t)"), tbank[:, 0:256])
            yield
            kvb = [ps[3 + 2 * (i % 2)], ps[4 + 2 * (i % 2)]]
            for half in range(2):
                for c in range(2):
                    k.mm(kvb[half][:, :], lT[:, c, :], Wukv[:, c, 512 * half:512 * (half + 1)],
                         start=(c == 0), stop=(c == 1))
            kk = qk32[par]
            for half in range(2):
                kvv = kvb[half][:, :].rearrange("p (h e) -> p h e", h=4)
                k.copy('dve', kk[:, 4 * half:4 * half + 4, 0:64], kvv[:, :, 0:64])
                k.copy('dve', vx[:, i, 4 * half:4 * half + 4, 0:64], kvv[:, :, 64:128])
            k.copy('pool', kk[:, :, 64:96], bcast(kr[par][:, :], 1, 8))
            yield
            kf = qkbf[par]
            yield from qk_norm_rope(ph, Wk_[par], kk, kf, gqk[:, 96:192], csT[:, i, :, :], csT[:, i, ::-1, :], mhalf, use_act=True)
            kbank = ps[7][:, :].bitcast(BF)
            for h in range(8):
                k.tr(kbank[0:96, 128 * h:128 * (h + 1)], kf[:, h, :], ident[:])
            k.copy('dve', kT[0:96, :, 128 * i:128 * (i + 1)], kbank[0:96, :].rearrange("p (h t) -> p h t", h=8))
            yield

        def q_tile(i, hb, j, q_T, bk=None):
            par = i % 2
            bA, bB = bk if bk is not None else (ps[6], ps[7])
            lat = bA[:, 0:384]
            for c in range(8):
                k.mm(lat, hb[:, c, 128 * j:128 * (j + 1)], Wq[:, c, :], start=(c == 0), stop=(c == 7))
            yield
            lsb = Wk_[par]['sq'][:].rearrange("p a b -> p (a b)")[:, 0:384]
            k.copy('dve', lsb, lat)
            k.P.dve(lambda e: e.scalar_tensor_tensor(out=junk[par][:, 0:384], in0=lsb, scalar=1.0, in1=lsb, op0=ALU.mult,
                                                     op1=ALU.mult, accum_out=ssl[:, NT + i:NT + i + 1]),
                    [lsb], [junk[par][:, 0:384], ssl[:, NT + i:NT + i + 1]])
            rstd_pool(k, rsl[:, NT + i:NT + i + 1], ssl[:, NT + i:NT + i + 1], 384, mhalf[:, 0:1], tsl[:, NT + i:NT + i + 1])
            yield
            ln = latn[par]
            k.ts('dve', ln[:, 0:384], lsb, rsl[:, NT + i:NT + i + 1])
            yield
            tbank = bB[:, :].bitcast(BF)
            for c in range(3):
                k.tr(tbank[:, 128 * c:128 * (c + 1)], ln[:, 128 * c:128 * (c + 1)], ident[:])
            yield
            lT = latT[par]
            k.copy('dve', lT[:].rearrange("p c t -> p (c t)"), tbank[:, 0:384])
            yield
            qq = qk32[par]
            for half in range(2):
                qb = bA[:, 0:384]
                for c in range(3):
                    k.mm(qb, lT[:, c, :], Wuq[:, c, 384 * half:384 * (half + 1)], start=(c == 0), stop=(c == 2))
                yield
                k.copy('dve', qq[:, 4 * half:4 * half + 4, :], qb.rearrange("p (h e) -> p h e", h=4))
                yield
            qf = qkbf[par]
            yield from qk_norm_rope(ph, Wk_[par], qq, qf, gqk[:, 0:96], csT[:, i, :, :], csT[:, i, ::-1, :], mhalf, use_act=(i < 4))
            yield
            yield
            yield
            yield
            qbank = bB[:, :].bitcast(BF)
            for h in range(8):
                k.tr(qbank[0:96, 128 * h:128 * (h + 1)], qf[:, h, :], ident[:])
            yield
            k.copy('dve', q_T[0:96, :, 128 * j:128 * (j + 1)], qbank[0:96, :].rearrange("p (h t) -> p h t", h=8))
            yield

        def kv_block(tb):
            hb = hbuf[tb % 2]
            return [kv_tile(tb * 4 + j, hb, j) for j in range(4)]

        def q_chunk_gens(qc, two_sets=False):
            hb = hbuf[qc % 2]
            k.dma(hb[:].rearrange("p c t -> p (c t)"), T['hT_d'][qc])
            bks = [(ps[6], ps[7]), (ps[0], ps[1])]
            return [q_tile(qc * 4 + j, hb, j, qT[qc % 2], bks[j % 2] if two_sets else None) for j in range(4)]

        gens = []
        for tb in range(DBG.get('nprep', NB)):
            gens += kv_block(tb)
        run_interleaved(gens, 3)
        nqc = DBG.get('nqc', NB)
        if nqc:
            run_interleaved(q_chunk_gens(0, two_sets=True), 2)

        scale = 1.0 / math.sqrt(96.0)
        NH = DBG.get('nh', 8)
        pend = [None]
        for qc in range(nqc):
            q_T = qT[qc % 2]
            a_t = at[0]
            nxt = q_chunk_gens(qc + 1) if qc + 1 < nqc else []
            nxt_active = []
            steps = [(h, kt) for h in range(NH) for kt in range(NT)]

            def S(idx):
                h, kt = steps[idx]
                k.mm(ps[idx % 3][:, :], kT[0:96, h, 128 * kt:128 * (kt + 1)], q_T[0:96, h, :])

            def fin_a(h):
                k.recip(rsum[h % 2][64:65, :], ps[3 + (h % 2)][64:65, :])

            def fin_b(h):
                k.mm(ps[5][0:64, :], ones[64:65, :], rsum[h % 2][64:65, :])

            def fin_c(h, a_t):
                k.copy('dve', bcs[h % 2][:], ps[5][0:64, :])
                k.tt('dve', a_t[:, h, :], ps[3 + (h % 2)][0:64, :], bcs[h % 2][:], ALU.mult)

            S(0)
            S(1)
            for idx, (h, kt) in enumerate(steps):
                p_t = pt[idx % 3]
                k.act(p_t[:], ps[idx % 3][:, :], AF.Exp, scale=scale)
                if idx + 2 < len(steps):
                    S(idx + 2)
                ob = ps[3 + (h % 2)]
                k.mm(ob[0:65, :], vx[:, kt, h, :], p_t[:], start=(kt == 0), stop=(kt == NT - 1))
                if pend[0] is not None:
                    if kt == 1:
                        pend[0][0]()
                    elif kt == 10:
                        pend[0][1]()
                    elif kt == 14:
                        pend[0][2]()
                        pend[0] = None
                if kt == NT - 1:
                    last = (h == NH - 1)
                    pend[0] = (lambda h=h: fin_a(h), lambda h=h: fin_b(h),
                               (lambda h=h, a_t=a_t, qc=qc, last=last, fin_c=fin_c: (fin_c(h, a_t), k.dma(T['at_d'][qc], a_t[:].rearrange("p h t -> p (h t)")) if last else None)))
                if idx % 3 == 2:
                    while nxt and len(nxt_active) < 1:
                        nxt_active.append(nxt.pop(0))
                    for g in list(nxt_active):
                        try:
                            next(g)
                        except StopIteration:
                            nxt_active.remove(g)
            run_interleaved(nxt_active + nxt, 1)

        if pend[0] is not None:
            pend[0][0]()
            pend[0][1]()
            pend[0][2]()


def phase4(nc, T):
    with Phase(nc, "p4") as ph:
        k = ph.k
        ps = ph.ps
        w_in_v = T['w_in'].rearrange("(k p) n -> p k n", p=128)
        Wz = ph.sb("Wz", [128, 8, 512], BF)
        Wg = ph.sb("Wg", [128, 8, 2048], BF)
        Wao = ph.sb("Wao", [128, 4, D], BF)
        Who = ph.sb("Who", [128, 4, D], BF)
        Wout = ph.sb("Wout", [128, 8, D], BF)
        bg = ph.sb("bg", [128, 16], F32)
        load_w(ph, Wz[:], w_in_v[:, :, COL_ZA:COL_ZA + 512])
        for q4 in range(4):
            load_w(ph, Wg[:, :, 512 * q4:512 * (q4 + 1)], w_in_v[:, :, COL_GH + 512 * q4:COL_GH + 512 * (q4 + 1)])
        load_w(ph, Wao[:], T['w_attn_out'].rearrange("(hp p) n -> p hp n", p=128))
        load_w(ph, Who[:], T['w_hy_out'].rearrange("(k p) n -> p k n", p=128))
        load_w(ph, Wout[:], T['w_out'].rearrange("(k p) n -> p k n", p=128))
        k.dma(bg[:], T['bgT'][:, :])
        hbuf = [ph.sb(f"hbuf{i}", [128, 8, 512], BF) for i in range(2)]
        atb = [ph.sb(f"atb{i}", [128, 4, 512], BF) for i in range(2)]
        yzb = [ph.sb(f"yzb{i}", [128, 4, 512], BF) for i in range(2)]
        xt = [ph.sb(f"xt{i}", [128, D], F32) for i in range(4)]
        ot = [ph.sb(f"ot{i}", [128, D], F32) for i in range(2)]
        sz = [ph.sb(f"sz{i}", [128, 512], BF) for i in range(2)]
        ya = ph.sb("ya", [128, 4, 512], BF)
        gh = [ph.sb(f"gh{i}", [128, 512], BF) for i in range(2)]
        ga = [ph.sb(f"ga{i}", [128, 512], BF) for i in range(2)]
        m1 = [ph.sb(f"m1{i}", [128, 512], F32) for i in range(2)]
        m2 = [ph.sb(f"m2{i}", [128, 512], F32) for i in range(2)]
        mg = [ph.sb(f"mg{i}", [128, 8, 512], BF) for i in range(2)]
        def load_block(tb):
            hb = hbuf[tb % 2]
            a_b = atb[tb % 2]
            y_b = yzb[tb % 2]
            k.dma(hb[:].rearrange("p c t -> p (c t)"), T['hT_d'][tb])
            atv = T['at_d'][tb].rearrange("p (hp two t) -> p two hp t", two=2, t=512)
            k.dma(a_b[0:64, :, :], atv[:, 0, :, :])
            k.dma(a_b[64:128, :, :], atv[:, 1, :, :])
            k.dma(y_b[:].rearrange("p c t -> p (c t)"), T['yz_d'][tb])

        load_block(0)
        for tb in range(NB):
            hb = hbuf[tb % 2]
            a_b = atb[tb % 2]
            y_b = yzb[tb % 2]
            for j in range(4):
                k.dma(xt[j][:], T['x'][128 * (tb * 4 + j):128 * (tb * 4 + j + 1), :])
            for hp in range(4):
                zb = ps[hp % 2][:, :]
                for c in range(8):
                    k.mm(zb, Wz[:, c, 128 * hp:128 * (hp + 1)], hb[:, c, :], start=(c == 0), stop=(c == 7))
                s_z = sz[hp % 2]
                k.act(s_z[:], zb, AF.Silu)
                k.tt('pool', ya[:, hp, :], a_b[:, hp, :], s_z[:], ALU.mult)
            m_g = mg[tb % 2]
            for dc in range(8):
                g1 = ps[2 + (dc % 2)]
                g2 = ps[4 + (dc % 2)]
                for c in range(8):
                    k.mm(g1[:, :], Wg[:, c, 128 * dc:128 * (dc + 1)], hb[:, c, :], start=(c == 0), stop=(c == 7))
                for c in range(8):
                    k.mm(g2[:, :], Wg[:, c, 1024 + 128 * dc:1024 + 128 * (dc + 1)], hb[:, c, :],
                         start=(c == 0), stop=(c == 7))
                k.act(gh[dc % 2][:], g1[:, :], AF.Sigmoid, bias=bg[:, dc:dc + 1])
                k.act(ga[dc % 2][:], g2[:, :], AF.Sigmoid, bias=bg[:, 8 + dc:9 + dc])
                uh = ps[6]
                ua = ps[7]
                for c in range(4):
                    k.mm(uh[:, :], Who[:, c, 128 * dc:128 * (dc + 1)], y_b[:, c, :], start=(c == 0), stop=(c == 3))
                for hp in range(4):
                    k.mm(ua[:, :], Wao[:, hp, 128 * dc:128 * (dc + 1)], ya[:, hp, :], start=(hp == 0), stop=(hp == 3))
                k.tt('dve', m1[dc % 2][:], uh[:, :], gh[dc % 2][:], ALU.mult)
                k.tt('dve', m2[dc % 2][:], ua[:, :], ga[dc % 2][:], ALU.mult)
                k.tt('pool', m_g[:, dc, :], m1[dc % 2][:], m2[dc % 2][:], ALU.add)
            if tb + 1 < NB:
                load_block(tb + 1)
            for j in range(4):
                i = tb * 4 + j
                x_t = xt[j]
                o_t = ot[i % 2]
                for half in range(2):
                    fb = ps[half]
                    for c in range(8):
                        k.mm(fb[:, :], m_g[:, c, 128 * j:128 * (j + 1)], Wout[:, c, 512 * half:512 * (half + 1)],
                             start=(c == 0), stop=(c == 7))
                    k.tt('dve', o_t[:, 512 * half:512 * (half + 1)], fb[:, :], x_t[:, 512 * half:512 * (half + 1)], ALU.add)
                k.dma(T['out'][128 * i:128 * (i + 1), :], o_t[:])


def fft_constants():
    C = {}
    n = NF
    s2 = np.arange(128, dtype=np.float64)[:, None]
    f2 = np.arange(128, dtype=np.float64)[None, :]
    th = 2 * np.pi * (f2 + 0.5) * s2 / 256.0
    C['FA1'] = np.concatenate([np.cos(th), -np.sin(th)], 1)
    th2 = 2 * np.pi * (f2 + 0.5) * (s2 + 128) / 256.0
    C['FA2'] = -np.concatenate([np.cos(th2), -np.sin(th2)], 1)
    s1 = np.arange(32, dtype=np.float64)
    tw = np.exp(-2j * np.pi * (np.arange(128)[None, :] + 0.5) * s1[:, None] / n)
    twq = np.tile(tw, (4, 1))
    C['TWa'] = np.concatenate([twq.real, twq.real], 1)
    C['TWb'] = np.concatenate([-twq.imag, twq.imag], 1)
    W = np.exp(-2j * np.pi * np.outer(s1, s1) / 32.0)
    Wq = np.kron(np.eye(4), W)
    C['WBr'] = Wq.real
    C['WBi'] = Wq.imag
    C['WBni'] = -Wq.imag
    Wi = np.exp(2j * np.pi * np.outer(s1, s1) / 32.0)
    Wiq = np.kron(np.eye(4), Wi)
    C['WI1'] = np.concatenate([Wiq.real, Wiq.imag], 1)
    C['WI2'] = np.concatenate([-Wiq.imag, Wiq.real], 1)
    twi = np.exp(2j * np.pi * (np.arange(128)[:, None] + 0.5) * s1[None, :] / n)
    twiq = np.tile(twi, (1, 4))
    C['TIa'] = np.concatenate([twiq.real, twiq.real], 1)
    C['TIb'] = np.concatenate([-twiq.imag, twiq.imag], 1)
    t2 = np.arange(128, dtype=np.float64)[None, :]
    f2c = np.arange(128, dtype=np.float64)[:, None]
    th3 = 2 * np.pi * (f2c + 0.5) * t2 / 256.0
    C['FIr'] = (2.0 / n) * np.cos(th3)
    C['FIi'] = -(2.0 / n) * np.sin(th3)
    return C


def filter_constants():
    C = {}
    f32 = np.float32
    t = np.linspace(0.0, 1.0, L, dtype=f32)[:, None]
    bands = 16
    f = np.linspace(1e-4, bands - 1, bands, dtype=f32)
    ang = (f32(2.0 * np.pi / L) * np.arange(L, dtype=f32)[:, None] * f[None, :]).astype(f32)
    z = np.concatenate([t, np.cos(ang).astype(f32), -np.sin(ang).astype(f32)], axis=-1).astype(f32)
    zs = np.zeros((128, L), f32)
    zs[0:33, :] = z.T
    zs[64:97, :] = z[::-1].T
    hi = zs.astype(ml_dtypes.bfloat16)
    lo = (zs - hi.astype(f32)).astype(ml_dtypes.bfloat16)
    C['zs_hi'] = hi
    C['zs_lo'] = lo
    tl = t[:, 0]
    tf = np.zeros((128, 2, 32), f32)
    pidx = np.arange(128)[:, None] * 32 + np.arange(32)[None, :]
    tf[:, 0, :] = tl[pidx]
    tf[:, 1, :] = tl[4095 - pidx]
    C['tfull'] = tf.reshape(128, 64)
    MIN_DECAY = math.log(1e-2) / 1.5
    MAX_DECAY = math.log(1e-2) / 0.3
    deltas = np.abs(np.linspace(MIN_DECAY, MAX_DECAY, HYW, dtype=f32)).astype(f32)
    C['negd'] = (-deltas)[None, :].astype(f32)
    return C


def _sin_layer(ph, W, pre_ps, fr, fb, out32):
    k = ph.k
    a, kk = W['a'], W['kk']
    k.ts('dve', a[:], pre_ps, fr, fb, op0=ALU.mult, op1=ALU.add)
    yield
    k.ts('dve', kk[:], a[:], 1.0 / (2 * math.pi), MAGIC, op0=ALU.mult, op1=ALU.add)
    yield
    k.ts('dve', kk[:], kk[:], -MAGIC, None, op0=ALU.add)
    yield
    k.stt(a[:], kk[:], -2 * math.pi, a[:], ALU.mult, ALU.add)
    yield
    k.ts('dve', a[:], a[:], -3.14159, 3.14159, op0=ALU.max, op1=ALU.min)
    yield
    k.act(out32, a[:], AF.Sin)
    yield


def _hilo(ph, hi, lo, src32, tmp32):
    k = ph.k
    k.copy('dve', hi, src32)
    k.copy('pool', tmp32, hi)
    k.tt('pool', lo, src32, tmp32, ALU.subtract)


def phase2a_gen(ph, T, banks):
    k = ph.k
    zs_hi = ph.sb("zs_hi", [128, L], BF)
    zs_lo = ph.sb("zs_lo", [128, L], BF)
    W1 = ph.sb("W1", [128, 128], F32)
    W2 = ph.sb("W2", [128, 128], F32)
    W1h = ph.sb("W1h", [128, 128], BF)
    W1l = ph.sb("W1l", [128, 128], BF)
    W2h = ph.sb("W2h", [128, 128], BF)
    W2l = ph.sb("W2l", [128, 128], BF)
    wt = ph.sb("wt", [128, 128], F32)
    mv = ph.sb("mv", [128, 4], F32)
    fb = ph.sb("fb", [128, 2], F32)
    S = [dict(a=ph.sb(f"a{i}", [128, 512], F32), kk=ph.sb(f"kk{i}", [128, 512], F32), h1=ph.sb(f"h1_{i}", [128, 512], F32),
              h1h=ph.sb(f"h1h{i}", [128, 512], BF), h1l=ph.sb(f"h1l{i}", [128, 512], BF), t32=ph.sb(f"t32_{i}", [128, 512], F32),
              h2=ph.sb(f"h2_{i}", [128, 512], F32), h2b=ph.sb(f"h2b{i}", [128, 512], BF)) for i in range(2)]
    k.dma(zs_hi[:], T['zs_hi'][:, :])
    k.dma(zs_lo[:], T['zs_lo'][:, :])
    k.dma(W1[:], T['W1blk'][:, :])
    k.dma(W2[:], T['W2blk'][:, :])
    k.dma(mv[:], T['mlpv'][:, :])
    _hilo(ph, W1h[:], W1l[:], W1[:], wt[:])
    _hilo(ph, W2h[:], W2l[:], W2[:], wt[:])
    k.tt('dve', fb[:, 0:1], mv[:, 0:1], mv[:, 1:2], ALU.mult)
    k.tt('dve', fb[:, 1:2], mv[:, 2:3], mv[:, 3:4], ALU.mult)
    yield

    def chunk(cch):
        s_ = S[cch % 2]
        sl = slice(512 * cch, 512 * (cch + 1))
        b1 = banks[cch % 2]
        k.mm(b1[:, :], W1h[:], zs_hi[:, sl], start=True, stop=False)
        k.mm(b1[:, :], W1h[:], zs_lo[:, sl], start=False, stop=False)
        k.mm(b1[:, :], W1l[:], zs_hi[:, sl], start=False, stop=True)
        yield
        yield from _sin_layer(ph, s_, b1[:, :], mv[:, 0:1], fb[:, 0:1], s_['h1'][:])
        _hilo(ph, s_['h1h'][:], s_['h1l'][:], s_['h1'][:], s_['t32'][:])
        yield
        b2 = banks[2 + cch % 2]
        k.mm(b2[:, :], W2h[:], s_['h1h'][:], start=True, stop=False)
        k.mm(b2[:, :], W2h[:], s_['h1l'][:], start=False, stop=False)
        k.mm(b2[:, :], W2l[:], s_['h1h'][:], start=False, stop=True)
        yield
        yield from _sin_layer(ph, s_, b2[:, :], mv[:, 2:3], fb[:, 1:2], s_['h2'][:])
        k.copy('pool', s_['h2b'][:], s_['h2'][:])
        k.dma(T['h2_d'][:, sl], s_['h2b'][:])
        yield

    gens = [chunk(c) for c in range(NB)]
    active = []
    while gens or active:
        while gens and len(active) < 2:
            active.append(gens.pop(0))
        for g in list(active):
            try:
                next(g)
            except StopIteration:
                active.remove(g)
        yield


def _cmul_tab(ph, W, src, Ta, Tb, out_bf):
    k = ph.k
    P1, P2 = W
    sw = src.rearrange("p (r f) -> p r f", r=2)[:, ::-1, :]
    k.tt('dve', P1[:], src, Ta, ALU.mult)
    k.tt('dve', P2[:].rearrange("p (r f) -> p r f", r=2), sw, Tb.rearrange("p (r f) -> p r f", r=2), ALU.mult)
    k.tt('pool', out_bf, P1[:], P2[:], ALU.add)


def phase2b(nc, T):
    with Phase(nc, "p2b") as ph:
        k = ph.k
        ps = ph.ps
        ident = ph.sb("ident", [128, 128], BF)
        make_ident(ph, ident)
        w_in_v = T['w_in'].rearrange("(k p) n -> p k n", p=128)
        cols = (COL_V, COL_X1, COL_X2, COL_ZH)
        ar = ph.sb("arena", [128, 24592], BF)
        Wblk = ar[:, 20486:24582].rearrange("p (k w c) -> p k w c", k=8, w=4)
        for w in range(4):
            load_w(ph, Wblk[:, :, w, :], w_in_v[:, :, cols[w]:cols[w] + 128])
        cb16 = {}
        for nm, w in (('FA1', 256), ('FA2', 256), ('WBr', 128), ('WBi', 128), ('WBni', 128), ('WI1', 256), ('WI2', 256),
                      ('FIr', 128), ('FIi', 128)):
            cb16[nm] = ph.sb(nm, [128, w], BF)
            k.dma(cb16[nm][:], T[nm][:, :])
        c32 = {}
        for nm in ('TWa', 'TWb', 'TIa', 'TIb'):
            c32[nm] = ph.sb(nm, [128, 256], F32)
            k.dma(c32[nm][:], T[nm][:, :])
        h2s = ph.sb("h2s", [128, L], BF)
        k.dma(h2s[:], T['h2_d'][:, :])
        h2p = ph.sb("h2p", [128, 32, 128], BF)
        k.copy('pool', h2p[:], h2s[:].rearrange("q (p s) -> q s p", s=32))
        W3 = ph.sb("W3", [128, 2048], BF)
        load_w(ph, W3[:], T['W3blk'][:, :])
        wsh = ph.sb("wsh", [128, 12, 4], F32)
        k.dma(wsh[:].rearrange("p a b -> p (a b)"), T['wsh'][:, :])
        biasT = ph.sb("biasT", [128, 2, 128], F32)
        k.dma(biasT[:].rearrange("p a b -> p (a b)"), T['biasT'][:, :])
        negd = ph.sb("negd", [128, HYW], F32)
        k.dma(negd[:], T['negd'][0:1, :].partition_broadcast(128))
        tfull = ph.sb("tfull", [128, 2, 32], F32)
        k.dma(tfull[:].rearrange("p a b -> p (a b)"), T['tfull'][:, :])

        hbuf = [ph.sb(f"hbuf{i}", [128, 8, 512], BF) for i in range(2)]
        raw = [ar[:, 4098 * i:4098 * (i + 1)] for i in range(3)]
        ub_ = [ar[:, 12294 + 4096 * i:12294 + 4096 * (i + 1)] for i in range(2)]
        k_tm = ar[:, 0:8192].rearrange("p (o d c s) -> p o d c s", o=2, d=2, c=64)
        AB = ar[:, 8192:12288].rearrange("p (d c s) -> p d c s", d=2, c=64)
        Ksp = ar[:, 12288:20480].rearrange("p (o g r f) -> p o g r f", o=2, g=16, r=2)
        Gbuf = ar[:, 20480:24576].rearrange("p (r c s) -> p r c s", r=2, c=64)
        sz = ph.sb("sz", [128, L], BF)
        tm = [ph.sb(f"tm{i}", [128, 128, 32], BF) for i in range(3)]
        z2_tm = ph.sb("z2_tm", [128, 128, 32], BF)
        y_sc = ph.sb("y_sc", [128, 32, 128], BF)
        yzb = ph.sb("yzb", [128, L], BF)
        arg32 = ph.sb("arg32", [128, 4096], F32)
        PW = [(ph.sb(f"P1_{i}", [128, 256], F32), ph.sb(f"P2_{i}", [128, 256], F32)) for i in range(2)]
        Zp = [ph.sb(f"Zp{i}", [128, 256], BF) for i in range(2)]
        Yb = [ph.sb(f"Yb{i}", [128, 256], BF) for i in range(2)]
        Kev = [ph.sb(f"Kev{i}", [128, 256], BF) for i in range(2)]
        Esb = [[ph.sb(f"E{st}_{i}", [128, 256], BF) for i in range(2)] for st in range(3)]
        cnt = [0]

        ZA = [ph.sb(f"ZA{i}", [128, 512], BF) for i in range(2)]
        ZB = [ph.sb(f"ZB{i}", [128, 512], BF) for i in range(2)]
        YA = [ph.sb(f"YA{i}", [128, 512], BF) for i in range(2)]
        YB = [ph.sb(f"YB{i}", [128, 512], BF) for i in range(2)]
        G2 = ph.sb("G2", [128, 2, 64, 32], BF)
        WI1n = ph.sb("WI1n", [128, 256], BF)
        k.ts('pool', WI1n[:], cb16['WI1'][:], -1.0, None, op0=ALU.mult)

        def v4(ap):
            return ap.rearrange("p (u r f) -> p u r f", u=2, r=2)

        def tab4(t):
            return bcast(t.rearrange("p (r f) -> p r f", r=2), 1, 2)

        def run_skewed(items, hook=None):
            n = len(items)
            depth = max(len(it) for it in items)
            for t in range(n + depth - 1):
                for s_ in reversed(range(depth)):
                    i = t - s_
                    if 0 <= i < n and s_ < len(items[i]):
                        items[i][s_](i)
                if hook is not None:
                    hook(t)

        def cmul_pair(bank, Ta, Tb, outA, outB):
            k.tt('dve', outA, v4(bank), tab4(Ta), ALU.mult)
            k.tt('dve', outB, v4(bank)[:, :, ::-1, :], tab4(Tb), ALU.mult)

        def st_za(lhs_of, q):
            def f(i):
                bank = ps[i % 2]
                for u in range(2):
                    l1, l2 = lhs_of(2 * q + u)
                    za = bank[:, 256 * u:256 * (u + 1)]
                    k.mm(za, l1, cb16['FA1'][:], start=True, stop=(l2 is None))
                    if l2 is not None:
                        k.mm(za, l2, cb16['FA2'][:], start=False, stop=True)
            return f

        def st_tw(i):
            cmul_pair(ps[i % 2][:, :], c32['TWa'][:], c32['TWb'][:], v4(ZA[i % 2][:]), v4(ZB[i % 2][:]))

        def st_ub(i):
            bank = ps[2 + i % 2]
            for u in range(2):
                ub = bank[:, 256 * u:256 * (u + 1)]
                for n_, z_p in enumerate((ZA[i % 2], ZB[i % 2])):
                    zr = z_p[:, 256 * u:256 * u + 128]
                    zi = z_p[:, 256 * u + 128:256 * (u + 1)]
                    k.mm(ub[:, 0:128], cb16['WBr'][:], zr, start=(n_ == 0), stop=False)
                    k.mm(ub[:, 0:128], cb16['WBni'][:], zi, start=False, stop=(n_ == 1))
                for n_, z_p in enumerate((ZA[i % 2], ZB[i % 2])):
                    zr = z_p[:, 256 * u:256 * u + 128]
                    zi = z_p[:, 256 * u + 128:256 * (u + 1)]
                    k.mm(ub[:, 128:256], cb16['WBi'][:], zr, start=(n_ == 0), stop=False)
                    k.mm(ub[:, 128:256], cb16['WBr'][:], zi, start=False, stop=(n_ == 1))

        def filt_gen(cb, hbk, bank_fixed):
            gcol = 128 * cb + 64 * hbk
            k.tt('dve', arg32[:].rearrange("p (d c s) -> p d c s", d=2, c=64),
                 bcast(bcast(negd[:, gcol:gcol + 64], 1, 2), 3, 32),
                 bcast(tfull[:, :, :], 2, 64), ALU.mult)
            yield
            k.act(AB.rearrange("p d c s -> p (d c s)"), arg32[:], AF.Exp)
            yield
            wc0 = 256 * (2 * cb + hbk)
            for s1 in range(32):
                kb_ = (bank_fixed if bank_fixed is not None else ps[6 + s1 % 2])[:, 0:256]
                k.mm(kb_, h2p[:, s1, :], W3[:, wc0:wc0 + 256])
                abv = AB[:, :, :, s1].rearrange("p d c -> p (d c)")
                k.tt('dve', k_tm[:, :, :, :, s1].rearrange("p o d c -> p o (d c)"),
                     kb_.rearrange("p (o x) -> p o x", o=2), bcast(abv, 1, 2), ALU.mult)
                yield

        pref = [None]
        for cb in range(DBG.get('ncb', 4)):
            for w in range(4):
                if cb > 0:
                    load_w(ph, Wblk[:, :, w, :], w_in_v[:, :, cols[w] + 128 * cb:cols[w] + 128 * (cb + 1)])
            for w in range(3):
                k.memset('pool', raw[w][:, 0:1], 0.0)
                k.memset('pool', raw[w][:, 4097:4098], 0.0)
            for tb in range(NB):
                hb = hbuf[tb % 2]
                k.dma(hb[:].rearrange("p c t -> p (c t)"), T['hT_d'][tb])
                for w in range(4):
                    bank = ps[(tb * 4 + w) % 2]
                    for c in range(8):
                        k.mm(bank[:, :], Wblk[:, c, w, :], hb[:, c, :], start=(c == 0), stop=(c == 7))
                    if w < 3:
                        k.act(raw[w][:, 1 + 512 * tb:1 + 512 * (tb + 1)], bank[:, :], AF.Copy)
                    else:
                        k.act(sz[:, 512 * tb:512 * (tb + 1)], bank[:, :], AF.Silu)
            if DBG.get('s2b', 9) < 2: continue
            for w in range(3):
                u = ub_[w % 2]
                j = 4 * w + cb
                k.ts('dve', u, raw[w][:, 1:4097], wsh[:, j, 1:2], wsh[:, j, 3:4], op0=ALU.mult, op1=ALU.add)
                k.stt(u, raw[w][:, 0:4096], wsh[:, j, 0:1], u, ALU.mult, ALU.add)
                k.stt(u, raw[w][:, 2:4098], wsh[:, j, 2:3], u, ALU.mult, ALU.add)
                for a in range(4):
                    pv = ps[2 + a % 2][:, :].bitcast(BF)
                    for e in range(8):
                        s1 = 8 * a + e
                        k.tr(pv[:, 128 * e:128 * (e + 1)], u[:, s1:4096:32], ident[:])
                    k.copy('dve', tm[w][:, :, 8 * a:8 * a + 8], pv.rearrange("p (s c) -> p c s", s=8))
            if DBG.get('dump_tm'):
                k.dma(T['dbg_tm'][:, :], tm[DBG['dump_tm'] - 1][:].rearrange("p c s -> p (c s)"))
            if DBG.get('s2b', 9) < 3: continue
            for hbk in range(DBG.get('nhbk', 2)):
                c0 = 64 * hbk
                gcol = 128 * cb + c0
                if hbk == 0 or not DBG.get('pref', 1):
                    for _ in filt_gen(cb, hbk, None):
                        pass
                else:
                    for _ in pref[0]:
                        pass
                if DBG.get('s2b', 9) < 4: continue
                def spec_lhs(gi):
                    o, g = divmod(gi, 16)
                    return (k_tm[:, o, 0, 4 * g:4 * g + 4, :].rearrange("p c s -> p (c s)"),
                            k_tm[:, o, 1, 4 * g:4 * g + 4, :].rearrange("p c s -> p (c s)"))

                def st_kev(q):
                    def f(i):
                        o, gp = divmod(q, 8)
                        k.copy('act', Ksp[:, o, 2 * gp:2 * gp + 2, :, :].rearrange("p g r f -> p (g r f)"), ps[2 + i % 2][:, :])
                        if gp == 7:
                            gg0 = gcol // 4
                            k.tt('pool', Ksp[:, o, :, 0, :], Ksp[:, o, :, 0, :], bcast(biasT[:, o, gg0:gg0 + 16], 2, 128), ALU.add)
                    return f

                def conv_lhs_of(src):
                    def f(g):
                        return (src[:, c0 + 4 * g:c0 + 4 * g + 4, :].rearrange("p c s -> p (c s)"), None)
                    return f

                def st_mul(o, q):
                    def f(i):
                        bank = ps[2 + i % 2][:, :]
                        kr_ = bcast(Ksp[:, o, 2 * q:2 * q + 2, 0, :], 2, 2)
                        ki_ = bcast(Ksp[:, o, 2 * q:2 * q + 2, 1, :], 2, 2)
                        k.tt('dve', v4(YA[i % 2][:]), v4(bank), kr_, ALU.mult)
                        k.tt('dve', v4(YB[i % 2][:]), v4(bank)[:, :, ::-1, :], ki_, ALU.mult)
                    return f

                def st_gb(i):
                    bank = ps[4 + i % 2]
                    ya, yb_ = YA[i % 2], YB[i % 2]
                    for u in range(2):
                        gb = bank[:, 256 * u:256 * (u + 1)]
                        k.mm(gb, ya[:, 256 * u:256 * u + 128], cb16['WI1'][:], start=True, stop=False)
                        k.mm(gb, yb_[:, 256 * u:256 * u + 128], WI1n[:], start=False, stop=False)
                        k.mm(gb, ya[:, 256 * u + 128:256 * (u + 1)], cb16['WI2'][:], start=False, stop=False)
                        k.mm(gb, yb_[:, 256 * u + 128:256 * (u + 1)], cb16['WI2'][:], start=False, stop=True)

                def st_itw(o, q):
                    def f(i):
                        bank = ps[4 + i % 2][:, :]
                        g1o = Gbuf[:, :, 8 * q:8 * q + 8, :].rearrange("p r (u c) s -> p r u (c s)", u=2)
                        g2o = G2[:, :, 8 * q:8 * q + 8, :].rearrange("p r (u c) s -> p r u (c s)", u=2)
                        k.tt('dve', g1o, v4(bank).rearrange("p u r f -> p r u f"),
                             tab4(c32['TIa'][:]).rearrange("p u r f -> p r u f"), ALU.mult)
                        k.tt('dve', g2o, v4(bank)[:, :, ::-1, :].rearrange("p u r f -> p r u f"),
                             tab4(c32['TIb'][:]).rearrange("p u r f -> p r u f"), ALU.mult)
                    return f

                def st_inva(o, q):
                    def f(i):
                        if q % 2 == 1:
                            cc = q // 2
                            gate = tm[1] if o == 0 else tm[2]
                            yb = ps[6]
                            for n_, gsrc in enumerate((Gbuf, G2)):
                                k.mm(yb[:, :], cb16['FIr'][:], gsrc[:, 0, 16 * cc:16 * cc + 16, :].rearrange("p c s -> p (c s)"),
                                     start=(n_ == 0), stop=False)
                                k.mm(yb[:, :], cb16['FIi'][:], gsrc[:, 1, 16 * cc:16 * cc + 16, :].rearrange("p c s -> p (c s)"),
                                     start=False, stop=(n_ == 1))
                            cs = slice(c0 + 16 * cc, c0 + 16 * cc + 16)
                            if o == 0:
                                k.tt('dve', z2_tm[:, cs, :], yb[:, :].rearrange("p (c s) -> p c s", c=16), gate[:, cs, :], ALU.mult)
                            else:
                                k.tt('dve', y_sc[:, :, cs].rearrange("p s c -> p c s"),
                                     yb[:, :].rearrange("p (c s) -> p c s", c=16), gate[:, cs, :], ALU.mult)
                    return f

                items = []
                for q in range(16):
                    items.append([st_za(spec_lhs, q), st_tw, st_ub, st_kev(q)])
                for o in range(DBG.get('nord', 2)):
                    src = tm[0] if o == 0 else z2_tm
                    if o == 1:
                        items += [[] for _ in range(DBG.get('gap', 0))]
                    for q in range(8):
                        items.append([st_za(conv_lhs_of(src), q), st_tw, st_ub, st_mul(o, q), st_gb, st_itw(o, q), st_inva(o, q)])
                hook = None
                if hbk == 0 and DBG.get('pref', 1):
                    pref[0] = filt_gen(cb, 1, ps[7])

                    def hook(t, g=pref[0]):
                        if t >= 19:
                            next(g, None)
                            next(g, None)
                run_skewed(items, hook)
            if DBG.get('dump_z2'):
                k.dma(T['dbg_tm'][:, :], z2_tm[:].rearrange("p c s -> p (c s)"))
            if DBG.get('s2b', 9) < 6: continue
            for a in range(4):
                pv = ps[2 + a % 2][:, :].bitcast(BF)
                for e in range(8):
                    k.tr(pv[:, 128 * e:128 * (e + 1)], y_sc[:, 8 * a + e, :], ident[:])
                k.tt('dve', yzb[:].rearrange("c (p s) -> c p s", s=32)[:, :, 8 * a:8 * a + 8],
                     pv.rearrange("c (s p) -> c p s", s=8),
                     sz[:].rearrange("c (p s) -> c p s", s=32)[:, :, 8 * a:8 * a + 8], ALU.mult)
            for tb in range(NB):
                k.dma(T['yz_d'][tb][:, 512 * cb:512 * (cb + 1)], yzb[:, 512 * tb:512 * (tb + 1)])


def phase2(nc, T):
    if 'b' in DBG.get('p2', 'ab'):
        phase2b(nc, T)


def _bf(a):
    return np.asarray(a, np.float32).astype(ml_dtypes.bfloat16)


_CONST_CACHE = {}


def host_constants():
    if _CONST_CACHE:
        return _CONST_CACHE
    C = {}
    pos = np.arange(L, dtype=np.float32)
    inv_freq = (np.float32(10000.0) ** (-np.arange(0, 32, 2, dtype=np.float32) / np.float32(32))).astype(np.float32)
    ang = (pos[:, None] * inv_freq[None, :]).astype(np.float32)
    C['cosT'] = np.ascontiguousarray(np.cos(ang).astype(np.float32).reshape(NT, 128, 16).transpose(1, 0, 2).reshape(128, NT * 16))
    C['sinT'] = np.ascontiguousarray(np.sin(ang).astype(np.float32).reshape(NT, 128, 16).transpose(1, 0, 2).reshape(128, NT * 16))
    F = fft_constants()
    for nm in ('FA1', 'FA2', 'WBr', 'WBi', 'WBni', 'WI1', 'WI2', 'FIr', 'FIi'):
        C[nm] = np.ascontiguousarray(_bf(F[nm]))
    for nm in ('TWa', 'TWb', 'TIa', 'TIb'):
        C[nm] = np.ascontiguousarray(F[nm].astype(np.float32))
    C.update(filter_constants())
    _CONST_CACHE.update(C)
    return _CONST_CACHE


def prep_inputs(inp, b):
    f32 = np.float32
    m = {}
    m['x'] = np.ascontiguousarray(inp['x'][b], dtype=f32)
    m['w_in'] = np.ascontiguousarray(inp['w_in'][0], dtype=f32)
    m['gT'] = np.ascontiguousarray(inp['g_norm'][0].reshape(8, 128).T, dtype=f32)
    m['bgT'] = np.ascontiguousarray(inp['b_gate'][0].reshape(16, 128).T, dtype=f32)
    m['w_uq'] = np.ascontiguousarray(inp['w_uq'][0], dtype=f32)
    m['w_ukv'] = np.ascontiguousarray(inp['w_ukv'][0], dtype=f32)
    m['gcqT'] = np.ascontiguousarray(inp['g_cq'][0].reshape(3, 128).T, dtype=f32)
    m['gckvT'] = np.ascontiguousarray(inp['g_ckv'][0].reshape(2, 128).T, dtype=f32)
    m['gqk'] = np.ascontiguousarray(np.concatenate([inp['g_qn'][0], inp['g_kn'][0]])[None, :], dtype=f32)
    m['w_attn_out'] = np.ascontiguousarray(inp['w_attn_out'][0], dtype=f32)
    m['w_hy_out'] = np.ascontiguousarray(inp['w_hy_out'][0], dtype=f32)
    m['w_out'] = np.ascontiguousarray(inp['w_out'][0], dtype=f32)
    wsh = np.zeros((128, 12, 4), f32)
    wsh[:, :, 0:3] = inp['w_short'][0].reshape(3, 12, 128).transpose(2, 1, 0)
    wsh[:, :, 3] = inp['b_short'][0].reshape(12, 128).T
    m['wsh'] = wsh.reshape(128, 48)
    hb = inp['hy_bias'][0]
    bT = hb.reshape(2, 128, 4).transpose(2, 0, 1)
    m['biasT'] = np.ascontiguousarray(np.repeat(bT[:, None], 32, axis=1).reshape(128, 256), dtype=f32)
    W1 = np.zeros((128, 128), f32)
    W1[0:33, 0:64] = inp['w_f1'][0]
    W1[64:97, 64:128] = inp['w_f1'][0]
    m['W1blk'] = W1
    W2 = np.zeros((128, 128), f32)
    W2[0:64, 0:64] = inp['w_f2'][0]
    W2[64:128, 64:128] = inp['w_f2'][0]
    m['W2blk'] = W2
    mv = np.zeros((128, 4), f32)
    for jj, nm in enumerate(('freq_1', 'b_f1', 'freq_2', 'b_f2')):
        mv[0:64, jj] = inp[nm][0]
        mv[64:128, jj] = inp[nm][0]
    m['mlpv'] = mv
    w3 = inp['w_f3'][0].reshape(64, 2, 2, 8, 64)
    W3 = np.zeros((128, 8, 2, 2, 64), f32)
    for dd in range(2):
        W3[64 * dd:64 * (dd + 1), :, :, dd, :] = w3[:, :, dd, :, :].transpose(0, 2, 1, 3)
    m['W3blk'] = W3.reshape(128, 2048)
    C = host_constants()
    for nm in CONST_NAMES:
        m[nm] = C[nm]
    return m


IN_SHAPES = {
    'x': ([L, D], F32), 'w_in': ([D, 5280], F32), 'gT': ([128, 8], F32), 'bgT': ([128, 16], F32),
    'w_uq': ([384, 768], F32), 'w_ukv': ([256, 1024], F32), 'gcqT': ([128, 3], F32), 'gckvT': ([128, 2], F32),
    'gqk': ([1, 192], F32), 'w_attn_out': ([512, D], F32), 'w_hy_out': ([512, D], F32), 'w_out': ([D, D], F32),
    'cosT': ([128, NT * 16], F32), 'sinT': ([128, NT * 16], F32),
    'wsh': ([128, 48], F32), 'biasT': ([128, 256], F32), 'W1blk': ([128, 128], F32), 'W2blk': ([128, 128], F32),
    'mlpv': ([128, 4], F32), 'W3blk': ([128, 2048], F32),
    'FA1': ([128, 256], BF), 'FA2': ([128, 256], BF), 'WBr': ([128, 128], BF), 'WBi': ([128, 128], BF),
    'WBni': ([128, 128], BF), 'WI1': ([128, 256], BF), 'WI2': ([128, 256], BF), 'FIr': ([128, 128], BF),
    'FIi': ([128, 128], BF), 'TWa': ([128, 256], F32), 'TWb': ([128, 256], F32), 'TIa': ([128, 256], F32),
    'TIb': ([128, 256], F32), 'zs_hi': ([128, L], BF), 'zs_lo': ([128, L], BF), 'tfull': ([128, 64], F32),
    'negd': ([1, HYW], F32),
}
CONST_NAMES = ('cosT', 'sinT', 'FA1', 'FA2', 'WBr', 'WBi', 'WBni', 'WI1', 'WI2', 'FIr', 'FIi', 'TWa', 'TWb', 'TIa', 'TIb',
               'zs_hi', 'zs_lo', 'tfull', 'negd')


def build_nc(debug=None):
    debug = debug or set()
    nc = bass.Bass("TRN2", target_bir_lowering=False)
    T = {}
    for name, (shape, dt) in IN_SHAPES.items():
        T[name] = nc.dram_tensor(name, shape, dt, kind="ExternalInput").ap()
    T['out'] = nc.dram_tensor("out", [L, D], F32, kind="ExternalOutput").ap()
    skind = dict(kind="ExternalOutput") if 'dump' in debug else {}
    T['hT_d'] = nc.dram_tensor("hT_d", [NB, 128, 8 * 512], BF, **skind).ap()
    T['at_d'] = nc.dram_tensor("at_d", [NB, 64, 8 * 512], BF, **skind).ap()
    T['h2_d'] = nc.dram_tensor("h2_d", [128, L], BF, **skind).ap()
    if 'dump' in debug:
        T['dbg_tm'] = nc.dram_tensor("dbg_tm", [128, 4096], BF, kind="ExternalOutput").ap()
        T['dbg_k'] = nc.dram_tensor("dbg_k", [128, 8192], BF, kind="ExternalOutput").ap()
        T['dbg_ks'] = nc.dram_tensor("dbg_ks", [128, 8192], BF, kind="ExternalOutput").ap()
    if 'yz_in' in debug:
        T['yz_d'] = nc.dram_tensor("yz_d", [NB, 128, 4 * 512], BF, kind="ExternalInput").ap()
    else:
        T['yz_d'] = nc.dram_tensor("yz_d", [NB, 128, 4 * 512], BF, **skind).ap()
    phases = debug & {'p1', 'p2', 'p3', 'p4'} or {'p1', 'p2', 'p3', 'p4'}
    do_p2 = 'p2' in phases and 'yz_in' not in debug
    if 'p1' in phases or do_p2:
        phase12(nc, T, with_p1=('p1' in phases), with_p2a=do_p2)
    if do_p2:
        phase2(nc, T)
    if 'p3' in phases:
        phase3(nc, T)
    if 'p4' in phases:
        phase4(nc, T)
    return nc


def kernel(**inputs):
    inp = {k_: np.asarray(v) for k_, v in inputs.items()}
    nc = build_nc()
    in_maps = [prep_inputs(inp, b) for b in range(8)]
    res = run_bass_kernel_spmd(nc, in_maps, core_ids=list(range(8)))
    out = np.stack([np.asarray(r['out'], dtype=np.float32) for r in res.results], axis=0)
    return out
```

```python
import concourse.bass as bass
import concourse.mybir as mybir

_ESZ = {}


def _esize(dt):
    s = _ESZ.get(dt)
    if s is None:
        n = str(dt)
        if '32' in n:
            s = 4
        elif '16' in n:
            s = 2
        elif '8' in n:
            s = 1
        else:
            s = 4
        _ESZ[dt] = s
    return s


def footprint(ap):
    t = ap.tensor
    name = t.name
    es = _esize(ap.dtype)
    apl = ap.ap
    off = int(ap.offset) * es
    space = str(type(t).__name__)
    if 'DRam' in space:
        lo = off
        hi = off
        for st, cnt in apl:
            if cnt > 1:
                d = (cnt - 1) * st * es
                if d > 0:
                    hi += d
                else:
                    lo += d
        return (name, 0, 1, lo, hi + es)
    pstep, pcnt = apl[0]
    pstep_b = pstep * es
    if pstep_b > 0:
        p0 = off // pstep_b
        f0 = off % pstep_b
    else:
        p0 = 0
        f0 = off
    lo = f0
    hi = f0
    for st, cnt in apl[1:]:
        if cnt > 1:
            d = (cnt - 1) * st * es
            if d > 0:
                hi += d
            else:
                lo += d
    return (name, p0, p0 + pcnt, lo, hi + es)


COMPUTE = ('pe', 'act', 'dve', 'pool')
QUEUES = ('pe', 'act', 'dve', 'pool', 'sp')
QIDX = {q: i for i, q in enumerate(QUEUES)}


class _Op:
    __slots__ = ('q', 'fn', 'dma', 'idx', 'gid', 'waits_c', 'waits_d', 'signal', 'snap', 'slot', 'slot_cnt', 'prev_slot')


class Prog:
    def __init__(self, nc, dma_slots=None):
        self.nc = nc
        self.streams = {q: [] for q in QUEUES}
        self.recs = {}
        self.known = {q: [-1] * len(QUEUES) for q in QUEUES}
        self.known_dma = {q: set() for q in QUEUES}
        self.ops = []
        self.dma_slots = dma_slots or {'sp': 8, 'pool': 4, 'act': 4}
        self.dma_count = {q: 0 for q in QUEUES}
        self.dma_ops = {q: [] for q in QUEUES}
        self.n_comp = {q: 0 for q in QUEUES}

    def add(self, q, fn, reads=(), writes=(), dma=False):
        op = _Op()
        op.q = q
        op.fn = fn
        op.dma = dma
        op.gid = len(self.ops)
        op.signal = dma
        op.waits_c = []
        op.waits_d = []
        op.slot = None
        op.prev_slot = None
        stream = self.streams[q]
        if not dma:
            op.idx = self.n_comp[q]
            self.n_comp[q] += 1
        else:
            op.idx = -1
        deps_c = {}
        deps_d = set()

        def scan(fp, is_write):
            name, p0, p1, f0, f1 = fp
            lst = self.recs.get(name)
            if not lst:
                return
            for r in lst:
                (rp0, rp1, rf0, rf1, rw, rop) = r
                if not (is_write or rw):
                    continue
                if rp1 <= p0 or p1 <= rp0 or rf1 <= f0 or f1 <= rf0:
                    continue
                if rop.dma:
                    deps_d.add(rop)
                else:
                    e = rop.q
                    if deps_c.get(e, -1) < rop.idx:
                        deps_c[e] = rop.idx

        rfps = [footprint(a) for a in reads]
        wfps = [footprint(a) for a in writes]
        for fp in rfps:
            scan(fp, False)
        for fp in wfps:
            scan(fp, True)
        known = self.known[q]
        kd = self.known_dma[q]
        for e, i in deps_c.items():
            ei = QIDX[e]
            if e == q and not dma:
                if q == 'pe':
                    continue
            if i <= known[ei]:
                continue
            op.waits_c.append((e, i))
            src = self.comp_ops[e][i]
            src.signal = True
            known[ei] = i
            for k, v in enumerate(src.snap):
                if v > known[k]:
                    known[k] = v
        for d in sorted(deps_d, key=lambda o: o.gid):
            if d.gid in kd:
                continue
            op.waits_d.append(d)
            kd.add(d.gid)
            for k, v in enumerate(d.snap):
                if v > known[k]:
                    known[k] = v
        if dma:
            n = self.dma_count[q]
            R = self.dma_slots[q]
            op.slot = n % R
            op.slot_cnt = n // R + 1
            if n >= R:
                prev = self.dma_ops[q][n - R]
                op.prev_slot = prev
                kd.add(prev.gid)
            self.dma_count[q] = n + 1
            self.dma_ops[q].append(op)
        op.snap = tuple(known)
        if not dma:
            self.comp_ops[q].append(op)
        for fp, is_write in [(f, False) for f in rfps] + [(f, True) for f in wfps]:
            name, p0, p1, f0, f1 = fp
            lst = self.recs.setdefault(name, [])
            if is_write:
                lst[:] = [r for r in lst if not (r[0] >= p0 and r[1] <= p1 and r[2] >= f0 and r[3] <= f1)]
            else:
                if not dma:
                    lst[:] = [r for r in lst if not (r[4] is False and (not r[5].dma) and r[5].q == q
                                                     and r[0] == p0 and r[1] == p1 and r[2] == f0 and r[3] == f1)]
            lst.append((p0, p1, f0, f1, is_write, op))
        stream.append(op)
        self.ops.append(op)
        return op

    comp_ops = None

    def start(self):
        self.comp_ops = {q: [] for q in QUEUES}

    def pe(self, fn, reads, writes):
        return self.add('pe', fn, reads, writes)

    def act(self, fn, reads, writes):
        return self.add('act', fn, reads, writes)

    def dve(self, fn, reads, writes):
        return self.add('dve', fn, reads, writes)

    def pool(self, fn, reads, writes):
        return self.add('pool', fn, reads, writes)

    def dma(self, out, in_, q='sp', **kw):
        return self.add(q, lambda e: e.dma_start(out=out, in_=in_, **kw), [in_], [out], dma=True)

    def emit(self, block, sems_c, sems_d):
        cum = {}
        for e in QUEUES:
            c = 0
            arr = []
            for o in self.comp_ops[e]:
                if o.signal:
                    c += 1
                arr.append(c)
            cum[e] = arr
        self.cum = cum

        def gen(q):
            def body(eng):
                for o in self.streams[q]:
                    for (e, i) in o.waits_c:
                        eng.wait_ge(sems_c[e], cum[e][i])
                    for d in o.waits_d:
                        eng.wait_ge(sems_d[d.q][d.slot], 16 * d.slot_cnt)
                    if o.prev_slot is not None:
                        p = o.prev_slot
                        eng.wait_ge(sems_d[p.q][p.slot], 16 * p.slot_cnt)
                    ins = o.fn(eng)
                    if o.dma:
                        ins.then_inc(sems_d[q][o.slot], 16)
                    elif o.signal:
                        ins.then_inc(sems_c[q], 1)
                R = self.dma_slots.get(q, 0)
                n = self.dma_count[q]
                for o in self.dma_ops[q][max(0, n - R):]:
                    eng.wait_ge(sems_d[q][o.slot], 16 * o.slot_cnt)
            return body

        if self.streams['pe']:
            block.tensor(gen('pe'))
        if self.streams['act']:
            block.scalar(gen('act'))
        if self.streams['dve']:
            block.vector(gen('dve'))
        if self.streams['pool']:
            block.gpsimd(gen('pool'))
        if self.streams['sp']:
            block.sync(gen('sp'))

import math
from contextlib import ExitStack
import numpy as np
import ml_dtypes
from concourse.bass_utils import run_bass_kernel_spmd

F32 = mybir.dt.float32
BF = mybir.dt.bfloat16
AF = mybir.ActivationFunctionType
ALU = mybir.AluOpType
AX = mybir.AxisListType

L = 4096
D = 1024
NT = 32
NB = 8
EPS = 1e-6
NF = 8192
HYW = 512
COL_V, COL_X1, COL_X2, COL_ZH = 0, 512, 1024, 1536
COL_CQ, COL_CKV, COL_KR, COL_ZA = 2048, 2432, 2688, 2720
COL_GH, COL_GA = 3232, 4256
MAGIC = 12582912.0
DBG = {}


def bcast(ap, axis, n):
    a = ap.unsqueeze(axis)
    shp = list(a.shape)
    shp[axis] = n
    return a.to_broadcast(shp)


class K:
    def __init__(self, P):
        self.P = P

    def mm(self, out, lhsT, rhs, start=True, stop=True):
        self.P.pe(lambda e: e.matmul(out, lhsT=lhsT, rhs=rhs, start=start, stop=stop), [lhsT, rhs], [out])

    def tr(self, out, in_, ident):
        self.P.pe(lambda e: e.transpose(out=out, in_=in_, identity=ident), [in_, ident], [out])

    def act(self, out, in_, func, bias=None, scale=None, accum_out=None):
        kw = {}
        reads = [in_]
        writes = [out]
        if bias is not None:
            kw['bias'] = bias
            if not isinstance(bias, (int, float)):
                reads.append(bias)
        if scale is not None:
            kw['scale'] = scale
            if not isinstance(scale, (int, float)):
                reads.append(scale)
        if accum_out is not None:
            kw['accum_out'] = accum_out
            writes.append(accum_out)
        self.P.act(lambda e: e.activation(out=out, in_=in_, func=func, **kw), reads, writes)

    def tt(self, eng, out, in0, in1, op):
        self.P.add(eng, lambda e: e.tensor_tensor(out=out, in0=in0, in1=in1, op=op), [in0, in1], [out])

    def ts(self, eng, out, in0, s1, s2=None, op0=ALU.mult, op1=None):
        reads = [in0]
        if not isinstance(s1, (int, float)):
            reads.append(s1)
        if s2 is not None and not isinstance(s2, (int, float)):
            reads.append(s2)
        if op1 is None:
            self.P.add(eng, lambda e: e.tensor_scalar(out=out, in0=in0, scalar1=s1, scalar2=None, op0=op0), reads, [out])
        else:
            self.P.add(eng, lambda e: e.tensor_scalar(out=out, in0=in0, scalar1=s1, scalar2=s2, op0=op0, op1=op1), reads, [out])

    def stt(self, out, in0, scalar, in1, op0, op1):
        reads = [in0, in1]
        if not isinstance(scalar, (int, float)):
            reads.append(scalar)
        self.P.dve(lambda e: e.scalar_tensor_tensor(out=out, in0=in0, scalar=scalar, in1=in1, op0=op0, op1=op1), reads, [out])

    def copy(self, eng, out, in_):
        if eng == 'act':
            self.act(out, in_, AF.Copy)
        else:
            self.P.add(eng, lambda e: e.tensor_copy(out=out, in_=in_), [in_], [out])

    def recip(self, out, in_):
        self.P.dve(lambda e: e.reciprocal(out=out, in_=in_), [in_], [out])

    def reduce_add(self, out, in_):
        self.P.dve(lambda e: e.tensor_reduce(out=out, in_=in_, axis=AX.X, op=ALU.add), [in_], [out])

    def memset(self, eng, ap, val):
        self.P.add(eng, lambda e: e.memset(ap, val), [], [ap])

    def dma(self, out, in_, q='sp'):
        self.P.dma(out, in_, q=q)


def _dump(P):
    cum = {}
    for e in QUEUES:
        c = 0
        arr = []
        for o in P.comp_ops[e]:
            if o.signal:
                c += 1
            arr.append(c)
        cum[e] = arr
    for q in QUEUES:
        print("== stream", q)
        for o in P.streams[q]:
            w = [f"{e}>={cum[e][i]}(op{i})" for e, i in o.waits_c] + [f"dma[{d.q}{d.slot}]>={16*d.slot_cnt}" for d in o.waits_d]
            if o.prev_slot is not None:
                w.append(f"prev dma[{o.prev_slot.q}{o.prev_slot.slot}]>={16*o.prev_slot.slot_cnt}")
            tag = f"DMA slot{o.slot} cnt{o.slot_cnt}" if o.dma else (f"op{o.idx} sig={cum[q][o.idx] if o.signal else '-'}")
            print("   ", tag, getattr(o, 'desc', ''), "waits:", w)


class Phase:
    def __init__(self, nc, name):
        self.nc = nc
        self.name = name
        self.es = ExitStack()

    def __enter__(self):
        nc = self.nc
        es = self.es
        es.__enter__()
        self.ps = [es.enter_context(nc.psum_tensor(f"{self.name}_ps{i}", [128, 512], F32)) for i in range(8)]
        self.sems_c = {e: es.enter_context(nc.semaphore(f"{self.name}_sc_{e}")) for e in QUEUES}
        self.sems_d = {q: [es.enter_context(nc.semaphore(f"{self.name}_sd_{q}{i}")) for i in range(n)]
                       for q, n in (('sp', 8), ('pool', 4), ('act', 4))}
        self.P = Prog(nc)
        self.P.start()
        self.k = K(self.P)
        return self

    def sb(self, name, shape, dt):
        return self.es.enter_context(self.nc.sbuf_tensor(f"{self.name}_{name}", shape, dt))

    def __exit__(self, *a):
        if a[0] is None:
            self.es.enter_context(self.nc.allow_low_precision("bf16 operands / intermediates by design"))
            block = self.es.enter_context(self.nc.Block())
            if DBG.get('dump') == self.name:
                _dump(self.P)
            self.P.emit(block, self.sems_c, self.sems_d)
        return self.es.__exit__(*a)


def make_ident(ph, ident):
    identf = ph.sb("identf", [128, 128], F32)
    ph.k.memset('pool', identf[:], 0.0)
    ph.P.pool(lambda e: e.affine_select(out=identf[:], in_=identf[:], pattern=[[-1, 128]], compare_op=ALU.not_equal,
                                        fill=1.0, base=0, channel_multiplier=1), [identf[:]], [identf[:]])
    ph.k.copy('dve', ident[:], identf[:])


def load_w(ph, dst, src_ap):
    ph.k.dma(dst, src_ap, q='pool')


def rstd_from_ss(k, rs_col, ss_col, n):
    k.act(rs_col, ss_col, AF.Sqrt, bias=EPS, scale=1.0 / n)
    k.recip(rs_col, rs_col)


def phase1_gen(ph, T, banks):
    k = ph.k
    xt = [ph.sb(f"xt{i}", [128, D], F32) for i in range(3)]
    xn = [ph.sb(f"xn{i}", [128, D], BF) for i in range(2)]
    junk = ph.sb("junk", [128, D], BF)
    ss = ph.sb("ss", [128, NT], F32)
    rs = ph.sb("rs", [128, NT], F32)
    ts_ = ph.sb("ts_", [128, NT], F32)
    mh = ph.sb("mh", [128, 1], F32)
    gT = ph.sb("gT", [128, 8], F32)
    ident = ph.sb("ident", [128, 128], BF)
    hb = [ph.sb(f"hb{i}", [128, 8, 512], BF) for i in range(2)]
    make_ident(ph, ident)
    k.memset('pool', mh[:], -0.5)
    k.dma(gT[:], T['gT'][:, :])
    pend = [None]
    for i0 in range(2):
        k.dma(xt[i0 % 3][:], T['x'][128 * i0:128 * (i0 + 1), :])
    for i in range(NT):
        x_t = xt[i % 3]
        if i + 2 < NT:
            k.dma(xt[(i + 2) % 3][:], T['x'][128 * (i + 2):128 * (i + 3), :])
        k.act(junk[:], x_t[:], AF.Square, accum_out=ss[:, i:i + 1])
        rstd_pool(k, rs[:, i:i + 1], ss[:, i:i + 1], D, mh[:, 0:1], ts_[:, i:i + 1])
        x_n = xn[i % 2]
        k.ts('dve', x_n[:], x_t[:], rs[:, i:i + 1])
        bank = banks[i % len(banks)]
        pv = bank[:, :].bitcast(BF)
        for c in range(8):
            k.tr(pv[:, 128 * c:128 * (c + 1)], x_n[:, 128 * c:128 * (c + 1)], ident[:])
        if pend[0] is not None:
            pend[0]()

        def evac(i=i, pv=pv):
            h_b = hb[(i // 4) % 2]
            j = i % 4
            k.tt('dve', h_b[:, :, 128 * j:128 * (j + 1)], pv.rearrange("p (c t) -> p c t", c=8),
                 bcast(gT[:, :], 2, 128), ALU.mult)
            if j == 3:
                k.dma(T['hT_d'][i // 4], h_b[:].rearrange("p c t -> p (c t)"))
        pend[0] = evac
        yield
    pend[0]()
    yield


def rstd_pool(k, rs, ss, n, mhalf, tmp):
    k.ts('dve', tmp, ss, 1.0 / n, EPS, op0=ALU.mult, op1=ALU.add)
    k.tt('pool', rs, tmp, mhalf, ALU.pow)


def phase12(nc, T, with_p1=True, with_p2a=True):
    with Phase(nc, "p12") as ph:
        g1 = phase1_gen(ph, T, ph.ps[0:4]) if with_p1 else iter(())
        g2 = phase2a_gen(ph, T, ph.ps[4:8]) if with_p2a else iter(())
        alive1, alive2 = True, True
        while alive1 or alive2:
            if alive1:
                try:
                    next(g1)
                except StopIteration:
                    alive1 = False
            for _ in range(4):
                if alive2:
                    try:
                        next(g2)
                    except StopIteration:
                        alive2 = False


def run_interleaved(gens, width=2):
    active = []
    gens = list(gens)
    while gens or active:
        while gens and len(active) < width:
            active.append(gens.pop(0))
        for g in list(active):
            try:
                next(g)
            except StopIteration:
                active.remove(g)


def qk_norm_rope(ph, W, src, dst, g_rep, cs_t, sc_t, mhalf, use_act=False):
    k = ph.k
    sq, ssq, rk, ta, tb_, tmp = W['sq'], W['ssq'], W['rk'], W['ta'], W['tb'], W['tmp']
    if use_act:
        k.act(sq[:].rearrange("p a b -> p (a b)"), src[:].rearrange("p a b -> p (a b)"), AF.Square)
    else:
        k.tt('pool', sq[:], src[:], src[:], ALU.mult)
    k.reduce_add(ssq[:], sq[:])
    rstd_pool(k, rk[:], ssq[:], 96, mhalf[:, 0:8], tmp[:])
    yield
    k.tt('dve', src[:], src[:], bcast(rk[:, :], 2, 96), ALU.mult)
    k.tt('dve', src[:], src[:], bcast(g_rep, 1, 8), ALU.mult)
    yield
    t1 = bcast(src[:, :, 64:80], 1, 2)
    t2 = bcast(src[:, :, 80:96], 1, 2)
    k.tt('pool', ta[:], t1, bcast(cs_t, 2, 8), ALU.mult)
    k.tt('pool', tb_[:], t2, bcast(sc_t, 2, 8), ALU.mult)
    k.tt('dve', dst[:, :, 64:80], ta[:, 0], tb_[:, 0], ALU.subtract)
    k.tt('dve', dst[:, :, 80:96], ta[:, 1], tb_[:, 1], ALU.add)
    k.copy('pool', dst[:, :, 0:64], src[:, :, 0:64])
    yield


def phase3(nc, T):
    with Phase(nc, "p3") as ph:
        k = ph.k
        ps = ph.ps
        ident = ph.sb("ident", [128, 128], BF)
        make_ident(ph, ident)
        w_in_v = T['w_in'].rearrange("(k p) n -> p k n", p=128)
        Wkv = ph.sb("Wkv", [128, 8, 288], BF)
        Wq = ph.sb("Wq", [128, 8, 384], BF)
        Wuq = ph.sb("Wuq", [128, 3, 768], BF)
        Wukv = ph.sb("Wukv", [128, 2, 1024], BF)
        gcq = ph.sb("gcq", [128, 3], F32)
        gckv = ph.sb("gckv", [128, 2], F32)
        gqk = ph.sb("gqk", [128, 192], F32)
        csT = ph.sb("csT", [128, NT, 2, 16], F32)
        mhalf = ph.sb("mhalf", [128, 8], F32)
        k.memset('pool', mhalf[:], -0.5)
        load_w(ph, Wkv[:], w_in_v[:, :, COL_CKV:COL_CKV + 288])
        load_w(ph, Wq[:], w_in_v[:, :, COL_CQ:COL_CQ + 384])
        load_w(ph, Wuq[:], T['w_uq'].rearrange("(k p) n -> p k n", p=128))
        load_w(ph, Wukv[:], T['w_ukv'].rearrange("(k p) n -> p k n", p=128))
        k.dma(gcq[:], T['gcqT'][:, :])
        k.dma(gckv[:], T['gckvT'][:, :])
        k.dma(gqk[:], T['gqk'][0:1, :].partition_broadcast(128))
        cosv = T['cosT'].rearrange("p (a b) -> p a b", b=16)
        sinv = T['sinT'].rearrange("p (a b) -> p a b", b=16)
        k.dma(csT[:, :, 0, :], cosv)
        k.dma(csT[:, :, 1, :], sinv)
        k.tt('pool', Wuq[:], Wuq[:], bcast(gcq[:, :], 2, 768), ALU.mult)
        k.tt('pool', Wukv[:], Wukv[:], bcast(gckv[:, :], 2, 1024), ALU.mult)

        kT = ph.sb("kT", [128, 8, L], BF)
        vx = ph.sb("vx", [128, NT, 8, 65], BF)
        k.memset('pool', vx[:, :, :, 64:65], 1.0)
        ones = ph.sb("ones", [128, 64], BF)
        k.memset('pool', ones[:], 1.0)
        hbuf = [ph.sb(f"hbuf{i}", [128, 8, 512], BF) for i in range(2)]
        junk = [ph.sb(f"junk{i}", [128, 384], BF) for i in range(3)]
        ssl = ph.sb("ssl", [128, 2 * NT], F32)
        rsl = ph.sb("rsl", [128, 2 * NT], F32)
        tsl = ph.sb("tsl", [128, 2 * NT], F32)
        latn = [ph.sb(f"latn{i}", [128, 384], BF) for i in range(3)]
        latT = [ph.sb(f"latT{i}", [128, 3, 128], BF) for i in range(3)]
        kr = [ph.sb(f"kr{i}", [128, 32], F32) for i in range(3)]
        qk32 = [ph.sb(f"qk32_{i}", [128, 8, 96], F32) for i in range(3)]
        qkbf = [ph.sb(f"qkbf{i}", [128, 8, 96], BF) for i in range(3)]
        Wk_ = [dict(sq=ph.sb(f"sq{i}", [128, 8, 96], F32), ssq=ph.sb(f"ssq{i}", [128, 8], F32),
                    rk=ph.sb(f"rk{i}", [128, 8], F32), tmp=ph.sb(f"tmpn{i}", [128, 8], F32),
                    ta=ph.sb(f"ta{i}", [128, 2, 8, 16], F32), tb=ph.sb(f"tb{i}", [128, 2, 8, 16], F32)) for i in range(3)]
        qT = [ph.sb(f"qT{i}", [128, 8, 512], BF) for i in range(2)]
        pt = [ph.sb(f"pt{i}", [128, 512], BF) for i in range(3)]
        rsum = [ph.sb(f"rsum{i}", [128, 512], BF) for i in range(2)]
        bcs = [ph.sb(f"bcs{i}", [64, 512], BF) for i in range(2)]
        at = [ph.sb(f"at{i}", [64, 8, 512], BF) for i in range(1)]

        def sumsq(i, col, src_ps, n):
            k.act(junk[i % 3][:, 0:n], src_ps, AF.Square, accum_out=ssl[:, col:col + 1])
            rstd_pool(k, rsl[:, col:col + 1], ssl[:, col:col + 1], n, mhalf[:, 0:1], tsl[:, col:col + 1])

        def kv_tile(i, hb, j):
            par = i % 3
            if j == 0:
                k.dma(hb[:].rearrange("p c t -> p (c t)"), T['hT_d'][i // 4])
            lat = ps[par][:, 0:288]
            for c in range(8):
                k.mm(lat, hb[:, c, 128 * j:128 * (j + 1)], Wkv[:, c, :], start=(c == 0), stop=(c == 7))
            sumsq(i, i, lat[:, 0:256], 256)
            yield
            ln = latn[par]
            k.act(ln[:, 0:256], lat[:, 0:256], AF.Copy, scale=rsl[:, i:i + 1])
            k.copy('act', kr[par][:], lat[:, 256:288])
            tbank = ps[7][:, :].bitcast(BF)
            for c in range(2):
                k.tr(tbank[:, 128 * c:128 * (c + 1)], ln[:, 128 * c:128 * (c + 1)], ident[:])
            lT = latT[par]
            k.copy('dve', lT[:, 0:2, :].rearrange("p c t -> p (c t)"), tbank[:, 0:256])
            yield
            kvb = [ps[3 + 2 * (i % 2)], ps[4 + 2 * (i % 2)]]
            for half in range(2):
                for c in range(2):
                    k.mm(kvb[half][:, :], lT[:, c, :], Wukv[:, c, 512 * half:512 * (half + 1)],
                         start=(c == 0), stop=(c == 1))
            kk = qk32[par]
            for half in range(2):
                kvv = kvb[half][:, :].rearrange("p (h e) -> p h e", h=4)
                k.copy('dve', kk[:, 4 * half:4 * half + 4, 0:64], kvv[:, :, 0:64])
                k.copy('dve', vx[:, i, 4 * half:4 * half + 4, 0:64], kvv[:, :, 64:128])
            k.copy('pool', kk[:, :, 64:96], bcast(kr[par][:, :], 1, 8))
            yield
            kf = qkbf[par]
            yield from qk_norm_rope(ph, Wk_[par], kk, kf, gqk[:, 96:192], csT[:, i, :, :], csT[:, i, ::-1, :], mhalf, use_act=True)
            kbank = ps[7][:, :].bitcast(BF)
            for h in range(8):
                k.tr(kbank[0:96, 128 * h:128 * (h + 1)], kf[:, h, :], ident[:])
            k.copy('dve', kT[0:96, :, 128 * i:128 * (i + 1)], kbank[0:96, :].rearrange("p (h t) -> p h t", h=8))
            yield

        def q_tile(i, hb, j, q_T, bk=None):
            par = i % 2
            bA, bB = bk if bk is not None else (ps[6], ps[7])
            lat = bA[:, 0:384]
            for c in range(8):
                k.mm(lat, hb[:, c, 128 * j:128 * (j + 1)], Wq[:, c, :], start=(c == 0), stop=(c == 7))
            yield
            lsb = Wk_[par]['sq'][:].rearrange("p a b -> p (a b)")[:, 0:384]
            k.copy('dve', lsb, lat)
            k.P.dve(lambda e: e.scalar_tensor_tensor(out=junk[par][:, 0:384], in0=lsb, scalar=1.0, in1=lsb, op0=ALU.mult,
                                                     op1=ALU.mult, accum_out=ssl[:, NT + i:NT + i + 1]),
                    [lsb], [junk[par][:, 0:384], ssl[:, NT + i:NT + i + 1]])
            rstd_pool(k, rsl[:, NT + i:NT + i + 1], ssl[:, NT + i:NT + i + 1], 384, mhalf[:, 0:1], tsl[:, NT + i:NT + i + 1])
            yield
            ln = latn[par]
            k.ts('dve', ln[:, 0:384], lsb, rsl[:, NT + i:NT + i + 1])
            yield
            tbank = bB[:, :].bitcast(BF)
            for c in range(3):
                k.tr(tbank[:, 128 * c:128 * (c + 1)], ln[:, 128 * c:128 * (c + 1)], ident[:])
            yield
            lT = latT[par]
            k.copy('dve', lT[:].rearrange("p c t -> p (c t)"), tbank[:, 0:384])
            yield
            qq = qk32[par]
            for half in range(2):
                qb = bA[:, 0:384]
                for c in range(3):
                    k.mm(qb, lT[:, c, :], Wuq[:, c, 384 * half:384 * (half + 1)], start=(c == 0), stop=(c == 2))
                yield
                k.copy('dve', qq[:, 4 * half:4 * half + 4, :], qb.rearrange("p (h e) -> p h e", h=4))
                yield
            qf = qkbf[par]
            yield from qk_norm_rope(ph, Wk_[par], qq, qf, gqk[:, 0:96], csT[:, i, :, :], csT[:, i, ::-1, :], mhalf, use_act=(i < 4))
            yield
            yield
            yield
            yield
            qbank = bB[:, :].bitcast(BF)
            for h in range(8):
                k.tr(qbank[0:96, 128 * h:128 * (h + 1)], qf[:, h, :], ident[:])
            yield
            k.copy('dve', q_T[0:96, :, 128 * j:128 * (j + 1)], qbank[0:96, :].rearrange("p (h t) -> p h t", h=8))
            yield

        def kv_block(tb):
            hb = hbuf[tb % 2]
            return [kv_tile(tb * 4 + j, hb, j) for j in range(4)]

        def q_chunk_gens(qc, two_sets=False):
            hb = hbuf[qc % 2]
            k.dma(hb[:].rearrange("p c t -> p (c t)"), T['hT_d'][qc])
            bks = [(ps[6], ps[7]), (ps[0], ps[1])]
            return [q_tile(qc * 4 + j, hb, j, qT[qc % 2], bks[j % 2] if two_sets else None) for j in range(4)]

        gens = []
        for tb in range(DBG.get('nprep', NB)):
            gens += kv_block(tb)
        run_interleaved(gens, 3)
        nqc = DBG.get('nqc', NB)
        if nqc:
            run_interleaved(q_chunk_gens(0, two_sets=True), 2)

        scale = 1.0 / math.sqrt(96.0)
        NH = DBG.get('nh', 8)
        pend = [None]
        for qc in range(nqc):
            q_T = qT[qc % 2]
            a_t = at[0]
            nxt = q_chunk_gens(qc + 1) if qc + 1 < nqc else []
            nxt_active = []
            steps = [(h, kt) for h in range(NH) for kt in range(NT)]

            def S(idx):
                h, kt = steps[idx]
                k.mm(ps[idx % 3][:, :], kT[0:96, h, 128 * kt:128 * (kt + 1)], q_T[0:96, h, :])

            def fin_a(h):
                k.recip(rsum[h % 2][64:65, :], ps[3 + (h % 2)][64:65, :])

            def fin_b(h):
                k.mm(ps[5][0:64, :], ones[64:65, :], rsum[h % 2][64:65, :])

            def fin_c(h, a_t):
                k.copy('dve', bcs[h % 2][:], ps[5][0:64, :])
                k.tt('dve', a_t[:, h, :], ps[3 + (h % 2)][0:64, :], bcs[h % 2][:], ALU.mult)

            S(0)
            S(1)
            for idx, (h, kt) in enumerate(steps):
                p_t = pt[idx % 3]
                k.act(p_t[:], ps[idx % 3][:, :], AF.Exp, scale=scale)
                if idx + 2 < len(steps):
                    S(idx + 2)
                ob = ps[3 + (h % 2)]
                k.mm(ob[0:65, :], vx[:, kt, h, :], p_t[:], start=(kt == 0), stop=(kt == NT - 1))
                if pend[0] is not None:
                    if kt == 1:
                        pend[0][0]()
                    elif kt == 10:
                        pend[0][1]()
                    elif kt == 14:
                        pend[0][2]()
                        pend[0] = None
                if kt == NT - 1:
                    last = (h == NH - 1)
                    pend[0] = (lambda h=h: fin_a(h), lambda h=h: fin_b(h),
                               (lambda h=h, a_t=a_t, qc=qc, last=last, fin_c=fin_c: (fin_c(h, a_t), k.dma(T['at_d'][qc], a_t[:].rearrange("p h t -> p (h t)")) if last else None)))
                if idx % 3 == 2:
                    while nxt and len(nxt_active) < 1:
                        nxt_active.append(nxt.pop(0))
                    for g in list(nxt_active):
                        try:
                            next(g)
                        except StopIteration:
                            nxt_active.remove(g)
            run_interleaved(nxt_active + nxt, 1)

        if pend[0] is not None:
            pend[0][0]()
            pend[0][1]()
            pend[0][2]()


def phase4(nc, T):
    with Phase(nc, "p4") as ph:
        k = ph.k
        ps = ph.ps
        w_in_v = T['w_in'].rearrange("(k p) n -> p k n", p=128)
        Wz = ph.sb("Wz", [128, 8, 512], BF)
        Wg = ph.sb("Wg", [128, 8, 2048], BF)
        Wao = ph.sb("Wao", [128, 4, D], BF)
        Who = ph.sb("Who", [128, 4, D], BF)
        Wout = ph.sb("Wout", [128, 8, D], BF)
        bg = ph.sb("bg", [128, 16], F32)
        load_w(ph, Wz[:], w_in_v[:, :, COL_ZA:COL_ZA + 512])
        for q4 in range(4):
            load_w(ph, Wg[:, :, 512 * q4:512 * (q4 + 1)], w_in_v[:, :, COL_GH + 512 * q4:COL_GH + 512 * (q4 + 1)])
        load_w(ph, Wao[:], T['w_attn_out'].rearrange("(hp p) n -> p hp n", p=128))
        load_w(ph, Who[:], T['w_hy_out'].rearrange("(k p) n -> p k n", p=128))
        load_w(ph, Wout[:], T['w_out'].rearrange("(k p) n -> p k n", p=128))
        k.dma(bg[:], T['bgT'][:, :])
        hbuf = [ph.sb(f"hbuf{i}", [128, 8, 512], BF) for i in range(2)]
        atb = [ph.sb(f"atb{i}", [128, 4, 512], BF) for i in range(2)]
        yzb = [ph.sb(f"yzb{i}", [128, 4, 512], BF) for i in range(2)]
        xt = [ph.sb(f"xt{i}", [128, D], F32) for i in range(4)]
        ot = [ph.sb(f"ot{i}", [128, D], F32) for i in range(2)]
        sz = [ph.sb(f"sz{i}", [128, 512], BF) for i in range(2)]
        ya = ph.sb("ya", [128, 4, 512], BF)
        gh = [ph.sb(f"gh{i}", [128, 512], BF) for i in range(2)]
        ga = [ph.sb(f"ga{i}", [128, 512], BF) for i in range(2)]
        m1 = [ph.sb(f"m1{i}", [128, 512], F32) for i in range(2)]
        m2 = [ph.sb(f"m2{i}", [128, 512], F32) for i in range(2)]
        mg = [ph.sb(f"mg{i}", [128, 8, 512], BF) for i in range(2)]
        def load_block(tb):
            hb = hbuf[tb % 2]
            a_b = atb[tb % 2]
            y_b = yzb[tb % 2]
            k.dma(hb[:].rearrange("p c t -> p (c t)"), T['hT_d'][tb])
            atv = T['at_d'][tb].rearrange("p (hp two t) -> p two hp t", two=2, t=512)
            k.dma(a_b[0:64, :, :], atv[:, 0, :, :])
            k.dma(a_b[64:128, :, :], atv[:, 1, :, :])
            k.dma(y_b[:].rearrange("p c t -> p (c t)"), T['yz_d'][tb])

        load_block(0)
        for tb in range(NB):
            hb = hbuf[tb % 2]
            a_b = atb[tb % 2]
            y_b = yzb[tb % 2]
            for j in range(4):
                k.dma(xt[j][:], T['x'][128 * (tb * 4 + j):128 * (tb * 4 + j + 1), :])
            for hp in range(4):
                zb = ps[hp % 2][:, :]
                for c in range(8):
                    k.mm(zb, Wz[:, c, 128 * hp:128 * (hp + 1)], hb[:, c, :], start=(c == 0), stop=(c == 7))
                s_z = sz[hp % 2]
                k.act(s_z[:], zb, AF.Silu)
                k.tt('pool', ya[:, hp, :], a_b[:, hp, :], s_z[:], ALU.mult)
            m_g = mg[tb % 2]
            for dc in range(8):
                g1 = ps[2 + (dc % 2)]
                g2 = ps[4 + (dc % 2)]
                for c in range(8):
                    k.mm(g1[:, :], Wg[:, c, 128 * dc:128 * (dc + 1)], hb[:, c, :], start=(c == 0), stop=(c == 7))
                for c in range(8):
                    k.mm(g2[:, :], Wg[:, c, 1024 + 128 * dc:1024 + 128 * (dc + 1)], hb[:, c, :],
                         start=(c == 0), stop=(c == 7))
                k.act(gh[dc % 2][:], g1[:, :], AF.Sigmoid, bias=bg[:, dc:dc + 1])
                k.act(ga[dc % 2][:], g2[:, :], AF.Sigmoid, bias=bg[:, 8 + dc:9 + dc])
                uh = ps[6]
                ua = ps[7]
                for c in range(4):
                    k.mm(uh[:, :], Who[:, c, 128 * dc:128 * (dc + 1)], y_b[:, c, :], start=(c == 0), stop=(c == 3))
                for hp in range(4):
                    k.mm(ua[:, :], Wao[:, hp, 128 * dc:128 * (dc + 1)], ya[:, hp, :], start=(hp == 0), stop=(hp == 3))
                k.tt('dve', m1[dc % 2][:], uh[:, :], gh[dc % 2][:], ALU.mult)
                k.tt('dve', m2[dc % 2][:], ua[:, :], ga[dc % 2][:], ALU.mult)
                k.tt('pool', m_g[:, dc, :], m1[dc % 2][:], m2[dc % 2][:], ALU.add)
            if tb + 1 < NB:
                load_block(tb + 1)
            for j in range(4):
                i = tb * 4 + j
                x_t = xt[j]
                o_t = ot[i % 2]
                for half in range(2):
                    fb = ps[half]
                    for c in range(8):
                        k.mm(fb[:, :], m_g[:, c, 128 * j:128 * (j + 1)], Wout[:, c, 512 * half:512 * (half + 1)],
                             start=(c == 0), stop=(c == 7))
                    k.tt('dve', o_t[:, 512 * half:512 * (half + 1)], fb[:, :], x_t[:, 512 * half:512 * (half + 1)], ALU.add)
                k.dma(T['out'][128 * i:128 * (i + 1), :], o_t[:])


def fft_constants():
    C = {}
    n = NF
    s2 = np.arange(128, dtype=np.float64)[:, None]
    f2 = np.arange(128, dtype=np.float64)[None, :]
    th = 2 * np.pi * (f2 + 0.5) * s2 / 256.0
    C['FA1'] = np.concatenate([np.cos(th), -np.sin(th)], 1)
    th2 = 2 * np.pi * (f2 + 0.5) * (s2 + 128) / 256.0
    C['FA2'] = -np.concatenate([np.cos(th2), -np.sin(th2)], 1)
    s1 = np.arange(32, dtype=np.float64)
    tw = np.exp(-2j * np.pi * (np.arange(128)[None, :] + 0.5) * s1[:, None] / n)
    twq = np.tile(tw, (4, 1))
    C['TWa'] = np.concatenate([twq.real, twq.real], 1)
    C['TWb'] = np.concatenate([-twq.imag, twq.imag], 1)
    W = np.exp(-2j * np.pi * np.outer(s1, s1) / 32.0)
    Wq = np.kron(np.eye(4), W)
    C['WBr'] = Wq.real
    C['WBi'] = Wq.imag
    C['WBni'] = -Wq.imag
    Wi = np.exp(2j * np.pi * np.outer(s1, s1) / 32.0)
    Wiq = np.kron(np.eye(4), Wi)
    C['WI1'] = np.concatenate([Wiq.real, Wiq.imag], 1)
    C['WI2'] = np.concatenate([-Wiq.imag, Wiq.real], 1)
    twi = np.exp(2j * np.pi * (np.arange(128)[:, None] + 0.5) * s1[None, :] / n)
    twiq = np.tile(twi, (1, 4))
    C['TIa'] = np.concatenate([twiq.real, twiq.real], 1)
    C['TIb'] = np.concatenate([-twiq.imag, twiq.imag], 1)
    t2 = np.arange(128, dtype=np.float64)[None, :]
    f2c = np.arange(128, dtype=np.float64)[:, None]
    th3 = 2 * np.pi * (f2c + 0.5) * t2 / 256.0
    C['FIr'] = (2.0 / n) * np.cos(th3)
    C['FIi'] = -(2.0 / n) * np.sin(th3)
    return C


def filter_constants():
    C = {}
    f32 = np.float32
    t = np.linspace(0.0, 1.0, L, dtype=f32)[:, None]
    bands = 16
    f = np.linspace(1e-4, bands - 1, bands, dtype=f32)
    ang = (f32(2.0 * np.pi / L) * np.arange(L, dtype=f32)[:, None] * f[None, :]).astype(f32)
    z = np.concatenate([t, np.cos(ang).astype(f32), -np.sin(ang).astype(f32)], axis=-1).astype(f32)
    zs = np.zeros((128, L), f32)
    zs[0:33, :] = z.T
    zs[64:97, :] = z[::-1].T
    hi = zs.astype(ml_dtypes.bfloat16)
    lo = (zs - hi.astype(f32)).astype(ml_dtypes.bfloat16)
    C['zs_hi'] = hi
    C['zs_lo'] = lo
    tl = t[:, 0]
    tf = np.zeros((128, 2, 32), f32)
    pidx = np.arange(128)[:, None] * 32 + np.arange(32)[None, :]
    tf[:, 0, :] = tl[pidx]
    tf[:, 1, :] = tl[4095 - pidx]
    C['tfull'] = tf.reshape(128, 64)
    MIN_DECAY = math.log(1e-2) / 1.5
    MAX_DECAY = math.log(1e-2) / 0.3
    deltas = np.abs(np.linspace(MIN_DECAY, MAX_DECAY, HYW, dtype=f32)).astype(f32)
    C['negd'] = (-deltas)[None, :].astype(f32)
    return C


def _sin_layer(ph, W, pre_ps, fr, fb, out32):
    k = ph.k
    a, kk = W['a'], W['kk']
    k.ts('dve', a[:], pre_ps, fr, fb, op0=ALU.mult, op1=ALU.add)
    yield
    k.ts('dve', kk[:], a[:], 1.0 / (2 * math.pi), MAGIC, op0=ALU.mult, op1=ALU.add)
    yield
    k.ts('dve', kk[:], kk[:], -MAGIC, None, op0=ALU.add)
    yield
    k.stt(a[:], kk[:], -2 * math.pi, a[:], ALU.mult, ALU.add)
    yield
    k.ts('dve', a[:], a[:], -3.14159, 3.14159, op0=ALU.max, op1=ALU.min)
    yield
    k.act(out32, a[:], AF.Sin)
    yield


def _hilo(ph, hi, lo, src32, tmp32):
    k = ph.k
    k.copy('dve', hi, src32)
    k.copy('pool', tmp32, hi)
    k.tt('pool', lo, src32, tmp32, ALU.subtract)


def phase2a_gen(ph, T, banks):
    k = ph.k
    zs_hi = ph.sb("zs_hi", [128, L], BF)
    zs_lo = ph.sb("zs_lo", [128, L], BF)
    W1 = ph.sb("W1", [128, 128], F32)
    W2 = ph.sb("W2", [128, 128], F32)
    W1h = ph.sb("W1h", [128, 128], BF)
    W1l = ph.sb("W1l", [128, 128], BF)
    W2h = ph.sb("W2h", [128, 128], BF)
    W2l = ph.sb("W2l", [128, 128], BF)
    wt = ph.sb("wt", [128, 128], F32)
    mv = ph.sb("mv", [128, 4], F32)
    fb = ph.sb("fb", [128, 2], F32)
    S = [dict(a=ph.sb(f"a{i}", [128, 512], F32), kk=ph.sb(f"kk{i}", [128, 512], F32), h1=ph.sb(f"h1_{i}", [128, 512], F32),
              h1h=ph.sb(f"h1h{i}", [128, 512], BF), h1l=ph.sb(f"h1l{i}", [128, 512], BF), t32=ph.sb(f"t32_{i}", [128, 512], F32),
              h2=ph.sb(f"h2_{i}", [128, 512], F32), h2b=ph.sb(f"h2b{i}", [128, 512], BF)) for i in range(2)]
    k.dma(zs_hi[:], T['zs_hi'][:, :])
    k.dma(zs_lo[:], T['zs_lo'][:, :])
    k.dma(W1[:], T['W1blk'][:, :])
    k.dma(W2[:], T['W2blk'][:, :])
    k.dma(mv[:], T['mlpv'][:, :])
    _hilo(ph, W1h[:], W1l[:], W1[:], wt[:])
    _hilo(ph, W2h[:], W2l[:], W2[:], wt[:])
    k.tt('dve', fb[:, 0:1], mv[:, 0:1], mv[:, 1:2], ALU.mult)
    k.tt('dve', fb[:, 1:2], mv[:, 2:3], mv[:, 3:4], ALU.mult)
    yield

    def chunk(cch):
        s_ = S[cch % 2]
        sl = slice(512 * cch, 512 * (cch + 1))
        b1 = banks[cch % 2]
        k.mm(b1[:, :], W1h[:], zs_hi[:, sl], start=True, stop=False)
        k.mm(b1[:, :], W1h[:], zs_lo[:, sl], start=False, stop=False)
        k.mm(b1[:, :], W1l[:], zs_hi[:, sl], start=False, stop=True)
        yield
        yield from _sin_layer(ph, s_, b1[:, :], mv[:, 0:1], fb[:, 0:1], s_['h1'][:])
        _hilo(ph, s_['h1h'][:], s_['h1l'][:], s_['h1'][:], s_['t32'][:])
        yield
        b2 = banks[2 + cch % 2]
        k.mm(b2[:, :], W2h[:], s_['h1h'][:], start=True, stop=False)
        k.mm(b2[:, :], W2h[:], s_['h1l'][:], start=False, stop=False)
        k.mm(b2[:, :], W2l[:], s_['h1h'][:], start=False, stop=True)
        yield
        yield from _sin_layer(ph, s_, b2[:, :], mv[:, 2:3], fb[:, 1:2], s_['h2'][:])
        k.copy('pool', s_['h2b'][:], s_['h2'][:])
        k.dma(T['h2_d'][:, sl], s_['h2b'][:])
        yield

    gens = [chunk(c) for c in range(NB)]
    active = []
    while gens or active:
        while gens and len(active) < 2:
            active.append(gens.pop(0))
        for g in list(active):
            try:
                next(g)
            except StopIteration:
                active.remove(g)
        yield


def _cmul_tab(ph, W, src, Ta, Tb, out_bf):
    k = ph.k
    P1, P2 = W
    sw = src.rearrange("p (r f) -> p r f", r=2)[:, ::-1, :]
    k.tt('dve', P1[:], src, Ta, ALU.mult)
    k.tt('dve', P2[:].rearrange("p (r f) -> p r f", r=2), sw, Tb.rearrange("p (r f) -> p r f", r=2), ALU.mult)
    k.tt('pool', out_bf, P1[:], P2[:], ALU.add)


def phase2b(nc, T):
    with Phase(nc, "p2b") as ph:
        k = ph.k
        ps = ph.ps
        ident = ph.sb("ident", [128, 128], BF)
        make_ident(ph, ident)
        w_in_v = T['w_in'].rearrange("(k p) n -> p k n", p=128)
        cols = (COL_V, COL_X1, COL_X2, COL_ZH)
        ar = ph.sb("arena", [128, 24592], BF)
        Wblk = ar[:, 20486:24582].rearrange("p (k w c) -> p k w c", k=8, w=4)
        for w in range(4):
            load_w(ph, Wblk[:, :, w, :], w_in_v[:, :, cols[w]:cols[w] + 128])
        cb16 = {}
        for nm, w in (('FA1', 256), ('FA2', 256), ('WBr', 128), ('WBi', 128), ('WBni', 128), ('WI1', 256), ('WI2', 256),
                      ('FIr', 128), ('FIi', 128)):
            cb16[nm] = ph.sb(nm, [128, w], BF)
            k.dma(cb16[nm][:], T[nm][:, :])
        c32 = {}
        for nm in ('TWa', 'TWb', 'TIa', 'TIb'):
            c32[nm] = ph.sb(nm, [128, 256], F32)
            k.dma(c32[nm][:], T[nm][:, :])
        h2s = ph.sb("h2s", [128, L], BF)
        k.dma(h2s[:], T['h2_d'][:, :])
        h2p = ph.sb("h2p", [128, 32, 128], BF)
        k.copy('pool', h2p[:], h2s[:].rearrange("q (p s) -> q s p", s=32))
        W3 = ph.sb("W3", [128, 2048], BF)
        load_w(ph, W3[:], T['W3blk'][:, :])
        wsh = ph.sb("wsh", [128, 12, 4], F32)
        k.dma(wsh[:].rearrange("p a b -> p (a b)"), T['wsh'][:, :])
        biasT = ph.sb("biasT", [128, 2, 128], F32)
        k.dma(biasT[:].rearrange("p a b -> p (a b)"), T['biasT'][:, :])
        negd = ph.sb("negd", [128, HYW], F32)
        k.dma(negd[:], T['negd'][0:1, :].partition_broadcast(128))
        tfull = ph.sb("tfull", [128, 2, 32], F32)
        k.dma(tfull[:].rearrange("p a b -> p (a b)"), T['tfull'][:, :])

        hbuf = [ph.sb(f"hbuf{i}", [128, 8, 512], BF) for i in range(2)]
        raw = [ar[:, 4098 * i:4098 * (i + 1)] for i in range(3)]
        ub_ = [ar[:, 12294 + 4096 * i:12294 + 4096 * (i + 1)] for i in range(2)]
        k_tm = ar[:, 0:8192].rearrange("p (o d c s) -> p o d c s", o=2, d=2, c=64)
        AB = ar[:, 8192:12288].rearrange("p (d c s) -> p d c s", d=2, c=64)
        Ksp = ar[:, 12288:20480].rearrange("p (o g r f) -> p o g r f", o=2, g=16, r=2)
        Gbuf = ar[:, 20480:24576].rearrange("p (r c s) -> p r c s", r=2, c=64)
        sz = ph.sb("sz", [128, L], BF)
        tm = [ph.sb(f"tm{i}", [128, 128, 32], BF) for i in range(3)]
        z2_tm = ph.sb("z2_tm", [128, 128, 32], BF)
        y_sc = ph.sb("y_sc", [128, 32, 128], BF)
        yzb = ph.sb("yzb", [128, L], BF)
        arg32 = ph.sb("arg32", [128, 4096], F32)
        PW = [(ph.sb(f"P1_{i}", [128, 256], F32), ph.sb(f"P2_{i}", [128, 256], F32)) for i in range(2)]
        Zp = [ph.sb(f"Zp{i}", [128, 256], BF) for i in range(2)]
        Yb = [ph.sb(f"Yb{i}", [128, 256], BF) for i in range(2)]
        Kev = [ph.sb(f"Kev{i}", [128, 256], BF) for i in range(2)]
        Esb = [[ph.sb(f"E{st}_{i}", [128, 256], BF) for i in range(2)] for st in range(3)]
        cnt = [0]

        ZA = [ph.sb(f"ZA{i}", [128, 512], BF) for i in range(2)]
        ZB = [ph.sb(f"ZB{i}", [128, 512], BF) for i in range(2)]
        YA = [ph.sb(f"YA{i}", [128, 512], BF) for i in range(2)]
        YB = [ph.sb(f"YB{i}", [128, 512], BF) for i in range(2)]
        G2 = ph.sb("G2", [128, 2, 64, 32], BF)
        WI1n = ph.sb("WI1n", [128, 256], BF)
        k.ts('pool', WI1n[:], cb16['WI1'][:], -1.0, None, op0=ALU.mult)

        def v4(ap):
            return ap.rearrange("p (u r f) -> p u r f", u=2, r=2)

        def tab4(t):
            return bcast(t.rearrange("p (r f) -> p r f", r=2), 1, 2)

        def run_skewed(items, hook=None):
            n = len(items)
            depth = max(len(it) for it in items)
            for t in range(n + depth - 1):
                for s_ in reversed(range(depth)):
                    i = t - s_
                    if 0 <= i < n and s_ < len(items[i]):
                        items[i][s_](i)
                if hook is not None:
                    hook(t)

        def cmul_pair(bank, Ta, Tb, outA, outB):
            k.tt('dve', outA, v4(bank), tab4(Ta), ALU.mult)
            k.tt('dve', outB, v4(bank)[:, :, ::-1, :], tab4(Tb), ALU.mult)

        def st_za(lhs_of, q):
            def f(i):
                bank = ps[i % 2]
                for u in range(2):
                    l1, l2 = lhs_of(2 * q + u)
                    za = bank[:, 256 * u:256 * (u + 1)]
                    k.mm(za, l1, cb16['FA1'][:], start=True, stop=(l2 is None))
                    if l2 is not None:
                        k.mm(za, l2, cb16['FA2'][:], start=False, stop=True)
            return f

        def st_tw(i):
            cmul_pair(ps[i % 2][:, :], c32['TWa'][:], c32['TWb'][:], v4(ZA[i % 2][:]), v4(ZB[i % 2][:]))

        def st_ub(i):
            bank = ps[2 + i % 2]
            for u in range(2):
                ub = bank[:, 256 * u:256 * (u + 1)]
                for n_, z_p in enumerate((ZA[i % 2], ZB[i % 2])):
                    zr = z_p[:, 256 * u:256 * u + 128]
                    zi = z_p[:, 256 * u + 128:256 * (u + 1)]
                    k.mm(ub[:, 0:128], cb16['WBr'][:], zr, start=(n_ == 0), stop=False)
                    k.mm(ub[:, 0:128], cb16['WBni'][:], zi, start=False, stop=(n_ == 1))
                for n_, z_p in enumerate((ZA[i % 2], ZB[i % 2])):
                    zr = z_p[:, 256 * u:256 * u + 128]
                    zi = z_p[:, 256 * u + 128:256 * (u + 1)]
                    k.mm(ub[:, 128:256], cb16['WBi'][:], zr, start=(n_ == 0), stop=False)
                    k.mm(ub[:, 128:256], cb16['WBr'][:], zi, start=False, stop=(n_ == 1))

        def filt_gen(cb, hbk, bank_fixed):
            gcol = 128 * cb + 64 * hbk
            k.tt('dve', arg32[:].rearrange("p (d c s) -> p d c s", d=2, c=64),
                 bcast(bcast(negd[:, gcol:gcol + 64], 1, 2), 3, 32),
                 bcast(tfull[:, :, :], 2, 64), ALU.mult)
            yield
            k.act(AB.rearrange("p d c s -> p (d c s)"), arg32[:], AF.Exp)
            yield
            wc0 = 256 * (2 * cb + hbk)
            for s1 in range(32):
                kb_ = (bank_fixed if bank_fixed is not None else ps[6 + s1 % 2])[:, 0:256]
                k.mm(kb_, h2p[:, s1, :], W3[:, wc0:wc0 + 256])
                abv = AB[:, :, :, s1].rearrange("p d c -> p (d c)")
                k.tt('dve', k_tm[:, :, :, :, s1].rearrange("p o d c -> p o (d c)"),
                     kb_.rearrange("p (o x) -> p o x", o=2), bcast(abv, 1, 2), ALU.mult)
                yield

        pref = [None]
        pre_h = set()
        for cb in range(DBG.get('ncb', 4)):
            for w in range(4):
                if cb > 0:
                    load_w(ph, Wblk[:, :, w, :], w_in_v[:, :, cols[w] + 128 * cb:cols[w] + 128 * (cb + 1)])
            for w in range(3):
                k.memset('pool', raw[w][:, 0:1], 0.0)
                k.memset('pool', raw[w][:, 4097:4098], 0.0)
            for tb in range(NB):
                hb = hbuf[tb % 2]
                if tb in pre_h:
                    pre_h.discard(tb)
                else:
                    k.dma(hb[:].rearrange("p c t -> p (c t)"), T['hT_d'][tb])
                for w in range(4):
                    bank = ps[(tb * 4 + w) % 2]
                    for c in range(8):
                        k.mm(bank[:, :], Wblk[:, c, w, :], hb[:, c, :], start=(c == 0), stop=(c == 7))
                    if w < 3:
                        k.act(raw[w][:, 1 + 512 * tb:1 + 512 * (tb + 1)], bank[:, :], AF.Copy)
                    else:
                        k.act(sz[:, 512 * tb:512 * (tb + 1)], bank[:, :], AF.Silu)
            if DBG.get('s2b', 9) < 2: continue
            for w in range(3):
                u = ub_[w % 2]
                j = 4 * w + cb
                k.ts('dve', u, raw[w][:, 1:4097], wsh[:, j, 1:2], wsh[:, j, 3:4], op0=ALU.mult, op1=ALU.add)
                k.stt(u, raw[w][:, 0:4096], wsh[:, j, 0:1], u, ALU.mult, ALU.add)
                k.stt(u, raw[w][:, 2:4098], wsh[:, j, 2:3], u, ALU.mult, ALU.add)
                for a in range(4):
                    pv = ps[2 + a % 2][:, :].bitcast(BF)
                    for e in range(8):
                        s1 = 8 * a + e
                        k.tr(pv[:, 128 * e:128 * (e + 1)], u[:, s1:4096:32], ident[:])
                    k.copy('dve', tm[w][:, :, 8 * a:8 * a + 8], pv.rearrange("p (s c) -> p c s", s=8))
            if DBG.get('dump_tm'):
                k.dma(T['dbg_tm'][:, :], tm[DBG['dump_tm'] - 1][:].rearrange("p c s -> p (c s)"))
            if DBG.get('s2b', 9) < 3: continue
            for hbk in range(DBG.get('nhbk', 2)):
                c0 = 64 * hbk
                gcol = 128 * cb + c0
                if hbk == 0 or not DBG.get('pref', 1):
                    for _ in filt_gen(cb, hbk, None):
                        pass
                else:
                    for _ in pref[0]:
                        pass
                if DBG.get('s2b', 9) < 4: continue
                def spec_lhs(gi):
                    o, g = divmod(gi, 16)
                    return (k_tm[:, o, 0, 4 * g:4 * g + 4, :].rearrange("p c s -> p (c s)"),
                            k_tm[:, o, 1, 4 * g:4 * g + 4, :].rearrange("p c s -> p (c s)"))

                def st_kev(q):
                    def f(i):
                        o, gp = divmod(q, 8)
                        k.copy('act', Ksp[:, o, 2 * gp:2 * gp + 2, :, :].rearrange("p g r f -> p (g r f)"), ps[2 + i % 2][:, :])
                        if gp == 7:
                            gg0 = gcol // 4
                            k.tt('pool', Ksp[:, o, :, 0, :], Ksp[:, o, :, 0, :], bcast(biasT[:, o, gg0:gg0 + 16], 2, 128), ALU.add)
                    return f

                def conv_lhs_of(src):
                    def f(g):
                        return (src[:, c0 + 4 * g:c0 + 4 * g + 4, :].rearrange("p c s -> p (c s)"), None)
                    return f

                def st_mul(o, q):
                    def f(i):
                        bank = ps[2 + i % 2][:, :]
                        kr_ = bcast(Ksp[:, o, 2 * q:2 * q + 2, 0, :], 2, 2)
                        ki_ = bcast(Ksp[:, o, 2 * q:2 * q + 2, 1, :], 2, 2)
                        k.tt('dve', v4(YA[i % 2][:]), v4(bank), kr_, ALU.mult)
                        k.tt('dve', v4(YB[i % 2][:]), v4(bank)[:, :, ::-1, :], ki_, ALU.mult)
                    return f

                def st_gb(i):
                    bank = ps[4 + i % 2]
                    ya, yb_ = YA[i % 2], YB[i % 2]
                    for u in range(2):
                        gb = bank[:, 256 * u:256 * (u + 1)]
                        k.mm(gb, ya[:, 256 * u:256 * u + 128], cb16['WI1'][:], start=True, stop=False)
                        k.mm(gb, yb_[:, 256 * u:256 * u + 128], WI1n[:], start=False, stop=False)
                        k.mm(gb, ya[:, 256 * u + 128:256 * (u + 1)], cb16['WI2'][:], start=False, stop=False)
                        k.mm(gb, yb_[:, 256 * u + 128:256 * (u + 1)], cb16['WI2'][:], start=False, stop=True)

                def st_itw(o, q):
                    def f(i):
                        bank = ps[4 + i % 2][:, :]
                        g1o = Gbuf[:, :, 8 * q:8 * q + 8, :].rearrange("p r (u c) s -> p r u (c s)", u=2)
                        g2o = G2[:, :, 8 * q:8 * q + 8, :].rearrange("p r (u c) s -> p r u (c s)", u=2)
                        k.tt('dve', g1o, v4(bank).rearrange("p u r f -> p r u f"),
                             tab4(c32['TIa'][:]).rearrange("p u r f -> p r u f"), ALU.mult)
                        k.tt('dve', g2o, v4(bank)[:, :, ::-1, :].rearrange("p u r f -> p r u f"),
                             tab4(c32['TIb'][:]).rearrange("p u r f -> p r u f"), ALU.mult)
                    return f

                def st_inva(o, q):
                    def f(i):
                        if q % 2 == 1:
                            cc = q // 2
                            gate = tm[1] if o == 0 else tm[2]
                            yb = ps[6]
                            for n_, gsrc in enumerate((Gbuf, G2)):
                                k.mm(yb[:, :], cb16['FIr'][:], gsrc[:, 0, 16 * cc:16 * cc + 16, :].rearrange("p c s -> p (c s)"),
                                     start=(n_ == 0), stop=False)
                                k.mm(yb[:, :], cb16['FIi'][:], gsrc[:, 1, 16 * cc:16 * cc + 16, :].rearrange("p c s -> p (c s)"),
                                     start=False, stop=(n_ == 1))
                            cs = slice(c0 + 16 * cc, c0 + 16 * cc + 16)
                            if o == 0:
                                k.tt('dve', z2_tm[:, cs, :], yb[:, :].rearrange("p (c s) -> p c s", c=16), gate[:, cs, :], ALU.mult)
                            else:
                                k.tt('dve', y_sc[:, :, cs].rearrange("p s c -> p c s"),
                                     yb[:, :].rearrange("p (c s) -> p c s", c=16), gate[:, cs, :], ALU.mult)
                    return f

                items = []
                for q in range(16):
                    items.append([st_za(spec_lhs, q), st_tw, st_ub, st_kev(q)])
                for o in range(DBG.get('nord', 2)):
                    src = tm[0] if o == 0 else z2_tm
                    if o == 1:
                        items += [[] for _ in range(DBG.get('gap', 0))]
                    for q in range(8):
                        items.append([st_za(conv_lhs_of(src), q), st_tw, st_ub, st_mul(o, q), st_gb, st_itw(o, q), st_inva(o, q)])
                hook = None
                if hbk == 0 and DBG.get('pref', 1):
                    pref[0] = filt_gen(cb, 1, ps[7])

                    def hook(t, g=pref[0]):
                        if t >= 19:
                            next(g, None)
                            next(g, None)
                run_skewed(items, hook)
            if DBG.get('dump_z2'):
                k.dma(T['dbg_tm'][:, :], z2_tm[:].rearrange("p c s -> p (c s)"))
            if DBG.get('s2b', 9) < 6: continue
            if cb + 1 < DBG.get('ncb', 4):
                for tb_ in range(2):
                    k.dma(hbuf[tb_][:].rearrange("p c t -> p (c t)"), T['hT_d'][tb_])
                    pre_h.add(tb_)
            for a in range(4):
                pv = ps[2 + a % 2][:, :].bitcast(BF)
                for e in range(8):
                    k.tr(pv[:, 128 * e:128 * (e + 1)], y_sc[:, 8 * a + e, :], ident[:])
                k.tt('dve', yzb[:].rearrange("c (p s) -> c p s", s=32)[:, :, 8 * a:8 * a + 8],
                     pv.rearrange("c (s p) -> c p s", s=8),
                     sz[:].rearrange("c (p s) -> c p s", s=32)[:, :, 8 * a:8 * a + 8], ALU.mult)
            for tb in range(NB):
                k.dma(T['yz_d'][tb][:, 512 * cb:512 * (cb + 1)], yzb[:, 512 * tb:512 * (tb + 1)])


def phase2(nc, T):
    if 'b' in DBG.get('p2', 'ab'):
        phase2b(nc, T)


def _bf(a):
    return np.asarray(a, np.float32).astype(ml_dtypes.bfloat16)


_CONST_CACHE = {}


def host_constants():
    if _CONST_CACHE:
        return _CONST_CACHE
    C = {}
    pos = np.arange(L, dtype=np.float32)
    inv_freq = (np.float32(10000.0) ** (-np.arange(0, 32, 2, dtype=np.float32) / np.float32(32))).astype(np.float32)
    ang = (pos[:, None] * inv_freq[None, :]).astype(np.float32)
    C['cosT'] = np.ascontiguousarray(np.cos(ang).astype(np.float32).reshape(NT, 128, 16).transpose(1, 0, 2).reshape(128, NT * 16))
    C['sinT'] = np.ascontiguousarray(np.sin(ang).astype(np.float32).reshape(NT, 128, 16).transpose(1, 0, 2).reshape(128, NT * 16))
    F = fft_constants()
    for nm in ('FA1', 'FA2', 'WBr', 'WBi', 'WBni', 'WI1', 'WI2', 'FIr', 'FIi'):
        C[nm] = np.ascontiguousarray(_bf(F[nm]))
    for nm in ('TWa', 'TWb', 'TIa', 'TIb'):
        C[nm] = np.ascontiguousarray(F[nm].astype(np.float32))
    C.update(filter_constants())
    _CONST_CACHE.update(C)
    return _CONST_CACHE


def prep_inputs(inp, b):
    f32 = np.float32
    m = {}
    m['x'] = np.ascontiguousarray(inp['x'][b], dtype=f32)
    m['w_in'] = np.ascontiguousarray(inp['w_in'][0], dtype=f32)
    m['gT'] = np.ascontiguousarray(inp['g_norm'][0].reshape(8, 128).T, dtype=f32)
    m['bgT'] = np.ascontiguousarray(inp['b_gate'][0].reshape(16, 128).T, dtype=f32)
    m['w_uq'] = np.ascontiguousarray(inp['w_uq'][0], dtype=f32)
    m['w_ukv'] = np.ascontiguousarray(inp['w_ukv'][0], dtype=f32)
    m['gcqT'] = np.ascontiguousarray(inp['g_cq'][0].reshape(3, 128).T, dtype=f32)
    m['gckvT'] = np.ascontiguousarray(inp['g_ckv'][0].reshape(2, 128).T, dtype=f32)
    m['gqk'] = np.ascontiguousarray(np.concatenate([inp['g_qn'][0], inp['g_kn'][0]])[None, :], dtype=f32)
    m['w_attn_out'] = np.ascontiguousarray(inp['w_attn_out'][0], dtype=f32)
    m['w_hy_out'] = np.ascontiguousarray(inp['w_hy_out'][0], dtype=f32)
    m['w_out'] = np.ascontiguousarray(inp['w_out'][0], dtype=f32)
    wsh = np.zeros((128, 12, 4), f32)
    wsh[:, :, 0:3] = inp['w_short'][0].reshape(3, 12, 128).transpose(2, 1, 0)
    wsh[:, :, 3] = inp['b_short'][0].reshape(12, 128).T
    m['wsh'] = wsh.reshape(128, 48)
    hb = inp['hy_bias'][0]
    bT = hb.reshape(2, 128, 4).transpose(2, 0, 1)
    m['biasT'] = np.ascontiguousarray(np.repeat(bT[:, None], 32, axis=1).reshape(128, 256), dtype=f32)
    W1 = np.zeros((128, 128), f32)
    W1[0:33, 0:64] = inp['w_f1'][0]
    W1[64:97, 64:128] = inp['w_f1'][0]
    m['W1blk'] = W1
    W2 = np.zeros((128, 128), f32)
    W2[0:64, 0:64] = inp['w_f2'][0]
    W2[64:128, 64:128] = inp['w_f2'][0]
    m['W2blk'] = W2
    mv = np.zeros((128, 4), f32)
    for jj, nm in enumerate(('freq_1', 'b_f1', 'freq_2', 'b_f2')):
        mv[0:64, jj] = inp[nm][0]
        mv[64:128, jj] = inp[nm][0]
    m['mlpv'] = mv
    w3 = inp['w_f3'][0].reshape(64, 2, 2, 8, 64)
    W3 = np.zeros((128, 8, 2, 2, 64), f32)
    for dd in range(2):
        W3[64 * dd:64 * (dd + 1), :, :, dd, :] = w3[:, :, dd, :, :].transpose(0, 2, 1, 3)
    m['W3blk'] = W3.reshape(128, 2048)
    C = host_constants()
    for nm in CONST_NAMES:
        m[nm] = C[nm]
    return m


IN_SHAPES = {
    'x': ([L, D], F32), 'w_in': ([D, 5280], F32), 'gT': ([128, 8], F32), 'bgT': ([128, 16], F32),
    'w_uq': ([384, 768], F32), 'w_ukv': ([256, 1024], F32), 'gcqT': ([128, 3], F32), 'gckvT': ([128, 2], F32),
    'gqk': ([1, 192], F32), 'w_attn_out': ([512, D], F32), 'w_hy_out': ([512, D], F32), 'w_out': ([D, D], F32),
    'cosT': ([128, NT * 16], F32), 'sinT': ([128, NT * 16], F32),
    'wsh': ([128, 48], F32), 'biasT': ([128, 256], F32), 'W1blk': ([128, 128], F32), 'W2blk': ([128, 128], F32),
    'mlpv': ([128, 4], F32), 'W3blk': ([128, 2048], F32),
    'FA1': ([128, 256], BF), 'FA2': ([128, 256], BF), 'WBr': ([128, 128], BF), 'WBi': ([128, 128], BF),
    'WBni': ([128, 128], BF), 'WI1': ([128, 256], BF), 'WI2': ([128, 256], BF), 'FIr': ([128, 128], BF),
    'FIi': ([128, 128], BF), 'TWa': ([128, 256], F32), 'TWb': ([128, 256], F32), 'TIa': ([128, 256], F32),
    'TIb': ([128, 256], F32), 'zs_hi': ([128, L], BF), 'zs_lo': ([128, L], BF), 'tfull': ([128, 64], F32),
    'negd': ([1, HYW], F32),
}
CONST_NAMES = ('cosT', 'sinT', 'FA1', 'FA2', 'WBr', 'WBi', 'WBni', 'WI1', 'WI2', 'FIr', 'FIi', 'TWa', 'TWb', 'TIa', 'TIb',
               'zs_hi', 'zs_lo', 'tfull', 'negd')


def build_nc(debug=None):
    debug = debug or set()
    nc = bass.Bass("TRN2", target_bir_lowering=False)
    T = {}
    for name, (shape, dt) in IN_SHAPES.items():
        T[name] = nc.dram_tensor(name, shape, dt, kind="ExternalInput").ap()
    T['out'] = nc.dram_tensor("out", [L, D], F32, kind="ExternalOutput").ap()
    skind = dict(kind="ExternalOutput") if 'dump' in debug else {}
    T['hT_d'] = nc.dram_tensor("hT_d", [NB, 128, 8 * 512], BF, **skind).ap()
    T['at_d'] = nc.dram_tensor("at_d", [NB, 64, 8 * 512], BF, **skind).ap()
    T['h2_d'] = nc.dram_tensor("h2_d", [128, L], BF, **skind).ap()
    if 'dump' in debug:
        T['dbg_tm'] = nc.dram_tensor("dbg_tm", [128, 4096], BF, kind="ExternalOutput").ap()
        T['dbg_k'] = nc.dram_tensor("dbg_k", [128, 8192], BF, kind="ExternalOutput").ap()
        T['dbg_ks'] = nc.dram_tensor("dbg_ks", [128, 8192], BF, kind="ExternalOutput").ap()
    if 'yz_in' in debug:
        T['yz_d'] = nc.dram_tensor("yz_d", [NB, 128, 4 * 512], BF, kind="ExternalInput").ap()
    else:
        T['yz_d'] = nc.dram_tensor("yz_d", [NB, 128, 4 * 512], BF, **skind).ap()
    phases = debug & {'p1', 'p2', 'p3', 'p4'} or {'p1', 'p2', 'p3', 'p4'}
    do_p2 = 'p2' in phases and 'yz_in' not in debug
    if 'p1' in phases or do_p2:
        phase12(nc, T, with_p1=('p1' in phases), with_p2a=do_p2)
    if do_p2:
        phase2(nc, T)
    if 'p3' in phases:
        phase3(nc, T)
    if 'p4' in phases:
        phase4(nc, T)
    return nc


def kernel(**inputs):
    inp = {k_: np.asarray(v) for k_, v in inputs.items()}
    nc = build_nc()
    in_maps = [prep_inputs(inp, b) for b in range(8)]
    res = run_bass_kernel_spmd(nc, in_maps, core_ids=list(range(8)))
    out = np.stack([np.asarray(r['out'], dtype=np.float32) for r in res.results], axis=0)
    return out
```
